# Optimizing a Trainium2 kernel written in Bass

```python
import math
import jax, jax.numpy as jnp
from jax import lax
import numpy as np

D_MODEL = 1024
BATCH = 8
SEQ = 8192
DEPTH = 1
DEC_BATCH = 8
DEC_SEQ = 4096
PAST_LEN = 128

D_RWKV = 512
HEAD_DIM = 64
N_HEADS = D_RWKV // HEAD_DIM
D_S5 = D_MODEL - D_RWKV
S5_GROUP = 16
N_S5_GROUPS = D_S5 // S5_GROUP
S5_STATE = 64
DECAY_LORA = 64
ICLR_LORA = 64
GATE_LORA = 128
N_DIR = 2
D_FF = int(math.ceil(8 * D_MODEL / 3 / 256)) * 256
RWKV_COLS = 3 * D_RWKV + N_DIR * DECAY_LORA + N_DIR * ICLR_LORA + GATE_LORA
D_IN_PROJ = RWKV_COLS + D_S5
RMS_EPS = 1e-6
LNX_EPS = 64e-5

kernel_name = 'hybrid_rwkv7_s5_adaln_encoder'


def rmsnorm(x, g):
    xf = x.astype(jnp.float32)
    y = xf * lax.rsqrt(jnp.mean(xf * xf, axis=-1, keepdims=True) + RMS_EPS)
    return (y * g.astype(jnp.float32)).astype(x.dtype)


def centred_shift(p):
    prev = jnp.pad(p[:, :-1], ((0, 0), (1, 0), (0, 0)))
    nxt = jnp.pad(p[:, 1:], ((0, 0), (0, 1), (0, 0)))
    return 0.5 * (prev + nxt)


def rwkv7_scan(r, w, k, v, kk, a, reverse):
    bsz, _, n_h, n_d = r.shape

    def step(S, inp):
        r_t, w_t, k_t, v_t, kk_t, a_t = inp
        sa = jnp.einsum('bhvk,bhk->bhv', S, -kk_t)
        S = (S * w_t[:, :, None, :] + sa[..., None] * (kk_t * a_t)[:, :, None, :]
             + v_t[..., None] * k_t[:, :, None, :])
        return S, jnp.einsum('bhvk,bhk->bhv', S, r_t)

    xs = tuple(jnp.moveaxis(t, 1, 0) for t in (r, w, k, v, kk, a))
    S0 = jnp.zeros((bsz, n_h, n_d, n_d), jnp.float32)
    _, y = lax.scan(step, S0, xs, reverse=reverse)
    return jnp.moveaxis(y, 0, 1)


def rwkv7_mixer(p, mu_shift, w0, w2, a0, a2, g2, k_k, k_a, r_k, lnx_g, lnx_b):
    bsz, t_len, _ = p.shape
    p = p + mu_shift * (centred_shift(p) - p)
    pf = p.astype(jnp.float32)
    heads = lambda t: t.reshape(bsz, t_len, N_HEADS, HEAD_DIM)
    o = 0
    r = pf[..., o:o + D_RWKV]; o += D_RWKV
    k = pf[..., o:o + D_RWKV]; o += D_RWKV
    v = pf[..., o:o + D_RWKV]; o += D_RWKV
    xw = pf[..., o:o + N_DIR * DECAY_LORA].reshape(bsz, t_len, N_DIR, DECAY_LORA); o += N_DIR * DECAY_LORA
    xa = pf[..., o:o + N_DIR * ICLR_LORA].reshape(bsz, t_len, N_DIR, ICLR_LORA); o += N_DIR * ICLR_LORA
    xg = pf[..., o:o + GATE_LORA]
    g = jax.nn.sigmoid(xg) @ g2.astype(jnp.float32)
    kk = heads(k * k_k.astype(jnp.float32))
    kk = kk * lax.rsqrt(jnp.maximum(jnp.sum(kk * kk, axis=-1, keepdims=True), 1e-24))
    rh, vh = heads(r), heads(v)
    r_k_f = r_k.astype(jnp.float32)
    k_a_f = k_a.astype(jnp.float32)
    ys, bonuses = [], []
    for d in range(N_DIR):
        wd = -jax.nn.softplus(-(w0[d].astype(jnp.float32) + jnp.tanh(xw[:, :, d]) @ w2[d].astype(jnp.float32))) - 0.5
        decay = jnp.exp(-jnp.exp(wd))
        ad = jax.nn.sigmoid(a0[d].astype(jnp.float32) + xa[:, :, d] @ a2[d].astype(jnp.float32))
        kd = heads(k * (1.0 + (ad - 1.0) * k_a_f))
        ys.append(rwkv7_scan(rh, heads(decay), kd, vh, kk, heads(ad), reverse=(d == 1)))
        bonuses.append(jnp.sum(rh * kd * r_k_f, axis=-1, keepdims=True) * vh)
    y = ys[0] + ys[1]
    mean = jnp.mean(y, axis=-1, keepdims=True)
    var = jnp.mean(jnp.square(y - mean), axis=-1, keepdims=True)
    y = ((y - mean) * lax.rsqrt(var + LNX_EPS) * lnx_g.astype(jnp.float32).reshape(N_HEADS, HEAD_DIM)
         + lnx_b.astype(jnp.float32).reshape(N_HEADS, HEAD_DIM))
    y = y + bonuses[0] + bonuses[1]
    return (y.reshape(bsz, t_len, D_RWKV) * g).astype(p.dtype)


def _diag_combine(e1, e2):
    a1, b1 = e1
    a2, b2 = e2
    return (a1 * a2, a2 * b1 + b2)


def s5_mixer(u, lam_re, lam_im, log_dt, b_re, b_im, c_re, c_im, d_skip, w_glu, b_glu, s5_out_g):
    f32 = jnp.float32
    bsz, t_len, _ = u.shape
    uf = u.astype(f32).reshape(bsz, t_len, N_S5_GROUPS, S5_GROUP)
    uc = uf.astype(jnp.complex64)
    y = d_skip.astype(f32) * uf
    for d in range(N_DIR):
        lam = lax.complex(lam_re[d].astype(f32), lam_im[d].astype(f32))
        dt = jnp.exp(log_dt[d].astype(f32))[:, None]
        a_bar = jnp.exp(lam * dt)
        b_bar = ((a_bar - 1.0) / lam)[..., None] * lax.complex(b_re[d].astype(f32), b_im[d].astype(f32))
        bu = jnp.einsum('gph,btgh->btgp', b_bar, uc)
        _, states = lax.associative_scan(
            _diag_combine, (jnp.broadcast_to(a_bar, bu.shape), bu), reverse=(d == 1), axis=1)
        c_mat = lax.complex(c_re[d].astype(f32), c_im[d].astype(f32))
        y = y + jnp.einsum('ghp,btgp->btgh', c_mat, states).real
    z = jax.nn.gelu(y)
    z = z * jax.nn.sigmoid(jnp.einsum('btgh,ghk->btgk', z, w_glu.astype(f32)) + b_glu.astype(f32))
    return rmsnorm(z.reshape(bsz, t_len, D_S5), s5_out_g).astype(u.dtype)


def encoder_layer(x, c, norm1_g, w_ada, b_ada, w_in, mu_shift, w0, w2, a0, a2, g2, k_k, k_a, r_k,
                  lnx_g, lnx_b, lam_re, lam_im, log_dt, b_re, b_im, c_re, c_im, d_skip, w_glu, b_glu,
                  s5_out_g, w_out, norm2_g, w_ff1, w_ff3, w_ff2):
    mod = (jax.nn.silu(c) @ w_ada + b_ada)[:, None, :]
    shift1, scale1, gate1, shift2, scale2, gate2 = jnp.split(mod, 6, axis=-1)
    h = rmsnorm(x, norm1_g) * (1.0 + scale1) + shift1
    proj = h @ w_in
    y_rwkv = rwkv7_mixer(proj[..., :RWKV_COLS], mu_shift, w0, w2, a0, a2, g2, k_k, k_a, r_k, lnx_g, lnx_b)
    y_s5 = s5_mixer(proj[..., RWKV_COLS:], lam_re, lam_im, log_dt, b_re, b_im, c_re, c_im,
                    d_skip, w_glu, b_glu, s5_out_g)
    x = x + gate1 * (jnp.concatenate([y_rwkv, y_s5], axis=-1) @ w_out)
    h = rmsnorm(x, norm2_g) * (1.0 + scale2) + shift2
    f = (jax.nn.silu(h @ w_ff1) * (h @ w_ff3)) @ w_ff2
    return x + gate2 * f


def encoder_trunk(x, c, layer_params, final_g):
    for i in range(DEPTH):
        x = encoder_layer(x, c, *[p[i] for p in layer_params])
    return rmsnorm(x, final_g)


def setup_inputs(seed: int = 0) -> dict:
    key = jax.random.key(seed)
    ks = jax.random.split(key, 36)
    f32 = jnp.float32
    L, G, P, H = DEPTH, N_S5_GROUPS, S5_STATE, S5_GROUP

    def nrm(k, shape, scale):
        return scale * jax.random.normal(k, shape, f32)

    return {
        'x_prompt': nrm(ks[0], (BATCH, SEQ, D_MODEL), 1.0),
        'x_sample': nrm(ks[1], (DEC_BATCH, DEC_SEQ, D_MODEL), 1.0),
        'c_prompt': nrm(ks[2], (BATCH, D_MODEL), 1.0),
        'c_sample': nrm(ks[3], (DEC_BATCH, D_MODEL), 1.0),
        'norm1_g': 1.0 + nrm(ks[4], (L, D_MODEL), 0.02),
        'w_ada': nrm(ks[5], (L, D_MODEL, 6 * D_MODEL), 0.5 * D_MODEL ** -0.5),
        'b_ada': nrm(ks[6], (L, 6 * D_MODEL), 0.02),
        'w_in': nrm(ks[7], (L, D_MODEL, D_IN_PROJ), D_MODEL ** -0.5),
        'mu_shift': jax.random.uniform(ks[8], (L, RWKV_COLS), f32, 0.2, 0.8),
        'w0': jnp.broadcast_to(jnp.linspace(-6.0, -1.0, D_RWKV, dtype=f32), (L, N_DIR, D_RWKV))
              + nrm(ks[9], (L, N_DIR, D_RWKV), 0.1),
        'w2': nrm(ks[10], (L, N_DIR, DECAY_LORA, D_RWKV), 0.1 * DECAY_LORA ** -0.5),
        'a0': nrm(ks[11], (L, N_DIR, D_RWKV), 0.1),
        'a2': nrm(ks[12], (L, N_DIR, ICLR_LORA, D_RWKV), 0.5 * ICLR_LORA ** -0.5),
        'g2': nrm(ks[13], (L, GATE_LORA, D_RWKV), GATE_LORA ** -0.5),
        'k_k': 0.85 + nrm(ks[14], (L, D_RWKV), 0.02),
        'k_a': 1.0 + nrm(ks[15], (L, D_RWKV), 0.02),
        'r_k': -0.04 + nrm(ks[16], (L, N_HEADS, HEAD_DIM), 0.1),
        'lnx_g': 1.0 + nrm(ks[17], (L, D_RWKV), 0.02),
        'lnx_b': nrm(ks[18], (L, D_RWKV), 0.02),
        'lam_re': -0.5 + nrm(ks[19], (L, N_DIR, G, P), 0.01),
        'lam_im': jnp.pi * jnp.arange(P, dtype=f32) + nrm(ks[20], (L, N_DIR, G, P), 0.01),
        'log_dt': jax.random.uniform(ks[21], (L, N_DIR, G), f32, math.log(1e-3), math.log(1e-1)),
        'b_re': nrm(ks[22], (L, N_DIR, G, P, H), (2 * H) ** -0.5),
        'b_im': nrm(ks[23], (L, N_DIR, G, P, H), (2 * H) ** -0.5),
        'c_re': nrm(ks[24], (L, N_DIR, G, H, P), (2 * P) ** -0.5),
        'c_im': nrm(ks[25], (L, N_DIR, G, H, P), (2 * P) ** -0.5),
        'd_skip': nrm(ks[26], (L, G, H), 1.0),
        'w_glu': nrm(ks[27], (L, G, H, H), H ** -0.5),
        'b_glu': nrm(ks[28], (L, G, H), 0.02),
        's5_out_g': 1.0 + nrm(ks[29], (L, D_S5), 0.02),
        'w_out': nrm(ks[30], (L, D_MODEL, D_MODEL), D_MODEL ** -0.5),
        'norm2_g': 1.0 + nrm(ks[31], (L, D_MODEL), 0.02),
        'w_ff1': nrm(ks[32], (L, D_MODEL, D_FF), D_MODEL ** -0.5),
        'w_ff3': nrm(ks[33], (L, D_MODEL, D_FF), D_MODEL ** -0.5),
        'w_ff2': nrm(ks[34], (L, D_FF, D_MODEL), D_FF ** -0.5),
        'final_g': 1.0 + nrm(ks[35], (D_MODEL,), 0.02),
    }


def reference(x_prompt, x_sample, c_prompt, c_sample, norm1_g, w_ada, b_ada, w_in, mu_shift, w0, w2,
              a0, a2, g2, k_k, k_a, r_k, lnx_g, lnx_b, lam_re, lam_im, log_dt, b_re, b_im, c_re, c_im,
              d_skip, w_glu, b_glu, s5_out_g, w_out, norm2_g, w_ff1, w_ff3, w_ff2, final_g):
    layer_params = (norm1_g, w_ada, b_ada, w_in, mu_shift, w0, w2, a0, a2, g2, k_k, k_a, r_k,
                    lnx_g, lnx_b, lam_re, lam_im, log_dt, b_re, b_im, c_re, c_im, d_skip, w_glu,
                    b_glu, s5_out_g, w_out, norm2_g, w_ff1, w_ff3, w_ff2)
    y_prompt = encoder_trunk(x_prompt, c_prompt, layer_params, final_g)
    y_sample = encoder_trunk(x_sample, c_sample, layer_params, final_g)
    return (y_prompt, y_sample)
```

```python
import os
import math
import numpy as np
import concourse.bass as bass
import concourse.mybir as mybir
from concourse.bass_utils import run_bass_kernel_spmd

F32 = mybir.dt.float32
BF16 = mybir.dt.bfloat16
AF = mybir.ActivationFunctionType
ALU = mybir.AluOpType
AX = mybir.AxisListType

D = 1024
DFF = 2816
NPROJ = 2432
RW = 1920
NDS = 12
RMS_EPS = 1e-6
LNX_EPS = 64e-5


class Buf:
    __slots__ = ("w", "r")

    def __init__(self):
        self.w = None
        self.r = {}


class MK:
    BLK = {"pe": "tensor", "dve": "vector", "act": "scalar", "pool": "gpsimd", "sp": "sync"}

    def __init__(self, nc, same=True):
        self.nc = nc
        self.same = same
        self.names = ["pe", "dve", "act", "pool", "sp"]
        self.sem = {k: nc.alloc_semaphore(name="s_" + k) for k in self.names}
        self.cnt = {k: 0 for k in self.names}
        self.seen = {k: {} for k in self.names}
        self.prog = {k: [] for k in self.names}
        self.dsem = [nc.alloc_semaphore(name="d%d" % i) for i in range(NDS)]
        self.dcnt = [0] * NDS
        self.dnext = 0
        self.deferred = None

    def semof(self, key):
        if isinstance(key, tuple):
            return self.dsem[key[1]]
        return self.sem[key]

    def _deps(self, e, reads, writes):
        deps = {}

        def add(k, v):
            if deps.get(k, 0) < v:
                deps[k] = v

        for b in reads:
            if b.w:
                add(*b.w)
        for b in writes:
            if b.w:
                add(*b.w)
            for k, v in b.r.items():
                add(k, v)
        out = []
        for k, v in deps.items():
            if k == e and (e == "pe" or not self.same):
                continue
            if self.seen[e].get(k, 0) >= v:
                continue
            self.seen[e][k] = v
            out.append((k, v))
        return out

    def _mark(self, tok, reads, writes):
        k, v = tok
        for b in reads:
            if b.r.get(k, 0) < v:
                b.r[k] = v
        for b in writes:
            b.w = tok
            b.r = {}

    def op(self, e, fn, reads=(), writes=()):
        if self.deferred is not None:
            self.deferred.append((0, e, fn, reads, writes))
            return
        waits = self._deps(e, reads, writes)
        self.cnt[e] += 1
        tok = (e, self.cnt[e])
        self.prog[e].append((waits, fn, self.sem[e], 1))
        self._mark(tok, reads, writes)

    def replay(self, pending, n):
        keep = self.deferred
        self.deferred = None
        last = None
        cnt = 0
        while pending and (cnt < n or last == "pe"):
            kind, e, fn, reads, writes = pending.pop(0)
            (self.dma if kind else self.op)(e, fn, reads, writes)
            last = e if not kind else None
            cnt += 1
        self.deferred = keep

    def dma(self, q, fn, reads=(), writes=()):
        if self.deferred is not None:
            self.deferred.append((1, q, fn, reads, writes))
            return
        i = self.dnext
        self.dnext = (i + 1) % NDS
        key = ("d", i)
        waits = self._deps(q, reads, writes)
        if self.dcnt[i] > 0 and self.seen[q].get(key, 0) < self.dcnt[i]:
            waits.append((key, self.dcnt[i]))
            self.seen[q][key] = self.dcnt[i]
        self.dcnt[i] += 16
        tok = (key, self.dcnt[i])
        self.prog[q].append((waits, fn, self.dsem[i], 16))
        self._mark(tok, reads, writes)

    def flush(self, final=False):
        nc = self.nc
        fin = []
        for i in (range(NDS) if final else []):
            if self.dcnt[i] > 0:
                fin.append((("d", i), self.dcnt[i]))
        for k in (self.names if final else []):
            if k != "sp" and self.cnt[k] > 0:
                fin.append((k, self.cnt[k]))
        with nc.Block() as block:
            for e in self.names:
                prog = self.prog[e]
                extra = fin if e == "sp" else []

                def body(eng, prog=prog, extra=extra):
                    for waits, fn, sem, inc in prog:
                        for k, v in waits:
                            eng.wait_ge(self.semof(k), v)
                        fn(eng).then_inc(sem, inc)
                    for k, v in extra:
                        eng.wait_ge(self.semof(k), v)

                getattr(block, self.BLK[e])(body)
        self.prog = {k: [] for k in self.names}

    def emit(self):
        self.flush(final=True)


def build_program(TS, dbg=False, upto=99):
    import contextlib
    nc = bass.Bass("TRN2", target_bir_lowering=False)
    mk = MK(nc)
    TT = sum(TS)
    NS = len(TS)
    WP = TT + 2 * NS
    SEQ = []
    o = 0
    for s, T in enumerate(TS):
        SEQ.append((o, o + 2 * s, T))
        o += T

    def din(name, shape):
        return nc.dram_tensor(name, list(shape), F32, kind="ExternalInput").ap()

    def dscr(name, shape):
        return nc.dram_tensor(name, list(shape), F32, kind=("ExternalOutput" if dbg else "Internal")).ap()

    x_in = din("x", (TT, D))
    c_in = din("c", (NS, D))
    norm1_g = din("norm1_g", (D,))
    w_ada = din("w_ada", (D, 6 * D))
    b_ada = din("b_ada", (6 * D,))
    w_in = din("w_in", (D, NPROJ))
    mu_shift = din("mu_shift", (RW,))
    w0 = din("w0", (2, 512)); w2 = din("w2", (2, 64, 512))
    a0 = din("a0", (2, 512)); a2 = din("a2", (2, 64, 512))
    g2 = din("g2", (128, 512))
    k_k = din("k_k", (512,)); k_a = din("k_a", (512,)); r_k = din("r_k", (512,))
    lnx_g = din("lnx_g", (512,)); lnx_b = din("lnx_b", (512,))
    lam_re = din("lam_re", (2, 32, 64)); lam_im = din("lam_im", (2, 32, 64)); log_dt = din("log_dt", (2, 32))
    b_re = din("b_re", (2, 32, 64, 16)); b_im = din("b_im", (2, 32, 64, 16))
    c_re = din("c_re", (2, 32, 16, 64)); c_im = din("c_im", (2, 32, 16, 64))
    d_skip = din("d_skip", (512,)); w_glu = din("w_glu", (32, 16, 16)); b_glu = din("b_glu", (512,))
    s5_out_g = din("s5_out_g", (512,))
    w_out = din("w_out", (D, D)); norm2_g = din("norm2_g", (D,))
    w_ff1 = din("w_ff1", (D, DFF)); w_ff3 = din("w_ff3", (D, DFF)); w_ff2 = din("w_ff2", (DFF, D))
    final_g = din("final_g", (D,))
    y_out = nc.dram_tensor("y", [TT, D], F32, kind="ExternalOutput").ap()

    Pscr = dscr("Pscr", (NPROJ, WP))
    XT = dscr("XT", (D, TT))
    YD = dscr("YD", (2, 64, 8, TT))
    BD = dscr("BD", (2, 64, 8, TT))
    YS = dscr("YS", (512, TT))
    X1 = dscr("X1", (D, TT))
    MODS = dscr("MODS", (128, 48 * NS))
    b_P = Buf(); b_XT = Buf(); b_YD = Buf(); b_BD = Buf(); b_YS = Buf(); b_X1 = Buf(); b_MODS = Buf()

    def tt(e, out, a, b, op, r, w):
        mk.op(e, lambda E: E.tensor_tensor(out=out, in0=a, in1=b, op=op), r, w)

    def ts(e, out, a, s1, s2, op0, op1, r, w):
        if op1 is None:
            mk.op(e, lambda E: E.tensor_scalar(out=out, in0=a, scalar1=s1, scalar2=None, op0=op0), r, w)
        else:
            mk.op(e, lambda E: E.tensor_scalar(out=out, in0=a, scalar1=s1, scalar2=s2, op0=op0, op1=op1), r, w)

    def stt(out, a, sc, b, op0, op1, r, w):
        mk.op("dve", lambda E: E.scalar_tensor_tensor(out=out, in0=a, scalar=sc, in1=b, op0=op0, op1=op1), r, w)

    def act(out, a, func, r, w, bias=0.0, scale=1.0):
        mk.op("act", lambda E: E.activation(out=out, in_=a, func=func, bias=bias, scale=scale), r, w)

    def cp(e, out, a, r, w):
        if e == "act":
            mk.op("act", lambda E: E.activation(out=out, in_=a, func=AF.Copy), r, w)
        else:
            mk.op(e, lambda E: E.tensor_copy(out=out, in_=a), r, w)

    def mm(out, lhsT, rhs, st, sp_, r, w):
        mk.op("pe", lambda E: E.matmul(out=out, lhsT=lhsT, rhs=rhs, start=st, stop=sp_), r, w)

    def dma(q, out, in_, r, w, slow=False):
        if slow:
            mk.dma(q, lambda E: E.dma_start(out=out, in_=in_, allow_slow_non_contiguous=True), r, w)
        else:
            mk.dma(q, lambda E: E.dma_start(out=out, in_=in_), r, w)

    def scope(pfx=""):
        es = contextlib.ExitStack()

        def sb(name, shape, dt=F32):
            return es.enter_context(nc.sbuf_tensor(pfx + name, list(shape), dt))

        def ps(name, shape, dt=F32):
            return es.enter_context(nc.psum_tensor(pfx + name, list(shape), dt))
        return es, sb, ps

    def consts(sb):
        ident = sb("ident", [128, 128]); b_ident = Buf()
        mk.op("pool", lambda E: E.memset(ident[:], 1.0), (), [b_ident])
        mk.op("pool", lambda E: E.affine_select(out=ident[:], in_=ident[:], pattern=[[-1, 128]],
                                                compare_op=ALU.is_equal, fill=0.0, base=0, channel_multiplier=1),
              [b_ident], [b_ident])
        return ident, b_ident
    outer, osb, ops_ = scope("o_")
    ident, b_ident = consts(osb)
    modT = osb("modT", [128, 48, NS]); b_mod = Buf()
    sc1 = osb("sc1", [128, 8, NS]); b_sc1 = Buf()

    def pass0():
        es, sb, ps = scope("p0_")
        with es:
            ones_bf = sb("ones_bf", [128, 128], BF16); b_ones = Buf()
            mk.op("pool", lambda E: E.memset(ones_bf[:], 1.0), (), [b_ones])
            cT = sb("cT", [128, 8, NS]); b_cT = Buf()
            scT = sb("scT", [128, 8, NS]); b_scT = Buf()
            for s in range(NS):
                dma("sp", cT[:, :, s], c_in[s].rearrange("(k p) -> p k", p=128), (), [b_cT], slow=True)
            act(scT[:], cT[:], AF.Silu, [b_cT], [b_scT])
            badaT = sb("badaT", [128, 48]); b_bada = Buf()
            dma("sp", badaT[:], b_ada.rearrange("(k p) -> p k", p=128), (), [b_bada], slow=True)
            g1T = sb("g1T", [128, 8]); b_g1 = Buf()
            dma("sp", g1T[:], norm1_g.rearrange("(k p) -> p k", p=128), (), [b_g1], slow=True)
            wada_t = [sb("wada%d" % i, [128, 8, 256]) for i in range(2)]
            b_wada = [Buf(), Buf()]
            ps_mod_full = ps("ps_mod", [128, 512]); b_psmod = Buf()
            ps_mod = ps_mod_full[:, 0:4 * NS].rearrange("p (a b) -> p a b", b=NS)
            for slab in range(24):
                wt = wada_t[slab % 2]; bw = b_wada[slab % 2]
                dma("sp" if slab % 2 == 0 else "act", wt[:],
                    w_ada[:, slab * 256:(slab + 1) * 256].rearrange("(k p) n -> p k n", p=128), (), [bw])
                for j in range(2):
                    for k in range(8):
                        mm(ps_mod[:, j, :], wt[:, k, j * 128:(j + 1) * 128], scT[:, k, :], k == 0, k == 7,
                           [bw, b_scT], [b_psmod])
                for s in range(NS):
                    tt("dve", modT[:, slab * 2:(slab + 1) * 2, s], ps_mod[:, 0:2, s],
                       badaT[:, slab * 2:(slab + 1) * 2], ALU.add, [b_psmod, b_bada], [b_mod])
            for s in range(NS):
                stt(sc1[:, :, s], modT[:, 8:16, s], 1.0, g1T[:], ALU.add, ALU.mult, [b_mod, b_g1], [b_sc1])

            w_in_bf = sb("w_in_bf", [128, 8, NPROJ], BF16); b_win = Buf()
            wst = [sb("wst%d" % i, [128, NPROJ]) for i in range(2)]; b_wst = [Buf(), Buf()]
            for k in range(8):
                dma("sp", wst[k % 2][:], w_in[k * 128:(k + 1) * 128, :], (), [b_wst[k % 2]])
                cp("pool", w_in_bf[:, k, :], wst[k % 2][:], [b_wst[k % 2]], [b_win])

            NT = 512
            xtm = [sb("xtm%d" % i, [128, 4, D]) for i in range(2)]; b_xtm = [Buf(), Buf()]
            xT = sb("xT", [128, 8, NT]); b_xT = Buf()
            sq = sb("sq", [128, 8, NT], BF16); b_sq = Buf()
            rstd = sb("rstd", [128, NT]); b_rstd = Buf()
            tmp = sb("tmp0", [128, NT]); b_tmp = Buf()
            hT = sb("hT", [128, 8, NT], BF16); b_hT = Buf()
            pev = [sb("pev%d" % i, [128, NT]) for i in range(3)]; b_pev = [Buf() for _ in range(3)]
            zcol = sb("zcol", [128, 1]); b_zcol = Buf()
            mk.op("pool", lambda E: E.memset(zcol[:], 0.0), (), [b_zcol])
            pst = [ps("pst%d" % i, [128, NT]) for i in range(4)]; b_pst = [Buf() for _ in range(4)]
            psm = [ps("psm%d" % i, [128, NT]) for i in range(3)]; b_psm = [Buf() for _ in range(3)]
            for s, (toff, coff, T) in enumerate(SEQ):
                for mc in range(19):
                    for cc in (coff, coff + T + 1):
                        dma("sp", Pscr[mc * 128:(mc + 1) * 128, cc:cc + 1], zcol[:], [b_zcol], [b_P], slow=True)
                for ti in range(T // NT):
                    t0 = toff + ti * NT
                    xt = xtm[ti % 2]; bx = b_xtm[ti % 2]
                    dma("sp", xt[:], x_in[t0:t0 + NT, :].rearrange("(j p) d -> p j d", p=128), (), [bx])
                    for dc in range(8):
                        pt = pst[dc % 4]; bp = b_pst[dc % 4]
                        for j in range(4):
                            mk.op("pe", lambda E, pt=pt, xt=xt, j=j, dc=dc: E.transpose(
                                out=pt[:, j * 128:(j + 1) * 128], in_=xt[:, j, dc * 128:(dc + 1) * 128],
                                identity=ident[:]), [bx, b_ident], [bp])
                        cp("dve", xT[:, dc, :], pt[:], [bp], [b_xT])
                        act(sq[:, dc, :], pt[:], AF.Square, [bp, b_xT], [b_sq])
                    dma("act", XT[:, t0:t0 + NT].rearrange("(k p) t -> p k t", p=128), xT[:], [b_xT], [b_XT])
                    pm = psm[0]; bpm = b_psm[0]
                    for dc in range(8):
                        mm(pm[:], ones_bf[:], sq[:, dc, :], dc == 0, dc == 7, [b_sq, b_ones], [bpm])
                    act(rstd[:], pm[:], AF.Sqrt, [bpm], [b_rstd], bias=RMS_EPS, scale=1.0 / D)
                    mk.op("dve", lambda E: E.reciprocal(out=rstd[:], in_=rstd[:]), [b_rstd], [b_rstd])
                    for dc in range(8):
                        tt("dve", tmp[:], xT[:, dc, :], rstd[:], ALU.mult, [b_xT, b_rstd], [b_tmp])
                        ts("dve", hT[:, dc, :], tmp[:], sc1[:, dc, s:s + 1], modT[:, dc, s:s + 1], ALU.mult, ALU.add,
                           [b_tmp, b_sc1, b_mod], [b_hT])
                    for mc in range(19):
                        i3 = mc % 3
                        pm = psm[i3]; bpm = b_psm[i3]
                        for dc in range(8):
                            mm(pm[:], w_in_bf[:, dc, mc * 128:(mc + 1) * 128], hT[:, dc, :], dc == 0, dc == 7,
                               [b_win, b_hT], [bpm])
                        pv = pev[i3]; bpv = b_pev[i3]
                        cp("act", pv[:], pm[:], [bpm], [bpv])
                        cc = coff + 1 + ti * NT
                        dma("sp", Pscr[mc * 128:(mc + 1) * 128, cc:cc + NT], pv[:], [bpv], [b_P])
            mk.flush(final=True)

    pass0()
    def rwkv_pass(d):
        rev = (d == 1)
        es, sb, ps = scope("rw%d_" % d)
        with es:
            NT2 = 128
            psr = [ps("psr%d" % i, [128, 2048]) for i in range(2)]
            b_psr = [Buf() for _ in range(2)]
            pctr = [0]

            def nextps():
                i = pctr[0] % 2
                pctr[0] += 1
                return psr[i], b_psr[i]

            def T4(name):
                return sb(name, [64, 8, NT2]), Buf()

            def ldp(name, src512):
                t = sb(name, [64, 8]); b = Buf()
                dma("sp", t[:], src512.rearrange("(h p) -> p h", p=64), (), [b], slow=True)
                return t, b

            mu3 = sb("mu3", [64, 24]); b_mu3 = Buf()
            dma("sp", mu3[:], mu_shift[0:1536].rearrange("(g p) -> p g", p=64), (), [b_mu3], slow=True)
            hm3 = sb("hm3", [64, 24]); om3 = sb("om3", [64, 24]); b_hm3 = Buf()
            ts("dve", hm3[:], mu3[:], 0.5, None, ALU.mult, None, [b_mu3], [b_hm3])
            ts("dve", om3[:], mu3[:], -1.0, 1.0, ALU.mult, ALU.add, [b_mu3], [b_hm3])
            muw = sb("muw", [64, 2]); b_muw = Buf()
            dma("sp", muw[:, 0:1], mu_shift[1536 + 64 * d:1600 + 64 * d].rearrange("(p o) -> p o", o=1), (), [b_muw], slow=True)
            dma("sp", muw[:, 1:2], mu_shift[1664 + 64 * d:1728 + 64 * d].rearrange("(p o) -> p o", o=1), (), [b_muw], slow=True)
            hmw = sb("hmw", [64, 2]); omw = sb("omw", [64, 2]); b_hmw = Buf()
            ts("dve", hmw[:], muw[:], 0.5, None, ALU.mult, None, [b_muw], [b_hmw])
            ts("dve", omw[:], muw[:], -1.0, 1.0, ALU.mult, ALU.add, [b_muw], [b_hmw])
            w0d, b_w0d = ldp("w0d", w0[d]); a0d, b_a0d = ldp("a0d", a0[d])
            kk_, b_kk_ = ldp("kk_", k_k); ka_, b_ka_ = ldp("ka_", k_a); rk_, b_rk_ = ldp("rk_", r_k)
            omka = sb("omka", [64, 8]); b_omka = Buf()
            ts("dve", omka[:], ka_[:], -1.0, 1.0, ALU.mult, ALU.add, [b_ka_], [b_omka])
            w2d = sb("w2d", [64, 512]); a2d = sb("a2d", [64, 512]); b_w2d = Buf()
            dma("sp", w2d[:], w2[d], (), [b_w2d]); dma("sp", a2d[:], a2[d], (), [b_w2d])
            ones64 = sb("ones64", [64, 64]); b_c = Buf()
            mk.op("pool", lambda E: E.memset(ones64[:], 1.0), (), [b_c])
            maskA = sb("maskA", [64, 128]); maskL = sb("maskL", [64, 64]); MS = sb("MS", [64, 8 * NT2])
            mk.op("pool", lambda E: E.memset(maskA[:], 1.0), (), [b_c])
            mk.op("pool", lambda E: E.memset(maskL[:], 1.0), (), [b_c])
            mk.op("pool", lambda E: E.memset(MS[:], 1.0), (), [b_c])
            zc_ = 63 if rev else 0
            mk.op("pool", lambda E: E.memset(MS[:].rearrange("p (a l) -> p a l", l=64)[:, :, zc_:zc_ + 1], 0.0), [b_c], [b_c])

            def asel(ap, upper, strict):
                pat = [[1, 64]] if upper else [[-1, 64]]
                cm = -1 if upper else 1
                mk.op("pool", lambda E: E.affine_select(out=ap, in_=ap, pattern=pat, compare_op=ALU.is_ge, fill=0.0,
                                                        base=(-1 if strict else 0), channel_multiplier=cm),
                      [b_c], [b_c])
            asel(maskA[:, 0:64], not rev, True)
            asel(maskA[:, 64:128], not rev, False)
            asel(maskL[:], rev, True)
            mA = maskA[:, None, :].broadcast_to([64, 8, 128])
            mL = maskL[:, None, :].broadcast_to([64, 8, 64])
            id64 = ident[0:64, 0:64]
            idbc = ident[0:64, None, 0:64].broadcast_to([64, 8, 64])

            Lr = [sb("Lq%d" % q, [64, 8, NT2 + 2]) for q in range(2)]; b_L = [Buf() for _ in range(2)]
            Lr.append(Lr[0]); b_L.append(b_L[0])
            XW = sb("XW", [64, NT2 + 2]); XA = sb("XA", [64, NT2 + 2]); b_XW = Buf(); b_XA = Buf()
            T1, b_T1 = T4("T1")
            SH = [T4("SH%d" % q) for q in range(2)]
            (Rp, b_Rp), (Kp, b_Kp) = SH
            tt0, cp0 = tt, cp
            tw = sb("tw", [64, NT2]); b_tw = Buf()
            xwp = sb("xwp", [64, NT2]); xap = sb("xap", [64, NT2]); b_xwp = Buf(); b_xap = Buf()
            XB, b_XB = T4("XB"); E2, b_E2 = T4("E2"); AD, b_AD = T4("AD"); KR, b_KR = T4("KR")
            SS, b_SS = T4("SS"); KD, b_KD = T4("KD"); AB, b_AB = T4("AB"); BON, b_BON = XB, b_XB
            G, b_G = T4("G"); D1, b_D1 = E2, b_E2; D2, b_D2 = AD, b_AD; EP, b_EP = XB, b_XB; EN, b_EN = SS, b_SS
            T2, b_T2 = SS, b_SS
            SD = F32
            SETS = []
            for i_ in range(2):
                st_ = []
                for nm, shp in (("AR", [64, 8, 2, 128]), ("KT", [64, 8, NT2]), ("BT", [64, 8, NT2]), ("KH", [64, 8, NT2]),
                                ("BH", [64, 8, NT2]), ("Vp", [64, 8, NT2]), ("GL", [64, 16])):
                    st_ += [sb("%s_%d" % (nm, i_), shp), Buf()]
                SETS.append(st_)
            YT, b_YT = T4("YT")
            MT1 = sb("MT1", [64, 2, 8, 128], SD); MT2 = sb("MT2", [64, 2, 8, 128], SD); b_MT1 = Buf(); b_MT2 = Buf()
            P0 = sb("P0", [64, 2, 8, 64], SD); b_P0 = Buf()
            PP = [sb("PP%d" % i, [64, 2, 16, 64], SD) for i in range(2)]; b_PP = [Buf(), Buf()]
            Zt = [sb("Zt%d" % i, [64, 2, 8, 128], SD) for i in range(2)]; b_Zt = [Buf() for _ in range(2)]
            VT = sb("VT", [64, 2, 8, 64], SD); BHt = sb("BHt", [64, 2, 8, 64], SD); KHt = sb("KHt", [64, 2, 8, 64], SD)
            QT = sb("QT", [64, 2, 8, 64]); MM = sb("MM", [64, 2, 8, 64]); DG = P0
            b_VT = Buf(); b_BHt = Buf(); b_KHt = Buf(); b_QT = Buf(); b_MM = Buf(); b_DG = b_P0
            STt = [sb("ST%d" % i, [64, 8, 64]) for i in range(2)]; b_ST = [Buf(), Buf()]
            mA16 = maskA[:, None, :].broadcast_to([64, 16, 128])
            mL16 = maskL[:, None, :].broadcast_to([64, 16, 64])
            idbc16 = ident[0:64, None, 0:64].broadcast_to([64, 16, 64])

            def f16(t):
                return t[:].rearrange("p c h n -> p (c h) n")

            def pv(p, lo, n):
                return p[0:64, lo:lo + 16 * n].rearrange("p (a n) -> p a n", n=n)

            def v3(p, n):
                return p[0:64, 0:8 * n].rearrange("p (h n) -> p h n", n=n)

            def bc(t, lo, hi, n):
                return t[:, lo:hi, None].broadcast_to([64, hi - lo, n])

            def c4(t):
                return t[:].rearrange("p h (c l) -> p h c l", l=64)

            for s, (toff, coff, T) in enumerate(SEQ):
                sti_ = [0]
                mk.op("pool", lambda E: E.memset(STt[0][:], 0.0), (), [b_ST[0]])
                ntile = T // NT2
                order = list(range(ntile - 1, -1, -1) if rev else range(ntile))

                def prep(ti, AR, b_AR, KT, b_KT, BT, b_BT, KH, b_KH, BH, b_BH, Vp, b_Vp, GL, b_GL):
                    ARb, b_ARb = AR, b_AR
                    tl = ti * NT2
                    c0 = coff + tl
                    tg = toff + tl
                    def ldq(q):
                        dma("sp" if q != 1 else "act", Lr[q][:],
                            Pscr[q * 512:(q + 1) * 512, c0:c0 + NT2 + 2].rearrange("(h p) t -> p h t", p=64),
                            [b_P], [b_L[q]])

                    def shq(q):
                        Lq = Lr[q]; S_, bS = (SH[q] if q < 2 else (Vp, b_Vp))
                        tt("pool", T1[:], Lq[:, :, 0:NT2], Lq[:, :, 2:NT2 + 2], ALU.add, [b_L[q]], [b_T1])
                        tt("pool", T1[:], T1[:], bc(hm3, 8 * q, 8 * q + 8, NT2), ALU.mult, [b_T1, b_hm3], [b_T1])
                        tt("pool", S_[:], Lq[:, :, 1:NT2 + 1], bc(om3, 8 * q, 8 * q + 8, NT2), ALU.mult,
                           [b_L[q], b_hm3], [bS])
                        tt("pool", S_[:], S_[:], T1[:], ALU.add, [bS, b_T1], [bS])
                    ldq(0); ldq(1)
                    dma("sp", XW[:], Pscr[1536 + 64 * d:1600 + 64 * d, c0:c0 + NT2 + 2], [b_P], [b_XW])
                    dma("act", XA[:], Pscr[1664 + 64 * d:1728 + 64 * d, c0:c0 + NT2 + 2], [b_P], [b_XA])
                    shq(0); ldq(2); shq(1); shq(2)
                    for (X_, bX, o_, bo, j) in ((XW, b_XW, xwp, b_xwp, 0), (XA, b_XA, xap, b_xap, 1)):
                        tt("dve", tw[:], X_[:, 0:NT2], X_[:, 2:NT2 + 2], ALU.add, [bX], [b_tw])
                        ts("dve", tw[:], tw[:], hmw[:, j:j + 1], None, ALU.mult, None, [b_tw, b_hmw], [b_tw])
                        stt(o_[:], X_[:, 1:NT2 + 1], omw[:, j:j + 1], tw[:], ALU.mult, ALU.add, [bX, b_hmw, b_tw], [bo])
                    act(xwp[:], xwp[:], AF.Tanh, [b_xwp], [b_xwp])
                    for hh in range(2):
                        pa, bpa = nextps()
                        for j in range(4):
                            h = 4 * hh + j
                            mm(pa[0:64, j * NT2:(j + 1) * NT2], w2d[:, h * 64:(h + 1) * 64], xwp[:], True, True,
                               [b_w2d, b_xwp], [bpa])
                        tt("dve", XB[:, 4 * hh:4 * hh + 4, :], pa[0:64, 0:4 * NT2].rearrange("p (h n) -> p h n", n=NT2),
                           bc(w0d, 4 * hh, 4 * hh + 4, NT2), ALU.add, [bpa, b_w0d], [b_XB])
                    act(XB[:], XB[:], AF.Exp, [b_XB], [b_XB], scale=-1.0)
                    act(XB[:], XB[:], AF.Ln, [b_XB], [b_XB], bias=1.0)
                    act(E2[:], XB[:], AF.Exp, [b_XB], [b_E2], bias=-0.5, scale=-1.0)
                    for hh in range(2):
                        pa, bpa = nextps()
                        for j in range(4):
                            h = 4 * hh + j
                            mm(pa[0:64, j * NT2:(j + 1) * NT2], a2d[:, h * 64:(h + 1) * 64], xap[:], True, True,
                               [b_w2d, b_xap], [bpa])
                        tt("dve", AD[:, 4 * hh:4 * hh + 4, :], pa[0:64, 0:4 * NT2].rearrange("p (h n) -> p h n", n=NT2),
                           bc(a0d, 4 * hh, 4 * hh + 4, NT2), ALU.add, [bpa, b_a0d], [b_AD])
                    act(AD[:], AD[:], AF.Sigmoid, [b_AD], [b_AD])
                    tt("pool", KR[:], Kp[:], bc(kk_, 0, 8, NT2), ALU.mult, [b_Kp, b_kk_], [b_KR])
                    tt("pool", T1[:], KR[:], KR[:], ALU.mult, [b_KR], [b_T1])
                    for hh in range(2):
                        pa, bpa = nextps()
                        for j in range(4):
                            h = 4 * hh + j
                            mm(pa[0:64, j * NT2:(j + 1) * NT2], ones64[:], T1[:, h, :], True, True, [b_c, b_T1], [bpa])
                        ts("dve", SS[:, 4 * hh:4 * hh + 4, :], pa[0:64, 0:4 * NT2].rearrange("p (h n) -> p h n", n=NT2),
                           1e-24, None, ALU.max, None, [bpa], [b_SS])
                    act(SS[:], SS[:], AF.Sqrt, [b_SS], [b_SS])
                    mk.op("dve", lambda E: E.reciprocal(out=SS[:], in_=SS[:]), [b_SS], [b_SS])
                    tt("pool", KR[:], KR[:], SS[:], ALU.mult, [b_KR, b_SS], [b_KR])
                    tt("pool", T2[:], AD[:], bc(ka_, 0, 8, NT2), ALU.mult, [b_AD, b_ka_], [b_T2])
                    tt("pool", T2[:], T2[:], bc(omka, 0, 8, NT2), ALU.add, [b_T2, b_omka], [b_T2])
                    tt("pool", KD[:], T2[:], Kp[:], ALU.mult, [b_T2, b_Kp], [b_KD])
                    tt("dve", AB[:], AD[:], KR[:], ALU.mult, [b_AD, b_KR], [b_AB])
                    tt("pool", T1[:], Rp[:], KD[:], ALU.mult, [b_Rp, b_KD], [b_T1])
                    tt("pool", T1[:], T1[:], bc(rk_, 0, 8, NT2), ALU.mult, [b_T1, b_rk_], [b_T1])
                    for hh in range(2):
                        pa, bpa = nextps()
                        for j in range(4):
                            h = 4 * hh + j
                            mm(pa[0:64, j * NT2:(j + 1) * NT2], ones64[:], T1[:, h, :], True, True, [b_c, b_T1], [bpa])
                        tt("dve", BON[:, 4 * hh:4 * hh + 4, :], pa[0:64, 0:4 * NT2].rearrange("p (h n) -> p h n", n=NT2),
                           Vp[:, 4 * hh:4 * hh + 4, :], ALU.mult, [bpa, b_Vp], [b_BON])
                    dma("sp", BD[d, :, :, tg:tg + NT2], BON[:], [b_BON], [b_BD])
                    E2f = E2[:].rearrange("p h t -> p (h t)"); Gf = G[:].rearrange("p h t -> p (h t)"); MSf = MS[:]
                    if rev:
                        E2f = E2f[:, ::-1]; Gf = Gf[:, ::-1]; MSf = MSf[:, ::-1]
                    mk.op("dve", lambda E, Gf=Gf, MSf=MSf, E2f=E2f: E.tensor_tensor_scan(
                        out=Gf, data0=MSf, data1=E2f, initial=0.0, op0=ALU.mult, op1=ALU.add), [b_E2, b_c], [b_G])
                    tt("pool", D1[:], G[:], E2[:], ALU.subtract, [b_G, b_E2], [b_D1])
                    Gv = G[:].rearrange("p h (c l) -> p (h c) l", l=64)
                    ti_ = 0 if rev else 63
                    totb = Gv[:, :, ti_:ti_ + 1].broadcast_to([64, 16, 64])
                    tt("pool", D2[:].rearrange("p h (c l) -> p (h c) l", l=64), Gv, totb, ALU.subtract, [b_G], [b_D2])
                    act(EP[:], G[:], AF.Exp, [b_G], [b_EP])
                    act(EN[:], G[:], AF.Exp, [b_G], [b_EN], scale=-1.0)
                    act(D1[:], D1[:], AF.Exp, [b_D1], [b_D1], scale=-1.0)
                    act(D2[:], D2[:], AF.Exp, [b_D2], [b_D2])
                    act(GL[:].rearrange("p (a o) -> p a o", o=1), Gv[:, :, ti_:ti_ + 1], AF.Exp, [b_G], [b_GL], scale=-1.0)
                    stt(AR[:, :, :, 0:64], c4(KR), -1.0, c4(D1), ALU.mult, ALU.mult, [b_KR, b_D1], [b_AR])
                    tt("pool", AR[:, :, :, 64:128], c4(Rp), c4(EN), ALU.mult, [b_Rp, b_EN], [b_AR])
                    tt("pool", KT[:], KD[:], EP[:], ALU.mult, [b_KD, b_EP], [b_KT])
                    tt("dve", BT[:], AB[:], EP[:], ALU.mult, [b_AB, b_EP], [b_BT])
                    tt("pool", KH[:], KD[:], D2[:], ALU.mult, [b_KD, b_D2], [b_KH])
                    tt("dve", BH[:], AB[:], D2[:], ALU.mult, [b_AB, b_D2], [b_BH])

                def chunk(ti, pend, AR, b_AR, KT, b_KT, BT, b_BT, KH, b_KH, BH, b_BH, Vp, b_Vp, GL, b_GL):
                    ARb, b_ARb = AR, b_AR
                    tg = toff + ti * NT2

                    def tt(*a):
                        tt0(*a)
                        mk.replay(pend, 2)

                    def cp(*a):
                        cp0(*a)
                        mk.replay(pend, 2)
                    CS = [slice(0, 64), slice(64, 128)]
                    p1, bp1 = nextps()
                    for c in range(2):
                        for h in range(8):
                            mm(p1[0:64, (c * 8 + h) * 128:(c * 8 + h + 1) * 128], BT[:, h, CS[c]], ARb[:, h, c, :], True, True,
                               [b_BT, b_ARb], [bp1])
                    tt("dve", f16(MT1), pv(p1, 0, 128), mA16, ALU.mult, [bp1, b_c], [b_MT1])
                    p2, bp2 = nextps()
                    for c in range(2):
                        for h in range(8):
                            mm(p2[0:64, (c * 8 + h) * 128:(c * 8 + h + 1) * 128], KT[:, h, CS[c]], ARb[:, h, c, :], True, True,
                               [b_KT, b_ARb], [bp2])
                    tt("dve", f16(MT2), pv(p2, 0, 128), mA16, ALU.mult, [bp2, b_c], [b_MT2])
                    p3, bp3 = nextps()
                    for c in range(2):
                        for h in range(8):
                            mm(p3[0:64, (c * 8 + h) * 64:(c * 8 + h + 1) * 64], ARb[:, h, c, 0:64], BT[:, h, CS[c]], True, True,
                               [b_ARb, b_BT], [bp3])
                    tt("dve", f16(P0), pv(p3, 0, 64), mL16, ALU.mult, [bp3, b_c], [b_P0])
                    Z0 = Zt[0]; bZ0 = b_Zt[0]
                    p4, bp4 = nextps()
                    for c in range(2):
                        for h in range(8):
                            mk.op("pe", lambda E, p4=p4, o=(c * 8 + h) * 64, a=AR[:, h, c, 0:64]: E.transpose(
                                out=p4[0:64, o:o + 64], in_=a, identity=id64), [b_AR, b_ident], [bp4])
                            mk.op("pe", lambda E, p4=p4, o=1024 + (c * 8 + h) * 64, a=Vp[:, h, CS[c]]: E.transpose(
                                out=p4[0:64, o:o + 64], in_=a, identity=id64), [b_Vp, b_ident], [bp4])
                    cp0("act", Z0[:, :, :, 0:64].rearrange("p c h n -> p (c h) n"), pv(p4, 0, 64), [bp4], [bZ0])
                    cp("act", f16(VT), pv(p4, 1024, 64), [bp4], [b_VT])
                    p5, bp5 = nextps()
                    for c in range(2):
                        for h in range(8):
                            mk.op("pe", lambda E, p5=p5, o=(c * 8 + h) * 64, a=BH[:, h, CS[c]]: E.transpose(
                                out=p5[0:64, o:o + 64], in_=a, identity=id64), [b_BH, b_ident], [bp5])
                            mk.op("pe", lambda E, p5=p5, o=1024 + (c * 8 + h) * 64, a=KH[:, h, CS[c]]: E.transpose(
                                out=p5[0:64, o:o + 64], in_=a, identity=id64), [b_KH, b_ident], [bp5])
                    cp0("dve", f16(BHt), pv(p5, 0, 64), [bp5], [b_BHt])
                    cp("dve", f16(KHt), pv(p5, 1024, 64), [bp5], [b_KHt])
                    p6, bp6 = nextps()
                    for c in range(2):
                        for h in range(8):
                            mm(p6[0:64, (c * 8 + h) * 64:(c * 8 + h + 1) * 64], MT2[:, c, h, 0:64], VT[:, c, h, :], True, True,
                               [b_MT2, b_VT], [bp6])
                    cp("act", Z0[:, :, :, 64:128].rearrange("p c h n -> p (c h) n"), pv(p6, 0, 64), [bp6], [bZ0])
                    zi = 0
                    Pv = lambda c, h: P0[:, c, h, :]
                    PTv = lambda c, h: MT1[:, c, h, 0:64]
                    bP = b_P0; bPT = b_MT1
                    for it in range(6):
                        Zc = Zt[zi]; bZc = b_Zt[zi]; Zn = Zt[1 - zi]; bZn = b_Zt[1 - zi]
                        p7, bp7 = nextps()
                        for c in range(2):
                            for h in range(8):
                                mm(p7[0:64, (c * 8 + h) * 128:(c * 8 + h + 1) * 128], PTv(c, h), Zc[:, c, h, :], True, True,
                                   [bPT, bZc], [bp7])
                        tt("dve", f16(Zn), pv(p7, 0, 128), f16(Zc), ALU.add, [bp7, bZc], [bZn])
                        zi = 1 - zi
                        if it < 5:
                            nx = it % 2
                            p8, bp8 = nextps()
                            for c in range(2):
                                for h in range(8):
                                    mm(p8[0:64, (c * 8 + h) * 64:(c * 8 + h + 1) * 64], PTv(c, h), Pv(c, h), True, True,
                                       [bPT, bP], [bp8])
                                    mm(p8[0:64, 1024 + (c * 8 + h) * 64:1024 + (c * 8 + h + 1) * 64], Pv(c, h), PTv(c, h), True, True,
                                       [bPT, bP], [bp8])
                            cp("act", PP[nx][:].rearrange("p k a n -> p (k a) n"),
                               p8[0:64, 0:2048].rearrange("p (a n) -> p a n", n=64), [bp8], [b_PP[nx]])
                            Pv = lambda c, h, nx=nx: PP[nx][:, 0, c * 8 + h, :]
                            PTv = lambda c, h, nx=nx: PP[nx][:, 1, c * 8 + h, :]
                            bP = b_PP[nx]; bPT = b_PP[nx]
                    Zf = Zt[zi]; bZf = b_Zt[zi]
                    p9, bp9 = nextps()
                    for c in range(2):
                        for h in range(8):
                            mm(p9[0:64, (c * 8 + h) * 64:(c * 8 + h + 1) * 64], Zf[:, c, h, 0:64], MT1[:, c, h, 64:128], True, True,
                               [bZf, b_MT1], [bp9])
                            mm(p9[0:64, 1024 + (c * 8 + h) * 64:1024 + (c * 8 + h + 1) * 64], Zf[:, c, h, 0:64], BHt[:, c, h, :], True, True,
                               [bZf, b_BHt], [bp9])
                    tt0("dve", QT[:], p9[0:64, 0:1024].rearrange("p (c h n) -> p c h n", c=2, h=8),
                       AR[:, :, :, 64:128].rearrange("p h c n -> p c h n"), ALU.add, [bp9, b_AR], [b_QT])
                    GLc = GL[:].rearrange("p (h c) -> p c h", c=2)[:, :, :, None].broadcast_to([64, 2, 8, 64])
                    tt0("pool", DG[:], ident[0:64, None, None, 0:64].broadcast_to([64, 2, 8, 64]), GLc, ALU.mult,
                       [b_ident, b_GL], [b_DG])
                    tt("dve", f16(MM), pv(p9, 1024, 64), f16(DG), ALU.add, [bp9, b_DG], [b_MM])
                    for c in (range(1, -1, -1) if rev else range(2)):
                        sti = sti_[0]
                        ST = STt[sti]; bST = b_ST[sti]; STn = STt[1 - sti]; bSTn = b_ST[1 - sti]
                        p11, bp11 = nextps()
                        for h in range(8):
                            o_ = p11[0:64, h * 64:(h + 1) * 64]
                            mm(o_, ST[:, h, :], QT[:, c, h, :], True, False, [bST, b_QT], [bp11])
                            mm(o_, Zf[:, c, h, 64:128], MT1[:, c, h, 64:128], False, False, [bZf, b_MT1], [bp11])
                            mm(o_, VT[:, c, h, :], MT2[:, c, h, 64:128], False, True, [b_VT, b_MT2], [bp11])
                        cp("act", YT[:, :, CS[c]], v3(p11, 64), [bp11], [b_YT])
                        p12, bp12 = nextps()
                        for h in range(8):
                            o_ = p12[0:64, h * 64:(h + 1) * 64]
                            mm(o_, MM[:, c, h, :], ST[:, h, :], True, False, [b_MM, bST], [bp12])
                            mm(o_, BHt[:, c, h, :], Zf[:, c, h, 64:128], False, False, [b_BHt, bZf], [bp12])
                            mm(o_, KHt[:, c, h, :], VT[:, c, h, :], False, True, [b_KHt, b_VT], [bp12])
                        cp("dve", STn[:], v3(p12, 64), [bp12], [bSTn])
                        sti_[0] = 1 - sti
                    dma("sp", YD[d, :, :, tg:tg + NT2], YT[:], [b_YT], [b_YD])

                PIPE = os.environ.get('RW_NOPIPE') != '1'
                if PIPE:
                    prep(order[0], *SETS[0])
                for idx, ti in enumerate(order):
                    pend = []
                    if not PIPE:
                        prep(ti, *SETS[idx % 2])
                    elif idx + 1 < len(order):
                        mk.deferred = pend
                        prep(order[idx + 1], *SETS[(idx + 1) % 2])
                        mk.deferred = None
                    if os.environ.get('RW_PIPE_MODE') == 'start':
                        mk.replay(pend, len(pend))
                    chunk(ti, pend, *SETS[idx % 2])
                    mk.replay(pend, len(pend))
            mk.flush(final=True)

    if upto >= 1:
        rwkv_pass(0)
        rwkv_pass(1)

    def s5_pass():
        es, sb, ps = scope("s5_")
        with es:
            TWO_PI = 2.0 * math.pi
            pz = [ps("pz%d" % i, [128, 512]) for i in range(8)]
            b_pz = [Buf() for _ in range(8)]
            pctr = [0]

            def nextps():
                i = pctr[0] % 8
                pctr[0] += 1
                return pz[i], b_pz[i]

            NLV = 10
            identb = sb("identb", [64, 64], BF16)
            dsk = sb("dsk", [16, 32])
            SQr = sb("SQr", [64, NLV, 64]); SQi = sb("SQi", [64, NLV, 64]); SQin = sb("SQin", [64, NLV, 64])
            LTr = sb("LTr", [64, 8, 64, 16], BF16); LTi = sb("LTi", [64, 8, 64, 16], BF16)
            OTr = sb("OTr", [64, 8, 64, 16], BF16); OTn = sb("OTn", [64, 8, 64, 16], BF16)
            CRb = sb("CRb", [64, 64, 16], BF16); CInb = sb("CInb", [64, 64, 16], BF16)
            bt_ = Buf()
            es2, sb2, ps2_ = scope("s5t_")
            ones1 = sb2("ones1", [1, 64]); row = sb2("row", [1, 64])
            mk.op("pool", lambda E: E.memset(ones1[:], 1.0), (), [bt_])
            dma("sp", row[:], log_dt.rearrange("d g -> (d g)").rearrange("(o n) -> o n", o=1), (), [bt_])
            cp("dve", identb[:], ident[0:64, 0:64], [b_ident], [bt_])
            LR = sb2("LR", [64, 64]); LI = sb2("LI", [64, 64])
            dma("sp", LR[:].rearrange("p (d g) -> p d g", d=2), lam_re.rearrange("d g p -> p d g"), (), [bt_], slow=True)
            dma("act", LI[:].rearrange("p (d g) -> p d g", d=2), lam_im.rearrange("d g p -> p d g"), (), [bt_], slow=True)
            BR = sb2("BR", [64, 64, 16]); BI = sb2("BI", [64, 64, 16])
            dma("sp", BR[:].rearrange("p (d g) h -> p d g h", d=2), b_re.rearrange("d g p h -> p d g h"), (), [bt_])
            dma("act", BI[:].rearrange("p (d g) h -> p d g h", d=2), b_im.rearrange("d g p h -> p d g h"), (), [bt_])
            CR = sb2("CR", [64, 64, 16]); CI = sb2("CI", [64, 64, 16])
            cnat = sb2("cnat", [128, 8, 64])
            for (src, dst) in ((c_re, CR), (c_im, CI)):
                dma("sp", cnat[:], src.rearrange("d g h p -> (d g h) p").rearrange("(k q) p -> q k p", q=128), [bt_], [bt_])
                for k in range(8):
                    pq, bq = nextps()
                    mk.op("pe", lambda E, pq=pq, k=k: E.transpose(out=pq[0:64, 0:128], in_=cnat[:, k, :], identity=ident[:]),
                          [bt_, b_ident], [bq])
                    cp("dve", dst[:, k * 8:(k + 1) * 8, :], pq[0:64, 0:128].rearrange("p (g h) -> p g h", h=16), [bq], [bt_])
            dma("sp", dsk[:], d_skip.rearrange("(g h) -> h g", h=16), (), [bt_], slow=True)
            DT = sb2("DT", [64, 64])
            pq, bq = nextps()
            mm(pq[0:64, 0:64], ones1[:], row[:], True, True, [bt_], [bq])
            act(DT[:], pq[0:64, 0:64], AF.Exp, [bq], [bt_])

            def T64(name):
                return sb2(name, [64, 64])
            ZR = T64("ZR"); ZI = T64("ZI"); EPs = T64("EPs"); COS = T64("COS"); SIN = T64("SIN")
            tA = T64("tA"); tB = T64("tB"); tC = T64("tC"); tI = sb2("tI", [64, 64], mybir.dt.int32)
            tt("dve", ZR[:], LR[:], DT[:], ALU.mult, [bt_], [bt_])
            tt("dve", ZI[:], LI[:], DT[:], ALU.mult, [bt_], [bt_])
            act(EPs[:], ZR[:], AF.Exp, [bt_], [bt_])
            for (dst, offs) in ((SIN, 64.0), (COS, 64.25)):
                ts("dve", tA[:], ZI[:], 1.0 / TWO_PI, offs, ALU.mult, ALU.add, [bt_], [bt_])
                cp("dve", tI[:], tA[:], [bt_], [bt_])
                cp("dve", tB[:], tI[:], [bt_], [bt_])
                tt("dve", tA[:], tA[:], tB[:], ALU.subtract, [bt_], [bt_])
                ts("dve", tB[:], tA[:], 0.5, None, ALU.is_gt, None, [bt_], [bt_])
                tt("dve", tA[:], tA[:], tB[:], ALU.subtract, [bt_], [bt_])
                act(dst[:], tA[:], AF.Sin, [bt_], [bt_], scale=TWO_PI)
            PWr = sb2("PWr", [64, 9, 64]); PWi = sb2("PWi", [64, 9, 64])
            mk.op("pool", lambda E: E.memset(PWr[:, 0, :], 1.0), (), [bt_])
            mk.op("pool", lambda E: E.memset(PWi[:, 0, :], 0.0), (), [bt_])
            tt("dve", PWr[:, 1, :], EPs[:], COS[:], ALU.mult, [bt_], [bt_])
            tt("dve", PWi[:, 1, :], EPs[:], SIN[:], ALU.mult, [bt_], [bt_])

            def cmul(or_, oi_, ar, ai, br, bi, n3=None):
                tt("dve", tA[:], ai, bi, ALU.mult, [bt_], [bt_])
                tt("dve", tB[:], ai, br, ALU.mult, [bt_], [bt_])
                tt("dve", tC[:], ar, br, ALU.mult, [bt_], [bt_])
                tt("dve", or_, tC[:], tA[:], ALU.subtract, [bt_], [bt_])
                tt("dve", tC[:], ar, bi, ALU.mult, [bt_], [bt_])
                tt("dve", oi_, tC[:], tB[:], ALU.add, [bt_], [bt_])
            for j in range(2, 9):
                cmul(PWr[:, j, :], PWi[:, j, :], PWr[:, j - 1, :], PWi[:, j - 1, :], PWr[:, 1, :], PWi[:, 1, :])
            NLV = 10
            cp("dve", SQr[:, 0, :], PWr[:, 8, :], [bt_], [bt_]); cp("dve", SQi[:, 0, :], PWi[:, 8, :], [bt_], [bt_])
            for k in range(1, NLV):
                cmul(SQr[:, k, :], SQi[:, k, :], SQr[:, k - 1, :], SQi[:, k - 1, :], SQr[:, k - 1, :], SQi[:, k - 1, :])
            ts("dve", SQin[:], SQi[:], -1.0, None, ALU.mult, None, [bt_], [bt_])
            CFr = T64("CFr"); CFi = T64("CFi"); DEN = T64("DEN"); NR = T64("NR")
            ts("dve", NR[:], PWr[:, 1, :], -1.0, None, ALU.add, None, [bt_], [bt_])
            tt("dve", tA[:], LR[:], LR[:], ALU.mult, [bt_], [bt_])
            tt("dve", tB[:], LI[:], LI[:], ALU.mult, [bt_], [bt_])
            tt("dve", DEN[:], tA[:], tB[:], ALU.add, [bt_], [bt_])
            mk.op("dve", lambda E: E.reciprocal(out=DEN[:], in_=DEN[:]), [bt_], [bt_])
            tt("dve", tA[:], NR[:], LR[:], ALU.mult, [bt_], [bt_])
            tt("dve", tB[:], PWi[:, 1, :], LI[:], ALU.mult, [bt_], [bt_])
            tt("dve", tA[:], tA[:], tB[:], ALU.add, [bt_], [bt_])
            tt("dve", CFr[:], tA[:], DEN[:], ALU.mult, [bt_], [bt_])
            tt("dve", tA[:], PWi[:, 1, :], LR[:], ALU.mult, [bt_], [bt_])
            tt("dve", tB[:], NR[:], LI[:], ALU.mult, [bt_], [bt_])
            tt("dve", tA[:], tA[:], tB[:], ALU.subtract, [bt_], [bt_])
            tt("dve", CFi[:], tA[:], DEN[:], ALU.mult, [bt_], [bt_])
            BbR = sb2("BbR", [64, 64, 16]); BbI = sb2("BbI", [64, 64, 16])
            X1t = sb2("X1t", [64, 64, 16]); X2t = sb2("X2t", [64, 64, 16])

            def b16(t2):
                return t2[:, :, None].broadcast_to([64, 64, 16])

            def cmul3(or_, oi_neg, ar2, ai2, br3, bi3, e1="dve", e2="pool"):
                tt(e1, X1t[:], br3, b16(ar2), ALU.mult, [bt_], [bt_])
                tt(e1, X2t[:], bi3, b16(ai2), ALU.mult, [bt_], [bt_])
                tt(e1, or_, X1t[:], X2t[:], ALU.subtract, [bt_], [bt_])
                tt(e1, X1t[:], bi3, b16(ar2), ALU.mult, [bt_], [bt_])
                tt(e1, X2t[:], br3, b16(ai2), ALU.mult, [bt_], [bt_])
                if oi_neg[1]:
                    tt(e1, X1t[:], X1t[:], X2t[:], ALU.add, [bt_], [bt_])
                    ts(e1, oi_neg[0], X1t[:], -1.0, None, ALU.mult, None, [bt_], [bt_])
                else:
                    tt(e1, oi_neg[0], X1t[:], X2t[:], ALU.add, [bt_], [bt_])
            cmul3(BbR[:], (BbI[:], False), CFr[:], CFi[:], BR[:], BI[:])
            for j in range(8):
                cmul3(LTr[:, j], (LTi[:, j], False), PWr[:, j, :], PWi[:, j, :], BbR[:], BbI[:])
                cmul3(OTr[:, j], (OTn[:, j], True), PWr[:, j + 1, :], PWi[:, j + 1, :], CR[:], CI[:])
            cp("dve", CRb[:], CR[:], [bt_], [bt_]); ts("dve", CInb[:], CI[:], -1.0, None, ALU.mult, None, [bt_], [bt_])

            mk.flush(final=True)
            es2.close()
            KTg = [sb("KTg%d" % i, [16, 15, 16], BF16) for i in range(2)]; b_KTg = [Buf(), Buf()]
            CTg = [sb("CTg%d" % i, [16, 32, 64], BF16) for i in range(2)]; b_CTg = [Buf(), Buf()]
            UGN = 2048
            ug = sb("ug", [16, UGN]); b_ug = Buf()
            ub = [sb("ub%d" % i, [16, 8192], BF16) for i in range(2)]; b_ub = [Buf(), Buf()]
            yg = sb("yg", [16, 4096]); b_yg = Buf()
            NBM = 1024
            Wt = [[[sb("W%d%d%d" % (pp, d, c), [64, NBM + 1]) for c in range(2)] for d in range(2)] for pp in range(2)]
            b_Wt = [[Buf() for d in range(2)] for pp in range(2)]
            Sb = [[[sb("Sb%d%d%d" % (i, d, c), [64, NBM + 1], BF16) for c in range(2)] for d in range(2)] for i in range(2)]
            b_Sb = [Buf(), Buf()]
            ptmp = [sb("ptmp%d" % i, [64, NBM + 1]) for i in range(2)]; b_ptmp = Buf()
            for pp in range(2):
                for d in range(2):
                    for c in range(2):
                        mk.op("pool", lambda E, t=Wt[pp][d][c]: E.memset(t[:], 0.0), (), [b_Wt[pp][d]])
            id16 = ident[0:16, 0:16]

            def gconsts(g):
                par = g % 2
                pk, bpk = nextps()
                for idx in range(15):
                    if idx == 0:
                        terms = [(0, 0), (1, 0)]
                    elif idx < 8:
                        terms = [(0, idx)]
                    else:
                        terms = [(1, idx - 7)]
                    n = 0
                    for (d, tau) in terms:
                        gi = d * 32 + g
                        mm(pk[0:16, idx * 16:(idx + 1) * 16], LTr[:, tau, gi, :], CRb[:, gi, :], n == 0, False, [bt_], [bpk]); n += 1
                        mm(pk[0:16, idx * 16:(idx + 1) * 16], LTi[:, tau, gi, :], CInb[:, gi, :], False, n == 2 * len(terms) - 1, [bt_], [bpk]); n += 1
                cp("dve", KTg[par][:].rearrange("p a b -> p (a b)"), pk[0:16, 0:240], [bpk], [b_KTg[par]])
                stt(KTg[par][:, 0, :], id16, dsk[:, g:g + 1], KTg[par][:, 0, :], ALU.mult, ALU.add, [b_KTg[par], bt_, b_ident], [b_KTg[par]])
                for q in range(4):
                    pc_, bpc = nextps()
                    for j in range(8):
                        i = q * 8 + j
                        d = i // 16; s_ = (i // 2) % 8; c = i % 2
                        e_ = (7 - s_) if d == 0 else s_
                        src = (LTr if c == 0 else LTi)[:, e_, d * 32 + g, :]
                        mm(pc_[0:16, j * 64:(j + 1) * 64], src, identb[:], True, True, [bt_], [bpc])
                    cp("act", CTg[par][:, q * 8:(q + 1) * 8, :].rearrange("p a b -> p (a b)"),
                       pc_[0:16, 0:512], [bpc], [b_CTg[par]])

            def dims(s):
                toff, coff, T = SEQ[s]
                nblk = T // 8
                BW = min(512, nblk)
                return toff, coff, T, nblk, BW, 8 * BW, nblk // BW

            def front(g, s, st):
                par = g % 2
                toff, coff, T, nblk, BW, TW, nbt = dims(s)
                nlv = int(math.log2(nblk))
                UG = min(UGN, T)
                for hf in range(T // UG):
                    dma("sp", ug[:, 0:UG], Pscr[1920 + 16 * g:1936 + 16 * g, coff + 1 + hf * UG:coff + 1 + (hf + 1) * UG],
                        [b_P], [b_ug])
                    cp("act", ub[st][:, hf * UG:(hf + 1) * UG], ug[:, 0:UG], [b_ug], [b_ub[st]])
                if nblk < NBM:
                    for d in range(2):
                        for c in range(2):
                            mk.op("pool", lambda E, t=Wt[0][d][c]: E.memset(t[:], 0.0), (), [b_Wt[0][d]])
                            mk.op("pool", lambda E, t=Wt[1][d][c]: E.memset(t[:], 0.0), (), [b_Wt[1][d]])
                for bt in range(nbt):
                    for d in range(2):
                        for c in range(2):
                            pw_, bpw = nextps()
                            for s_ in range(8):
                                mm(pw_[0:64, 0:BW], CTg[par][:, (d * 8 + s_) * 2 + c, :],
                                   ub[st][:, bt * TW:(bt + 1) * TW].rearrange("p (b s) -> p s b", s=8)[:, s_, :],
                                   s_ == 0, s_ == 7, [b_CTg[par], b_ub[st]], [bpw])
                            o0 = bt * BW + (1 if d == 0 else 0)
                            cp("act", Wt[0][d][c][:, o0:o0 + BW], pw_[0:64, 0:BW], [bpw], [b_Wt[0][d]])
                for d in range(2):
                    gi = d * 32 + g
                    cur = 0
                    lo = 1 if d == 0 else 0
                    for k in range(nlv):
                        sh = 1 << k
                        Wc = Wt[cur][d]; Wn = Wt[1 - cur][d]
                        bWc = b_Wt[cur][d]; bWn = b_Wt[1 - cur][d]
                        n_ = nblk - sh
                        if d == 0:
                            dst = slice(lo + sh, lo + nblk); srcs = slice(lo, lo + n_); keep = slice(lo, lo + sh)
                        else:
                            dst = slice(lo, lo + n_); srcs = slice(lo + sh, lo + nblk); keep = slice(lo + n_, lo + nblk)
                        ar = SQr[:, k, gi:gi + 1]; ai = SQi[:, k, gi:gi + 1]; ain = SQin[:, k, gi:gi + 1]
                        if d == 0:
                            stt(Wn[0][:, dst], Wc[0][:, srcs], ar, Wc[0][:, dst], ALU.mult, ALU.add, [bWc, bt_], [bWn])
                            stt(Wn[0][:, dst], Wc[1][:, srcs], ain, Wn[0][:, dst], ALU.mult, ALU.add, [bWc, bWn, bt_], [bWn])
                            stt(Wn[1][:, dst], Wc[1][:, srcs], ar, Wc[1][:, dst], ALU.mult, ALU.add, [bWc, bt_], [bWn])
                            stt(Wn[1][:, dst], Wc[0][:, srcs], ai, Wn[1][:, dst], ALU.mult, ALU.add, [bWc, bWn, bt_], [bWn])
                        else:
                            n2 = n_
                            for (o_, a_, b__, sa, sb_) in ((0, 0, 1, ar, ain), (1, 1, 0, ar, ai)):
                                ts("pool", ptmp[0][:, 0:n2], Wc[a_][:, srcs], sa, 0.0, ALU.mult, ALU.add, [bWc, bt_], [b_ptmp])
                                ts("pool", ptmp[1][:, 0:n2], Wc[b__][:, srcs], sb_, 0.0, ALU.mult, ALU.add, [bWc, bt_], [b_ptmp])
                                tt("pool", ptmp[0][:, 0:n2], ptmp[0][:, 0:n2], ptmp[1][:, 0:n2], ALU.add, [b_ptmp], [b_ptmp])
                                tt("pool", Wn[o_][:, dst], Wc[o_][:, dst], ptmp[0][:, 0:n2], ALU.add, [bWc, b_ptmp], [bWn])
                        cp("act", Wn[0][:, keep], Wc[0][:, keep], [bWc], [bWn])
                        cp("act", Wn[1][:, keep], Wc[1][:, keep], [bWc], [bWn])
                        cur = 1 - cur
                    for c in range(2):
                        if d == 0:
                            cp("act", Sb[st][d][c][:, 1:nblk + 1], Wt[cur][d][c][:, 1:nblk + 1], [b_Wt[cur][d]], [b_Sb[st]])
                            mk.op("pool", lambda E, t=Sb[st][d][c]: E.memset(t[:, 0:1], 0.0), (), [b_Sb[st]])
                        else:
                            cp("act", Sb[st][d][c][:, 0:nblk], Wt[cur][d][c][:, 0:nblk], [b_Wt[cur][d]], [b_Sb[st]])
                            mk.op("pool", lambda E, t=Sb[st][d][c], nblk=nblk: E.memset(t[:, nblk:nblk + 1], 0.0), (), [b_Sb[st]])
                    if cur != 0:
                        pass

            def back(g, s, st):
                par = g % 2
                toff, coff, T, nblk, BW, TW, nbt = dims(s)
                for bt in range(nbt):
                    ubv = ub[st][:, bt * TW:(bt + 1) * TW].rearrange("p (b s) -> p s b", s=8)
                    ygv = yg[:, 0:TW].rearrange("p (b s) -> p s b", s=8)
                    for t in range(8):
                        py, bpy = nextps()
                        for s_ in range(8):
                            idx = 0 if s_ == t else ((t - s_) if s_ < t else (7 + s_ - t))
                            mm(py[0:16, 0:BW], KTg[par][:, idx, :], ubv[:, s_, :], s_ == 0, False, [b_KTg[par], b_ub[st]], [bpy])
                        b0 = bt * BW
                        mm(py[0:16, 0:BW], OTr[:, t, g, :], Sb[st][0][0][:, b0:b0 + BW], False, False, [bt_, b_Sb[st]], [bpy])
                        mm(py[0:16, 0:BW], OTn[:, t, g, :], Sb[st][0][1][:, b0:b0 + BW], False, False, [bt_, b_Sb[st]], [bpy])
                        mm(py[0:16, 0:BW], OTr[:, 7 - t, 32 + g, :], Sb[st][1][0][:, b0 + 1:b0 + 1 + BW], False, False, [bt_, b_Sb[st]], [bpy])
                        mm(py[0:16, 0:BW], OTn[:, 7 - t, 32 + g, :], Sb[st][1][1][:, b0 + 1:b0 + 1 + BW], False, True, [bt_, b_Sb[st]], [bpy])
                        cp("act", ygv[:, t, :], py[0:16, 0:BW], [bpy], [b_yg])
                    dma("sp", YS[16 * g:16 * g + 16, toff + bt * TW:toff + (bt + 1) * TW], yg[:, 0:TW], [b_yg], [b_YS])

            units = [(g, s) for g in range(32) for s in range(NS)]
            prev = None
            for ui, (g, s) in enumerate(units):
                if s == 0:
                    gconsts(g)
                front(g, s, ui % 2)
                if prev is not None:
                    back(prev[0], prev[1], (ui - 1) % 2)
                prev = (g, s)
            back(prev[0], prev[1], (len(units) - 1) % 2)
            mk.flush(final=True)

    if upto >= 2:
        s5_pass()

    def mix_pass():
        es, sb, ps = scope("mx_")
        with es:
            N = 256
            pz = [ps("pz%d" % i, [128, 512]) for i in range(8)]
            b_pz = [Buf() for _ in range(8)]
            pctr = [0]

            def nextps():
                i = pctr[0] % 8
                pctr[0] += 1
                return pz[i], b_pz[i]
            bc_ = Buf()
            o64 = sb("o64", [64, 64])
            mk.op("pool", lambda E: E.memset(o64[:], 1.0 / 64.0), (), [bc_])
            ones_bf = sb("ones_bf", [128, 128], BF16)
            mk.op("pool", lambda E: E.memset(ones_bf[:], 1.0), (), [bc_])
            lg = sb("lg", [64, 8]); lb = sb("lb", [64, 8])
            dma("sp", lg[:], lnx_g.rearrange("(h p) -> p h", p=64), (), [bc_], slow=True)
            dma("sp", lb[:], lnx_b.rearrange("(h p) -> p h", p=64), (), [bc_], slow=True)
            g2t = sb("g2t", [128, 512]); dma("sp", g2t[:], g2, (), [bc_])
            mug = sb("mug", [128, 1]); hmg = sb("hmg", [128, 1]); omg = sb("omg", [128, 1])
            dma("sp", mug[:], mu_shift[1792:1920].rearrange("(p o) -> p o", o=1), (), [bc_], slow=True)
            ts("dve", hmg[:], mug[:], 0.5, None, ALU.mult, None, [bc_], [bc_])
            ts("dve", omg[:], mug[:], -1.0, 1.0, ALU.mult, ALU.add, [bc_], [bc_])
            bgl = sb("bgl", [128, 4]); s5g = sb("s5g", [128, 4])
            dma("sp", bgl[:], b_glu.rearrange("(q p) -> p q", p=128), (), [bc_], slow=True)
            dma("sp", s5g[:], s5_out_g.rearrange("(q p) -> p q", p=128), (), [bc_], slow=True)
            wst = sb("wst", [128, 4, 128])
            mk.op("pool", lambda E: E.memset(wst[:], 0.0), (), [bc_])
            for g in range(32):
                r0 = (g % 8) * 16
                dma("sp" if g % 2 == 0 else "act", wst[r0:r0 + 16, g // 8, r0:r0 + 16], w_glu[g], [bc_], [bc_])
            Wbd = sb("Wbd", [128, 4, 128], BF16)
            cp("dve", Wbd[:], wst[:], [bc_], [bc_])
            wo_r = sb("wo_r", [64, 8, D], BF16); wo_s = sb("wo_s", [128, 4, D], BF16)
            stg = [sb("stg%d" % i, [128, D]) for i in range(2)]; b_stg = [Buf(), Buf()]
            for h in range(8):
                st_ = stg[h % 2]; bs_ = b_stg[h % 2]
                dma("sp", st_[0:64, :], w_out[h * 64:(h + 1) * 64, :], (), [bs_])
                cp("pool", wo_r[:, h, :], st_[0:64, :], [bs_], [bc_])
            for q in range(4):
                st_ = stg[q % 2]; bs_ = b_stg[q % 2]
                dma("sp", st_[:], w_out[512 + q * 128:512 + (q + 1) * 128, :], (), [bs_])
                cp("pool", wo_s[:, q, :], st_[:], [bs_], [bc_])

            def T4(name, dt=F32):
                return sb(name, [64, 8, N], dt), Buf()
            YF, b_YF = T4("YF"); YB, b_YB = T4("YB"); BF_, b_BF = T4("BF"); BB, b_BB = T4("BB")
            Ym, b_Ym = T4("Ym"); SQt, b_SQt = T4("SQt"); RS, b_RS = T4("RS")
            yr, b_yr = T4("yr", BF16)
            XG = sb("XG", [128, N + 2]); b_XG = Buf(); tw = sb("tw", [128, N]); b_tw = Buf()
            sg = sb("sg", [128, N]); b_sg = Buf()
            S5 = sb("S5", [128, 4, N]); b_S5 = Buf(); Z1 = sb("Z1", [128, 4, N]); b_Z1 = Buf()
            Z2 = sb("Z2", [128, 4, N]); b_Z2 = Buf(); Zb = sb("Zb", [128, 4, N], BF16); b_Zb = Buf()
            r2 = sb("r2", [128, N]); b_r2 = Buf()
            ysb = sb("ysb", [128, 4, N], BF16); b_ysb = Buf()
            xTt = sb("xTt", [128, 8, N]); b_xTt = Buf(); X1t = sb("X1t", [128, 8, N]); b_X1t = Buf()

            def bc(t, n):
                return t[:, :, None].broadcast_to([64, 8, n])

            def fl(t):
                return t[:].rearrange("p h t -> p (h t)")
            for s, (toff, coff, T) in enumerate(SEQ):
                for ti in range(T // N):
                    tl = ti * N; tg = toff + tl; c0 = coff + tl
                    dma("sp", YF[:], YD[0, :, :, tg:tg + N], [b_YD], [b_YF])
                    dma("act", YB[:], YD[1, :, :, tg:tg + N], [b_YD], [b_YB])
                    dma("sp", BF_[:], BD[0, :, :, tg:tg + N], [b_BD], [b_BF])
                    dma("act", BB[:], BD[1, :, :, tg:tg + N], [b_BD], [b_BB])
                    dma("sp", XG[:], Pscr[1792:1920, c0:c0 + N + 2], [b_P], [b_XG])
                    dma("act", S5[:], YS[:, tg:tg + N].rearrange("(q p) t -> p q t", p=128), [b_YS], [b_S5])
                    dma("sp", xTt[:], XT[:, tg:tg + N].rearrange("(k p) t -> p k t", p=128), [b_XT], [b_xTt])
                    tt("pool", Ym[:], YF[:], YB[:], ALU.add, [b_YF, b_YB], [b_Ym])
                    for j in range(4):
                        pm_, bpm = nextps()
                        mm(pm_[0:64, :], o64[:], fl(Ym)[:, j * 512:(j + 1) * 512], True, True, [bc_, b_Ym], [bpm])
                        tt("dve", fl(YF)[:, j * 512:(j + 1) * 512], fl(Ym)[:, j * 512:(j + 1) * 512], pm_[0:64, :],
                           ALU.subtract, [bpm, b_Ym], [b_YF])
                    tt("pool", SQt[:], YF[:], YF[:], ALU.mult, [b_YF], [b_SQt])
                    for j in range(4):
                        pm_, bpm = nextps()
                        mm(pm_[0:64, :], o64[:], fl(SQt)[:, j * 512:(j + 1) * 512], True, True, [bc_, b_SQt], [bpm])
                        act(fl(RS)[:, j * 512:(j + 1) * 512], pm_[0:64, :], AF.Sqrt, [bpm], [b_RS], bias=LNX_EPS)
                    mk.op("dve", lambda E: E.reciprocal(out=RS[:], in_=RS[:]), [b_RS], [b_RS])
                    tt("pool", Ym[:], YF[:], RS[:], ALU.mult, [b_YF, b_RS], [b_Ym])
                    tt("pool", Ym[:], Ym[:], bc(lg, N), ALU.mult, [b_Ym, bc_], [b_Ym])
                    tt("pool", Ym[:], Ym[:], bc(lb, N), ALU.add, [b_Ym, bc_], [b_Ym])
                    tt("pool", Ym[:], Ym[:], BF_[:], ALU.add, [b_Ym, b_BF], [b_Ym])
                    tt("pool", Ym[:], Ym[:], BB[:], ALU.add, [b_Ym, b_BB], [b_Ym])
                    tt("dve", tw[:], XG[:, 0:N], XG[:, 2:N + 2], ALU.add, [b_XG], [b_tw])
                    ts("dve", tw[:], tw[:], hmg[:, 0:1], None, ALU.mult, None, [b_tw, bc_], [b_tw])
                    stt(sg[:], XG[:, 1:N + 1], omg[:, 0:1], tw[:], ALU.mult, ALU.add, [b_XG, b_tw, bc_], [b_sg])
                    act(sg[:], sg[:], AF.Sigmoid, [b_sg], [b_sg])
                    for h2_ in range(4):
                        pm_, bpm = nextps()
                        for j in range(2):
                            h = 2 * h2_ + j
                            mm(pm_[0:64, j * N:(j + 1) * N], g2t[:, h * 64:(h + 1) * 64], sg[:], True, True, [bc_, b_sg], [bpm])
                        tt("dve", yr[:, 2 * h2_:2 * h2_ + 2, :], pm_[0:64, :].rearrange("p (h n) -> p h n", n=N),
                           Ym[:, 2 * h2_:2 * h2_ + 2, :], ALU.mult, [bpm, b_Ym], [b_yr])
                    K0 = 2.0 * math.sqrt(2.0 / math.pi)
                    tt("pool", Z1[:], S5[:], S5[:], ALU.mult, [b_S5], [b_Z1])
                    ts("dve", Z1[:], Z1[:], 0.044715, 1.0, ALU.mult, ALU.add, [b_Z1], [b_Z1])
                    tt("pool", Z1[:], Z1[:], S5[:], ALU.mult, [b_Z1, b_S5], [b_Z1])
                    act(Z1[:], Z1[:], AF.Sigmoid, [b_Z1], [b_Z1], scale=K0)
                    tt("pool", Z1[:], Z1[:], S5[:], ALU.mult, [b_Z1, b_S5], [b_Z1])
                    cp("dve", Zb[:], Z1[:], [b_Z1], [b_Zb])
                    for q in range(4):
                        pm_, bpm = nextps()
                        mm(pm_[:, 0:N], Wbd[:, q, :], Zb[:, q, :], True, True, [bc_, b_Zb], [bpm])
                        mk.op("act", lambda E, pm_=pm_, q=q: E.activation(out=Z2[:, q, :], in_=pm_[:, 0:N], func=AF.Sigmoid,
                                                                        bias=bgl[:, q:q + 1], scale=1.0), [bpm, bc_], [b_Z2])
                    tt("pool", Z2[:], Z2[:], Z1[:], ALU.mult, [b_Z2, b_Z1], [b_Z2])
                    tt("pool", Zb[:], Z2[:], Z2[:], ALU.mult, [b_Z2], [b_Zb])
                    pm_, bpm = nextps()
                    for q in range(4):
                        mm(pm_[:, 0:N], ones_bf[:], Zb[:, q, :], q == 0, q == 3, [bc_, b_Zb], [bpm])
                    act(r2[:], pm_[:, 0:N], AF.Sqrt, [bpm], [b_r2], bias=RMS_EPS, scale=1.0 / 512.0)
                    mk.op("dve", lambda E: E.reciprocal(out=r2[:], in_=r2[:]), [b_r2], [b_r2])
                    for q in range(4):
                        stt(ysb[:, q, :], Z2[:, q, :], s5g[:, q:q + 1], r2[:], ALU.mult, ALU.mult, [b_Z2, b_r2, bc_], [b_ysb])
                    for dm in range(8):
                        pm_, bpm = nextps()
                        for h in range(8):
                            mm(pm_[:, 0:N], wo_r[:, h, dm * 128:(dm + 1) * 128], yr[:, h, :], h == 0, False, [bc_, b_yr], [bpm])
                        for q in range(4):
                            mm(pm_[:, 0:N], wo_s[:, q, dm * 128:(dm + 1) * 128], ysb[:, q, :], False, q == 3, [bc_, b_ysb], [bpm])
                        stt(X1t[:, dm, :], pm_[:, 0:N], modT[:, 16 + dm, s:s + 1], xTt[:, dm, :], ALU.mult, ALU.add,
                            [bpm, b_mod, b_xTt], [b_X1t])
                    dma("sp", X1[:, tg:tg + N].rearrange("(k p) t -> p k t", p=128), X1t[:], [b_X1t], [b_X1])
            mk.flush(final=True)

    if upto >= 3:
        mix_pass()

    def ffn_pass():
        es, sb, ps = scope("ff_")
        with es:
            N = 256
            pz = [ps("pz%d" % i, [128, 512]) for i in range(8)]
            b_pz = [Buf() for _ in range(8)]
            pctr = [0]

            def nextps():
                i = pctr[0] % 8
                pctr[0] += 1
                return pz[i], b_pz[i]
            bc_ = Buf()
            ones_bf = sb("ones_bf", [128, 128], BF16)
            mk.op("pool", lambda E: E.memset(ones_bf[:], 1.0), (), [bc_])
            n2g = sb("n2g", [128, 8]); fg = sb("fg", [128, 8]); sc2 = sb("sc2", [128, 8, NS])
            dma("sp", n2g[:], norm2_g.rearrange("(k p) -> p k", p=128), (), [bc_], slow=True)
            dma("sp", fg[:], final_g.rearrange("(k p) -> p k", p=128), (), [bc_], slow=True)
            for s in range(NS):
                stt(sc2[:, :, s], modT[:, 32:40, s], 1.0, n2g[:], ALU.add, ALU.mult, [b_mod, bc_], [bc_])
            w1 = sb("w1", [128, 8, DFF], BF16); w3 = sb("w3", [128, 8, DFF], BF16); w2_ = sb("w2_", [128, 22, D], BF16)
            stg = [sb("stg%d" % i, [128, 1408]) for i in range(2)]; b_stg = [Buf(), Buf()]
            n = 0
            for (src, dst) in ((w_ff1, w1), (w_ff3, w3)):
                for k in range(8):
                    for hf in range(2):
                        st_ = stg[n % 2]; bs_ = b_stg[n % 2]; n += 1
                        dma("sp" if n % 2 else "act", st_[:], src[k * 128:(k + 1) * 128, hf * 1408:(hf + 1) * 1408], (), [bs_])
                        cp("pool" if n % 2 else "dve", dst[:, k, hf * 1408:(hf + 1) * 1408], st_[:], [bs_], [bc_])
            for k in range(22):
                st_ = stg[n % 2]; bs_ = b_stg[n % 2]; n += 1
                dma("sp" if n % 2 else "act", st_[:, 0:D], w_ff2[k * 128:(k + 1) * 128, :], (), [bs_])
                cp("pool" if n % 2 else "dve", w2_[:, k, :], st_[:, 0:D], [bs_], [bc_])
            X1t = sb("X1t", [128, 8, N]); b_X1t = Buf()
            sq = sb("sq", [128, 8, N], BF16); b_sq = Buf()
            rstd = sb("rstd", [128, N]); b_rstd = Buf(); tmp = sb("tmp", [128, N]); b_tmp = Buf()
            h2 = sb("h2", [128, 8, N], BF16); b_h2 = Buf()
            fm = sb("fm", [128, 22, N], BF16); b_fm = Buf()
            av = [sb("av%d" % i, [128, N]) for i in range(2)]; b_av = [Buf(), Buf()]
            X2t = sb("X2t", [128, 8, N]); b_X2t = Buf()
            ytm = sb("ytm", [128, 2, D]); b_ytm = Buf()

            def rms_bc(src, bsrc):
                for dc in range(8):
                    act(sq[:, dc, :], src[:, dc, :], AF.Square, [bsrc], [b_sq])
                pm_, bpm = nextps()
                for dc in range(8):
                    mm(pm_[:, 0:N], ones_bf[:], sq[:, dc, :], dc == 0, dc == 7, [bc_, b_sq], [bpm])
                act(rstd[:], pm_[:, 0:N], AF.Sqrt, [bpm], [b_rstd], bias=RMS_EPS, scale=1.0 / D)
                mk.op("dve", lambda E: E.reciprocal(out=rstd[:], in_=rstd[:]), [b_rstd], [b_rstd])
            for s, (toff, coff, T) in enumerate(SEQ):
                for ti in range(T // N):
                    tg = toff + ti * N
                    dma("sp", X1t[:], X1[:, tg:tg + N].rearrange("(k p) t -> p k t", p=128), [b_X1], [b_X1t])
                    rms_bc(X1t, b_X1t)
                    for dc in range(8):
                        tt("dve", tmp[:], X1t[:, dc, :], rstd[:], ALU.mult, [b_X1t, b_rstd], [b_tmp])
                        ts("dve", h2[:, dc, :], tmp[:], sc2[:, dc, s:s + 1], modT[:, 24 + dc, s:s + 1], ALU.mult, ALU.add,
                           [b_tmp, bc_, b_mod], [b_h2])
                    for mc in range(22):
                        p1, bp1 = nextps()
                        for dc in range(8):
                            mm(p1[:, 0:N], w1[:, dc, mc * 128:(mc + 1) * 128], h2[:, dc, :], dc == 0, dc == 7, [bc_, b_h2], [bp1])
                        p3, bp3 = nextps()
                        for dc in range(8):
                            mm(p3[:, 0:N], w3[:, dc, mc * 128:(mc + 1) * 128], h2[:, dc, :], dc == 0, dc == 7, [bc_, b_h2], [bp3])
                        a_ = av[mc % 2]; ba_ = b_av[mc % 2]
                        act(a_[:], p1[:, 0:N], AF.Silu, [bp1], [ba_])
                        tt("dve", fm[:, mc, :], a_[:], p3[:, 0:N], ALU.mult, [ba_, bp3], [b_fm])
                    for dm in range(8):
                        pm_, bpm = nextps()
                        for mc in range(22):
                            mm(pm_[:, 0:N], w2_[:, mc, dm * 128:(dm + 1) * 128], fm[:, mc, :], mc == 0, mc == 21, [bc_, b_fm], [bpm])
                        stt(X2t[:, dm, :], pm_[:, 0:N], modT[:, 40 + dm, s:s + 1], X1t[:, dm, :], ALU.mult, ALU.add,
                            [bpm, b_mod, b_X1t], [b_X2t])
                    rms_bc(X2t, b_X2t)
                    for dc in range(8):
                        stt(X2t[:, dc, :], X2t[:, dc, :], fg[:, dc:dc + 1], rstd[:], ALU.mult, ALU.mult,
                            [b_X2t, bc_, b_rstd], [b_X2t])
                    for j in range(N // 128):
                        for dq in range(2):
                            pm_, bpm = nextps()
                            for k in range(4):
                                dc = dq * 4 + k
                                mk.op("pe", lambda E, pm_=pm_, dc=dc, j=j, k=k: E.transpose(
                                    out=pm_[:, k * 128:(k + 1) * 128], in_=X2t[:, dc, j * 128:(j + 1) * 128], identity=ident[:]),
                                    [b_X2t, b_ident], [bpm])
                            cp("act" if dq == 0 else "dve", ytm[:, j, dq * 512:(dq + 1) * 512], pm_[:], [bpm], [b_ytm])
                    dma("sp", y_out[tg:tg + N, :].rearrange("(j p) d -> p j d", p=128), ytm[:], [b_ytm], [])
            mk.flush(final=True)

    if upto >= 4:
        ffn_pass()

    outer.close()
    return nc


def core_inputs(P, x, c):
    f = np.float32
    m = {"x": x, "c": c}
    for k in ("norm1_g", "w_ada", "b_ada", "w_in", "mu_shift", "w0", "w2", "a0", "a2", "g2", "k_k", "k_a",
              "lnx_g", "lnx_b", "lam_re", "lam_im", "log_dt", "b_re", "b_im", "c_re", "c_im", "w_glu",
              "s5_out_g", "w_out", "norm2_g", "w_ff1", "w_ff3", "w_ff2", "final_g"):
        m[k] = P[k]
    m["r_k"] = P["r_k"].reshape(512)
    m["d_skip"] = P["d_skip"].reshape(512)
    m["b_glu"] = P["b_glu"].reshape(512)
    return {k: np.ascontiguousarray(v, dtype=f) for k, v in m.items()}


_T_PROMPT = 8192
_T_SAMPLE = 4096


def kernel(**inputs):
    n = 8
    P = {}
    for k, v in inputs.items():
        if k in ("x_prompt", "x_sample", "c_prompt", "c_sample"):
            continue
        v = np.asarray(v)
        P[k] = v if k == "final_g" else v[0]
    xp = np.asarray(inputs["x_prompt"]); xs = np.asarray(inputs["x_sample"])
    cpr = np.asarray(inputs["c_prompt"]); cs = np.asarray(inputs["c_sample"])
    TS = [xp.shape[1], xs.shape[1]]
    nc = build_program(TS, upto=int(os.environ.get('KUPTO', '99')))
    in_maps = []
    for b in range(n):
        x = np.concatenate([xp[b], xs[b]], axis=0)
        c = np.stack([cpr[b], cs[b]], axis=0)
        in_maps.append(core_inputs(P, x, c))
    res = run_bass_kernel_spmd(nc, in_maps, core_ids=list(range(n)))
    yp = np.stack([res.results[b]["y"][:TS[0]] for b in range(n)], axis=0).astype(np.float32)
    ys = np.stack([res.results[b]["y"][TS[0]:] for b in range(n)], axis=0).astype(np.float32)
    return (yp, ys)
```

```python
import os
import math
import numpy as np
import concourse.bass as bass
import concourse.mybir as mybir
from concourse.bass_utils import run_bass_kernel_spmd

F32 = mybir.dt.float32
BF16 = mybir.dt.bfloat16
AF = mybir.ActivationFunctionType
ALU = mybir.AluOpType
AX = mybir.AxisListType

D = 1024
DFF = 2816
NPROJ = 2432
RW = 1920
NDS = 12
RMS_EPS = 1e-6
LNX_EPS = 64e-5


class Buf:
    __slots__ = ("w", "r")

    def __init__(self):
        self.w = None
        self.r = {}


class MK:
    BLK = {"pe": "tensor", "dve": "vector", "act": "scalar", "pool": "gpsimd", "sp": "sync"}

    def __init__(self, nc, same=True):
        self.nc = nc
        self.same = same
        self.names = ["pe", "dve", "act", "pool", "sp"]
        self.sem = {k: nc.alloc_semaphore(name="s_" + k) for k in self.names}
        self.cnt = {k: 0 for k in self.names}
        self.seen = {k: {} for k in self.names}
        self.prog = {k: [] for k in self.names}
        self.dsem = [nc.alloc_semaphore(name="d%d" % i) for i in range(NDS)]
        self.dcnt = [0] * NDS
        self.dnext = 0
        self.deferred = None

    def semof(self, key):
        if isinstance(key, tuple):
            return self.dsem[key[1]]
        return self.sem[key]

    def _deps(self, e, reads, writes):
        deps = {}

        def add(k, v):
            if deps.get(k, 0) < v:
                deps[k] = v

        for b in reads:
            if b.w:
                add(*b.w)
        for b in writes:
            if b.w:
                add(*b.w)
            for k, v in b.r.items():
                add(k, v)
        out = []
        for k, v in deps.items():
            if k == e and (e == "pe" or not self.same):
                continue
            if self.seen[e].get(k, 0) >= v:
                continue
            self.seen[e][k] = v
            out.append((k, v))
        return out

    def _mark(self, tok, reads, writes):
        k, v = tok
        for b in reads:
            if b.r.get(k, 0) < v:
                b.r[k] = v
        for b in writes:
            b.w = tok
            b.r = {}

    def op(self, e, fn, reads=(), writes=()):
        if self.deferred is not None:
            self.deferred.append((0, e, fn, reads, writes))
            return
        waits = self._deps(e, reads, writes)
        self.cnt[e] += 1
        tok = (e, self.cnt[e])
        self.prog[e].append((waits, fn, self.sem[e], 1))
        self._mark(tok, reads, writes)

    def replay(self, pending, n):
        keep = self.deferred
        self.deferred = None
        last = None
        cnt = 0
        while pending and (cnt < n or last == "pe"):
            kind, e, fn, reads, writes = pending.pop(0)
            (self.dma if kind else self.op)(e, fn, reads, writes)
            last = e if not kind else None
            cnt += 1
        self.deferred = keep

    def dma(self, q, fn, reads=(), writes=()):
        if self.deferred is not None:
            self.deferred.append((1, q, fn, reads, writes))
            return
        i = self.dnext
        self.dnext = (i + 1) % NDS
        key = ("d", i)
        waits = self._deps(q, reads, writes)
        if self.dcnt[i] > 0 and self.seen[q].get(key, 0) < self.dcnt[i]:
            waits.append((key, self.dcnt[i]))
            self.seen[q][key] = self.dcnt[i]
        self.dcnt[i] += 16
        tok = (key, self.dcnt[i])
        self.prog[q].append((waits, fn, self.dsem[i], 16))
        self._mark(tok, reads, writes)

    def flush(self, final=False):
        nc = self.nc
        fin = []
        for i in (range(NDS) if final else []):
            if self.dcnt[i] > 0:
                fin.append((("d", i), self.dcnt[i]))
        for k in (self.names if final else []):
            if k != "sp" and self.cnt[k] > 0:
                fin.append((k, self.cnt[k]))
        with nc.Block() as block:
            for e in self.names:
                prog = self.prog[e]
                extra = fin if e == "sp" else []

                def body(eng, prog=prog, extra=extra):
                    for waits, fn, sem, inc in prog:
                        for k, v in waits:
                            eng.wait_ge(self.semof(k), v)
                        fn(eng).then_inc(sem, inc)
                    for k, v in extra:
                        eng.wait_ge(self.semof(k), v)

                getattr(block, self.BLK[e])(body)
        self.prog = {k: [] for k in self.names}

    def emit(self):
        self.flush(final=True)


def build_program(TS, dbg=False, upto=99):
    import contextlib
    nc = bass.Bass("TRN2", target_bir_lowering=False)
    mk = MK(nc)
    TT = sum(TS)
    NS = len(TS)
    WP = TT + 2 * NS
    SEQ = []
    o = 0
    for s, T in enumerate(TS):
        SEQ.append((o, o + 2 * s, T))
        o += T

    def din(name, shape):
        return nc.dram_tensor(name, list(shape), F32, kind="ExternalInput").ap()

    def dscr(name, shape):
        return nc.dram_tensor(name, list(shape), F32, kind=("ExternalOutput" if dbg else "Internal")).ap()

    x_in = din("x", (TT, D))
    c_in = din("c", (NS, D))
    norm1_g = din("norm1_g", (D,))
    w_ada = din("w_ada", (D, 6 * D))
    b_ada = din("b_ada", (6 * D,))
    w_in = din("w_in", (D, NPROJ))
    mu_shift = din("mu_shift", (RW,))
    w0 = din("w0", (2, 512)); w2 = din("w2", (2, 64, 512))
    a0 = din("a0", (2, 512)); a2 = din("a2", (2, 64, 512))
    g2 = din("g2", (128, 512))
    k_k = din("k_k", (512,)); k_a = din("k_a", (512,)); r_k = din("r_k", (512,))
    lnx_g = din("lnx_g", (512,)); lnx_b = din("lnx_b", (512,))
    lam_re = din("lam_re", (2, 32, 64)); lam_im = din("lam_im", (2, 32, 64)); log_dt = din("log_dt", (2, 32))
    b_re = din("b_re", (2, 32, 64, 16)); b_im = din("b_im", (2, 32, 64, 16))
    c_re = din("c_re", (2, 32, 16, 64)); c_im = din("c_im", (2, 32, 16, 64))
    d_skip = din("d_skip", (512,)); w_glu = din("w_glu", (32, 16, 16)); b_glu = din("b_glu", (512,))
    s5_out_g = din("s5_out_g", (512,))
    w_out = din("w_out", (D, D)); norm2_g = din("norm2_g", (D,))
    w_ff1 = din("w_ff1", (D, DFF)); w_ff3 = din("w_ff3", (D, DFF)); w_ff2 = din("w_ff2", (DFF, D))
    final_g = din("final_g", (D,))
    y_out = nc.dram_tensor("y", [TT, D], F32, kind="ExternalOutput").ap()

    Pscr = dscr("Pscr", (NPROJ, WP))
    XT = dscr("XT", (D, TT))
    YD = dscr("YD", (2, 64, 8, TT))
    BD = dscr("BD", (2, 64, 8, TT))
    YS = dscr("YS", (512, TT))
    X1 = dscr("X1", (D, TT))
    MODS = dscr("MODS", (128, 48 * NS))
    b_P = Buf(); b_XT = Buf(); b_YD = Buf(); b_BD = Buf(); b_YS = Buf(); b_X1 = Buf(); b_MODS = Buf()

    def tt(e, out, a, b, op, r, w):
        mk.op(e, lambda E: E.tensor_tensor(out=out, in0=a, in1=b, op=op), r, w)

    def ts(e, out, a, s1, s2, op0, op1, r, w):
        if op1 is None:
            mk.op(e, lambda E: E.tensor_scalar(out=out, in0=a, scalar1=s1, scalar2=None, op0=op0), r, w)
        else:
            mk.op(e, lambda E: E.tensor_scalar(out=out, in0=a, scalar1=s1, scalar2=s2, op0=op0, op1=op1), r, w)

    def stt(out, a, sc, b, op0, op1, r, w):
        mk.op("dve", lambda E: E.scalar_tensor_tensor(out=out, in0=a, scalar=sc, in1=b, op0=op0, op1=op1), r, w)

    def act(out, a, func, r, w, bias=0.0, scale=1.0):
        mk.op("act", lambda E: E.activation(out=out, in_=a, func=func, bias=bias, scale=scale), r, w)

    def cp(e, out, a, r, w):
        if e == "act":
            mk.op("act", lambda E: E.activation(out=out, in_=a, func=AF.Copy), r, w)
        else:
            mk.op(e, lambda E: E.tensor_copy(out=out, in_=a), r, w)

    def mm(out, lhsT, rhs, st, sp_, r, w):
        mk.op("pe", lambda E: E.matmul(out=out, lhsT=lhsT, rhs=rhs, start=st, stop=sp_), r, w)

    def dma(q, out, in_, r, w, slow=False):
        if slow:
            mk.dma(q, lambda E: E.dma_start(out=out, in_=in_, allow_slow_non_contiguous=True), r, w)
        else:
            mk.dma(q, lambda E: E.dma_start(out=out, in_=in_), r, w)

    def scope(pfx=""):
        es = contextlib.ExitStack()

        def sb(name, shape, dt=F32):
            return es.enter_context(nc.sbuf_tensor(pfx + name, list(shape), dt))

        def ps(name, shape, dt=F32):
            return es.enter_context(nc.psum_tensor(pfx + name, list(shape), dt))
        return es, sb, ps

    def consts(sb):
        ident = sb("ident", [128, 128]); b_ident = Buf()
        mk.op("pool", lambda E: E.memset(ident[:], 1.0), (), [b_ident])
        mk.op("pool", lambda E: E.affine_select(out=ident[:], in_=ident[:], pattern=[[-1, 128]],
                                                compare_op=ALU.is_equal, fill=0.0, base=0, channel_multiplier=1),
              [b_ident], [b_ident])
        return ident, b_ident
    outer, osb, ops_ = scope("o_")
    ident, b_ident = consts(osb)
    modT = osb("modT", [128, 48, NS]); b_mod = Buf()
    sc1 = osb("sc1", [128, 8, NS]); b_sc1 = Buf()

    def pass0():
        es, sb, ps = scope("p0_")
        with es:
            ones_bf = sb("ones_bf", [128, 128], BF16); b_ones = Buf()
            mk.op("pool", lambda E: E.memset(ones_bf[:], 1.0), (), [b_ones])
            cT = sb("cT", [128, 8, NS]); b_cT = Buf()
            scT = sb("scT", [128, 8, NS]); b_scT = Buf()
            for s in range(NS):
                dma("sp", cT[:, :, s], c_in[s].rearrange("(k p) -> p k", p=128), (), [b_cT], slow=True)
            act(scT[:], cT[:], AF.Silu, [b_cT], [b_scT])
            badaT = sb("badaT", [128, 48]); b_bada = Buf()
            dma("sp", badaT[:], b_ada.rearrange("(k p) -> p k", p=128), (), [b_bada], slow=True)
            g1T = sb("g1T", [128, 8]); b_g1 = Buf()
            dma("sp", g1T[:], norm1_g.rearrange("(k p) -> p k", p=128), (), [b_g1], slow=True)
            wada_t = [sb("wada%d" % i, [128, 8, 256]) for i in range(2)]
            b_wada = [Buf(), Buf()]
            ps_mod_full = ps("ps_mod", [128, 512]); b_psmod = Buf()
            ps_mod = ps_mod_full[:, 0:4 * NS].rearrange("p (a b) -> p a b", b=NS)
            for slab in range(24):
                wt = wada_t[slab % 2]; bw = b_wada[slab % 2]
                dma("sp" if slab % 2 == 0 else "act", wt[:],
                    w_ada[:, slab * 256:(slab + 1) * 256].rearrange("(k p) n -> p k n", p=128), (), [bw])
                for j in range(2):
                    for k in range(8):
                        mm(ps_mod[:, j, :], wt[:, k, j * 128:(j + 1) * 128], scT[:, k, :], k == 0, k == 7,
                           [bw, b_scT], [b_psmod])
                for s in range(NS):
                    tt("dve", modT[:, slab * 2:(slab + 1) * 2, s], ps_mod[:, 0:2, s],
                       badaT[:, slab * 2:(slab + 1) * 2], ALU.add, [b_psmod, b_bada], [b_mod])
            for s in range(NS):
                stt(sc1[:, :, s], modT[:, 8:16, s], 1.0, g1T[:], ALU.add, ALU.mult, [b_mod, b_g1], [b_sc1])

            w_in_bf = sb("w_in_bf", [128, 8, NPROJ], BF16); b_win = Buf()
            wst = [sb("wst%d" % i, [128, NPROJ]) for i in range(2)]; b_wst = [Buf(), Buf()]
            for k in range(8):
                dma("sp", wst[k % 2][:], w_in[k * 128:(k + 1) * 128, :], (), [b_wst[k % 2]])
                cp("pool", w_in_bf[:, k, :], wst[k % 2][:], [b_wst[k % 2]], [b_win])

            NT = 512
            xtm = [sb("xtm%d" % i, [128, 4, D]) for i in range(2)]; b_xtm = [Buf(), Buf()]
            xT = sb("xT", [128, 8, NT]); b_xT = Buf()
            sq = sb("sq", [128, 8, NT], BF16); b_sq = Buf()
            rstd = sb("rstd", [128, NT]); b_rstd = Buf()
            tmp = sb("tmp0", [128, NT]); b_tmp = Buf()
            hT = sb("hT", [128, 8, NT], BF16); b_hT = Buf()
            pev = [sb("pev%d" % i, [128, NT]) for i in range(3)]; b_pev = [Buf() for _ in range(3)]
            zcol = sb("zcol", [128, 1]); b_zcol = Buf()
            mk.op("pool", lambda E: E.memset(zcol[:], 0.0), (), [b_zcol])
            pst = [ps("pst%d" % i, [128, NT]) for i in range(4)]; b_pst = [Buf() for _ in range(4)]
            psm = [ps("psm%d" % i, [128, NT]) for i in range(3)]; b_psm = [Buf() for _ in range(3)]
            for s, (toff, coff, T) in enumerate(SEQ):
                for mc in range(19):
                    for cc in (coff, coff + T + 1):
                        dma("sp", Pscr[mc * 128:(mc + 1) * 128, cc:cc + 1], zcol[:], [b_zcol], [b_P], slow=True)
                for ti in range(T // NT):
                    t0 = toff + ti * NT
                    xt = xtm[ti % 2]; bx = b_xtm[ti % 2]
                    dma("sp", xt[:], x_in[t0:t0 + NT, :].rearrange("(j p) d -> p j d", p=128), (), [bx])
                    for dc in range(8):
                        pt = pst[dc % 4]; bp = b_pst[dc % 4]
                        for j in range(4):
                            mk.op("pe", lambda E, pt=pt, xt=xt, j=j, dc=dc: E.transpose(
                                out=pt[:, j * 128:(j + 1) * 128], in_=xt[:, j, dc * 128:(dc + 1) * 128],
                                identity=ident[:]), [bx, b_ident], [bp])
                        cp("dve", xT[:, dc, :], pt[:], [bp], [b_xT])
                        act(sq[:, dc, :], pt[:], AF.Square, [bp, b_xT], [b_sq])
                    dma("act", XT[:, t0:t0 + NT].rearrange("(k p) t -> p k t", p=128), xT[:], [b_xT], [b_XT])
                    pm = psm[0]; bpm = b_psm[0]
                    for dc in range(8):
                        mm(pm[:], ones_bf[:], sq[:, dc, :], dc == 0, dc == 7, [b_sq, b_ones], [bpm])
                    act(rstd[:], pm[:], AF.Sqrt, [bpm], [b_rstd], bias=RMS_EPS, scale=1.0 / D)
                    mk.op("dve", lambda E: E.reciprocal(out=rstd[:], in_=rstd[:]), [b_rstd], [b_rstd])
                    for dc in range(8):
                        tt("dve", tmp[:], xT[:, dc, :], rstd[:], ALU.mult, [b_xT, b_rstd], [b_tmp])
                        ts("dve", hT[:, dc, :], tmp[:], sc1[:, dc, s:s + 1], modT[:, dc, s:s + 1], ALU.mult, ALU.add,
                           [b_tmp, b_sc1, b_mod], [b_hT])
                    for mc in range(19):
                        i3 = mc % 3
                        pm = psm[i3]; bpm = b_psm[i3]
                        for dc in range(8):
                            mm(pm[:], w_in_bf[:, dc, mc * 128:(mc + 1) * 128], hT[:, dc, :], dc == 0, dc == 7,
                               [b_win, b_hT], [bpm])
                        pv = pev[i3]; bpv = b_pev[i3]
                        cp("act", pv[:], pm[:], [bpm], [bpv])
                        cc = coff + 1 + ti * NT
                        dma("sp", Pscr[mc * 128:(mc + 1) * 128, cc:cc + NT], pv[:], [bpv], [b_P])
            mk.flush(final=True)

    pass0()
    def rwkv_pass(d):
        rev = (d == 1)
        es, sb, ps = scope("rw%d_" % d)
        with es:
            NT2 = 128
            psr = [ps("psr%d" % i, [128, 2048]) for i in range(2)]
            b_psr = [Buf() for _ in range(2)]
            pctr = [0]

            def nextps():
                i = pctr[0] % 2
                pctr[0] += 1
                return psr[i], b_psr[i]

            def T4(name):
                return sb(name, [64, 8, NT2]), Buf()

            def ldp(name, src512):
                t = sb(name, [64, 8]); b = Buf()
                dma("sp", t[:], src512.rearrange("(h p) -> p h", p=64), (), [b], slow=True)
                return t, b

            mu3 = sb("mu3", [64, 24]); b_mu3 = Buf()
            dma("sp", mu3[:], mu_shift[0:1536].rearrange("(g p) -> p g", p=64), (), [b_mu3], slow=True)
            hm3 = sb("hm3", [64, 24]); om3 = sb("om3", [64, 24]); b_hm3 = Buf()
            ts("dve", hm3[:], mu3[:], 0.5, None, ALU.mult, None, [b_mu3], [b_hm3])
            ts("dve", om3[:], mu3[:], -1.0, 1.0, ALU.mult, ALU.add, [b_mu3], [b_hm3])
            muw = sb("muw", [64, 2]); b_muw = Buf()
            dma("sp", muw[:, 0:1], mu_shift[1536 + 64 * d:1600 + 64 * d].rearrange("(p o) -> p o", o=1), (), [b_muw], slow=True)
            dma("sp", muw[:, 1:2], mu_shift[1664 + 64 * d:1728 + 64 * d].rearrange("(p o) -> p o", o=1), (), [b_muw], slow=True)
            hmw = sb("hmw", [64, 2]); omw = sb("omw", [64, 2]); b_hmw = Buf()
            ts("dve", hmw[:], muw[:], 0.5, None, ALU.mult, None, [b_muw], [b_hmw])
            ts("dve", omw[:], muw[:], -1.0, 1.0, ALU.mult, ALU.add, [b_muw], [b_hmw])
            w0d, b_w0d = ldp("w0d", w0[d]); a0d, b_a0d = ldp("a0d", a0[d])
            kk_, b_kk_ = ldp("kk_", k_k); ka_, b_ka_ = ldp("ka_", k_a); rk_, b_rk_ = ldp("rk_", r_k)
            omka = sb("omka", [64, 8]); b_omka = Buf()
            ts("dve", omka[:], ka_[:], -1.0, 1.0, ALU.mult, ALU.add, [b_ka_], [b_omka])
            w2d = sb("w2d", [64, 512]); a2d = sb("a2d", [64, 512]); b_w2d = Buf()
            dma("sp", w2d[:], w2[d], (), [b_w2d]); dma("sp", a2d[:], a2[d], (), [b_w2d])
            ones64 = sb("ones64", [64, 64]); b_c = Buf()
            mk.op("pool", lambda E: E.memset(ones64[:], 1.0), (), [b_c])
            maskA = sb("maskA", [64, 128]); maskL = sb("maskL", [64, 64]); MS = sb("MS", [64, 8 * NT2])
            mk.op("pool", lambda E: E.memset(maskA[:], 1.0), (), [b_c])
            mk.op("pool", lambda E: E.memset(maskL[:], 1.0), (), [b_c])
            mk.op("pool", lambda E: E.memset(MS[:], 1.0), (), [b_c])
            zc_ = 63 if rev else 0
            mk.op("pool", lambda E: E.memset(MS[:].rearrange("p (a l) -> p a l", l=64)[:, :, zc_:zc_ + 1], 0.0), [b_c], [b_c])

            def asel(ap, upper, strict):
                pat = [[1, 64]] if upper else [[-1, 64]]
                cm = -1 if upper else 1
                mk.op("pool", lambda E: E.affine_select(out=ap, in_=ap, pattern=pat, compare_op=ALU.is_ge, fill=0.0,
                                                        base=(-1 if strict else 0), channel_multiplier=cm),
                      [b_c], [b_c])
            asel(maskA[:, 0:64], not rev, True)
            asel(maskA[:, 64:128], not rev, False)
            asel(maskL[:], rev, True)
            mA = maskA[:, None, :].broadcast_to([64, 8, 128])
            mL = maskL[:, None, :].broadcast_to([64, 8, 64])
            id64 = ident[0:64, 0:64]
            idbc = ident[0:64, None, 0:64].broadcast_to([64, 8, 64])

            Lr = [sb("Lq%d" % q, [64, 8, NT2 + 2]) for q in range(2)]; b_L = [Buf() for _ in range(2)]
            Lr.append(Lr[0]); b_L.append(b_L[0])
            XW = sb("XW", [64, NT2 + 2]); XA = sb("XA", [64, NT2 + 2]); b_XW = Buf(); b_XA = Buf()
            T1, b_T1 = T4("T1")
            SH = [T4("SH%d" % q) for q in range(2)]
            (Rp, b_Rp), (Kp, b_Kp) = SH
            tt0, cp0 = tt, cp
            tw = sb("tw", [64, NT2]); b_tw = Buf()
            xwp = sb("xwp", [64, NT2]); xap = sb("xap", [64, NT2]); b_xwp = Buf(); b_xap = Buf()
            XB, b_XB = T4("XB"); E2, b_E2 = T4("E2"); AD, b_AD = T4("AD"); KR, b_KR = T4("KR")
            SS, b_SS = T4("SS"); KD, b_KD = T4("KD"); AB, b_AB = T4("AB"); BON, b_BON = XB, b_XB
            G, b_G = T4("G"); D1, b_D1 = E2, b_E2; D2, b_D2 = AD, b_AD; EP, b_EP = XB, b_XB; EN, b_EN = SS, b_SS
            T2, b_T2 = SS, b_SS
            SD = F32
            SETS = []
            for i_ in range(2):
                st_ = []
                for nm, shp in (("AR", [64, 8, 2, 128]), ("KT", [64, 8, NT2]), ("BT", [64, 8, NT2]), ("KH", [64, 8, NT2]),
                                ("BH", [64, 8, NT2]), ("Vp", [64, 8, NT2]), ("GL", [64, 16])):
                    st_ += [sb("%s_%d" % (nm, i_), shp), Buf()]
                SETS.append(st_)
            YT, b_YT = T4("YT")
            MT1 = sb("MT1", [64, 2, 8, 128], SD); MT2 = sb("MT2", [64, 2, 8, 128], SD); b_MT1 = Buf(); b_MT2 = Buf()
            P0 = sb("P0", [64, 2, 8, 64], SD); b_P0 = Buf()
            PP = [sb("PP%d" % i, [64, 2, 16, 64], SD) for i in range(2)]; b_PP = [Buf(), Buf()]
            Zt = [sb("Zt%d" % i, [64, 2, 8, 128], SD) for i in range(2)]; b_Zt = [Buf() for _ in range(2)]
            VT = sb("VT", [64, 2, 8, 64], SD); BHt = sb("BHt", [64, 2, 8, 64], SD); KHt = sb("KHt", [64, 2, 8, 64], SD)
            QT = sb("QT", [64, 2, 8, 64]); MM = sb("MM", [64, 2, 8, 64]); DG = P0
            b_VT = Buf(); b_BHt = Buf(); b_KHt = Buf(); b_QT = Buf(); b_MM = Buf(); b_DG = b_P0
            STt = [sb("ST%d" % i, [64, 8, 64]) for i in range(2)]; b_ST = [Buf(), Buf()]
            mA16 = maskA[:, None, :].broadcast_to([64, 16, 128])
            mL16 = maskL[:, None, :].broadcast_to([64, 16, 64])
            idbc16 = ident[0:64, None, 0:64].broadcast_to([64, 16, 64])

            def f16(t):
                return t[:].rearrange("p c h n -> p (c h) n")

            def pv(p, lo, n):
                return p[0:64, lo:lo + 16 * n].rearrange("p (a n) -> p a n", n=n)

            def v3(p, n):
                return p[0:64, 0:8 * n].rearrange("p (h n) -> p h n", n=n)

            def bc(t, lo, hi, n):
                return t[:, lo:hi, None].broadcast_to([64, hi - lo, n])

            def c4(t):
                return t[:].rearrange("p h (c l) -> p h c l", l=64)

            for s, (toff, coff, T) in enumerate(SEQ):
                sti_ = [0]
                mk.op("pool", lambda E: E.memset(STt[0][:], 0.0), (), [b_ST[0]])
                ntile = T // NT2
                order = list(range(ntile - 1, -1, -1) if rev else range(ntile))

                def prep(ti, AR, b_AR, KT, b_KT, BT, b_BT, KH, b_KH, BH, b_BH, Vp, b_Vp, GL, b_GL):
                    ARb, b_ARb = AR, b_AR
                    tl = ti * NT2
                    c0 = coff + tl
                    tg = toff + tl
                    def ldq(q):
                        dma("sp" if q != 1 else "act", Lr[q][:],
                            Pscr[q * 512:(q + 1) * 512, c0:c0 + NT2 + 2].rearrange("(h p) t -> p h t", p=64),
                            [b_P], [b_L[q]])

                    def shq(q):
                        Lq = Lr[q]; S_, bS = (SH[q] if q < 2 else (Vp, b_Vp))
                        tt("pool", T1[:], Lq[:, :, 0:NT2], Lq[:, :, 2:NT2 + 2], ALU.add, [b_L[q]], [b_T1])
                        tt("pool", T1[:], T1[:], bc(hm3, 8 * q, 8 * q + 8, NT2), ALU.mult, [b_T1, b_hm3], [b_T1])
                        tt("pool", S_[:], Lq[:, :, 1:NT2 + 1], bc(om3, 8 * q, 8 * q + 8, NT2), ALU.mult,
                           [b_L[q], b_hm3], [bS])
                        tt("pool", S_[:], S_[:], T1[:], ALU.add, [bS, b_T1], [bS])
                    ldq(0); ldq(1)
                    dma("sp", XW[:], Pscr[1536 + 64 * d:1600 + 64 * d, c0:c0 + NT2 + 2], [b_P], [b_XW])
                    dma("act", XA[:], Pscr[1664 + 64 * d:1728 + 64 * d, c0:c0 + NT2 + 2], [b_P], [b_XA])
                    shq(0); ldq(2); shq(1); shq(2)
                    for (X_, bX, o_, bo, j) in ((XW, b_XW, xwp, b_xwp, 0), (XA, b_XA, xap, b_xap, 1)):
                        tt("dve", tw[:], X_[:, 0:NT2], X_[:, 2:NT2 + 2], ALU.add, [bX], [b_tw])
                        ts("dve", tw[:], tw[:], hmw[:, j:j + 1], None, ALU.mult, None, [b_tw, b_hmw], [b_tw])
                        stt(o_[:], X_[:, 1:NT2 + 1], omw[:, j:j + 1], tw[:], ALU.mult, ALU.add, [bX, b_hmw, b_tw], [bo])
                    act(xwp[:], xwp[:], AF.Tanh, [b_xwp], [b_xwp])
                    for hh in range(2):
                        pa, bpa = nextps()
                        for j in range(4):
                            h = 4 * hh + j
                            mm(pa[0:64, j * NT2:(j + 1) * NT2], w2d[:, h * 64:(h + 1) * 64], xwp[:], True, True,
                               [b_w2d, b_xwp], [bpa])
                        tt("dve", XB[:, 4 * hh:4 * hh + 4, :], pa[0:64, 0:4 * NT2].rearrange("p (h n) -> p h n", n=NT2),
                           bc(w0d, 4 * hh, 4 * hh + 4, NT2), ALU.add, [bpa, b_w0d], [b_XB])
                    act(XB[:], XB[:], AF.Exp, [b_XB], [b_XB], scale=-1.0)
                    act(XB[:], XB[:], AF.Ln, [b_XB], [b_XB], bias=1.0)
                    act(E2[:], XB[:], AF.Exp, [b_XB], [b_E2], bias=-0.5, scale=-1.0)
                    for hh in range(2):
                        pa, bpa = nextps()
                        for j in range(4):
                            h = 4 * hh + j
                            mm(pa[0:64, j * NT2:(j + 1) * NT2], a2d[:, h * 64:(h + 1) * 64], xap[:], True, True,
                               [b_w2d, b_xap], [bpa])
                        tt("dve", AD[:, 4 * hh:4 * hh + 4, :], pa[0:64, 0:4 * NT2].rearrange("p (h n) -> p h n", n=NT2),
                           bc(a0d, 4 * hh, 4 * hh + 4, NT2), ALU.add, [bpa, b_a0d], [b_AD])
                    act(AD[:], AD[:], AF.Sigmoid, [b_AD], [b_AD])
                    tt("pool", KR[:], Kp[:], bc(kk_, 0, 8, NT2), ALU.mult, [b_Kp, b_kk_], [b_KR])
                    tt("pool", T1[:], KR[:], KR[:], ALU.mult, [b_KR], [b_T1])
                    for hh in range(2):
                        pa, bpa = nextps()
                        for j in range(4):
                            h = 4 * hh + j
                            mm(pa[0:64, j * NT2:(j + 1) * NT2], ones64[:], T1[:, h, :], True, True, [b_c, b_T1], [bpa])
                        ts("dve", SS[:, 4 * hh:4 * hh + 4, :], pa[0:64, 0:4 * NT2].rearrange("p (h n) -> p h n", n=NT2),
                           1e-24, None, ALU.max, None, [bpa], [b_SS])
                    act(SS[:], SS[:], AF.Sqrt, [b_SS], [b_SS])
                    mk.op("dve", lambda E: E.reciprocal(out=SS[:], in_=SS[:]), [b_SS], [b_SS])
                    tt("pool", KR[:], KR[:], SS[:], ALU.mult, [b_KR, b_SS], [b_KR])
                    tt("pool", T2[:], AD[:], bc(ka_, 0, 8, NT2), ALU.mult, [b_AD, b_ka_], [b_T2])
                    tt("pool", T2[:], T2[:], bc(omka, 0, 8, NT2), ALU.add, [b_T2, b_omka], [b_T2])
                    tt("pool", KD[:], T2[:], Kp[:], ALU.mult, [b_T2, b_Kp], [b_KD])
                    tt("dve", AB[:], AD[:], KR[:], ALU.mult, [b_AD, b_KR], [b_AB])
                    tt("pool", T1[:], Rp[:], KD[:], ALU.mult, [b_Rp, b_KD], [b_T1])
                    tt("pool", T1[:], T1[:], bc(rk_, 0, 8, NT2), ALU.mult, [b_T1, b_rk_], [b_T1])
                    for hh in range(2):
                        pa, bpa = nextps()
                        for j in range(4):
                            h = 4 * hh + j
                            mm(pa[0:64, j * NT2:(j + 1) * NT2], ones64[:], T1[:, h, :], True, True, [b_c, b_T1], [bpa])
                        tt("dve", BON[:, 4 * hh:4 * hh + 4, :], pa[0:64, 0:4 * NT2].rearrange("p (h n) -> p h n", n=NT2),
                           Vp[:, 4 * hh:4 * hh + 4, :], ALU.mult, [bpa, b_Vp], [b_BON])
                    dma("sp", BD[d, :, :, tg:tg + NT2], BON[:], [b_BON], [b_BD])
                    E2f = E2[:].rearrange("p h t -> p (h t)"); Gf = G[:].rearrange("p h t -> p (h t)"); MSf = MS[:]
                    if rev:
                        E2f = E2f[:, ::-1]; Gf = Gf[:, ::-1]; MSf = MSf[:, ::-1]
                    mk.op("dve", lambda E, Gf=Gf, MSf=MSf, E2f=E2f: E.tensor_tensor_scan(
                        out=Gf, data0=MSf, data1=E2f, initial=0.0, op0=ALU.mult, op1=ALU.add), [b_E2, b_c], [b_G])
                    tt("pool", D1[:], G[:], E2[:], ALU.subtract, [b_G, b_E2], [b_D1])
                    Gv = G[:].rearrange("p h (c l) -> p (h c) l", l=64)
                    ti_ = 0 if rev else 63
                    totb = Gv[:, :, ti_:ti_ + 1].broadcast_to([64, 16, 64])
                    tt("pool", D2[:].rearrange("p h (c l) -> p (h c) l", l=64), Gv, totb, ALU.subtract, [b_G], [b_D2])
                    act(EP[:], G[:], AF.Exp, [b_G], [b_EP])
                    act(EN[:], G[:], AF.Exp, [b_G], [b_EN], scale=-1.0)
                    act(D1[:], D1[:], AF.Exp, [b_D1], [b_D1], scale=-1.0)
                    act(D2[:], D2[:], AF.Exp, [b_D2], [b_D2])
                    act(GL[:].rearrange("p (a o) -> p a o", o=1), Gv[:, :, ti_:ti_ + 1], AF.Exp, [b_G], [b_GL], scale=-1.0)
                    stt(AR[:, :, :, 0:64], c4(KR), -1.0, c4(D1), ALU.mult, ALU.mult, [b_KR, b_D1], [b_AR])
                    tt("pool", AR[:, :, :, 64:128], c4(Rp), c4(EN), ALU.mult, [b_Rp, b_EN], [b_AR])
                    tt("pool", KT[:], KD[:], EP[:], ALU.mult, [b_KD, b_EP], [b_KT])
                    tt("dve", BT[:], AB[:], EP[:], ALU.mult, [b_AB, b_EP], [b_BT])
                    tt("pool", KH[:], KD[:], D2[:], ALU.mult, [b_KD, b_D2], [b_KH])
                    tt("dve", BH[:], AB[:], D2[:], ALU.mult, [b_AB, b_D2], [b_BH])

                def chunk(ti, pend, AR, b_AR, KT, b_KT, BT, b_BT, KH, b_KH, BH, b_BH, Vp, b_Vp, GL, b_GL):
                    ARb, b_ARb = AR, b_AR
                    tg = toff + ti * NT2

                    def tt(*a):
                        tt0(*a)
                        mk.replay(pend, 2)

                    def cp(*a):
                        cp0(*a)
                        mk.replay(pend, 2)
                    CS = [slice(0, 64), slice(64, 128)]
                    p1, bp1 = nextps()
                    for c in range(2):
                        for h in range(8):
                            mm(p1[0:64, (c * 8 + h) * 128:(c * 8 + h + 1) * 128], BT[:, h, CS[c]], ARb[:, h, c, :], True, True,
                               [b_BT, b_ARb], [bp1])
                    tt("dve", f16(MT1), pv(p1, 0, 128), mA16, ALU.mult, [bp1, b_c], [b_MT1])
                    p2, bp2 = nextps()
                    for c in range(2):
                        for h in range(8):
                            mm(p2[0:64, (c * 8 + h) * 128:(c * 8 + h + 1) * 128], KT[:, h, CS[c]], ARb[:, h, c, :], True, True,
                               [b_KT, b_ARb], [bp2])
                    tt("dve", f16(MT2), pv(p2, 0, 128), mA16, ALU.mult, [bp2, b_c], [b_MT2])
                    p3, bp3 = nextps()
                    for c in range(2):
                        for h in range(8):
                            mm(p3[0:64, (c * 8 + h) * 64:(c * 8 + h + 1) * 64], ARb[:, h, c, 0:64], BT[:, h, CS[c]], True, True,
                               [b_ARb, b_BT], [bp3])
                    tt("dve", f16(P0), pv(p3, 0, 64), mL16, ALU.mult, [bp3, b_c], [b_P0])
                    Z0 = Zt[0]; bZ0 = b_Zt[0]
                    p4, bp4 = nextps()
                    for c in range(2):
                        for h in range(8):
                            mk.op("pe", lambda E, p4=p4, o=(c * 8 + h) * 64, a=AR[:, h, c, 0:64]: E.transpose(
                                out=p4[0:64, o:o + 64], in_=a, identity=id64), [b_AR, b_ident], [bp4])
                            mk.op("pe", lambda E, p4=p4, o=1024 + (c * 8 + h) * 64, a=Vp[:, h, CS[c]]: E.transpose(
                                out=p4[0:64, o:o + 64], in_=a, identity=id64), [b_Vp, b_ident], [bp4])
                    cp0("act", Z0[:, :, :, 0:64].rearrange("p c h n -> p (c h) n"), pv(p4, 0, 64), [bp4], [bZ0])
                    cp("act", f16(VT), pv(p4, 1024, 64), [bp4], [b_VT])
                    p5, bp5 = nextps()
                    for c in range(2):
                        for h in range(8):
                            mk.op("pe", lambda E, p5=p5, o=(c * 8 + h) * 64, a=BH[:, h, CS[c]]: E.transpose(
                                out=p5[0:64, o:o + 64], in_=a, identity=id64), [b_BH, b_ident], [bp5])
                            mk.op("pe", lambda E, p5=p5, o=1024 + (c * 8 + h) * 64, a=KH[:, h, CS[c]]: E.transpose(
                                out=p5[0:64, o:o + 64], in_=a, identity=id64), [b_KH, b_ident], [bp5])
                    cp0("dve", f16(BHt), pv(p5, 0, 64), [bp5], [b_BHt])
                    cp("dve", f16(KHt), pv(p5, 1024, 64), [bp5], [b_KHt])
                    p6, bp6 = nextps()
                    for c in range(2):
                        for h in range(8):
                            mm(p6[0:64, (c * 8 + h) * 64:(c * 8 + h + 1) * 64], MT2[:, c, h, 0:64], VT[:, c, h, :], True, True,
                               [b_MT2, b_VT], [bp6])
                    cp("act", Z0[:, :, :, 64:128].rearrange("p c h n -> p (c h) n"), pv(p6, 0, 64), [bp6], [bZ0])
                    zi = 0
                    Pv = lambda c, h: P0[:, c, h, :]
                    PTv = lambda c, h: MT1[:, c, h, 0:64]
                    bP = b_P0; bPT = b_MT1
                    for it in range(6):
                        Zc = Zt[zi]; bZc = b_Zt[zi]; Zn = Zt[1 - zi]; bZn = b_Zt[1 - zi]
                        p7, bp7 = nextps()
                        for c in range(2):
                            for h in range(8):
                                mm(p7[0:64, (c * 8 + h) * 128:(c * 8 + h + 1) * 128], PTv(c, h), Zc[:, c, h, :], True, True,
                                   [bPT, bZc], [bp7])
                        tt("dve", f16(Zn), pv(p7, 0, 128), f16(Zc), ALU.add, [bp7, bZc], [bZn])
                        zi = 1 - zi
                        if it < 5:
                            nx = it % 2
                            p8, bp8 = nextps()
                            for c in range(2):
                                for h in range(8):
                                    mm(p8[0:64, (c * 8 + h) * 64:(c * 8 + h + 1) * 64], PTv(c, h), Pv(c, h), True, True,
                                       [bPT, bP], [bp8])
                                    mm(p8[0:64, 1024 + (c * 8 + h) * 64:1024 + (c * 8 + h + 1) * 64], Pv(c, h), PTv(c, h), True, True,
                                       [bPT, bP], [bp8])
                            cp("act", PP[nx][:].rearrange("p k a n -> p (k a) n"),
                               p8[0:64, 0:2048].rearrange("p (a n) -> p a n", n=64), [bp8], [b_PP[nx]])
                            Pv = lambda c, h, nx=nx: PP[nx][:, 0, c * 8 + h, :]
                            PTv = lambda c, h, nx=nx: PP[nx][:, 1, c * 8 + h, :]
                            bP = b_PP[nx]; bPT = b_PP[nx]
                    Zf = Zt[zi]; bZf = b_Zt[zi]
                    p9, bp9 = nextps()
                    for c in range(2):
                        for h in range(8):
                            mm(p9[0:64, (c * 8 + h) * 64:(c * 8 + h + 1) * 64], Zf[:, c, h, 0:64], MT1[:, c, h, 64:128], True, True,
                               [bZf, b_MT1], [bp9])
                            mm(p9[0:64, 1024 + (c * 8 + h) * 64:1024 + (c * 8 + h + 1) * 64], Zf[:, c, h, 0:64], BHt[:, c, h, :], True, True,
                               [bZf, b_BHt], [bp9])
                    tt0("dve", QT[:], p9[0:64, 0:1024].rearrange("p (c h n) -> p c h n", c=2, h=8),
                       AR[:, :, :, 64:128].rearrange("p h c n -> p c h n"), ALU.add, [bp9, b_AR], [b_QT])
                    GLc = GL[:].rearrange("p (h c) -> p c h", c=2)[:, :, :, None].broadcast_to([64, 2, 8, 64])
                    tt0("pool", DG[:], ident[0:64, None, None, 0:64].broadcast_to([64, 2, 8, 64]), GLc, ALU.mult,
                       [b_ident, b_GL], [b_DG])
                    tt("dve", f16(MM), pv(p9, 1024, 64), f16(DG), ALU.add, [bp9, b_DG], [b_MM])
                    for c in (range(1, -1, -1) if rev else range(2)):
                        sti = sti_[0]
                        ST = STt[sti]; bST = b_ST[sti]; STn = STt[1 - sti]; bSTn = b_ST[1 - sti]
                        p11, bp11 = nextps()
                        for h in range(8):
                            o_ = p11[0:64, h * 64:(h + 1) * 64]
                            mm(o_, ST[:, h, :], QT[:, c, h, :], True, False, [bST, b_QT], [bp11])
                            mm(o_, Zf[:, c, h, 64:128], MT1[:, c, h, 64:128], False, False, [bZf, b_MT1], [bp11])
                            mm(o_, VT[:, c, h, :], MT2[:, c, h, 64:128], False, True, [b_VT, b_MT2], [bp11])
                        cp("act", YT[:, :, CS[c]], v3(p11, 64), [bp11], [b_YT])
                        p12, bp12 = nextps()
                        for h in range(8):
                            o_ = p12[0:64, h * 64:(h + 1) * 64]
                            mm(o_, MM[:, c, h, :], ST[:, h, :], True, False, [b_MM, bST], [bp12])
                            mm(o_, BHt[:, c, h, :], Zf[:, c, h, 64:128], False, False, [b_BHt, bZf], [bp12])
                            mm(o_, KHt[:, c, h, :], VT[:, c, h, :], False, True, [b_KHt, b_VT], [bp12])
                        cp("dve", STn[:], v3(p12, 64), [bp12], [bSTn])
                        sti_[0] = 1 - sti
                    dma("sp", YD[d, :, :, tg:tg + NT2], YT[:], [b_YT], [b_YD])

                PIPE = os.environ.get('RW_NOPIPE') != '1'
                if PIPE:
                    prep(order[0], *SETS[0])
                for idx, ti in enumerate(order):
                    pend = []
                    if not PIPE:
                        prep(ti, *SETS[idx % 2])
                    elif idx + 1 < len(order):
                        mk.deferred = pend
                        prep(order[idx + 1], *SETS[(idx + 1) % 2])
                        mk.deferred = None
                    if os.environ.get('RW_PIPE_MODE') == 'start':
                        mk.replay(pend, len(pend))
                    chunk(ti, pend, *SETS[idx % 2])
                    mk.replay(pend, len(pend))
            mk.flush(final=True)

    if upto >= 1:
        rwkv_pass(0)
        rwkv_pass(1)

    def s5_pass():
        es, sb, ps = scope("s5_")
        with es:
            TWO_PI = 2.0 * math.pi
            pz = [ps("pz%d" % i, [128, 512]) for i in range(8)]
            b_pz = [Buf() for _ in range(8)]
            pctr = [0]

            def nextps():
                i = pctr[0] % 8
                pctr[0] += 1
                return pz[i], b_pz[i]

            NLV = 10
            identb = sb("identb", [64, 64], BF16)
            dsk = sb("dsk", [16, 32])
            SQr = sb("SQr", [64, NLV, 64]); SQi = sb("SQi", [64, NLV, 64]); SQin = sb("SQin", [64, NLV, 64])
            LTr = sb("LTr", [64, 8, 64, 16], BF16); LTi = sb("LTi", [64, 8, 64, 16], BF16)
            OTr = sb("OTr", [64, 8, 64, 16], BF16); OTn = sb("OTn", [64, 8, 64, 16], BF16)
            CRb = sb("CRb", [64, 64, 16], BF16); CInb = sb("CInb", [64, 64, 16], BF16)
            bt_ = Buf()
            es2, sb2, ps2_ = scope("s5t_")
            ones1 = sb2("ones1", [1, 64]); row = sb2("row", [1, 64])
            mk.op("pool", lambda E: E.memset(ones1[:], 1.0), (), [bt_])
            dma("sp", row[:], log_dt.rearrange("d g -> (d g)").rearrange("(o n) -> o n", o=1), (), [bt_])
            cp("dve", identb[:], ident[0:64, 0:64], [b_ident], [bt_])
            LR = sb2("LR", [64, 64]); LI = sb2("LI", [64, 64])
            dma("sp", LR[:].rearrange("p (d g) -> p d g", d=2), lam_re.rearrange("d g p -> p d g"), (), [bt_], slow=True)
            dma("act", LI[:].rearrange("p (d g) -> p d g", d=2), lam_im.rearrange("d g p -> p d g"), (), [bt_], slow=True)
            BR = sb2("BR", [64, 64, 16]); BI = sb2("BI", [64, 64, 16])
            dma("sp", BR[:].rearrange("p (d g) h -> p d g h", d=2), b_re.rearrange("d g p h -> p d g h"), (), [bt_])
            dma("act", BI[:].rearrange("p (d g) h -> p d g h", d=2), b_im.rearrange("d g p h -> p d g h"), (), [bt_])
            CR = sb2("CR", [64, 64, 16]); CI = sb2("CI", [64, 64, 16])
            cnat = sb2("cnat", [128, 8, 64])
            for (src, dst) in ((c_re, CR), (c_im, CI)):
                dma("sp", cnat[:], src.rearrange("d g h p -> (d g h) p").rearrange("(k q) p -> q k p", q=128), [bt_], [bt_])
                for k in range(8):
                    pq, bq = nextps()
                    mk.op("pe", lambda E, pq=pq, k=k: E.transpose(out=pq[0:64, 0:128], in_=cnat[:, k, :], identity=ident[:]),
                          [bt_, b_ident], [bq])
                    cp("dve", dst[:, k * 8:(k + 1) * 8, :], pq[0:64, 0:128].rearrange("p (g h) -> p g h", h=16), [bq], [bt_])
            dma("sp", dsk[:], d_skip.rearrange("(g h) -> h g", h=16), (), [bt_], slow=True)
            DT = sb2("DT", [64, 64])
            pq, bq = nextps()
            mm(pq[0:64, 0:64], ones1[:], row[:], True, True, [bt_], [bq])
            act(DT[:], pq[0:64, 0:64], AF.Exp, [bq], [bt_])

            def T64(name):
                return sb2(name, [64, 64])
            ZR = T64("ZR"); ZI = T64("ZI"); EPs = T64("EPs"); COS = T64("COS"); SIN = T64("SIN")
            tA = T64("tA"); tB = T64("tB"); tC = T64("tC"); tI = sb2("tI", [64, 64], mybir.dt.int32)
            tt("dve", ZR[:], LR[:], DT[:], ALU.mult, [bt_], [bt_])
            tt("dve", ZI[:], LI[:], DT[:], ALU.mult, [bt_], [bt_])
            act(EPs[:], ZR[:], AF.Exp, [bt_], [bt_])
            for (dst, offs) in ((SIN, 64.0), (COS, 64.25)):
                ts("dve", tA[:], ZI[:], 1.0 / TWO_PI, offs, ALU.mult, ALU.add, [bt_], [bt_])
                cp("dve", tI[:], tA[:], [bt_], [bt_])
                cp("dve", tB[:], tI[:], [bt_], [bt_])
                tt("dve", tA[:], tA[:], tB[:], ALU.subtract, [bt_], [bt_])
                ts("dve", tB[:], tA[:], 0.5, None, ALU.is_gt, None, [bt_], [bt_])
                tt("dve", tA[:], tA[:], tB[:], ALU.subtract, [bt_], [bt_])
                act(dst[:], tA[:], AF.Sin, [bt_], [bt_], scale=TWO_PI)
            PWr = sb2("PWr", [64, 9, 64]); PWi = sb2("PWi", [64, 9, 64])
            mk.op("pool", lambda E: E.memset(PWr[:, 0, :], 1.0), (), [bt_])
            mk.op("pool", lambda E: E.memset(PWi[:, 0, :], 0.0), (), [bt_])
            tt("dve", PWr[:, 1, :], EPs[:], COS[:], ALU.mult, [bt_], [bt_])
            tt("dve", PWi[:, 1, :], EPs[:], SIN[:], ALU.mult, [bt_], [bt_])

            def cmul(or_, oi_, ar, ai, br, bi, n3=None):
                tt("dve", tA[:], ai, bi, ALU.mult, [bt_], [bt_])
                tt("dve", tB[:], ai, br, ALU.mult, [bt_], [bt_])
                tt("dve", tC[:], ar, br, ALU.mult, [bt_], [bt_])
                tt("dve", or_, tC[:], tA[:], ALU.subtract, [bt_], [bt_])
                tt("dve", tC[:], ar, bi, ALU.mult, [bt_], [bt_])
                tt("dve", oi_, tC[:], tB[:], ALU.add, [bt_], [bt_])
            for j in range(2, 9):
                cmul(PWr[:, j, :], PWi[:, j, :], PWr[:, j - 1, :], PWi[:, j - 1, :], PWr[:, 1, :], PWi[:, 1, :])
            NLV = 10
            cp("dve", SQr[:, 0, :], PWr[:, 8, :], [bt_], [bt_]); cp("dve", SQi[:, 0, :], PWi[:, 8, :], [bt_], [bt_])
            for k in range(1, NLV):
                cmul(SQr[:, k, :], SQi[:, k, :], SQr[:, k - 1, :], SQi[:, k - 1, :], SQr[:, k - 1, :], SQi[:, k - 1, :])
            ts("dve", SQin[:], SQi[:], -1.0, None, ALU.mult, None, [bt_], [bt_])
            CFr = T64("CFr"); CFi = T64("CFi"); DEN = T64("DEN"); NR = T64("NR")
            ts("dve", NR[:], PWr[:, 1, :], -1.0, None, ALU.add, None, [bt_], [bt_])
            tt("dve", tA[:], LR[:], LR[:], ALU.mult, [bt_], [bt_])
            tt("dve", tB[:], LI[:], LI[:], ALU.mult, [bt_], [bt_])
            tt("dve", DEN[:], tA[:], tB[:], ALU.add, [bt_], [bt_])
            mk.op("dve", lambda E: E.reciprocal(out=DEN[:], in_=DEN[:]), [bt_], [bt_])
            tt("dve", tA[:], NR[:], LR[:], ALU.mult, [bt_], [bt_])
            tt("dve", tB[:], PWi[:, 1, :], LI[:], ALU.mult, [bt_], [bt_])
            tt("dve", tA[:], tA[:], tB[:], ALU.add, [bt_], [bt_])
            tt("dve", CFr[:], tA[:], DEN[:], ALU.mult, [bt_], [bt_])
            tt("dve", tA[:], PWi[:, 1, :], LR[:], ALU.mult, [bt_], [bt_])
            tt("dve", tB[:], NR[:], LI[:], ALU.mult, [bt_], [bt_])
            tt("dve", tA[:], tA[:], tB[:], ALU.subtract, [bt_], [bt_])
            tt("dve", CFi[:], tA[:], DEN[:], ALU.mult, [bt_], [bt_])
            BbR = sb2("BbR", [64, 64, 16]); BbI = sb2("BbI", [64, 64, 16])
            X1t = sb2("X1t", [64, 64, 16]); X2t = sb2("X2t", [64, 64, 16])

            def b16(t2):
                return t2[:, :, None].broadcast_to([64, 64, 16])

            def cmul3(or_, oi_neg, ar2, ai2, br3, bi3, e1="dve", e2="pool"):
                tt(e1, X1t[:], br3, b16(ar2), ALU.mult, [bt_], [bt_])
                tt(e1, X2t[:], bi3, b16(ai2), ALU.mult, [bt_], [bt_])
                tt(e1, or_, X1t[:], X2t[:], ALU.subtract, [bt_], [bt_])
                tt(e1, X1t[:], bi3, b16(ar2), ALU.mult, [bt_], [bt_])
                tt(e1, X2t[:], br3, b16(ai2), ALU.mult, [bt_], [bt_])
                if oi_neg[1]:
                    tt(e1, X1t[:], X1t[:], X2t[:], ALU.add, [bt_], [bt_])
                    ts(e1, oi_neg[0], X1t[:], -1.0, None, ALU.mult, None, [bt_], [bt_])
                else:
                    tt(e1, oi_neg[0], X1t[:], X2t[:], ALU.add, [bt_], [bt_])
            cmul3(BbR[:], (BbI[:], False), CFr[:], CFi[:], BR[:], BI[:])
            for j in range(8):
                cmul3(LTr[:, j], (LTi[:, j], False), PWr[:, j, :], PWi[:, j, :], BbR[:], BbI[:])
                cmul3(OTr[:, j], (OTn[:, j], True), PWr[:, j + 1, :], PWi[:, j + 1, :], CR[:], CI[:])
            cp("dve", CRb[:], CR[:], [bt_], [bt_]); ts("dve", CInb[:], CI[:], -1.0, None, ALU.mult, None, [bt_], [bt_])

            mk.flush(final=True)
            es2.close()
            KTg = [sb("KTg%d" % i, [16, 15, 16], BF16) for i in range(2)]; b_KTg = [Buf(), Buf()]
            CTg = [sb("CTg%d" % i, [16, 32, 64], BF16) for i in range(2)]; b_CTg = [Buf(), Buf()]
            UGN = 2048
            ug = sb("ug", [16, UGN]); b_ug = Buf()
            ub = [sb("ub%d" % i, [16, 8, 1024], BF16) for i in range(2)]; b_ub = [Buf(), Buf()]
            yg = sb("yg", [16, 4096]); b_yg = Buf()
            NBM = 1024
            Wt = [[[sb("W%d%d%d" % (pp, d, c), [64, NBM + 1]) for c in range(2)] for d in range(2)] for pp in range(2)]
            b_Wt = [[Buf() for d in range(2)] for pp in range(2)]
            Sb = [[[sb("Sb%d%d%d" % (i, d, c), [64, NBM + 1], BF16) for c in range(2)] for d in range(2)] for i in range(2)]
            b_Sb = [Buf(), Buf()]
            for pp in range(2):
                for d in range(2):
                    for c in range(2):
                        mk.op("pool", lambda E, t=Wt[pp][d][c]: E.memset(t[:], 0.0), (), [b_Wt[pp][d]])
            id16 = ident[0:16, 0:16]

            def gconsts(g):
                par = g % 2
                pk, bpk = nextps()
                for idx in range(15):
                    if idx == 0:
                        terms = [(0, 0), (1, 0)]
                    elif idx < 8:
                        terms = [(0, idx)]
                    else:
                        terms = [(1, idx - 7)]
                    n = 0
                    for (d, tau) in terms:
                        gi = d * 32 + g
                        mm(pk[0:16, idx * 16:(idx + 1) * 16], LTr[:, tau, gi, :], CRb[:, gi, :], n == 0, False, [bt_], [bpk]); n += 1
                        mm(pk[0:16, idx * 16:(idx + 1) * 16], LTi[:, tau, gi, :], CInb[:, gi, :], False, n == 2 * len(terms) - 1, [bt_], [bpk]); n += 1
                cp("dve", KTg[par][:].rearrange("p a b -> p (a b)"), pk[0:16, 0:240], [bpk], [b_KTg[par]])
                stt(KTg[par][:, 0, :], id16, dsk[:, g:g + 1], KTg[par][:, 0, :], ALU.mult, ALU.add, [b_KTg[par], bt_, b_ident], [b_KTg[par]])
                for q in range(4):
                    pc_, bpc = nextps()
                    for j in range(8):
                        i = q * 8 + j
                        d = i // 16; s_ = (i // 2) % 8; c = i % 2
                        e_ = (7 - s_) if d == 0 else s_
                        src = (LTr if c == 0 else LTi)[:, e_, d * 32 + g, :]
                        mm(pc_[0:16, j * 64:(j + 1) * 64], src, identb[:], True, True, [bt_], [bpc])
                    cp("act", CTg[par][:, q * 8:(q + 1) * 8, :].rearrange("p a b -> p (a b)"),
                       pc_[0:16, 0:512], [bpc], [b_CTg[par]])

            def dims(s):
                toff, coff, T = SEQ[s]
                nblk = T // 8
                BW = min(512, nblk)
                return toff, coff, T, nblk, BW, 8 * BW, nblk // BW

            def front(g, s, st):
                par = g % 2
                toff, coff, T, nblk, BW, TW, nbt = dims(s)
                nlv = int(math.log2(nblk))
                UG = min(UGN, T)
                for hf in range(T // UG):
                    dma("sp", ug[:, 0:UG], Pscr[1920 + 16 * g:1936 + 16 * g, coff + 1 + hf * UG:coff + 1 + (hf + 1) * UG],
                        [b_P], [b_ug])
                    cp("act", ub[st][:, :, hf * (UG // 8):(hf + 1) * (UG // 8)], ug[:, 0:UG].rearrange("p (b s) -> p s b", s=8),
                       [b_ug], [b_ub[st]])
                if nblk < NBM:
                    for d in range(2):
                        for c in range(2):
                            mk.op("pool", lambda E, t=Wt[0][d][c]: E.memset(t[:], 0.0), (), [b_Wt[0][d]])
                            mk.op("pool", lambda E, t=Wt[1][d][c]: E.memset(t[:], 0.0), (), [b_Wt[1][d]])
                for bt in range(nbt):
                    for d in range(2):
                        for c in range(2):
                            pw_, bpw = nextps()
                            for s_ in range(8):
                                mm(pw_[0:64, 0:BW], CTg[par][:, (d * 8 + s_) * 2 + c, :],
                                   ub[st][:, s_, bt * BW:(bt + 1) * BW],
                                   s_ == 0, s_ == 7, [b_CTg[par], b_ub[st]], [bpw])
                            o0 = bt * BW + (1 if d == 0 else 0)
                            cp("act", Wt[0][d][c][:, o0:o0 + BW], pw_[0:64, 0:BW], [bpw], [b_Wt[0][d]])
                cur = 0
                for k in range(nlv):
                    sh = 1 << k
                    n_ = nblk - sh
                    for d in range(2):
                        gi = d * 32 + g
                        lo = 1 if d == 0 else 0
                        Wc = Wt[cur][d]; Wn = Wt[1 - cur][d]
                        bWc = b_Wt[cur][d]; bWn = b_Wt[1 - cur][d]
                        if d == 0:
                            dst = slice(lo + sh, lo + nblk); srcs = slice(lo, lo + n_); keep = slice(lo, lo + sh)
                        else:
                            dst = slice(lo, lo + n_); srcs = slice(lo + sh, lo + nblk); keep = slice(lo + n_, lo + nblk)
                        ar = SQr[:, k, gi:gi + 1]; ai = SQi[:, k, gi:gi + 1]; ain = SQin[:, k, gi:gi + 1]
                        stt(Wn[0][:, dst], Wc[0][:, srcs], ar, Wc[0][:, dst], ALU.mult, ALU.add, [bWc, bt_], [bWn])
                        stt(Wn[1][:, dst], Wc[1][:, srcs], ar, Wc[1][:, dst], ALU.mult, ALU.add, [bWc, bt_], [bWn])
                        stt(Wn[0][:, dst], Wc[1][:, srcs], ain, Wn[0][:, dst], ALU.mult, ALU.add, [bWc, bWn, bt_], [bWn])
                        stt(Wn[1][:, dst], Wc[0][:, srcs], ai, Wn[1][:, dst], ALU.mult, ALU.add, [bWc, bWn, bt_], [bWn])
                        cp("pool", Wn[0][:, keep], Wc[0][:, keep], [bWc], [bWn])
                        cp("pool", Wn[1][:, keep], Wc[1][:, keep], [bWc], [bWn])
                    cur = 1 - cur
                for d in range(2):
                    for c in range(2):
                        if d == 0:
                            cp("act", Sb[st][d][c][:, 1:nblk + 1], Wt[cur][d][c][:, 1:nblk + 1], [b_Wt[cur][d]], [b_Sb[st]])
                            mk.op("pool", lambda E, t=Sb[st][d][c]: E.memset(t[:, 0:1], 0.0), (), [b_Sb[st]])
                        else:
                            cp("act", Sb[st][d][c][:, 0:nblk], Wt[cur][d][c][:, 0:nblk], [b_Wt[cur][d]], [b_Sb[st]])
                            mk.op("pool", lambda E, t=Sb[st][d][c], nblk=nblk: E.memset(t[:, nblk:nblk + 1], 0.0), (), [b_Sb[st]])

            def back(g, s, st):
                par = g % 2
                toff, coff, T, nblk, BW, TW, nbt = dims(s)
                for bt in range(nbt):
                    ubv = ub[st][:, :, bt * BW:(bt + 1) * BW]
                    ygv = yg[:, 0:TW].rearrange("p (b s) -> p s b", s=8)
                    for t in range(8):
                        py, bpy = nextps()
                        for s_ in range(8):
                            idx = 0 if s_ == t else ((t - s_) if s_ < t else (7 + s_ - t))
                            mm(py[0:16, 0:BW], KTg[par][:, idx, :], ubv[:, s_, :], s_ == 0, False, [b_KTg[par], b_ub[st]], [bpy])
                        b0 = bt * BW
                        mm(py[0:16, 0:BW], OTr[:, t, g, :], Sb[st][0][0][:, b0:b0 + BW], False, False, [bt_, b_Sb[st]], [bpy])
                        mm(py[0:16, 0:BW], OTn[:, t, g, :], Sb[st][0][1][:, b0:b0 + BW], False, False, [bt_, b_Sb[st]], [bpy])
                        mm(py[0:16, 0:BW], OTr[:, 7 - t, 32 + g, :], Sb[st][1][0][:, b0 + 1:b0 + 1 + BW], False, False, [bt_, b_Sb[st]], [bpy])
                        mm(py[0:16, 0:BW], OTn[:, 7 - t, 32 + g, :], Sb[st][1][1][:, b0 + 1:b0 + 1 + BW], False, True, [bt_, b_Sb[st]], [bpy])
                        cp("act", ygv[:, t, :], py[0:16, 0:BW], [bpy], [b_yg])
                    dma("sp", YS[16 * g:16 * g + 16, toff + bt * TW:toff + (bt + 1) * TW], yg[:, 0:TW], [b_yg], [b_YS])

            units = [(g, s) for g in range(32) for s in range(NS)]
            prev = None
            for ui, (g, s) in enumerate(units):
                if s == 0:
                    gconsts(g)
                front(g, s, ui % 2)
                if prev is not None:
                    back(prev[0], prev[1], (ui - 1) % 2)
                prev = (g, s)
            back(prev[0], prev[1], (len(units) - 1) % 2)
            mk.flush(final=True)

    if upto >= 2:
        s5_pass()

    def mix_pass():
        es, sb, ps = scope("mx_")
        with es:
            N = 256
            pz = [ps("pz%d" % i, [128, 512]) for i in range(8)]
            b_pz = [Buf() for _ in range(8)]
            pctr = [0]

            def nextps():
                i = pctr[0] % 8
                pctr[0] += 1
                return pz[i], b_pz[i]
            bc_ = Buf()
            o64 = sb("o64", [64, 64])
            mk.op("pool", lambda E: E.memset(o64[:], 1.0 / 64.0), (), [bc_])
            ones_bf = sb("ones_bf", [128, 128], BF16)
            mk.op("pool", lambda E: E.memset(ones_bf[:], 1.0), (), [bc_])
            lg = sb("lg", [64, 8]); lb = sb("lb", [64, 8])
            dma("sp", lg[:], lnx_g.rearrange("(h p) -> p h", p=64), (), [bc_], slow=True)
            dma("sp", lb[:], lnx_b.rearrange("(h p) -> p h", p=64), (), [bc_], slow=True)
            g2t = sb("g2t", [128, 512]); dma("sp", g2t[:], g2, (), [bc_])
            mug = sb("mug", [128, 1]); hmg = sb("hmg", [128, 1]); omg = sb("omg", [128, 1])
            dma("sp", mug[:], mu_shift[1792:1920].rearrange("(p o) -> p o", o=1), (), [bc_], slow=True)
            ts("dve", hmg[:], mug[:], 0.5, None, ALU.mult, None, [bc_], [bc_])
            ts("dve", omg[:], mug[:], -1.0, 1.0, ALU.mult, ALU.add, [bc_], [bc_])
            bgl = sb("bgl", [128, 4]); s5g = sb("s5g", [128, 4])
            dma("sp", bgl[:], b_glu.rearrange("(q p) -> p q", p=128), (), [bc_], slow=True)
            dma("sp", s5g[:], s5_out_g.rearrange("(q p) -> p q", p=128), (), [bc_], slow=True)
            wst = sb("wst", [128, 4, 128])
            mk.op("pool", lambda E: E.memset(wst[:], 0.0), (), [bc_])
            for g in range(32):
                r0 = (g % 8) * 16
                dma("sp" if g % 2 == 0 else "act", wst[r0:r0 + 16, g // 8, r0:r0 + 16], w_glu[g], [bc_], [bc_])
            Wbd = sb("Wbd", [128, 4, 128], BF16)
            cp("dve", Wbd[:], wst[:], [bc_], [bc_])
            wo_r = sb("wo_r", [64, 8, D], BF16); wo_s = sb("wo_s", [128, 4, D], BF16)
            stg = [sb("stg%d" % i, [128, D]) for i in range(2)]; b_stg = [Buf(), Buf()]
            for h in range(8):
                st_ = stg[h % 2]; bs_ = b_stg[h % 2]
                dma("sp", st_[0:64, :], w_out[h * 64:(h + 1) * 64, :], (), [bs_])
                cp("pool", wo_r[:, h, :], st_[0:64, :], [bs_], [bc_])
            for q in range(4):
                st_ = stg[q % 2]; bs_ = b_stg[q % 2]
                dma("sp", st_[:], w_out[512 + q * 128:512 + (q + 1) * 128, :], (), [bs_])
                cp("pool", wo_s[:, q, :], st_[:], [bs_], [bc_])

            def T4(name, dt=F32):
                return sb(name, [64, 8, N], dt), Buf()
            YF, b_YF = T4("YF"); YB, b_YB = T4("YB"); BF_, b_BF = T4("BF"); BB, b_BB = T4("BB")
            Ym, b_Ym = T4("Ym"); SQt, b_SQt = T4("SQt"); RS, b_RS = T4("RS")
            yr, b_yr = T4("yr", BF16)
            XG = sb("XG", [128, N + 2]); b_XG = Buf(); tw = sb("tw", [128, N]); b_tw = Buf()
            sg = sb("sg", [128, N]); b_sg = Buf()
            S5 = sb("S5", [128, 4, N]); b_S5 = Buf(); Z1 = sb("Z1", [128, 4, N]); b_Z1 = Buf()
            Z2 = sb("Z2", [128, 4, N]); b_Z2 = Buf(); Zb = sb("Zb", [128, 4, N], BF16); b_Zb = Buf()
            r2 = sb("r2", [128, N]); b_r2 = Buf()
            ysb = sb("ysb", [128, 4, N], BF16); b_ysb = Buf()
            xTt = sb("xTt", [128, 8, N]); b_xTt = Buf(); X1t = sb("X1t", [128, 8, N]); b_X1t = Buf()

            def bc(t, n):
                return t[:, :, None].broadcast_to([64, 8, n])

            def fl(t):
                return t[:].rearrange("p h t -> p (h t)")
            for s, (toff, coff, T) in enumerate(SEQ):
                for ti in range(T // N):
                    tl = ti * N; tg = toff + tl; c0 = coff + tl
                    dma("sp", YF[:], YD[0, :, :, tg:tg + N], [b_YD], [b_YF])
                    dma("act", YB[:], YD[1, :, :, tg:tg + N], [b_YD], [b_YB])
                    dma("sp", BF_[:], BD[0, :, :, tg:tg + N], [b_BD], [b_BF])
                    dma("act", BB[:], BD[1, :, :, tg:tg + N], [b_BD], [b_BB])
                    dma("sp", XG[:], Pscr[1792:1920, c0:c0 + N + 2], [b_P], [b_XG])
                    dma("act", S5[:], YS[:, tg:tg + N].rearrange("(q p) t -> p q t", p=128), [b_YS], [b_S5])
                    dma("sp", xTt[:], XT[:, tg:tg + N].rearrange("(k p) t -> p k t", p=128), [b_XT], [b_xTt])
                    tt("pool", Ym[:], YF[:], YB[:], ALU.add, [b_YF, b_YB], [b_Ym])
                    for j in range(4):
                        pm_, bpm = nextps()
                        mm(pm_[0:64, :], o64[:], fl(Ym)[:, j * 512:(j + 1) * 512], True, True, [bc_, b_Ym], [bpm])
                        tt("dve", fl(YF)[:, j * 512:(j + 1) * 512], fl(Ym)[:, j * 512:(j + 1) * 512], pm_[0:64, :],
                           ALU.subtract, [bpm, b_Ym], [b_YF])
                    tt("pool", SQt[:], YF[:], YF[:], ALU.mult, [b_YF], [b_SQt])
                    for j in range(4):
                        pm_, bpm = nextps()
                        mm(pm_[0:64, :], o64[:], fl(SQt)[:, j * 512:(j + 1) * 512], True, True, [bc_, b_SQt], [bpm])
                        act(fl(RS)[:, j * 512:(j + 1) * 512], pm_[0:64, :], AF.Sqrt, [bpm], [b_RS], bias=LNX_EPS)
                    mk.op("dve", lambda E: E.reciprocal(out=RS[:], in_=RS[:]), [b_RS], [b_RS])
                    tt("pool", Ym[:], YF[:], RS[:], ALU.mult, [b_YF, b_RS], [b_Ym])
                    tt("pool", Ym[:], Ym[:], bc(lg, N), ALU.mult, [b_Ym, bc_], [b_Ym])
                    tt("pool", Ym[:], Ym[:], bc(lb, N), ALU.add, [b_Ym, bc_], [b_Ym])
                    tt("pool", Ym[:], Ym[:], BF_[:], ALU.add, [b_Ym, b_BF], [b_Ym])
                    tt("pool", Ym[:], Ym[:], BB[:], ALU.add, [b_Ym, b_BB], [b_Ym])
                    tt("dve", tw[:], XG[:, 0:N], XG[:, 2:N + 2], ALU.add, [b_XG], [b_tw])
                    ts("dve", tw[:], tw[:], hmg[:, 0:1], None, ALU.mult, None, [b_tw, bc_], [b_tw])
                    stt(sg[:], XG[:, 1:N + 1], omg[:, 0:1], tw[:], ALU.mult, ALU.add, [b_XG, b_tw, bc_], [b_sg])
                    act(sg[:], sg[:], AF.Sigmoid, [b_sg], [b_sg])
                    for h2_ in range(4):
                        pm_, bpm = nextps()
                        for j in range(2):
                            h = 2 * h2_ + j
                            mm(pm_[0:64, j * N:(j + 1) * N], g2t[:, h * 64:(h + 1) * 64], sg[:], True, True, [bc_, b_sg], [bpm])
                        tt("dve", yr[:, 2 * h2_:2 * h2_ + 2, :], pm_[0:64, :].rearrange("p (h n) -> p h n", n=N),
                           Ym[:, 2 * h2_:2 * h2_ + 2, :], ALU.mult, [bpm, b_Ym], [b_yr])
                    K0 = 2.0 * math.sqrt(2.0 / math.pi)
                    tt("pool", Z1[:], S5[:], S5[:], ALU.mult, [b_S5], [b_Z1])
                    ts("dve", Z1[:], Z1[:], 0.044715, 1.0, ALU.mult, ALU.add, [b_Z1], [b_Z1])
                    tt("pool", Z1[:], Z1[:], S5[:], ALU.mult, [b_Z1, b_S5], [b_Z1])
                    act(Z1[:], Z1[:], AF.Sigmoid, [b_Z1], [b_Z1], scale=K0)
                    tt("pool", Z1[:], Z1[:], S5[:], ALU.mult, [b_Z1, b_S5], [b_Z1])
                    cp("dve", Zb[:], Z1[:], [b_Z1], [b_Zb])
                    for q in range(4):
                        pm_, bpm = nextps()
                        mm(pm_[:, 0:N], Wbd[:, q, :], Zb[:, q, :], True, True, [bc_, b_Zb], [bpm])
                        mk.op("act", lambda E, pm_=pm_, q=q: E.activation(out=Z2[:, q, :], in_=pm_[:, 0:N], func=AF.Sigmoid,
                                                                        bias=bgl[:, q:q + 1], scale=1.0), [bpm, bc_], [b_Z2])
                    tt("pool", Z2[:], Z2[:], Z1[:], ALU.mult, [b_Z2, b_Z1], [b_Z2])
                    tt("pool", Zb[:], Z2[:], Z2[:], ALU.mult, [b_Z2], [b_Zb])
                    pm_, bpm = nextps()
                    for q in range(4):
                        mm(pm_[:, 0:N], ones_bf[:], Zb[:, q, :], q == 0, q == 3, [bc_, b_Zb], [bpm])
                    act(r2[:], pm_[:, 0:N], AF.Sqrt, [bpm], [b_r2], bias=RMS_EPS, scale=1.0 / 512.0)
                    mk.op("dve", lambda E: E.reciprocal(out=r2[:], in_=r2[:]), [b_r2], [b_r2])
                    for q in range(4):
                        stt(ysb[:, q, :], Z2[:, q, :], s5g[:, q:q + 1], r2[:], ALU.mult, ALU.mult, [b_Z2, b_r2, bc_], [b_ysb])
                    for dm in range(8):
                        pm_, bpm = nextps()
                        for h in range(8):
                            mm(pm_[:, 0:N], wo_r[:, h, dm * 128:(dm + 1) * 128], yr[:, h, :], h == 0, False, [bc_, b_yr], [bpm])
                        for q in range(4):
                            mm(pm_[:, 0:N], wo_s[:, q, dm * 128:(dm + 1) * 128], ysb[:, q, :], False, q == 3, [bc_, b_ysb], [bpm])
                        stt(X1t[:, dm, :], pm_[:, 0:N], modT[:, 16 + dm, s:s + 1], xTt[:, dm, :], ALU.mult, ALU.add,
                            [bpm, b_mod, b_xTt], [b_X1t])
                    dma("sp", X1[:, tg:tg + N].rearrange("(k p) t -> p k t", p=128), X1t[:], [b_X1t], [b_X1])
            mk.flush(final=True)

    if upto >= 3:
        mix_pass()

    def ffn_pass():
        es, sb, ps = scope("ff_")
        with es:
            N = 256
            pz = [ps("pz%d" % i, [128, 512]) for i in range(8)]
            b_pz = [Buf() for _ in range(8)]
            pctr = [0]

            def nextps():
                i = pctr[0] % 8
                pctr[0] += 1
                return pz[i], b_pz[i]
            bc_ = Buf()
            ones_bf = sb("ones_bf", [128, 128], BF16)
            mk.op("pool", lambda E: E.memset(ones_bf[:], 1.0), (), [bc_])
            n2g = sb("n2g", [128, 8]); fg = sb("fg", [128, 8]); sc2 = sb("sc2", [128, 8, NS])
            dma("sp", n2g[:], norm2_g.rearrange("(k p) -> p k", p=128), (), [bc_], slow=True)
            dma("sp", fg[:], final_g.rearrange("(k p) -> p k", p=128), (), [bc_], slow=True)
            for s in range(NS):
                stt(sc2[:, :, s], modT[:, 32:40, s], 1.0, n2g[:], ALU.add, ALU.mult, [b_mod, bc_], [bc_])
            w1 = sb("w1", [128, 8, DFF], BF16); w3 = sb("w3", [128, 8, DFF], BF16); w2_ = sb("w2_", [128, 22, D], BF16)
            stg = [sb("stg%d" % i, [128, 1408]) for i in range(2)]; b_stg = [Buf(), Buf()]
            n = 0
            for (src, dst) in ((w_ff1, w1), (w_ff3, w3)):
                for k in range(8):
                    for hf in range(2):
                        st_ = stg[n % 2]; bs_ = b_stg[n % 2]; n += 1
                        dma("sp" if n % 2 else "act", st_[:], src[k * 128:(k + 1) * 128, hf * 1408:(hf + 1) * 1408], (), [bs_])
                        cp("pool" if n % 2 else "dve", dst[:, k, hf * 1408:(hf + 1) * 1408], st_[:], [bs_], [bc_])
            for k in range(22):
                st_ = stg[n % 2]; bs_ = b_stg[n % 2]; n += 1
                dma("sp" if n % 2 else "act", st_[:, 0:D], w_ff2[k * 128:(k + 1) * 128, :], (), [bs_])
                cp("pool" if n % 2 else "dve", w2_[:, k, :], st_[:, 0:D], [bs_], [bc_])
            X1t = sb("X1t", [128, 8, N]); b_X1t = Buf()
            sq = sb("sq", [128, 8, N], BF16); b_sq = Buf()
            rstd = sb("rstd", [128, N]); b_rstd = Buf(); tmp = sb("tmp", [128, N]); b_tmp = Buf()
            h2 = sb("h2", [128, 8, N], BF16); b_h2 = Buf()
            fm = sb("fm", [128, 22, N], BF16); b_fm = Buf()
            av = [sb("av%d" % i, [128, N]) for i in range(2)]; b_av = [Buf(), Buf()]
            X2t = sb("X2t", [128, 8, N]); b_X2t = Buf()
            ytm = sb("ytm", [128, 2, D]); b_ytm = Buf()

            def rms_bc(src, bsrc):
                for dc in range(8):
                    act(sq[:, dc, :], src[:, dc, :], AF.Square, [bsrc], [b_sq])
                pm_, bpm = nextps()
                for dc in range(8):
                    mm(pm_[:, 0:N], ones_bf[:], sq[:, dc, :], dc == 0, dc == 7, [bc_, b_sq], [bpm])
                act(rstd[:], pm_[:, 0:N], AF.Sqrt, [bpm], [b_rstd], bias=RMS_EPS, scale=1.0 / D)
                mk.op("dve", lambda E: E.reciprocal(out=rstd[:], in_=rstd[:]), [b_rstd], [b_rstd])
            for s, (toff, coff, T) in enumerate(SEQ):
                for ti in range(T // N):
                    tg = toff + ti * N
                    dma("sp", X1t[:], X1[:, tg:tg + N].rearrange("(k p) t -> p k t", p=128), [b_X1], [b_X1t])
                    rms_bc(X1t, b_X1t)
                    for dc in range(8):
                        tt("dve", tmp[:], X1t[:, dc, :], rstd[:], ALU.mult, [b_X1t, b_rstd], [b_tmp])
                        ts("dve", h2[:, dc, :], tmp[:], sc2[:, dc, s:s + 1], modT[:, 24 + dc, s:s + 1], ALU.mult, ALU.add,
                           [b_tmp, bc_, b_mod], [b_h2])
                    for mc in range(22):
                        p1, bp1 = nextps()
                        for dc in range(8):
                            mm(p1[:, 0:N], w1[:, dc, mc * 128:(mc + 1) * 128], h2[:, dc, :], dc == 0, dc == 7, [bc_, b_h2], [bp1])
                        p3, bp3 = nextps()
                        for dc in range(8):
                            mm(p3[:, 0:N], w3[:, dc, mc * 128:(mc + 1) * 128], h2[:, dc, :], dc == 0, dc == 7, [bc_, b_h2], [bp3])
                        a_ = av[mc % 2]; ba_ = b_av[mc % 2]
                        act(a_[:], p1[:, 0:N], AF.Silu, [bp1], [ba_])
                        tt("dve", fm[:, mc, :], a_[:], p3[:, 0:N], ALU.mult, [ba_, bp3], [b_fm])
                    for dm in range(8):
                        pm_, bpm = nextps()
                        for mc in range(22):
                            mm(pm_[:, 0:N], w2_[:, mc, dm * 128:(dm + 1) * 128], fm[:, mc, :], mc == 0, mc == 21, [bc_, b_fm], [bpm])
                        stt(X2t[:, dm, :], pm_[:, 0:N], modT[:, 40 + dm, s:s + 1], X1t[:, dm, :], ALU.mult, ALU.add,
                            [bpm, b_mod, b_X1t], [b_X2t])
                    rms_bc(X2t, b_X2t)
                    for dc in range(8):
                        stt(X2t[:, dc, :], X2t[:, dc, :], fg[:, dc:dc + 1], rstd[:], ALU.mult, ALU.mult,
                            [b_X2t, bc_, b_rstd], [b_X2t])
                    for j in range(N // 128):
                        for dq in range(2):
                            pm_, bpm = nextps()
                            for k in range(4):
                                dc = dq * 4 + k
                                mk.op("pe", lambda E, pm_=pm_, dc=dc, j=j, k=k: E.transpose(
                                    out=pm_[:, k * 128:(k + 1) * 128], in_=X2t[:, dc, j * 128:(j + 1) * 128], identity=ident[:]),
                                    [b_X2t, b_ident], [bpm])
                            cp("act" if dq == 0 else "dve", ytm[:, j, dq * 512:(dq + 1) * 512], pm_[:], [bpm], [b_ytm])
                    dma("sp", y_out[tg:tg + N, :].rearrange("(j p) d -> p j d", p=128), ytm[:], [b_ytm], [])
            mk.flush(final=True)

    if upto >= 4:
        ffn_pass()

    outer.close()
    return nc


def core_inputs(P, x, c):
    f = np.float32
    m = {"x": x, "c": c}
    for k in ("norm1_g", "w_ada", "b_ada", "w_in", "mu_shift", "w0", "w2", "a0", "a2", "g2", "k_k", "k_a",
              "lnx_g", "lnx_b", "lam_re", "lam_im", "log_dt", "b_re", "b_im", "c_re", "c_im", "w_glu",
              "s5_out_g", "w_out", "norm2_g", "w_ff1", "w_ff3", "w_ff2", "final_g"):
        m[k] = P[k]
    m["r_k"] = P["r_k"].reshape(512)
    m["d_skip"] = P["d_skip"].reshape(512)
    m["b_glu"] = P["b_glu"].reshape(512)
    return {k: np.ascontiguousarray(v, dtype=f) for k, v in m.items()}


_T_PROMPT = 8192
_T_SAMPLE = 4096


def kernel(**inputs):
    n = 8
    P = {}
    for k, v in inputs.items():
        if k in ("x_prompt", "x_sample", "c_prompt", "c_sample"):
            continue
        v = np.asarray(v)
        P[k] = v if k == "final_g" else v[0]
    xp = np.asarray(inputs["x_prompt"]); xs = np.asarray(inputs["x_sample"])
    cpr = np.asarray(inputs["c_prompt"]); cs = np.asarray(inputs["c_sample"])
    TS = [xp.shape[1], xs.shape[1]]
    nc = build_program(TS, upto=int(os.environ.get('KUPTO', '99')))
    in_maps = []
    for b in range(n):
        x = np.concatenate([xp[b], xs[b]], axis=0)
        c = np.stack([cpr[b], cs[b]], axis=0)
        in_maps.append(core_inputs(P, x, c))
    res = run_bass_kernel_spmd(nc, in_maps, core_ids=list(range(n)))
    yp = np.stack([res.results[b]["y"][:TS[0]] for b in range(n)], axis=0).astype(np.float32)
    ys = np.stack([res.results[b]["y"][TS[0]:] for b in range(n)], axis=0).astype(np.float32)
    return (yp, ys)
```

```python
import os
import math
import numpy as np
import concourse.bass as bass
import concourse.mybir as mybir
from concourse.bass_utils import run_bass_kernel_spmd

F32 = mybir.dt.float32
BF16 = mybir.dt.bfloat16
AF = mybir.ActivationFunctionType
ALU = mybir.AluOpType
AX = mybir.AxisListType

D = 1024
DFF = 2816
NPROJ = 2432
RW = 1920
NDS = 12
RMS_EPS = 1e-6
LNX_EPS = 64e-5


class Buf:
    __slots__ = ("w", "r")

    def __init__(self):
        self.w = None
        self.r = {}


class MK:
    BLK = {"pe": "tensor", "dve": "vector", "act": "scalar", "pool": "gpsimd", "sp": "sync"}

    def __init__(self, nc, same=True):
        self.nc = nc
        self.same = same
        self.names = ["pe", "dve", "act", "pool", "sp"]
        self.sem = {k: nc.alloc_semaphore(name="s_" + k) for k in self.names}
        self.cnt = {k: 0 for k in self.names}
        self.seen = {k: {} for k in self.names}
        self.prog = {k: [] for k in self.names}
        self.dsem = [nc.alloc_semaphore(name="d%d" % i) for i in range(NDS)]
        self.dcnt = [0] * NDS
        self.dnext = 0
        self.deferred = None

    def semof(self, key):
        if isinstance(key, tuple):
            return self.dsem[key[1]]
        return self.sem[key]

    def _deps(self, e, reads, writes):
        deps = {}

        def add(k, v):
            if deps.get(k, 0) < v:
                deps[k] = v

        for b in reads:
            if b.w:
                add(*b.w)
        for b in writes:
            if b.w:
                add(*b.w)
            for k, v in b.r.items():
                add(k, v)
        out = []
        for k, v in deps.items():
            if k == e and (e == "pe" or not self.same):
                continue
            if self.seen[e].get(k, 0) >= v:
                continue
            self.seen[e][k] = v
            out.append((k, v))
        return out

    def _mark(self, tok, reads, writes):
        k, v = tok
        for b in reads:
            if b.r.get(k, 0) < v:
                b.r[k] = v
        for b in writes:
            b.w = tok
            b.r = {}

    def op(self, e, fn, reads=(), writes=()):
        if self.deferred is not None:
            self.deferred.append((0, e, fn, reads, writes))
            return
        waits = self._deps(e, reads, writes)
        self.cnt[e] += 1
        tok = (e, self.cnt[e])
        self.prog[e].append((waits, fn, self.sem[e], 1))
        self._mark(tok, reads, writes)

    def replay(self, pending, n):
        keep = self.deferred
        self.deferred = None
        last = None
        cnt = 0
        while pending and (cnt < n or last == "pe"):
            kind, e, fn, reads, writes = pending.pop(0)
            (self.dma if kind else self.op)(e, fn, reads, writes)
            last = e if not kind else None
            cnt += 1
        self.deferred = keep

    def dma(self, q, fn, reads=(), writes=()):
        if self.deferred is not None:
            self.deferred.append((1, q, fn, reads, writes))
            return
        i = self.dnext
        self.dnext = (i + 1) % NDS
        key = ("d", i)
        waits = self._deps(q, reads, writes)
        if self.dcnt[i] > 0 and self.seen[q].get(key, 0) < self.dcnt[i]:
            waits.append((key, self.dcnt[i]))
            self.seen[q][key] = self.dcnt[i]
        self.dcnt[i] += 16
        tok = (key, self.dcnt[i])
        self.prog[q].append((waits, fn, self.dsem[i], 16))
        self._mark(tok, reads, writes)

    def flush(self, final=False):
        nc = self.nc
        fin = []
        for i in (range(NDS) if final else []):
            if self.dcnt[i] > 0:
                fin.append((("d", i), self.dcnt[i]))
        for k in (self.names if final else []):
            if k != "sp" and self.cnt[k] > 0:
                fin.append((k, self.cnt[k]))
        with nc.Block() as block:
            for e in self.names:
                prog = self.prog[e]
                extra = fin if e == "sp" else []

                def body(eng, prog=prog, extra=extra):
                    for waits, fn, sem, inc in prog:
                        for k, v in waits:
                            eng.wait_ge(self.semof(k), v)
                        fn(eng).then_inc(sem, inc)
                    for k, v in extra:
                        eng.wait_ge(self.semof(k), v)

                getattr(block, self.BLK[e])(body)
        self.prog = {k: [] for k in self.names}

    def emit(self):
        self.flush(final=True)


def build_program(TS, dbg=False, upto=99):
    import contextlib
    nc = bass.Bass("TRN2", target_bir_lowering=False)
    mk = MK(nc, same=(os.environ.get("MK_SAME", "1") == "1"))
    TT = sum(TS)
    NS = len(TS)
    WP = TT + 2 * NS
    SEQ = []
    o = 0
    for s, T in enumerate(TS):
        SEQ.append((o, o + 2 * s, T))
        o += T

    def din(name, shape):
        return nc.dram_tensor(name, list(shape), F32, kind="ExternalInput").ap()

    def dscr(name, shape):
        return nc.dram_tensor(name, list(shape), F32, kind=("ExternalOutput" if dbg else "Internal")).ap()

    x_in = din("x", (TT, D))
    c_in = din("c", (NS, D))
    norm1_g = din("norm1_g", (D,))
    w_ada = din("w_ada", (D, 6 * D))
    b_ada = din("b_ada", (6 * D,))
    w_in = din("w_in", (D, NPROJ))
    mu_shift = din("mu_shift", (RW,))
    w0 = din("w0", (2, 512)); w2 = din("w2", (2, 64, 512))
    a0 = din("a0", (2, 512)); a2 = din("a2", (2, 64, 512))
    g2 = din("g2", (128, 512))
    k_k = din("k_k", (512,)); k_a = din("k_a", (512,)); r_k = din("r_k", (512,))
    lnx_g = din("lnx_g", (512,)); lnx_b = din("lnx_b", (512,))
    lam_re = din("lam_re", (2, 32, 64)); lam_im = din("lam_im", (2, 32, 64)); log_dt = din("log_dt", (2, 32))
    b_re = din("b_re", (2, 32, 64, 16)); b_im = din("b_im", (2, 32, 64, 16))
    c_re = din("c_re", (2, 32, 16, 64)); c_im = din("c_im", (2, 32, 16, 64))
    d_skip = din("d_skip", (512,)); w_glu = din("w_glu", (32, 16, 16)); b_glu = din("b_glu", (512,))
    s5_out_g = din("s5_out_g", (512,))
    w_out = din("w_out", (D, D)); norm2_g = din("norm2_g", (D,))
    w_ff1 = din("w_ff1", (D, DFF)); w_ff3 = din("w_ff3", (D, DFF)); w_ff2 = din("w_ff2", (DFF, D))
    final_g = din("final_g", (D,))
    y_out = nc.dram_tensor("y", [TT, D], F32, kind="ExternalOutput").ap()

    Pscr = dscr("Pscr", (NPROJ, WP))
    XT = dscr("XT", (D, TT))
    YD = dscr("YD", (2, 64, 8, TT))
    BD = dscr("BD", (2, 64, 8, TT))
    YS = dscr("YS", (512, TT))
    X1 = dscr("X1", (D, TT))
    MODS = dscr("MODS", (128, 48 * NS))
    b_P = Buf(); b_XT = Buf(); b_YD = Buf(); b_BD = Buf(); b_YS = Buf(); b_X1 = Buf(); b_MODS = Buf()

    def tt(e, out, a, b, op, r, w):
        mk.op(e, lambda E: E.tensor_tensor(out=out, in0=a, in1=b, op=op), r, w)

    def ts(e, out, a, s1, s2, op0, op1, r, w):
        if op1 is None:
            mk.op(e, lambda E: E.tensor_scalar(out=out, in0=a, scalar1=s1, scalar2=None, op0=op0), r, w)
        else:
            mk.op(e, lambda E: E.tensor_scalar(out=out, in0=a, scalar1=s1, scalar2=s2, op0=op0, op1=op1), r, w)

    def stt(out, a, sc, b, op0, op1, r, w):
        mk.op("dve", lambda E: E.scalar_tensor_tensor(out=out, in0=a, scalar=sc, in1=b, op0=op0, op1=op1), r, w)

    def act(out, a, func, r, w, bias=0.0, scale=1.0):
        mk.op("act", lambda E: E.activation(out=out, in_=a, func=func, bias=bias, scale=scale), r, w)

    def cp(e, out, a, r, w):
        if e == "act":
            mk.op("act", lambda E: E.activation(out=out, in_=a, func=AF.Copy), r, w)
        else:
            mk.op(e, lambda E: E.tensor_copy(out=out, in_=a), r, w)

    def mm(out, lhsT, rhs, st, sp_, r, w):
        mk.op("pe", lambda E: E.matmul(out=out, lhsT=lhsT, rhs=rhs, start=st, stop=sp_), r, w)

    def dma(q, out, in_, r, w, slow=False):
        if slow:
            mk.dma(q, lambda E: E.dma_start(out=out, in_=in_, allow_slow_non_contiguous=True), r, w)
        else:
            mk.dma(q, lambda E: E.dma_start(out=out, in_=in_), r, w)

    def scope(pfx=""):
        es = contextlib.ExitStack()

        def sb(name, shape, dt=F32):
            return es.enter_context(nc.sbuf_tensor(pfx + name, list(shape), dt))

        def ps(name, shape, dt=F32):
            return es.enter_context(nc.psum_tensor(pfx + name, list(shape), dt))
        return es, sb, ps

    def consts(sb):
        ident = sb("ident", [128, 128]); b_ident = Buf()
        mk.op("pool", lambda E: E.memset(ident[:], 1.0), (), [b_ident])
        mk.op("pool", lambda E: E.affine_select(out=ident[:], in_=ident[:], pattern=[[-1, 128]],
                                                compare_op=ALU.is_equal, fill=0.0, base=0, channel_multiplier=1),
              [b_ident], [b_ident])
        return ident, b_ident
    outer, osb, ops_ = scope("o_")
    ident, b_ident = consts(osb)
    modT = osb("modT", [128, 48, NS]); b_mod = Buf()
    sc1 = osb("sc1", [128, 8, NS]); b_sc1 = Buf()

    def pass0():
        es, sb, ps = scope("p0_")
        with es:
            ones_bf = sb("ones_bf", [128, 128], BF16); b_ones = Buf()
            mk.op("pool", lambda E: E.memset(ones_bf[:], 1.0), (), [b_ones])
            cT = sb("cT", [128, 8, NS]); b_cT = Buf()
            scT = sb("scT", [128, 8, NS]); b_scT = Buf()
            for s in range(NS):
                dma("sp", cT[:, :, s], c_in[s].rearrange("(k p) -> p k", p=128), (), [b_cT], slow=True)
            act(scT[:], cT[:], AF.Silu, [b_cT], [b_scT])
            badaT = sb("badaT", [128, 48]); b_bada = Buf()
            dma("sp", badaT[:], b_ada.rearrange("(k p) -> p k", p=128), (), [b_bada], slow=True)
            g1T = sb("g1T", [128, 8]); b_g1 = Buf()
            dma("sp", g1T[:], norm1_g.rearrange("(k p) -> p k", p=128), (), [b_g1], slow=True)
            wada_t = [sb("wada%d" % i, [128, 8, 256]) for i in range(2)]
            b_wada = [Buf(), Buf()]
            ps_mod_full = ps("ps_mod", [128, 512]); b_psmod = Buf()
            ps_mod = ps_mod_full[:, 0:4 * NS].rearrange("p (a b) -> p a b", b=NS)
            for slab in range(24):
                wt = wada_t[slab % 2]; bw = b_wada[slab % 2]
                dma("sp" if slab % 2 == 0 else "act", wt[:],
                    w_ada[:, slab * 256:(slab + 1) * 256].rearrange("(k p) n -> p k n", p=128), (), [bw])
                for j in range(2):
                    for k in range(8):
                        mm(ps_mod[:, j, :], wt[:, k, j * 128:(j + 1) * 128], scT[:, k, :], k == 0, k == 7,
                           [bw, b_scT], [b_psmod])
                for s in range(NS):
                    tt("dve", modT[:, slab * 2:(slab + 1) * 2, s], ps_mod[:, 0:2, s],
                       badaT[:, slab * 2:(slab + 1) * 2], ALU.add, [b_psmod, b_bada], [b_mod])
            for s in range(NS):
                stt(sc1[:, :, s], modT[:, 8:16, s], 1.0, g1T[:], ALU.add, ALU.mult, [b_mod, b_g1], [b_sc1])

            w_in_bf = sb("w_in_bf", [128, 8, NPROJ], BF16); b_win = Buf()
            wst = [sb("wst%d" % i, [128, NPROJ]) for i in range(2)]; b_wst = [Buf(), Buf()]
            for k in range(8):
                dma("sp", wst[k % 2][:], w_in[k * 128:(k + 1) * 128, :], (), [b_wst[k % 2]])
                cp("pool", w_in_bf[:, k, :], wst[k % 2][:], [b_wst[k % 2]], [b_win])

            NT = 512
            xtm = [sb("xtm%d" % i, [128, 4, D]) for i in range(2)]; b_xtm = [Buf(), Buf()]
            xT = sb("xT", [128, 8, NT]); b_xT = Buf()
            sq = sb("sq", [128, 8, NT], BF16); b_sq = Buf()
            rstd = sb("rstd", [128, NT]); b_rstd = Buf()
            tmp = sb("tmp0", [128, NT]); b_tmp = Buf()
            hT = sb("hT", [128, 8, NT], BF16); b_hT = Buf()
            pev = [sb("pev%d" % i, [128, NT]) for i in range(3)]; b_pev = [Buf() for _ in range(3)]
            zcol = sb("zcol", [128, 1]); b_zcol = Buf()
            mk.op("pool", lambda E: E.memset(zcol[:], 0.0), (), [b_zcol])
            pst = [ps("pst%d" % i, [128, NT]) for i in range(4)]; b_pst = [Buf() for _ in range(4)]
            psm = [ps("psm%d" % i, [128, NT]) for i in range(3)]; b_psm = [Buf() for _ in range(3)]
            for s, (toff, coff, T) in enumerate(SEQ):
                for mc in range(19):
                    for cc in (coff, coff + T + 1):
                        dma("sp", Pscr[mc * 128:(mc + 1) * 128, cc:cc + 1], zcol[:], [b_zcol], [b_P], slow=True)
                for ti in range(T // NT):
                    t0 = toff + ti * NT
                    xt = xtm[ti % 2]; bx = b_xtm[ti % 2]
                    dma("sp", xt[:], x_in[t0:t0 + NT, :].rearrange("(j p) d -> p j d", p=128), (), [bx])
                    for dc in range(8):
                        pt = pst[dc % 4]; bp = b_pst[dc % 4]
                        for j in range(4):
                            mk.op("pe", lambda E, pt=pt, xt=xt, j=j, dc=dc: E.transpose(
                                out=pt[:, j * 128:(j + 1) * 128], in_=xt[:, j, dc * 128:(dc + 1) * 128],
                                identity=ident[:]), [bx, b_ident], [bp])
                        cp("dve", xT[:, dc, :], pt[:], [bp], [b_xT])
                        act(sq[:, dc, :], pt[:], AF.Square, [bp, b_xT], [b_sq])
                    dma("act", XT[:, t0:t0 + NT].rearrange("(k p) t -> p k t", p=128), xT[:], [b_xT], [b_XT])
                    pm = psm[0]; bpm = b_psm[0]
                    for dc in range(8):
                        mm(pm[:], ones_bf[:], sq[:, dc, :], dc == 0, dc == 7, [b_sq, b_ones], [bpm])
                    act(rstd[:], pm[:], AF.Sqrt, [bpm], [b_rstd], bias=RMS_EPS, scale=1.0 / D)
                    mk.op("dve", lambda E: E.reciprocal(out=rstd[:], in_=rstd[:]), [b_rstd], [b_rstd])
                    for dc in range(8):
                        tt("dve", tmp[:], xT[:, dc, :], rstd[:], ALU.mult, [b_xT, b_rstd], [b_tmp])
                        ts("dve", hT[:, dc, :], tmp[:], sc1[:, dc, s:s + 1], modT[:, dc, s:s + 1], ALU.mult, ALU.add,
                           [b_tmp, b_sc1, b_mod], [b_hT])
                    for mc in range(19):
                        i3 = mc % 3
                        pm = psm[i3]; bpm = b_psm[i3]
                        for dc in range(8):
                            mm(pm[:], w_in_bf[:, dc, mc * 128:(mc + 1) * 128], hT[:, dc, :], dc == 0, dc == 7,
                               [b_win, b_hT], [bpm])
                        pv = pev[i3]; bpv = b_pev[i3]
                        cp("act", pv[:], pm[:], [bpm], [bpv])
                        cc = coff + 1 + ti * NT
                        dma("sp", Pscr[mc * 128:(mc + 1) * 128, cc:cc + NT], pv[:], [bpv], [b_P])
            mk.flush(final=True)

    pass0()
    def rwkv_pass(d):
        rev = (d == 1)
        es, sb, ps = scope("rw%d_" % d)
        with es:
            NT2 = 128
            psr = [ps("psr%d" % i, [128, 2048]) for i in range(2)]
            b_psr = [Buf() for _ in range(2)]
            pctr = [0]

            def nextps():
                i = pctr[0] % 2
                pctr[0] += 1
                return psr[i], b_psr[i]

            def T4(name):
                return sb(name, [64, 8, NT2]), Buf()

            def ldp(name, src512):
                t = sb(name, [64, 8]); b = Buf()
                dma("sp", t[:], src512.rearrange("(h p) -> p h", p=64), (), [b], slow=True)
                return t, b

            mu3 = sb("mu3", [64, 24]); b_mu3 = Buf()
            dma("sp", mu3[:], mu_shift[0:1536].rearrange("(g p) -> p g", p=64), (), [b_mu3], slow=True)
            hm3 = sb("hm3", [64, 24]); om3 = sb("om3", [64, 24]); b_hm3 = Buf()
            ts("dve", hm3[:], mu3[:], 0.5, None, ALU.mult, None, [b_mu3], [b_hm3])
            ts("dve", om3[:], mu3[:], -1.0, 1.0, ALU.mult, ALU.add, [b_mu3], [b_hm3])
            muw = sb("muw", [64, 2]); b_muw = Buf()
            dma("sp", muw[:, 0:1], mu_shift[1536 + 64 * d:1600 + 64 * d].rearrange("(p o) -> p o", o=1), (), [b_muw], slow=True)
            dma("sp", muw[:, 1:2], mu_shift[1664 + 64 * d:1728 + 64 * d].rearrange("(p o) -> p o", o=1), (), [b_muw], slow=True)
            hmw = sb("hmw", [64, 2]); omw = sb("omw", [64, 2]); b_hmw = Buf()
            ts("dve", hmw[:], muw[:], 0.5, None, ALU.mult, None, [b_muw], [b_hmw])
            ts("dve", omw[:], muw[:], -1.0, 1.0, ALU.mult, ALU.add, [b_muw], [b_hmw])
            w0d, b_w0d = ldp("w0d", w0[d]); a0d, b_a0d = ldp("a0d", a0[d])
            kk_, b_kk_ = ldp("kk_", k_k); ka_, b_ka_ = ldp("ka_", k_a); rk_, b_rk_ = ldp("rk_", r_k)
            omka = sb("omka", [64, 8]); b_omka = Buf()
            ts("dve", omka[:], ka_[:], -1.0, 1.0, ALU.mult, ALU.add, [b_ka_], [b_omka])
            w2d = sb("w2d", [64, 512]); a2d = sb("a2d", [64, 512]); b_w2d = Buf()
            dma("sp", w2d[:], w2[d], (), [b_w2d]); dma("sp", a2d[:], a2[d], (), [b_w2d])
            ones64 = sb("ones64", [64, 64]); b_c = Buf()
            mk.op("pool", lambda E: E.memset(ones64[:], 1.0), (), [b_c])
            maskA = sb("maskA", [64, 128]); maskL = sb("maskL", [64, 64]); MS = sb("MS", [64, 8 * NT2])
            mk.op("pool", lambda E: E.memset(maskA[:], 1.0), (), [b_c])
            mk.op("pool", lambda E: E.memset(maskL[:], 1.0), (), [b_c])
            mk.op("pool", lambda E: E.memset(MS[:], 1.0), (), [b_c])
            zc_ = 63 if rev else 0
            mk.op("pool", lambda E: E.memset(MS[:].rearrange("p (a l) -> p a l", l=64)[:, :, zc_:zc_ + 1], 0.0), [b_c], [b_c])

            def asel(ap, upper, strict):
                pat = [[1, 64]] if upper else [[-1, 64]]
                cm = -1 if upper else 1
                mk.op("pool", lambda E: E.affine_select(out=ap, in_=ap, pattern=pat, compare_op=ALU.is_ge, fill=0.0,
                                                        base=(-1 if strict else 0), channel_multiplier=cm),
                      [b_c], [b_c])
            asel(maskA[:, 0:64], not rev, True)
            asel(maskA[:, 64:128], not rev, False)
            asel(maskL[:], rev, True)
            mA = maskA[:, None, :].broadcast_to([64, 8, 128])
            mL = maskL[:, None, :].broadcast_to([64, 8, 64])
            id64 = ident[0:64, 0:64]
            idbc = ident[0:64, None, 0:64].broadcast_to([64, 8, 64])

            Lr = [sb("Lq%d" % q, [64, 8, NT2 + 2]) for q in range(2)]; b_L = [Buf() for _ in range(2)]
            Lr.append(Lr[0]); b_L.append(b_L[0])
            XW = sb("XW", [64, NT2 + 2]); XA = sb("XA", [64, NT2 + 2]); b_XW = Buf(); b_XA = Buf()
            T1, b_T1 = T4("T1")
            SH = [T4("SH%d" % q) for q in range(2)]
            (Rp, b_Rp), (Kp, b_Kp) = SH
            tt0, cp0 = tt, cp
            tw = sb("tw", [64, NT2]); b_tw = Buf()
            xwp = sb("xwp", [64, NT2]); xap = sb("xap", [64, NT2]); b_xwp = Buf(); b_xap = Buf()
            XB, b_XB = T4("XB"); E2, b_E2 = T4("E2"); AD, b_AD = T4("AD"); KR, b_KR = T4("KR")
            SS, b_SS = T4("SS"); KD, b_KD = T4("KD"); AB, b_AB = T4("AB"); BON, b_BON = XB, b_XB
            G, b_G = T4("G"); D1, b_D1 = E2, b_E2; D2, b_D2 = AD, b_AD; EP, b_EP = XB, b_XB; EN, b_EN = SS, b_SS
            T2, b_T2 = SS, b_SS
            SD = F32
            SETS = []
            for i_ in range(2):
                st_ = []
                for nm, shp in (("AR", [64, 8, 2, 128]), ("KT", [64, 8, NT2]), ("BT", [64, 8, NT2]), ("KH", [64, 8, NT2]),
                                ("BH", [64, 8, NT2]), ("Vp", [64, 8, NT2]), ("GL", [64, 16])):
                    st_ += [sb("%s_%d" % (nm, i_), shp), Buf()]
                SETS.append(st_)
            YT, b_YT = T4("YT")
            MT1 = sb("MT1", [64, 2, 8, 128], SD); MT2 = sb("MT2", [64, 2, 8, 128], SD); b_MT1 = Buf(); b_MT2 = Buf()
            P0 = sb("P0", [64, 2, 8, 64], SD); b_P0 = Buf()
            PP = [sb("PP%d" % i, [64, 2, 16, 64], SD) for i in range(2)]; b_PP = [Buf(), Buf()]
            Zt = [sb("Zt%d" % i, [64, 2, 8, 128], SD) for i in range(2)]; b_Zt = [Buf() for _ in range(2)]
            VT = sb("VT", [64, 2, 8, 64], SD); BHt = sb("BHt", [64, 2, 8, 64], SD); KHt = sb("KHt", [64, 2, 8, 64], SD)
            QT = sb("QT", [64, 2, 8, 64]); MM = sb("MM", [64, 2, 8, 64]); DG = P0
            b_VT = Buf(); b_BHt = Buf(); b_KHt = Buf(); b_QT = Buf(); b_MM = Buf(); b_DG = b_P0
            STt = [sb("ST%d" % i, [64, 8, 64]) for i in range(2)]; b_ST = [Buf(), Buf()]
            mA16 = maskA[:, None, :].broadcast_to([64, 16, 128])
            mL16 = maskL[:, None, :].broadcast_to([64, 16, 64])
            idbc16 = ident[0:64, None, 0:64].broadcast_to([64, 16, 64])

            def f16(t):
                return t[:].rearrange("p c h n -> p (c h) n")

            def pv(p, lo, n):
                return p[0:64, lo:lo + 16 * n].rearrange("p (a n) -> p a n", n=n)

            def v3(p, n):
                return p[0:64, 0:8 * n].rearrange("p (h n) -> p h n", n=n)

            def bc(t, lo, hi, n):
                return t[:, lo:hi, None].broadcast_to([64, hi - lo, n])

            def c4(t):
                return t[:].rearrange("p h (c l) -> p h c l", l=64)

            for s, (toff, coff, T) in enumerate(SEQ):
                sti_ = [0]
                mk.op("pool", lambda E: E.memset(STt[0][:], 0.0), (), [b_ST[0]])
                ntile = T // NT2
                order = list(range(ntile - 1, -1, -1) if rev else range(ntile))

                def prep(ti, AR, b_AR, KT, b_KT, BT, b_BT, KH, b_KH, BH, b_BH, Vp, b_Vp, GL, b_GL):
                    ARb, b_ARb = AR, b_AR
                    tl = ti * NT2
                    c0 = coff + tl
                    tg = toff + tl
                    def ldq(q):
                        dma("sp" if q != 1 else "act", Lr[q][:],
                            Pscr[q * 512:(q + 1) * 512, c0:c0 + NT2 + 2].rearrange("(h p) t -> p h t", p=64),
                            [b_P], [b_L[q]])

                    def shq(q):
                        Lq = Lr[q]; S_, bS = (SH[q] if q < 2 else (Vp, b_Vp))
                        tt("pool", T1[:], Lq[:, :, 0:NT2], Lq[:, :, 2:NT2 + 2], ALU.add, [b_L[q]], [b_T1])
                        tt("pool", T1[:], T1[:], bc(hm3, 8 * q, 8 * q + 8, NT2), ALU.mult, [b_T1, b_hm3], [b_T1])
                        tt("pool", S_[:], Lq[:, :, 1:NT2 + 1], bc(om3, 8 * q, 8 * q + 8, NT2), ALU.mult,
                           [b_L[q], b_hm3], [bS])
                        tt("pool", S_[:], S_[:], T1[:], ALU.add, [bS, b_T1], [bS])
                    ldq(0); ldq(1)
                    dma("sp", XW[:], Pscr[1536 + 64 * d:1600 + 64 * d, c0:c0 + NT2 + 2], [b_P], [b_XW])
                    dma("act", XA[:], Pscr[1664 + 64 * d:1728 + 64 * d, c0:c0 + NT2 + 2], [b_P], [b_XA])
                    shq(0); ldq(2); shq(1); shq(2)
                    for (X_, bX, o_, bo, j) in ((XW, b_XW, xwp, b_xwp, 0), (XA, b_XA, xap, b_xap, 1)):
                        tt("dve", tw[:], X_[:, 0:NT2], X_[:, 2:NT2 + 2], ALU.add, [bX], [b_tw])
                        ts("dve", tw[:], tw[:], hmw[:, j:j + 1], None, ALU.mult, None, [b_tw, b_hmw], [b_tw])
                        stt(o_[:], X_[:, 1:NT2 + 1], omw[:, j:j + 1], tw[:], ALU.mult, ALU.add, [bX, b_hmw, b_tw], [bo])
                    act(xwp[:], xwp[:], AF.Tanh, [b_xwp], [b_xwp])
                    for hh in range(2):
                        pa, bpa = nextps()
                        for j in range(4):
                            h = 4 * hh + j
                            mm(pa[0:64, j * NT2:(j + 1) * NT2], w2d[:, h * 64:(h + 1) * 64], xwp[:], True, True,
                               [b_w2d, b_xwp], [bpa])
                        tt("dve", XB[:, 4 * hh:4 * hh + 4, :], pa[0:64, 0:4 * NT2].rearrange("p (h n) -> p h n", n=NT2),
                           bc(w0d, 4 * hh, 4 * hh + 4, NT2), ALU.add, [bpa, b_w0d], [b_XB])
                    act(XB[:], XB[:], AF.Exp, [b_XB], [b_XB], scale=-1.0)
                    act(XB[:], XB[:], AF.Ln, [b_XB], [b_XB], bias=1.0)
                    act(E2[:], XB[:], AF.Exp, [b_XB], [b_E2], bias=-0.5, scale=-1.0)
                    for hh in range(2):
                        pa, bpa = nextps()
                        for j in range(4):
                            h = 4 * hh + j
                            mm(pa[0:64, j * NT2:(j + 1) * NT2], a2d[:, h * 64:(h + 1) * 64], xap[:], True, True,
                               [b_w2d, b_xap], [bpa])
                        tt("dve", AD[:, 4 * hh:4 * hh + 4, :], pa[0:64, 0:4 * NT2].rearrange("p (h n) -> p h n", n=NT2),
                           bc(a0d, 4 * hh, 4 * hh + 4, NT2), ALU.add, [bpa, b_a0d], [b_AD])
                    act(AD[:], AD[:], AF.Sigmoid, [b_AD], [b_AD])
                    tt("pool", KR[:], Kp[:], bc(kk_, 0, 8, NT2), ALU.mult, [b_Kp, b_kk_], [b_KR])
                    tt("pool", T1[:], KR[:], KR[:], ALU.mult, [b_KR], [b_T1])
                    for hh in range(2):
                        pa, bpa = nextps()
                        for j in range(4):
                            h = 4 * hh + j
                            mm(pa[0:64, j * NT2:(j + 1) * NT2], ones64[:], T1[:, h, :], True, True, [b_c, b_T1], [bpa])
                        ts("dve", SS[:, 4 * hh:4 * hh + 4, :], pa[0:64, 0:4 * NT2].rearrange("p (h n) -> p h n", n=NT2),
                           1e-24, None, ALU.max, None, [bpa], [b_SS])
                    act(SS[:], SS[:], AF.Sqrt, [b_SS], [b_SS])
                    mk.op("dve", lambda E: E.reciprocal(out=SS[:], in_=SS[:]), [b_SS], [b_SS])
                    tt("pool", KR[:], KR[:], SS[:], ALU.mult, [b_KR, b_SS], [b_KR])
                    tt("pool", T2[:], AD[:], bc(ka_, 0, 8, NT2), ALU.mult, [b_AD, b_ka_], [b_T2])
                    tt("pool", T2[:], T2[:], bc(omka, 0, 8, NT2), ALU.add, [b_T2, b_omka], [b_T2])
                    tt("pool", KD[:], T2[:], Kp[:], ALU.mult, [b_T2, b_Kp], [b_KD])
                    tt("dve", AB[:], AD[:], KR[:], ALU.mult, [b_AD, b_KR], [b_AB])
                    tt("pool", T1[:], Rp[:], KD[:], ALU.mult, [b_Rp, b_KD], [b_T1])
                    tt("pool", T1[:], T1[:], bc(rk_, 0, 8, NT2), ALU.mult, [b_T1, b_rk_], [b_T1])
                    for hh in range(2):
                        pa, bpa = nextps()
                        for j in range(4):
                            h = 4 * hh + j
                            mm(pa[0:64, j * NT2:(j + 1) * NT2], ones64[:], T1[:, h, :], True, True, [b_c, b_T1], [bpa])
                        tt("dve", BON[:, 4 * hh:4 * hh + 4, :], pa[0:64, 0:4 * NT2].rearrange("p (h n) -> p h n", n=NT2),
                           Vp[:, 4 * hh:4 * hh + 4, :], ALU.mult, [bpa, b_Vp], [b_BON])
                    dma("sp", BD[d, :, :, tg:tg + NT2], BON[:], [b_BON], [b_BD])
                    E2f = E2[:].rearrange("p h t -> p (h t)"); Gf = G[:].rearrange("p h t -> p (h t)"); MSf = MS[:]
                    if rev:
                        E2f = E2f[:, ::-1]; Gf = Gf[:, ::-1]; MSf = MSf[:, ::-1]
                    mk.op("dve", lambda E, Gf=Gf, MSf=MSf, E2f=E2f: E.tensor_tensor_scan(
                        out=Gf, data0=MSf, data1=E2f, initial=0.0, op0=ALU.mult, op1=ALU.add), [b_E2, b_c], [b_G])
                    tt("pool", D1[:], G[:], E2[:], ALU.subtract, [b_G, b_E2], [b_D1])
                    Gv = G[:].rearrange("p h (c l) -> p (h c) l", l=64)
                    ti_ = 0 if rev else 63
                    totb = Gv[:, :, ti_:ti_ + 1].broadcast_to([64, 16, 64])
                    tt("pool", D2[:].rearrange("p h (c l) -> p (h c) l", l=64), Gv, totb, ALU.subtract, [b_G], [b_D2])
                    act(EP[:], G[:], AF.Exp, [b_G], [b_EP])
                    act(EN[:], G[:], AF.Exp, [b_G], [b_EN], scale=-1.0)
                    act(D1[:], D1[:], AF.Exp, [b_D1], [b_D1], scale=-1.0)
                    act(D2[:], D2[:], AF.Exp, [b_D2], [b_D2])
                    act(GL[:].rearrange("p (a o) -> p a o", o=1), Gv[:, :, ti_:ti_ + 1], AF.Exp, [b_G], [b_GL], scale=-1.0)
                    stt(AR[:, :, :, 0:64], c4(KR), -1.0, c4(D1), ALU.mult, ALU.mult, [b_KR, b_D1], [b_AR])
                    tt("pool", AR[:, :, :, 64:128], c4(Rp), c4(EN), ALU.mult, [b_Rp, b_EN], [b_AR])
                    tt("pool", KT[:], KD[:], EP[:], ALU.mult, [b_KD, b_EP], [b_KT])
                    tt("dve", BT[:], AB[:], EP[:], ALU.mult, [b_AB, b_EP], [b_BT])
                    tt("pool", KH[:], KD[:], D2[:], ALU.mult, [b_KD, b_D2], [b_KH])
                    tt("dve", BH[:], AB[:], D2[:], ALU.mult, [b_AB, b_D2], [b_BH])

                def chunk(ti, pend, AR, b_AR, KT, b_KT, BT, b_BT, KH, b_KH, BH, b_BH, Vp, b_Vp, GL, b_GL):
                    ARb, b_ARb = AR, b_AR
                    tg = toff + ti * NT2

                    def tt(*a):
                        tt0(*a)
                        mk.replay(pend, 2)

                    def cp(*a):
                        cp0(*a)
                        mk.replay(pend, 2)
                    CS = [slice(0, 64), slice(64, 128)]
                    p1, bp1 = nextps()
                    for c in range(2):
                        for h in range(8):
                            mm(p1[0:64, (c * 8 + h) * 128:(c * 8 + h + 1) * 128], BT[:, h, CS[c]], ARb[:, h, c, :], True, True,
                               [b_BT, b_ARb], [bp1])
                    tt("dve", f16(MT1), pv(p1, 0, 128), mA16, ALU.mult, [bp1, b_c], [b_MT1])
                    p2, bp2 = nextps()
                    for c in range(2):
                        for h in range(8):
                            mm(p2[0:64, (c * 8 + h) * 128:(c * 8 + h + 1) * 128], KT[:, h, CS[c]], ARb[:, h, c, :], True, True,
                               [b_KT, b_ARb], [bp2])
                    tt("dve", f16(MT2), pv(p2, 0, 128), mA16, ALU.mult, [bp2, b_c], [b_MT2])
                    p3, bp3 = nextps()
                    for c in range(2):
                        for h in range(8):
                            mm(p3[0:64, (c * 8 + h) * 64:(c * 8 + h + 1) * 64], ARb[:, h, c, 0:64], BT[:, h, CS[c]], True, True,
                               [b_ARb, b_BT], [bp3])
                    tt("dve", f16(P0), pv(p3, 0, 64), mL16, ALU.mult, [bp3, b_c], [b_P0])
                    Z0 = Zt[0]; bZ0 = b_Zt[0]
                    p4, bp4 = nextps()
                    for c in range(2):
                        for h in range(8):
                            mk.op("pe", lambda E, p4=p4, o=(c * 8 + h) * 64, a=AR[:, h, c, 0:64]: E.transpose(
                                out=p4[0:64, o:o + 64], in_=a, identity=id64), [b_AR, b_ident], [bp4])
                            mk.op("pe", lambda E, p4=p4, o=1024 + (c * 8 + h) * 64, a=Vp[:, h, CS[c]]: E.transpose(
                                out=p4[0:64, o:o + 64], in_=a, identity=id64), [b_Vp, b_ident], [bp4])
                    cp0("act", Z0[:, :, :, 0:64].rearrange("p c h n -> p (c h) n"), pv(p4, 0, 64), [bp4], [bZ0])
                    cp("act", f16(VT), pv(p4, 1024, 64), [bp4], [b_VT])
                    p5, bp5 = nextps()
                    for c in range(2):
                        for h in range(8):
                            mk.op("pe", lambda E, p5=p5, o=(c * 8 + h) * 64, a=BH[:, h, CS[c]]: E.transpose(
                                out=p5[0:64, o:o + 64], in_=a, identity=id64), [b_BH, b_ident], [bp5])
                            mk.op("pe", lambda E, p5=p5, o=1024 + (c * 8 + h) * 64, a=KH[:, h, CS[c]]: E.transpose(
                                out=p5[0:64, o:o + 64], in_=a, identity=id64), [b_KH, b_ident], [bp5])
                    cp0("dve", f16(BHt), pv(p5, 0, 64), [bp5], [b_BHt])
                    cp("dve", f16(KHt), pv(p5, 1024, 64), [bp5], [b_KHt])
                    p6, bp6 = nextps()
                    for c in range(2):
                        for h in range(8):
                            mm(p6[0:64, (c * 8 + h) * 64:(c * 8 + h + 1) * 64], MT2[:, c, h, 0:64], VT[:, c, h, :], True, True,
                               [b_MT2, b_VT], [bp6])
                    cp("act", Z0[:, :, :, 64:128].rearrange("p c h n -> p (c h) n"), pv(p6, 0, 64), [bp6], [bZ0])
                    zi = 0
                    Pv = lambda c, h: P0[:, c, h, :]
                    PTv = lambda c, h: MT1[:, c, h, 0:64]
                    bP = b_P0; bPT = b_MT1
                    for it in range(6):
                        Zc = Zt[zi]; bZc = b_Zt[zi]; Zn = Zt[1 - zi]; bZn = b_Zt[1 - zi]
                        PTc, Pc, bPTc, bPc = PTv, Pv, bPT, bP
                        if it < 5:
                            nx = it % 2
                            p8, bp8 = nextps()
                            for c in range(2):
                                for h in range(8):
                                    mm(p8[0:64, (c * 8 + h) * 64:(c * 8 + h + 1) * 64], PTc(c, h), Pc(c, h), True, True,
                                       [bPTc, bPc], [bp8])
                                    mm(p8[0:64, 1024 + (c * 8 + h) * 64:1024 + (c * 8 + h + 1) * 64], Pc(c, h), PTc(c, h), True, True,
                                       [bPTc, bPc], [bp8])
                            cp("act", PP[nx][:].rearrange("p k a n -> p (k a) n"),
                               p8[0:64, 0:2048].rearrange("p (a n) -> p a n", n=64), [bp8], [b_PP[nx]])
                            Pv = lambda c, h, nx=nx: PP[nx][:, 0, c * 8 + h, :]
                            PTv = lambda c, h, nx=nx: PP[nx][:, 1, c * 8 + h, :]
                            bP = b_PP[nx]; bPT = b_PP[nx]
                        p7, bp7 = nextps()
                        for c in range(2):
                            for h in range(8):
                                mm(p7[0:64, (c * 8 + h) * 128:(c * 8 + h + 1) * 128], PTc(c, h), Zc[:, c, h, :], True, True,
                                   [bPTc, bZc], [bp7])
                        tt("dve", f16(Zn), pv(p7, 0, 128), f16(Zc), ALU.add, [bp7, bZc], [bZn])
                        zi = 1 - zi
                    Zf = Zt[zi]; bZf = b_Zt[zi]
                    p9, bp9 = nextps()
                    for c in range(2):
                        for h in range(8):
                            mm(p9[0:64, (c * 8 + h) * 64:(c * 8 + h + 1) * 64], Zf[:, c, h, 0:64], MT1[:, c, h, 64:128], True, True,
                               [bZf, b_MT1], [bp9])
                            mm(p9[0:64, 1024 + (c * 8 + h) * 64:1024 + (c * 8 + h + 1) * 64], Zf[:, c, h, 0:64], BHt[:, c, h, :], True, True,
                               [bZf, b_BHt], [bp9])
                    tt0("dve", QT[:], p9[0:64, 0:1024].rearrange("p (c h n) -> p c h n", c=2, h=8),
                       AR[:, :, :, 64:128].rearrange("p h c n -> p c h n"), ALU.add, [bp9, b_AR], [b_QT])
                    GLc = GL[:].rearrange("p (h c) -> p c h", c=2)[:, :, :, None].broadcast_to([64, 2, 8, 64])
                    tt0("pool", DG[:], ident[0:64, None, None, 0:64].broadcast_to([64, 2, 8, 64]), GLc, ALU.mult,
                       [b_ident, b_GL], [b_DG])
                    tt("dve", f16(MM), pv(p9, 1024, 64), f16(DG), ALU.add, [bp9, b_DG], [b_MM])
                    for c in (range(1, -1, -1) if rev else range(2)):
                        sti = sti_[0]
                        ST = STt[sti]; bST = b_ST[sti]; STn = STt[1 - sti]; bSTn = b_ST[1 - sti]
                        p11, bp11 = nextps()
                        for h in range(8):
                            o_ = p11[0:64, h * 64:(h + 1) * 64]
                            mm(o_, ST[:, h, :], QT[:, c, h, :], True, False, [bST, b_QT], [bp11])
                            mm(o_, Zf[:, c, h, 64:128], MT1[:, c, h, 64:128], False, False, [bZf, b_MT1], [bp11])
                            mm(o_, VT[:, c, h, :], MT2[:, c, h, 64:128], False, True, [b_VT, b_MT2], [bp11])
                        cp("act", YT[:, :, CS[c]], v3(p11, 64), [bp11], [b_YT])
                        p12, bp12 = nextps()
                        for h in range(8):
                            o_ = p12[0:64, h * 64:(h + 1) * 64]
                            mm(o_, MM[:, c, h, :], ST[:, h, :], True, False, [b_MM, bST], [bp12])
                            mm(o_, BHt[:, c, h, :], Zf[:, c, h, 64:128], False, False, [b_BHt, bZf], [bp12])
                            mm(o_, KHt[:, c, h, :], VT[:, c, h, :], False, True, [b_KHt, b_VT], [bp12])
                        cp("dve", STn[:], v3(p12, 64), [bp12], [bSTn])
                        sti_[0] = 1 - sti
                    dma("sp", YD[d, :, :, tg:tg + NT2], YT[:], [b_YT], [b_YD])

                PIPE = os.environ.get('RW_NOPIPE') != '1'
                if PIPE:
                    prep(order[0], *SETS[0])
                for idx, ti in enumerate(order):
                    pend = []
                    if not PIPE:
                        prep(ti, *SETS[idx % 2])
                    elif idx + 1 < len(order):
                        mk.deferred = pend
                        prep(order[idx + 1], *SETS[(idx + 1) % 2])
                        mk.deferred = None
                    if os.environ.get('RW_PIPE_MODE') == 'start':
                        mk.replay(pend, len(pend))
                    chunk(ti, pend, *SETS[idx % 2])
                    mk.replay(pend, len(pend))
            mk.flush(final=True)

    if upto >= 1:
        rwkv_pass(0)
        rwkv_pass(1)

    def s5_pass():
        es, sb, ps = scope("s5_")
        with es:
            TWO_PI = 2.0 * math.pi
            pz = [ps("pz%d" % i, [128, 512]) for i in range(8)]
            b_pz = [Buf() for _ in range(8)]
            pctr = [0]

            def nextps():
                i = pctr[0] % 8
                pctr[0] += 1
                return pz[i], b_pz[i]

            NLV = 10
            identb = sb("identb", [64, 64], BF16)
            dsk = sb("dsk", [16, 32])
            SQr = sb("SQr", [64, NLV, 64]); SQi = sb("SQi", [64, NLV, 64]); SQin = sb("SQin", [64, NLV, 64])
            LTr = sb("LTr", [64, 8, 64, 16], BF16); LTi = sb("LTi", [64, 8, 64, 16], BF16)
            OTr = sb("OTr", [64, 8, 64, 16], BF16); OTn = sb("OTn", [64, 8, 64, 16], BF16)
            CRb = sb("CRb", [64, 64, 16], BF16); CInb = sb("CInb", [64, 64, 16], BF16)
            bt_ = Buf()
            es2, sb2, ps2_ = scope("s5t_")
            ones1 = sb2("ones1", [1, 64]); row = sb2("row", [1, 64])
            mk.op("pool", lambda E: E.memset(ones1[:], 1.0), (), [bt_])
            dma("sp", row[:], log_dt.rearrange("d g -> (d g)").rearrange("(o n) -> o n", o=1), (), [bt_])
            cp("dve", identb[:], ident[0:64, 0:64], [b_ident], [bt_])
            LR = sb2("LR", [64, 64]); LI = sb2("LI", [64, 64])
            dma("sp", LR[:].rearrange("p (d g) -> p d g", d=2), lam_re.rearrange("d g p -> p d g"), (), [bt_], slow=True)
            dma("act", LI[:].rearrange("p (d g) -> p d g", d=2), lam_im.rearrange("d g p -> p d g"), (), [bt_], slow=True)
            BR = sb2("BR", [64, 64, 16]); BI = sb2("BI", [64, 64, 16])
            dma("sp", BR[:].rearrange("p (d g) h -> p d g h", d=2), b_re.rearrange("d g p h -> p d g h"), (), [bt_])
            dma("act", BI[:].rearrange("p (d g) h -> p d g h", d=2), b_im.rearrange("d g p h -> p d g h"), (), [bt_])
            CR = sb2("CR", [64, 64, 16]); CI = sb2("CI", [64, 64, 16])
            cnat = sb2("cnat", [128, 8, 64])
            for (src, dst) in ((c_re, CR), (c_im, CI)):
                dma("sp", cnat[:], src.rearrange("d g h p -> (d g h) p").rearrange("(k q) p -> q k p", q=128), [bt_], [bt_])
                for k in range(8):
                    pq, bq = nextps()
                    mk.op("pe", lambda E, pq=pq, k=k: E.transpose(out=pq[0:64, 0:128], in_=cnat[:, k, :], identity=ident[:]),
                          [bt_, b_ident], [bq])
                    cp("dve", dst[:, k * 8:(k + 1) * 8, :], pq[0:64, 0:128].rearrange("p (g h) -> p g h", h=16), [bq], [bt_])
            dma("sp", dsk[:], d_skip.rearrange("(g h) -> h g", h=16), (), [bt_], slow=True)
            DT = sb2("DT", [64, 64])
            pq, bq = nextps()
            mm(pq[0:64, 0:64], ones1[:], row[:], True, True, [bt_], [bq])
            act(DT[:], pq[0:64, 0:64], AF.Exp, [bq], [bt_])

            def T64(name):
                return sb2(name, [64, 64])
            ZR = T64("ZR"); ZI = T64("ZI"); EPs = T64("EPs"); COS = T64("COS"); SIN = T64("SIN")
            tA = T64("tA"); tB = T64("tB"); tC = T64("tC"); tI = sb2("tI", [64, 64], mybir.dt.int32)
            tt("dve", ZR[:], LR[:], DT[:], ALU.mult, [bt_], [bt_])
            tt("dve", ZI[:], LI[:], DT[:], ALU.mult, [bt_], [bt_])
            act(EPs[:], ZR[:], AF.Exp, [bt_], [bt_])
            for (dst, offs) in ((SIN, 64.0), (COS, 64.25)):
                ts("dve", tA[:], ZI[:], 1.0 / TWO_PI, offs, ALU.mult, ALU.add, [bt_], [bt_])
                cp("dve", tI[:], tA[:], [bt_], [bt_])
                cp("dve", tB[:], tI[:], [bt_], [bt_])
                tt("dve", tA[:], tA[:], tB[:], ALU.subtract, [bt_], [bt_])
                ts("dve", tB[:], tA[:], 0.5, None, ALU.is_gt, None, [bt_], [bt_])
                tt("dve", tA[:], tA[:], tB[:], ALU.subtract, [bt_], [bt_])
                act(dst[:], tA[:], AF.Sin, [bt_], [bt_], scale=TWO_PI)
            PWr = sb2("PWr", [64, 9, 64]); PWi = sb2("PWi", [64, 9, 64])
            mk.op("pool", lambda E: E.memset(PWr[:, 0, :], 1.0), (), [bt_])
            mk.op("pool", lambda E: E.memset(PWi[:, 0, :], 0.0), (), [bt_])
            tt("dve", PWr[:, 1, :], EPs[:], COS[:], ALU.mult, [bt_], [bt_])
            tt("dve", PWi[:, 1, :], EPs[:], SIN[:], ALU.mult, [bt_], [bt_])

            def cmul(or_, oi_, ar, ai, br, bi, n3=None):
                tt("dve", tA[:], ai, bi, ALU.mult, [bt_], [bt_])
                tt("dve", tB[:], ai, br, ALU.mult, [bt_], [bt_])
                tt("dve", tC[:], ar, br, ALU.mult, [bt_], [bt_])
                tt("dve", or_, tC[:], tA[:], ALU.subtract, [bt_], [bt_])
                tt("dve", tC[:], ar, bi, ALU.mult, [bt_], [bt_])
                tt("dve", oi_, tC[:], tB[:], ALU.add, [bt_], [bt_])
            for j in range(2, 9):
                cmul(PWr[:, j, :], PWi[:, j, :], PWr[:, j - 1, :], PWi[:, j - 1, :], PWr[:, 1, :], PWi[:, 1, :])
            NLV = 10
            cp("dve", SQr[:, 0, :], PWr[:, 8, :], [bt_], [bt_]); cp("dve", SQi[:, 0, :], PWi[:, 8, :], [bt_], [bt_])
            for k in range(1, NLV):
                cmul(SQr[:, k, :], SQi[:, k, :], SQr[:, k - 1, :], SQi[:, k - 1, :], SQr[:, k - 1, :], SQi[:, k - 1, :])
            ts("dve", SQin[:], SQi[:], -1.0, None, ALU.mult, None, [bt_], [bt_])
            CFr = T64("CFr"); CFi = T64("CFi"); DEN = T64("DEN"); NR = T64("NR")
            ts("dve", NR[:], PWr[:, 1, :], -1.0, None, ALU.add, None, [bt_], [bt_])
            tt("dve", tA[:], LR[:], LR[:], ALU.mult, [bt_], [bt_])
            tt("dve", tB[:], LI[:], LI[:], ALU.mult, [bt_], [bt_])
            tt("dve", DEN[:], tA[:], tB[:], ALU.add, [bt_], [bt_])
            mk.op("dve", lambda E: E.reciprocal(out=DEN[:], in_=DEN[:]), [bt_], [bt_])
            tt("dve", tA[:], NR[:], LR[:], ALU.mult, [bt_], [bt_])
            tt("dve", tB[:], PWi[:, 1, :], LI[:], ALU.mult, [bt_], [bt_])
            tt("dve", tA[:], tA[:], tB[:], ALU.add, [bt_], [bt_])
            tt("dve", CFr[:], tA[:], DEN[:], ALU.mult, [bt_], [bt_])
            tt("dve", tA[:], PWi[:, 1, :], LR[:], ALU.mult, [bt_], [bt_])
            tt("dve", tB[:], NR[:], LI[:], ALU.mult, [bt_], [bt_])
            tt("dve", tA[:], tA[:], tB[:], ALU.subtract, [bt_], [bt_])
            tt("dve", CFi[:], tA[:], DEN[:], ALU.mult, [bt_], [bt_])
            BbR = sb2("BbR", [64, 64, 16]); BbI = sb2("BbI", [64, 64, 16])
            X1t = sb2("X1t", [64, 64, 16]); X2t = sb2("X2t", [64, 64, 16])

            def b16(t2):
                return t2[:, :, None].broadcast_to([64, 64, 16])

            def cmul3(or_, oi_neg, ar2, ai2, br3, bi3, e1="dve", e2="pool"):
                tt(e1, X1t[:], br3, b16(ar2), ALU.mult, [bt_], [bt_])
                tt(e1, X2t[:], bi3, b16(ai2), ALU.mult, [bt_], [bt_])
                tt(e1, or_, X1t[:], X2t[:], ALU.subtract, [bt_], [bt_])
                tt(e1, X1t[:], bi3, b16(ar2), ALU.mult, [bt_], [bt_])
                tt(e1, X2t[:], br3, b16(ai2), ALU.mult, [bt_], [bt_])
                if oi_neg[1]:
                    tt(e1, X1t[:], X1t[:], X2t[:], ALU.add, [bt_], [bt_])
                    ts(e1, oi_neg[0], X1t[:], -1.0, None, ALU.mult, None, [bt_], [bt_])
                else:
                    tt(e1, oi_neg[0], X1t[:], X2t[:], ALU.add, [bt_], [bt_])
            cmul3(BbR[:], (BbI[:], False), CFr[:], CFi[:], BR[:], BI[:])
            for j in range(8):
                cmul3(LTr[:, j], (LTi[:, j], False), PWr[:, j, :], PWi[:, j, :], BbR[:], BbI[:])
                cmul3(OTr[:, j], (OTn[:, j], True), PWr[:, j + 1, :], PWi[:, j + 1, :], CR[:], CI[:])
            cp("dve", CRb[:], CR[:], [bt_], [bt_]); ts("dve", CInb[:], CI[:], -1.0, None, ALU.mult, None, [bt_], [bt_])

            mk.flush(final=True)
            es2.close()
            KTg = [sb("KTg%d" % i, [16, 15, 16], BF16) for i in range(2)]; b_KTg = [Buf(), Buf()]
            CTg = [sb("CTg%d" % i, [16, 32, 64], BF16) for i in range(2)]; b_CTg = [Buf(), Buf()]
            UGN = 2048
            ug = sb("ug", [16, UGN]); b_ug = Buf()
            ub = [sb("ub%d" % i, [16, 8, 1024], BF16) for i in range(2)]; b_ub = [Buf(), Buf()]
            yg = sb("yg", [16, 4096]); b_yg = Buf()
            NBM = 1024
            Wt = [[[sb("W%d%d%d" % (pp, d, c), [64, NBM + 1]) for c in range(2)] for d in range(2)] for pp in range(2)]
            b_Wt = [[Buf() for d in range(2)] for pp in range(2)]
            Sb = [[[sb("Sb%d%d%d" % (i, d, c), [64, NBM + 1], BF16) for c in range(2)] for d in range(2)] for i in range(2)]
            b_Sb = [Buf(), Buf()]
            for pp in range(2):
                for d in range(2):
                    for c in range(2):
                        mk.op("pool", lambda E, t=Wt[pp][d][c]: E.memset(t[:], 0.0), (), [b_Wt[pp][d]])
            id16 = ident[0:16, 0:16]

            def gconsts(g):
                par = g % 2
                pk, bpk = nextps()
                for idx in range(15):
                    if idx == 0:
                        terms = [(0, 0), (1, 0)]
                    elif idx < 8:
                        terms = [(0, idx)]
                    else:
                        terms = [(1, idx - 7)]
                    n = 0
                    for (d, tau) in terms:
                        gi = d * 32 + g
                        mm(pk[0:16, idx * 16:(idx + 1) * 16], LTr[:, tau, gi, :], CRb[:, gi, :], n == 0, False, [bt_], [bpk]); n += 1
                        mm(pk[0:16, idx * 16:(idx + 1) * 16], LTi[:, tau, gi, :], CInb[:, gi, :], False, n == 2 * len(terms) - 1, [bt_], [bpk]); n += 1
                cp("dve", KTg[par][:].rearrange("p a b -> p (a b)"), pk[0:16, 0:240], [bpk], [b_KTg[par]])
                stt(KTg[par][:, 0, :], id16, dsk[:, g:g + 1], KTg[par][:, 0, :], ALU.mult, ALU.add, [b_KTg[par], bt_, b_ident], [b_KTg[par]])
                for q in range(4):
                    pc_, bpc = nextps()
                    for j in range(8):
                        i = q * 8 + j
                        d = i // 16; s_ = (i // 2) % 8; c = i % 2
                        e_ = (7 - s_) if d == 0 else s_
                        src = (LTr if c == 0 else LTi)[:, e_, d * 32 + g, :]
                        mm(pc_[0:16, j * 64:(j + 1) * 64], src, identb[:], True, True, [bt_], [bpc])
                    cp("act", CTg[par][:, q * 8:(q + 1) * 8, :].rearrange("p a b -> p (a b)"),
                       pc_[0:16, 0:512], [bpc], [b_CTg[par]])

            def dims(s):
                toff, coff, T = SEQ[s]
                nblk = T // 8
                BW = min(512, nblk)
                return toff, coff, T, nblk, BW, 8 * BW, nblk // BW

            def front(g, s, st):
                par = g % 2
                toff, coff, T, nblk, BW, TW, nbt = dims(s)
                nlv = int(math.log2(nblk))
                UG = min(UGN, T)
                for hf in range(T // UG):
                    dma("sp", ug[:, 0:UG], Pscr[1920 + 16 * g:1936 + 16 * g, coff + 1 + hf * UG:coff + 1 + (hf + 1) * UG],
                        [b_P], [b_ug])
                    cp("act", ub[st][:, :, hf * (UG // 8):(hf + 1) * (UG // 8)], ug[:, 0:UG].rearrange("p (b s) -> p s b", s=8),
                       [b_ug], [b_ub[st]])
                if nblk < NBM:
                    for d in range(2):
                        for c in range(2):
                            mk.op("pool", lambda E, t=Wt[0][d][c]: E.memset(t[:], 0.0), (), [b_Wt[0][d]])
                            mk.op("pool", lambda E, t=Wt[1][d][c]: E.memset(t[:], 0.0), (), [b_Wt[1][d]])
                for bt in range(nbt):
                    for d in range(2):
                        for c in range(2):
                            pw_, bpw = nextps()
                            for s_ in range(8):
                                mm(pw_[0:64, 0:BW], CTg[par][:, (d * 8 + s_) * 2 + c, :],
                                   ub[st][:, s_, bt * BW:(bt + 1) * BW],
                                   s_ == 0, s_ == 7, [b_CTg[par], b_ub[st]], [bpw])
                            o0 = bt * BW + (1 if d == 0 else 0)
                            cp("act", Wt[0][d][c][:, o0:o0 + BW], pw_[0:64, 0:BW], [bpw], [b_Wt[0][d]])
                cur = 0
                for k in range(nlv):
                    sh = 1 << k
                    n_ = nblk - sh
                    for d in range(2):
                        gi = d * 32 + g
                        lo = 1 if d == 0 else 0
                        Wc = Wt[cur][d]; Wn = Wt[1 - cur][d]
                        bWc = b_Wt[cur][d]; bWn = b_Wt[1 - cur][d]
                        if d == 0:
                            dst = slice(lo + sh, lo + nblk); srcs = slice(lo, lo + n_); keep = slice(lo, lo + sh)
                        else:
                            dst = slice(lo, lo + n_); srcs = slice(lo + sh, lo + nblk); keep = slice(lo + n_, lo + nblk)
                        ar = SQr[:, k, gi:gi + 1]; ai = SQi[:, k, gi:gi + 1]; ain = SQin[:, k, gi:gi + 1]
                        stt(Wn[0][:, dst], Wc[0][:, srcs], ar, Wc[0][:, dst], ALU.mult, ALU.add, [bWc, bt_], [bWn])
                        stt(Wn[1][:, dst], Wc[1][:, srcs], ar, Wc[1][:, dst], ALU.mult, ALU.add, [bWc, bt_], [bWn])
                        stt(Wn[0][:, dst], Wc[1][:, srcs], ain, Wn[0][:, dst], ALU.mult, ALU.add, [bWc, bWn, bt_], [bWn])
                        stt(Wn[1][:, dst], Wc[0][:, srcs], ai, Wn[1][:, dst], ALU.mult, ALU.add, [bWc, bWn, bt_], [bWn])
                        cp("pool", Wn[0][:, keep], Wc[0][:, keep], [bWc], [bWn])
                        cp("pool", Wn[1][:, keep], Wc[1][:, keep], [bWc], [bWn])
                    cur = 1 - cur
                for d in range(2):
                    for c in range(2):
                        if d == 0:
                            cp("act", Sb[st][d][c][:, 1:nblk + 1], Wt[cur][d][c][:, 1:nblk + 1], [b_Wt[cur][d]], [b_Sb[st]])
                            mk.op("pool", lambda E, t=Sb[st][d][c]: E.memset(t[:, 0:1], 0.0), (), [b_Sb[st]])
                        else:
                            cp("act", Sb[st][d][c][:, 0:nblk], Wt[cur][d][c][:, 0:nblk], [b_Wt[cur][d]], [b_Sb[st]])
                            mk.op("pool", lambda E, t=Sb[st][d][c], nblk=nblk: E.memset(t[:, nblk:nblk + 1], 0.0), (), [b_Sb[st]])

            def back(g, s, st):
                par = g % 2
                toff, coff, T, nblk, BW, TW, nbt = dims(s)
                for bt in range(nbt):
                    ubv = ub[st][:, :, bt * BW:(bt + 1) * BW]
                    ygv = yg[:, 0:TW].rearrange("p (b s) -> p s b", s=8)
                    for t in range(8):
                        py, bpy = nextps()
                        for s_ in range(8):
                            idx = 0 if s_ == t else ((t - s_) if s_ < t else (7 + s_ - t))
                            mm(py[0:16, 0:BW], KTg[par][:, idx, :], ubv[:, s_, :], s_ == 0, False, [b_KTg[par], b_ub[st]], [bpy])
                        b0 = bt * BW
                        mm(py[0:16, 0:BW], OTr[:, t, g, :], Sb[st][0][0][:, b0:b0 + BW], False, False, [bt_, b_Sb[st]], [bpy])
                        mm(py[0:16, 0:BW], OTn[:, t, g, :], Sb[st][0][1][:, b0:b0 + BW], False, False, [bt_, b_Sb[st]], [bpy])
                        mm(py[0:16, 0:BW], OTr[:, 7 - t, 32 + g, :], Sb[st][1][0][:, b0 + 1:b0 + 1 + BW], False, False, [bt_, b_Sb[st]], [bpy])
                        mm(py[0:16, 0:BW], OTn[:, 7 - t, 32 + g, :], Sb[st][1][1][:, b0 + 1:b0 + 1 + BW], False, True, [bt_, b_Sb[st]], [bpy])
                        cp("act", ygv[:, t, :], py[0:16, 0:BW], [bpy], [b_yg])
                    dma("sp", YS[16 * g:16 * g + 16, toff + bt * TW:toff + (bt + 1) * TW], yg[:, 0:TW], [b_yg], [b_YS])

            units = [(g, s) for g in range(32) for s in range(NS)]
            prev = None
            for ui, (g, s) in enumerate(units):
                if s == 0:
                    gconsts(g)
                front(g, s, ui % 2)
                if prev is not None:
                    back(prev[0], prev[1], (ui - 1) % 2)
                prev = (g, s)
            back(prev[0], prev[1], (len(units) - 1) % 2)
            mk.flush(final=True)

    if upto >= 2:
        s5_pass()

    def mix_pass():
        es, sb, ps = scope("mx_")
        with es:
            N = 256
            pz = [ps("pz%d" % i, [128, 512]) for i in range(8)]
            b_pz = [Buf() for _ in range(8)]
            pctr = [0]

            def nextps():
                i = pctr[0] % 8
                pctr[0] += 1
                return pz[i], b_pz[i]
            bc_ = Buf()
            o64 = sb("o64", [64, 64])
            mk.op("pool", lambda E: E.memset(o64[:], 1.0 / 64.0), (), [bc_])
            ones_bf = sb("ones_bf", [128, 128], BF16)
            mk.op("pool", lambda E: E.memset(ones_bf[:], 1.0), (), [bc_])
            lg = sb("lg", [64, 8]); lb = sb("lb", [64, 8])
            dma("sp", lg[:], lnx_g.rearrange("(h p) -> p h", p=64), (), [bc_], slow=True)
            dma("sp", lb[:], lnx_b.rearrange("(h p) -> p h", p=64), (), [bc_], slow=True)
            g2t = sb("g2t", [128, 512]); dma("sp", g2t[:], g2, (), [bc_])
            mug = sb("mug", [128, 1]); hmg = sb("hmg", [128, 1]); omg = sb("omg", [128, 1])
            dma("sp", mug[:], mu_shift[1792:1920].rearrange("(p o) -> p o", o=1), (), [bc_], slow=True)
            ts("dve", hmg[:], mug[:], 0.5, None, ALU.mult, None, [bc_], [bc_])
            ts("dve", omg[:], mug[:], -1.0, 1.0, ALU.mult, ALU.add, [bc_], [bc_])
            bgl = sb("bgl", [128, 4]); s5g = sb("s5g", [128, 4])
            dma("sp", bgl[:], b_glu.rearrange("(q p) -> p q", p=128), (), [bc_], slow=True)
            dma("sp", s5g[:], s5_out_g.rearrange("(q p) -> p q", p=128), (), [bc_], slow=True)
            wst = sb("wst", [128, 4, 128])
            mk.op("pool", lambda E: E.memset(wst[:], 0.0), (), [bc_])
            for g in range(32):
                r0 = (g % 8) * 16
                dma("sp" if g % 2 == 0 else "act", wst[r0:r0 + 16, g // 8, r0:r0 + 16], w_glu[g], [bc_], [bc_])
            Wbd = sb("Wbd", [128, 4, 128], BF16)
            cp("dve", Wbd[:], wst[:], [bc_], [bc_])
            wo_r = sb("wo_r", [64, 8, D], BF16); wo_s = sb("wo_s", [128, 4, D], BF16)
            stg = [sb("stg%d" % i, [128, D]) for i in range(2)]; b_stg = [Buf(), Buf()]
            for h in range(8):
                st_ = stg[h % 2]; bs_ = b_stg[h % 2]
                dma("sp", st_[0:64, :], w_out[h * 64:(h + 1) * 64, :], (), [bs_])
                cp("pool", wo_r[:, h, :], st_[0:64, :], [bs_], [bc_])
            for q in range(4):
                st_ = stg[q % 2]; bs_ = b_stg[q % 2]
                dma("sp", st_[:], w_out[512 + q * 128:512 + (q + 1) * 128, :], (), [bs_])
                cp("pool", wo_s[:, q, :], st_[:], [bs_], [bc_])

            def T4(name, dt=F32):
                return sb(name, [64, 8, N], dt), Buf()
            YF, b_YF = T4("YF"); YB, b_YB = T4("YB"); BF_, b_BF = T4("BF"); BB, b_BB = T4("BB")
            Ym, b_Ym = T4("Ym"); SQt, b_SQt = T4("SQt"); RS, b_RS = T4("RS")
            yr, b_yr = T4("yr", BF16)
            XG = sb("XG", [128, N + 2]); b_XG = Buf(); tw = sb("tw", [128, N]); b_tw = Buf()
            sg = sb("sg", [128, N]); b_sg = Buf()
            S5 = sb("S5", [128, 4, N]); b_S5 = Buf(); Z1 = sb("Z1", [128, 4, N]); b_Z1 = Buf()
            Z2 = sb("Z2", [128, 4, N]); b_Z2 = Buf(); Zb = sb("Zb", [128, 4, N], BF16); b_Zb = Buf()
            r2 = sb("r2", [128, N]); b_r2 = Buf()
            ysb = sb("ysb", [128, 4, N], BF16); b_ysb = Buf()
            xTt = sb("xTt", [128, 8, N]); b_xTt = Buf(); X1t = sb("X1t", [128, 8, N]); b_X1t = Buf()

            def bc(t, n):
                return t[:, :, None].broadcast_to([64, 8, n])

            def fl(t):
                return t[:].rearrange("p h t -> p (h t)")
            for s, (toff, coff, T) in enumerate(SEQ):
                for ti in range(T // N):
                    tl = ti * N; tg = toff + tl; c0 = coff + tl
                    dma("sp", YF[:], YD[0, :, :, tg:tg + N], [b_YD], [b_YF])
                    dma("act", YB[:], YD[1, :, :, tg:tg + N], [b_YD], [b_YB])
                    dma("sp", BF_[:], BD[0, :, :, tg:tg + N], [b_BD], [b_BF])
                    dma("act", BB[:], BD[1, :, :, tg:tg + N], [b_BD], [b_BB])
                    dma("sp", XG[:], Pscr[1792:1920, c0:c0 + N + 2], [b_P], [b_XG])
                    dma("act", S5[:], YS[:, tg:tg + N].rearrange("(q p) t -> p q t", p=128), [b_YS], [b_S5])
                    dma("sp", xTt[:], XT[:, tg:tg + N].rearrange("(k p) t -> p k t", p=128), [b_XT], [b_xTt])
                    tt("pool", Ym[:], YF[:], YB[:], ALU.add, [b_YF, b_YB], [b_Ym])
                    for j in range(4):
                        pm_, bpm = nextps()
                        mm(pm_[0:64, :], o64[:], fl(Ym)[:, j * 512:(j + 1) * 512], True, True, [bc_, b_Ym], [bpm])
                        tt("dve", fl(YF)[:, j * 512:(j + 1) * 512], fl(Ym)[:, j * 512:(j + 1) * 512], pm_[0:64, :],
                           ALU.subtract, [bpm, b_Ym], [b_YF])
                    tt("pool", SQt[:], YF[:], YF[:], ALU.mult, [b_YF], [b_SQt])
                    for j in range(4):
                        pm_, bpm = nextps()
                        mm(pm_[0:64, :], o64[:], fl(SQt)[:, j * 512:(j + 1) * 512], True, True, [bc_, b_SQt], [bpm])
                        act(fl(RS)[:, j * 512:(j + 1) * 512], pm_[0:64, :], AF.Sqrt, [bpm], [b_RS], bias=LNX_EPS)
                    mk.op("dve", lambda E: E.reciprocal(out=RS[:], in_=RS[:]), [b_RS], [b_RS])
                    tt("pool", Ym[:], YF[:], RS[:], ALU.mult, [b_YF, b_RS], [b_Ym])
                    tt("pool", Ym[:], Ym[:], bc(lg, N), ALU.mult, [b_Ym, bc_], [b_Ym])
                    tt("pool", Ym[:], Ym[:], bc(lb, N), ALU.add, [b_Ym, bc_], [b_Ym])
                    tt("pool", Ym[:], Ym[:], BF_[:], ALU.add, [b_Ym, b_BF], [b_Ym])
                    tt("pool", Ym[:], Ym[:], BB[:], ALU.add, [b_Ym, b_BB], [b_Ym])
                    tt("dve", tw[:], XG[:, 0:N], XG[:, 2:N + 2], ALU.add, [b_XG], [b_tw])
                    ts("dve", tw[:], tw[:], hmg[:, 0:1], None, ALU.mult, None, [b_tw, bc_], [b_tw])
                    stt(sg[:], XG[:, 1:N + 1], omg[:, 0:1], tw[:], ALU.mult, ALU.add, [b_XG, b_tw, bc_], [b_sg])
                    act(sg[:], sg[:], AF.Sigmoid, [b_sg], [b_sg])
                    for h2_ in range(4):
                        pm_, bpm = nextps()
                        for j in range(2):
                            h = 2 * h2_ + j
                            mm(pm_[0:64, j * N:(j + 1) * N], g2t[:, h * 64:(h + 1) * 64], sg[:], True, True, [bc_, b_sg], [bpm])
                        tt("dve", yr[:, 2 * h2_:2 * h2_ + 2, :], pm_[0:64, :].rearrange("p (h n) -> p h n", n=N),
                           Ym[:, 2 * h2_:2 * h2_ + 2, :], ALU.mult, [bpm, b_Ym], [b_yr])
                    K0 = 2.0 * math.sqrt(2.0 / math.pi)
                    tt("pool", Z1[:], S5[:], S5[:], ALU.mult, [b_S5], [b_Z1])
                    ts("dve", Z1[:], Z1[:], 0.044715, 1.0, ALU.mult, ALU.add, [b_Z1], [b_Z1])
                    tt("pool", Z1[:], Z1[:], S5[:], ALU.mult, [b_Z1, b_S5], [b_Z1])
                    act(Z1[:], Z1[:], AF.Sigmoid, [b_Z1], [b_Z1], scale=K0)
                    tt("pool", Z1[:], Z1[:], S5[:], ALU.mult, [b_Z1, b_S5], [b_Z1])
                    cp("dve", Zb[:], Z1[:], [b_Z1], [b_Zb])
                    for q in range(4):
                        pm_, bpm = nextps()
                        mm(pm_[:, 0:N], Wbd[:, q, :], Zb[:, q, :], True, True, [bc_, b_Zb], [bpm])
                        mk.op("act", lambda E, pm_=pm_, q=q: E.activation(out=Z2[:, q, :], in_=pm_[:, 0:N], func=AF.Sigmoid,
                                                                        bias=bgl[:, q:q + 1], scale=1.0), [bpm, bc_], [b_Z2])
                    tt("pool", Z2[:], Z2[:], Z1[:], ALU.mult, [b_Z2, b_Z1], [b_Z2])
                    tt("pool", Zb[:], Z2[:], Z2[:], ALU.mult, [b_Z2], [b_Zb])
                    pm_, bpm = nextps()
                    for q in range(4):
                        mm(pm_[:, 0:N], ones_bf[:], Zb[:, q, :], q == 0, q == 3, [bc_, b_Zb], [bpm])
                    act(r2[:], pm_[:, 0:N], AF.Sqrt, [bpm], [b_r2], bias=RMS_EPS, scale=1.0 / 512.0)
                    mk.op("dve", lambda E: E.reciprocal(out=r2[:], in_=r2[:]), [b_r2], [b_r2])
                    for q in range(4):
                        stt(ysb[:, q, :], Z2[:, q, :], s5g[:, q:q + 1], r2[:], ALU.mult, ALU.mult, [b_Z2, b_r2, bc_], [b_ysb])
                    for dm in range(8):
                        pm_, bpm = nextps()
                        for h in range(8):
                            mm(pm_[:, 0:N], wo_r[:, h, dm * 128:(dm + 1) * 128], yr[:, h, :], h == 0, False, [bc_, b_yr], [bpm])
                        for q in range(4):
                            mm(pm_[:, 0:N], wo_s[:, q, dm * 128:(dm + 1) * 128], ysb[:, q, :], False, q == 3, [bc_, b_ysb], [bpm])
                        stt(X1t[:, dm, :], pm_[:, 0:N], modT[:, 16 + dm, s:s + 1], xTt[:, dm, :], ALU.mult, ALU.add,
                            [bpm, b_mod, b_xTt], [b_X1t])
                    dma("sp", X1[:, tg:tg + N].rearrange("(k p) t -> p k t", p=128), X1t[:], [b_X1t], [b_X1])
            mk.flush(final=True)

    if upto >= 3:
        mix_pass()

    def ffn_pass():
        es, sb, ps = scope("ff_")
        with es:
            N = 256
            pz = [ps("pz%d" % i, [128, 512]) for i in range(8)]
            b_pz = [Buf() for _ in range(8)]
            pctr = [0]

            def nextps():
                i = pctr[0] % 8
                pctr[0] += 1
                return pz[i], b_pz[i]
            bc_ = Buf()
            ones_bf = sb("ones_bf", [128, 128], BF16)
            mk.op("pool", lambda E: E.memset(ones_bf[:], 1.0), (), [bc_])
            n2g = sb("n2g", [128, 8]); fg = sb("fg", [128, 8]); sc2 = sb("sc2", [128, 8, NS])
            dma("sp", n2g[:], norm2_g.rearrange("(k p) -> p k", p=128), (), [bc_], slow=True)
            dma("sp", fg[:], final_g.rearrange("(k p) -> p k", p=128), (), [bc_], slow=True)
            for s in range(NS):
                stt(sc2[:, :, s], modT[:, 32:40, s], 1.0, n2g[:], ALU.add, ALU.mult, [b_mod, bc_], [bc_])
            w1 = sb("w1", [128, 8, DFF], BF16); w3 = sb("w3", [128, 8, DFF], BF16); w2_ = sb("w2_", [128, 22, D], BF16)
            stg = [sb("stg%d" % i, [128, 1408]) for i in range(2)]; b_stg = [Buf(), Buf()]
            n = 0
            for (src, dst) in ((w_ff1, w1), (w_ff3, w3)):
                for k in range(8):
                    for hf in range(2):
                        st_ = stg[n % 2]; bs_ = b_stg[n % 2]; n += 1
                        dma("sp" if n % 2 else "act", st_[:], src[k * 128:(k + 1) * 128, hf * 1408:(hf + 1) * 1408], (), [bs_])
                        cp("pool" if n % 2 else "dve", dst[:, k, hf * 1408:(hf + 1) * 1408], st_[:], [bs_], [bc_])
            for k in range(22):
                st_ = stg[n % 2]; bs_ = b_stg[n % 2]; n += 1
                dma("sp" if n % 2 else "act", st_[:, 0:D], w_ff2[k * 128:(k + 1) * 128, :], (), [bs_])
                cp("pool" if n % 2 else "dve", w2_[:, k, :], st_[:, 0:D], [bs_], [bc_])
            X1t = sb("X1t", [128, 8, N]); b_X1t = Buf()
            sq = sb("sq", [128, 8, N], BF16); b_sq = Buf()
            rstd = sb("rstd", [128, N]); b_rstd = Buf(); tmp = sb("tmp", [128, N]); b_tmp = Buf()
            h2 = sb("h2", [128, 8, N], BF16); b_h2 = Buf()
            fm = sb("fm", [128, 22, N], BF16); b_fm = Buf()
            av = [sb("av%d" % i, [128, N]) for i in range(2)]; b_av = [Buf(), Buf()]
            X2t = sb("X2t", [128, 8, N]); b_X2t = Buf()
            ytm = sb("ytm", [128, 2, D]); b_ytm = Buf()

            def rms_bc(src, bsrc):
                for dc in range(8):
                    act(sq[:, dc, :], src[:, dc, :], AF.Square, [bsrc], [b_sq])
                pm_, bpm = nextps()
                for dc in range(8):
                    mm(pm_[:, 0:N], ones_bf[:], sq[:, dc, :], dc == 0, dc == 7, [bc_, b_sq], [bpm])
                act(rstd[:], pm_[:, 0:N], AF.Sqrt, [bpm], [b_rstd], bias=RMS_EPS, scale=1.0 / D)
                mk.op("dve", lambda E: E.reciprocal(out=rstd[:], in_=rstd[:]), [b_rstd], [b_rstd])
            for s, (toff, coff, T) in enumerate(SEQ):
                for ti in range(T // N):
                    tg = toff + ti * N
                    dma("sp", X1t[:], X1[:, tg:tg + N].rearrange("(k p) t -> p k t", p=128), [b_X1], [b_X1t])
                    rms_bc(X1t, b_X1t)
                    for dc in range(8):
                        tt("dve", tmp[:], X1t[:, dc, :], rstd[:], ALU.mult, [b_X1t, b_rstd], [b_tmp])
                        ts("dve", h2[:, dc, :], tmp[:], sc2[:, dc, s:s + 1], modT[:, 24 + dc, s:s + 1], ALU.mult, ALU.add,
                           [b_tmp, bc_, b_mod], [b_h2])
                    for mc in range(22):
                        p1, bp1 = nextps()
                        for dc in range(8):
                            mm(p1[:, 0:N], w1[:, dc, mc * 128:(mc + 1) * 128], h2[:, dc, :], dc == 0, dc == 7, [bc_, b_h2], [bp1])
                        p3, bp3 = nextps()
                        for dc in range(8):
                            mm(p3[:, 0:N], w3[:, dc, mc * 128:(mc + 1) * 128], h2[:, dc, :], dc == 0, dc == 7, [bc_, b_h2], [bp3])
                        a_ = av[mc % 2]; ba_ = b_av[mc % 2]
                        act(a_[:], p1[:, 0:N], AF.Silu, [bp1], [ba_])
                        tt("dve", fm[:, mc, :], a_[:], p3[:, 0:N], ALU.mult, [ba_, bp3], [b_fm])
                    for dm in range(8):
                        pm_, bpm = nextps()
                        for mc in range(22):
                            mm(pm_[:, 0:N], w2_[:, mc, dm * 128:(dm + 1) * 128], fm[:, mc, :], mc == 0, mc == 21, [bc_, b_fm], [bpm])
                        stt(X2t[:, dm, :], pm_[:, 0:N], modT[:, 40 + dm, s:s + 1], X1t[:, dm, :], ALU.mult, ALU.add,
                            [bpm, b_mod, b_X1t], [b_X2t])
                    rms_bc(X2t, b_X2t)
                    for dc in range(8):
                        stt(X2t[:, dc, :], X2t[:, dc, :], fg[:, dc:dc + 1], rstd[:], ALU.mult, ALU.mult,
                            [b_X2t, bc_, b_rstd], [b_X2t])
                    for j in range(N // 128):
                        for dq in range(2):
                            pm_, bpm = nextps()
                            for k in range(4):
                                dc = dq * 4 + k
                                mk.op("pe", lambda E, pm_=pm_, dc=dc, j=j, k=k: E.transpose(
                                    out=pm_[:, k * 128:(k + 1) * 128], in_=X2t[:, dc, j * 128:(j + 1) * 128], identity=ident[:]),
                                    [b_X2t, b_ident], [bpm])
                            cp("act" if dq == 0 else "dve", ytm[:, j, dq * 512:(dq + 1) * 512], pm_[:], [bpm], [b_ytm])
                    dma("sp", y_out[tg:tg + N, :].rearrange("(j p) d -> p j d", p=128), ytm[:], [b_ytm], [])
            mk.flush(final=True)

    if upto >= 4:
        ffn_pass()

    outer.close()
    return nc


def core_inputs(P, x, c):
    f = np.float32
    m = {"x": x, "c": c}
    for k in ("norm1_g", "w_ada", "b_ada", "w_in", "mu_shift", "w0", "w2", "a0", "a2", "g2", "k_k", "k_a",
              "lnx_g", "lnx_b", "lam_re", "lam_im", "log_dt", "b_re", "b_im", "c_re", "c_im", "w_glu",
              "s5_out_g", "w_out", "norm2_g", "w_ff1", "w_ff3", "w_ff2", "final_g"):
        m[k] = P[k]
    m["r_k"] = P["r_k"].reshape(512)
    m["d_skip"] = P["d_skip"].reshape(512)
    m["b_glu"] = P["b_glu"].reshape(512)
    return {k: np.ascontiguousarray(v, dtype=f) for k, v in m.items()}


_T_PROMPT = 8192
_T_SAMPLE = 4096


def kernel(**inputs):
    n = 8
    P = {}
    for k, v in inputs.items():
        if k in ("x_prompt", "x_sample", "c_prompt", "c_sample"):
            continue
        v = np.asarray(v)
        P[k] = v if k == "final_g" else v[0]
    xp = np.asarray(inputs["x_prompt"]); xs = np.asarray(inputs["x_sample"])
    cpr = np.asarray(inputs["c_prompt"]); cs = np.asarray(inputs["c_sample"])
    TS = [xp.shape[1], xs.shape[1]]
    nc = build_program(TS, upto=int(os.environ.get('KUPTO', '99')))
    in_maps = []
    for b in range(n):
        x = np.concatenate([xp[b], xs[b]], axis=0)
        c = np.stack([cpr[b], cs[b]], axis=0)
        in_maps.append(core_inputs(P, x, c))
    res = run_bass_kernel_spmd(nc, in_maps, core_ids=list(range(n)))
    yp = np.stack([res.results[b]["y"][:TS[0]] for b in range(n)], axis=0).astype(np.float32)
    ys = np.stack([res.results[b]["y"][TS[0]:] for b in range(n)], axis=0).astype(np.float32)
    return (yp, ys)
```

```python
import os
import math
import numpy as np
import concourse.bass as bass
import concourse.mybir as mybir
from concourse.bass_utils import run_bass_kernel_spmd

F32 = mybir.dt.float32
BF16 = mybir.dt.bfloat16
AF = mybir.ActivationFunctionType
ALU = mybir.AluOpType
AX = mybir.AxisListType

D = 1024
DFF = 2816
NPROJ = 2432
RW = 1920
NDS = 12
RMS_EPS = 1e-6
LNX_EPS = 64e-5


class Buf:
    __slots__ = ("w", "r")

    def __init__(self):
        self.w = None
        self.r = {}


class MK:
    BLK = {"pe": "tensor", "dve": "vector", "act": "scalar", "pool": "gpsimd", "sp": "sync"}

    def __init__(self, nc, same=True):
        self.nc = nc
        self.same = same
        self.names = ["pe", "dve", "act", "pool", "sp"]
        self.sem = {k: nc.alloc_semaphore(name="s_" + k) for k in self.names}
        self.cnt = {k: 0 for k in self.names}
        self.seen = {k: {} for k in self.names}
        self.prog = {k: [] for k in self.names}
        self.dsem = [nc.alloc_semaphore(name="d%d" % i) for i in range(NDS)]
        self.dcnt = [0] * NDS
        self.dnext = 0
        self.deferred = None

    def semof(self, key):
        if isinstance(key, tuple):
            return self.dsem[key[1]]
        return self.sem[key]

    def _deps(self, e, reads, writes):
        deps = {}

        def add(k, v):
            if deps.get(k, 0) < v:
                deps[k] = v

        for b in reads:
            if b.w:
                add(*b.w)
        for b in writes:
            if b.w:
                add(*b.w)
            for k, v in b.r.items():
                add(k, v)
        out = []
        for k, v in deps.items():
            if k == e and (e == "pe" or not self.same):
                continue
            if self.seen[e].get(k, 0) >= v:
                continue
            self.seen[e][k] = v
            out.append((k, v))
        return out

    def _mark(self, tok, reads, writes):
        k, v = tok
        for b in reads:
            if b.r.get(k, 0) < v:
                b.r[k] = v
        for b in writes:
            b.w = tok
            b.r = {}

    def op(self, e, fn, reads=(), writes=()):
        if self.deferred is not None:
            self.deferred.append((0, e, fn, reads, writes))
            return
        waits = self._deps(e, reads, writes)
        self.cnt[e] += 1
        tok = (e, self.cnt[e])
        self.prog[e].append((waits, fn, self.sem[e], 1))
        self._mark(tok, reads, writes)

    def replay(self, pending, n):
        keep = self.deferred
        self.deferred = None
        last = None
        cnt = 0
        while pending and (cnt < n or last == "pe"):
            kind, e, fn, reads, writes = pending.pop(0)
            (self.dma if kind else self.op)(e, fn, reads, writes)
            last = e if not kind else None
            cnt += 1
        self.deferred = keep

    def dma(self, q, fn, reads=(), writes=()):
        if self.deferred is not None:
            self.deferred.append((1, q, fn, reads, writes))
            return
        i = self.dnext
        self.dnext = (i + 1) % NDS
        key = ("d", i)
        waits = self._deps(q, reads, writes)
        if self.dcnt[i] > 0 and self.seen[q].get(key, 0) < self.dcnt[i]:
            waits.append((key, self.dcnt[i]))
            self.seen[q][key] = self.dcnt[i]
        self.dcnt[i] += 16
        tok = (key, self.dcnt[i])
        self.prog[q].append((waits, fn, self.dsem[i], 16))
        self._mark(tok, reads, writes)

    def flush(self, final=False):
        nc = self.nc
        fin = []
        for i in (range(NDS) if final else []):
            if self.dcnt[i] > 0:
                fin.append((("d", i), self.dcnt[i]))
        for k in (self.names if final else []):
            if k != "sp" and self.cnt[k] > 0:
                fin.append((k, self.cnt[k]))
        with nc.Block() as block:
            for e in self.names:
                prog = self.prog[e]
                extra = fin if e == "sp" else []

                def body(eng, prog=prog, extra=extra):
                    for waits, fn, sem, inc in prog:
                        for k, v in waits:
                            eng.wait_ge(self.semof(k), v)
                        fn(eng).then_inc(sem, inc)
                    for k, v in extra:
                        eng.wait_ge(self.semof(k), v)

                getattr(block, self.BLK[e])(body)
        self.prog = {k: [] for k in self.names}

    def emit(self):
        self.flush(final=True)


def build_program(TS, dbg=False, upto=99):
    import contextlib
    nc = bass.Bass("TRN2", target_bir_lowering=False)
    mk = MK(nc, same=(os.environ.get("MK_SAME", "1") == "1"))
    TT = sum(TS)
    NS = len(TS)
    WP = TT + 2 * NS
    SEQ = []
    o = 0
    for s, T in enumerate(TS):
        SEQ.append((o, o + 2 * s, T))
        o += T

    def din(name, shape):
        return nc.dram_tensor(name, list(shape), F32, kind="ExternalInput").ap()

    def dscr(name, shape):
        return nc.dram_tensor(name, list(shape), F32, kind=("ExternalOutput" if dbg else "Internal")).ap()

    x_in = din("x", (TT, D))
    c_in = din("c", (NS, D))
    norm1_g = din("norm1_g", (D,))
    w_ada = din("w_ada", (D, 6 * D))
    b_ada = din("b_ada", (6 * D,))
    w_in = din("w_in", (D, NPROJ))
    mu_shift = din("mu_shift", (RW,))
    w0 = din("w0", (2, 512)); w2 = din("w2", (2, 64, 512))
    a0 = din("a0", (2, 512)); a2 = din("a2", (2, 64, 512))
    g2 = din("g2", (128, 512))
    k_k = din("k_k", (512,)); k_a = din("k_a", (512,)); r_k = din("r_k", (512,))
    lnx_g = din("lnx_g", (512,)); lnx_b = din("lnx_b", (512,))
    lam_re = din("lam_re", (2, 32, 64)); lam_im = din("lam_im", (2, 32, 64)); log_dt = din("log_dt", (2, 32))
    b_re = din("b_re", (2, 32, 64, 16)); b_im = din("b_im", (2, 32, 64, 16))
    c_re = din("c_re", (2, 32, 16, 64)); c_im = din("c_im", (2, 32, 16, 64))
    d_skip = din("d_skip", (512,)); w_glu = din("w_glu", (32, 16, 16)); b_glu = din("b_glu", (512,))
    s5_out_g = din("s5_out_g", (512,))
    w_out = din("w_out", (D, D)); norm2_g = din("norm2_g", (D,))
    w_ff1 = din("w_ff1", (D, DFF)); w_ff3 = din("w_ff3", (D, DFF)); w_ff2 = din("w_ff2", (DFF, D))
    final_g = din("final_g", (D,))
    y_out = nc.dram_tensor("y", [TT, D], F32, kind="ExternalOutput").ap()

    Pscr = dscr("Pscr", (NPROJ, WP))
    XT = dscr("XT", (D, TT))
    YD = dscr("YD", (2, 64, 8, TT))
    BD = dscr("BD", (2, 64, 8, TT))
    YS = dscr("YS", (512, TT))
    X1 = dscr("X1", (D, TT))
    MODS = dscr("MODS", (128, 48 * NS))
    b_P = Buf(); b_XT = Buf(); b_YD = Buf(); b_BD = Buf(); b_YS = Buf(); b_X1 = Buf(); b_MODS = Buf()

    def tt(e, out, a, b, op, r, w):
        mk.op(e, lambda E: E.tensor_tensor(out=out, in0=a, in1=b, op=op), r, w)

    def ts(e, out, a, s1, s2, op0, op1, r, w):
        if op1 is None:
            mk.op(e, lambda E: E.tensor_scalar(out=out, in0=a, scalar1=s1, scalar2=None, op0=op0), r, w)
        else:
            mk.op(e, lambda E: E.tensor_scalar(out=out, in0=a, scalar1=s1, scalar2=s2, op0=op0, op1=op1), r, w)

    def stt(out, a, sc, b, op0, op1, r, w):
        mk.op("dve", lambda E: E.scalar_tensor_tensor(out=out, in0=a, scalar=sc, in1=b, op0=op0, op1=op1), r, w)

    def act(out, a, func, r, w, bias=0.0, scale=1.0):
        mk.op("act", lambda E: E.activation(out=out, in_=a, func=func, bias=bias, scale=scale), r, w)

    def cp(e, out, a, r, w):
        if e == "act":
            mk.op("act", lambda E: E.activation(out=out, in_=a, func=AF.Copy), r, w)
        else:
            mk.op(e, lambda E: E.tensor_copy(out=out, in_=a), r, w)

    def mm(out, lhsT, rhs, st, sp_, r, w):
        mk.op("pe", lambda E: E.matmul(out=out, lhsT=lhsT, rhs=rhs, start=st, stop=sp_), r, w)

    F32R = mybir.dt.float32r
    USE_R = os.environ.get("RW_F32R", "1") == "1"

    def RR(ap):
        return ap.bitcast(F32R) if USE_R else ap

    def mmr(out, lhsT, rhs, r, w):
        mk.op("pe", lambda E: E.matmul(out=out, lhsT=lhsT.bitcast(F32R), rhs=rhs.bitcast(F32R), start=True, stop=True), r, w)

    def dma(q, out, in_, r, w, slow=False):
        if slow:
            mk.dma(q, lambda E: E.dma_start(out=out, in_=in_, allow_slow_non_contiguous=True), r, w)
        else:
            mk.dma(q, lambda E: E.dma_start(out=out, in_=in_), r, w)

    def scope(pfx=""):
        es = contextlib.ExitStack()

        def sb(name, shape, dt=F32):
            return es.enter_context(nc.sbuf_tensor(pfx + name, list(shape), dt))

        def ps(name, shape, dt=F32):
            return es.enter_context(nc.psum_tensor(pfx + name, list(shape), dt))
        return es, sb, ps

    def consts(sb):
        ident = sb("ident", [128, 128]); b_ident = Buf()
        mk.op("pool", lambda E: E.memset(ident[:], 1.0), (), [b_ident])
        mk.op("pool", lambda E: E.affine_select(out=ident[:], in_=ident[:], pattern=[[-1, 128]],
                                                compare_op=ALU.is_equal, fill=0.0, base=0, channel_multiplier=1),
              [b_ident], [b_ident])
        return ident, b_ident
    outer, osb, ops_ = scope("o_")
    ident, b_ident = consts(osb)
    modT = osb("modT", [128, 48, NS]); b_mod = Buf()
    sc1 = osb("sc1", [128, 8, NS]); b_sc1 = Buf()

    def pass0():
        es, sb, ps = scope("p0_")
        with es:
            ones_bf = sb("ones_bf", [128, 128], BF16); b_ones = Buf()
            mk.op("pool", lambda E: E.memset(ones_bf[:], 1.0), (), [b_ones])
            cT = sb("cT", [128, 8, NS]); b_cT = Buf()
            scT = sb("scT", [128, 8, NS]); b_scT = Buf()
            for s in range(NS):
                dma("sp", cT[:, :, s], c_in[s].rearrange("(k p) -> p k", p=128), (), [b_cT], slow=True)
            act(scT[:], cT[:], AF.Silu, [b_cT], [b_scT])
            badaT = sb("badaT", [128, 48]); b_bada = Buf()
            dma("sp", badaT[:], b_ada.rearrange("(k p) -> p k", p=128), (), [b_bada], slow=True)
            g1T = sb("g1T", [128, 8]); b_g1 = Buf()
            dma("sp", g1T[:], norm1_g.rearrange("(k p) -> p k", p=128), (), [b_g1], slow=True)
            wada_t = [sb("wada%d" % i, [128, 8, 256]) for i in range(2)]
            b_wada = [Buf(), Buf()]
            ps_mod_full = ps("ps_mod", [128, 512]); b_psmod = Buf()
            ps_mod = ps_mod_full[:, 0:4 * NS].rearrange("p (a b) -> p a b", b=NS)
            for slab in range(24):
                wt = wada_t[slab % 2]; bw = b_wada[slab % 2]
                dma("sp" if slab % 2 == 0 else "act", wt[:],
                    w_ada[:, slab * 256:(slab + 1) * 256].rearrange("(k p) n -> p k n", p=128), (), [bw])
                for j in range(2):
                    for k in range(8):
                        mm(ps_mod[:, j, :], wt[:, k, j * 128:(j + 1) * 128], scT[:, k, :], k == 0, k == 7,
                           [bw, b_scT], [b_psmod])
                for s in range(NS):
                    tt("dve", modT[:, slab * 2:(slab + 1) * 2, s], ps_mod[:, 0:2, s],
                       badaT[:, slab * 2:(slab + 1) * 2], ALU.add, [b_psmod, b_bada], [b_mod])
            for s in range(NS):
                stt(sc1[:, :, s], modT[:, 8:16, s], 1.0, g1T[:], ALU.add, ALU.mult, [b_mod, b_g1], [b_sc1])

            w_in_bf = sb("w_in_bf", [128, 8, NPROJ], BF16); b_win = Buf()
            wst = [sb("wst%d" % i, [128, NPROJ]) for i in range(2)]; b_wst = [Buf(), Buf()]
            for k in range(8):
                dma("sp", wst[k % 2][:], w_in[k * 128:(k + 1) * 128, :], (), [b_wst[k % 2]])
                cp("pool", w_in_bf[:, k, :], wst[k % 2][:], [b_wst[k % 2]], [b_win])

            NT = 512
            xtm = [sb("xtm%d" % i, [128, 4, D]) for i in range(2)]; b_xtm = [Buf(), Buf()]
            xT = sb("xT", [128, 8, NT]); b_xT = Buf()
            sq = sb("sq", [128, 8, NT], BF16); b_sq = Buf()
            rstd = sb("rstd", [128, NT]); b_rstd = Buf()
            tmp = sb("tmp0", [128, NT]); b_tmp = Buf()
            hT = sb("hT", [128, 8, NT], BF16); b_hT = Buf()
            pev = [sb("pev%d" % i, [128, NT]) for i in range(3)]; b_pev = [Buf() for _ in range(3)]
            zcol = sb("zcol", [128, 1]); b_zcol = Buf()
            mk.op("pool", lambda E: E.memset(zcol[:], 0.0), (), [b_zcol])
            pst = [ps("pst%d" % i, [128, NT]) for i in range(4)]; b_pst = [Buf() for _ in range(4)]
            psm = [ps("psm%d" % i, [128, NT]) for i in range(3)]; b_psm = [Buf() for _ in range(3)]
            for s, (toff, coff, T) in enumerate(SEQ):
                for mc in range(19):
                    for cc in (coff, coff + T + 1):
                        dma("sp", Pscr[mc * 128:(mc + 1) * 128, cc:cc + 1], zcol[:], [b_zcol], [b_P], slow=True)
                for ti in range(T // NT):
                    t0 = toff + ti * NT
                    xt = xtm[ti % 2]; bx = b_xtm[ti % 2]
                    dma("sp", xt[:], x_in[t0:t0 + NT, :].rearrange("(j p) d -> p j d", p=128), (), [bx])
                    for dc in range(8):
                        pt = pst[dc % 4]; bp = b_pst[dc % 4]
                        for j in range(4):
                            mk.op("pe", lambda E, pt=pt, xt=xt, j=j, dc=dc: E.transpose(
                                out=pt[:, j * 128:(j + 1) * 128], in_=xt[:, j, dc * 128:(dc + 1) * 128],
                                identity=ident[:]), [bx, b_ident], [bp])
                        cp("dve", xT[:, dc, :], pt[:], [bp], [b_xT])
                        act(sq[:, dc, :], pt[:], AF.Square, [bp, b_xT], [b_sq])
                    dma("act", XT[:, t0:t0 + NT].rearrange("(k p) t -> p k t", p=128), xT[:], [b_xT], [b_XT])
                    pm = psm[0]; bpm = b_psm[0]
                    for dc in range(8):
                        mm(pm[:], ones_bf[:], sq[:, dc, :], dc == 0, dc == 7, [b_sq, b_ones], [bpm])
                    act(rstd[:], pm[:], AF.Sqrt, [bpm], [b_rstd], bias=RMS_EPS, scale=1.0 / D)
                    mk.op("dve", lambda E: E.reciprocal(out=rstd[:], in_=rstd[:]), [b_rstd], [b_rstd])
                    for dc in range(8):
                        tt("dve", tmp[:], xT[:, dc, :], rstd[:], ALU.mult, [b_xT, b_rstd], [b_tmp])
                        ts("dve", hT[:, dc, :], tmp[:], sc1[:, dc, s:s + 1], modT[:, dc, s:s + 1], ALU.mult, ALU.add,
                           [b_tmp, b_sc1, b_mod], [b_hT])
                    for mc in range(19):
                        i3 = mc % 3
                        pm = psm[i3]; bpm = b_psm[i3]
                        for dc in range(8):
                            mm(pm[:], w_in_bf[:, dc, mc * 128:(mc + 1) * 128], hT[:, dc, :], dc == 0, dc == 7,
                               [b_win, b_hT], [bpm])
                        pv = pev[i3]; bpv = b_pev[i3]
                        cp("act", pv[:], pm[:], [bpm], [bpv])
                        cc = coff + 1 + ti * NT
                        dma("sp", Pscr[mc * 128:(mc + 1) * 128, cc:cc + NT], pv[:], [bpv], [b_P])
            mk.flush(final=True)

    pass0()
    def rwkv_pass(d):
        rev = (d == 1)
        es, sb, ps = scope("rw%d_" % d)
        with es:
            NT2 = 128
            psr = [ps("psr%d" % i, [128, 2048]) for i in range(2)]
            b_psr = [Buf() for _ in range(2)]
            pctr = [0]

            def nextps():
                i = pctr[0] % 2
                pctr[0] += 1
                return psr[i], b_psr[i]

            def T4(name):
                return sb(name, [64, 8, NT2]), Buf()

            def ldp(name, src512):
                t = sb(name, [64, 8]); b = Buf()
                dma("sp", t[:], src512.rearrange("(h p) -> p h", p=64), (), [b], slow=True)
                return t, b

            mu3 = sb("mu3", [64, 24]); b_mu3 = Buf()
            dma("sp", mu3[:], mu_shift[0:1536].rearrange("(g p) -> p g", p=64), (), [b_mu3], slow=True)
            hm3 = sb("hm3", [64, 24]); om3 = sb("om3", [64, 24]); b_hm3 = Buf()
            ts("dve", hm3[:], mu3[:], 0.5, None, ALU.mult, None, [b_mu3], [b_hm3])
            ts("dve", om3[:], mu3[:], -1.0, 1.0, ALU.mult, ALU.add, [b_mu3], [b_hm3])
            muw = sb("muw", [64, 2]); b_muw = Buf()
            dma("sp", muw[:, 0:1], mu_shift[1536 + 64 * d:1600 + 64 * d].rearrange("(p o) -> p o", o=1), (), [b_muw], slow=True)
            dma("sp", muw[:, 1:2], mu_shift[1664 + 64 * d:1728 + 64 * d].rearrange("(p o) -> p o", o=1), (), [b_muw], slow=True)
            hmw = sb("hmw", [64, 2]); omw = sb("omw", [64, 2]); b_hmw = Buf()
            ts("dve", hmw[:], muw[:], 0.5, None, ALU.mult, None, [b_muw], [b_hmw])
            ts("dve", omw[:], muw[:], -1.0, 1.0, ALU.mult, ALU.add, [b_muw], [b_hmw])
            w0d, b_w0d = ldp("w0d", w0[d]); a0d, b_a0d = ldp("a0d", a0[d])
            kk_, b_kk_ = ldp("kk_", k_k); ka_, b_ka_ = ldp("ka_", k_a); rk_, b_rk_ = ldp("rk_", r_k)
            omka = sb("omka", [64, 8]); b_omka = Buf()
            ts("dve", omka[:], ka_[:], -1.0, 1.0, ALU.mult, ALU.add, [b_ka_], [b_omka])
            w2d = sb("w2d", [64, 512]); a2d = sb("a2d", [64, 512]); b_w2d = Buf()
            dma("sp", w2d[:], w2[d], (), [b_w2d]); dma("sp", a2d[:], a2[d], (), [b_w2d])
            ones64 = sb("ones64", [64, 64]); b_c = Buf()
            mk.op("pool", lambda E: E.memset(ones64[:], 1.0), (), [b_c])
            maskA = sb("maskA", [64, 128]); maskL = sb("maskL", [64, 64]); MS = sb("MS", [64, 8 * NT2])
            mk.op("pool", lambda E: E.memset(maskA[:], 1.0), (), [b_c])
            mk.op("pool", lambda E: E.memset(maskL[:], 1.0), (), [b_c])
            mk.op("pool", lambda E: E.memset(MS[:], 1.0), (), [b_c])
            zc_ = 63 if rev else 0
            mk.op("pool", lambda E: E.memset(MS[:].rearrange("p (a l) -> p a l", l=64)[:, :, zc_:zc_ + 1], 0.0), [b_c], [b_c])

            def asel(ap, upper, strict):
                pat = [[1, 64]] if upper else [[-1, 64]]
                cm = -1 if upper else 1
                mk.op("pool", lambda E: E.affine_select(out=ap, in_=ap, pattern=pat, compare_op=ALU.is_ge, fill=0.0,
                                                        base=(-1 if strict else 0), channel_multiplier=cm),
                      [b_c], [b_c])
            asel(maskA[:, 0:64], not rev, True)
            asel(maskA[:, 64:128], not rev, False)
            asel(maskL[:], rev, True)
            mA = maskA[:, None, :].broadcast_to([64, 8, 128])
            mL = maskL[:, None, :].broadcast_to([64, 8, 64])
            id64 = ident[0:64, 0:64]
            idbc = ident[0:64, None, 0:64].broadcast_to([64, 8, 64])

            Lr = [sb("Lq%d" % q, [64, 8, NT2 + 2]) for q in range(2)]; b_L = [Buf() for _ in range(2)]
            Lr.append(Lr[0]); b_L.append(b_L[0])
            XW = sb("XW", [64, NT2 + 2]); XA = sb("XA", [64, NT2 + 2]); b_XW = Buf(); b_XA = Buf()
            T1, b_T1 = T4("T1")
            SH = [T4("SH%d" % q) for q in range(2)]
            (Rp, b_Rp), (Kp, b_Kp) = SH
            tt0, cp0 = tt, cp
            tw = sb("tw", [64, NT2]); b_tw = Buf()
            xwp = sb("xwp", [64, NT2]); xap = sb("xap", [64, NT2]); b_xwp = Buf(); b_xap = Buf()
            XB, b_XB = T4("XB"); E2, b_E2 = T4("E2"); AD, b_AD = T4("AD"); KR, b_KR = T4("KR")
            SS, b_SS = T4("SS"); KD, b_KD = T4("KD"); AB, b_AB = T4("AB"); BON, b_BON = XB, b_XB
            G, b_G = T4("G"); D1, b_D1 = E2, b_E2; D2, b_D2 = AD, b_AD; EP, b_EP = XB, b_XB; EN, b_EN = SS, b_SS
            T2, b_T2 = SS, b_SS
            SD = F32
            SETS = []
            for i_ in range(2):
                st_ = []
                for nm, shp in (("AR", [64, 8, 2, 128]), ("KT", [64, 8, NT2]), ("BT", [64, 8, NT2]), ("KH", [64, 8, NT2]),
                                ("BH", [64, 8, NT2]), ("Vp", [64, 8, NT2]), ("GL", [64, 16])):
                    st_ += [sb("%s_%d" % (nm, i_), shp), Buf()]
                SETS.append(st_)
            YT, b_YT = T4("YT")
            MT1 = sb("MT1", [64, 2, 8, 128], SD); MT2 = sb("MT2", [64, 2, 8, 128], SD); b_MT1 = Buf(); b_MT2 = Buf()
            P0 = sb("P0", [64, 17, 64], SD); b_P0 = Buf()
            PP = [sb("PP%d" % i, [64, 33, 64], SD) for i in range(2)]; b_PP = [Buf(), Buf()]
            Zt = [sb("Zt%d" % i, [64, 2, 8, 128], SD) for i in range(2)]; b_Zt = [Buf() for _ in range(2)]
            VT = sb("VT", [64, 2, 8, 64], SD); BHt = sb("BHt", [64, 2, 8, 64], SD); KHt = sb("KHt", [64, 2, 8, 64], SD)
            QT = sb("QT", [64, 2, 8, 64]); MM = sb("MM", [64, 2, 8, 64]); DG = sb("DG", [64, 17, 64])
            b_VT = Buf(); b_BHt = Buf(); b_KHt = Buf(); b_QT = Buf(); b_MM = Buf(); b_DG = Buf()
            STt = [sb("ST%d" % i, [64, 8, 64]) for i in range(2)]; b_ST = [Buf(), Buf()]
            ts("dve", RR(P0[:, 16, :]), ones64[:], 0.0, None, ALU.mult, None, [b_c], [b_P0])
            for i_ in range(2):
                ts("dve", RR(PP[i_][:, 32, :]), ones64[:], 0.0, None, ALU.mult, None, [b_c], [b_PP[i_]])
            mA16 = maskA[:, None, :].broadcast_to([64, 16, 128])
            mL16 = maskL[:, None, :].broadcast_to([64, 16, 64])
            idbc16 = ident[0:64, None, 0:64].broadcast_to([64, 16, 64])

            def f16(t):
                return t[:].rearrange("p c h n -> p (c h) n")

            def pv(p, lo, n):
                return p[0:64, lo:lo + 16 * n].rearrange("p (a n) -> p a n", n=n)

            def v3(p, n):
                return p[0:64, 0:8 * n].rearrange("p (h n) -> p h n", n=n)

            def bc(t, lo, hi, n):
                return t[:, lo:hi, None].broadcast_to([64, hi - lo, n])

            def c4(t):
                return t[:].rearrange("p h (c l) -> p h c l", l=64)

            for s, (toff, coff, T) in enumerate(SEQ):
                sti_ = [0]
                mk.op("pool", lambda E: E.memset(STt[0][:], 0.0), (), [b_ST[0]])
                ntile = T // NT2
                order = list(range(ntile - 1, -1, -1) if rev else range(ntile))

                def prep(ti, AR, b_AR, KT, b_KT, BT, b_BT, KH, b_KH, BH, b_BH, Vp, b_Vp, GL, b_GL):
                    ARb, b_ARb = AR, b_AR
                    tl = ti * NT2
                    c0 = coff + tl
                    tg = toff + tl
                    def ldq(q):
                        dma("sp" if q != 1 else "act", Lr[q][:],
                            Pscr[q * 512:(q + 1) * 512, c0:c0 + NT2 + 2].rearrange("(h p) t -> p h t", p=64),
                            [b_P], [b_L[q]])

                    def shq(q):
                        Lq = Lr[q]; S_, bS = (SH[q] if q < 2 else (Vp, b_Vp))
                        tt("pool", T1[:], Lq[:, :, 0:NT2], Lq[:, :, 2:NT2 + 2], ALU.add, [b_L[q]], [b_T1])
                        tt("pool", T1[:], T1[:], bc(hm3, 8 * q, 8 * q + 8, NT2), ALU.mult, [b_T1, b_hm3], [b_T1])
                        tt("pool", S_[:], Lq[:, :, 1:NT2 + 1], bc(om3, 8 * q, 8 * q + 8, NT2), ALU.mult,
                           [b_L[q], b_hm3], [bS])
                        tt("pool", S_[:], S_[:], T1[:], ALU.add, [bS, b_T1], [bS])
                    ldq(0); ldq(1)
                    dma("sp", XW[:], Pscr[1536 + 64 * d:1600 + 64 * d, c0:c0 + NT2 + 2], [b_P], [b_XW])
                    dma("act", XA[:], Pscr[1664 + 64 * d:1728 + 64 * d, c0:c0 + NT2 + 2], [b_P], [b_XA])
                    shq(0); ldq(2); shq(1); shq(2)
                    for (X_, bX, o_, bo, j) in ((XW, b_XW, xwp, b_xwp, 0), (XA, b_XA, xap, b_xap, 1)):
                        tt("dve", tw[:], X_[:, 0:NT2], X_[:, 2:NT2 + 2], ALU.add, [bX], [b_tw])
                        ts("dve", tw[:], tw[:], hmw[:, j:j + 1], None, ALU.mult, None, [b_tw, b_hmw], [b_tw])
                        stt(o_[:], X_[:, 1:NT2 + 1], omw[:, j:j + 1], tw[:], ALU.mult, ALU.add, [bX, b_hmw, b_tw], [bo])
                    act(xwp[:], xwp[:], AF.Tanh, [b_xwp], [b_xwp])
                    for hh in range(2):
                        pa, bpa = nextps()
                        for j in range(4):
                            h = 4 * hh + j
                            mm(pa[0:64, j * NT2:(j + 1) * NT2], w2d[:, h * 64:(h + 1) * 64], xwp[:], True, True,
                               [b_w2d, b_xwp], [bpa])
                        tt("dve", XB[:, 4 * hh:4 * hh + 4, :], pa[0:64, 0:4 * NT2].rearrange("p (h n) -> p h n", n=NT2),
                           bc(w0d, 4 * hh, 4 * hh + 4, NT2), ALU.add, [bpa, b_w0d], [b_XB])
                    act(XB[:], XB[:], AF.Exp, [b_XB], [b_XB], scale=-1.0)
                    act(XB[:], XB[:], AF.Ln, [b_XB], [b_XB], bias=1.0)
                    act(E2[:], XB[:], AF.Exp, [b_XB], [b_E2], bias=-0.5, scale=-1.0)
                    for hh in range(2):
                        pa, bpa = nextps()
                        for j in range(4):
                            h = 4 * hh + j
                            mm(pa[0:64, j * NT2:(j + 1) * NT2], a2d[:, h * 64:(h + 1) * 64], xap[:], True, True,
                               [b_w2d, b_xap], [bpa])
                        tt("dve", AD[:, 4 * hh:4 * hh + 4, :], pa[0:64, 0:4 * NT2].rearrange("p (h n) -> p h n", n=NT2),
                           bc(a0d, 4 * hh, 4 * hh + 4, NT2), ALU.add, [bpa, b_a0d], [b_AD])
                    act(AD[:], AD[:], AF.Sigmoid, [b_AD], [b_AD])
                    tt("pool", KR[:], Kp[:], bc(kk_, 0, 8, NT2), ALU.mult, [b_Kp, b_kk_], [b_KR])
                    tt("pool", T1[:], KR[:], KR[:], ALU.mult, [b_KR], [b_T1])
                    for hh in range(2):
                        pa, bpa = nextps()
                        for j in range(4):
                            h = 4 * hh + j
                            mm(pa[0:64, j * NT2:(j + 1) * NT2], ones64[:], T1[:, h, :], True, True, [b_c, b_T1], [bpa])
                        ts("dve", SS[:, 4 * hh:4 * hh + 4, :], pa[0:64, 0:4 * NT2].rearrange("p (h n) -> p h n", n=NT2),
                           1e-24, None, ALU.max, None, [bpa], [b_SS])
                    act(SS[:], SS[:], AF.Sqrt, [b_SS], [b_SS])
                    mk.op("dve", lambda E: E.reciprocal(out=SS[:], in_=SS[:]), [b_SS], [b_SS])
                    tt("pool", KR[:], KR[:], SS[:], ALU.mult, [b_KR, b_SS], [b_KR])
                    tt("pool", T2[:], AD[:], bc(ka_, 0, 8, NT2), ALU.mult, [b_AD, b_ka_], [b_T2])
                    tt("pool", T2[:], T2[:], bc(omka, 0, 8, NT2), ALU.add, [b_T2, b_omka], [b_T2])
                    tt("pool", KD[:], T2[:], Kp[:], ALU.mult, [b_T2, b_Kp], [b_KD])
                    tt("dve", AB[:], AD[:], KR[:], ALU.mult, [b_AD, b_KR], [b_AB])
                    tt("pool", T1[:], Rp[:], KD[:], ALU.mult, [b_Rp, b_KD], [b_T1])
                    tt("pool", T1[:], T1[:], bc(rk_, 0, 8, NT2), ALU.mult, [b_T1, b_rk_], [b_T1])
                    for hh in range(2):
                        pa, bpa = nextps()
                        for j in range(4):
                            h = 4 * hh + j
                            mm(pa[0:64, j * NT2:(j + 1) * NT2], ones64[:], T1[:, h, :], True, True, [b_c, b_T1], [bpa])
                        tt("dve", BON[:, 4 * hh:4 * hh + 4, :], pa[0:64, 0:4 * NT2].rearrange("p (h n) -> p h n", n=NT2),
                           Vp[:, 4 * hh:4 * hh + 4, :], ALU.mult, [bpa, b_Vp], [b_BON])
                    dma("sp", BD[d, :, :, tg:tg + NT2], BON[:], [b_BON], [b_BD])
                    E2f = E2[:].rearrange("p h t -> p (h t)"); Gf = G[:].rearrange("p h t -> p (h t)"); MSf = MS[:]
                    if rev:
                        E2f = E2f[:, ::-1]; Gf = Gf[:, ::-1]; MSf = MSf[:, ::-1]
                    mk.op("dve", lambda E, Gf=Gf, MSf=MSf, E2f=E2f: E.tensor_tensor_scan(
                        out=Gf, data0=MSf, data1=E2f, initial=0.0, op0=ALU.mult, op1=ALU.add), [b_E2, b_c], [b_G])
                    tt("pool", D1[:], G[:], E2[:], ALU.subtract, [b_G, b_E2], [b_D1])
                    Gv = G[:].rearrange("p h (c l) -> p (h c) l", l=64)
                    ti_ = 0 if rev else 63
                    totb = Gv[:, :, ti_:ti_ + 1].broadcast_to([64, 16, 64])
                    tt("pool", D2[:].rearrange("p h (c l) -> p (h c) l", l=64), Gv, totb, ALU.subtract, [b_G], [b_D2])
                    act(EP[:], G[:], AF.Exp, [b_G], [b_EP])
                    act(EN[:], G[:], AF.Exp, [b_G], [b_EN], scale=-1.0)
                    act(D1[:], D1[:], AF.Exp, [b_D1], [b_D1], scale=-1.0)
                    act(D2[:], D2[:], AF.Exp, [b_D2], [b_D2])
                    act(GL[:].rearrange("p (a o) -> p a o", o=1), Gv[:, :, ti_:ti_ + 1], AF.Exp, [b_G], [b_GL], scale=-1.0)
                    stt(RR(AR[:, :, :, 0:64]), c4(KR), -1.0, c4(D1), ALU.mult, ALU.mult, [b_KR, b_D1], [b_AR])
                    tt("pool", RR(AR[:, :, :, 64:128]), c4(Rp), c4(EN), ALU.mult, [b_Rp, b_EN], [b_AR])
                    tt("pool", KT[:], KD[:], EP[:], ALU.mult, [b_KD, b_EP], [b_KT])
                    tt("dve", RR(BT[:]), AB[:], EP[:], ALU.mult, [b_AB, b_EP], [b_BT])
                    tt("pool", KH[:], KD[:], D2[:], ALU.mult, [b_KD, b_D2], [b_KH])
                    tt("dve", BH[:], AB[:], D2[:], ALU.mult, [b_AB, b_D2], [b_BH])

                def chunk(ti, pend, AR, b_AR, KT, b_KT, BT, b_BT, KH, b_KH, BH, b_BH, Vp, b_Vp, GL, b_GL):
                    ARb, b_ARb = AR, b_AR
                    tg = toff + ti * NT2

                    def tt(*a):
                        tt0(*a)
                        mk.replay(pend, 2)

                    def cp(*a):
                        cp0(*a)
                        mk.replay(pend, 2)
                    CS = [slice(0, 64), slice(64, 128)]
                    p1, bp1 = nextps()
                    for c in range(2):
                        for h in range(8):
                            mm(p1[0:64, (c * 8 + h) * 128:(c * 8 + h + 1) * 128], BT[:, h, CS[c]], ARb[:, h, c, :], True, True,
                               [b_BT, b_ARb], [bp1])
                    tt("dve", RR(f16(MT1)), pv(p1, 0, 128), mA16, ALU.mult, [bp1, b_c], [b_MT1])
                    p2, bp2 = nextps()
                    for c in range(2):
                        for h in range(8):
                            mm(p2[0:64, (c * 8 + h) * 128:(c * 8 + h + 1) * 128], KT[:, h, CS[c]], ARb[:, h, c, :], True, True,
                               [b_KT, b_ARb], [bp2])
                    tt("dve", RR(f16(MT2)), pv(p2, 0, 128), mA16, ALU.mult, [bp2, b_c], [b_MT2])
                    p3, bp3 = nextps()
                    for c in range(2):
                        for h in range(8):
                            if USE_R:
                                mmr(p3[0:128, (c * 8 + h) * 64:(c * 8 + h + 1) * 64], ARb[:, h, c, :], BT[:, h, CS[c]], [b_ARb, b_BT], [bp3])
                            else:
                                mm(p3[0:64, (c * 8 + h) * 64:(c * 8 + h + 1) * 64], ARb[:, h, c, 0:64], BT[:, h, CS[c]], True, True,
                                   [b_ARb, b_BT], [bp3])
                    tt("dve", RR(P0[:, 0:16, :]), pv(p3, 0, 64), mL16, ALU.mult, [bp3, b_c], [b_P0])
                    Z0 = Zt[0]; bZ0 = b_Zt[0]
                    p4, bp4 = nextps()
                    for c in range(2):
                        for h in range(8):
                            mk.op("pe", lambda E, p4=p4, o=(c * 8 + h) * 64, a=AR[:, h, c, 0:64]: E.transpose(
                                out=p4[0:64, o:o + 64], in_=a, identity=id64), [b_AR, b_ident], [bp4])
                            mk.op("pe", lambda E, p4=p4, o=1024 + (c * 8 + h) * 64, a=Vp[:, h, CS[c]]: E.transpose(
                                out=p4[0:64, o:o + 64], in_=a, identity=id64), [b_Vp, b_ident], [bp4])
                    cp0("act", RR(Z0[:, :, :, 0:64].rearrange("p c h n -> p (c h) n")), pv(p4, 0, 64), [bp4], [bZ0])
                    cp("act", RR(f16(VT)), pv(p4, 1024, 64), [bp4], [b_VT])
                    p5, bp5 = nextps()
                    for c in range(2):
                        for h in range(8):
                            mk.op("pe", lambda E, p5=p5, o=(c * 8 + h) * 64, a=BH[:, h, CS[c]]: E.transpose(
                                out=p5[0:64, o:o + 64], in_=a, identity=id64), [b_BH, b_ident], [bp5])
                            mk.op("pe", lambda E, p5=p5, o=1024 + (c * 8 + h) * 64, a=KH[:, h, CS[c]]: E.transpose(
                                out=p5[0:64, o:o + 64], in_=a, identity=id64), [b_KH, b_ident], [bp5])
                    cp0("dve", RR(f16(BHt)), pv(p5, 0, 64), [bp5], [b_BHt])
                    cp("dve", f16(KHt), pv(p5, 1024, 64), [bp5], [b_KHt])
                    p6, bp6 = nextps()
                    for c in range(2):
                        for h in range(8):
                            if USE_R:
                                mmr(p6[0:128, (c * 8 + h) * 64:(c * 8 + h + 1) * 64], MT2[:, c, h, :], VT[:, c, h, :], [b_MT2, b_VT], [bp6])
                            else:
                                mm(p6[0:64, (c * 8 + h) * 64:(c * 8 + h + 1) * 64], MT2[:, c, h, 0:64], VT[:, c, h, :], True, True,
                                   [b_MT2, b_VT], [bp6])
                    cp("act", RR(Z0[:, :, :, 64:128].rearrange("p c h n -> p (c h) n")), pv(p6, 0, 64), [bp6], [bZ0])
                    zi = 0
                    P0f = P0[:].rearrange("p a n -> p (a n)")
                    Pv = lambda c, h: P0[:, c * 8 + h, :]
                    Pw = lambda c, h: P0f[:, (c * 8 + h) * 64:(c * 8 + h) * 64 + 128]
                    PTv = lambda c, h: MT1[:, c, h, 0:64]
                    PTw = lambda c, h: MT1[:, c, h, :]
                    bP = b_P0; bPT = b_MT1
                    for it in range(6):
                        Zc = Zt[zi]; bZc = b_Zt[zi]; Zn = Zt[1 - zi]; bZn = b_Zt[1 - zi]
                        PTc, Pc, PTcw, Pcw, bPTc, bPc = PTv, Pv, PTw, Pw, bPT, bP
                        if it < 5:
                            nx = it % 2
                            p8, bp8 = nextps()
                            for c in range(2):
                                for h in range(8):
                                    o1 = (c * 8 + h) * 64
                                    if USE_R:
                                        mmr(p8[0:128, o1:o1 + 64], PTcw(c, h), Pc(c, h), [bPTc, bPc], [bp8])
                                        mmr(p8[0:128, 1024 + o1:1024 + o1 + 64], Pcw(c, h), PTc(c, h), [bPTc, bPc], [bp8])
                                    else:
                                        mm(p8[0:64, o1:o1 + 64], PTc(c, h), Pc(c, h), True, True, [bPTc, bPc], [bp8])
                                        mm(p8[0:64, 1024 + o1:1024 + o1 + 64], Pc(c, h), PTc(c, h), True, True, [bPTc, bPc], [bp8])
                            cp("act", RR(PP[nx][:, 0:32, :]), p8[0:64, 0:2048].rearrange("p (a n) -> p a n", n=64), [bp8], [b_PP[nx]])
                            PPf = PP[nx][:].rearrange("p a n -> p (a n)")
                            Pv = lambda c, h, nx=nx: PP[nx][:, c * 8 + h, :]
                            PTv = lambda c, h, nx=nx: PP[nx][:, 16 + c * 8 + h, :]
                            Pw = lambda c, h, PPf=PPf: PPf[:, (c * 8 + h) * 64:(c * 8 + h) * 64 + 128]
                            PTw = lambda c, h, PPf=PPf: PPf[:, (16 + c * 8 + h) * 64:(16 + c * 8 + h) * 64 + 128]
                            bP = b_PP[nx]; bPT = b_PP[nx]
                        p7, bp7 = nextps()
                        for c in range(2):
                            for h in range(8):
                                o1 = (c * 8 + h) * 128
                                if USE_R:
                                    mmr(p7[0:128, o1:o1 + 128], PTcw(c, h), Zc[:, c, h, :], [bPTc, bZc], [bp7])
                                else:
                                    mm(p7[0:64, o1:o1 + 128], PTc(c, h), Zc[:, c, h, :], True, True, [bPTc, bZc], [bp7])
                        tt("dve", RR(f16(Zn)), pv(p7, 0, 128), f16(Zc), ALU.add, [bp7, bZc], [bZn])
                        zi = 1 - zi
                    Zf = Zt[zi]; bZf = b_Zt[zi]
                    p9, bp9 = nextps()
                    for c in range(2):
                        for h in range(8):
                            if USE_R:
                                mmr(p9[0:128, (c * 8 + h) * 64:(c * 8 + h + 1) * 64], Zf[:, c, h, :], MT1[:, c, h, 64:128], [bZf, b_MT1], [bp9])
                                mmr(p9[0:128, 1024 + (c * 8 + h) * 64:1024 + (c * 8 + h + 1) * 64], Zf[:, c, h, :], BHt[:, c, h, :], [bZf, b_BHt], [bp9])
                            else:
                                mm(p9[0:64, (c * 8 + h) * 64:(c * 8 + h + 1) * 64], Zf[:, c, h, 0:64], MT1[:, c, h, 64:128], True, True,
                                   [bZf, b_MT1], [bp9])
                                mm(p9[0:64, 1024 + (c * 8 + h) * 64:1024 + (c * 8 + h + 1) * 64], Zf[:, c, h, 0:64], BHt[:, c, h, :], True, True,
                                   [bZf, b_BHt], [bp9])
                    tt0("dve", QT[:], p9[0:64, 0:1024].rearrange("p (c h n) -> p c h n", c=2, h=8),
                       AR[:, :, :, 64:128].rearrange("p h c n -> p c h n"), ALU.add, [bp9, b_AR], [b_QT])
                    GLc = GL[:].rearrange("p (h c) -> p c h", c=2)[:, :, :, None].broadcast_to([64, 2, 8, 64])
                    tt0("pool", DG[:, 0:16, :].rearrange("p (c h) n -> p c h n", c=2),
                        ident[0:64, None, None, 0:64].broadcast_to([64, 2, 8, 64]), GLc, ALU.mult, [b_ident, b_GL], [b_DG])
                    tt("dve", f16(MM), pv(p9, 1024, 64), DG[:, 0:16, :], ALU.add, [bp9, b_DG], [b_MM])
                    for c in (range(1, -1, -1) if rev else range(2)):
                        sti = sti_[0]
                        ST = STt[sti]; bST = b_ST[sti]; STn = STt[1 - sti]; bSTn = b_ST[1 - sti]
                        p11, bp11 = nextps()
                        for h in range(8):
                            o_ = p11[0:64, h * 64:(h + 1) * 64]
                            mm(o_, ST[:, h, :], QT[:, c, h, :], True, False, [bST, b_QT], [bp11])
                            mm(o_, Zf[:, c, h, 64:128], MT1[:, c, h, 64:128], False, False, [bZf, b_MT1], [bp11])
                            mm(o_, VT[:, c, h, :], MT2[:, c, h, 64:128], False, True, [b_VT, b_MT2], [bp11])
                        cp("act", YT[:, :, CS[c]], v3(p11, 64), [bp11], [b_YT])
                        p12, bp12 = nextps()
                        for h in range(8):
                            o_ = p12[0:64, h * 64:(h + 1) * 64]
                            mm(o_, MM[:, c, h, :], ST[:, h, :], True, False, [b_MM, bST], [bp12])
                            mm(o_, BHt[:, c, h, :], Zf[:, c, h, 64:128], False, False, [b_BHt, bZf], [bp12])
                            mm(o_, KHt[:, c, h, :], VT[:, c, h, :], False, True, [b_KHt, b_VT], [bp12])
                        cp("dve", STn[:], v3(p12, 64), [bp12], [bSTn])
                        sti_[0] = 1 - sti
                    dma("sp", YD[d, :, :, tg:tg + NT2], YT[:], [b_YT], [b_YD])

                PIPE = os.environ.get('RW_NOPIPE') != '1'
                if PIPE:
                    prep(order[0], *SETS[0])
                for idx, ti in enumerate(order):
                    pend = []
                    if not PIPE:
                        prep(ti, *SETS[idx % 2])
                    elif idx + 1 < len(order):
                        mk.deferred = pend
                        prep(order[idx + 1], *SETS[(idx + 1) % 2])
                        mk.deferred = None
                    if os.environ.get('RW_PIPE_MODE') == 'start':
                        mk.replay(pend, len(pend))
                    chunk(ti, pend, *SETS[idx % 2])
                    mk.replay(pend, len(pend))
            mk.flush(final=True)

    if upto >= 1:
        rwkv_pass(0)
        rwkv_pass(1)

    def s5_pass():
        es, sb, ps = scope("s5_")
        with es:
            TWO_PI = 2.0 * math.pi
            pz = [ps("pz%d" % i, [128, 512]) for i in range(8)]
            b_pz = [Buf() for _ in range(8)]
            pctr = [0]

            def nextps():
                i = pctr[0] % 8
                pctr[0] += 1
                return pz[i], b_pz[i]

            NLV = 10
            identb = sb("identb", [64, 64], BF16)
            dsk = sb("dsk", [16, 32])
            SQr = sb("SQr", [64, NLV, 64]); SQi = sb("SQi", [64, NLV, 64]); SQin = sb("SQin", [64, NLV, 64])
            LTr = sb("LTr", [64, 8, 64, 16], BF16); LTi = sb("LTi", [64, 8, 64, 16], BF16)
            OTr = sb("OTr", [64, 8, 64, 16], BF16); OTn = sb("OTn", [64, 8, 64, 16], BF16)
            CRb = sb("CRb", [64, 64, 16], BF16); CInb = sb("CInb", [64, 64, 16], BF16)
            bt_ = Buf()
            es2, sb2, ps2_ = scope("s5t_")
            ones1 = sb2("ones1", [1, 64]); row = sb2("row", [1, 64])
            mk.op("pool", lambda E: E.memset(ones1[:], 1.0), (), [bt_])
            dma("sp", row[:], log_dt.rearrange("d g -> (d g)").rearrange("(o n) -> o n", o=1), (), [bt_])
            cp("dve", identb[:], ident[0:64, 0:64], [b_ident], [bt_])
            LR = sb2("LR", [64, 64]); LI = sb2("LI", [64, 64])
            dma("sp", LR[:].rearrange("p (d g) -> p d g", d=2), lam_re.rearrange("d g p -> p d g"), (), [bt_], slow=True)
            dma("act", LI[:].rearrange("p (d g) -> p d g", d=2), lam_im.rearrange("d g p -> p d g"), (), [bt_], slow=True)
            BR = sb2("BR", [64, 64, 16]); BI = sb2("BI", [64, 64, 16])
            dma("sp", BR[:].rearrange("p (d g) h -> p d g h", d=2), b_re.rearrange("d g p h -> p d g h"), (), [bt_])
            dma("act", BI[:].rearrange("p (d g) h -> p d g h", d=2), b_im.rearrange("d g p h -> p d g h"), (), [bt_])
            CR = sb2("CR", [64, 64, 16]); CI = sb2("CI", [64, 64, 16])
            cnat = sb2("cnat", [128, 8, 64])
            for (src, dst) in ((c_re, CR), (c_im, CI)):
                dma("sp", cnat[:], src.rearrange("d g h p -> (d g h) p").rearrange("(k q) p -> q k p", q=128), [bt_], [bt_])
                for k in range(8):
                    pq, bq = nextps()
                    mk.op("pe", lambda E, pq=pq, k=k: E.transpose(out=pq[0:64, 0:128], in_=cnat[:, k, :], identity=ident[:]),
                          [bt_, b_ident], [bq])
                    cp("dve", dst[:, k * 8:(k + 1) * 8, :], pq[0:64, 0:128].rearrange("p (g h) -> p g h", h=16), [bq], [bt_])
            dma("sp", dsk[:], d_skip.rearrange("(g h) -> h g", h=16), (), [bt_], slow=True)
            DT = sb2("DT", [64, 64])
            pq, bq = nextps()
            mm(pq[0:64, 0:64], ones1[:], row[:], True, True, [bt_], [bq])
            act(DT[:], pq[0:64, 0:64], AF.Exp, [bq], [bt_])

            def T64(name):
                return sb2(name, [64, 64])
            ZR = T64("ZR"); ZI = T64("ZI"); EPs = T64("EPs"); COS = T64("COS"); SIN = T64("SIN")
            tA = T64("tA"); tB = T64("tB"); tC = T64("tC"); tI = sb2("tI", [64, 64], mybir.dt.int32)
            tt("dve", ZR[:], LR[:], DT[:], ALU.mult, [bt_], [bt_])
            tt("dve", ZI[:], LI[:], DT[:], ALU.mult, [bt_], [bt_])
            act(EPs[:], ZR[:], AF.Exp, [bt_], [bt_])
            for (dst, offs) in ((SIN, 64.0), (COS, 64.25)):
                ts("dve", tA[:], ZI[:], 1.0 / TWO_PI, offs, ALU.mult, ALU.add, [bt_], [bt_])
                cp("dve", tI[:], tA[:], [bt_], [bt_])
                cp("dve", tB[:], tI[:], [bt_], [bt_])
                tt("dve", tA[:], tA[:], tB[:], ALU.subtract, [bt_], [bt_])
                ts("dve", tB[:], tA[:], 0.5, None, ALU.is_gt, None, [bt_], [bt_])
                tt("dve", tA[:], tA[:], tB[:], ALU.subtract, [bt_], [bt_])
                act(dst[:], tA[:], AF.Sin, [bt_], [bt_], scale=TWO_PI)
            PWr = sb2("PWr", [64, 9, 64]); PWi = sb2("PWi", [64, 9, 64])
            mk.op("pool", lambda E: E.memset(PWr[:, 0, :], 1.0), (), [bt_])
            mk.op("pool", lambda E: E.memset(PWi[:, 0, :], 0.0), (), [bt_])
            tt("dve", PWr[:, 1, :], EPs[:], COS[:], ALU.mult, [bt_], [bt_])
            tt("dve", PWi[:, 1, :], EPs[:], SIN[:], ALU.mult, [bt_], [bt_])

            def cmul(or_, oi_, ar, ai, br, bi, n3=None):
                tt("dve", tA[:], ai, bi, ALU.mult, [bt_], [bt_])
                tt("dve", tB[:], ai, br, ALU.mult, [bt_], [bt_])
                tt("dve", tC[:], ar, br, ALU.mult, [bt_], [bt_])
                tt("dve", or_, tC[:], tA[:], ALU.subtract, [bt_], [bt_])
                tt("dve", tC[:], ar, bi, ALU.mult, [bt_], [bt_])
                tt("dve", oi_, tC[:], tB[:], ALU.add, [bt_], [bt_])
            for j in range(2, 9):
                cmul(PWr[:, j, :], PWi[:, j, :], PWr[:, j - 1, :], PWi[:, j - 1, :], PWr[:, 1, :], PWi[:, 1, :])
            NLV = 10
            cp("dve", SQr[:, 0, :], PWr[:, 8, :], [bt_], [bt_]); cp("dve", SQi[:, 0, :], PWi[:, 8, :], [bt_], [bt_])
            for k in range(1, NLV):
                cmul(SQr[:, k, :], SQi[:, k, :], SQr[:, k - 1, :], SQi[:, k - 1, :], SQr[:, k - 1, :], SQi[:, k - 1, :])
            ts("dve", SQin[:], SQi[:], -1.0, None, ALU.mult, None, [bt_], [bt_])
            CFr = T64("CFr"); CFi = T64("CFi"); DEN = T64("DEN"); NR = T64("NR")
            ts("dve", NR[:], PWr[:, 1, :], -1.0, None, ALU.add, None, [bt_], [bt_])
            tt("dve", tA[:], LR[:], LR[:], ALU.mult, [bt_], [bt_])
            tt("dve", tB[:], LI[:], LI[:], ALU.mult, [bt_], [bt_])
            tt("dve", DEN[:], tA[:], tB[:], ALU.add, [bt_], [bt_])
            mk.op("dve", lambda E: E.reciprocal(out=DEN[:], in_=DEN[:]), [bt_], [bt_])
            tt("dve", tA[:], NR[:], LR[:], ALU.mult, [bt_], [bt_])
            tt("dve", tB[:], PWi[:, 1, :], LI[:], ALU.mult, [bt_], [bt_])
            tt("dve", tA[:], tA[:], tB[:], ALU.add, [bt_], [bt_])
            tt("dve", CFr[:], tA[:], DEN[:], ALU.mult, [bt_], [bt_])
            tt("dve", tA[:], PWi[:, 1, :], LR[:], ALU.mult, [bt_], [bt_])
            tt("dve", tB[:], NR[:], LI[:], ALU.mult, [bt_], [bt_])
            tt("dve", tA[:], tA[:], tB[:], ALU.subtract, [bt_], [bt_])
            tt("dve", CFi[:], tA[:], DEN[:], ALU.mult, [bt_], [bt_])
            BbR = sb2("BbR", [64, 64, 16]); BbI = sb2("BbI", [64, 64, 16])
            X1t = sb2("X1t", [64, 64, 16]); X2t = sb2("X2t", [64, 64, 16])

            def b16(t2):
                return t2[:, :, None].broadcast_to([64, 64, 16])

            def cmul3(or_, oi_neg, ar2, ai2, br3, bi3, e1="dve", e2="pool"):
                tt(e1, X1t[:], br3, b16(ar2), ALU.mult, [bt_], [bt_])
                tt(e1, X2t[:], bi3, b16(ai2), ALU.mult, [bt_], [bt_])
                tt(e1, or_, X1t[:], X2t[:], ALU.subtract, [bt_], [bt_])
                tt(e1, X1t[:], bi3, b16(ar2), ALU.mult, [bt_], [bt_])
                tt(e1, X2t[:], br3, b16(ai2), ALU.mult, [bt_], [bt_])
                if oi_neg[1]:
                    tt(e1, X1t[:], X1t[:], X2t[:], ALU.add, [bt_], [bt_])
                    ts(e1, oi_neg[0], X1t[:], -1.0, None, ALU.mult, None, [bt_], [bt_])
                else:
                    tt(e1, oi_neg[0], X1t[:], X2t[:], ALU.add, [bt_], [bt_])
            cmul3(BbR[:], (BbI[:], False), CFr[:], CFi[:], BR[:], BI[:])
            for j in range(8):
                cmul3(LTr[:, j], (LTi[:, j], False), PWr[:, j, :], PWi[:, j, :], BbR[:], BbI[:])
                cmul3(OTr[:, j], (OTn[:, j], True), PWr[:, j + 1, :], PWi[:, j + 1, :], CR[:], CI[:])
            cp("dve", CRb[:], CR[:], [bt_], [bt_]); ts("dve", CInb[:], CI[:], -1.0, None, ALU.mult, None, [bt_], [bt_])

            mk.flush(final=True)
            es2.close()
            KTg = [sb("KTg%d" % i, [16, 15, 16], BF16) for i in range(2)]; b_KTg = [Buf(), Buf()]
            CTg = [sb("CTg%d" % i, [16, 32, 64], BF16) for i in range(2)]; b_CTg = [Buf(), Buf()]
            UGN = 2048
            ug = sb("ug", [16, UGN]); b_ug = Buf()
            ub = [sb("ub%d" % i, [16, 8, 1024], BF16) for i in range(2)]; b_ub = [Buf(), Buf()]
            yg = sb("yg", [16, 4096]); b_yg = Buf()
            NBM = 1024
            Wt = [[[sb("W%d%d%d" % (pp, d, c), [64, NBM + 1]) for c in range(2)] for d in range(2)] for pp in range(2)]
            b_Wt = [[Buf() for d in range(2)] for pp in range(2)]
            Sb = [[[sb("Sb%d%d%d" % (i, d, c), [64, NBM + 1], BF16) for c in range(2)] for d in range(2)] for i in range(2)]
            b_Sb = [Buf(), Buf()]
            for pp in range(2):
                for d in range(2):
                    for c in range(2):
                        mk.op("pool", lambda E, t=Wt[pp][d][c]: E.memset(t[:], 0.0), (), [b_Wt[pp][d]])
            id16 = ident[0:16, 0:16]

            def gconsts(g):
                par = g % 2
                pk, bpk = nextps()
                for idx in range(15):
                    if idx == 0:
                        terms = [(0, 0), (1, 0)]
                    elif idx < 8:
                        terms = [(0, idx)]
                    else:
                        terms = [(1, idx - 7)]
                    n = 0
                    for (d, tau) in terms:
                        gi = d * 32 + g
                        mm(pk[0:16, idx * 16:(idx + 1) * 16], LTr[:, tau, gi, :], CRb[:, gi, :], n == 0, False, [bt_], [bpk]); n += 1
                        mm(pk[0:16, idx * 16:(idx + 1) * 16], LTi[:, tau, gi, :], CInb[:, gi, :], False, n == 2 * len(terms) - 1, [bt_], [bpk]); n += 1
                cp("dve", KTg[par][:].rearrange("p a b -> p (a b)"), pk[0:16, 0:240], [bpk], [b_KTg[par]])
                stt(KTg[par][:, 0, :], id16, dsk[:, g:g + 1], KTg[par][:, 0, :], ALU.mult, ALU.add, [b_KTg[par], bt_, b_ident], [b_KTg[par]])
                for q in range(4):
                    pc_, bpc = nextps()
                    for j in range(8):
                        i = q * 8 + j
                        d = i // 16; s_ = (i // 2) % 8; c = i % 2
                        e_ = (7 - s_) if d == 0 else s_
                        src = (LTr if c == 0 else LTi)[:, e_, d * 32 + g, :]
                        mm(pc_[0:16, j * 64:(j + 1) * 64], src, identb[:], True, True, [bt_], [bpc])
                    cp("act", CTg[par][:, q * 8:(q + 1) * 8, :].rearrange("p a b -> p (a b)"),
                       pc_[0:16, 0:512], [bpc], [b_CTg[par]])

            def dims(s):
                toff, coff, T = SEQ[s]
                nblk = T // 8
                BW = min(512, nblk)
                return toff, coff, T, nblk, BW, 8 * BW, nblk // BW

            def front(g, s, st):
                par = g % 2
                toff, coff, T, nblk, BW, TW, nbt = dims(s)
                nlv = int(math.log2(nblk))
                UG = min(UGN, T)
                for hf in range(T // UG):
                    dma("sp", ug[:, 0:UG], Pscr[1920 + 16 * g:1936 + 16 * g, coff + 1 + hf * UG:coff + 1 + (hf + 1) * UG],
                        [b_P], [b_ug])
                    cp("act", ub[st][:, :, hf * (UG // 8):(hf + 1) * (UG // 8)], ug[:, 0:UG].rearrange("p (b s) -> p s b", s=8),
                       [b_ug], [b_ub[st]])
                if nblk < NBM:
                    for d in range(2):
                        for c in range(2):
                            mk.op("pool", lambda E, t=Wt[0][d][c]: E.memset(t[:], 0.0), (), [b_Wt[0][d]])
                            mk.op("pool", lambda E, t=Wt[1][d][c]: E.memset(t[:], 0.0), (), [b_Wt[1][d]])
                for bt in range(nbt):
                    for d in range(2):
                        for c in range(2):
                            pw_, bpw = nextps()
                            for s_ in range(8):
                                mm(pw_[0:64, 0:BW], CTg[par][:, (d * 8 + s_) * 2 + c, :],
                                   ub[st][:, s_, bt * BW:(bt + 1) * BW],
                                   s_ == 0, s_ == 7, [b_CTg[par], b_ub[st]], [bpw])
                            o0 = bt * BW + (1 if d == 0 else 0)
                            cp("act", Wt[0][d][c][:, o0:o0 + BW], pw_[0:64, 0:BW], [bpw], [b_Wt[0][d]])
                cur = 0
                for k in range(nlv):
                    sh = 1 << k
                    n_ = nblk - sh
                    for d in range(2):
                        gi = d * 32 + g
                        lo = 1 if d == 0 else 0
                        Wc = Wt[cur][d]; Wn = Wt[1 - cur][d]
                        bWc = b_Wt[cur][d]; bWn = b_Wt[1 - cur][d]
                        if d == 0:
                            dst = slice(lo + sh, lo + nblk); srcs = slice(lo, lo + n_); keep = slice(lo, lo + sh)
                        else:
                            dst = slice(lo, lo + n_); srcs = slice(lo + sh, lo + nblk); keep = slice(lo + n_, lo + nblk)
                        ar = SQr[:, k, gi:gi + 1]; ai = SQi[:, k, gi:gi + 1]; ain = SQin[:, k, gi:gi + 1]
                        stt(Wn[0][:, dst], Wc[0][:, srcs], ar, Wc[0][:, dst], ALU.mult, ALU.add, [bWc, bt_], [bWn])
                        stt(Wn[1][:, dst], Wc[1][:, srcs], ar, Wc[1][:, dst], ALU.mult, ALU.add, [bWc, bt_], [bWn])
                        stt(Wn[0][:, dst], Wc[1][:, srcs], ain, Wn[0][:, dst], ALU.mult, ALU.add, [bWc, bWn, bt_], [bWn])
                        stt(Wn[1][:, dst], Wc[0][:, srcs], ai, Wn[1][:, dst], ALU.mult, ALU.add, [bWc, bWn, bt_], [bWn])
                        cp("pool", Wn[0][:, keep], Wc[0][:, keep], [bWc], [bWn])
                        cp("pool", Wn[1][:, keep], Wc[1][:, keep], [bWc], [bWn])
                    cur = 1 - cur
                for d in range(2):
                    for c in range(2):
                        if d == 0:
                            cp("act", Sb[st][d][c][:, 1:nblk + 1], Wt[cur][d][c][:, 1:nblk + 1], [b_Wt[cur][d]], [b_Sb[st]])
                            mk.op("pool", lambda E, t=Sb[st][d][c]: E.memset(t[:, 0:1], 0.0), (), [b_Sb[st]])
                        else:
                            cp("act", Sb[st][d][c][:, 0:nblk], Wt[cur][d][c][:, 0:nblk], [b_Wt[cur][d]], [b_Sb[st]])
                            mk.op("pool", lambda E, t=Sb[st][d][c], nblk=nblk: E.memset(t[:, nblk:nblk + 1], 0.0), (), [b_Sb[st]])

            def back(g, s, st):
                par = g % 2
                toff, coff, T, nblk, BW, TW, nbt = dims(s)
                for bt in range(nbt):
                    ubv = ub[st][:, :, bt * BW:(bt + 1) * BW]
                    ygv = yg[:, 0:TW].rearrange("p (b s) -> p s b", s=8)
                    for t in range(8):
                        py, bpy = nextps()
                        for s_ in range(8):
                            idx = 0 if s_ == t else ((t - s_) if s_ < t else (7 + s_ - t))
                            mm(py[0:16, 0:BW], KTg[par][:, idx, :], ubv[:, s_, :], s_ == 0, False, [b_KTg[par], b_ub[st]], [bpy])
                        b0 = bt * BW
                        mm(py[0:16, 0:BW], OTr[:, t, g, :], Sb[st][0][0][:, b0:b0 + BW], False, False, [bt_, b_Sb[st]], [bpy])
                        mm(py[0:16, 0:BW], OTn[:, t, g, :], Sb[st][0][1][:, b0:b0 + BW], False, False, [bt_, b_Sb[st]], [bpy])
                        mm(py[0:16, 0:BW], OTr[:, 7 - t, 32 + g, :], Sb[st][1][0][:, b0 + 1:b0 + 1 + BW], False, False, [bt_, b_Sb[st]], [bpy])
                        mm(py[0:16, 0:BW], OTn[:, 7 - t, 32 + g, :], Sb[st][1][1][:, b0 + 1:b0 + 1 + BW], False, True, [bt_, b_Sb[st]], [bpy])
                        cp("act", ygv[:, t, :], py[0:16, 0:BW], [bpy], [b_yg])
                    dma("sp", YS[16 * g:16 * g + 16, toff + bt * TW:toff + (bt + 1) * TW], yg[:, 0:TW], [b_yg], [b_YS])

            units = [(g, s) for g in range(32) for s in range(NS)]
            prev = None
            for ui, (g, s) in enumerate(units):
                if s == 0:
                    gconsts(g)
                front(g, s, ui % 2)
                if prev is not None:
                    back(prev[0], prev[1], (ui - 1) % 2)
                prev = (g, s)
            back(prev[0], prev[1], (len(units) - 1) % 2)
            mk.flush(final=True)

    if upto >= 2:
        s5_pass()

    def mix_pass():
        es, sb, ps = scope("mx_")
        with es:
            N = 256
            pz = [ps("pz%d" % i, [128, 512]) for i in range(8)]
            b_pz = [Buf() for _ in range(8)]
            pctr = [0]

            def nextps():
                i = pctr[0] % 8
                pctr[0] += 1
                return pz[i], b_pz[i]
            bc_ = Buf()
            o64 = sb("o64", [64, 64])
            mk.op("pool", lambda E: E.memset(o64[:], 1.0 / 64.0), (), [bc_])
            ones_bf = sb("ones_bf", [128, 128], BF16)
            mk.op("pool", lambda E: E.memset(ones_bf[:], 1.0), (), [bc_])
            lg = sb("lg", [64, 8]); lb = sb("lb", [64, 8])
            dma("sp", lg[:], lnx_g.rearrange("(h p) -> p h", p=64), (), [bc_], slow=True)
            dma("sp", lb[:], lnx_b.rearrange("(h p) -> p h", p=64), (), [bc_], slow=True)
            g2t = sb("g2t", [128, 512]); dma("sp", g2t[:], g2, (), [bc_])
            mug = sb("mug", [128, 1]); hmg = sb("hmg", [128, 1]); omg = sb("omg", [128, 1])
            dma("sp", mug[:], mu_shift[1792:1920].rearrange("(p o) -> p o", o=1), (), [bc_], slow=True)
            ts("dve", hmg[:], mug[:], 0.5, None, ALU.mult, None, [bc_], [bc_])
            ts("dve", omg[:], mug[:], -1.0, 1.0, ALU.mult, ALU.add, [bc_], [bc_])
            bgl = sb("bgl", [128, 4]); s5g = sb("s5g", [128, 4])
            dma("sp", bgl[:], b_glu.rearrange("(q p) -> p q", p=128), (), [bc_], slow=True)
            dma("sp", s5g[:], s5_out_g.rearrange("(q p) -> p q", p=128), (), [bc_], slow=True)
            wst = sb("wst", [128, 4, 128])
            mk.op("pool", lambda E: E.memset(wst[:], 0.0), (), [bc_])
            for g in range(32):
                r0 = (g % 8) * 16
                dma("sp" if g % 2 == 0 else "act", wst[r0:r0 + 16, g // 8, r0:r0 + 16], w_glu[g], [bc_], [bc_])
            Wbd = sb("Wbd", [128, 4, 128], BF16)
            cp("dve", Wbd[:], wst[:], [bc_], [bc_])
            wo_r = sb("wo_r", [64, 8, D], BF16); wo_s = sb("wo_s", [128, 4, D], BF16)
            stg = [sb("stg%d" % i, [128, D]) for i in range(2)]; b_stg = [Buf(), Buf()]
            for h in range(8):
                st_ = stg[h % 2]; bs_ = b_stg[h % 2]
                dma("sp", st_[0:64, :], w_out[h * 64:(h + 1) * 64, :], (), [bs_])
                cp("pool", wo_r[:, h, :], st_[0:64, :], [bs_], [bc_])
            for q in range(4):
                st_ = stg[q % 2]; bs_ = b_stg[q % 2]
                dma("sp", st_[:], w_out[512 + q * 128:512 + (q + 1) * 128, :], (), [bs_])
                cp("pool", wo_s[:, q, :], st_[:], [bs_], [bc_])

            def T4(name, dt=F32):
                return sb(name, [64, 8, N], dt), Buf()
            YF, b_YF = T4("YF"); YB, b_YB = T4("YB"); BF_, b_BF = T4("BF"); BB, b_BB = T4("BB")
            Ym, b_Ym = T4("Ym"); SQt, b_SQt = T4("SQt"); RS, b_RS = T4("RS")
            yr, b_yr = T4("yr", BF16)
            XG = sb("XG", [128, N + 2]); b_XG = Buf(); tw = sb("tw", [128, N]); b_tw = Buf()
            sg = sb("sg", [128, N]); b_sg = Buf()
            S5 = sb("S5", [128, 4, N]); b_S5 = Buf(); Z1 = sb("Z1", [128, 4, N]); b_Z1 = Buf()
            Z2 = sb("Z2", [128, 4, N]); b_Z2 = Buf(); Zb = sb("Zb", [128, 4, N], BF16); b_Zb = Buf()
            r2 = sb("r2", [128, N]); b_r2 = Buf()
            ysb = sb("ysb", [128, 4, N], BF16); b_ysb = Buf()
            xTt = sb("xTt", [128, 8, N]); b_xTt = Buf(); X1t = sb("X1t", [128, 8, N]); b_X1t = Buf()

            def bc(t, n):
                return t[:, :, None].broadcast_to([64, 8, n])

            def fl(t):
                return t[:].rearrange("p h t -> p (h t)")
            for s, (toff, coff, T) in enumerate(SEQ):
                for ti in range(T // N):
                    tl = ti * N; tg = toff + tl; c0 = coff + tl
                    dma("sp", YF[:], YD[0, :, :, tg:tg + N], [b_YD], [b_YF])
                    dma("act", YB[:], YD[1, :, :, tg:tg + N], [b_YD], [b_YB])
                    dma("sp", BF_[:], BD[0, :, :, tg:tg + N], [b_BD], [b_BF])
                    dma("act", BB[:], BD[1, :, :, tg:tg + N], [b_BD], [b_BB])
                    dma("sp", XG[:], Pscr[1792:1920, c0:c0 + N + 2], [b_P], [b_XG])
                    dma("act", S5[:], YS[:, tg:tg + N].rearrange("(q p) t -> p q t", p=128), [b_YS], [b_S5])
                    dma("sp", xTt[:], XT[:, tg:tg + N].rearrange("(k p) t -> p k t", p=128), [b_XT], [b_xTt])
                    tt("pool", Ym[:], YF[:], YB[:], ALU.add, [b_YF, b_YB], [b_Ym])
                    for j in range(4):
                        pm_, bpm = nextps()
                        mm(pm_[0:64, :], o64[:], fl(Ym)[:, j * 512:(j + 1) * 512], True, True, [bc_, b_Ym], [bpm])
                        tt("dve", fl(YF)[:, j * 512:(j + 1) * 512], fl(Ym)[:, j * 512:(j + 1) * 512], pm_[0:64, :],
                           ALU.subtract, [bpm, b_Ym], [b_YF])
                    tt("pool", SQt[:], YF[:], YF[:], ALU.mult, [b_YF], [b_SQt])
                    for j in range(4):
                        pm_, bpm = nextps()
                        mm(pm_[0:64, :], o64[:], fl(SQt)[:, j * 512:(j + 1) * 512], True, True, [bc_, b_SQt], [bpm])
                        act(fl(RS)[:, j * 512:(j + 1) * 512], pm_[0:64, :], AF.Sqrt, [bpm], [b_RS], bias=LNX_EPS)
                    mk.op("dve", lambda E: E.reciprocal(out=RS[:], in_=RS[:]), [b_RS], [b_RS])
                    tt("pool", Ym[:], YF[:], RS[:], ALU.mult, [b_YF, b_RS], [b_Ym])
                    tt("pool", Ym[:], Ym[:], bc(lg, N), ALU.mult, [b_Ym, bc_], [b_Ym])
                    tt("pool", Ym[:], Ym[:], bc(lb, N), ALU.add, [b_Ym, bc_], [b_Ym])
                    tt("pool", Ym[:], Ym[:], BF_[:], ALU.add, [b_Ym, b_BF], [b_Ym])
                    tt("pool", Ym[:], Ym[:], BB[:], ALU.add, [b_Ym, b_BB], [b_Ym])
                    tt("dve", tw[:], XG[:, 0:N], XG[:, 2:N + 2], ALU.add, [b_XG], [b_tw])
                    ts("dve", tw[:], tw[:], hmg[:, 0:1], None, ALU.mult, None, [b_tw, bc_], [b_tw])
                    stt(sg[:], XG[:, 1:N + 1], omg[:, 0:1], tw[:], ALU.mult, ALU.add, [b_XG, b_tw, bc_], [b_sg])
                    act(sg[:], sg[:], AF.Sigmoid, [b_sg], [b_sg])
                    for h2_ in range(4):
                        pm_, bpm = nextps()
                        for j in range(2):
                            h = 2 * h2_ + j
                            mm(pm_[0:64, j * N:(j + 1) * N], g2t[:, h * 64:(h + 1) * 64], sg[:], True, True, [bc_, b_sg], [bpm])
                        tt("dve", yr[:, 2 * h2_:2 * h2_ + 2, :], pm_[0:64, :].rearrange("p (h n) -> p h n", n=N),
                           Ym[:, 2 * h2_:2 * h2_ + 2, :], ALU.mult, [bpm, b_Ym], [b_yr])
                    K0 = 2.0 * math.sqrt(2.0 / math.pi)
                    tt("pool", Z1[:], S5[:], S5[:], ALU.mult, [b_S5], [b_Z1])
                    ts("dve", Z1[:], Z1[:], 0.044715, 1.0, ALU.mult, ALU.add, [b_Z1], [b_Z1])
                    tt("pool", Z1[:], Z1[:], S5[:], ALU.mult, [b_Z1, b_S5], [b_Z1])
                    act(Z1[:], Z1[:], AF.Sigmoid, [b_Z1], [b_Z1], scale=K0)
                    tt("pool", Z1[:], Z1[:], S5[:], ALU.mult, [b_Z1, b_S5], [b_Z1])
                    cp("dve", Zb[:], Z1[:], [b_Z1], [b_Zb])
                    for q in range(4):
                        pm_, bpm = nextps()
                        mm(pm_[:, 0:N], Wbd[:, q, :], Zb[:, q, :], True, True, [bc_, b_Zb], [bpm])
                        mk.op("act", lambda E, pm_=pm_, q=q: E.activation(out=Z2[:, q, :], in_=pm_[:, 0:N], func=AF.Sigmoid,
                                                                        bias=bgl[:, q:q + 1], scale=1.0), [bpm, bc_], [b_Z2])
                    tt("pool", Z2[:], Z2[:], Z1[:], ALU.mult, [b_Z2, b_Z1], [b_Z2])
                    tt("pool", Zb[:], Z2[:], Z2[:], ALU.mult, [b_Z2], [b_Zb])
                    pm_, bpm = nextps()
                    for q in range(4):
                        mm(pm_[:, 0:N], ones_bf[:], Zb[:, q, :], q == 0, q == 3, [bc_, b_Zb], [bpm])
                    act(r2[:], pm_[:, 0:N], AF.Sqrt, [bpm], [b_r2], bias=RMS_EPS, scale=1.0 / 512.0)
                    mk.op("dve", lambda E: E.reciprocal(out=r2[:], in_=r2[:]), [b_r2], [b_r2])
                    for q in range(4):
                        stt(ysb[:, q, :], Z2[:, q, :], s5g[:, q:q + 1], r2[:], ALU.mult, ALU.mult, [b_Z2, b_r2, bc_], [b_ysb])
                    for dm in range(8):
                        pm_, bpm = nextps()
                        for h in range(8):
                            mm(pm_[:, 0:N], wo_r[:, h, dm * 128:(dm + 1) * 128], yr[:, h, :], h == 0, False, [bc_, b_yr], [bpm])
                        for q in range(4):
                            mm(pm_[:, 0:N], wo_s[:, q, dm * 128:(dm + 1) * 128], ysb[:, q, :], False, q == 3, [bc_, b_ysb], [bpm])
                        stt(X1t[:, dm, :], pm_[:, 0:N], modT[:, 16 + dm, s:s + 1], xTt[:, dm, :], ALU.mult, ALU.add,
                            [bpm, b_mod, b_xTt], [b_X1t])
                    dma("sp", X1[:, tg:tg + N].rearrange("(k p) t -> p k t", p=128), X1t[:], [b_X1t], [b_X1])
            mk.flush(final=True)

    if upto >= 3:
        mix_pass()

    def ffn_pass():
        es, sb, ps = scope("ff_")
        with es:
            N = 256
            pz = [ps("pz%d" % i, [128, 512]) for i in range(8)]
            b_pz = [Buf() for _ in range(8)]
            pctr = [0]

            def nextps():
                i = pctr[0] % 8
                pctr[0] += 1
                return pz[i], b_pz[i]
            bc_ = Buf()
            ones_bf = sb("ones_bf", [128, 128], BF16)
            mk.op("pool", lambda E: E.memset(ones_bf[:], 1.0), (), [bc_])
            n2g = sb("n2g", [128, 8]); fg = sb("fg", [128, 8]); sc2 = sb("sc2", [128, 8, NS])
            dma("sp", n2g[:], norm2_g.rearrange("(k p) -> p k", p=128), (), [bc_], slow=True)
            dma("sp", fg[:], final_g.rearrange("(k p) -> p k", p=128), (), [bc_], slow=True)
            for s in range(NS):
                stt(sc2[:, :, s], modT[:, 32:40, s], 1.0, n2g[:], ALU.add, ALU.mult, [b_mod, bc_], [bc_])
            w1 = sb("w1", [128, 8, DFF], BF16); w3 = sb("w3", [128, 8, DFF], BF16); w2_ = sb("w2_", [128, 22, D], BF16)
            stg = [sb("stg%d" % i, [128, 1408]) for i in range(2)]; b_stg = [Buf(), Buf()]
            n = 0
            for (src, dst) in ((w_ff1, w1), (w_ff3, w3)):
                for k in range(8):
                    for hf in range(2):
                        st_ = stg[n % 2]; bs_ = b_stg[n % 2]; n += 1
                        dma("sp" if n % 2 else "act", st_[:], src[k * 128:(k + 1) * 128, hf * 1408:(hf + 1) * 1408], (), [bs_])
                        cp("pool" if n % 2 else "dve", dst[:, k, hf * 1408:(hf + 1) * 1408], st_[:], [bs_], [bc_])
            for k in range(22):
                st_ = stg[n % 2]; bs_ = b_stg[n % 2]; n += 1
                dma("sp" if n % 2 else "act", st_[:, 0:D], w_ff2[k * 128:(k + 1) * 128, :], (), [bs_])
                cp("pool" if n % 2 else "dve", w2_[:, k, :], st_[:, 0:D], [bs_], [bc_])
            X1t = sb("X1t", [128, 8, N]); b_X1t = Buf()
            sq = sb("sq", [128, 8, N], BF16); b_sq = Buf()
            rstd = sb("rstd", [128, N]); b_rstd = Buf(); tmp = sb("tmp", [128, N]); b_tmp = Buf()
            h2 = sb("h2", [128, 8, N], BF16); b_h2 = Buf()
            fm = sb("fm", [128, 22, N], BF16); b_fm = Buf()
            av = [sb("av%d" % i, [128, N]) for i in range(2)]; b_av = [Buf(), Buf()]
            X2t = sb("X2t", [128, 8, N]); b_X2t = Buf()
            ytm = sb("ytm", [128, 2, D]); b_ytm = Buf()

            def rms_bc(src, bsrc):
                for dc in range(8):
                    act(sq[:, dc, :], src[:, dc, :], AF.Square, [bsrc], [b_sq])
                pm_, bpm = nextps()
                for dc in range(8):
                    mm(pm_[:, 0:N], ones_bf[:], sq[:, dc, :], dc == 0, dc == 7, [bc_, b_sq], [bpm])
                act(rstd[:], pm_[:, 0:N], AF.Sqrt, [bpm], [b_rstd], bias=RMS_EPS, scale=1.0 / D)
                mk.op("dve", lambda E: E.reciprocal(out=rstd[:], in_=rstd[:]), [b_rstd], [b_rstd])
            for s, (toff, coff, T) in enumerate(SEQ):
                for ti in range(T // N):
                    tg = toff + ti * N
                    dma("sp", X1t[:], X1[:, tg:tg + N].rearrange("(k p) t -> p k t", p=128), [b_X1], [b_X1t])
                    rms_bc(X1t, b_X1t)
                    for dc in range(8):
                        tt("dve", tmp[:], X1t[:, dc, :], rstd[:], ALU.mult, [b_X1t, b_rstd], [b_tmp])
                        ts("dve", h2[:, dc, :], tmp[:], sc2[:, dc, s:s + 1], modT[:, 24 + dc, s:s + 1], ALU.mult, ALU.add,
                           [b_tmp, bc_, b_mod], [b_h2])
                    for mc in range(22):
                        p1, bp1 = nextps()
                        for dc in range(8):
                            mm(p1[:, 0:N], w1[:, dc, mc * 128:(mc + 1) * 128], h2[:, dc, :], dc == 0, dc == 7, [bc_, b_h2], [bp1])
                        p3, bp3 = nextps()
                        for dc in range(8):
                            mm(p3[:, 0:N], w3[:, dc, mc * 128:(mc + 1) * 128], h2[:, dc, :], dc == 0, dc == 7, [bc_, b_h2], [bp3])
                        a_ = av[mc % 2]; ba_ = b_av[mc % 2]
                        act(a_[:], p1[:, 0:N], AF.Silu, [bp1], [ba_])
                        tt("dve", fm[:, mc, :], a_[:], p3[:, 0:N], ALU.mult, [ba_, bp3], [b_fm])
                    for dm in range(8):
                        pm_, bpm = nextps()
                        for mc in range(22):
                            mm(pm_[:, 0:N], w2_[:, mc, dm * 128:(dm + 1) * 128], fm[:, mc, :], mc == 0, mc == 21, [bc_, b_fm], [bpm])
                        stt(X2t[:, dm, :], pm_[:, 0:N], modT[:, 40 + dm, s:s + 1], X1t[:, dm, :], ALU.mult, ALU.add,
                            [bpm, b_mod, b_X1t], [b_X2t])
                    rms_bc(X2t, b_X2t)
                    for dc in range(8):
                        stt(X2t[:, dc, :], X2t[:, dc, :], fg[:, dc:dc + 1], rstd[:], ALU.mult, ALU.mult,
                            [b_X2t, bc_, b_rstd], [b_X2t])
                    for j in range(N // 128):
                        for dq in range(2):
                            pm_, bpm = nextps()
                            for k in range(4):
                                dc = dq * 4 + k
                                mk.op("pe", lambda E, pm_=pm_, dc=dc, j=j, k=k: E.transpose(
                                    out=pm_[:, k * 128:(k + 1) * 128], in_=X2t[:, dc, j * 128:(j + 1) * 128], identity=ident[:]),
                                    [b_X2t, b_ident], [bpm])
                            cp("act" if dq == 0 else "dve", ytm[:, j, dq * 512:(dq + 1) * 512], pm_[:], [bpm], [b_ytm])
                    dma("sp", y_out[tg:tg + N, :].rearrange("(j p) d -> p j d", p=128), ytm[:], [b_ytm], [])
            mk.flush(final=True)

    if upto >= 4:
        ffn_pass()

    outer.close()
    return nc


def core_inputs(P, x, c):
    f = np.float32
    m = {"x": x, "c": c}
    for k in ("norm1_g", "w_ada", "b_ada", "w_in", "mu_shift", "w0", "w2", "a0", "a2", "g2", "k_k", "k_a",
              "lnx_g", "lnx_b", "lam_re", "lam_im", "log_dt", "b_re", "b_im", "c_re", "c_im", "w_glu",
              "s5_out_g", "w_out", "norm2_g", "w_ff1", "w_ff3", "w_ff2", "final_g"):
        m[k] = P[k]
    m["r_k"] = P["r_k"].reshape(512)
    m["d_skip"] = P["d_skip"].reshape(512)
    m["b_glu"] = P["b_glu"].reshape(512)
    return {k: np.ascontiguousarray(v, dtype=f) for k, v in m.items()}


_T_PROMPT = 8192
_T_SAMPLE = 4096


def kernel(**inputs):
    n = 8
    P = {}
    for k, v in inputs.items():
        if k in ("x_prompt", "x_sample", "c_prompt", "c_sample"):
            continue
        v = np.asarray(v)
        P[k] = v if k == "final_g" else v[0]
    xp = np.asarray(inputs["x_prompt"]); xs = np.asarray(inputs["x_sample"])
    cpr = np.asarray(inputs["c_prompt"]); cs = np.asarray(inputs["c_sample"])
    TS = [xp.shape[1], xs.shape[1]]
    nc = build_program(TS, upto=int(os.environ.get('KUPTO', '99')))
    in_maps = []
    for b in range(n):
        x = np.concatenate([xp[b], xs[b]], axis=0)
        c = np.stack([cpr[b], cs[b]], axis=0)
        in_maps.append(core_inputs(P, x, c))
    res = run_bass_kernel_spmd(nc, in_maps, core_ids=list(range(n)))
    yp = np.stack([res.results[b]["y"][:TS[0]] for b in range(n)], axis=0).astype(np.float32)
    ys = np.stack([res.results[b]["y"][TS[0]:] for b in range(n)], axis=0).astype(np.float32)
    return (yp, ys)
```

```python
import os
import math
import numpy as np
import concourse.bass as bass
import concourse.mybir as mybir
from concourse.bass_utils import run_bass_kernel_spmd

F32 = mybir.dt.float32
BF16 = mybir.dt.bfloat16
AF = mybir.ActivationFunctionType
ALU = mybir.AluOpType
AX = mybir.AxisListType

D = 1024
DFF = 2816
NPROJ = 2432
RW = 1920
NDS = 12
RMS_EPS = 1e-6
LNX_EPS = 64e-5


class Buf:
    __slots__ = ("w", "r")

    def __init__(self):
        self.w = None
        self.r = {}


class MK:
    BLK = {"pe": "tensor", "dve": "vector", "act": "scalar", "pool": "gpsimd", "sp": "sync"}

    def __init__(self, nc, same=True):
        self.nc = nc
        self.same = same
        self.names = ["pe", "dve", "act", "pool", "sp"]
        self.sem = {k: nc.alloc_semaphore(name="s_" + k) for k in self.names}
        self.cnt = {k: 0 for k in self.names}
        self.seen = {k: {} for k in self.names}
        self.prog = {k: [] for k in self.names}
        self.dsem = [nc.alloc_semaphore(name="d%d" % i) for i in range(NDS)]
        self.dcnt = [0] * NDS
        self.dnext = 0
        self.deferred = None

    def semof(self, key):
        if isinstance(key, tuple):
            return self.dsem[key[1]]
        return self.sem[key]

    def _deps(self, e, reads, writes):
        deps = {}

        def add(k, v):
            if deps.get(k, 0) < v:
                deps[k] = v

        for b in reads:
            if b.w:
                add(*b.w)
        for b in writes:
            if b.w:
                add(*b.w)
            for k, v in b.r.items():
                add(k, v)
        out = []
        for k, v in deps.items():
            if k == e and (e == "pe" or not self.same):
                continue
            if self.seen[e].get(k, 0) >= v:
                continue
            self.seen[e][k] = v
            out.append((k, v))
        return out

    def _mark(self, tok, reads, writes):
        k, v = tok
        for b in reads:
            if b.r.get(k, 0) < v:
                b.r[k] = v
        for b in writes:
            b.w = tok
            b.r = {}

    def op(self, e, fn, reads=(), writes=()):
        if self.deferred is not None:
            self.deferred.append((0, e, fn, reads, writes))
            return
        waits = self._deps(e, reads, writes)
        self.cnt[e] += 1
        tok = (e, self.cnt[e])
        self.prog[e].append((waits, fn, self.sem[e], 1))
        self._mark(tok, reads, writes)

    def replay(self, pending, n):
        keep = self.deferred
        self.deferred = None
        last = None
        cnt = 0
        while pending and (cnt < n or last == "pe"):
            kind, e, fn, reads, writes = pending.pop(0)
            (self.dma if kind else self.op)(e, fn, reads, writes)
            last = e if not kind else None
            cnt += 1
        self.deferred = keep

    def dma(self, q, fn, reads=(), writes=()):
        if self.deferred is not None:
            self.deferred.append((1, q, fn, reads, writes))
            return
        i = self.dnext
        self.dnext = (i + 1) % NDS
        key = ("d", i)
        waits = self._deps(q, reads, writes)
        if self.dcnt[i] > 0 and self.seen[q].get(key, 0) < self.dcnt[i]:
            waits.append((key, self.dcnt[i]))
            self.seen[q][key] = self.dcnt[i]
        self.dcnt[i] += 16
        tok = (key, self.dcnt[i])
        self.prog[q].append((waits, fn, self.dsem[i], 16))
        self._mark(tok, reads, writes)

    def flush(self, final=False):
        nc = self.nc
        fin = []
        for i in (range(NDS) if final else []):
            if self.dcnt[i] > 0:
                fin.append((("d", i), self.dcnt[i]))
        for k in (self.names if final else []):
            if k != "sp" and self.cnt[k] > 0:
                fin.append((k, self.cnt[k]))
        with nc.Block() as block:
            for e in self.names:
                prog = self.prog[e]
                extra = fin if e == "sp" else []

                def body(eng, prog=prog, extra=extra):
                    for waits, fn, sem, inc in prog:
                        for k, v in waits:
                            eng.wait_ge(self.semof(k), v)
                        fn(eng).then_inc(sem, inc)
                    for k, v in extra:
                        eng.wait_ge(self.semof(k), v)

                getattr(block, self.BLK[e])(body)
        self.prog = {k: [] for k in self.names}

    def emit(self):
        self.flush(final=True)


def build_program(TS, dbg=False, upto=99):
    import contextlib
    nc = bass.Bass("TRN2", target_bir_lowering=False)
    mk = MK(nc, same=(os.environ.get("MK_SAME", "1") == "1"))
    TT = sum(TS)
    NS = len(TS)
    WP = TT + 2 * NS
    SEQ = []
    o = 0
    for s, T in enumerate(TS):
        SEQ.append((o, o + 2 * s, T))
        o += T

    def din(name, shape):
        return nc.dram_tensor(name, list(shape), F32, kind="ExternalInput").ap()

    def dscr(name, shape):
        return nc.dram_tensor(name, list(shape), F32, kind=("ExternalOutput" if dbg else "Internal")).ap()

    x_in = din("x", (TT, D))
    c_in = din("c", (NS, D))
    norm1_g = din("norm1_g", (D,))
    w_ada = din("w_ada", (D, 6 * D))
    b_ada = din("b_ada", (6 * D,))
    w_in = din("w_in", (D, NPROJ))
    mu_shift = din("mu_shift", (RW,))
    w0 = din("w0", (2, 512)); w2 = din("w2", (2, 64, 512))
    a0 = din("a0", (2, 512)); a2 = din("a2", (2, 64, 512))
    g2 = din("g2", (128, 512))
    k_k = din("k_k", (512,)); k_a = din("k_a", (512,)); r_k = din("r_k", (512,))
    lnx_g = din("lnx_g", (512,)); lnx_b = din("lnx_b", (512,))
    lam_re = din("lam_re", (2, 32, 64)); lam_im = din("lam_im", (2, 32, 64)); log_dt = din("log_dt", (2, 32))
    b_re = din("b_re", (2, 32, 64, 16)); b_im = din("b_im", (2, 32, 64, 16))
    c_re = din("c_re", (2, 32, 16, 64)); c_im = din("c_im", (2, 32, 16, 64))
    d_skip = din("d_skip", (512,)); w_glu = din("w_glu", (32, 16, 16)); b_glu = din("b_glu", (512,))
    s5_out_g = din("s5_out_g", (512,))
    w_out = din("w_out", (D, D)); norm2_g = din("norm2_g", (D,))
    w_ff1 = din("w_ff1", (D, DFF)); w_ff3 = din("w_ff3", (D, DFF)); w_ff2 = din("w_ff2", (DFF, D))
    final_g = din("final_g", (D,))
    y_out = nc.dram_tensor("y", [TT, D], F32, kind="ExternalOutput").ap()

    Pscr = dscr("Pscr", (NPROJ, WP))
    XT = dscr("XT", (D, TT))
    YD = dscr("YD", (2, 64, 8, TT))
    BD = dscr("BD", (2, 64, 8, TT))
    YS = dscr("YS", (512, TT))
    X1 = dscr("X1", (D, TT))
    MODS = dscr("MODS", (128, 48 * NS))
    b_P = Buf(); b_XT = Buf(); b_YD = Buf(); b_BD = Buf(); b_YS = Buf(); b_X1 = Buf(); b_MODS = Buf()

    def tt(e, out, a, b, op, r, w):
        mk.op(e, lambda E: E.tensor_tensor(out=out, in0=a, in1=b, op=op), r, w)

    def ts(e, out, a, s1, s2, op0, op1, r, w):
        if op1 is None:
            mk.op(e, lambda E: E.tensor_scalar(out=out, in0=a, scalar1=s1, scalar2=None, op0=op0), r, w)
        else:
            mk.op(e, lambda E: E.tensor_scalar(out=out, in0=a, scalar1=s1, scalar2=s2, op0=op0, op1=op1), r, w)

    def stt(out, a, sc, b, op0, op1, r, w):
        mk.op("dve", lambda E: E.scalar_tensor_tensor(out=out, in0=a, scalar=sc, in1=b, op0=op0, op1=op1), r, w)

    def act(out, a, func, r, w, bias=0.0, scale=1.0):
        mk.op("act", lambda E: E.activation(out=out, in_=a, func=func, bias=bias, scale=scale), r, w)

    def cp(e, out, a, r, w):
        if e == "act":
            mk.op("act", lambda E: E.activation(out=out, in_=a, func=AF.Copy), r, w)
        else:
            mk.op(e, lambda E: E.tensor_copy(out=out, in_=a), r, w)

    def mm(out, lhsT, rhs, st, sp_, r, w):
        mk.op("pe", lambda E: E.matmul(out=out, lhsT=lhsT, rhs=rhs, start=st, stop=sp_), r, w)

    F32R = mybir.dt.float32r
    USE_R = os.environ.get("RW_F32R", "1") == "1"

    def RR(ap):
        return ap.bitcast(F32R) if USE_R else ap

    def mmr(out, lhsT, rhs, r, w, st=True, sp_=True):
        mk.op("pe", lambda E: E.matmul(out=out, lhsT=lhsT.bitcast(F32R), rhs=rhs.bitcast(F32R), start=st, stop=sp_), r, w)

    def dma(q, out, in_, r, w, slow=False):
        if slow:
            mk.dma(q, lambda E: E.dma_start(out=out, in_=in_, allow_slow_non_contiguous=True), r, w)
        else:
            mk.dma(q, lambda E: E.dma_start(out=out, in_=in_), r, w)

    def scope(pfx=""):
        es = contextlib.ExitStack()

        def sb(name, shape, dt=F32):
            return es.enter_context(nc.sbuf_tensor(pfx + name, list(shape), dt))

        def ps(name, shape, dt=F32):
            return es.enter_context(nc.psum_tensor(pfx + name, list(shape), dt))
        return es, sb, ps

    def consts(sb):
        ident = sb("ident", [128, 128]); b_ident = Buf()
        mk.op("pool", lambda E: E.memset(ident[:], 1.0), (), [b_ident])
        mk.op("pool", lambda E: E.affine_select(out=ident[:], in_=ident[:], pattern=[[-1, 128]],
                                                compare_op=ALU.is_equal, fill=0.0, base=0, channel_multiplier=1),
              [b_ident], [b_ident])
        return ident, b_ident
    outer, osb, ops_ = scope("o_")
    ident, b_ident = consts(osb)
    modT = osb("modT", [128, 48, NS]); b_mod = Buf()
    sc1 = osb("sc1", [128, 8, NS]); b_sc1 = Buf()

    def pass0():
        es, sb, ps = scope("p0_")
        with es:
            ones_bf = sb("ones_bf", [128, 128], BF16); b_ones = Buf()
            mk.op("pool", lambda E: E.memset(ones_bf[:], 1.0), (), [b_ones])
            cT = sb("cT", [128, 8, NS]); b_cT = Buf()
            scT = sb("scT", [128, 8, NS]); b_scT = Buf()
            for s in range(NS):
                dma("sp", cT[:, :, s], c_in[s].rearrange("(k p) -> p k", p=128), (), [b_cT], slow=True)
            act(scT[:], cT[:], AF.Silu, [b_cT], [b_scT])
            badaT = sb("badaT", [128, 48]); b_bada = Buf()
            dma("sp", badaT[:], b_ada.rearrange("(k p) -> p k", p=128), (), [b_bada], slow=True)
            g1T = sb("g1T", [128, 8]); b_g1 = Buf()
            dma("sp", g1T[:], norm1_g.rearrange("(k p) -> p k", p=128), (), [b_g1], slow=True)
            wada_t = [sb("wada%d" % i, [128, 8, 256]) for i in range(2)]
            b_wada = [Buf(), Buf()]
            ps_mod_full = ps("ps_mod", [128, 512]); b_psmod = Buf()
            ps_mod = ps_mod_full[:, 0:4 * NS].rearrange("p (a b) -> p a b", b=NS)
            for slab in range(24):
                wt = wada_t[slab % 2]; bw = b_wada[slab % 2]
                dma("sp" if slab % 2 == 0 else "act", wt[:],
                    w_ada[:, slab * 256:(slab + 1) * 256].rearrange("(k p) n -> p k n", p=128), (), [bw])
                for j in range(2):
                    for k in range(8):
                        mm(ps_mod[:, j, :], wt[:, k, j * 128:(j + 1) * 128], scT[:, k, :], k == 0, k == 7,
                           [bw, b_scT], [b_psmod])
                for s in range(NS):
                    tt("dve", modT[:, slab * 2:(slab + 1) * 2, s], ps_mod[:, 0:2, s],
                       badaT[:, slab * 2:(slab + 1) * 2], ALU.add, [b_psmod, b_bada], [b_mod])
            for s in range(NS):
                stt(sc1[:, :, s], modT[:, 8:16, s], 1.0, g1T[:], ALU.add, ALU.mult, [b_mod, b_g1], [b_sc1])

            w_in_bf = sb("w_in_bf", [128, 8, NPROJ], BF16); b_win = Buf()
            wst = [sb("wst%d" % i, [128, NPROJ]) for i in range(2)]; b_wst = [Buf(), Buf()]
            for k in range(8):
                dma("sp", wst[k % 2][:], w_in[k * 128:(k + 1) * 128, :], (), [b_wst[k % 2]])
                cp("pool", w_in_bf[:, k, :], wst[k % 2][:], [b_wst[k % 2]], [b_win])

            NT = 512
            xtm = [sb("xtm%d" % i, [128, 4, D]) for i in range(2)]; b_xtm = [Buf(), Buf()]
            xT = sb("xT", [128, 8, NT]); b_xT = Buf()
            sq = sb("sq", [128, 8, NT], BF16); b_sq = Buf()
            rstd = sb("rstd", [128, NT]); b_rstd = Buf()
            tmp = sb("tmp0", [128, NT]); b_tmp = Buf()
            hT = sb("hT", [128, 8, NT], BF16); b_hT = Buf()
            pev = [sb("pev%d" % i, [128, NT]) for i in range(3)]; b_pev = [Buf() for _ in range(3)]
            zcol = sb("zcol", [128, 1]); b_zcol = Buf()
            mk.op("pool", lambda E: E.memset(zcol[:], 0.0), (), [b_zcol])
            pst = [ps("pst%d" % i, [128, NT]) for i in range(4)]; b_pst = [Buf() for _ in range(4)]
            psm = [ps("psm%d" % i, [128, NT]) for i in range(3)]; b_psm = [Buf() for _ in range(3)]
            for s, (toff, coff, T) in enumerate(SEQ):
                for mc in range(19):
                    for cc in (coff, coff + T + 1):
                        dma("sp", Pscr[mc * 128:(mc + 1) * 128, cc:cc + 1], zcol[:], [b_zcol], [b_P], slow=True)
                for ti in range(T // NT):
                    t0 = toff + ti * NT
                    xt = xtm[ti % 2]; bx = b_xtm[ti % 2]
                    dma("sp", xt[:], x_in[t0:t0 + NT, :].rearrange("(j p) d -> p j d", p=128), (), [bx])
                    for dc in range(8):
                        pt = pst[dc % 4]; bp = b_pst[dc % 4]
                        for j in range(4):
                            mk.op("pe", lambda E, pt=pt, xt=xt, j=j, dc=dc: E.transpose(
                                out=pt[:, j * 128:(j + 1) * 128], in_=xt[:, j, dc * 128:(dc + 1) * 128],
                                identity=ident[:]), [bx, b_ident], [bp])
                        cp("dve", xT[:, dc, :], pt[:], [bp], [b_xT])
                        act(sq[:, dc, :], pt[:], AF.Square, [bp, b_xT], [b_sq])
                    dma("act", XT[:, t0:t0 + NT].rearrange("(k p) t -> p k t", p=128), xT[:], [b_xT], [b_XT])
                    pm = psm[0]; bpm = b_psm[0]
                    for dc in range(8):
                        mm(pm[:], ones_bf[:], sq[:, dc, :], dc == 0, dc == 7, [b_sq, b_ones], [bpm])
                    act(rstd[:], pm[:], AF.Sqrt, [bpm], [b_rstd], bias=RMS_EPS, scale=1.0 / D)
                    mk.op("dve", lambda E: E.reciprocal(out=rstd[:], in_=rstd[:]), [b_rstd], [b_rstd])
                    for dc in range(8):
                        tt("dve", tmp[:], xT[:, dc, :], rstd[:], ALU.mult, [b_xT, b_rstd], [b_tmp])
                        ts("dve", hT[:, dc, :], tmp[:], sc1[:, dc, s:s + 1], modT[:, dc, s:s + 1], ALU.mult, ALU.add,
                           [b_tmp, b_sc1, b_mod], [b_hT])
                    for mc in range(19):
                        i3 = mc % 3
                        pm = psm[i3]; bpm = b_psm[i3]
                        for dc in range(8):
                            mm(pm[:], w_in_bf[:, dc, mc * 128:(mc + 1) * 128], hT[:, dc, :], dc == 0, dc == 7,
                               [b_win, b_hT], [bpm])
                        pv = pev[i3]; bpv = b_pev[i3]
                        cp("act", pv[:], pm[:], [bpm], [bpv])
                        cc = coff + 1 + ti * NT
                        dma("sp", Pscr[mc * 128:(mc + 1) * 128, cc:cc + NT], pv[:], [bpv], [b_P])
            mk.flush(final=True)

    pass0()
    def rwkv_pass(d):
        rev = (d == 1)
        es, sb, ps = scope("rw%d_" % d)
        with es:
            NT2 = 128
            psr = [ps("psr%d" % i, [128, 2048]) for i in range(2)]
            b_psr = [Buf() for _ in range(2)]
            pctr = [0]

            def nextps():
                i = pctr[0] % 2
                pctr[0] += 1
                return psr[i], b_psr[i]

            def T4(name):
                return sb(name, [64, 8, NT2]), Buf()

            def ldp(name, src512):
                t = sb(name, [64, 8]); b = Buf()
                dma("sp", t[:], src512.rearrange("(h p) -> p h", p=64), (), [b], slow=True)
                return t, b

            mu3 = sb("mu3", [64, 24]); b_mu3 = Buf()
            dma("sp", mu3[:], mu_shift[0:1536].rearrange("(g p) -> p g", p=64), (), [b_mu3], slow=True)
            hm3 = sb("hm3", [64, 24]); om3 = sb("om3", [64, 24]); b_hm3 = Buf()
            ts("dve", hm3[:], mu3[:], 0.5, None, ALU.mult, None, [b_mu3], [b_hm3])
            ts("dve", om3[:], mu3[:], -1.0, 1.0, ALU.mult, ALU.add, [b_mu3], [b_hm3])
            muw = sb("muw", [64, 2]); b_muw = Buf()
            dma("sp", muw[:, 0:1], mu_shift[1536 + 64 * d:1600 + 64 * d].rearrange("(p o) -> p o", o=1), (), [b_muw], slow=True)
            dma("sp", muw[:, 1:2], mu_shift[1664 + 64 * d:1728 + 64 * d].rearrange("(p o) -> p o", o=1), (), [b_muw], slow=True)
            hmw = sb("hmw", [64, 2]); omw = sb("omw", [64, 2]); b_hmw = Buf()
            ts("dve", hmw[:], muw[:], 0.5, None, ALU.mult, None, [b_muw], [b_hmw])
            ts("dve", omw[:], muw[:], -1.0, 1.0, ALU.mult, ALU.add, [b_muw], [b_hmw])
            w0d, b_w0d = ldp("w0d", w0[d]); a0d, b_a0d = ldp("a0d", a0[d])
            kk_, b_kk_ = ldp("kk_", k_k); ka_, b_ka_ = ldp("ka_", k_a); rk_, b_rk_ = ldp("rk_", r_k)
            omka = sb("omka", [64, 8]); b_omka = Buf()
            ts("dve", omka[:], ka_[:], -1.0, 1.0, ALU.mult, ALU.add, [b_ka_], [b_omka])
            w2d = sb("w2d", [64, 512]); a2d = sb("a2d", [64, 512]); b_w2d = Buf()
            dma("sp", w2d[:], w2[d], (), [b_w2d]); dma("sp", a2d[:], a2[d], (), [b_w2d])
            ones64 = sb("ones64", [64, 64]); b_c = Buf()
            mk.op("pool", lambda E: E.memset(ones64[:], 1.0), (), [b_c])
            maskA = sb("maskA", [64, 128]); maskL = sb("maskL", [64, 64]); MS = sb("MS", [64, 8 * NT2])
            mk.op("pool", lambda E: E.memset(maskA[:], 1.0), (), [b_c])
            mk.op("pool", lambda E: E.memset(maskL[:], 1.0), (), [b_c])
            mk.op("pool", lambda E: E.memset(MS[:], 1.0), (), [b_c])
            zc_ = 63 if rev else 0
            mk.op("pool", lambda E: E.memset(MS[:].rearrange("p (a l) -> p a l", l=64)[:, :, zc_:zc_ + 1], 0.0), [b_c], [b_c])

            def asel(ap, upper, strict):
                pat = [[1, 64]] if upper else [[-1, 64]]
                cm = -1 if upper else 1
                mk.op("pool", lambda E: E.affine_select(out=ap, in_=ap, pattern=pat, compare_op=ALU.is_ge, fill=0.0,
                                                        base=(-1 if strict else 0), channel_multiplier=cm),
                      [b_c], [b_c])
            asel(maskA[:, 0:64], not rev, True)
            asel(maskA[:, 64:128], not rev, False)
            asel(maskL[:], rev, True)
            mA = maskA[:, None, :].broadcast_to([64, 8, 128])
            mL = maskL[:, None, :].broadcast_to([64, 8, 64])
            id64 = ident[0:64, 0:64]
            idbc = ident[0:64, None, 0:64].broadcast_to([64, 8, 64])

            Lr = [sb("Lq%d" % q, [64, 8, NT2 + 2]) for q in range(2)]; b_L = [Buf() for _ in range(2)]
            Lr.append(Lr[0]); b_L.append(b_L[0])
            XW = sb("XW", [64, NT2 + 2]); XA = sb("XA", [64, NT2 + 2]); b_XW = Buf(); b_XA = Buf()
            T1, b_T1 = T4("T1")
            SH = [T4("SH%d" % q) for q in range(2)]
            (Rp, b_Rp), (Kp, b_Kp) = SH
            tt0, cp0 = tt, cp
            tw = sb("tw", [64, NT2]); b_tw = Buf()
            xwp = sb("xwp", [64, NT2]); xap = sb("xap", [64, NT2]); b_xwp = Buf(); b_xap = Buf()
            XB, b_XB = T4("XB"); E2, b_E2 = T4("E2"); AD, b_AD = T4("AD"); KR, b_KR = T4("KR")
            SS, b_SS = T4("SS"); KD, b_KD = T4("KD"); AB, b_AB = T4("AB"); BON, b_BON = XB, b_XB
            G, b_G = T1, b_T1; D1, b_D1 = E2, b_E2; D2, b_D2 = AD, b_AD; EP, b_EP = XB, b_XB; EN, b_EN = SS, b_SS
            T2, b_T2 = SS, b_SS
            SD = F32
            SETS = []
            for i_ in range(2):
                st_ = []
                for nm, shp in (("AR", [64, 8, 2, 128]), ("KT", [64, 9, NT2]), ("BT", [64, 9, NT2]), ("KH", [64, 8, NT2]),
                                ("BH", [64, 8, NT2]), ("Vp", [64, 8, NT2]), ("GL", [64, 16])):
                    st_ += [sb("%s_%d" % (nm, i_), shp), Buf()]
                SETS.append(st_)
            YT, b_YT = T4("YT")
            MT1 = sb("MT1", [64, 2, 8, 128], SD); MT2 = sb("MT2", [64, 2, 8, 128], SD); b_MT1 = Buf(); b_MT2 = Buf()
            P0 = sb("P0", [64, 17, 64], SD); b_P0 = Buf()
            PP = [sb("PP%d" % i, [64, 33, 64], SD) for i in range(2)]; b_PP = [Buf(), Buf()]
            Zt = [sb("Zt%d" % i, [64, 17, 128], SD) for i in range(2)]; b_Zt = [Buf() for _ in range(2)]
            VT = sb("VT", [64, 17, 64], SD); BHt = sb("BHt", [64, 17, 64], SD); KHt = sb("KHt", [64, 17, 64], SD)
            QT = sb("QT", [64, 2, 8, 64]); MM = sb("MM", [64, 17, 64]); DG = sb("DG", [64, 17, 64])
            b_VT = Buf(); b_BHt = Buf(); b_KHt = Buf(); b_QT = Buf(); b_MM = Buf(); b_DG = Buf()
            STt = [sb("ST%d" % i, [64, 9, 64]) for i in range(2)]; b_ST = [Buf(), Buf()]
            for t_, b__, r_ in ((Zt[0], b_Zt[0], 16), (Zt[1], b_Zt[1], 16)):
                ts("dve", RR(t_[:, r_, :]), maskA[:], 0.0, None, ALU.mult, None, [b_c], [b__])
            for t_, b__ in ((VT, b_VT), (BHt, b_BHt), (KHt, b_KHt), (MM, b_MM)):
                ts("dve", RR(t_[:, 16, :]), ones64[:], 0.0, None, ALU.mult, None, [b_c], [b__])
            for i_ in range(2):
                ts("dve", RR(STt[i_][:, 8, :]), ones64[:], 0.0, None, ALU.mult, None, [b_c], [b_ST[i_]])
                ts("dve", RR(SETS[i_][2][:, 8, :]), maskA[:], 0.0, None, ALU.mult, None, [b_c], [SETS[i_][3]])
                ts("dve", RR(SETS[i_][4][:, 8, :]), maskA[:], 0.0, None, ALU.mult, None, [b_c], [SETS[i_][5]])
            ts("dve", RR(P0[:, 16, :]), ones64[:], 0.0, None, ALU.mult, None, [b_c], [b_P0])
            for i_ in range(2):
                ts("dve", RR(PP[i_][:, 32, :]), ones64[:], 0.0, None, ALU.mult, None, [b_c], [b_PP[i_]])
            mA16 = maskA[:, None, :].broadcast_to([64, 16, 128])
            mL16 = maskL[:, None, :].broadcast_to([64, 16, 64])
            idbc16 = ident[0:64, None, 0:64].broadcast_to([64, 16, 64])

            def f16(t):
                if len(t.shape) == 3:
                    return t[:, 0:16, :]
                return t[:].rearrange("p c h n -> p (c h) n")

            def wd(t, blk, n, off=0):
                fl = t[:].rearrange("p a n -> p (a n)")
                return fl[:, blk * n + off:blk * n + off + 128]

            def pv(p, lo, n):
                return p[0:64, lo:lo + 16 * n].rearrange("p (a n) -> p a n", n=n)

            def v3(p, n):
                return p[0:64, 0:8 * n].rearrange("p (h n) -> p h n", n=n)

            def bc(t, lo, hi, n):
                return t[:, lo:hi, None].broadcast_to([64, hi - lo, n])

            def c4(t):
                return t[:].rearrange("p h (c l) -> p h c l", l=64)

            for s, (toff, coff, T) in enumerate(SEQ):
                sti_ = [0]
                ts("dve", RR(STt[0][:, 0:8, :]), STt[1][:, 0:8, :], 0.0, None, ALU.mult, None, [b_ST[1]], [b_ST[0]])
                ntile = T // NT2
                order = list(range(ntile - 1, -1, -1) if rev else range(ntile))

                def prep(ti, AR, b_AR, KT, b_KT, BT, b_BT, KH, b_KH, BH, b_BH, Vp, b_Vp, GL, b_GL):
                    ARb, b_ARb = AR, b_AR
                    tl = ti * NT2
                    c0 = coff + tl
                    tg = toff + tl
                    def ldq(q):
                        dma("sp" if q != 1 else "act", Lr[q][:],
                            Pscr[q * 512:(q + 1) * 512, c0:c0 + NT2 + 2].rearrange("(h p) t -> p h t", p=64),
                            [b_P], [b_L[q]])

                    def shq(q):
                        Lq = Lr[q]; S_, bS = (SH[q] if q < 2 else (Vp, b_Vp))
                        tt("pool", T1[:], Lq[:, :, 0:NT2], Lq[:, :, 2:NT2 + 2], ALU.add, [b_L[q]], [b_T1])
                        tt("pool", T1[:], T1[:], bc(hm3, 8 * q, 8 * q + 8, NT2), ALU.mult, [b_T1, b_hm3], [b_T1])
                        tt("pool", S_[:], Lq[:, :, 1:NT2 + 1], bc(om3, 8 * q, 8 * q + 8, NT2), ALU.mult,
                           [b_L[q], b_hm3], [bS])
                        tt("pool", S_[:], S_[:], T1[:], ALU.add, [bS, b_T1], [bS])
                    ldq(0); ldq(1)
                    dma("sp", XW[:], Pscr[1536 + 64 * d:1600 + 64 * d, c0:c0 + NT2 + 2], [b_P], [b_XW])
                    dma("act", XA[:], Pscr[1664 + 64 * d:1728 + 64 * d, c0:c0 + NT2 + 2], [b_P], [b_XA])
                    shq(0); ldq(2); shq(1); shq(2)
                    for (X_, bX, o_, bo, j) in ((XW, b_XW, xwp, b_xwp, 0), (XA, b_XA, xap, b_xap, 1)):
                        tt("dve", tw[:], X_[:, 0:NT2], X_[:, 2:NT2 + 2], ALU.add, [bX], [b_tw])
                        ts("dve", tw[:], tw[:], hmw[:, j:j + 1], None, ALU.mult, None, [b_tw, b_hmw], [b_tw])
                        stt(o_[:], X_[:, 1:NT2 + 1], omw[:, j:j + 1], tw[:], ALU.mult, ALU.add, [bX, b_hmw, b_tw], [bo])
                    act(xwp[:], xwp[:], AF.Tanh, [b_xwp], [b_xwp])
                    for hh in range(2):
                        pa, bpa = nextps()
                        for j in range(4):
                            h = 4 * hh + j
                            mm(pa[0:64, j * NT2:(j + 1) * NT2], w2d[:, h * 64:(h + 1) * 64], xwp[:], True, True,
                               [b_w2d, b_xwp], [bpa])
                        tt("dve", XB[:, 4 * hh:4 * hh + 4, :], pa[0:64, 0:4 * NT2].rearrange("p (h n) -> p h n", n=NT2),
                           bc(w0d, 4 * hh, 4 * hh + 4, NT2), ALU.add, [bpa, b_w0d], [b_XB])
                    act(XB[:], XB[:], AF.Exp, [b_XB], [b_XB], scale=-1.0)
                    act(XB[:], XB[:], AF.Ln, [b_XB], [b_XB], bias=1.0)
                    act(E2[:], XB[:], AF.Exp, [b_XB], [b_E2], bias=-0.5, scale=-1.0)
                    for hh in range(2):
                        pa, bpa = nextps()
                        for j in range(4):
                            h = 4 * hh + j
                            mm(pa[0:64, j * NT2:(j + 1) * NT2], a2d[:, h * 64:(h + 1) * 64], xap[:], True, True,
                               [b_w2d, b_xap], [bpa])
                        tt("dve", AD[:, 4 * hh:4 * hh + 4, :], pa[0:64, 0:4 * NT2].rearrange("p (h n) -> p h n", n=NT2),
                           bc(a0d, 4 * hh, 4 * hh + 4, NT2), ALU.add, [bpa, b_a0d], [b_AD])
                    act(AD[:], AD[:], AF.Sigmoid, [b_AD], [b_AD])
                    tt("pool", KR[:], Kp[:], bc(kk_, 0, 8, NT2), ALU.mult, [b_Kp, b_kk_], [b_KR])
                    tt("pool", T1[:], KR[:], KR[:], ALU.mult, [b_KR], [b_T1])
                    for hh in range(2):
                        pa, bpa = nextps()
                        for j in range(4):
                            h = 4 * hh + j
                            mm(pa[0:64, j * NT2:(j + 1) * NT2], ones64[:], T1[:, h, :], True, True, [b_c, b_T1], [bpa])
                        ts("dve", SS[:, 4 * hh:4 * hh + 4, :], pa[0:64, 0:4 * NT2].rearrange("p (h n) -> p h n", n=NT2),
                           1e-24, None, ALU.max, None, [bpa], [b_SS])
                    act(SS[:], SS[:], AF.Sqrt, [b_SS], [b_SS])
                    mk.op("dve", lambda E: E.reciprocal(out=SS[:], in_=SS[:]), [b_SS], [b_SS])
                    tt("pool", KR[:], KR[:], SS[:], ALU.mult, [b_KR, b_SS], [b_KR])
                    tt("pool", T2[:], AD[:], bc(ka_, 0, 8, NT2), ALU.mult, [b_AD, b_ka_], [b_T2])
                    tt("pool", T2[:], T2[:], bc(omka, 0, 8, NT2), ALU.add, [b_T2, b_omka], [b_T2])
                    tt("pool", KD[:], T2[:], Kp[:], ALU.mult, [b_T2, b_Kp], [b_KD])
                    tt("dve", AB[:], AD[:], KR[:], ALU.mult, [b_AD, b_KR], [b_AB])
                    tt("pool", T1[:], Rp[:], KD[:], ALU.mult, [b_Rp, b_KD], [b_T1])
                    tt("pool", T1[:], T1[:], bc(rk_, 0, 8, NT2), ALU.mult, [b_T1, b_rk_], [b_T1])
                    for hh in range(2):
                        pa, bpa = nextps()
                        for j in range(4):
                            h = 4 * hh + j
                            mm(pa[0:64, j * NT2:(j + 1) * NT2], ones64[:], T1[:, h, :], True, True, [b_c, b_T1], [bpa])
                        tt("dve", BON[:, 4 * hh:4 * hh + 4, :], pa[0:64, 0:4 * NT2].rearrange("p (h n) -> p h n", n=NT2),
                           Vp[:, 4 * hh:4 * hh + 4, :], ALU.mult, [bpa, b_Vp], [b_BON])
                    dma("sp", BD[d, :, :, tg:tg + NT2], BON[:], [b_BON], [b_BD])
                    E2f = E2[:].rearrange("p h t -> p (h t)"); Gf = G[:].rearrange("p h t -> p (h t)"); MSf = MS[:]
                    if rev:
                        E2f = E2f[:, ::-1]; Gf = Gf[:, ::-1]; MSf = MSf[:, ::-1]
                    mk.op("dve", lambda E, Gf=Gf, MSf=MSf, E2f=E2f: E.tensor_tensor_scan(
                        out=Gf, data0=MSf, data1=E2f, initial=0.0, op0=ALU.mult, op1=ALU.add), [b_E2, b_c], [b_G])
                    tt("pool", D1[:], G[:], E2[:], ALU.subtract, [b_G, b_E2], [b_D1])
                    Gv = G[:].rearrange("p h (c l) -> p (h c) l", l=64)
                    ti_ = 0 if rev else 63
                    totb = Gv[:, :, ti_:ti_ + 1].broadcast_to([64, 16, 64])
                    tt("pool", D2[:].rearrange("p h (c l) -> p (h c) l", l=64), Gv, totb, ALU.subtract, [b_G], [b_D2])
                    act(EP[:], G[:], AF.Exp, [b_G], [b_EP])
                    act(EN[:], G[:], AF.Exp, [b_G], [b_EN], scale=-1.0)
                    act(D1[:], D1[:], AF.Exp, [b_D1], [b_D1], scale=-1.0)
                    act(D2[:], D2[:], AF.Exp, [b_D2], [b_D2])
                    act(GL[:].rearrange("p (a o) -> p a o", o=1), Gv[:, :, ti_:ti_ + 1], AF.Exp, [b_G], [b_GL], scale=-1.0)
                    stt(RR(AR[:, :, :, 0:64]), c4(KR), -1.0, c4(D1), ALU.mult, ALU.mult, [b_KR, b_D1], [b_AR])
                    tt("pool", RR(AR[:, :, :, 64:128]), c4(Rp), c4(EN), ALU.mult, [b_Rp, b_EN], [b_AR])
                    tt("pool", RR(KT[:, 0:8, :]), KD[:], EP[:], ALU.mult, [b_KD, b_EP], [b_KT])
                    tt("dve", RR(BT[:, 0:8, :]), AB[:], EP[:], ALU.mult, [b_AB, b_EP], [b_BT])
                    tt("pool", KH[:], KD[:], D2[:], ALU.mult, [b_KD, b_D2], [b_KH])
                    tt("dve", BH[:], AB[:], D2[:], ALU.mult, [b_AB, b_D2], [b_BH])

                def chunk(ti, pend, AR, b_AR, KT, b_KT, BT, b_BT, KH, b_KH, BH, b_BH, Vp, b_Vp, GL, b_GL):
                    ARb, b_ARb = AR, b_AR
                    tg = toff + ti * NT2

                    def tt(*a):
                        tt0(*a)
                        mk.replay(pend, 2)

                    def cp(*a):
                        cp0(*a)
                        mk.replay(pend, 2)
                    CS = [slice(0, 64), slice(64, 128)]
                    p1, bp1 = nextps()
                    for c in range(2):
                        for h in range(8):
                            mmr(p1[0:128, (c * 8 + h) * 128:(c * 8 + h + 1) * 128], wd(BT, h, 128, c * 64), ARb[:, h, c, :],
                                [b_BT, b_ARb], [bp1])
                    tt("dve", RR(f16(MT1)), pv(p1, 0, 128), mA16, ALU.mult, [bp1, b_c], [b_MT1])
                    p2, bp2 = nextps()
                    for c in range(2):
                        for h in range(8):
                            mmr(p2[0:128, (c * 8 + h) * 128:(c * 8 + h + 1) * 128], wd(KT, h, 128, c * 64), ARb[:, h, c, :],
                                [b_KT, b_ARb], [bp2])
                    tt("dve", RR(f16(MT2)), pv(p2, 0, 128), mA16, ALU.mult, [bp2, b_c], [b_MT2])
                    p3, bp3 = nextps()
                    for c in range(2):
                        for h in range(8):
                            if USE_R:
                                mmr(p3[0:128, (c * 8 + h) * 64:(c * 8 + h + 1) * 64], ARb[:, h, c, :], BT[:, h, CS[c]], [b_ARb, b_BT], [bp3])
                            else:
                                mm(p3[0:64, (c * 8 + h) * 64:(c * 8 + h + 1) * 64], ARb[:, h, c, 0:64], BT[:, h, CS[c]], True, True,
                                   [b_ARb, b_BT], [bp3])
                    tt("dve", RR(P0[:, 0:16, :]), pv(p3, 0, 64), mL16, ALU.mult, [bp3, b_c], [b_P0])
                    Z0 = Zt[0]; bZ0 = b_Zt[0]
                    p4, bp4 = nextps()
                    for c in range(2):
                        for h in range(8):
                            mk.op("pe", lambda E, p4=p4, o=(c * 8 + h) * 64, a=AR[:, h, c, 0:64]: E.transpose(
                                out=p4[0:64, o:o + 64], in_=a, identity=id64), [b_AR, b_ident], [bp4])
                            mk.op("pe", lambda E, p4=p4, o=1024 + (c * 8 + h) * 64, a=Vp[:, h, CS[c]]: E.transpose(
                                out=p4[0:64, o:o + 64], in_=a, identity=id64), [b_Vp, b_ident], [bp4])
                    cp0("act", RR(Z0[:, 0:16, 0:64]), pv(p4, 0, 64), [bp4], [bZ0])
                    cp("act", RR(f16(VT)), pv(p4, 1024, 64), [bp4], [b_VT])
                    p5, bp5 = nextps()
                    for c in range(2):
                        for h in range(8):
                            mk.op("pe", lambda E, p5=p5, o=(c * 8 + h) * 64, a=BH[:, h, CS[c]]: E.transpose(
                                out=p5[0:64, o:o + 64], in_=a, identity=id64), [b_BH, b_ident], [bp5])
                            mk.op("pe", lambda E, p5=p5, o=1024 + (c * 8 + h) * 64, a=KH[:, h, CS[c]]: E.transpose(
                                out=p5[0:64, o:o + 64], in_=a, identity=id64), [b_KH, b_ident], [bp5])
                    cp0("dve", RR(f16(BHt)), pv(p5, 0, 64), [bp5], [b_BHt])
                    cp("dve", RR(f16(KHt)), pv(p5, 1024, 64), [bp5], [b_KHt])
                    p6, bp6 = nextps()
                    for c in range(2):
                        for h in range(8):
                            if USE_R:
                                mmr(p6[0:128, (c * 8 + h) * 64:(c * 8 + h + 1) * 64], MT2[:, c, h, :], VT[:, c * 8 + h, :], [b_MT2, b_VT], [bp6])
                            else:
                                mm(p6[0:64, (c * 8 + h) * 64:(c * 8 + h + 1) * 64], MT2[:, c, h, 0:64], VT[:, c, h, :], True, True,
                                   [b_MT2, b_VT], [bp6])
                    cp("act", RR(Z0[:, 0:16, 64:128]), pv(p6, 0, 64), [bp6], [bZ0])
                    zi = 0
                    P0f = P0[:].rearrange("p a n -> p (a n)")
                    Pv = lambda c, h: P0[:, c * 8 + h, :]
                    Pw = lambda c, h: P0f[:, (c * 8 + h) * 64:(c * 8 + h) * 64 + 128]
                    PTv = lambda c, h: MT1[:, c, h, 0:64]
                    PTw = lambda c, h: MT1[:, c, h, :]
                    bP = b_P0; bPT = b_MT1
                    for it in range(6):
                        Zc = Zt[zi]; bZc = b_Zt[zi]; Zn = Zt[1 - zi]; bZn = b_Zt[1 - zi]
                        PTc, Pc, PTcw, Pcw, bPTc, bPc = PTv, Pv, PTw, Pw, bPT, bP
                        if it < 5:
                            nx = it % 2
                            p8, bp8 = nextps()
                            for c in range(2):
                                for h in range(8):
                                    o1 = (c * 8 + h) * 64
                                    if USE_R:
                                        mmr(p8[0:128, o1:o1 + 64], PTcw(c, h), Pc(c, h), [bPTc, bPc], [bp8])
                                        mmr(p8[0:128, 1024 + o1:1024 + o1 + 64], Pcw(c, h), PTc(c, h), [bPTc, bPc], [bp8])
                                    else:
                                        mm(p8[0:64, o1:o1 + 64], PTc(c, h), Pc(c, h), True, True, [bPTc, bPc], [bp8])
                                        mm(p8[0:64, 1024 + o1:1024 + o1 + 64], Pc(c, h), PTc(c, h), True, True, [bPTc, bPc], [bp8])
                            cp("act", RR(PP[nx][:, 0:32, :]), p8[0:64, 0:2048].rearrange("p (a n) -> p a n", n=64), [bp8], [b_PP[nx]])
                            PPf = PP[nx][:].rearrange("p a n -> p (a n)")
                            Pv = lambda c, h, nx=nx: PP[nx][:, c * 8 + h, :]
                            PTv = lambda c, h, nx=nx: PP[nx][:, 16 + c * 8 + h, :]
                            Pw = lambda c, h, PPf=PPf: PPf[:, (c * 8 + h) * 64:(c * 8 + h) * 64 + 128]
                            PTw = lambda c, h, PPf=PPf: PPf[:, (16 + c * 8 + h) * 64:(16 + c * 8 + h) * 64 + 128]
                            bP = b_PP[nx]; bPT = b_PP[nx]
                        p7, bp7 = nextps()
                        for c in range(2):
                            for h in range(8):
                                o1 = (c * 8 + h) * 128
                                if USE_R:
                                    mmr(p7[0:128, o1:o1 + 128], PTcw(c, h), Zc[:, c * 8 + h, :], [bPTc, bZc], [bp7])
                                else:
                                    mm(p7[0:64, o1:o1 + 128], PTc(c, h), Zc[:, c, h, :], True, True, [bPTc, bZc], [bp7])
                        tt("dve", RR(f16(Zn)), pv(p7, 0, 128), f16(Zc), ALU.add, [bp7, bZc], [bZn])
                        zi = 1 - zi
                    Zf = Zt[zi]; bZf = b_Zt[zi]
                    p9, bp9 = nextps()
                    for c in range(2):
                        for h in range(8):
                            if USE_R:
                                mmr(p9[0:128, (c * 8 + h) * 64:(c * 8 + h + 1) * 64], Zf[:, c * 8 + h, :], MT1[:, c, h, 64:128], [bZf, b_MT1], [bp9])
                                mmr(p9[0:128, 1024 + (c * 8 + h) * 64:1024 + (c * 8 + h + 1) * 64], Zf[:, c * 8 + h, :], BHt[:, c * 8 + h, :], [bZf, b_BHt], [bp9])
                            else:
                                mm(p9[0:64, (c * 8 + h) * 64:(c * 8 + h + 1) * 64], Zf[:, c, h, 0:64], MT1[:, c, h, 64:128], True, True,
                                   [bZf, b_MT1], [bp9])
                                mm(p9[0:64, 1024 + (c * 8 + h) * 64:1024 + (c * 8 + h + 1) * 64], Zf[:, c, h, 0:64], BHt[:, c, h, :], True, True,
                                   [bZf, b_BHt], [bp9])
                    tt0("dve", RR(QT[:]), p9[0:64, 0:1024].rearrange("p (c h n) -> p c h n", c=2, h=8),
                       AR[:, :, :, 64:128].rearrange("p h c n -> p c h n"), ALU.add, [bp9, b_AR], [b_QT])
                    GLc = GL[:].rearrange("p (h c) -> p c h", c=2)[:, :, :, None].broadcast_to([64, 2, 8, 64])
                    tt0("pool", DG[:, 0:16, :].rearrange("p (c h) n -> p c h n", c=2),
                        ident[0:64, None, None, 0:64].broadcast_to([64, 2, 8, 64]), GLc, ALU.mult, [b_ident, b_GL], [b_DG])
                    tt("dve", RR(f16(MM)), pv(p9, 1024, 64), DG[:, 0:16, :], ALU.add, [bp9, b_DG], [b_MM])
                    for c in (range(1, -1, -1) if rev else range(2)):
                        sti = sti_[0]
                        ST = STt[sti]; bST = b_ST[sti]; STn = STt[1 - sti]; bSTn = b_ST[1 - sti]
                        p11, bp11 = nextps()
                        for h in range(8):
                            o_ = p11[0:128, h * 64:(h + 1) * 64]
                            k_ = c * 8 + h
                            mmr(o_, wd(ST, h, 64), QT[:, c, h, :], [bST, b_QT], [bp11], True, False)
                            mmr(o_, wd(Zf, k_, 128, 64), MT1[:, c, h, 64:128], [bZf, b_MT1], [bp11], False, False)
                            mmr(o_, wd(VT, k_, 64), MT2[:, c, h, 64:128], [b_VT, b_MT2], [bp11], False, True)
                        cp("act", YT[:, :, CS[c]], v3(p11, 64), [bp11], [b_YT])
                        p12, bp12 = nextps()
                        for h in range(8):
                            o_ = p12[0:128, h * 64:(h + 1) * 64]
                            k_ = c * 8 + h
                            mmr(o_, wd(MM, k_, 64), ST[:, h, :], [b_MM, bST], [bp12], True, False)
                            mmr(o_, wd(BHt, k_, 64), Zf[:, k_, 64:128], [b_BHt, bZf], [bp12], False, False)
                            mmr(o_, wd(KHt, k_, 64), VT[:, k_, :], [b_KHt, b_VT], [bp12], False, True)
                        cp("dve", RR(STn[:, 0:8, :]), v3(p12, 64), [bp12], [bSTn])
                        sti_[0] = 1 - sti
                    dma("sp", YD[d, :, :, tg:tg + NT2], YT[:], [b_YT], [b_YD])

                PIPE = os.environ.get('RW_NOPIPE') != '1'
                if PIPE:
                    prep(order[0], *SETS[0])
                for idx, ti in enumerate(order):
                    pend = []
                    if not PIPE:
                        prep(ti, *SETS[idx % 2])
                    elif idx + 1 < len(order):
                        mk.deferred = pend
                        prep(order[idx + 1], *SETS[(idx + 1) % 2])
                        mk.deferred = None
                    if os.environ.get('RW_PIPE_MODE') == 'start':
                        mk.replay(pend, len(pend))
                    chunk(ti, pend, *SETS[idx % 2])
                    mk.replay(pend, len(pend))
            mk.flush(final=True)

    if upto >= 1:
        rwkv_pass(0)
        rwkv_pass(1)

    def s5_pass():
        es, sb, ps = scope("s5_")
        with es:
            TWO_PI = 2.0 * math.pi
            pz = [ps("pz%d" % i, [128, 512]) for i in range(8)]
            b_pz = [Buf() for _ in range(8)]
            pctr = [0]

            def nextps():
                i = pctr[0] % 8
                pctr[0] += 1
                return pz[i], b_pz[i]

            NLV = 10
            identb = sb("identb", [64, 64], BF16)
            dsk = sb("dsk", [16, 32])
            SQr = sb("SQr", [64, NLV, 64]); SQi = sb("SQi", [64, NLV, 64]); SQin = sb("SQin", [64, NLV, 64])
            LTr = sb("LTr", [64, 8, 64, 16], BF16); LTi = sb("LTi", [64, 8, 64, 16], BF16)
            OTr = sb("OTr", [64, 8, 64, 16], BF16); OTn = sb("OTn", [64, 8, 64, 16], BF16)
            CRb = sb("CRb", [64, 64, 16], BF16); CInb = sb("CInb", [64, 64, 16], BF16)
            bt_ = Buf()
            es2, sb2, ps2_ = scope("s5t_")
            ones1 = sb2("ones1", [1, 64]); row = sb2("row", [1, 64])
            mk.op("pool", lambda E: E.memset(ones1[:], 1.0), (), [bt_])
            dma("sp", row[:], log_dt.rearrange("d g -> (d g)").rearrange("(o n) -> o n", o=1), (), [bt_])
            cp("dve", identb[:], ident[0:64, 0:64], [b_ident], [bt_])
            LR = sb2("LR", [64, 64]); LI = sb2("LI", [64, 64])
            dma("sp", LR[:].rearrange("p (d g) -> p d g", d=2), lam_re.rearrange("d g p -> p d g"), (), [bt_], slow=True)
            dma("act", LI[:].rearrange("p (d g) -> p d g", d=2), lam_im.rearrange("d g p -> p d g"), (), [bt_], slow=True)
            BR = sb2("BR", [64, 64, 16]); BI = sb2("BI", [64, 64, 16])
            dma("sp", BR[:].rearrange("p (d g) h -> p d g h", d=2), b_re.rearrange("d g p h -> p d g h"), (), [bt_])
            dma("act", BI[:].rearrange("p (d g) h -> p d g h", d=2), b_im.rearrange("d g p h -> p d g h"), (), [bt_])
            CR = sb2("CR", [64, 64, 16]); CI = sb2("CI", [64, 64, 16])
            cnat = sb2("cnat", [128, 8, 64])
            for (src, dst) in ((c_re, CR), (c_im, CI)):
                dma("sp", cnat[:], src.rearrange("d g h p -> (d g h) p").rearrange("(k q) p -> q k p", q=128), [bt_], [bt_])
                for k in range(8):
                    pq, bq = nextps()
                    mk.op("pe", lambda E, pq=pq, k=k: E.transpose(out=pq[0:64, 0:128], in_=cnat[:, k, :], identity=ident[:]),
                          [bt_, b_ident], [bq])
                    cp("dve", dst[:, k * 8:(k + 1) * 8, :], pq[0:64, 0:128].rearrange("p (g h) -> p g h", h=16), [bq], [bt_])
            dma("sp", dsk[:], d_skip.rearrange("(g h) -> h g", h=16), (), [bt_], slow=True)
            DT = sb2("DT", [64, 64])
            pq, bq = nextps()
            mm(pq[0:64, 0:64], ones1[:], row[:], True, True, [bt_], [bq])
            act(DT[:], pq[0:64, 0:64], AF.Exp, [bq], [bt_])

            def T64(name):
                return sb2(name, [64, 64])
            ZR = T64("ZR"); ZI = T64("ZI"); EPs = T64("EPs"); COS = T64("COS"); SIN = T64("SIN")
            tA = T64("tA"); tB = T64("tB"); tC = T64("tC"); tI = sb2("tI", [64, 64], mybir.dt.int32)
            tt("dve", ZR[:], LR[:], DT[:], ALU.mult, [bt_], [bt_])
            tt("dve", ZI[:], LI[:], DT[:], ALU.mult, [bt_], [bt_])
            act(EPs[:], ZR[:], AF.Exp, [bt_], [bt_])
            for (dst, offs) in ((SIN, 64.0), (COS, 64.25)):
                ts("dve", tA[:], ZI[:], 1.0 / TWO_PI, offs, ALU.mult, ALU.add, [bt_], [bt_])
                cp("dve", tI[:], tA[:], [bt_], [bt_])
                cp("dve", tB[:], tI[:], [bt_], [bt_])
                tt("dve", tA[:], tA[:], tB[:], ALU.subtract, [bt_], [bt_])
                ts("dve", tB[:], tA[:], 0.5, None, ALU.is_gt, None, [bt_], [bt_])
                tt("dve", tA[:], tA[:], tB[:], ALU.subtract, [bt_], [bt_])
                act(dst[:], tA[:], AF.Sin, [bt_], [bt_], scale=TWO_PI)
            PWr = sb2("PWr", [64, 9, 64]); PWi = sb2("PWi", [64, 9, 64])
            mk.op("pool", lambda E: E.memset(PWr[:, 0, :], 1.0), (), [bt_])
            mk.op("pool", lambda E: E.memset(PWi[:, 0, :], 0.0), (), [bt_])
            tt("dve", PWr[:, 1, :], EPs[:], COS[:], ALU.mult, [bt_], [bt_])
            tt("dve", PWi[:, 1, :], EPs[:], SIN[:], ALU.mult, [bt_], [bt_])

            def cmul(or_, oi_, ar, ai, br, bi, n3=None):
                tt("dve", tA[:], ai, bi, ALU.mult, [bt_], [bt_])
                tt("dve", tB[:], ai, br, ALU.mult, [bt_], [bt_])
                tt("dve", tC[:], ar, br, ALU.mult, [bt_], [bt_])
                tt("dve", or_, tC[:], tA[:], ALU.subtract, [bt_], [bt_])
                tt("dve", tC[:], ar, bi, ALU.mult, [bt_], [bt_])
                tt("dve", oi_, tC[:], tB[:], ALU.add, [bt_], [bt_])
            for j in range(2, 9):
                cmul(PWr[:, j, :], PWi[:, j, :], PWr[:, j - 1, :], PWi[:, j - 1, :], PWr[:, 1, :], PWi[:, 1, :])
            NLV = 10
            cp("dve", SQr[:, 0, :], PWr[:, 8, :], [bt_], [bt_]); cp("dve", SQi[:, 0, :], PWi[:, 8, :], [bt_], [bt_])
            for k in range(1, NLV):
                cmul(SQr[:, k, :], SQi[:, k, :], SQr[:, k - 1, :], SQi[:, k - 1, :], SQr[:, k - 1, :], SQi[:, k - 1, :])
            ts("dve", SQin[:], SQi[:], -1.0, None, ALU.mult, None, [bt_], [bt_])
            CFr = T64("CFr"); CFi = T64("CFi"); DEN = T64("DEN"); NR = T64("NR")
            ts("dve", NR[:], PWr[:, 1, :], -1.0, None, ALU.add, None, [bt_], [bt_])
            tt("dve", tA[:], LR[:], LR[:], ALU.mult, [bt_], [bt_])
            tt("dve", tB[:], LI[:], LI[:], ALU.mult, [bt_], [bt_])
            tt("dve", DEN[:], tA[:], tB[:], ALU.add, [bt_], [bt_])
            mk.op("dve", lambda E: E.reciprocal(out=DEN[:], in_=DEN[:]), [bt_], [bt_])
            tt("dve", tA[:], NR[:], LR[:], ALU.mult, [bt_], [bt_])
            tt("dve", tB[:], PWi[:, 1, :], LI[:], ALU.mult, [bt_], [bt_])
            tt("dve", tA[:], tA[:], tB[:], ALU.add, [bt_], [bt_])
            tt("dve", CFr[:], tA[:], DEN[:], ALU.mult, [bt_], [bt_])
            tt("dve", tA[:], PWi[:, 1, :], LR[:], ALU.mult, [bt_], [bt_])
            tt("dve", tB[:], NR[:], LI[:], ALU.mult, [bt_], [bt_])
            tt("dve", tA[:], tA[:], tB[:], ALU.subtract, [bt_], [bt_])
            tt("dve", CFi[:], tA[:], DEN[:], ALU.mult, [bt_], [bt_])
            BbR = sb2("BbR", [64, 64, 16]); BbI = sb2("BbI", [64, 64, 16])
            X1t = sb2("X1t", [64, 64, 16]); X2t = sb2("X2t", [64, 64, 16])

            def b16(t2):
                return t2[:, :, None].broadcast_to([64, 64, 16])

            def cmul3(or_, oi_neg, ar2, ai2, br3, bi3, e1="dve", e2="pool"):
                tt(e1, X1t[:], br3, b16(ar2), ALU.mult, [bt_], [bt_])
                tt(e1, X2t[:], bi3, b16(ai2), ALU.mult, [bt_], [bt_])
                tt(e1, or_, X1t[:], X2t[:], ALU.subtract, [bt_], [bt_])
                tt(e1, X1t[:], bi3, b16(ar2), ALU.mult, [bt_], [bt_])
                tt(e1, X2t[:], br3, b16(ai2), ALU.mult, [bt_], [bt_])
                if oi_neg[1]:
                    tt(e1, X1t[:], X1t[:], X2t[:], ALU.add, [bt_], [bt_])
                    ts(e1, oi_neg[0], X1t[:], -1.0, None, ALU.mult, None, [bt_], [bt_])
                else:
                    tt(e1, oi_neg[0], X1t[:], X2t[:], ALU.add, [bt_], [bt_])
            cmul3(BbR[:], (BbI[:], False), CFr[:], CFi[:], BR[:], BI[:])
            for j in range(8):
                cmul3(LTr[:, j], (LTi[:, j], False), PWr[:, j, :], PWi[:, j, :], BbR[:], BbI[:])
                cmul3(OTr[:, j], (OTn[:, j], True), PWr[:, j + 1, :], PWi[:, j + 1, :], CR[:], CI[:])
            cp("dve", CRb[:], CR[:], [bt_], [bt_]); ts("dve", CInb[:], CI[:], -1.0, None, ALU.mult, None, [bt_], [bt_])

            mk.flush(final=True)
            es2.close()
            KTg = [sb("KTg%d" % i, [16, 15, 16], BF16) for i in range(2)]; b_KTg = [Buf(), Buf()]
            CTg = [sb("CTg%d" % i, [16, 32, 64], BF16) for i in range(2)]; b_CTg = [Buf(), Buf()]
            UGN = 2048
            ug = sb("ug", [16, UGN]); b_ug = Buf()
            ub = [sb("ub%d" % i, [16, 8, 1024], BF16) for i in range(2)]; b_ub = [Buf(), Buf()]
            yg = sb("yg", [16, 4096]); b_yg = Buf()
            NBM = 1024
            Wt = [[[sb("W%d%d%d" % (pp, d, c), [64, NBM + 1]) for c in range(2)] for d in range(2)] for pp in range(2)]
            b_Wt = [[Buf() for d in range(2)] for pp in range(2)]
            Sb = [[[sb("Sb%d%d%d" % (i, d, c), [64, NBM + 1], BF16) for c in range(2)] for d in range(2)] for i in range(2)]
            b_Sb = [Buf(), Buf()]
            for pp in range(2):
                for d in range(2):
                    for c in range(2):
                        mk.op("pool", lambda E, t=Wt[pp][d][c]: E.memset(t[:], 0.0), (), [b_Wt[pp][d]])
            id16 = ident[0:16, 0:16]

            def gconsts(g):
                par = g % 2
                pk, bpk = nextps()
                for idx in range(15):
                    if idx == 0:
                        terms = [(0, 0), (1, 0)]
                    elif idx < 8:
                        terms = [(0, idx)]
                    else:
                        terms = [(1, idx - 7)]
                    n = 0
                    for (d, tau) in terms:
                        gi = d * 32 + g
                        mm(pk[0:16, idx * 16:(idx + 1) * 16], LTr[:, tau, gi, :], CRb[:, gi, :], n == 0, False, [bt_], [bpk]); n += 1
                        mm(pk[0:16, idx * 16:(idx + 1) * 16], LTi[:, tau, gi, :], CInb[:, gi, :], False, n == 2 * len(terms) - 1, [bt_], [bpk]); n += 1
                cp("dve", KTg[par][:].rearrange("p a b -> p (a b)"), pk[0:16, 0:240], [bpk], [b_KTg[par]])
                stt(KTg[par][:, 0, :], id16, dsk[:, g:g + 1], KTg[par][:, 0, :], ALU.mult, ALU.add, [b_KTg[par], bt_, b_ident], [b_KTg[par]])
                for q in range(4):
                    pc_, bpc = nextps()
                    for j in range(8):
                        i = q * 8 + j
                        d = i // 16; s_ = (i // 2) % 8; c = i % 2
                        e_ = (7 - s_) if d == 0 else s_
                        src = (LTr if c == 0 else LTi)[:, e_, d * 32 + g, :]
                        mm(pc_[0:16, j * 64:(j + 1) * 64], src, identb[:], True, True, [bt_], [bpc])
                    cp("act", CTg[par][:, q * 8:(q + 1) * 8, :].rearrange("p a b -> p (a b)"),
                       pc_[0:16, 0:512], [bpc], [b_CTg[par]])

            def dims(s):
                toff, coff, T = SEQ[s]
                nblk = T // 8
                BW = min(512, nblk)
                return toff, coff, T, nblk, BW, 8 * BW, nblk // BW

            def front(g, s, st):
                par = g % 2
                toff, coff, T, nblk, BW, TW, nbt = dims(s)
                nlv = int(math.log2(nblk))
                UG = min(UGN, T)
                for hf in range(T // UG):
                    dma("sp", ug[:, 0:UG], Pscr[1920 + 16 * g:1936 + 16 * g, coff + 1 + hf * UG:coff + 1 + (hf + 1) * UG],
                        [b_P], [b_ug])
                    cp("act", ub[st][:, :, hf * (UG // 8):(hf + 1) * (UG // 8)], ug[:, 0:UG].rearrange("p (b s) -> p s b", s=8),
                       [b_ug], [b_ub[st]])
                if nblk < NBM:
                    for d in range(2):
                        for c in range(2):
                            mk.op("pool", lambda E, t=Wt[0][d][c]: E.memset(t[:], 0.0), (), [b_Wt[0][d]])
                            mk.op("pool", lambda E, t=Wt[1][d][c]: E.memset(t[:], 0.0), (), [b_Wt[1][d]])
                for bt in range(nbt):
                    for d in range(2):
                        for c in range(2):
                            pw_, bpw = nextps()
                            for s_ in range(8):
                                mm(pw_[0:64, 0:BW], CTg[par][:, (d * 8 + s_) * 2 + c, :],
                                   ub[st][:, s_, bt * BW:(bt + 1) * BW],
                                   s_ == 0, s_ == 7, [b_CTg[par], b_ub[st]], [bpw])
                            o0 = bt * BW + (1 if d == 0 else 0)
                            cp("act", Wt[0][d][c][:, o0:o0 + BW], pw_[0:64, 0:BW], [bpw], [b_Wt[0][d]])
                cur = 0
                for k in range(nlv):
                    sh = 1 << k
                    n_ = nblk - sh
                    for d in range(2):
                        gi = d * 32 + g
                        lo = 1 if d == 0 else 0
                        Wc = Wt[cur][d]; Wn = Wt[1 - cur][d]
                        bWc = b_Wt[cur][d]; bWn = b_Wt[1 - cur][d]
                        if d == 0:
                            dst = slice(lo + sh, lo + nblk); srcs = slice(lo, lo + n_); keep = slice(lo, lo + sh)
                        else:
                            dst = slice(lo, lo + n_); srcs = slice(lo + sh, lo + nblk); keep = slice(lo + n_, lo + nblk)
                        ar = SQr[:, k, gi:gi + 1]; ai = SQi[:, k, gi:gi + 1]; ain = SQin[:, k, gi:gi + 1]
                        stt(Wn[0][:, dst], Wc[0][:, srcs], ar, Wc[0][:, dst], ALU.mult, ALU.add, [bWc, bt_], [bWn])
                        stt(Wn[1][:, dst], Wc[1][:, srcs], ar, Wc[1][:, dst], ALU.mult, ALU.add, [bWc, bt_], [bWn])
                        stt(Wn[0][:, dst], Wc[1][:, srcs], ain, Wn[0][:, dst], ALU.mult, ALU.add, [bWc, bWn, bt_], [bWn])
                        stt(Wn[1][:, dst], Wc[0][:, srcs], ai, Wn[1][:, dst], ALU.mult, ALU.add, [bWc, bWn, bt_], [bWn])
                        cp("pool", Wn[0][:, keep], Wc[0][:, keep], [bWc], [bWn])
                        cp("pool", Wn[1][:, keep], Wc[1][:, keep], [bWc], [bWn])
                    cur = 1 - cur
                for d in range(2):
                    for c in range(2):
                        if d == 0:
                            cp("act", Sb[st][d][c][:, 1:nblk + 1], Wt[cur][d][c][:, 1:nblk + 1], [b_Wt[cur][d]], [b_Sb[st]])
                            mk.op("pool", lambda E, t=Sb[st][d][c]: E.memset(t[:, 0:1], 0.0), (), [b_Sb[st]])
                        else:
                            cp("act", Sb[st][d][c][:, 0:nblk], Wt[cur][d][c][:, 0:nblk], [b_Wt[cur][d]], [b_Sb[st]])
                            mk.op("pool", lambda E, t=Sb[st][d][c], nblk=nblk: E.memset(t[:, nblk:nblk + 1], 0.0), (), [b_Sb[st]])

            def back(g, s, st):
                par = g % 2
                toff, coff, T, nblk, BW, TW, nbt = dims(s)
                for bt in range(nbt):
                    ubv = ub[st][:, :, bt * BW:(bt + 1) * BW]
                    ygv = yg[:, 0:TW].rearrange("p (b s) -> p s b", s=8)
                    for t in range(8):
                        py, bpy = nextps()
                        for s_ in range(8):
                            idx = 0 if s_ == t else ((t - s_) if s_ < t else (7 + s_ - t))
                            mm(py[0:16, 0:BW], KTg[par][:, idx, :], ubv[:, s_, :], s_ == 0, False, [b_KTg[par], b_ub[st]], [bpy])
                        b0 = bt * BW
                        mm(py[0:16, 0:BW], OTr[:, t, g, :], Sb[st][0][0][:, b0:b0 + BW], False, False, [bt_, b_Sb[st]], [bpy])
                        mm(py[0:16, 0:BW], OTn[:, t, g, :], Sb[st][0][1][:, b0:b0 + BW], False, False, [bt_, b_Sb[st]], [bpy])
                        mm(py[0:16, 0:BW], OTr[:, 7 - t, 32 + g, :], Sb[st][1][0][:, b0 + 1:b0 + 1 + BW], False, False, [bt_, b_Sb[st]], [bpy])
                        mm(py[0:16, 0:BW], OTn[:, 7 - t, 32 + g, :], Sb[st][1][1][:, b0 + 1:b0 + 1 + BW], False, True, [bt_, b_Sb[st]], [bpy])
                        cp("act", ygv[:, t, :], py[0:16, 0:BW], [bpy], [b_yg])
                    dma("sp", YS[16 * g:16 * g + 16, toff + bt * TW:toff + (bt + 1) * TW], yg[:, 0:TW], [b_yg], [b_YS])

            units = [(g, s) for g in range(32) for s in range(NS)]
            prev = None
            for ui, (g, s) in enumerate(units):
                if s == 0:
                    gconsts(g)
                front(g, s, ui % 2)
                if prev is not None:
                    back(prev[0], prev[1], (ui - 1) % 2)
                prev = (g, s)
            back(prev[0], prev[1], (len(units) - 1) % 2)
            mk.flush(final=True)

    if upto >= 2:
        s5_pass()

    def mix_pass():
        es, sb, ps = scope("mx_")
        with es:
            N = 256
            pz = [ps("pz%d" % i, [128, 512]) for i in range(8)]
            b_pz = [Buf() for _ in range(8)]
            pctr = [0]

            def nextps():
                i = pctr[0] % 8
                pctr[0] += 1
                return pz[i], b_pz[i]
            bc_ = Buf()
            o64 = sb("o64", [64, 64])
            mk.op("pool", lambda E: E.memset(o64[:], 1.0 / 64.0), (), [bc_])
            ones_bf = sb("ones_bf", [128, 128], BF16)
            mk.op("pool", lambda E: E.memset(ones_bf[:], 1.0), (), [bc_])
            lg = sb("lg", [64, 8]); lb = sb("lb", [64, 8])
            dma("sp", lg[:], lnx_g.rearrange("(h p) -> p h", p=64), (), [bc_], slow=True)
            dma("sp", lb[:], lnx_b.rearrange("(h p) -> p h", p=64), (), [bc_], slow=True)
            g2t = sb("g2t", [128, 512]); dma("sp", g2t[:], g2, (), [bc_])
            mug = sb("mug", [128, 1]); hmg = sb("hmg", [128, 1]); omg = sb("omg", [128, 1])
            dma("sp", mug[:], mu_shift[1792:1920].rearrange("(p o) -> p o", o=1), (), [bc_], slow=True)
            ts("dve", hmg[:], mug[:], 0.5, None, ALU.mult, None, [bc_], [bc_])
            ts("dve", omg[:], mug[:], -1.0, 1.0, ALU.mult, ALU.add, [bc_], [bc_])
            bgl = sb("bgl", [128, 4]); s5g = sb("s5g", [128, 4])
            dma("sp", bgl[:], b_glu.rearrange("(q p) -> p q", p=128), (), [bc_], slow=True)
            dma("sp", s5g[:], s5_out_g.rearrange("(q p) -> p q", p=128), (), [bc_], slow=True)
            wst = sb("wst", [128, 4, 128])
            mk.op("pool", lambda E: E.memset(wst[:], 0.0), (), [bc_])
            for g in range(32):
                r0 = (g % 8) * 16
                dma("sp" if g % 2 == 0 else "act", wst[r0:r0 + 16, g // 8, r0:r0 + 16], w_glu[g], [bc_], [bc_])
            Wbd = sb("Wbd", [128, 4, 128], BF16)
            cp("dve", Wbd[:], wst[:], [bc_], [bc_])
            wo_r = sb("wo_r", [64, 8, D], BF16); wo_s = sb("wo_s", [128, 4, D], BF16)
            stg = [sb("stg%d" % i, [128, D]) for i in range(2)]; b_stg = [Buf(), Buf()]
            for h in range(8):
                st_ = stg[h % 2]; bs_ = b_stg[h % 2]
                dma("sp", st_[0:64, :], w_out[h * 64:(h + 1) * 64, :], (), [bs_])
                cp("pool", wo_r[:, h, :], st_[0:64, :], [bs_], [bc_])
            for q in range(4):
                st_ = stg[q % 2]; bs_ = b_stg[q % 2]
                dma("sp", st_[:], w_out[512 + q * 128:512 + (q + 1) * 128, :], (), [bs_])
                cp("pool", wo_s[:, q, :], st_[:], [bs_], [bc_])

            def T4(name, dt=F32):
                return sb(name, [64, 8, N], dt), Buf()
            YF, b_YF = T4("YF"); YB, b_YB = T4("YB"); BF_, b_BF = T4("BF"); BB, b_BB = T4("BB")
            Ym, b_Ym = T4("Ym"); SQt, b_SQt = T4("SQt"); RS, b_RS = T4("RS")
            yr, b_yr = T4("yr", BF16)
            XG = sb("XG", [128, N + 2]); b_XG = Buf(); tw = sb("tw", [128, N]); b_tw = Buf()
            sg = sb("sg", [128, N]); b_sg = Buf()
            S5 = sb("S5", [128, 4, N]); b_S5 = Buf(); Z1 = sb("Z1", [128, 4, N]); b_Z1 = Buf()
            Z2 = sb("Z2", [128, 4, N]); b_Z2 = Buf(); Zb = sb("Zb", [128, 4, N], BF16); b_Zb = Buf()
            r2 = sb("r2", [128, N]); b_r2 = Buf()
            ysb = sb("ysb", [128, 4, N], BF16); b_ysb = Buf()
            xTt = sb("xTt", [128, 8, N]); b_xTt = Buf(); X1t = sb("X1t", [128, 8, N]); b_X1t = Buf()

            def bc(t, n):
                return t[:, :, None].broadcast_to([64, 8, n])

            def fl(t):
                return t[:].rearrange("p h t -> p (h t)")
            for s, (toff, coff, T) in enumerate(SEQ):
                for ti in range(T // N):
                    tl = ti * N; tg = toff + tl; c0 = coff + tl
                    dma("sp", YF[:], YD[0, :, :, tg:tg + N], [b_YD], [b_YF])
                    dma("act", YB[:], YD[1, :, :, tg:tg + N], [b_YD], [b_YB])
                    dma("sp", BF_[:], BD[0, :, :, tg:tg + N], [b_BD], [b_BF])
                    dma("act", BB[:], BD[1, :, :, tg:tg + N], [b_BD], [b_BB])
                    dma("sp", XG[:], Pscr[1792:1920, c0:c0 + N + 2], [b_P], [b_XG])
                    dma("act", S5[:], YS[:, tg:tg + N].rearrange("(q p) t -> p q t", p=128), [b_YS], [b_S5])
                    dma("sp", xTt[:], XT[:, tg:tg + N].rearrange("(k p) t -> p k t", p=128), [b_XT], [b_xTt])
                    tt("pool", Ym[:], YF[:], YB[:], ALU.add, [b_YF, b_YB], [b_Ym])
                    for j in range(4):
                        pm_, bpm = nextps()
                        mm(pm_[0:64, :], o64[:], fl(Ym)[:, j * 512:(j + 1) * 512], True, True, [bc_, b_Ym], [bpm])
                        tt("dve", fl(YF)[:, j * 512:(j + 1) * 512], fl(Ym)[:, j * 512:(j + 1) * 512], pm_[0:64, :],
                           ALU.subtract, [bpm, b_Ym], [b_YF])
                    tt("pool", SQt[:], YF[:], YF[:], ALU.mult, [b_YF], [b_SQt])
                    for j in range(4):
                        pm_, bpm = nextps()
                        mm(pm_[0:64, :], o64[:], fl(SQt)[:, j * 512:(j + 1) * 512], True, True, [bc_, b_SQt], [bpm])
                        act(fl(RS)[:, j * 512:(j + 1) * 512], pm_[0:64, :], AF.Sqrt, [bpm], [b_RS], bias=LNX_EPS)
                    mk.op("dve", lambda E: E.reciprocal(out=RS[:], in_=RS[:]), [b_RS], [b_RS])
                    tt("pool", Ym[:], YF[:], RS[:], ALU.mult, [b_YF, b_RS], [b_Ym])
                    tt("pool", Ym[:], Ym[:], bc(lg, N), ALU.mult, [b_Ym, bc_], [b_Ym])
                    tt("pool", Ym[:], Ym[:], bc(lb, N), ALU.add, [b_Ym, bc_], [b_Ym])
                    tt("pool", Ym[:], Ym[:], BF_[:], ALU.add, [b_Ym, b_BF], [b_Ym])
                    tt("pool", Ym[:], Ym[:], BB[:], ALU.add, [b_Ym, b_BB], [b_Ym])
                    tt("dve", tw[:], XG[:, 0:N], XG[:, 2:N + 2], ALU.add, [b_XG], [b_tw])
                    ts("dve", tw[:], tw[:], hmg[:, 0:1], None, ALU.mult, None, [b_tw, bc_], [b_tw])
                    stt(sg[:], XG[:, 1:N + 1], omg[:, 0:1], tw[:], ALU.mult, ALU.add, [b_XG, b_tw, bc_], [b_sg])
                    act(sg[:], sg[:], AF.Sigmoid, [b_sg], [b_sg])
                    for h2_ in range(4):
                        pm_, bpm = nextps()
                        for j in range(2):
                            h = 2 * h2_ + j
                            mm(pm_[0:64, j * N:(j + 1) * N], g2t[:, h * 64:(h + 1) * 64], sg[:], True, True, [bc_, b_sg], [bpm])
                        tt("dve", yr[:, 2 * h2_:2 * h2_ + 2, :], pm_[0:64, :].rearrange("p (h n) -> p h n", n=N),
                           Ym[:, 2 * h2_:2 * h2_ + 2, :], ALU.mult, [bpm, b_Ym], [b_yr])
                    K0 = 2.0 * math.sqrt(2.0 / math.pi)
                    tt("pool", Z1[:], S5[:], S5[:], ALU.mult, [b_S5], [b_Z1])
                    ts("dve", Z1[:], Z1[:], 0.044715, 1.0, ALU.mult, ALU.add, [b_Z1], [b_Z1])
                    tt("pool", Z1[:], Z1[:], S5[:], ALU.mult, [b_Z1, b_S5], [b_Z1])
                    act(Z1[:], Z1[:], AF.Sigmoid, [b_Z1], [b_Z1], scale=K0)
                    tt("pool", Z1[:], Z1[:], S5[:], ALU.mult, [b_Z1, b_S5], [b_Z1])
                    cp("dve", Zb[:], Z1[:], [b_Z1], [b_Zb])
                    for q in range(4):
                        pm_, bpm = nextps()
                        mm(pm_[:, 0:N], Wbd[:, q, :], Zb[:, q, :], True, True, [bc_, b_Zb], [bpm])
                        mk.op("act", lambda E, pm_=pm_, q=q: E.activation(out=Z2[:, q, :], in_=pm_[:, 0:N], func=AF.Sigmoid,
                                                                        bias=bgl[:, q:q + 1], scale=1.0), [bpm, bc_], [b_Z2])
                    tt("pool", Z2[:], Z2[:], Z1[:], ALU.mult, [b_Z2, b_Z1], [b_Z2])
                    tt("pool", Zb[:], Z2[:], Z2[:], ALU.mult, [b_Z2], [b_Zb])
                    pm_, bpm = nextps()
                    for q in range(4):
                        mm(pm_[:, 0:N], ones_bf[:], Zb[:, q, :], q == 0, q == 3, [bc_, b_Zb], [bpm])
                    act(r2[:], pm_[:, 0:N], AF.Sqrt, [bpm], [b_r2], bias=RMS_EPS, scale=1.0 / 512.0)
                    mk.op("dve", lambda E: E.reciprocal(out=r2[:], in_=r2[:]), [b_r2], [b_r2])
                    for q in range(4):
                        stt(ysb[:, q, :], Z2[:, q, :], s5g[:, q:q + 1], r2[:], ALU.mult, ALU.mult, [b_Z2, b_r2, bc_], [b_ysb])
                    for dm in range(8):
                        pm_, bpm = nextps()
                        for h in range(8):
                            mm(pm_[:, 0:N], wo_r[:, h, dm * 128:(dm + 1) * 128], yr[:, h, :], h == 0, False, [bc_, b_yr], [bpm])
                        for q in range(4):
                            mm(pm_[:, 0:N], wo_s[:, q, dm * 128:(dm + 1) * 128], ysb[:, q, :], False, q == 3, [bc_, b_ysb], [bpm])
                        stt(X1t[:, dm, :], pm_[:, 0:N], modT[:, 16 + dm, s:s + 1], xTt[:, dm, :], ALU.mult, ALU.add,
                            [bpm, b_mod, b_xTt], [b_X1t])
                    dma("sp", X1[:, tg:tg + N].rearrange("(k p) t -> p k t", p=128), X1t[:], [b_X1t], [b_X1])
            mk.flush(final=True)

    if upto >= 3:
        mix_pass()

    def ffn_pass():
        es, sb, ps = scope("ff_")
        with es:
            N = 256
            pz = [ps("pz%d" % i, [128, 512]) for i in range(8)]
            b_pz = [Buf() for _ in range(8)]
            pctr = [0]

            def nextps():
                i = pctr[0] % 8
                pctr[0] += 1
                return pz[i], b_pz[i]
            bc_ = Buf()
            ones_bf = sb("ones_bf", [128, 128], BF16)
            mk.op("pool", lambda E: E.memset(ones_bf[:], 1.0), (), [bc_])
            n2g = sb("n2g", [128, 8]); fg = sb("fg", [128, 8]); sc2 = sb("sc2", [128, 8, NS])
            dma("sp", n2g[:], norm2_g.rearrange("(k p) -> p k", p=128), (), [bc_], slow=True)
            dma("sp", fg[:], final_g.rearrange("(k p) -> p k", p=128), (), [bc_], slow=True)
            for s in range(NS):
                stt(sc2[:, :, s], modT[:, 32:40, s], 1.0, n2g[:], ALU.add, ALU.mult, [b_mod, bc_], [bc_])
            w1 = sb("w1", [128, 8, DFF], BF16); w3 = sb("w3", [128, 8, DFF], BF16); w2_ = sb("w2_", [128, 22, D], BF16)
            stg = [sb("stg%d" % i, [128, 1408]) for i in range(2)]; b_stg = [Buf(), Buf()]
            n = 0
            for (src, dst) in ((w_ff1, w1), (w_ff3, w3)):
                for k in range(8):
                    for hf in range(2):
                        st_ = stg[n % 2]; bs_ = b_stg[n % 2]; n += 1
                        dma("sp" if n % 2 else "act", st_[:], src[k * 128:(k + 1) * 128, hf * 1408:(hf + 1) * 1408], (), [bs_])
                        cp("pool" if n % 2 else "dve", dst[:, k, hf * 1408:(hf + 1) * 1408], st_[:], [bs_], [bc_])
            for k in range(22):
                st_ = stg[n % 2]; bs_ = b_stg[n % 2]; n += 1
                dma("sp" if n % 2 else "act", st_[:, 0:D], w_ff2[k * 128:(k + 1) * 128, :], (), [bs_])
                cp("pool" if n % 2 else "dve", w2_[:, k, :], st_[:, 0:D], [bs_], [bc_])
            X1t = sb("X1t", [128, 8, N]); b_X1t = Buf()
            sq = sb("sq", [128, 8, N], BF16); b_sq = Buf()
            rstd = sb("rstd", [128, N]); b_rstd = Buf(); tmp = sb("tmp", [128, N]); b_tmp = Buf()
            h2 = sb("h2", [128, 8, N], BF16); b_h2 = Buf()
            fm = sb("fm", [128, 22, N], BF16); b_fm = Buf()
            av = [sb("av%d" % i, [128, N]) for i in range(2)]; b_av = [Buf(), Buf()]
            X2t = sb("X2t", [128, 8, N]); b_X2t = Buf()
            ytm = sb("ytm", [128, 2, D]); b_ytm = Buf()

            def rms_bc(src, bsrc):
                for dc in range(8):
                    act(sq[:, dc, :], src[:, dc, :], AF.Square, [bsrc], [b_sq])
                pm_, bpm = nextps()
                for dc in range(8):
                    mm(pm_[:, 0:N], ones_bf[:], sq[:, dc, :], dc == 0, dc == 7, [bc_, b_sq], [bpm])
                act(rstd[:], pm_[:, 0:N], AF.Sqrt, [bpm], [b_rstd], bias=RMS_EPS, scale=1.0 / D)
                mk.op("dve", lambda E: E.reciprocal(out=rstd[:], in_=rstd[:]), [b_rstd], [b_rstd])
            for s, (toff, coff, T) in enumerate(SEQ):
                for ti in range(T // N):
                    tg = toff + ti * N
                    dma("sp", X1t[:], X1[:, tg:tg + N].rearrange("(k p) t -> p k t", p=128), [b_X1], [b_X1t])
                    rms_bc(X1t, b_X1t)
                    for dc in range(8):
                        tt("dve", tmp[:], X1t[:, dc, :], rstd[:], ALU.mult, [b_X1t, b_rstd], [b_tmp])
                        ts("dve", h2[:, dc, :], tmp[:], sc2[:, dc, s:s + 1], modT[:, 24 + dc, s:s + 1], ALU.mult, ALU.add,
                           [b_tmp, bc_, b_mod], [b_h2])
                    for mc in range(22):
                        p1, bp1 = nextps()
                        for dc in range(8):
                            mm(p1[:, 0:N], w1[:, dc, mc * 128:(mc + 1) * 128], h2[:, dc, :], dc == 0, dc == 7, [bc_, b_h2], [bp1])
                        p3, bp3 = nextps()
                        for dc in range(8):
                            mm(p3[:, 0:N], w3[:, dc, mc * 128:(mc + 1) * 128], h2[:, dc, :], dc == 0, dc == 7, [bc_, b_h2], [bp3])
                        a_ = av[mc % 2]; ba_ = b_av[mc % 2]
                        act(a_[:], p1[:, 0:N], AF.Silu, [bp1], [ba_])
                        tt("dve", fm[:, mc, :], a_[:], p3[:, 0:N], ALU.mult, [ba_, bp3], [b_fm])
                    for dm in range(8):
                        pm_, bpm = nextps()
                        for mc in range(22):
                            mm(pm_[:, 0:N], w2_[:, mc, dm * 128:(dm + 1) * 128], fm[:, mc, :], mc == 0, mc == 21, [bc_, b_fm], [bpm])
                        stt(X2t[:, dm, :], pm_[:, 0:N], modT[:, 40 + dm, s:s + 1], X1t[:, dm, :], ALU.mult, ALU.add,
                            [bpm, b_mod, b_X1t], [b_X2t])
                    rms_bc(X2t, b_X2t)
                    for dc in range(8):
                        stt(X2t[:, dc, :], X2t[:, dc, :], fg[:, dc:dc + 1], rstd[:], ALU.mult, ALU.mult,
                            [b_X2t, bc_, b_rstd], [b_X2t])
                    for j in range(N // 128):
                        for dq in range(2):
                            pm_, bpm = nextps()
                            for k in range(4):
                                dc = dq * 4 + k
                                mk.op("pe", lambda E, pm_=pm_, dc=dc, j=j, k=k: E.transpose(
                                    out=pm_[:, k * 128:(k + 1) * 128], in_=X2t[:, dc, j * 128:(j + 1) * 128], identity=ident[:]),
                                    [b_X2t, b_ident], [bpm])
                            cp("act" if dq == 0 else "dve", ytm[:, j, dq * 512:(dq + 1) * 512], pm_[:], [bpm], [b_ytm])
                    dma("sp", y_out[tg:tg + N, :].rearrange("(j p) d -> p j d", p=128), ytm[:], [b_ytm], [])
            mk.flush(final=True)

    if upto >= 4:
        ffn_pass()

    outer.close()
    return nc


def core_inputs(P, x, c):
    f = np.float32
    m = {"x": x, "c": c}
    for k in ("norm1_g", "w_ada", "b_ada", "w_in", "mu_shift", "w0", "w2", "a0", "a2", "g2", "k_k", "k_a",
              "lnx_g", "lnx_b", "lam_re", "lam_im", "log_dt", "b_re", "b_im", "c_re", "c_im", "w_glu",
              "s5_out_g", "w_out", "norm2_g", "w_ff1", "w_ff3", "w_ff2", "final_g"):
        m[k] = P[k]
    m["r_k"] = P["r_k"].reshape(512)
    m["d_skip"] = P["d_skip"].reshape(512)
    m["b_glu"] = P["b_glu"].reshape(512)
    return {k: np.ascontiguousarray(v, dtype=f) for k, v in m.items()}


_T_PROMPT = 8192
_T_SAMPLE = 4096


def kernel(**inputs):
    n = 8
    P = {}
    for k, v in inputs.items():
        if k in ("x_prompt", "x_sample", "c_prompt", "c_sample"):
            continue
        v = np.asarray(v)
        P[k] = v if k == "final_g" else v[0]
    xp = np.asarray(inputs["x_prompt"]); xs = np.asarray(inputs["x_sample"])
    cpr = np.asarray(inputs["c_prompt"]); cs = np.asarray(inputs["c_sample"])
    TS = [xp.shape[1], xs.shape[1]]
    nc = build_program(TS, upto=int(os.environ.get('KUPTO', '99')))
    in_maps = []
    for b in range(n):
        x = np.concatenate([xp[b], xs[b]], axis=0)
        c = np.stack([cpr[b], cs[b]], axis=0)
        in_maps.append(core_inputs(P, x, c))
    res = run_bass_kernel_spmd(nc, in_maps, core_ids=list(range(n)))
    yp = np.stack([res.results[b]["y"][:TS[0]] for b in range(n)], axis=0).astype(np.float32)
    ys = np.stack([res.results[b]["y"][TS[0]:] for b in range(n)], axis=0).astype(np.float32)
    return (yp, ys)
```

```python
import os
import math
import numpy as np
import concourse.bass as bass
import concourse.mybir as mybir
from concourse.bass_utils import run_bass_kernel_spmd

F32 = mybir.dt.float32
BF16 = mybir.dt.bfloat16
AF = mybir.ActivationFunctionType
ALU = mybir.AluOpType
AX = mybir.AxisListType

D = 1024
DFF = 2816
NPROJ = 2432
RW = 1920
NDS = 12
RMS_EPS = 1e-6
LNX_EPS = 64e-5


class Buf:
    __slots__ = ("w", "r")

    def __init__(self):
        self.w = None
        self.r = {}


class MK:
    BLK = {"pe": "tensor", "dve": "vector", "act": "scalar", "pool": "gpsimd", "sp": "sync"}

    def __init__(self, nc, same=True):
        self.nc = nc
        self.same = same
        self.names = ["pe", "dve", "act", "pool", "sp"]
        self.sem = {k: nc.alloc_semaphore(name="s_" + k) for k in self.names}
        self.cnt = {k: 0 for k in self.names}
        self.seen = {k: {} for k in self.names}
        self.prog = {k: [] for k in self.names}
        self.dsem = [nc.alloc_semaphore(name="d%d" % i) for i in range(NDS)]
        self.dcnt = [0] * NDS
        self.dnext = 0
        self.deferred = None

    def semof(self, key):
        if isinstance(key, tuple):
            return self.dsem[key[1]]
        return self.sem[key]

    def _deps(self, e, reads, writes):
        deps = {}

        def add(k, v):
            if deps.get(k, 0) < v:
                deps[k] = v

        for b in reads:
            if b.w:
                add(*b.w)
        for b in writes:
            if b.w:
                add(*b.w)
            for k, v in b.r.items():
                add(k, v)
        out = []
        for k, v in deps.items():
            if k == e and (e == "pe" or not self.same):
                continue
            if self.seen[e].get(k, 0) >= v:
                continue
            self.seen[e][k] = v
            out.append((k, v))
        return out

    def _mark(self, tok, reads, writes):
        k, v = tok
        for b in reads:
            if b.r.get(k, 0) < v:
                b.r[k] = v
        for b in writes:
            b.w = tok
            b.r = {}

    def op(self, e, fn, reads=(), writes=()):
        if self.deferred is not None:
            self.deferred.append((0, e, fn, reads, writes))
            return
        waits = self._deps(e, reads, writes)
        self.cnt[e] += 1
        tok = (e, self.cnt[e])
        self.prog[e].append((waits, fn, self.sem[e], 1))
        self._mark(tok, reads, writes)

    def replay(self, pending, n):
        keep = self.deferred
        self.deferred = None
        last = None
        cnt = 0
        while pending and (cnt < n or last == "pe"):
            kind, e, fn, reads, writes = pending.pop(0)
            (self.dma if kind else self.op)(e, fn, reads, writes)
            last = e if not kind else None
            cnt += 1
        self.deferred = keep

    def dma(self, q, fn, reads=(), writes=()):
        if self.deferred is not None:
            self.deferred.append((1, q, fn, reads, writes))
            return
        i = self.dnext
        self.dnext = (i + 1) % NDS
        key = ("d", i)
        waits = self._deps(q, reads, writes)
        if self.dcnt[i] > 0 and self.seen[q].get(key, 0) < self.dcnt[i]:
            waits.append((key, self.dcnt[i]))
            self.seen[q][key] = self.dcnt[i]
        self.dcnt[i] += 16
        tok = (key, self.dcnt[i])
        self.prog[q].append((waits, fn, self.dsem[i], 16))
        self._mark(tok, reads, writes)

    def flush(self, final=False):
        nc = self.nc
        fin = []
        for i in (range(NDS) if final else []):
            if self.dcnt[i] > 0:
                fin.append((("d", i), self.dcnt[i]))
        for k in (self.names if final else []):
            if k != "sp" and self.cnt[k] > 0:
                fin.append((k, self.cnt[k]))
        with nc.Block() as block:
            for e in self.names:
                prog = self.prog[e]
                extra = fin if e == "sp" else []

                def body(eng, prog=prog, extra=extra):
                    for waits, fn, sem, inc in prog:
                        for k, v in waits:
                            eng.wait_ge(self.semof(k), v)
                        fn(eng).then_inc(sem, inc)
                    for k, v in extra:
                        eng.wait_ge(self.semof(k), v)

                getattr(block, self.BLK[e])(body)
        self.prog = {k: [] for k in self.names}

    def emit(self):
        self.flush(final=True)


def build_program(TS, dbg=False, upto=99):
    import contextlib
    nc = bass.Bass("TRN2", target_bir_lowering=False)
    mk = MK(nc, same=(os.environ.get("MK_SAME", "1") == "1"))
    TT = sum(TS)
    NS = len(TS)
    WP = TT + 2 * NS
    SEQ = []
    o = 0
    for s, T in enumerate(TS):
        SEQ.append((o, o + 2 * s, T))
        o += T

    def din(name, shape):
        return nc.dram_tensor(name, list(shape), F32, kind="ExternalInput").ap()

    def dscr(name, shape):
        return nc.dram_tensor(name, list(shape), F32, kind=("ExternalOutput" if dbg else "Internal")).ap()

    x_in = din("x", (TT, D))
    c_in = din("c", (NS, D))
    norm1_g = din("norm1_g", (D,))
    w_ada = din("w_ada", (D, 6 * D))
    b_ada = din("b_ada", (6 * D,))
    w_in = din("w_in", (D, NPROJ))
    mu_shift = din("mu_shift", (RW,))
    w0 = din("w0", (2, 512)); w2 = din("w2", (2, 64, 512))
    a0 = din("a0", (2, 512)); a2 = din("a2", (2, 64, 512))
    g2 = din("g2", (128, 512))
    k_k = din("k_k", (512,)); k_a = din("k_a", (512,)); r_k = din("r_k", (512,))
    lnx_g = din("lnx_g", (512,)); lnx_b = din("lnx_b", (512,))
    lam_re = din("lam_re", (2, 32, 64)); lam_im = din("lam_im", (2, 32, 64)); log_dt = din("log_dt", (2, 32))
    b_re = din("b_re", (2, 32, 64, 16)); b_im = din("b_im", (2, 32, 64, 16))
    c_re = din("c_re", (2, 32, 16, 64)); c_im = din("c_im", (2, 32, 16, 64))
    d_skip = din("d_skip", (512,)); w_glu = din("w_glu", (32, 16, 16)); b_glu = din("b_glu", (512,))
    s5_out_g = din("s5_out_g", (512,))
    w_out = din("w_out", (D, D)); norm2_g = din("norm2_g", (D,))
    w_ff1 = din("w_ff1", (D, DFF)); w_ff3 = din("w_ff3", (D, DFF)); w_ff2 = din("w_ff2", (DFF, D))
    final_g = din("final_g", (D,))
    y_out = nc.dram_tensor("y", [TT, D], F32, kind="ExternalOutput").ap()

    Pscr = dscr("Pscr", (NPROJ, WP))
    XT = dscr("XT", (D, TT))
    YD = dscr("YD", (2, 64, 8, TT))
    BD = dscr("BD", (2, 64, 8, TT))
    YS = dscr("YS", (512, TT))
    X1 = dscr("X1", (D, TT))
    MODS = dscr("MODS", (128, 48 * NS))
    b_P = Buf(); b_XT = Buf(); b_YD = Buf(); b_BD = Buf(); b_YS = Buf(); b_X1 = Buf(); b_MODS = Buf()

    def tt(e, out, a, b, op, r, w):
        mk.op(e, lambda E: E.tensor_tensor(out=out, in0=a, in1=b, op=op), r, w)

    def ts(e, out, a, s1, s2, op0, op1, r, w):
        if op1 is None:
            mk.op(e, lambda E: E.tensor_scalar(out=out, in0=a, scalar1=s1, scalar2=None, op0=op0), r, w)
        else:
            mk.op(e, lambda E: E.tensor_scalar(out=out, in0=a, scalar1=s1, scalar2=s2, op0=op0, op1=op1), r, w)

    def stt(out, a, sc, b, op0, op1, r, w):
        mk.op("dve", lambda E: E.scalar_tensor_tensor(out=out, in0=a, scalar=sc, in1=b, op0=op0, op1=op1), r, w)

    def act(out, a, func, r, w, bias=0.0, scale=1.0):
        mk.op("act", lambda E: E.activation(out=out, in_=a, func=func, bias=bias, scale=scale), r, w)

    def cp(e, out, a, r, w):
        if e == "act":
            mk.op("act", lambda E: E.activation(out=out, in_=a, func=AF.Copy), r, w)
        else:
            mk.op(e, lambda E: E.tensor_copy(out=out, in_=a), r, w)

    def mm(out, lhsT, rhs, st, sp_, r, w):
        mk.op("pe", lambda E: E.matmul(out=out, lhsT=lhsT, rhs=rhs, start=st, stop=sp_), r, w)

    F32R = mybir.dt.float32r
    USE_R = os.environ.get("RW_F32R", "1") == "1"

    def RR(ap):
        return ap.bitcast(F32R) if USE_R else ap

    def mmr(out, lhsT, rhs, r, w, st=True, sp_=True):
        mk.op("pe", lambda E: E.matmul(out=out, lhsT=lhsT.bitcast(F32R), rhs=rhs.bitcast(F32R), start=st, stop=sp_), r, w)

    def dma(q, out, in_, r, w, slow=False):
        if slow:
            mk.dma(q, lambda E: E.dma_start(out=out, in_=in_, allow_slow_non_contiguous=True), r, w)
        else:
            mk.dma(q, lambda E: E.dma_start(out=out, in_=in_), r, w)

    def scope(pfx=""):
        es = contextlib.ExitStack()

        def sb(name, shape, dt=F32):
            return es.enter_context(nc.sbuf_tensor(pfx + name, list(shape), dt))

        def ps(name, shape, dt=F32):
            return es.enter_context(nc.psum_tensor(pfx + name, list(shape), dt))
        return es, sb, ps

    def consts(sb):
        ident = sb("ident", [128, 128]); b_ident = Buf()
        mk.op("pool", lambda E: E.memset(ident[:], 1.0), (), [b_ident])
        mk.op("pool", lambda E: E.affine_select(out=ident[:], in_=ident[:], pattern=[[-1, 128]],
                                                compare_op=ALU.is_equal, fill=0.0, base=0, channel_multiplier=1),
              [b_ident], [b_ident])
        return ident, b_ident
    outer, osb, ops_ = scope("o_")
    ident, b_ident = consts(osb)
    modT = osb("modT", [128, 48, NS]); b_mod = Buf()
    sc1 = osb("sc1", [128, 8, NS]); b_sc1 = Buf()

    def pass0():
        es, sb, ps = scope("p0_")
        with es:
            ones_bf = sb("ones_bf", [128, 128], BF16); b_ones = Buf()
            mk.op("pool", lambda E: E.memset(ones_bf[:], 1.0), (), [b_ones])
            cT = sb("cT", [128, 8, NS]); b_cT = Buf()
            scT = sb("scT", [128, 8, NS]); b_scT = Buf()
            for s in range(NS):
                dma("sp", cT[:, :, s], c_in[s].rearrange("(k p) -> p k", p=128), (), [b_cT], slow=True)
            act(scT[:], cT[:], AF.Silu, [b_cT], [b_scT])
            badaT = sb("badaT", [128, 48]); b_bada = Buf()
            dma("sp", badaT[:], b_ada.rearrange("(k p) -> p k", p=128), (), [b_bada], slow=True)
            g1T = sb("g1T", [128, 8]); b_g1 = Buf()
            dma("sp", g1T[:], norm1_g.rearrange("(k p) -> p k", p=128), (), [b_g1], slow=True)
            wada_t = [sb("wada%d" % i, [128, 8, 256]) for i in range(2)]
            b_wada = [Buf(), Buf()]
            ps_mod_full = ps("ps_mod", [128, 512]); b_psmod = Buf()
            ps_mod = ps_mod_full[:, 0:4 * NS].rearrange("p (a b) -> p a b", b=NS)
            for slab in range(24):
                wt = wada_t[slab % 2]; bw = b_wada[slab % 2]
                dma("sp" if slab % 2 == 0 else "act", wt[:],
                    w_ada[:, slab * 256:(slab + 1) * 256].rearrange("(k p) n -> p k n", p=128), (), [bw])
                for j in range(2):
                    for k in range(8):
                        mm(ps_mod[:, j, :], wt[:, k, j * 128:(j + 1) * 128], scT[:, k, :], k == 0, k == 7,
                           [bw, b_scT], [b_psmod])
                for s in range(NS):
                    tt("dve", modT[:, slab * 2:(slab + 1) * 2, s], ps_mod[:, 0:2, s],
                       badaT[:, slab * 2:(slab + 1) * 2], ALU.add, [b_psmod, b_bada], [b_mod])
            for s in range(NS):
                stt(sc1[:, :, s], modT[:, 8:16, s], 1.0, g1T[:], ALU.add, ALU.mult, [b_mod, b_g1], [b_sc1])

            w_in_bf = sb("w_in_bf", [128, 8, NPROJ], BF16); b_win = Buf()
            wst = [sb("wst%d" % i, [128, NPROJ]) for i in range(2)]; b_wst = [Buf(), Buf()]
            for k in range(8):
                dma("sp", wst[k % 2][:], w_in[k * 128:(k + 1) * 128, :], (), [b_wst[k % 2]])
                cp("pool", w_in_bf[:, k, :], wst[k % 2][:], [b_wst[k % 2]], [b_win])

            NT = 512
            xtm = [sb("xtm%d" % i, [128, 4, D]) for i in range(2)]; b_xtm = [Buf(), Buf()]
            xT = sb("xT", [128, 8, NT]); b_xT = Buf()
            sq = sb("sq", [128, 8, NT], BF16); b_sq = Buf()
            rstd = sb("rstd", [128, NT]); b_rstd = Buf()
            tmp = sb("tmp0", [128, NT]); b_tmp = Buf()
            hT = sb("hT", [128, 8, NT], BF16); b_hT = Buf()
            pev = [sb("pev%d" % i, [128, NT]) for i in range(3)]; b_pev = [Buf() for _ in range(3)]
            zcol = sb("zcol", [128, 1]); b_zcol = Buf()
            mk.op("pool", lambda E: E.memset(zcol[:], 0.0), (), [b_zcol])
            pst = [ps("pst%d" % i, [128, NT]) for i in range(4)]; b_pst = [Buf() for _ in range(4)]
            psm = [ps("psm%d" % i, [128, NT]) for i in range(3)]; b_psm = [Buf() for _ in range(3)]
            for s, (toff, coff, T) in enumerate(SEQ):
                for mc in range(19):
                    for cc in (coff, coff + T + 1):
                        dma("sp", Pscr[mc * 128:(mc + 1) * 128, cc:cc + 1], zcol[:], [b_zcol], [b_P], slow=True)
                for ti in range(T // NT):
                    t0 = toff + ti * NT
                    xt = xtm[ti % 2]; bx = b_xtm[ti % 2]
                    dma("sp", xt[:], x_in[t0:t0 + NT, :].rearrange("(j p) d -> p j d", p=128), (), [bx])
                    for dc in range(8):
                        pt = pst[dc % 4]; bp = b_pst[dc % 4]
                        for j in range(4):
                            mk.op("pe", lambda E, pt=pt, xt=xt, j=j, dc=dc: E.transpose(
                                out=pt[:, j * 128:(j + 1) * 128], in_=xt[:, j, dc * 128:(dc + 1) * 128],
                                identity=ident[:]), [bx, b_ident], [bp])
                        cp("dve", xT[:, dc, :], pt[:], [bp], [b_xT])
                        act(sq[:, dc, :], pt[:], AF.Square, [bp, b_xT], [b_sq])
                    dma("act", XT[:, t0:t0 + NT].rearrange("(k p) t -> p k t", p=128), xT[:], [b_xT], [b_XT])
                    pm = psm[0]; bpm = b_psm[0]
                    for dc in range(8):
                        mm(pm[:], ones_bf[:], sq[:, dc, :], dc == 0, dc == 7, [b_sq, b_ones], [bpm])
                    act(rstd[:], pm[:], AF.Sqrt, [bpm], [b_rstd], bias=RMS_EPS, scale=1.0 / D)
                    mk.op("dve", lambda E: E.reciprocal(out=rstd[:], in_=rstd[:]), [b_rstd], [b_rstd])
                    for dc in range(8):
                        tt("dve", tmp[:], xT[:, dc, :], rstd[:], ALU.mult, [b_xT, b_rstd], [b_tmp])
                        ts("dve", hT[:, dc, :], tmp[:], sc1[:, dc, s:s + 1], modT[:, dc, s:s + 1], ALU.mult, ALU.add,
                           [b_tmp, b_sc1, b_mod], [b_hT])
                    for mc in range(19):
                        i3 = mc % 3
                        pm = psm[i3]; bpm = b_psm[i3]
                        for dc in range(8):
                            mm(pm[:], w_in_bf[:, dc, mc * 128:(mc + 1) * 128], hT[:, dc, :], dc == 0, dc == 7,
                               [b_win, b_hT], [bpm])
                        pv = pev[i3]; bpv = b_pev[i3]
                        cp("act", pv[:], pm[:], [bpm], [bpv])
                        cc = coff + 1 + ti * NT
                        dma("sp", Pscr[mc * 128:(mc + 1) * 128, cc:cc + NT], pv[:], [bpv], [b_P])
            mk.flush(final=True)

    pass0()
    def rwkv_pass(d):
        rev = (d == 1)
        es, sb, ps = scope("rw%d_" % d)
        with es:
            NT2 = 128
            psr = [ps("psr%d" % i, [128, 1024]) for i in range(4)]
            b_psr = [Buf() for _ in range(4)]
            pctr = [0]

            def nextps():
                i = pctr[0] % 4
                pctr[0] += 1
                return psr[i], b_psr[i]

            def T4(name):
                return sb(name, [64, 8, NT2]), Buf()

            def ldp(name, src512):
                t = sb(name, [64, 8]); b = Buf()
                dma("sp", t[:], src512.rearrange("(h p) -> p h", p=64), (), [b], slow=True)
                return t, b

            mu3 = sb("mu3", [64, 24]); b_mu3 = Buf()
            dma("sp", mu3[:], mu_shift[0:1536].rearrange("(g p) -> p g", p=64), (), [b_mu3], slow=True)
            hm3 = sb("hm3", [64, 24]); om3 = sb("om3", [64, 24]); b_hm3 = Buf()
            ts("dve", hm3[:], mu3[:], 0.5, None, ALU.mult, None, [b_mu3], [b_hm3])
            ts("dve", om3[:], mu3[:], -1.0, 1.0, ALU.mult, ALU.add, [b_mu3], [b_hm3])
            muw = sb("muw", [64, 2]); b_muw = Buf()
            dma("sp", muw[:, 0:1], mu_shift[1536 + 64 * d:1600 + 64 * d].rearrange("(p o) -> p o", o=1), (), [b_muw], slow=True)
            dma("sp", muw[:, 1:2], mu_shift[1664 + 64 * d:1728 + 64 * d].rearrange("(p o) -> p o", o=1), (), [b_muw], slow=True)
            hmw = sb("hmw", [64, 2]); omw = sb("omw", [64, 2]); b_hmw = Buf()
            ts("dve", hmw[:], muw[:], 0.5, None, ALU.mult, None, [b_muw], [b_hmw])
            ts("dve", omw[:], muw[:], -1.0, 1.0, ALU.mult, ALU.add, [b_muw], [b_hmw])
            w0d, b_w0d = ldp("w0d", w0[d]); a0d, b_a0d = ldp("a0d", a0[d])
            kk_, b_kk_ = ldp("kk_", k_k); ka_, b_ka_ = ldp("ka_", k_a); rk_, b_rk_ = ldp("rk_", r_k)
            omka = sb("omka", [64, 8]); b_omka = Buf()
            ts("dve", omka[:], ka_[:], -1.0, 1.0, ALU.mult, ALU.add, [b_ka_], [b_omka])
            w2d = sb("w2d", [64, 512]); a2d = sb("a2d", [64, 512]); b_w2d = Buf()
            dma("sp", w2d[:], w2[d], (), [b_w2d]); dma("sp", a2d[:], a2[d], (), [b_w2d])
            ones64 = sb("ones64", [64, 64]); b_c = Buf()
            mk.op("pool", lambda E: E.memset(ones64[:], 1.0), (), [b_c])
            maskA = sb("maskA", [64, 128]); maskL = sb("maskL", [64, 64]); MS = sb("MS", [64, 8 * NT2])
            mk.op("pool", lambda E: E.memset(maskA[:], 1.0), (), [b_c])
            mk.op("pool", lambda E: E.memset(maskL[:], 1.0), (), [b_c])
            mk.op("pool", lambda E: E.memset(MS[:], 1.0), (), [b_c])
            zc_ = 63 if rev else 0
            mk.op("pool", lambda E: E.memset(MS[:].rearrange("p (a l) -> p a l", l=64)[:, :, zc_:zc_ + 1], 0.0), [b_c], [b_c])

            def asel(ap, upper, strict):
                pat = [[1, 64]] if upper else [[-1, 64]]
                cm = -1 if upper else 1
                mk.op("pool", lambda E: E.affine_select(out=ap, in_=ap, pattern=pat, compare_op=ALU.is_ge, fill=0.0,
                                                        base=(-1 if strict else 0), channel_multiplier=cm),
                      [b_c], [b_c])
            asel(maskA[:, 0:64], not rev, True)
            asel(maskA[:, 64:128], not rev, False)
            asel(maskL[:], rev, True)
            mA = maskA[:, None, :].broadcast_to([64, 8, 128])
            mL = maskL[:, None, :].broadcast_to([64, 8, 64])
            id64 = ident[0:64, 0:64]
            idbc = ident[0:64, None, 0:64].broadcast_to([64, 8, 64])

            Lr = [sb("Lq%d" % q, [64, 8, NT2 + 2]) for q in range(2)]; b_L = [Buf() for _ in range(2)]
            Lr.append(Lr[0]); b_L.append(b_L[0])
            XW = sb("XW", [64, NT2 + 2]); XA = sb("XA", [64, NT2 + 2]); b_XW = Buf(); b_XA = Buf()
            T1, b_T1 = T4("T1")
            SH = [T4("SH%d" % q) for q in range(2)]
            (Rp, b_Rp), (Kp, b_Kp) = SH
            tt0, cp0 = tt, cp
            tw = sb("tw", [64, NT2]); b_tw = Buf()
            xwp = sb("xwp", [64, NT2]); xap = sb("xap", [64, NT2]); b_xwp = Buf(); b_xap = Buf()
            XB, b_XB = T4("XB"); E2, b_E2 = T4("E2"); AD, b_AD = T4("AD"); KR, b_KR = T4("KR")
            SS, b_SS = T4("SS"); KD, b_KD = T4("KD"); AB, b_AB = T4("AB"); BON, b_BON = XB, b_XB
            G, b_G = T1, b_T1; D1, b_D1 = E2, b_E2; D2, b_D2 = AD, b_AD; EP, b_EP = XB, b_XB; EN, b_EN = SS, b_SS
            T2, b_T2 = SS, b_SS
            SD = F32
            SETS = []
            for i_ in range(2):
                st_ = []
                for nm, shp in (("AR", [64, 8, 2, 128]), ("KT", [64, 9, NT2]), ("BT", [64, 9, NT2]), ("KH", [64, 8, NT2]),
                                ("BH", [64, 8, NT2]), ("Vp", [64, 8, NT2]), ("GL", [64, 16])):
                    st_ += [sb("%s_%d" % (nm, i_), shp), Buf()]
                SETS.append(st_)
            YT, b_YT = T4("YT")
            MT1 = sb("MT1", [64, 2, 8, 128], SD); MT2 = sb("MT2", [64, 2, 8, 128], SD); b_MT1 = Buf(); b_MT2 = Buf()
            P0 = sb("P0", [64, 17, 64], SD); b_P0 = Buf()
            PP = [sb("PP%d" % i, [64, 33, 64], SD) for i in range(2)]; b_PP = [Buf(), Buf()]
            Zt = [sb("Zt%d" % i, [64, 17, 128], SD) for i in range(2)]; b_Zt = [Buf() for _ in range(2)]
            VT = sb("VT", [64, 17, 64], SD); BHt = sb("BHt", [64, 17, 64], SD); KHt = sb("KHt", [64, 17, 64], SD)
            QT = sb("QT", [64, 2, 8, 64]); MM = sb("MM", [64, 17, 64]); DG = sb("DG", [64, 17, 64])
            b_VT = Buf(); b_BHt = Buf(); b_KHt = Buf(); b_QT = Buf(); b_MM = Buf(); b_DG = Buf()
            STt = [sb("ST%d" % i, [64, 9, 64]) for i in range(2)]; b_ST = [Buf(), Buf()]
            for t_, b__, r_ in ((Zt[0], b_Zt[0], 16), (Zt[1], b_Zt[1], 16)):
                ts("dve", RR(t_[:, r_, :]), maskA[:], 0.0, None, ALU.mult, None, [b_c], [b__])
            for t_, b__ in ((VT, b_VT), (BHt, b_BHt), (KHt, b_KHt), (MM, b_MM)):
                ts("dve", RR(t_[:, 16, :]), ones64[:], 0.0, None, ALU.mult, None, [b_c], [b__])
            for i_ in range(2):
                ts("dve", RR(STt[i_][:, 8, :]), ones64[:], 0.0, None, ALU.mult, None, [b_c], [b_ST[i_]])
                ts("dve", RR(SETS[i_][2][:, 8, :]), maskA[:], 0.0, None, ALU.mult, None, [b_c], [SETS[i_][3]])
                ts("dve", RR(SETS[i_][4][:, 8, :]), maskA[:], 0.0, None, ALU.mult, None, [b_c], [SETS[i_][5]])
            ts("dve", RR(P0[:, 16, :]), ones64[:], 0.0, None, ALU.mult, None, [b_c], [b_P0])
            for i_ in range(2):
                ts("dve", RR(PP[i_][:, 32, :]), ones64[:], 0.0, None, ALU.mult, None, [b_c], [b_PP[i_]])
            mA16 = maskA[:, None, :].broadcast_to([64, 16, 128])
            mL16 = maskL[:, None, :].broadcast_to([64, 16, 64])
            idbc16 = ident[0:64, None, 0:64].broadcast_to([64, 16, 64])

            def f16(t):
                if len(t.shape) == 3:
                    return t[:, 0:16, :]
                return t[:].rearrange("p c h n -> p (c h) n")

            def wd(t, blk, n, off=0):
                fl = t[:].rearrange("p a n -> p (a n)")
                return fl[:, blk * n + off:blk * n + off + 128]

            def pv(p, lo, n):
                return p[0:64, lo:lo + 16 * n].rearrange("p (a n) -> p a n", n=n)

            def v3(p, n):
                return p[0:64, 0:8 * n].rearrange("p (h n) -> p h n", n=n)

            def bc(t, lo, hi, n):
                return t[:, lo:hi, None].broadcast_to([64, hi - lo, n])

            def c4(t):
                return t[:].rearrange("p h (c l) -> p h c l", l=64)

            for s, (toff, coff, T) in enumerate(SEQ):
                sti_ = [0]
                ts("dve", RR(STt[0][:, 0:8, :]), STt[1][:, 0:8, :], 0.0, None, ALU.mult, None, [b_ST[1]], [b_ST[0]])
                ntile = T // NT2
                order = list(range(ntile - 1, -1, -1) if rev else range(ntile))

                def prep(ti, AR, b_AR, KT, b_KT, BT, b_BT, KH, b_KH, BH, b_BH, Vp, b_Vp, GL, b_GL):
                    ARb, b_ARb = AR, b_AR
                    tl = ti * NT2
                    c0 = coff + tl
                    tg = toff + tl
                    def ldq(q):
                        dma("sp" if q != 1 else "act", Lr[q][:],
                            Pscr[q * 512:(q + 1) * 512, c0:c0 + NT2 + 2].rearrange("(h p) t -> p h t", p=64),
                            [b_P], [b_L[q]])

                    def shq(q):
                        Lq = Lr[q]; S_, bS = (SH[q] if q < 2 else (Vp, b_Vp))
                        tt("pool", T1[:], Lq[:, :, 0:NT2], Lq[:, :, 2:NT2 + 2], ALU.add, [b_L[q]], [b_T1])
                        tt("pool", T1[:], T1[:], bc(hm3, 8 * q, 8 * q + 8, NT2), ALU.mult, [b_T1, b_hm3], [b_T1])
                        tt("pool", S_[:], Lq[:, :, 1:NT2 + 1], bc(om3, 8 * q, 8 * q + 8, NT2), ALU.mult,
                           [b_L[q], b_hm3], [bS])
                        tt("pool", S_[:], S_[:], T1[:], ALU.add, [bS, b_T1], [bS])
                    ldq(0); ldq(1)
                    dma("sp", XW[:], Pscr[1536 + 64 * d:1600 + 64 * d, c0:c0 + NT2 + 2], [b_P], [b_XW])
                    dma("act", XA[:], Pscr[1664 + 64 * d:1728 + 64 * d, c0:c0 + NT2 + 2], [b_P], [b_XA])
                    shq(0); ldq(2); shq(1); shq(2)
                    for (X_, bX, o_, bo, j) in ((XW, b_XW, xwp, b_xwp, 0), (XA, b_XA, xap, b_xap, 1)):
                        tt("dve", tw[:], X_[:, 0:NT2], X_[:, 2:NT2 + 2], ALU.add, [bX], [b_tw])
                        ts("dve", tw[:], tw[:], hmw[:, j:j + 1], None, ALU.mult, None, [b_tw, b_hmw], [b_tw])
                        stt(o_[:], X_[:, 1:NT2 + 1], omw[:, j:j + 1], tw[:], ALU.mult, ALU.add, [bX, b_hmw, b_tw], [bo])
                    act(xwp[:], xwp[:], AF.Tanh, [b_xwp], [b_xwp])
                    for hh in range(2):
                        pa, bpa = nextps()
                        for j in range(4):
                            h = 4 * hh + j
                            mm(pa[0:64, j * NT2:(j + 1) * NT2], w2d[:, h * 64:(h + 1) * 64], xwp[:], True, True,
                               [b_w2d, b_xwp], [bpa])
                        tt("dve", XB[:, 4 * hh:4 * hh + 4, :], pa[0:64, 0:4 * NT2].rearrange("p (h n) -> p h n", n=NT2),
                           bc(w0d, 4 * hh, 4 * hh + 4, NT2), ALU.add, [bpa, b_w0d], [b_XB])
                    act(XB[:], XB[:], AF.Exp, [b_XB], [b_XB], scale=-1.0)
                    act(XB[:], XB[:], AF.Ln, [b_XB], [b_XB], bias=1.0)
                    act(E2[:], XB[:], AF.Exp, [b_XB], [b_E2], bias=-0.5, scale=-1.0)
                    for hh in range(2):
                        pa, bpa = nextps()
                        for j in range(4):
                            h = 4 * hh + j
                            mm(pa[0:64, j * NT2:(j + 1) * NT2], a2d[:, h * 64:(h + 1) * 64], xap[:], True, True,
                               [b_w2d, b_xap], [bpa])
                        tt("dve", AD[:, 4 * hh:4 * hh + 4, :], pa[0:64, 0:4 * NT2].rearrange("p (h n) -> p h n", n=NT2),
                           bc(a0d, 4 * hh, 4 * hh + 4, NT2), ALU.add, [bpa, b_a0d], [b_AD])
                    act(AD[:], AD[:], AF.Sigmoid, [b_AD], [b_AD])
                    tt("pool", KR[:], Kp[:], bc(kk_, 0, 8, NT2), ALU.mult, [b_Kp, b_kk_], [b_KR])
                    tt("pool", T1[:], KR[:], KR[:], ALU.mult, [b_KR], [b_T1])
                    for hh in range(2):
                        pa, bpa = nextps()
                        for j in range(4):
                            h = 4 * hh + j
                            mm(pa[0:64, j * NT2:(j + 1) * NT2], ones64[:], T1[:, h, :], True, True, [b_c, b_T1], [bpa])
                        ts("dve", SS[:, 4 * hh:4 * hh + 4, :], pa[0:64, 0:4 * NT2].rearrange("p (h n) -> p h n", n=NT2),
                           1e-24, None, ALU.max, None, [bpa], [b_SS])
                    act(SS[:], SS[:], AF.Sqrt, [b_SS], [b_SS])
                    mk.op("dve", lambda E: E.reciprocal(out=SS[:], in_=SS[:]), [b_SS], [b_SS])
                    tt("pool", KR[:], KR[:], SS[:], ALU.mult, [b_KR, b_SS], [b_KR])
                    tt("pool", T2[:], AD[:], bc(ka_, 0, 8, NT2), ALU.mult, [b_AD, b_ka_], [b_T2])
                    tt("pool", T2[:], T2[:], bc(omka, 0, 8, NT2), ALU.add, [b_T2, b_omka], [b_T2])
                    tt("pool", KD[:], T2[:], Kp[:], ALU.mult, [b_T2, b_Kp], [b_KD])
                    tt("dve", AB[:], AD[:], KR[:], ALU.mult, [b_AD, b_KR], [b_AB])
                    tt("pool", T1[:], Rp[:], KD[:], ALU.mult, [b_Rp, b_KD], [b_T1])
                    tt("pool", T1[:], T1[:], bc(rk_, 0, 8, NT2), ALU.mult, [b_T1, b_rk_], [b_T1])
                    for hh in range(2):
                        pa, bpa = nextps()
                        for j in range(4):
                            h = 4 * hh + j
                            mm(pa[0:64, j * NT2:(j + 1) * NT2], ones64[:], T1[:, h, :], True, True, [b_c, b_T1], [bpa])
                        tt("dve", BON[:, 4 * hh:4 * hh + 4, :], pa[0:64, 0:4 * NT2].rearrange("p (h n) -> p h n", n=NT2),
                           Vp[:, 4 * hh:4 * hh + 4, :], ALU.mult, [bpa, b_Vp], [b_BON])
                    dma("sp", BD[d, :, :, tg:tg + NT2], BON[:], [b_BON], [b_BD])
                    E2f = E2[:].rearrange("p h t -> p (h t)"); Gf = G[:].rearrange("p h t -> p (h t)"); MSf = MS[:]
                    if rev:
                        E2f = E2f[:, ::-1]; Gf = Gf[:, ::-1]; MSf = MSf[:, ::-1]
                    mk.op("dve", lambda E, Gf=Gf, MSf=MSf, E2f=E2f: E.tensor_tensor_scan(
                        out=Gf, data0=MSf, data1=E2f, initial=0.0, op0=ALU.mult, op1=ALU.add), [b_E2, b_c], [b_G])
                    tt("pool", D1[:], G[:], E2[:], ALU.subtract, [b_G, b_E2], [b_D1])
                    Gv = G[:].rearrange("p h (c l) -> p (h c) l", l=64)
                    ti_ = 0 if rev else 63
                    totb = Gv[:, :, ti_:ti_ + 1].broadcast_to([64, 16, 64])
                    tt("pool", D2[:].rearrange("p h (c l) -> p (h c) l", l=64), Gv, totb, ALU.subtract, [b_G], [b_D2])
                    act(EP[:], G[:], AF.Exp, [b_G], [b_EP])
                    act(EN[:], G[:], AF.Exp, [b_G], [b_EN], scale=-1.0)
                    act(D1[:], D1[:], AF.Exp, [b_D1], [b_D1], scale=-1.0)
                    act(D2[:], D2[:], AF.Exp, [b_D2], [b_D2])
                    act(GL[:].rearrange("p (a o) -> p a o", o=1), Gv[:, :, ti_:ti_ + 1], AF.Exp, [b_G], [b_GL], scale=-1.0)
                    stt(RR(AR[:, :, :, 0:64]), c4(KR), -1.0, c4(D1), ALU.mult, ALU.mult, [b_KR, b_D1], [b_AR])
                    tt("pool", RR(AR[:, :, :, 64:128]), c4(Rp), c4(EN), ALU.mult, [b_Rp, b_EN], [b_AR])
                    tt("pool", RR(KT[:, 0:8, :]), KD[:], EP[:], ALU.mult, [b_KD, b_EP], [b_KT])
                    tt("dve", RR(BT[:, 0:8, :]), AB[:], EP[:], ALU.mult, [b_AB, b_EP], [b_BT])
                    tt("pool", KH[:], KD[:], D2[:], ALU.mult, [b_KD, b_D2], [b_KH])
                    tt("dve", BH[:], AB[:], D2[:], ALU.mult, [b_AB, b_D2], [b_BH])

                def chunk(ti, pend, AR, b_AR, KT, b_KT, BT, b_BT, KH, b_KH, BH, b_BH, Vp, b_Vp, GL, b_GL):
                    ARb, b_ARb = AR, b_AR
                    tg = toff + ti * NT2

                    def tt(*a):
                        tt0(*a)
                        mk.replay(pend, 2)

                    def cp(*a):
                        cp0(*a)
                        mk.replay(pend, 2)
                    CS = [slice(0, 64), slice(64, 128)]
                    mA8 = maskA[:, None, :].broadcast_to([64, 8, 128])
                    mL8 = maskL[:, None, :].broadcast_to([64, 8, 64])
                    idbc8 = ident[0:64, None, 0:64].broadcast_to([64, 8, 64])
                    C2 = (0, 1)

                    def blk(c):
                        return slice(c * 8, (c + 1) * 8)

                    def p8v(p, lo, n):
                        return p[0:64, lo:lo + 8 * n].rearrange("p (a n) -> p a n", n=n)
                    for c in C2:
                        p1, bp1 = nextps()
                        for h in range(8):
                            mmr(p1[0:128, h * 128:(h + 1) * 128], wd(BT, h, 128, c * 64), ARb[:, h, c, :], [b_BT, b_ARb], [bp1])
                        tt("dve", RR(MT1[:, c]), p8v(p1, 0, 128), mA8, ALU.mult, [bp1, b_c], [b_MT1])
                    for c in C2:
                        p2, bp2 = nextps()
                        for h in range(8):
                            mmr(p2[0:128, h * 128:(h + 1) * 128], wd(KT, h, 128, c * 64), ARb[:, h, c, :], [b_KT, b_ARb], [bp2])
                        tt("dve", RR(MT2[:, c]), p8v(p2, 0, 128), mA8, ALU.mult, [bp2, b_c], [b_MT2])
                    for c in C2:
                        p3, bp3 = nextps()
                        for h in range(8):
                            mmr(p3[0:128, h * 64:(h + 1) * 64], ARb[:, h, c, :], BT[:, h, CS[c]], [b_ARb, b_BT], [bp3])
                        tt("dve", RR(P0[:, blk(c), :]), p8v(p3, 0, 64), mL8, ALU.mult, [bp3, b_c], [b_P0])
                    Z0 = Zt[0]; bZ0 = b_Zt[0]
                    for c in C2:
                        p4, bp4 = nextps()
                        for h in range(8):
                            mk.op("pe", lambda E, p4=p4, o=h * 64, a=AR[:, h, c, 0:64]: E.transpose(
                                out=p4[0:64, o:o + 64], in_=a, identity=id64), [b_AR, b_ident], [bp4])
                            mk.op("pe", lambda E, p4=p4, o=512 + h * 64, a=Vp[:, h, CS[c]]: E.transpose(
                                out=p4[0:64, o:o + 64], in_=a, identity=id64), [b_Vp, b_ident], [bp4])
                        cp0("act", RR(Z0[:, blk(c), 0:64]), p8v(p4, 0, 64), [bp4], [bZ0])
                        cp("act", RR(VT[:, blk(c), :]), p8v(p4, 512, 64), [bp4], [b_VT])
                    for c in C2:
                        p5, bp5 = nextps()
                        for h in range(8):
                            mk.op("pe", lambda E, p5=p5, o=h * 64, a=BH[:, h, CS[c]]: E.transpose(
                                out=p5[0:64, o:o + 64], in_=a, identity=id64), [b_BH, b_ident], [bp5])
                            mk.op("pe", lambda E, p5=p5, o=512 + h * 64, a=KH[:, h, CS[c]]: E.transpose(
                                out=p5[0:64, o:o + 64], in_=a, identity=id64), [b_KH, b_ident], [bp5])
                        cp0("dve", RR(BHt[:, blk(c), :]), p8v(p5, 0, 64), [bp5], [b_BHt])
                        cp("dve", RR(KHt[:, blk(c), :]), p8v(p5, 512, 64), [bp5], [b_KHt])
                    for c in C2:
                        p6, bp6 = nextps()
                        for h in range(8):
                            mmr(p6[0:128, h * 64:(h + 1) * 64], MT2[:, c, h, :], VT[:, c * 8 + h, :], [b_MT2, b_VT], [bp6])
                        cp("act", RR(Z0[:, blk(c), 64:128]), p8v(p6, 0, 64), [bp6], [bZ0])
                    zi = 0
                    Pv = lambda c, h: P0[:, c * 8 + h, :]
                    Pw = lambda c, h: wd(P0, c * 8 + h, 64)
                    PTv = lambda c, h: MT1[:, c, h, 0:64]
                    PTw = lambda c, h: MT1[:, c, h, :]
                    bP = b_P0; bPT = b_MT1
                    for it in range(6):
                        Zc = Zt[zi]; bZc = b_Zt[zi]; Zn = Zt[1 - zi]; bZn = b_Zt[1 - zi]
                        PTc, Pc, PTcw, Pcw, bPTc, bPc = PTv, Pv, PTw, Pw, bPT, bP
                        if it < 5:
                            nx = it % 2
                            for c in C2:
                                p8, bp8 = nextps()
                                for h in range(8):
                                    mmr(p8[0:128, h * 64:(h + 1) * 64], PTcw(c, h), Pc(c, h), [bPTc, bPc], [bp8])
                                    mmr(p8[0:128, 512 + h * 64:512 + (h + 1) * 64], Pcw(c, h), PTc(c, h), [bPTc, bPc], [bp8])
                                cp("act", RR(PP[nx][:, 0:32, :].rearrange("p (k a) n -> p k a n", k=2)[:, :, blk(c), :]),
                                   p8[0:64, 0:1024].rearrange("p (k a n) -> p k a n", k=2, a=8), [bp8], [b_PP[nx]])
                            Pv = lambda c, h, nx=nx: PP[nx][:, c * 8 + h, :]
                            PTv = lambda c, h, nx=nx: PP[nx][:, 16 + c * 8 + h, :]
                            Pw = lambda c, h, nx=nx: wd(PP[nx], c * 8 + h, 64)
                            PTw = lambda c, h, nx=nx: wd(PP[nx], 16 + c * 8 + h, 64)
                            bP = b_PP[nx]; bPT = b_PP[nx]
                        for c in C2:
                            p7, bp7 = nextps()
                            for h in range(8):
                                mmr(p7[0:128, h * 128:(h + 1) * 128], PTcw(c, h), Zc[:, c * 8 + h, :], [bPTc, bZc], [bp7])
                            tt("dve", RR(Zn[:, blk(c), :]), p8v(p7, 0, 128), Zc[:, blk(c), :], ALU.add, [bp7, bZc], [bZn])
                        zi = 1 - zi
                    Zf = Zt[zi]; bZf = b_Zt[zi]
                    for c in C2:
                        p9, bp9 = nextps()
                        for h in range(8):
                            mmr(p9[0:128, h * 64:(h + 1) * 64], Zf[:, c * 8 + h, :], MT1[:, c, h, 64:128], [bZf, b_MT1], [bp9])
                            mmr(p9[0:128, 512 + h * 64:512 + (h + 1) * 64], Zf[:, c * 8 + h, :], BHt[:, c * 8 + h, :], [bZf, b_BHt], [bp9])
                        tt0("dve", RR(QT[:, c]), p8v(p9, 0, 64), AR[:, :, c, 64:128], ALU.add, [bp9, b_AR], [b_QT])
                        GLc = GL[:].rearrange("p (h c) -> p c h", c=2)[:, c, :, None].broadcast_to([64, 8, 64])
                        tt0("pool", DG[:, blk(c), :], idbc8, GLc, ALU.mult, [b_ident, b_GL], [b_DG])
                        tt("dve", RR(MM[:, blk(c), :]), p8v(p9, 512, 64), DG[:, blk(c), :], ALU.add, [bp9, b_DG], [b_MM])
                    for c in (range(1, -1, -1) if rev else range(2)):
                        sti = sti_[0]
                        ST = STt[sti]; bST = b_ST[sti]; STn = STt[1 - sti]; bSTn = b_ST[1 - sti]
                        p11, bp11 = nextps()
                        for h in range(8):
                            o_ = p11[0:128, h * 64:(h + 1) * 64]
                            k_ = c * 8 + h
                            mmr(o_, wd(ST, h, 64), QT[:, c, h, :], [bST, b_QT], [bp11], True, False)
                            mmr(o_, wd(Zf, k_, 128, 64), MT1[:, c, h, 64:128], [bZf, b_MT1], [bp11], False, False)
                            mmr(o_, wd(VT, k_, 64), MT2[:, c, h, 64:128], [b_VT, b_MT2], [bp11], False, True)
                        cp("act", YT[:, :, CS[c]], v3(p11, 64), [bp11], [b_YT])
                        p12, bp12 = nextps()
                        for h in range(8):
                            o_ = p12[0:128, h * 64:(h + 1) * 64]
                            k_ = c * 8 + h
                            mmr(o_, wd(MM, k_, 64), ST[:, h, :], [b_MM, bST], [bp12], True, False)
                            mmr(o_, wd(BHt, k_, 64), Zf[:, k_, 64:128], [b_BHt, bZf], [bp12], False, False)
                            mmr(o_, wd(KHt, k_, 64), VT[:, k_, :], [b_KHt, b_VT], [bp12], False, True)
                        cp("dve", RR(STn[:, 0:8, :]), v3(p12, 64), [bp12], [bSTn])
                        sti_[0] = 1 - sti
                    dma("sp", YD[d, :, :, tg:tg + NT2], YT[:], [b_YT], [b_YD])

                PIPE = os.environ.get('RW_NOPIPE') != '1'
                if PIPE:
                    prep(order[0], *SETS[0])
                for idx, ti in enumerate(order):
                    pend = []
                    if not PIPE:
                        prep(ti, *SETS[idx % 2])
                    elif idx + 1 < len(order):
                        mk.deferred = pend
                        prep(order[idx + 1], *SETS[(idx + 1) % 2])
                        mk.deferred = None
                    if os.environ.get('RW_PIPE_MODE') == 'start':
                        mk.replay(pend, len(pend))
                    chunk(ti, pend, *SETS[idx % 2])
                    mk.replay(pend, len(pend))
            mk.flush(final=True)

    if upto >= 1:
        rwkv_pass(0)
        rwkv_pass(1)

    def s5_pass():
        es, sb, ps = scope("s5_")
        with es:
            TWO_PI = 2.0 * math.pi
            pz = [ps("pz%d" % i, [128, 512]) for i in range(8)]
            b_pz = [Buf() for _ in range(8)]
            pctr = [0]

            def nextps():
                i = pctr[0] % 8
                pctr[0] += 1
                return pz[i], b_pz[i]

            NLV = 10
            identb = sb("identb", [64, 64], BF16)
            dsk = sb("dsk", [16, 32])
            SQr = sb("SQr", [64, NLV, 64]); SQi = sb("SQi", [64, NLV, 64]); SQin = sb("SQin", [64, NLV, 64])
            LTr = sb("LTr", [64, 8, 64, 16], BF16); LTi = sb("LTi", [64, 8, 64, 16], BF16)
            OTr = sb("OTr", [64, 8, 64, 16], BF16); OTn = sb("OTn", [64, 8, 64, 16], BF16)
            CRb = sb("CRb", [64, 64, 16], BF16); CInb = sb("CInb", [64, 64, 16], BF16)
            bt_ = Buf()
            es2, sb2, ps2_ = scope("s5t_")
            ones1 = sb2("ones1", [1, 64]); row = sb2("row", [1, 64])
            mk.op("pool", lambda E: E.memset(ones1[:], 1.0), (), [bt_])
            dma("sp", row[:], log_dt.rearrange("d g -> (d g)").rearrange("(o n) -> o n", o=1), (), [bt_])
            cp("dve", identb[:], ident[0:64, 0:64], [b_ident], [bt_])
            LR = sb2("LR", [64, 64]); LI = sb2("LI", [64, 64])
            dma("sp", LR[:].rearrange("p (d g) -> p d g", d=2), lam_re.rearrange("d g p -> p d g"), (), [bt_], slow=True)
            dma("act", LI[:].rearrange("p (d g) -> p d g", d=2), lam_im.rearrange("d g p -> p d g"), (), [bt_], slow=True)
            BR = sb2("BR", [64, 64, 16]); BI = sb2("BI", [64, 64, 16])
            dma("sp", BR[:].rearrange("p (d g) h -> p d g h", d=2), b_re.rearrange("d g p h -> p d g h"), (), [bt_])
            dma("act", BI[:].rearrange("p (d g) h -> p d g h", d=2), b_im.rearrange("d g p h -> p d g h"), (), [bt_])
            CR = sb2("CR", [64, 64, 16]); CI = sb2("CI", [64, 64, 16])
            cnat = sb2("cnat", [128, 8, 64])
            for (src, dst) in ((c_re, CR), (c_im, CI)):
                dma("sp", cnat[:], src.rearrange("d g h p -> (d g h) p").rearrange("(k q) p -> q k p", q=128), [bt_], [bt_])
                for k in range(8):
                    pq, bq = nextps()
                    mk.op("pe", lambda E, pq=pq, k=k: E.transpose(out=pq[0:64, 0:128], in_=cnat[:, k, :], identity=ident[:]),
                          [bt_, b_ident], [bq])
                    cp("dve", dst[:, k * 8:(k + 1) * 8, :], pq[0:64, 0:128].rearrange("p (g h) -> p g h", h=16), [bq], [bt_])
            dma("sp", dsk[:], d_skip.rearrange("(g h) -> h g", h=16), (), [bt_], slow=True)
            DT = sb2("DT", [64, 64])
            pq, bq = nextps()
            mm(pq[0:64, 0:64], ones1[:], row[:], True, True, [bt_], [bq])
            act(DT[:], pq[0:64, 0:64], AF.Exp, [bq], [bt_])

            def T64(name):
                return sb2(name, [64, 64])
            ZR = T64("ZR"); ZI = T64("ZI"); EPs = T64("EPs"); COS = T64("COS"); SIN = T64("SIN")
            tA = T64("tA"); tB = T64("tB"); tC = T64("tC"); tI = sb2("tI", [64, 64], mybir.dt.int32)
            tt("dve", ZR[:], LR[:], DT[:], ALU.mult, [bt_], [bt_])
            tt("dve", ZI[:], LI[:], DT[:], ALU.mult, [bt_], [bt_])
            act(EPs[:], ZR[:], AF.Exp, [bt_], [bt_])
            for (dst, offs) in ((SIN, 64.0), (COS, 64.25)):
                ts("dve", tA[:], ZI[:], 1.0 / TWO_PI, offs, ALU.mult, ALU.add, [bt_], [bt_])
                cp("dve", tI[:], tA[:], [bt_], [bt_])
                cp("dve", tB[:], tI[:], [bt_], [bt_])
                tt("dve", tA[:], tA[:], tB[:], ALU.subtract, [bt_], [bt_])
                ts("dve", tB[:], tA[:], 0.5, None, ALU.is_gt, None, [bt_], [bt_])
                tt("dve", tA[:], tA[:], tB[:], ALU.subtract, [bt_], [bt_])
                act(dst[:], tA[:], AF.Sin, [bt_], [bt_], scale=TWO_PI)
            PWr = sb2("PWr", [64, 9, 64]); PWi = sb2("PWi", [64, 9, 64])
            mk.op("pool", lambda E: E.memset(PWr[:, 0, :], 1.0), (), [bt_])
            mk.op("pool", lambda E: E.memset(PWi[:, 0, :], 0.0), (), [bt_])
            tt("dve", PWr[:, 1, :], EPs[:], COS[:], ALU.mult, [bt_], [bt_])
            tt("dve", PWi[:, 1, :], EPs[:], SIN[:], ALU.mult, [bt_], [bt_])

            def cmul(or_, oi_, ar, ai, br, bi, n3=None):
                tt("dve", tA[:], ai, bi, ALU.mult, [bt_], [bt_])
                tt("dve", tB[:], ai, br, ALU.mult, [bt_], [bt_])
                tt("dve", tC[:], ar, br, ALU.mult, [bt_], [bt_])
                tt("dve", or_, tC[:], tA[:], ALU.subtract, [bt_], [bt_])
                tt("dve", tC[:], ar, bi, ALU.mult, [bt_], [bt_])
                tt("dve", oi_, tC[:], tB[:], ALU.add, [bt_], [bt_])
            for j in range(2, 9):
                cmul(PWr[:, j, :], PWi[:, j, :], PWr[:, j - 1, :], PWi[:, j - 1, :], PWr[:, 1, :], PWi[:, 1, :])
            NLV = 10
            cp("dve", SQr[:, 0, :], PWr[:, 8, :], [bt_], [bt_]); cp("dve", SQi[:, 0, :], PWi[:, 8, :], [bt_], [bt_])
            for k in range(1, NLV):
                cmul(SQr[:, k, :], SQi[:, k, :], SQr[:, k - 1, :], SQi[:, k - 1, :], SQr[:, k - 1, :], SQi[:, k - 1, :])
            ts("dve", SQin[:], SQi[:], -1.0, None, ALU.mult, None, [bt_], [bt_])
            CFr = T64("CFr"); CFi = T64("CFi"); DEN = T64("DEN"); NR = T64("NR")
            ts("dve", NR[:], PWr[:, 1, :], -1.0, None, ALU.add, None, [bt_], [bt_])
            tt("dve", tA[:], LR[:], LR[:], ALU.mult, [bt_], [bt_])
            tt("dve", tB[:], LI[:], LI[:], ALU.mult, [bt_], [bt_])
            tt("dve", DEN[:], tA[:], tB[:], ALU.add, [bt_], [bt_])
            mk.op("dve", lambda E: E.reciprocal(out=DEN[:], in_=DEN[:]), [bt_], [bt_])
            tt("dve", tA[:], NR[:], LR[:], ALU.mult, [bt_], [bt_])
            tt("dve", tB[:], PWi[:, 1, :], LI[:], ALU.mult, [bt_], [bt_])
            tt("dve", tA[:], tA[:], tB[:], ALU.add, [bt_], [bt_])
            tt("dve", CFr[:], tA[:], DEN[:], ALU.mult, [bt_], [bt_])
            tt("dve", tA[:], PWi[:, 1, :], LR[:], ALU.mult, [bt_], [bt_])
            tt("dve", tB[:], NR[:], LI[:], ALU.mult, [bt_], [bt_])
            tt("dve", tA[:], tA[:], tB[:], ALU.subtract, [bt_], [bt_])
            tt("dve", CFi[:], tA[:], DEN[:], ALU.mult, [bt_], [bt_])
            BbR = sb2("BbR", [64, 64, 16]); BbI = sb2("BbI", [64, 64, 16])
            X1t = sb2("X1t", [64, 64, 16]); X2t = sb2("X2t", [64, 64, 16])

            def b16(t2):
                return t2[:, :, None].broadcast_to([64, 64, 16])

            def cmul3(or_, oi_neg, ar2, ai2, br3, bi3, e1="dve", e2="pool"):
                tt(e1, X1t[:], br3, b16(ar2), ALU.mult, [bt_], [bt_])
                tt(e1, X2t[:], bi3, b16(ai2), ALU.mult, [bt_], [bt_])
                tt(e1, or_, X1t[:], X2t[:], ALU.subtract, [bt_], [bt_])
                tt(e1, X1t[:], bi3, b16(ar2), ALU.mult, [bt_], [bt_])
                tt(e1, X2t[:], br3, b16(ai2), ALU.mult, [bt_], [bt_])
                if oi_neg[1]:
                    tt(e1, X1t[:], X1t[:], X2t[:], ALU.add, [bt_], [bt_])
                    ts(e1, oi_neg[0], X1t[:], -1.0, None, ALU.mult, None, [bt_], [bt_])
                else:
                    tt(e1, oi_neg[0], X1t[:], X2t[:], ALU.add, [bt_], [bt_])
            cmul3(BbR[:], (BbI[:], False), CFr[:], CFi[:], BR[:], BI[:])
            for j in range(8):
                cmul3(LTr[:, j], (LTi[:, j], False), PWr[:, j, :], PWi[:, j, :], BbR[:], BbI[:])
                cmul3(OTr[:, j], (OTn[:, j], True), PWr[:, j + 1, :], PWi[:, j + 1, :], CR[:], CI[:])
            cp("dve", CRb[:], CR[:], [bt_], [bt_]); ts("dve", CInb[:], CI[:], -1.0, None, ALU.mult, None, [bt_], [bt_])

            mk.flush(final=True)
            es2.close()
            KTg = [sb("KTg%d" % i, [16, 15, 16], BF16) for i in range(2)]; b_KTg = [Buf(), Buf()]
            CTg = [sb("CTg%d" % i, [16, 32, 64], BF16) for i in range(2)]; b_CTg = [Buf(), Buf()]
            UGN = 2048
            ug = sb("ug", [16, UGN]); b_ug = Buf()
            ub = [sb("ub%d" % i, [16, 8, 1024], BF16) for i in range(2)]; b_ub = [Buf(), Buf()]
            yg = sb("yg", [16, 4096]); b_yg = Buf()
            NBM = 1024
            Wt = [[[sb("W%d%d%d" % (pp, d, c), [64, NBM + 1]) for c in range(2)] for d in range(2)] for pp in range(2)]
            b_Wt = [[Buf() for d in range(2)] for pp in range(2)]
            Sb = [[[sb("Sb%d%d%d" % (i, d, c), [64, NBM + 1], BF16) for c in range(2)] for d in range(2)] for i in range(2)]
            b_Sb = [Buf(), Buf()]
            for pp in range(2):
                for d in range(2):
                    for c in range(2):
                        mk.op("pool", lambda E, t=Wt[pp][d][c]: E.memset(t[:], 0.0), (), [b_Wt[pp][d]])
            id16 = ident[0:16, 0:16]

            def gconsts(g):
                par = g % 2
                pk, bpk = nextps()
                for idx in range(15):
                    if idx == 0:
                        terms = [(0, 0), (1, 0)]
                    elif idx < 8:
                        terms = [(0, idx)]
                    else:
                        terms = [(1, idx - 7)]
                    n = 0
                    for (d, tau) in terms:
                        gi = d * 32 + g
                        mm(pk[0:16, idx * 16:(idx + 1) * 16], LTr[:, tau, gi, :], CRb[:, gi, :], n == 0, False, [bt_], [bpk]); n += 1
                        mm(pk[0:16, idx * 16:(idx + 1) * 16], LTi[:, tau, gi, :], CInb[:, gi, :], False, n == 2 * len(terms) - 1, [bt_], [bpk]); n += 1
                cp("dve", KTg[par][:].rearrange("p a b -> p (a b)"), pk[0:16, 0:240], [bpk], [b_KTg[par]])
                stt(KTg[par][:, 0, :], id16, dsk[:, g:g + 1], KTg[par][:, 0, :], ALU.mult, ALU.add, [b_KTg[par], bt_, b_ident], [b_KTg[par]])
                for q in range(4):
                    pc_, bpc = nextps()
                    for j in range(8):
                        i = q * 8 + j
                        d = i // 16; s_ = (i // 2) % 8; c = i % 2
                        e_ = (7 - s_) if d == 0 else s_
                        src = (LTr if c == 0 else LTi)[:, e_, d * 32 + g, :]
                        mm(pc_[0:16, j * 64:(j + 1) * 64], src, identb[:], True, True, [bt_], [bpc])
                    cp("act", CTg[par][:, q * 8:(q + 1) * 8, :].rearrange("p a b -> p (a b)"),
                       pc_[0:16, 0:512], [bpc], [b_CTg[par]])

            def dims(s):
                toff, coff, T = SEQ[s]
                nblk = T // 8
                BW = min(512, nblk)
                return toff, coff, T, nblk, BW, 8 * BW, nblk // BW

            def front(g, s, st):
                par = g % 2
                toff, coff, T, nblk, BW, TW, nbt = dims(s)
                nlv = int(math.log2(nblk))
                UG = min(UGN, T)
                for hf in range(T // UG):
                    dma("sp", ug[:, 0:UG], Pscr[1920 + 16 * g:1936 + 16 * g, coff + 1 + hf * UG:coff + 1 + (hf + 1) * UG],
                        [b_P], [b_ug])
                    cp("act", ub[st][:, :, hf * (UG // 8):(hf + 1) * (UG // 8)], ug[:, 0:UG].rearrange("p (b s) -> p s b", s=8),
                       [b_ug], [b_ub[st]])
                if nblk < NBM:
                    for d in range(2):
                        for c in range(2):
                            mk.op("pool", lambda E, t=Wt[0][d][c]: E.memset(t[:], 0.0), (), [b_Wt[0][d]])
                            mk.op("pool", lambda E, t=Wt[1][d][c]: E.memset(t[:], 0.0), (), [b_Wt[1][d]])
                for bt in range(nbt):
                    for d in range(2):
                        for c in range(2):
                            pw_, bpw = nextps()
                            for s_ in range(8):
                                mm(pw_[0:64, 0:BW], CTg[par][:, (d * 8 + s_) * 2 + c, :],
                                   ub[st][:, s_, bt * BW:(bt + 1) * BW],
                                   s_ == 0, s_ == 7, [b_CTg[par], b_ub[st]], [bpw])
                            o0 = bt * BW + (1 if d == 0 else 0)
                            cp("act", Wt[0][d][c][:, o0:o0 + BW], pw_[0:64, 0:BW], [bpw], [b_Wt[0][d]])
                cur = 0
                for k in range(nlv):
                    sh = 1 << k
                    n_ = nblk - sh
                    for d in range(2):
                        gi = d * 32 + g
                        lo = 1 if d == 0 else 0
                        Wc = Wt[cur][d]; Wn = Wt[1 - cur][d]
                        bWc = b_Wt[cur][d]; bWn = b_Wt[1 - cur][d]
                        if d == 0:
                            dst = slice(lo + sh, lo + nblk); srcs = slice(lo, lo + n_); keep = slice(lo, lo + sh)
                        else:
                            dst = slice(lo, lo + n_); srcs = slice(lo + sh, lo + nblk); keep = slice(lo + n_, lo + nblk)
                        ar = SQr[:, k, gi:gi + 1]; ai = SQi[:, k, gi:gi + 1]; ain = SQin[:, k, gi:gi + 1]
                        stt(Wn[0][:, dst], Wc[0][:, srcs], ar, Wc[0][:, dst], ALU.mult, ALU.add, [bWc, bt_], [bWn])
                        stt(Wn[1][:, dst], Wc[1][:, srcs], ar, Wc[1][:, dst], ALU.mult, ALU.add, [bWc, bt_], [bWn])
                        stt(Wn[0][:, dst], Wc[1][:, srcs], ain, Wn[0][:, dst], ALU.mult, ALU.add, [bWc, bWn, bt_], [bWn])
                        stt(Wn[1][:, dst], Wc[0][:, srcs], ai, Wn[1][:, dst], ALU.mult, ALU.add, [bWc, bWn, bt_], [bWn])
                        cp("pool", Wn[0][:, keep], Wc[0][:, keep], [bWc], [bWn])
                        cp("pool", Wn[1][:, keep], Wc[1][:, keep], [bWc], [bWn])
                    cur = 1 - cur
                for d in range(2):
                    for c in range(2):
                        if d == 0:
                            cp("act", Sb[st][d][c][:, 1:nblk + 1], Wt[cur][d][c][:, 1:nblk + 1], [b_Wt[cur][d]], [b_Sb[st]])
                            mk.op("pool", lambda E, t=Sb[st][d][c]: E.memset(t[:, 0:1], 0.0), (), [b_Sb[st]])
                        else:
                            cp("act", Sb[st][d][c][:, 0:nblk], Wt[cur][d][c][:, 0:nblk], [b_Wt[cur][d]], [b_Sb[st]])
                            mk.op("pool", lambda E, t=Sb[st][d][c], nblk=nblk: E.memset(t[:, nblk:nblk + 1], 0.0), (), [b_Sb[st]])

            def back(g, s, st):
                par = g % 2
                toff, coff, T, nblk, BW, TW, nbt = dims(s)
                for bt in range(nbt):
                    ubv = ub[st][:, :, bt * BW:(bt + 1) * BW]
                    ygv = yg[:, 0:TW].rearrange("p (b s) -> p s b", s=8)
                    for t in range(8):
                        py, bpy = nextps()
                        for s_ in range(8):
                            idx = 0 if s_ == t else ((t - s_) if s_ < t else (7 + s_ - t))
                            mm(py[0:16, 0:BW], KTg[par][:, idx, :], ubv[:, s_, :], s_ == 0, False, [b_KTg[par], b_ub[st]], [bpy])
                        b0 = bt * BW
                        mm(py[0:16, 0:BW], OTr[:, t, g, :], Sb[st][0][0][:, b0:b0 + BW], False, False, [bt_, b_Sb[st]], [bpy])
                        mm(py[0:16, 0:BW], OTn[:, t, g, :], Sb[st][0][1][:, b0:b0 + BW], False, False, [bt_, b_Sb[st]], [bpy])
                        mm(py[0:16, 0:BW], OTr[:, 7 - t, 32 + g, :], Sb[st][1][0][:, b0 + 1:b0 + 1 + BW], False, False, [bt_, b_Sb[st]], [bpy])
                        mm(py[0:16, 0:BW], OTn[:, 7 - t, 32 + g, :], Sb[st][1][1][:, b0 + 1:b0 + 1 + BW], False, True, [bt_, b_Sb[st]], [bpy])
                        cp("act", ygv[:, t, :], py[0:16, 0:BW], [bpy], [b_yg])
                    dma("sp", YS[16 * g:16 * g + 16, toff + bt * TW:toff + (bt + 1) * TW], yg[:, 0:TW], [b_yg], [b_YS])

            units = [(g, s) for g in range(32) for s in range(NS)]
            prev = None
            for ui, (g, s) in enumerate(units):
                if s == 0:
                    gconsts(g)
                front(g, s, ui % 2)
                if prev is not None:
                    back(prev[0], prev[1], (ui - 1) % 2)
                prev = (g, s)
            back(prev[0], prev[1], (len(units) - 1) % 2)
            mk.flush(final=True)

    if upto >= 2:
        s5_pass()

    def mix_pass():
        es, sb, ps = scope("mx_")
        with es:
            N = 256
            pz = [ps("pz%d" % i, [128, 512]) for i in range(8)]
            b_pz = [Buf() for _ in range(8)]
            pctr = [0]

            def nextps():
                i = pctr[0] % 8
                pctr[0] += 1
                return pz[i], b_pz[i]
            bc_ = Buf()
            o64 = sb("o64", [64, 64])
            mk.op("pool", lambda E: E.memset(o64[:], 1.0 / 64.0), (), [bc_])
            ones_bf = sb("ones_bf", [128, 128], BF16)
            mk.op("pool", lambda E: E.memset(ones_bf[:], 1.0), (), [bc_])
            lg = sb("lg", [64, 8]); lb = sb("lb", [64, 8])
            dma("sp", lg[:], lnx_g.rearrange("(h p) -> p h", p=64), (), [bc_], slow=True)
            dma("sp", lb[:], lnx_b.rearrange("(h p) -> p h", p=64), (), [bc_], slow=True)
            g2t = sb("g2t", [128, 512]); dma("sp", g2t[:], g2, (), [bc_])
            mug = sb("mug", [128, 1]); hmg = sb("hmg", [128, 1]); omg = sb("omg", [128, 1])
            dma("sp", mug[:], mu_shift[1792:1920].rearrange("(p o) -> p o", o=1), (), [bc_], slow=True)
            ts("dve", hmg[:], mug[:], 0.5, None, ALU.mult, None, [bc_], [bc_])
            ts("dve", omg[:], mug[:], -1.0, 1.0, ALU.mult, ALU.add, [bc_], [bc_])
            bgl = sb("bgl", [128, 4]); s5g = sb("s5g", [128, 4])
            dma("sp", bgl[:], b_glu.rearrange("(q p) -> p q", p=128), (), [bc_], slow=True)
            dma("sp", s5g[:], s5_out_g.rearrange("(q p) -> p q", p=128), (), [bc_], slow=True)
            wst = sb("wst", [128, 4, 128])
            mk.op("pool", lambda E: E.memset(wst[:], 0.0), (), [bc_])
            for g in range(32):
                r0 = (g % 8) * 16
                dma("sp" if g % 2 == 0 else "act", wst[r0:r0 + 16, g // 8, r0:r0 + 16], w_glu[g], [bc_], [bc_])
            Wbd = sb("Wbd", [128, 4, 128], BF16)
            cp("dve", Wbd[:], wst[:], [bc_], [bc_])
            wo_r = sb("wo_r", [64, 8, D], BF16); wo_s = sb("wo_s", [128, 4, D], BF16)
            stg = [sb("stg%d" % i, [128, D]) for i in range(2)]; b_stg = [Buf(), Buf()]
            for h in range(8):
                st_ = stg[h % 2]; bs_ = b_stg[h % 2]
                dma("sp", st_[0:64, :], w_out[h * 64:(h + 1) * 64, :], (), [bs_])
                cp("pool", wo_r[:, h, :], st_[0:64, :], [bs_], [bc_])
            for q in range(4):
                st_ = stg[q % 2]; bs_ = b_stg[q % 2]
                dma("sp", st_[:], w_out[512 + q * 128:512 + (q + 1) * 128, :], (), [bs_])
                cp("pool", wo_s[:, q, :], st_[:], [bs_], [bc_])

            def T4(name, dt=F32):
                return sb(name, [64, 8, N], dt), Buf()
            YF, b_YF = T4("YF"); YB, b_YB = T4("YB"); BF_, b_BF = T4("BF"); BB, b_BB = T4("BB")
            Ym, b_Ym = T4("Ym"); SQt, b_SQt = T4("SQt"); RS, b_RS = T4("RS")
            yr, b_yr = T4("yr", BF16)
            XG = sb("XG", [128, N + 2]); b_XG = Buf(); tw = sb("tw", [128, N]); b_tw = Buf()
            sg = sb("sg", [128, N]); b_sg = Buf()
            S5 = sb("S5", [128, 4, N]); b_S5 = Buf(); Z1 = sb("Z1", [128, 4, N]); b_Z1 = Buf()
            Z2 = sb("Z2", [128, 4, N]); b_Z2 = Buf(); Zb = sb("Zb", [128, 4, N], BF16); b_Zb = Buf()
            r2 = sb("r2", [128, N]); b_r2 = Buf()
            ysb = sb("ysb", [128, 4, N], BF16); b_ysb = Buf()
            xTt = sb("xTt", [128, 8, N]); b_xTt = Buf(); X1t = sb("X1t", [128, 8, N]); b_X1t = Buf()

            def bc(t, n):
                return t[:, :, None].broadcast_to([64, 8, n])

            def fl(t):
                return t[:].rearrange("p h t -> p (h t)")
            for s, (toff, coff, T) in enumerate(SEQ):
                for ti in range(T // N):
                    tl = ti * N; tg = toff + tl; c0 = coff + tl
                    dma("sp", YF[:], YD[0, :, :, tg:tg + N], [b_YD], [b_YF])
                    dma("act", YB[:], YD[1, :, :, tg:tg + N], [b_YD], [b_YB])
                    dma("sp", BF_[:], BD[0, :, :, tg:tg + N], [b_BD], [b_BF])
                    dma("act", BB[:], BD[1, :, :, tg:tg + N], [b_BD], [b_BB])
                    dma("sp", XG[:], Pscr[1792:1920, c0:c0 + N + 2], [b_P], [b_XG])
                    dma("act", S5[:], YS[:, tg:tg + N].rearrange("(q p) t -> p q t", p=128), [b_YS], [b_S5])
                    dma("sp", xTt[:], XT[:, tg:tg + N].rearrange("(k p) t -> p k t", p=128), [b_XT], [b_xTt])
                    tt("pool", Ym[:], YF[:], YB[:], ALU.add, [b_YF, b_YB], [b_Ym])
                    for j in range(4):
                        pm_, bpm = nextps()
                        mm(pm_[0:64, :], o64[:], fl(Ym)[:, j * 512:(j + 1) * 512], True, True, [bc_, b_Ym], [bpm])
                        tt("dve", fl(YF)[:, j * 512:(j + 1) * 512], fl(Ym)[:, j * 512:(j + 1) * 512], pm_[0:64, :],
                           ALU.subtract, [bpm, b_Ym], [b_YF])
                    tt("pool", SQt[:], YF[:], YF[:], ALU.mult, [b_YF], [b_SQt])
                    for j in range(4):
                        pm_, bpm = nextps()
                        mm(pm_[0:64, :], o64[:], fl(SQt)[:, j * 512:(j + 1) * 512], True, True, [bc_, b_SQt], [bpm])
                        act(fl(RS)[:, j * 512:(j + 1) * 512], pm_[0:64, :], AF.Sqrt, [bpm], [b_RS], bias=LNX_EPS)
                    mk.op("dve", lambda E: E.reciprocal(out=RS[:], in_=RS[:]), [b_RS], [b_RS])
                    tt("pool", Ym[:], YF[:], RS[:], ALU.mult, [b_YF, b_RS], [b_Ym])
                    tt("pool", Ym[:], Ym[:], bc(lg, N), ALU.mult, [b_Ym, bc_], [b_Ym])
                    tt("pool", Ym[:], Ym[:], bc(lb, N), ALU.add, [b_Ym, bc_], [b_Ym])
                    tt("pool", Ym[:], Ym[:], BF_[:], ALU.add, [b_Ym, b_BF], [b_Ym])
                    tt("pool", Ym[:], Ym[:], BB[:], ALU.add, [b_Ym, b_BB], [b_Ym])
                    tt("dve", tw[:], XG[:, 0:N], XG[:, 2:N + 2], ALU.add, [b_XG], [b_tw])
                    ts("dve", tw[:], tw[:], hmg[:, 0:1], None, ALU.mult, None, [b_tw, bc_], [b_tw])
                    stt(sg[:], XG[:, 1:N + 1], omg[:, 0:1], tw[:], ALU.mult, ALU.add, [b_XG, b_tw, bc_], [b_sg])
                    act(sg[:], sg[:], AF.Sigmoid, [b_sg], [b_sg])
                    for h2_ in range(4):
                        pm_, bpm = nextps()
                        for j in range(2):
                            h = 2 * h2_ + j
                            mm(pm_[0:64, j * N:(j + 1) * N], g2t[:, h * 64:(h + 1) * 64], sg[:], True, True, [bc_, b_sg], [bpm])
                        tt("dve", yr[:, 2 * h2_:2 * h2_ + 2, :], pm_[0:64, :].rearrange("p (h n) -> p h n", n=N),
                           Ym[:, 2 * h2_:2 * h2_ + 2, :], ALU.mult, [bpm, b_Ym], [b_yr])
                    K0 = 2.0 * math.sqrt(2.0 / math.pi)
                    tt("pool", Z1[:], S5[:], S5[:], ALU.mult, [b_S5], [b_Z1])
                    ts("dve", Z1[:], Z1[:], 0.044715, 1.0, ALU.mult, ALU.add, [b_Z1], [b_Z1])
                    tt("pool", Z1[:], Z1[:], S5[:], ALU.mult, [b_Z1, b_S5], [b_Z1])
                    act(Z1[:], Z1[:], AF.Sigmoid, [b_Z1], [b_Z1], scale=K0)
                    tt("pool", Z1[:], Z1[:], S5[:], ALU.mult, [b_Z1, b_S5], [b_Z1])
                    cp("dve", Zb[:], Z1[:], [b_Z1], [b_Zb])
                    for q in range(4):
                        pm_, bpm = nextps()
                        mm(pm_[:, 0:N], Wbd[:, q, :], Zb[:, q, :], True, True, [bc_, b_Zb], [bpm])
                        mk.op("act", lambda E, pm_=pm_, q=q: E.activation(out=Z2[:, q, :], in_=pm_[:, 0:N], func=AF.Sigmoid,
                                                                        bias=bgl[:, q:q + 1], scale=1.0), [bpm, bc_], [b_Z2])
                    tt("pool", Z2[:], Z2[:], Z1[:], ALU.mult, [b_Z2, b_Z1], [b_Z2])
                    tt("pool", Zb[:], Z2[:], Z2[:], ALU.mult, [b_Z2], [b_Zb])
                    pm_, bpm = nextps()
                    for q in range(4):
                        mm(pm_[:, 0:N], ones_bf[:], Zb[:, q, :], q == 0, q == 3, [bc_, b_Zb], [bpm])
                    act(r2[:], pm_[:, 0:N], AF.Sqrt, [bpm], [b_r2], bias=RMS_EPS, scale=1.0 / 512.0)
                    mk.op("dve", lambda E: E.reciprocal(out=r2[:], in_=r2[:]), [b_r2], [b_r2])
                    for q in range(4):
                        stt(ysb[:, q, :], Z2[:, q, :], s5g[:, q:q + 1], r2[:], ALU.mult, ALU.mult, [b_Z2, b_r2, bc_], [b_ysb])
                    for dm in range(8):
                        pm_, bpm = nextps()
                        for h in range(8):
                            mm(pm_[:, 0:N], wo_r[:, h, dm * 128:(dm + 1) * 128], yr[:, h, :], h == 0, False, [bc_, b_yr], [bpm])
                        for q in range(4):
                            mm(pm_[:, 0:N], wo_s[:, q, dm * 128:(dm + 1) * 128], ysb[:, q, :], False, q == 3, [bc_, b_ysb], [bpm])
                        stt(X1t[:, dm, :], pm_[:, 0:N], modT[:, 16 + dm, s:s + 1], xTt[:, dm, :], ALU.mult, ALU.add,
                            [bpm, b_mod, b_xTt], [b_X1t])
                    dma("sp", X1[:, tg:tg + N].rearrange("(k p) t -> p k t", p=128), X1t[:], [b_X1t], [b_X1])
            mk.flush(final=True)

    if upto >= 3:
        mix_pass()

    def ffn_pass():
        es, sb, ps = scope("ff_")
        with es:
            N = 256
            pz = [ps("pz%d" % i, [128, 512]) for i in range(8)]
            b_pz = [Buf() for _ in range(8)]
            pctr = [0]

            def nextps():
                i = pctr[0] % 8
                pctr[0] += 1
                return pz[i], b_pz[i]
            bc_ = Buf()
            ones_bf = sb("ones_bf", [128, 128], BF16)
            mk.op("pool", lambda E: E.memset(ones_bf[:], 1.0), (), [bc_])
            n2g = sb("n2g", [128, 8]); fg = sb("fg", [128, 8]); sc2 = sb("sc2", [128, 8, NS])
            dma("sp", n2g[:], norm2_g.rearrange("(k p) -> p k", p=128), (), [bc_], slow=True)
            dma("sp", fg[:], final_g.rearrange("(k p) -> p k", p=128), (), [bc_], slow=True)
            for s in range(NS):
                stt(sc2[:, :, s], modT[:, 32:40, s], 1.0, n2g[:], ALU.add, ALU.mult, [b_mod, bc_], [bc_])
            w1 = sb("w1", [128, 8, DFF], BF16); w3 = sb("w3", [128, 8, DFF], BF16); w2_ = sb("w2_", [128, 22, D], BF16)
            stg = [sb("stg%d" % i, [128, 1408]) for i in range(2)]; b_stg = [Buf(), Buf()]
            n = 0
            for (src, dst) in ((w_ff1, w1), (w_ff3, w3)):
                for k in range(8):
                    for hf in range(2):
                        st_ = stg[n % 2]; bs_ = b_stg[n % 2]; n += 1
                        dma("sp" if n % 2 else "act", st_[:], src[k * 128:(k + 1) * 128, hf * 1408:(hf + 1) * 1408], (), [bs_])
                        cp("pool" if n % 2 else "dve", dst[:, k, hf * 1408:(hf + 1) * 1408], st_[:], [bs_], [bc_])
            for k in range(22):
                st_ = stg[n % 2]; bs_ = b_stg[n % 2]; n += 1
                dma("sp" if n % 2 else "act", st_[:, 0:D], w_ff2[k * 128:(k + 1) * 128, :], (), [bs_])
                cp("pool" if n % 2 else "dve", w2_[:, k, :], st_[:, 0:D], [bs_], [bc_])
            X1t = sb("X1t", [128, 8, N]); b_X1t = Buf()
            sq = sb("sq", [128, 8, N], BF16); b_sq = Buf()
            rstd = sb("rstd", [128, N]); b_rstd = Buf(); tmp = sb("tmp", [128, N]); b_tmp = Buf()
            h2 = sb("h2", [128, 8, N], BF16); b_h2 = Buf()
            fm = sb("fm", [128, 22, N], BF16); b_fm = Buf()
            av = [sb("av%d" % i, [128, N]) for i in range(2)]; b_av = [Buf(), Buf()]
            X2t = sb("X2t", [128, 8, N]); b_X2t = Buf()
            ytm = sb("ytm", [128, 2, D]); b_ytm = Buf()

            def rms_bc(src, bsrc):
                for dc in range(8):
                    act(sq[:, dc, :], src[:, dc, :], AF.Square, [bsrc], [b_sq])
                pm_, bpm = nextps()
                for dc in range(8):
                    mm(pm_[:, 0:N], ones_bf[:], sq[:, dc, :], dc == 0, dc == 7, [bc_, b_sq], [bpm])
                act(rstd[:], pm_[:, 0:N], AF.Sqrt, [bpm], [b_rstd], bias=RMS_EPS, scale=1.0 / D)
                mk.op("dve", lambda E: E.reciprocal(out=rstd[:], in_=rstd[:]), [b_rstd], [b_rstd])
            for s, (toff, coff, T) in enumerate(SEQ):
                for ti in range(T // N):
                    tg = toff + ti * N
                    dma("sp", X1t[:], X1[:, tg:tg + N].rearrange("(k p) t -> p k t", p=128), [b_X1], [b_X1t])
                    rms_bc(X1t, b_X1t)
                    for dc in range(8):
                        tt("dve", tmp[:], X1t[:, dc, :], rstd[:], ALU.mult, [b_X1t, b_rstd], [b_tmp])
                        ts("dve", h2[:, dc, :], tmp[:], sc2[:, dc, s:s + 1], modT[:, 24 + dc, s:s + 1], ALU.mult, ALU.add,
                           [b_tmp, bc_, b_mod], [b_h2])
                    for mc in range(22):
                        p1, bp1 = nextps()
                        for dc in range(8):
                            mm(p1[:, 0:N], w1[:, dc, mc * 128:(mc + 1) * 128], h2[:, dc, :], dc == 0, dc == 7, [bc_, b_h2], [bp1])
                        p3, bp3 = nextps()
                        for dc in range(8):
                            mm(p3[:, 0:N], w3[:, dc, mc * 128:(mc + 1) * 128], h2[:, dc, :], dc == 0, dc == 7, [bc_, b_h2], [bp3])
                        a_ = av[mc % 2]; ba_ = b_av[mc % 2]
                        act(a_[:], p1[:, 0:N], AF.Silu, [bp1], [ba_])
                        tt("dve", fm[:, mc, :], a_[:], p3[:, 0:N], ALU.mult, [ba_, bp3], [b_fm])
                    for dm in range(8):
                        pm_, bpm = nextps()
                        for mc in range(22):
                            mm(pm_[:, 0:N], w2_[:, mc, dm * 128:(dm + 1) * 128], fm[:, mc, :], mc == 0, mc == 21, [bc_, b_fm], [bpm])
                        stt(X2t[:, dm, :], pm_[:, 0:N], modT[:, 40 + dm, s:s + 1], X1t[:, dm, :], ALU.mult, ALU.add,
                            [bpm, b_mod, b_X1t], [b_X2t])
                    rms_bc(X2t, b_X2t)
                    for dc in range(8):
                        stt(X2t[:, dc, :], X2t[:, dc, :], fg[:, dc:dc + 1], rstd[:], ALU.mult, ALU.mult,
                            [b_X2t, bc_, b_rstd], [b_X2t])
                    for j in range(N // 128):
                        for dq in range(2):
                            pm_, bpm = nextps()
                            for k in range(4):
                                dc = dq * 4 + k
                                mk.op("pe", lambda E, pm_=pm_, dc=dc, j=j, k=k: E.transpose(
                                    out=pm_[:, k * 128:(k + 1) * 128], in_=X2t[:, dc, j * 128:(j + 1) * 128], identity=ident[:]),
                                    [b_X2t, b_ident], [bpm])
                            cp("act" if dq == 0 else "dve", ytm[:, j, dq * 512:(dq + 1) * 512], pm_[:], [bpm], [b_ytm])
                    dma("sp", y_out[tg:tg + N, :].rearrange("(j p) d -> p j d", p=128), ytm[:], [b_ytm], [])
            mk.flush(final=True)

    if upto >= 4:
        ffn_pass()

    outer.close()
    return nc


def core_inputs(P, x, c):
    f = np.float32
    m = {"x": x, "c": c}
    for k in ("norm1_g", "w_ada", "b_ada", "w_in", "mu_shift", "w0", "w2", "a0", "a2", "g2", "k_k", "k_a",
              "lnx_g", "lnx_b", "lam_re", "lam_im", "log_dt", "b_re", "b_im", "c_re", "c_im", "w_glu",
              "s5_out_g", "w_out", "norm2_g", "w_ff1", "w_ff3", "w_ff2", "final_g"):
        m[k] = P[k]
    m["r_k"] = P["r_k"].reshape(512)
    m["d_skip"] = P["d_skip"].reshape(512)
    m["b_glu"] = P["b_glu"].reshape(512)
    return {k: np.ascontiguousarray(v, dtype=f) for k, v in m.items()}


_T_PROMPT = 8192
_T_SAMPLE = 4096


def kernel(**inputs):
    n = 8
    P = {}
    for k, v in inputs.items():
        if k in ("x_prompt", "x_sample", "c_prompt", "c_sample"):
            continue
        v = np.asarray(v)
        P[k] = v if k == "final_g" else v[0]
    xp = np.asarray(inputs["x_prompt"]); xs = np.asarray(inputs["x_sample"])
    cpr = np.asarray(inputs["c_prompt"]); cs = np.asarray(inputs["c_sample"])
    TS = [xp.shape[1], xs.shape[1]]
    nc = build_program(TS, upto=int(os.environ.get('KUPTO', '99')))
    in_maps = []
    for b in range(n):
        x = np.concatenate([xp[b], xs[b]], axis=0)
        c = np.stack([cpr[b], cs[b]], axis=0)
        in_maps.append(core_inputs(P, x, c))
    res = run_bass_kernel_spmd(nc, in_maps, core_ids=list(range(n)))
    yp = np.stack([res.results[b]["y"][:TS[0]] for b in range(n)], axis=0).astype(np.float32)
    ys = np.stack([res.results[b]["y"][TS[0]:] for b in range(n)], axis=0).astype(np.float32)
    return (yp, ys)
```

```python
import os
import math
import numpy as np
import concourse.bass as bass
import concourse.mybir as mybir
from concourse.bass_utils import run_bass_kernel_spmd

F32 = mybir.dt.float32
BF16 = mybir.dt.bfloat16
AF = mybir.ActivationFunctionType
ALU = mybir.AluOpType
AX = mybir.AxisListType

D = 1024
DFF = 2816
NPROJ = 2432
RW = 1920
NDS = 12
RMS_EPS = 1e-6
LNX_EPS = 64e-5


class Buf:
    __slots__ = ("w", "r")

    def __init__(self):
        self.w = None
        self.r = {}


class MK:
    BLK = {"pe": "tensor", "dve": "vector", "act": "scalar", "pool": "gpsimd", "sp": "sync"}

    def __init__(self, nc, same=True):
        self.nc = nc
        self.same = same
        self.names = ["pe", "dve", "act", "pool", "sp"]
        self.sem = {k: nc.alloc_semaphore(name="s_" + k) for k in self.names}
        self.cnt = {k: 0 for k in self.names}
        self.seen = {k: {} for k in self.names}
        self.prog = {k: [] for k in self.names}
        self.dsem = [nc.alloc_semaphore(name="d%d" % i) for i in range(NDS)]
        self.dcnt = [0] * NDS
        self.dnext = 0
        self.deferred = None

    def semof(self, key):
        if isinstance(key, tuple):
            return self.dsem[key[1]]
        return self.sem[key]

    def _deps(self, e, reads, writes):
        deps = {}

        def add(k, v):
            if deps.get(k, 0) < v:
                deps[k] = v

        for b in reads:
            if b.w:
                add(*b.w)
        for b in writes:
            if b.w:
                add(*b.w)
            for k, v in b.r.items():
                add(k, v)
        out = []
        for k, v in deps.items():
            if k == e and (e == "pe" or not self.same):
                continue
            if self.seen[e].get(k, 0) >= v:
                continue
            self.seen[e][k] = v
            out.append((k, v))
        return out

    def _mark(self, tok, reads, writes):
        k, v = tok
        for b in reads:
            if b.r.get(k, 0) < v:
                b.r[k] = v
        for b in writes:
            b.w = tok
            b.r = {}

    def op(self, e, fn, reads=(), writes=()):
        if self.deferred is not None:
            self.deferred.append((0, e, fn, reads, writes))
            return
        waits = self._deps(e, reads, writes)
        self.cnt[e] += 1
        tok = (e, self.cnt[e])
        self.prog[e].append((waits, fn, self.sem[e], 1))
        self._mark(tok, reads, writes)

    def replay(self, pending, n):
        keep = self.deferred
        self.deferred = None
        last = None
        cnt = 0
        while pending and (cnt < n or last == "pe"):
            kind, e, fn, reads, writes = pending.pop(0)
            (self.dma if kind else self.op)(e, fn, reads, writes)
            last = e if not kind else None
            cnt += 1
        self.deferred = keep

    def dma(self, q, fn, reads=(), writes=()):
        if self.deferred is not None:
            self.deferred.append((1, q, fn, reads, writes))
            return
        i = self.dnext
        self.dnext = (i + 1) % NDS
        key = ("d", i)
        waits = self._deps(q, reads, writes)
        if self.dcnt[i] > 0 and self.seen[q].get(key, 0) < self.dcnt[i]:
            waits.append((key, self.dcnt[i]))
            self.seen[q][key] = self.dcnt[i]
        self.dcnt[i] += 16
        tok = (key, self.dcnt[i])
        self.prog[q].append((waits, fn, self.dsem[i], 16))
        self._mark(tok, reads, writes)

    def flush(self, final=False):
        nc = self.nc
        fin = []
        for i in (range(NDS) if final else []):
            if self.dcnt[i] > 0:
                fin.append((("d", i), self.dcnt[i]))
        for k in (self.names if final else []):
            if k != "sp" and self.cnt[k] > 0:
                fin.append((k, self.cnt[k]))
        with nc.Block() as block:
            for e in self.names:
                prog = self.prog[e]
                extra = fin if e == "sp" else []

                def body(eng, prog=prog, extra=extra):
                    for waits, fn, sem, inc in prog:
                        for k, v in waits:
                            eng.wait_ge(self.semof(k), v)
                        fn(eng).then_inc(sem, inc)
                    for k, v in extra:
                        eng.wait_ge(self.semof(k), v)

                getattr(block, self.BLK[e])(body)
        self.prog = {k: [] for k in self.names}

    def emit(self):
        self.flush(final=True)


def build_program(TS, dbg=False, upto=99):
    import contextlib
    nc = bass.Bass("TRN2", target_bir_lowering=False)
    mk = MK(nc, same=(os.environ.get("MK_SAME", "1") == "1"))
    TT = sum(TS)
    NS = len(TS)
    WP = TT + 2 * NS
    SEQ = []
    o = 0
    for s, T in enumerate(TS):
        SEQ.append((o, o + 2 * s, T))
        o += T

    def din(name, shape):
        return nc.dram_tensor(name, list(shape), F32, kind="ExternalInput").ap()

    def dscr(name, shape):
        return nc.dram_tensor(name, list(shape), F32, kind=("ExternalOutput" if dbg else "Internal")).ap()

    x_in = din("x", (TT, D))
    c_in = din("c", (NS, D))
    norm1_g = din("norm1_g", (D,))
    w_ada = din("w_ada", (D, 6 * D))
    b_ada = din("b_ada", (6 * D,))
    w_in = din("w_in", (D, NPROJ))
    mu_shift = din("mu_shift", (RW,))
    w0 = din("w0", (2, 512)); w2 = din("w2", (2, 64, 512))
    a0 = din("a0", (2, 512)); a2 = din("a2", (2, 64, 512))
    g2 = din("g2", (128, 512))
    k_k = din("k_k", (512,)); k_a = din("k_a", (512,)); r_k = din("r_k", (512,))
    lnx_g = din("lnx_g", (512,)); lnx_b = din("lnx_b", (512,))
    lam_re = din("lam_re", (2, 32, 64)); lam_im = din("lam_im", (2, 32, 64)); log_dt = din("log_dt", (2, 32))
    b_re = din("b_re", (2, 32, 64, 16)); b_im = din("b_im", (2, 32, 64, 16))
    c_re = din("c_re", (2, 32, 16, 64)); c_im = din("c_im", (2, 32, 16, 64))
    d_skip = din("d_skip", (512,)); w_glu = din("w_glu", (32, 16, 16)); b_glu = din("b_glu", (512,))
    s5_out_g = din("s5_out_g", (512,))
    w_out = din("w_out", (D, D)); norm2_g = din("norm2_g", (D,))
    w_ff1 = din("w_ff1", (D, DFF)); w_ff3 = din("w_ff3", (D, DFF)); w_ff2 = din("w_ff2", (DFF, D))
    final_g = din("final_g", (D,))
    y_out = nc.dram_tensor("y", [TT, D], F32, kind="ExternalOutput").ap()

    Pscr = dscr("Pscr", (NPROJ, WP))
    XT = dscr("XT", (D, TT))
    YD = dscr("YD", (2, 64, 8, TT))
    BD = dscr("BD", (2, 64, 8, TT))
    YS = dscr("YS", (512, TT))
    X1 = dscr("X1", (D, TT))
    MODS = dscr("MODS", (128, 48 * NS))
    b_P = Buf(); b_XT = Buf(); b_YD = Buf(); b_BD = Buf(); b_YS = Buf(); b_X1 = Buf(); b_MODS = Buf()

    def tt(e, out, a, b, op, r, w):
        mk.op(e, lambda E: E.tensor_tensor(out=out, in0=a, in1=b, op=op), r, w)

    def ts(e, out, a, s1, s2, op0, op1, r, w):
        if op1 is None:
            mk.op(e, lambda E: E.tensor_scalar(out=out, in0=a, scalar1=s1, scalar2=None, op0=op0), r, w)
        else:
            mk.op(e, lambda E: E.tensor_scalar(out=out, in0=a, scalar1=s1, scalar2=s2, op0=op0, op1=op1), r, w)

    def stt(out, a, sc, b, op0, op1, r, w):
        mk.op("dve", lambda E: E.scalar_tensor_tensor(out=out, in0=a, scalar=sc, in1=b, op0=op0, op1=op1), r, w)

    def act(out, a, func, r, w, bias=0.0, scale=1.0):
        mk.op("act", lambda E: E.activation(out=out, in_=a, func=func, bias=bias, scale=scale), r, w)

    def cp(e, out, a, r, w):
        if e == "act":
            mk.op("act", lambda E: E.activation(out=out, in_=a, func=AF.Copy), r, w)
        else:
            mk.op(e, lambda E: E.tensor_copy(out=out, in_=a), r, w)

    def mm(out, lhsT, rhs, st, sp_, r, w):
        mk.op("pe", lambda E: E.matmul(out=out, lhsT=lhsT, rhs=rhs, start=st, stop=sp_), r, w)

    F32R = mybir.dt.float32r
    USE_R = os.environ.get("RW_F32R", "1") == "1"

    def RR(ap):
        return ap.bitcast(F32R) if USE_R else ap

    def mmr(out, lhsT, rhs, r, w, st=True, sp_=True):
        mk.op("pe", lambda E: E.matmul(out=out, lhsT=lhsT.bitcast(F32R), rhs=rhs.bitcast(F32R), start=st, stop=sp_), r, w)

    def dma(q, out, in_, r, w, slow=False):
        if slow:
            mk.dma(q, lambda E: E.dma_start(out=out, in_=in_, allow_slow_non_contiguous=True), r, w)
        else:
            mk.dma(q, lambda E: E.dma_start(out=out, in_=in_), r, w)

    def scope(pfx=""):
        es = contextlib.ExitStack()

        def sb(name, shape, dt=F32):
            return es.enter_context(nc.sbuf_tensor(pfx + name, list(shape), dt))

        def ps(name, shape, dt=F32):
            return es.enter_context(nc.psum_tensor(pfx + name, list(shape), dt))
        return es, sb, ps

    def consts(sb):
        ident = sb("ident", [128, 128]); b_ident = Buf()
        mk.op("pool", lambda E: E.memset(ident[:], 1.0), (), [b_ident])
        mk.op("pool", lambda E: E.affine_select(out=ident[:], in_=ident[:], pattern=[[-1, 128]],
                                                compare_op=ALU.is_equal, fill=0.0, base=0, channel_multiplier=1),
              [b_ident], [b_ident])
        return ident, b_ident
    outer, osb, ops_ = scope("o_")
    ident, b_ident = consts(osb)
    modT = osb("modT", [128, 48, NS]); b_mod = Buf()
    sc1 = osb("sc1", [128, 8, NS]); b_sc1 = Buf()

    def pass0():
        es, sb, ps = scope("p0_")
        with es:
            ones_bf = sb("ones_bf", [128, 128], BF16); b_ones = Buf()
            mk.op("pool", lambda E: E.memset(ones_bf[:], 1.0), (), [b_ones])
            cT = sb("cT", [128, 8, NS]); b_cT = Buf()
            scT = sb("scT", [128, 8, NS]); b_scT = Buf()
            for s in range(NS):
                dma("sp", cT[:, :, s], c_in[s].rearrange("(k p) -> p k", p=128), (), [b_cT], slow=True)
            act(scT[:], cT[:], AF.Silu, [b_cT], [b_scT])
            badaT = sb("badaT", [128, 48]); b_bada = Buf()
            dma("sp", badaT[:], b_ada.rearrange("(k p) -> p k", p=128), (), [b_bada], slow=True)
            g1T = sb("g1T", [128, 8]); b_g1 = Buf()
            dma("sp", g1T[:], norm1_g.rearrange("(k p) -> p k", p=128), (), [b_g1], slow=True)
            wada_t = [sb("wada%d" % i, [128, 8, 256]) for i in range(2)]
            b_wada = [Buf(), Buf()]
            ps_mod_full = ps("ps_mod", [128, 512]); b_psmod = Buf()
            ps_mod = ps_mod_full[:, 0:4 * NS].rearrange("p (a b) -> p a b", b=NS)
            for slab in range(24):
                wt = wada_t[slab % 2]; bw = b_wada[slab % 2]
                dma("sp" if slab % 2 == 0 else "act", wt[:],
                    w_ada[:, slab * 256:(slab + 1) * 256].rearrange("(k p) n -> p k n", p=128), (), [bw])
                for j in range(2):
                    for k in range(8):
                        mm(ps_mod[:, j, :], wt[:, k, j * 128:(j + 1) * 128], scT[:, k, :], k == 0, k == 7,
                           [bw, b_scT], [b_psmod])
                for s in range(NS):
                    tt("dve", modT[:, slab * 2:(slab + 1) * 2, s], ps_mod[:, 0:2, s],
                       badaT[:, slab * 2:(slab + 1) * 2], ALU.add, [b_psmod, b_bada], [b_mod])
            for s in range(NS):
                stt(sc1[:, :, s], modT[:, 8:16, s], 1.0, g1T[:], ALU.add, ALU.mult, [b_mod, b_g1], [b_sc1])

            w_in_bf = sb("w_in_bf", [128, 8, NPROJ], BF16); b_win = Buf()
            wst = [sb("wst%d" % i, [128, NPROJ]) for i in range(2)]; b_wst = [Buf(), Buf()]
            for k in range(8):
                dma("sp", wst[k % 2][:], w_in[k * 128:(k + 1) * 128, :], (), [b_wst[k % 2]])
                cp("pool", w_in_bf[:, k, :], wst[k % 2][:], [b_wst[k % 2]], [b_win])

            NT = 512
            xtm = [sb("xtm%d" % i, [128, 4, D]) for i in range(2)]; b_xtm = [Buf(), Buf()]
            xT = sb("xT", [128, 8, NT]); b_xT = Buf()
            sq = sb("sq", [128, 8, NT], BF16); b_sq = Buf()
            rstd = sb("rstd", [128, NT]); b_rstd = Buf()
            tmp = sb("tmp0", [128, NT]); b_tmp = Buf()
            hT = sb("hT", [128, 8, NT], BF16); b_hT = Buf()
            pev = [sb("pev%d" % i, [128, NT]) for i in range(3)]; b_pev = [Buf() for _ in range(3)]
            zcol = sb("zcol", [128, 1]); b_zcol = Buf()
            mk.op("pool", lambda E: E.memset(zcol[:], 0.0), (), [b_zcol])
            pst = [ps("pst%d" % i, [128, NT]) for i in range(4)]; b_pst = [Buf() for _ in range(4)]
            psm = [ps("psm%d" % i, [128, NT]) for i in range(3)]; b_psm = [Buf() for _ in range(3)]
            for s, (toff, coff, T) in enumerate(SEQ):
                for mc in range(19):
                    for cc in (coff, coff + T + 1):
                        dma("sp", Pscr[mc * 128:(mc + 1) * 128, cc:cc + 1], zcol[:], [b_zcol], [b_P], slow=True)
                for ti in range(T // NT):
                    t0 = toff + ti * NT
                    xt = xtm[ti % 2]; bx = b_xtm[ti % 2]
                    dma("sp", xt[:], x_in[t0:t0 + NT, :].rearrange("(j p) d -> p j d", p=128), (), [bx])
                    for dc in range(8):
                        pt = pst[dc % 4]; bp = b_pst[dc % 4]
                        for j in range(4):
                            mk.op("pe", lambda E, pt=pt, xt=xt, j=j, dc=dc: E.transpose(
                                out=pt[:, j * 128:(j + 1) * 128], in_=xt[:, j, dc * 128:(dc + 1) * 128],
                                identity=ident[:]), [bx, b_ident], [bp])
                        cp("dve", xT[:, dc, :], pt[:], [bp], [b_xT])
                        act(sq[:, dc, :], pt[:], AF.Square, [bp, b_xT], [b_sq])
                    dma("act", XT[:, t0:t0 + NT].rearrange("(k p) t -> p k t", p=128), xT[:], [b_xT], [b_XT])
                    pm = psm[0]; bpm = b_psm[0]
                    for dc in range(8):
                        mm(pm[:], ones_bf[:], sq[:, dc, :], dc == 0, dc == 7, [b_sq, b_ones], [bpm])
                    act(rstd[:], pm[:], AF.Sqrt, [bpm], [b_rstd], bias=RMS_EPS, scale=1.0 / D)
                    mk.op("dve", lambda E: E.reciprocal(out=rstd[:], in_=rstd[:]), [b_rstd], [b_rstd])
                    for dc in range(8):
                        tt("dve", tmp[:], xT[:, dc, :], rstd[:], ALU.mult, [b_xT, b_rstd], [b_tmp])
                        ts("dve", hT[:, dc, :], tmp[:], sc1[:, dc, s:s + 1], modT[:, dc, s:s + 1], ALU.mult, ALU.add,
                           [b_tmp, b_sc1, b_mod], [b_hT])
                    for mc in range(19):
                        i3 = mc % 3
                        pm = psm[i3]; bpm = b_psm[i3]
                        for dc in range(8):
                            mm(pm[:], w_in_bf[:, dc, mc * 128:(mc + 1) * 128], hT[:, dc, :], dc == 0, dc == 7,
                               [b_win, b_hT], [bpm])
                        pv = pev[i3]; bpv = b_pev[i3]
                        cp("act", pv[:], pm[:], [bpm], [bpv])
                        cc = coff + 1 + ti * NT
                        dma("sp", Pscr[mc * 128:(mc + 1) * 128, cc:cc + NT], pv[:], [bpv], [b_P])
            mk.flush(final=True)

    pass0()
    def rwkv_pass(d):
        rev = (d == 1)
        es, sb, ps = scope("rw%d_" % d)
        with es:
            NT2 = 128
            psr = [ps("psr%d" % i, [128, 1024]) for i in range(4)]
            b_psr = [Buf() for _ in range(4)]
            pctr = [0]

            def nextps():
                i = pctr[0] % 4
                pctr[0] += 1
                return psr[i], b_psr[i]

            def T4(name):
                return sb(name, [64, 8, NT2]), Buf()

            def ldp(name, src512):
                t = sb(name, [64, 8]); b = Buf()
                dma("sp", t[:], src512.rearrange("(h p) -> p h", p=64), (), [b], slow=True)
                return t, b

            mu3 = sb("mu3", [64, 24]); b_mu3 = Buf()
            dma("sp", mu3[:], mu_shift[0:1536].rearrange("(g p) -> p g", p=64), (), [b_mu3], slow=True)
            hm3 = sb("hm3", [64, 24]); om3 = sb("om3", [64, 24]); b_hm3 = Buf()
            ts("dve", hm3[:], mu3[:], 0.5, None, ALU.mult, None, [b_mu3], [b_hm3])
            ts("dve", om3[:], mu3[:], -1.0, 1.0, ALU.mult, ALU.add, [b_mu3], [b_hm3])
            muw = sb("muw", [64, 2]); b_muw = Buf()
            dma("sp", muw[:, 0:1], mu_shift[1536 + 64 * d:1600 + 64 * d].rearrange("(p o) -> p o", o=1), (), [b_muw], slow=True)
            dma("sp", muw[:, 1:2], mu_shift[1664 + 64 * d:1728 + 64 * d].rearrange("(p o) -> p o", o=1), (), [b_muw], slow=True)
            hmw = sb("hmw", [64, 2]); omw = sb("omw", [64, 2]); b_hmw = Buf()
            ts("dve", hmw[:], muw[:], 0.5, None, ALU.mult, None, [b_muw], [b_hmw])
            ts("dve", omw[:], muw[:], -1.0, 1.0, ALU.mult, ALU.add, [b_muw], [b_hmw])
            w0d, b_w0d = ldp("w0d", w0[d]); a0d, b_a0d = ldp("a0d", a0[d])
            kk_, b_kk_ = ldp("kk_", k_k); ka_, b_ka_ = ldp("ka_", k_a); rk_, b_rk_ = ldp("rk_", r_k)
            omka = sb("omka", [64, 8]); b_omka = Buf()
            ts("dve", omka[:], ka_[:], -1.0, 1.0, ALU.mult, ALU.add, [b_ka_], [b_omka])
            w2d = sb("w2d", [64, 512]); a2d = sb("a2d", [64, 512]); b_w2d = Buf()
            dma("sp", w2d[:], w2[d], (), [b_w2d]); dma("sp", a2d[:], a2[d], (), [b_w2d])
            ones64 = sb("ones64", [64, 64]); b_c = Buf()
            mk.op("pool", lambda E: E.memset(ones64[:], 1.0), (), [b_c])
            maskA = sb("maskA", [64, 128]); maskL = sb("maskL", [64, 64]); MS = sb("MS", [64, 8 * NT2])
            mk.op("pool", lambda E: E.memset(maskA[:], 1.0), (), [b_c])
            mk.op("pool", lambda E: E.memset(maskL[:], 1.0), (), [b_c])
            mk.op("pool", lambda E: E.memset(MS[:], 1.0), (), [b_c])
            zc_ = 63 if rev else 0
            mk.op("pool", lambda E: E.memset(MS[:].rearrange("p (a l) -> p a l", l=64)[:, :, zc_:zc_ + 1], 0.0), [b_c], [b_c])

            def asel(ap, upper, strict):
                pat = [[1, 64]] if upper else [[-1, 64]]
                cm = -1 if upper else 1
                mk.op("pool", lambda E: E.affine_select(out=ap, in_=ap, pattern=pat, compare_op=ALU.is_ge, fill=0.0,
                                                        base=(-1 if strict else 0), channel_multiplier=cm),
                      [b_c], [b_c])
            asel(maskA[:, 0:64], not rev, True)
            asel(maskA[:, 64:128], not rev, False)
            asel(maskL[:], rev, True)
            mA = maskA[:, None, :].broadcast_to([64, 8, 128])
            mL = maskL[:, None, :].broadcast_to([64, 8, 64])
            id64 = ident[0:64, 0:64]
            idbc = ident[0:64, None, 0:64].broadcast_to([64, 8, 64])

            Lr = [sb("Lq%d" % q, [64, 8, NT2 + 2]) for q in range(2)]; b_L = [Buf() for _ in range(2)]
            Lr.append(Lr[0]); b_L.append(b_L[0])
            XW = sb("XW", [64, NT2 + 2]); XA = sb("XA", [64, NT2 + 2]); b_XW = Buf(); b_XA = Buf()
            T1, b_T1 = T4("T1")
            SH = [T4("SH%d" % q) for q in range(2)]
            (Rp, b_Rp), (Kp, b_Kp) = SH
            tt0, cp0 = tt, cp
            tw = sb("tw", [64, NT2]); b_tw = Buf()
            xwp = sb("xwp", [64, NT2]); xap = sb("xap", [64, NT2]); b_xwp = Buf(); b_xap = Buf()
            XB, b_XB = T4("XB"); E2, b_E2 = T4("E2"); AD, b_AD = T4("AD"); KR, b_KR = T4("KR")
            SS, b_SS = T4("SS"); KD, b_KD = T4("KD"); AB, b_AB = T4("AB"); BON, b_BON = XB, b_XB
            G, b_G = T1, b_T1; D1, b_D1 = E2, b_E2; D2, b_D2 = AD, b_AD; EP, b_EP = XB, b_XB; EN, b_EN = SS, b_SS
            T2, b_T2 = SS, b_SS
            SD = F32
            SETS = []
            for i_ in range(2):
                st_ = []
                for nm, shp in (("AR", [64, 8, 2, 128]), ("KT", [64, 9, NT2]), ("BT", [64, 9, NT2]), ("KH", [64, 8, NT2]),
                                ("BH", [64, 8, NT2]), ("Vp", [64, 8, NT2]), ("GL", [64, 16])):
                    st_ += [sb("%s_%d" % (nm, i_), shp), Buf()]
                SETS.append(st_)
            YT, b_YT = T4("YT")
            MT1 = sb("MT1", [64, 2, 8, 128], SD); MT2 = sb("MT2", [64, 2, 8, 128], SD); b_MT1 = Buf(); b_MT2 = Buf()
            P0 = sb("P0", [64, 17, 64], SD); b_P0 = Buf()
            PP = [sb("PP%d" % i, [64, 33, 64], SD) for i in range(2)]; b_PP = [Buf(), Buf()]
            Zt = [sb("Zt%d" % i, [64, 17, 128], SD) for i in range(2)]; b_Zt = [Buf() for _ in range(2)]
            VT = sb("VT", [64, 17, 64], SD); BHt = sb("BHt", [64, 17, 64], SD); KHt = sb("KHt", [64, 17, 64], SD)
            QT = sb("QT", [64, 2, 8, 64]); MM = sb("MM", [64, 17, 64]); DG = sb("DG", [64, 17, 64])
            b_VT = Buf(); b_BHt = Buf(); b_KHt = Buf(); b_QT = Buf(); b_MM = Buf(); b_DG = Buf()
            STt = [sb("ST%d" % i, [64, 9, 64]) for i in range(2)]; b_ST = [Buf(), Buf()]
            for t_, b__, r_ in ((Zt[0], b_Zt[0], 16), (Zt[1], b_Zt[1], 16)):
                ts("dve", RR(t_[:, r_, :]), maskA[:], 0.0, None, ALU.mult, None, [b_c], [b__])
            for t_, b__ in ((VT, b_VT), (BHt, b_BHt), (KHt, b_KHt), (MM, b_MM)):
                ts("dve", RR(t_[:, 16, :]), ones64[:], 0.0, None, ALU.mult, None, [b_c], [b__])
            for i_ in range(2):
                ts("dve", RR(STt[i_][:, 8, :]), ones64[:], 0.0, None, ALU.mult, None, [b_c], [b_ST[i_]])
                ts("dve", RR(SETS[i_][2][:, 8, :]), maskA[:], 0.0, None, ALU.mult, None, [b_c], [SETS[i_][3]])
                ts("dve", RR(SETS[i_][4][:, 8, :]), maskA[:], 0.0, None, ALU.mult, None, [b_c], [SETS[i_][5]])
            ts("dve", RR(P0[:, 16, :]), ones64[:], 0.0, None, ALU.mult, None, [b_c], [b_P0])
            for i_ in range(2):
                ts("dve", RR(PP[i_][:, 32, :]), ones64[:], 0.0, None, ALU.mult, None, [b_c], [b_PP[i_]])
            mA16 = maskA[:, None, :].broadcast_to([64, 16, 128])
            mL16 = maskL[:, None, :].broadcast_to([64, 16, 64])
            idbc16 = ident[0:64, None, 0:64].broadcast_to([64, 16, 64])

            def f16(t):
                if len(t.shape) == 3:
                    return t[:, 0:16, :]
                return t[:].rearrange("p c h n -> p (c h) n")

            def wd(t, blk, n, off=0):
                fl = t[:].rearrange("p a n -> p (a n)")
                return fl[:, blk * n + off:blk * n + off + 128]

            def pv(p, lo, n):
                return p[0:64, lo:lo + 16 * n].rearrange("p (a n) -> p a n", n=n)

            def v3(p, n):
                return p[0:64, 0:8 * n].rearrange("p (h n) -> p h n", n=n)

            def bc(t, lo, hi, n):
                return t[:, lo:hi, None].broadcast_to([64, hi - lo, n])

            def c4(t):
                return t[:].rearrange("p h (c l) -> p h c l", l=64)

            for s, (toff, coff, T) in enumerate(SEQ):
                sti_ = [0]
                ts("dve", RR(STt[0][:, 0:8, :]), STt[1][:, 0:8, :], 0.0, None, ALU.mult, None, [b_ST[1]], [b_ST[0]])
                ntile = T // NT2
                order = list(range(ntile - 1, -1, -1) if rev else range(ntile))

                def prep(ti, AR, b_AR, KT, b_KT, BT, b_BT, KH, b_KH, BH, b_BH, Vp, b_Vp, GL, b_GL):
                    ARb, b_ARb = AR, b_AR
                    tl = ti * NT2
                    c0 = coff + tl
                    tg = toff + tl
                    def ldq(q):
                        dma("sp" if q != 1 else "act", Lr[q][:],
                            Pscr[q * 512:(q + 1) * 512, c0:c0 + NT2 + 2].rearrange("(h p) t -> p h t", p=64),
                            [b_P], [b_L[q]])

                    def shq(q):
                        Lq = Lr[q]; S_, bS = (SH[q] if q < 2 else (Vp, b_Vp))
                        tt("pool", T1[:], Lq[:, :, 0:NT2], Lq[:, :, 2:NT2 + 2], ALU.add, [b_L[q]], [b_T1])
                        tt("pool", T1[:], T1[:], bc(hm3, 8 * q, 8 * q + 8, NT2), ALU.mult, [b_T1, b_hm3], [b_T1])
                        tt("pool", S_[:], Lq[:, :, 1:NT2 + 1], bc(om3, 8 * q, 8 * q + 8, NT2), ALU.mult,
                           [b_L[q], b_hm3], [bS])
                        tt("pool", S_[:], S_[:], T1[:], ALU.add, [bS, b_T1], [bS])
                    ldq(0); ldq(1)
                    dma("sp", XW[:], Pscr[1536 + 64 * d:1600 + 64 * d, c0:c0 + NT2 + 2], [b_P], [b_XW])
                    dma("act", XA[:], Pscr[1664 + 64 * d:1728 + 64 * d, c0:c0 + NT2 + 2], [b_P], [b_XA])
                    shq(0); ldq(2); shq(1); shq(2)
                    for (X_, bX, o_, bo, j) in ((XW, b_XW, xwp, b_xwp, 0), (XA, b_XA, xap, b_xap, 1)):
                        tt("dve", tw[:], X_[:, 0:NT2], X_[:, 2:NT2 + 2], ALU.add, [bX], [b_tw])
                        ts("dve", tw[:], tw[:], hmw[:, j:j + 1], None, ALU.mult, None, [b_tw, b_hmw], [b_tw])
                        stt(o_[:], X_[:, 1:NT2 + 1], omw[:, j:j + 1], tw[:], ALU.mult, ALU.add, [bX, b_hmw, b_tw], [bo])
                    act(xwp[:], xwp[:], AF.Tanh, [b_xwp], [b_xwp])
                    for hh in range(2):
                        pa, bpa = nextps()
                        for j in range(4):
                            h = 4 * hh + j
                            mm(pa[0:64, j * NT2:(j + 1) * NT2], w2d[:, h * 64:(h + 1) * 64], xwp[:], True, True,
                               [b_w2d, b_xwp], [bpa])
                        tt("dve", XB[:, 4 * hh:4 * hh + 4, :], pa[0:64, 0:4 * NT2].rearrange("p (h n) -> p h n", n=NT2),
                           bc(w0d, 4 * hh, 4 * hh + 4, NT2), ALU.add, [bpa, b_w0d], [b_XB])
                    act(XB[:], XB[:], AF.Exp, [b_XB], [b_XB], scale=-1.0)
                    act(XB[:], XB[:], AF.Ln, [b_XB], [b_XB], bias=1.0)
                    act(E2[:], XB[:], AF.Exp, [b_XB], [b_E2], bias=-0.5, scale=-1.0)
                    for hh in range(2):
                        pa, bpa = nextps()
                        for j in range(4):
                            h = 4 * hh + j
                            mm(pa[0:64, j * NT2:(j + 1) * NT2], a2d[:, h * 64:(h + 1) * 64], xap[:], True, True,
                               [b_w2d, b_xap], [bpa])
                        tt("dve", AD[:, 4 * hh:4 * hh + 4, :], pa[0:64, 0:4 * NT2].rearrange("p (h n) -> p h n", n=NT2),
                           bc(a0d, 4 * hh, 4 * hh + 4, NT2), ALU.add, [bpa, b_a0d], [b_AD])
                    act(AD[:], AD[:], AF.Sigmoid, [b_AD], [b_AD])
                    tt("pool", KR[:], Kp[:], bc(kk_, 0, 8, NT2), ALU.mult, [b_Kp, b_kk_], [b_KR])
                    tt("pool", T1[:], KR[:], KR[:], ALU.mult, [b_KR], [b_T1])
                    for hh in range(2):
                        pa, bpa = nextps()
                        for j in range(4):
                            h = 4 * hh + j
                            mm(pa[0:64, j * NT2:(j + 1) * NT2], ones64[:], T1[:, h, :], True, True, [b_c, b_T1], [bpa])
                        ts("dve", SS[:, 4 * hh:4 * hh + 4, :], pa[0:64, 0:4 * NT2].rearrange("p (h n) -> p h n", n=NT2),
                           1e-24, None, ALU.max, None, [bpa], [b_SS])
                    act(SS[:], SS[:], AF.Sqrt, [b_SS], [b_SS])
                    mk.op("dve", lambda E: E.reciprocal(out=SS[:], in_=SS[:]), [b_SS], [b_SS])
                    tt("pool", KR[:], KR[:], SS[:], ALU.mult, [b_KR, b_SS], [b_KR])
                    tt("pool", T2[:], AD[:], bc(ka_, 0, 8, NT2), ALU.mult, [b_AD, b_ka_], [b_T2])
                    tt("pool", T2[:], T2[:], bc(omka, 0, 8, NT2), ALU.add, [b_T2, b_omka], [b_T2])
                    tt("pool", KD[:], T2[:], Kp[:], ALU.mult, [b_T2, b_Kp], [b_KD])
                    tt("dve", AB[:], AD[:], KR[:], ALU.mult, [b_AD, b_KR], [b_AB])
                    tt("pool", T1[:], Rp[:], KD[:], ALU.mult, [b_Rp, b_KD], [b_T1])
                    tt("pool", T1[:], T1[:], bc(rk_, 0, 8, NT2), ALU.mult, [b_T1, b_rk_], [b_T1])
                    for hh in range(2):
                        pa, bpa = nextps()
                        for j in range(4):
                            h = 4 * hh + j
                            mm(pa[0:64, j * NT2:(j + 1) * NT2], ones64[:], T1[:, h, :], True, True, [b_c, b_T1], [bpa])
                        tt("dve", BON[:, 4 * hh:4 * hh + 4, :], pa[0:64, 0:4 * NT2].rearrange("p (h n) -> p h n", n=NT2),
                           Vp[:, 4 * hh:4 * hh + 4, :], ALU.mult, [bpa, b_Vp], [b_BON])
                    dma("sp", BD[d, :, :, tg:tg + NT2], BON[:], [b_BON], [b_BD])
                    E2f = E2[:].rearrange("p h t -> p (h t)"); Gf = G[:].rearrange("p h t -> p (h t)"); MSf = MS[:]
                    if rev:
                        E2f = E2f[:, ::-1]; Gf = Gf[:, ::-1]; MSf = MSf[:, ::-1]
                    mk.op("dve", lambda E, Gf=Gf, MSf=MSf, E2f=E2f: E.tensor_tensor_scan(
                        out=Gf, data0=MSf, data1=E2f, initial=0.0, op0=ALU.mult, op1=ALU.add), [b_E2, b_c], [b_G])
                    tt("pool", D1[:], G[:], E2[:], ALU.subtract, [b_G, b_E2], [b_D1])
                    Gv = G[:].rearrange("p h (c l) -> p (h c) l", l=64)
                    ti_ = 0 if rev else 63
                    totb = Gv[:, :, ti_:ti_ + 1].broadcast_to([64, 16, 64])
                    tt("pool", D2[:].rearrange("p h (c l) -> p (h c) l", l=64), Gv, totb, ALU.subtract, [b_G], [b_D2])
                    act(EP[:], G[:], AF.Exp, [b_G], [b_EP])
                    act(EN[:], G[:], AF.Exp, [b_G], [b_EN], scale=-1.0)
                    act(D1[:], D1[:], AF.Exp, [b_D1], [b_D1], scale=-1.0)
                    act(D2[:], D2[:], AF.Exp, [b_D2], [b_D2])
                    act(GL[:].rearrange("p (a o) -> p a o", o=1), Gv[:, :, ti_:ti_ + 1], AF.Exp, [b_G], [b_GL], scale=-1.0)
                    stt(RR(AR[:, :, :, 0:64]), c4(KR), -1.0, c4(D1), ALU.mult, ALU.mult, [b_KR, b_D1], [b_AR])
                    tt("pool", RR(AR[:, :, :, 64:128]), c4(Rp), c4(EN), ALU.mult, [b_Rp, b_EN], [b_AR])
                    tt("pool", RR(KT[:, 0:8, :]), KD[:], EP[:], ALU.mult, [b_KD, b_EP], [b_KT])
                    tt("dve", RR(BT[:, 0:8, :]), AB[:], EP[:], ALU.mult, [b_AB, b_EP], [b_BT])
                    tt("pool", KH[:], KD[:], D2[:], ALU.mult, [b_KD, b_D2], [b_KH])
                    tt("dve", BH[:], AB[:], D2[:], ALU.mult, [b_AB, b_D2], [b_BH])

                def chunk(ti, pend, AR, b_AR, KT, b_KT, BT, b_BT, KH, b_KH, BH, b_BH, Vp, b_Vp, GL, b_GL):
                    ARb, b_ARb = AR, b_AR
                    tg = toff + ti * NT2

                    def tt(*a):
                        tt0(*a)
                        mk.replay(pend, 2)

                    def cp(*a):
                        cp0(*a)
                        mk.replay(pend, 2)
                    CS = [slice(0, 64), slice(64, 128)]
                    mA8 = maskA[:, None, :].broadcast_to([64, 8, 128])
                    mL8 = maskL[:, None, :].broadcast_to([64, 8, 64])
                    idbc8 = ident[0:64, None, 0:64].broadcast_to([64, 8, 64])
                    C2 = (0, 1)

                    def blk(c):
                        return slice(c * 8, (c + 1) * 8)

                    def p8v(p, lo, n):
                        return p[0:64, lo:lo + 8 * n].rearrange("p (a n) -> p a n", n=n)
                    for c in C2:
                        p1, bp1 = nextps()
                        for h in range(8):
                            mmr(p1[0:128, h * 128:(h + 1) * 128], wd(BT, h, 128, c * 64), ARb[:, h, c, :], [b_BT, b_ARb], [bp1])
                        tt("dve", RR(MT1[:, c]), p8v(p1, 0, 128), mA8, ALU.mult, [bp1, b_c], [b_MT1])
                    for c in C2:
                        p2, bp2 = nextps()
                        for h in range(8):
                            mmr(p2[0:128, h * 128:(h + 1) * 128], wd(KT, h, 128, c * 64), ARb[:, h, c, :], [b_KT, b_ARb], [bp2])
                        tt("dve", RR(MT2[:, c]), p8v(p2, 0, 128), mA8, ALU.mult, [bp2, b_c], [b_MT2])
                    for c in C2:
                        p3, bp3 = nextps()
                        for h in range(8):
                            mmr(p3[0:128, h * 64:(h + 1) * 64], ARb[:, h, c, :], BT[:, h, CS[c]], [b_ARb, b_BT], [bp3])
                        tt("dve", RR(P0[:, blk(c), :]), p8v(p3, 0, 64), mL8, ALU.mult, [bp3, b_c], [b_P0])
                    Z0 = Zt[0]; bZ0 = b_Zt[0]
                    for c in C2:
                        p4, bp4 = nextps()
                        for h in range(8):
                            mk.op("pe", lambda E, p4=p4, o=h * 64, a=AR[:, h, c, 0:64]: E.transpose(
                                out=p4[0:64, o:o + 64], in_=a, identity=id64), [b_AR, b_ident], [bp4])
                            mk.op("pe", lambda E, p4=p4, o=512 + h * 64, a=Vp[:, h, CS[c]]: E.transpose(
                                out=p4[0:64, o:o + 64], in_=a, identity=id64), [b_Vp, b_ident], [bp4])
                        cp0("act", RR(Z0[:, blk(c), 0:64]), p8v(p4, 0, 64), [bp4], [bZ0])
                        cp("act", RR(VT[:, blk(c), :]), p8v(p4, 512, 64), [bp4], [b_VT])
                    for c in C2:
                        p5, bp5 = nextps()
                        for h in range(8):
                            mk.op("pe", lambda E, p5=p5, o=h * 64, a=BH[:, h, CS[c]]: E.transpose(
                                out=p5[0:64, o:o + 64], in_=a, identity=id64), [b_BH, b_ident], [bp5])
                            mk.op("pe", lambda E, p5=p5, o=512 + h * 64, a=KH[:, h, CS[c]]: E.transpose(
                                out=p5[0:64, o:o + 64], in_=a, identity=id64), [b_KH, b_ident], [bp5])
                        cp0("dve", RR(BHt[:, blk(c), :]), p8v(p5, 0, 64), [bp5], [b_BHt])
                        cp("dve", RR(KHt[:, blk(c), :]), p8v(p5, 512, 64), [bp5], [b_KHt])
                    for c in C2:
                        p6, bp6 = nextps()
                        for h in range(8):
                            mmr(p6[0:128, h * 64:(h + 1) * 64], MT2[:, c, h, :], VT[:, c * 8 + h, :], [b_MT2, b_VT], [bp6])
                        cp("act", RR(Z0[:, blk(c), 64:128]), p8v(p6, 0, 64), [bp6], [bZ0])
                    zi = 0
                    Pv = lambda c, h: P0[:, c * 8 + h, :]
                    Pw = lambda c, h: wd(P0, c * 8 + h, 64)
                    PTv = lambda c, h: MT1[:, c, h, 0:64]
                    PTw = lambda c, h: MT1[:, c, h, :]
                    bP = b_P0; bPT = b_MT1
                    for it in range(6):
                        Zc = Zt[zi]; bZc = b_Zt[zi]; Zn = Zt[1 - zi]; bZn = b_Zt[1 - zi]
                        PTc, Pc, PTcw, Pcw, bPTc, bPc = PTv, Pv, PTw, Pw, bPT, bP
                        if it < 5:
                            nx = it % 2
                            for c in C2:
                                p8, bp8 = nextps()
                                for h in range(8):
                                    mmr(p8[0:128, h * 64:(h + 1) * 64], PTcw(c, h), Pc(c, h), [bPTc, bPc], [bp8])
                                    mmr(p8[0:128, 512 + h * 64:512 + (h + 1) * 64], Pcw(c, h), PTc(c, h), [bPTc, bPc], [bp8])
                                cp("act", RR(PP[nx][:, 0:32, :].rearrange("p (k a) n -> p k a n", k=2)[:, :, blk(c), :]),
                                   p8[0:64, 0:1024].rearrange("p (k a n) -> p k a n", k=2, a=8), [bp8], [b_PP[nx]])
                            Pv = lambda c, h, nx=nx: PP[nx][:, c * 8 + h, :]
                            PTv = lambda c, h, nx=nx: PP[nx][:, 16 + c * 8 + h, :]
                            Pw = lambda c, h, nx=nx: wd(PP[nx], c * 8 + h, 64)
                            PTw = lambda c, h, nx=nx: wd(PP[nx], 16 + c * 8 + h, 64)
                            bP = b_PP[nx]; bPT = b_PP[nx]
                        for c in C2:
                            p7, bp7 = nextps()
                            for h in range(8):
                                mmr(p7[0:128, h * 128:(h + 1) * 128], PTcw(c, h), Zc[:, c * 8 + h, :], [bPTc, bZc], [bp7])
                            tt("dve", RR(Zn[:, blk(c), :]), p8v(p7, 0, 128), Zc[:, blk(c), :], ALU.add, [bp7, bZc], [bZn])
                        zi = 1 - zi
                    Zf = Zt[zi]; bZf = b_Zt[zi]
                    for c in C2:
                        p9, bp9 = nextps()
                        for h in range(8):
                            mmr(p9[0:128, h * 64:(h + 1) * 64], Zf[:, c * 8 + h, :], MT1[:, c, h, 64:128], [bZf, b_MT1], [bp9])
                            mmr(p9[0:128, 512 + h * 64:512 + (h + 1) * 64], Zf[:, c * 8 + h, :], BHt[:, c * 8 + h, :], [bZf, b_BHt], [bp9])
                        tt0("dve", RR(QT[:, c]), p8v(p9, 0, 64), AR[:, :, c, 64:128], ALU.add, [bp9, b_AR], [b_QT])
                        GLc = GL[:].rearrange("p (h c) -> p c h", c=2)[:, c, :, None].broadcast_to([64, 8, 64])
                        tt0("pool", DG[:, blk(c), :], idbc8, GLc, ALU.mult, [b_ident, b_GL], [b_DG])
                        tt("dve", RR(MM[:, blk(c), :]), p8v(p9, 512, 64), DG[:, blk(c), :], ALU.add, [bp9, b_DG], [b_MM])
                    for c in (range(1, -1, -1) if rev else range(2)):
                        sti = sti_[0]
                        ST = STt[sti]; bST = b_ST[sti]; STn = STt[1 - sti]; bSTn = b_ST[1 - sti]
                        p11, bp11 = nextps()
                        for h in range(8):
                            o_ = p11[0:128, h * 64:(h + 1) * 64]
                            k_ = c * 8 + h
                            mmr(o_, wd(ST, h, 64), QT[:, c, h, :], [bST, b_QT], [bp11], True, False)
                            mmr(o_, wd(Zf, k_, 128, 64), MT1[:, c, h, 64:128], [bZf, b_MT1], [bp11], False, False)
                            mmr(o_, wd(VT, k_, 64), MT2[:, c, h, 64:128], [b_VT, b_MT2], [bp11], False, True)
                        cp("act", YT[:, :, CS[c]], v3(p11, 64), [bp11], [b_YT])
                        p12, bp12 = nextps()
                        for h in range(8):
                            o_ = p12[0:128, h * 64:(h + 1) * 64]
                            k_ = c * 8 + h
                            mmr(o_, wd(MM, k_, 64), ST[:, h, :], [b_MM, bST], [bp12], True, False)
                            mmr(o_, wd(BHt, k_, 64), Zf[:, k_, 64:128], [b_BHt, bZf], [bp12], False, False)
                            mmr(o_, wd(KHt, k_, 64), VT[:, k_, :], [b_KHt, b_VT], [bp12], False, True)
                        cp("dve", RR(STn[:, 0:8, :]), v3(p12, 64), [bp12], [bSTn])
                        sti_[0] = 1 - sti
                    dma("sp", YD[d, :, :, tg:tg + NT2], YT[:], [b_YT], [b_YD])

                PIPE = os.environ.get('RW_NOPIPE') != '1'
                if PIPE:
                    prep(order[0], *SETS[0])
                for idx, ti in enumerate(order):
                    pend = []
                    if not PIPE:
                        prep(ti, *SETS[idx % 2])
                    elif idx + 1 < len(order):
                        mk.deferred = pend
                        prep(order[idx + 1], *SETS[(idx + 1) % 2])
                        mk.deferred = None
                    if os.environ.get('RW_PIPE_MODE') == 'start':
                        mk.replay(pend, len(pend))
                    chunk(ti, pend, *SETS[idx % 2])
                    mk.replay(pend, len(pend))
            mk.flush(final=True)

    if upto >= 1:
        rwkv_pass(0)
        rwkv_pass(1)

    def s5_pass():
        es, sb, ps = scope("s5_")
        with es:
            TWO_PI = 2.0 * math.pi
            pz = [ps("pz%d" % i, [128, 512]) for i in range(8)]
            b_pz = [Buf() for _ in range(8)]
            pctr = [0]

            def nextps():
                i = pctr[0] % 8
                pctr[0] += 1
                return pz[i], b_pz[i]

            NLV = 10
            identb = sb("identb", [64, 64], BF16)
            dsk = sb("dsk", [16, 32])
            SQr = sb("SQr", [64, NLV, 64]); SQi = sb("SQi", [64, NLV, 64]); SQin = sb("SQin", [64, NLV, 64])
            LTr = sb("LTr", [64, 8, 64, 16], BF16); LTi = sb("LTi", [64, 8, 64, 16], BF16)
            OTr = sb("OTr", [64, 8, 64, 16], BF16); OTn = sb("OTn", [64, 8, 64, 16], BF16)
            CRb = sb("CRb", [64, 64, 16], BF16); CInb = sb("CInb", [64, 64, 16], BF16)
            bt_ = Buf()
            es2, sb2, ps2_ = scope("s5t_")
            ones1 = sb2("ones1", [1, 64]); row = sb2("row", [1, 64])
            mk.op("pool", lambda E: E.memset(ones1[:], 1.0), (), [bt_])
            dma("sp", row[:], log_dt.rearrange("d g -> (d g)").rearrange("(o n) -> o n", o=1), (), [bt_])
            cp("dve", identb[:], ident[0:64, 0:64], [b_ident], [bt_])
            LR = sb2("LR", [64, 64]); LI = sb2("LI", [64, 64])
            dma("sp", LR[:].rearrange("p (d g) -> p d g", d=2), lam_re.rearrange("d g p -> p d g"), (), [bt_], slow=True)
            dma("act", LI[:].rearrange("p (d g) -> p d g", d=2), lam_im.rearrange("d g p -> p d g"), (), [bt_], slow=True)
            BR = sb2("BR", [64, 64, 16]); BI = sb2("BI", [64, 64, 16])
            dma("sp", BR[:].rearrange("p (d g) h -> p d g h", d=2), b_re.rearrange("d g p h -> p d g h"), (), [bt_])
            dma("act", BI[:].rearrange("p (d g) h -> p d g h", d=2), b_im.rearrange("d g p h -> p d g h"), (), [bt_])
            CR = sb2("CR", [64, 64, 16]); CI = sb2("CI", [64, 64, 16])
            cnat = sb2("cnat", [128, 8, 64])
            for (src, dst) in ((c_re, CR), (c_im, CI)):
                dma("sp", cnat[:], src.rearrange("d g h p -> (d g h) p").rearrange("(k q) p -> q k p", q=128), [bt_], [bt_])
                for k in range(8):
                    pq, bq = nextps()
                    mk.op("pe", lambda E, pq=pq, k=k: E.transpose(out=pq[0:64, 0:128], in_=cnat[:, k, :], identity=ident[:]),
                          [bt_, b_ident], [bq])
                    cp("dve", dst[:, k * 8:(k + 1) * 8, :], pq[0:64, 0:128].rearrange("p (g h) -> p g h", h=16), [bq], [bt_])
            dma("sp", dsk[:], d_skip.rearrange("(g h) -> h g", h=16), (), [bt_], slow=True)
            DT = sb2("DT", [64, 64])
            pq, bq = nextps()
            mm(pq[0:64, 0:64], ones1[:], row[:], True, True, [bt_], [bq])
            act(DT[:], pq[0:64, 0:64], AF.Exp, [bq], [bt_])

            def T64(name):
                return sb2(name, [64, 64])
            ZR = T64("ZR"); ZI = T64("ZI"); EPs = T64("EPs"); COS = T64("COS"); SIN = T64("SIN")
            tA = T64("tA"); tB = T64("tB"); tC = T64("tC"); tI = sb2("tI", [64, 64], mybir.dt.int32)
            tt("dve", ZR[:], LR[:], DT[:], ALU.mult, [bt_], [bt_])
            tt("dve", ZI[:], LI[:], DT[:], ALU.mult, [bt_], [bt_])
            act(EPs[:], ZR[:], AF.Exp, [bt_], [bt_])
            for (dst, offs) in ((SIN, 64.0), (COS, 64.25)):
                ts("dve", tA[:], ZI[:], 1.0 / TWO_PI, offs, ALU.mult, ALU.add, [bt_], [bt_])
                cp("dve", tI[:], tA[:], [bt_], [bt_])
                cp("dve", tB[:], tI[:], [bt_], [bt_])
                tt("dve", tA[:], tA[:], tB[:], ALU.subtract, [bt_], [bt_])
                ts("dve", tB[:], tA[:], 0.5, None, ALU.is_gt, None, [bt_], [bt_])
                tt("dve", tA[:], tA[:], tB[:], ALU.subtract, [bt_], [bt_])
                act(dst[:], tA[:], AF.Sin, [bt_], [bt_], scale=TWO_PI)
            PWr = sb2("PWr", [64, 9, 64]); PWi = sb2("PWi", [64, 9, 64])
            mk.op("pool", lambda E: E.memset(PWr[:, 0, :], 1.0), (), [bt_])
            mk.op("pool", lambda E: E.memset(PWi[:, 0, :], 0.0), (), [bt_])
            tt("dve", PWr[:, 1, :], EPs[:], COS[:], ALU.mult, [bt_], [bt_])
            tt("dve", PWi[:, 1, :], EPs[:], SIN[:], ALU.mult, [bt_], [bt_])

            def cmul(or_, oi_, ar, ai, br, bi, n3=None):
                tt("dve", tA[:], ai, bi, ALU.mult, [bt_], [bt_])
                tt("dve", tB[:], ai, br, ALU.mult, [bt_], [bt_])
                tt("dve", tC[:], ar, br, ALU.mult, [bt_], [bt_])
                tt("dve", or_, tC[:], tA[:], ALU.subtract, [bt_], [bt_])
                tt("dve", tC[:], ar, bi, ALU.mult, [bt_], [bt_])
                tt("dve", oi_, tC[:], tB[:], ALU.add, [bt_], [bt_])
            for j in range(2, 9):
                cmul(PWr[:, j, :], PWi[:, j, :], PWr[:, j - 1, :], PWi[:, j - 1, :], PWr[:, 1, :], PWi[:, 1, :])
            NLV = 10
            cp("dve", SQr[:, 0, :], PWr[:, 8, :], [bt_], [bt_]); cp("dve", SQi[:, 0, :], PWi[:, 8, :], [bt_], [bt_])
            for k in range(1, NLV):
                cmul(SQr[:, k, :], SQi[:, k, :], SQr[:, k - 1, :], SQi[:, k - 1, :], SQr[:, k - 1, :], SQi[:, k - 1, :])
            ts("dve", SQin[:], SQi[:], -1.0, None, ALU.mult, None, [bt_], [bt_])
            CFr = T64("CFr"); CFi = T64("CFi"); DEN = T64("DEN"); NR = T64("NR")
            ts("dve", NR[:], PWr[:, 1, :], -1.0, None, ALU.add, None, [bt_], [bt_])
            tt("dve", tA[:], LR[:], LR[:], ALU.mult, [bt_], [bt_])
            tt("dve", tB[:], LI[:], LI[:], ALU.mult, [bt_], [bt_])
            tt("dve", DEN[:], tA[:], tB[:], ALU.add, [bt_], [bt_])
            mk.op("dve", lambda E: E.reciprocal(out=DEN[:], in_=DEN[:]), [bt_], [bt_])
            tt("dve", tA[:], NR[:], LR[:], ALU.mult, [bt_], [bt_])
            tt("dve", tB[:], PWi[:, 1, :], LI[:], ALU.mult, [bt_], [bt_])
            tt("dve", tA[:], tA[:], tB[:], ALU.add, [bt_], [bt_])
            tt("dve", CFr[:], tA[:], DEN[:], ALU.mult, [bt_], [bt_])
            tt("dve", tA[:], PWi[:, 1, :], LR[:], ALU.mult, [bt_], [bt_])
            tt("dve", tB[:], NR[:], LI[:], ALU.mult, [bt_], [bt_])
            tt("dve", tA[:], tA[:], tB[:], ALU.subtract, [bt_], [bt_])
            tt("dve", CFi[:], tA[:], DEN[:], ALU.mult, [bt_], [bt_])
            BbR = sb2("BbR", [64, 64, 16]); BbI = sb2("BbI", [64, 64, 16])
            X1t = sb2("X1t", [64, 64, 16]); X2t = sb2("X2t", [64, 64, 16])

            def b16(t2):
                return t2[:, :, None].broadcast_to([64, 64, 16])

            def cmul3(or_, oi_neg, ar2, ai2, br3, bi3, e1="dve", e2="pool"):
                tt(e1, X1t[:], br3, b16(ar2), ALU.mult, [bt_], [bt_])
                tt(e1, X2t[:], bi3, b16(ai2), ALU.mult, [bt_], [bt_])
                tt(e1, or_, X1t[:], X2t[:], ALU.subtract, [bt_], [bt_])
                tt(e1, X1t[:], bi3, b16(ar2), ALU.mult, [bt_], [bt_])
                tt(e1, X2t[:], br3, b16(ai2), ALU.mult, [bt_], [bt_])
                if oi_neg[1]:
                    tt(e1, X1t[:], X1t[:], X2t[:], ALU.add, [bt_], [bt_])
                    ts(e1, oi_neg[0], X1t[:], -1.0, None, ALU.mult, None, [bt_], [bt_])
                else:
                    tt(e1, oi_neg[0], X1t[:], X2t[:], ALU.add, [bt_], [bt_])
            cmul3(BbR[:], (BbI[:], False), CFr[:], CFi[:], BR[:], BI[:])
            for j in range(8):
                cmul3(LTr[:, j], (LTi[:, j], False), PWr[:, j, :], PWi[:, j, :], BbR[:], BbI[:])
                cmul3(OTr[:, j], (OTn[:, j], True), PWr[:, j + 1, :], PWi[:, j + 1, :], CR[:], CI[:])
            cp("dve", CRb[:], CR[:], [bt_], [bt_]); ts("dve", CInb[:], CI[:], -1.0, None, ALU.mult, None, [bt_], [bt_])

            mk.flush(final=True)
            es2.close()
            KTg = [sb("KTg%d" % i, [16, 15, 16], BF16) for i in range(2)]; b_KTg = [Buf(), Buf()]
            CTg = [sb("CTg%d" % i, [16, 32, 64], BF16) for i in range(2)]; b_CTg = [Buf(), Buf()]
            UGN = 2048
            ugs = [sb("ug%d" % i, [16, UGN]) for i in range(2)]; b_ugs = [Buf(), Buf()]; ugc = [0]
            ub = [sb("ub%d" % i, [16, 8, 1024], BF16) for i in range(2)]; b_ub = [Buf(), Buf()]
            ygs = [sb("yg0", [16, 4096])] * 2; b_ygs = [Buf()] * 2; ygc = [0]
            NBM = 1024
            Wt = [[[sb("W%d%d%d" % (pp, d, c), [64, NBM + 1]) for c in range(2)] for d in range(2)] for pp in range(2)]
            b_Wt = [[Buf() for d in range(2)] for pp in range(2)]
            Sb = [[[sb("Sb%d%d%d" % (i, d, c), [64, NBM + 1], BF16) for c in range(2)] for d in range(2)] for i in range(2)]
            b_Sb = [Buf(), Buf()]
            for pp in range(2):
                for d in range(2):
                    for c in range(2):
                        mk.op("pool", lambda E, t=Wt[pp][d][c]: E.memset(t[:], 0.0), (), [b_Wt[pp][d]])
            id16 = ident[0:16, 0:16]

            def gconsts(g):
                par = g % 2
                pk, bpk = nextps()
                for idx in range(15):
                    if idx == 0:
                        terms = [(0, 0), (1, 0)]
                    elif idx < 8:
                        terms = [(0, idx)]
                    else:
                        terms = [(1, idx - 7)]
                    n = 0
                    for (d, tau) in terms:
                        gi = d * 32 + g
                        mm(pk[0:16, idx * 16:(idx + 1) * 16], LTr[:, tau, gi, :], CRb[:, gi, :], n == 0, False, [bt_], [bpk]); n += 1
                        mm(pk[0:16, idx * 16:(idx + 1) * 16], LTi[:, tau, gi, :], CInb[:, gi, :], False, n == 2 * len(terms) - 1, [bt_], [bpk]); n += 1
                cp("dve", KTg[par][:].rearrange("p a b -> p (a b)"), pk[0:16, 0:240], [bpk], [b_KTg[par]])
                stt(KTg[par][:, 0, :], id16, dsk[:, g:g + 1], KTg[par][:, 0, :], ALU.mult, ALU.add, [b_KTg[par], bt_, b_ident], [b_KTg[par]])
                for q in range(4):
                    pc_, bpc = nextps()
                    for j in range(8):
                        i = q * 8 + j
                        d = i // 16; s_ = (i // 2) % 8; c = i % 2
                        e_ = (7 - s_) if d == 0 else s_
                        src = (LTr if c == 0 else LTi)[:, e_, d * 32 + g, :]
                        mm(pc_[0:16, j * 64:(j + 1) * 64], src, identb[:], True, True, [bt_], [bpc])
                    cp("act", CTg[par][:, q * 8:(q + 1) * 8, :].rearrange("p a b -> p (a b)"),
                       pc_[0:16, 0:512], [bpc], [b_CTg[par]])

            def dims(s):
                toff, coff, T = SEQ[s]
                nblk = T // 8
                BW = min(512, nblk)
                return toff, coff, T, nblk, BW, 8 * BW, nblk // BW

            def front(g, s, st):
                par = g % 2
                toff, coff, T, nblk, BW, TW, nbt = dims(s)
                nlv = int(math.log2(nblk))
                UG = min(UGN, T)
                for hf in range(T // UG):
                    ug = ugs[ugc[0] % 2]; b_ug = b_ugs[ugc[0] % 2]; ugc[0] += 1
                    dma("sp" if hf % 2 == 0 else "act", ug[:, 0:UG],
                        Pscr[1920 + 16 * g:1936 + 16 * g, coff + 1 + hf * UG:coff + 1 + (hf + 1) * UG], [b_P], [b_ug])
                    cp("act", ub[st][:, :, hf * (UG // 8):(hf + 1) * (UG // 8)], ug[:, 0:UG].rearrange("p (b s) -> p s b", s=8),
                       [b_ug], [b_ub[st]])
                if nblk < NBM:
                    for d in range(2):
                        for c in range(2):
                            mk.op("pool", lambda E, t=Wt[0][d][c]: E.memset(t[:], 0.0), (), [b_Wt[0][d]])
                            mk.op("pool", lambda E, t=Wt[1][d][c]: E.memset(t[:], 0.0), (), [b_Wt[1][d]])
                for bt in range(nbt):
                    for d in range(2):
                        for c in range(2):
                            pw_, bpw = nextps()
                            for s_ in range(8):
                                mm(pw_[0:64, 0:BW], CTg[par][:, (d * 8 + s_) * 2 + c, :],
                                   ub[st][:, s_, bt * BW:(bt + 1) * BW],
                                   s_ == 0, s_ == 7, [b_CTg[par], b_ub[st]], [bpw])
                            o0 = bt * BW + (1 if d == 0 else 0)
                            cp("act", Wt[0][d][c][:, o0:o0 + BW], pw_[0:64, 0:BW], [bpw], [b_Wt[0][d]])
                cur = 0
                for k in range(nlv):
                    sh = 1 << k
                    n_ = nblk - sh
                    for d in range(2):
                        gi = d * 32 + g
                        lo = 1 if d == 0 else 0
                        Wc = Wt[cur][d]; Wn = Wt[1 - cur][d]
                        bWc = b_Wt[cur][d]; bWn = b_Wt[1 - cur][d]
                        if d == 0:
                            dst = slice(lo + sh, lo + nblk); srcs = slice(lo, lo + n_); keep = slice(lo, lo + sh)
                        else:
                            dst = slice(lo, lo + n_); srcs = slice(lo + sh, lo + nblk); keep = slice(lo + n_, lo + nblk)
                        ar = SQr[:, k, gi:gi + 1]; ai = SQi[:, k, gi:gi + 1]; ain = SQin[:, k, gi:gi + 1]
                        stt(Wn[0][:, dst], Wc[0][:, srcs], ar, Wc[0][:, dst], ALU.mult, ALU.add, [bWc, bt_], [bWn])
                        stt(Wn[1][:, dst], Wc[1][:, srcs], ar, Wc[1][:, dst], ALU.mult, ALU.add, [bWc, bt_], [bWn])
                        stt(Wn[0][:, dst], Wc[1][:, srcs], ain, Wn[0][:, dst], ALU.mult, ALU.add, [bWc, bWn, bt_], [bWn])
                        stt(Wn[1][:, dst], Wc[0][:, srcs], ai, Wn[1][:, dst], ALU.mult, ALU.add, [bWc, bWn, bt_], [bWn])
                        cp("pool", Wn[0][:, keep], Wc[0][:, keep], [bWc], [bWn])
                        cp("pool", Wn[1][:, keep], Wc[1][:, keep], [bWc], [bWn])
                    cur = 1 - cur
                for d in range(2):
                    for c in range(2):
                        if d == 0:
                            cp("act", Sb[st][d][c][:, 1:nblk + 1], Wt[cur][d][c][:, 1:nblk + 1], [b_Wt[cur][d]], [b_Sb[st]])
                            mk.op("pool", lambda E, t=Sb[st][d][c]: E.memset(t[:, 0:1], 0.0), (), [b_Sb[st]])
                        else:
                            cp("act", Sb[st][d][c][:, 0:nblk], Wt[cur][d][c][:, 0:nblk], [b_Wt[cur][d]], [b_Sb[st]])
                            mk.op("pool", lambda E, t=Sb[st][d][c], nblk=nblk: E.memset(t[:, nblk:nblk + 1], 0.0), (), [b_Sb[st]])

            def back(g, s, st):
                par = g % 2
                toff, coff, T, nblk, BW, TW, nbt = dims(s)
                for bt in range(nbt):
                    yg = ygs[ygc[0] % 2]; b_yg = b_ygs[ygc[0] % 2]; ygc[0] += 1
                    ubv = ub[st][:, :, bt * BW:(bt + 1) * BW]
                    ygv = yg[:, 0:TW].rearrange("p (b s) -> p s b", s=8)
                    for t in range(8):
                        py, bpy = nextps()
                        for s_ in range(8):
                            idx = 0 if s_ == t else ((t - s_) if s_ < t else (7 + s_ - t))
                            mm(py[0:16, 0:BW], KTg[par][:, idx, :], ubv[:, s_, :], s_ == 0, False, [b_KTg[par], b_ub[st]], [bpy])
                        b0 = bt * BW
                        mm(py[0:16, 0:BW], OTr[:, t, g, :], Sb[st][0][0][:, b0:b0 + BW], False, False, [bt_, b_Sb[st]], [bpy])
                        mm(py[0:16, 0:BW], OTn[:, t, g, :], Sb[st][0][1][:, b0:b0 + BW], False, False, [bt_, b_Sb[st]], [bpy])
                        mm(py[0:16, 0:BW], OTr[:, 7 - t, 32 + g, :], Sb[st][1][0][:, b0 + 1:b0 + 1 + BW], False, False, [bt_, b_Sb[st]], [bpy])
                        mm(py[0:16, 0:BW], OTn[:, 7 - t, 32 + g, :], Sb[st][1][1][:, b0 + 1:b0 + 1 + BW], False, True, [bt_, b_Sb[st]], [bpy])
                        cp("act", ygv[:, t, :], py[0:16, 0:BW], [bpy], [b_yg])
                    dma("sp", YS[16 * g:16 * g + 16, toff + bt * TW:toff + (bt + 1) * TW], yg[:, 0:TW], [b_yg], [b_YS])

            units = [(g, s) for g in range(32) for s in range(NS)]
            prev = None
            for ui, (g, s) in enumerate(units):
                if s == 0:
                    gconsts(g)
                front(g, s, ui % 2)
                if prev is not None:
                    back(prev[0], prev[1], (ui - 1) % 2)
                prev = (g, s)
            back(prev[0], prev[1], (len(units) - 1) % 2)
            mk.flush(final=True)

    if upto >= 2:
        s5_pass()

    def mix_pass():
        es, sb, ps = scope("mx_")
        with es:
            N = 256
            pz = [ps("pz%d" % i, [128, 512]) for i in range(8)]
            b_pz = [Buf() for _ in range(8)]
            pctr = [0]

            def nextps():
                i = pctr[0] % 8
                pctr[0] += 1
                return pz[i], b_pz[i]
            bc_ = Buf()
            o64 = sb("o64", [64, 64])
            mk.op("pool", lambda E: E.memset(o64[:], 1.0 / 64.0), (), [bc_])
            ones_bf = sb("ones_bf", [128, 128], BF16)
            mk.op("pool", lambda E: E.memset(ones_bf[:], 1.0), (), [bc_])
            lg = sb("lg", [64, 8]); lb = sb("lb", [64, 8])
            dma("sp", lg[:], lnx_g.rearrange("(h p) -> p h", p=64), (), [bc_], slow=True)
            dma("sp", lb[:], lnx_b.rearrange("(h p) -> p h", p=64), (), [bc_], slow=True)
            g2t = sb("g2t", [128, 512]); dma("sp", g2t[:], g2, (), [bc_])
            mug = sb("mug", [128, 1]); hmg = sb("hmg", [128, 1]); omg = sb("omg", [128, 1])
            dma("sp", mug[:], mu_shift[1792:1920].rearrange("(p o) -> p o", o=1), (), [bc_], slow=True)
            ts("dve", hmg[:], mug[:], 0.5, None, ALU.mult, None, [bc_], [bc_])
            ts("dve", omg[:], mug[:], -1.0, 1.0, ALU.mult, ALU.add, [bc_], [bc_])
            bgl = sb("bgl", [128, 4]); s5g = sb("s5g", [128, 4])
            dma("sp", bgl[:], b_glu.rearrange("(q p) -> p q", p=128), (), [bc_], slow=True)
            dma("sp", s5g[:], s5_out_g.rearrange("(q p) -> p q", p=128), (), [bc_], slow=True)
            wst = sb("wst", [128, 4, 128])
            mk.op("pool", lambda E: E.memset(wst[:], 0.0), (), [bc_])
            for g in range(32):
                r0 = (g % 8) * 16
                dma("sp" if g % 2 == 0 else "act", wst[r0:r0 + 16, g // 8, r0:r0 + 16], w_glu[g], [bc_], [bc_])
            Wbd = sb("Wbd", [128, 4, 128], BF16)
            cp("dve", Wbd[:], wst[:], [bc_], [bc_])
            wo_r = sb("wo_r", [64, 8, D], BF16); wo_s = sb("wo_s", [128, 4, D], BF16)
            stg = [sb("stg%d" % i, [128, D]) for i in range(2)]; b_stg = [Buf(), Buf()]
            for h in range(8):
                st_ = stg[h % 2]; bs_ = b_stg[h % 2]
                dma("sp", st_[0:64, :], w_out[h * 64:(h + 1) * 64, :], (), [bs_])
                cp("pool", wo_r[:, h, :], st_[0:64, :], [bs_], [bc_])
            for q in range(4):
                st_ = stg[q % 2]; bs_ = b_stg[q % 2]
                dma("sp", st_[:], w_out[512 + q * 128:512 + (q + 1) * 128, :], (), [bs_])
                cp("pool", wo_s[:, q, :], st_[:], [bs_], [bc_])

            def T4(name, dt=F32):
                return sb(name, [64, 8, N], dt), Buf()
            YF, b_YF = T4("YF"); YB, b_YB = T4("YB"); BF_, b_BF = T4("BF"); BB, b_BB = T4("BB")
            Ym, b_Ym = T4("Ym"); SQt, b_SQt = T4("SQt"); RS, b_RS = T4("RS")
            yr, b_yr = T4("yr", BF16)
            XG = sb("XG", [128, N + 2]); b_XG = Buf(); tw = sb("tw", [128, N]); b_tw = Buf()
            sg = sb("sg", [128, N]); b_sg = Buf()
            S5 = sb("S5", [128, 4, N]); b_S5 = Buf(); Z1 = sb("Z1", [128, 4, N]); b_Z1 = Buf()
            Z2 = sb("Z2", [128, 4, N]); b_Z2 = Buf(); Zb = sb("Zb", [128, 4, N], BF16); b_Zb = Buf()
            r2 = sb("r2", [128, N]); b_r2 = Buf()
            ysb = sb("ysb", [128, 4, N], BF16); b_ysb = Buf()
            xTt = sb("xTt", [128, 8, N]); b_xTt = Buf(); X1t = sb("X1t", [128, 8, N]); b_X1t = Buf()

            def bc(t, n):
                return t[:, :, None].broadcast_to([64, 8, n])

            def fl(t):
                return t[:].rearrange("p h t -> p (h t)")
            for s, (toff, coff, T) in enumerate(SEQ):
                for ti in range(T // N):
                    tl = ti * N; tg = toff + tl; c0 = coff + tl
                    dma("sp", YF[:], YD[0, :, :, tg:tg + N], [b_YD], [b_YF])
                    dma("act", YB[:], YD[1, :, :, tg:tg + N], [b_YD], [b_YB])
                    dma("sp", BF_[:], BD[0, :, :, tg:tg + N], [b_BD], [b_BF])
                    dma("act", BB[:], BD[1, :, :, tg:tg + N], [b_BD], [b_BB])
                    dma("sp", XG[:], Pscr[1792:1920, c0:c0 + N + 2], [b_P], [b_XG])
                    dma("act", S5[:], YS[:, tg:tg + N].rearrange("(q p) t -> p q t", p=128), [b_YS], [b_S5])
                    dma("sp", xTt[:], XT[:, tg:tg + N].rearrange("(k p) t -> p k t", p=128), [b_XT], [b_xTt])
                    tt("pool", Ym[:], YF[:], YB[:], ALU.add, [b_YF, b_YB], [b_Ym])
                    for j in range(4):
                        pm_, bpm = nextps()
                        mm(pm_[0:64, :], o64[:], fl(Ym)[:, j * 512:(j + 1) * 512], True, True, [bc_, b_Ym], [bpm])
                        tt("dve", fl(YF)[:, j * 512:(j + 1) * 512], fl(Ym)[:, j * 512:(j + 1) * 512], pm_[0:64, :],
                           ALU.subtract, [bpm, b_Ym], [b_YF])
                    tt("pool", SQt[:], YF[:], YF[:], ALU.mult, [b_YF], [b_SQt])
                    for j in range(4):
                        pm_, bpm = nextps()
                        mm(pm_[0:64, :], o64[:], fl(SQt)[:, j * 512:(j + 1) * 512], True, True, [bc_, b_SQt], [bpm])
                        act(fl(RS)[:, j * 512:(j + 1) * 512], pm_[0:64, :], AF.Sqrt, [bpm], [b_RS], bias=LNX_EPS)
                    mk.op("dve", lambda E: E.reciprocal(out=RS[:], in_=RS[:]), [b_RS], [b_RS])
                    tt("pool", Ym[:], YF[:], RS[:], ALU.mult, [b_YF, b_RS], [b_Ym])
                    tt("pool", Ym[:], Ym[:], bc(lg, N), ALU.mult, [b_Ym, bc_], [b_Ym])
                    tt("pool", Ym[:], Ym[:], bc(lb, N), ALU.add, [b_Ym, bc_], [b_Ym])
                    tt("pool", Ym[:], Ym[:], BF_[:], ALU.add, [b_Ym, b_BF], [b_Ym])
                    tt("pool", Ym[:], Ym[:], BB[:], ALU.add, [b_Ym, b_BB], [b_Ym])
                    tt("dve", tw[:], XG[:, 0:N], XG[:, 2:N + 2], ALU.add, [b_XG], [b_tw])
                    ts("dve", tw[:], tw[:], hmg[:, 0:1], None, ALU.mult, None, [b_tw, bc_], [b_tw])
                    stt(sg[:], XG[:, 1:N + 1], omg[:, 0:1], tw[:], ALU.mult, ALU.add, [b_XG, b_tw, bc_], [b_sg])
                    act(sg[:], sg[:], AF.Sigmoid, [b_sg], [b_sg])
                    for h2_ in range(4):
                        pm_, bpm = nextps()
                        for j in range(2):
                            h = 2 * h2_ + j
                            mm(pm_[0:64, j * N:(j + 1) * N], g2t[:, h * 64:(h + 1) * 64], sg[:], True, True, [bc_, b_sg], [bpm])
                        tt("dve", yr[:, 2 * h2_:2 * h2_ + 2, :], pm_[0:64, :].rearrange("p (h n) -> p h n", n=N),
                           Ym[:, 2 * h2_:2 * h2_ + 2, :], ALU.mult, [bpm, b_Ym], [b_yr])
                    K0 = 2.0 * math.sqrt(2.0 / math.pi)
                    tt("pool", Z1[:], S5[:], S5[:], ALU.mult, [b_S5], [b_Z1])
                    ts("dve", Z1[:], Z1[:], 0.044715, 1.0, ALU.mult, ALU.add, [b_Z1], [b_Z1])
                    tt("pool", Z1[:], Z1[:], S5[:], ALU.mult, [b_Z1, b_S5], [b_Z1])
                    act(Z1[:], Z1[:], AF.Sigmoid, [b_Z1], [b_Z1], scale=K0)
                    tt("pool", Z1[:], Z1[:], S5[:], ALU.mult, [b_Z1, b_S5], [b_Z1])
                    cp("dve", Zb[:], Z1[:], [b_Z1], [b_Zb])
                    for q in range(4):
                        pm_, bpm = nextps()
                        mm(pm_[:, 0:N], Wbd[:, q, :], Zb[:, q, :], True, True, [bc_, b_Zb], [bpm])
                        mk.op("act", lambda E, pm_=pm_, q=q: E.activation(out=Z2[:, q, :], in_=pm_[:, 0:N], func=AF.Sigmoid,
                                                                        bias=bgl[:, q:q + 1], scale=1.0), [bpm, bc_], [b_Z2])
                    tt("pool", Z2[:], Z2[:], Z1[:], ALU.mult, [b_Z2, b_Z1], [b_Z2])
                    tt("pool", Zb[:], Z2[:], Z2[:], ALU.mult, [b_Z2], [b_Zb])
                    pm_, bpm = nextps()
                    for q in range(4):
                        mm(pm_[:, 0:N], ones_bf[:], Zb[:, q, :], q == 0, q == 3, [bc_, b_Zb], [bpm])
                    act(r2[:], pm_[:, 0:N], AF.Sqrt, [bpm], [b_r2], bias=RMS_EPS, scale=1.0 / 512.0)
                    mk.op("dve", lambda E: E.reciprocal(out=r2[:], in_=r2[:]), [b_r2], [b_r2])
                    for q in range(4):
                        stt(ysb[:, q, :], Z2[:, q, :], s5g[:, q:q + 1], r2[:], ALU.mult, ALU.mult, [b_Z2, b_r2, bc_], [b_ysb])
                    for dm in range(8):
                        pm_, bpm = nextps()
                        for h in range(8):
                            mm(pm_[:, 0:N], wo_r[:, h, dm * 128:(dm + 1) * 128], yr[:, h, :], h == 0, False, [bc_, b_yr], [bpm])
                        for q in range(4):
                            mm(pm_[:, 0:N], wo_s[:, q, dm * 128:(dm + 1) * 128], ysb[:, q, :], False, q == 3, [bc_, b_ysb], [bpm])
                        stt(X1t[:, dm, :], pm_[:, 0:N], modT[:, 16 + dm, s:s + 1], xTt[:, dm, :], ALU.mult, ALU.add,
                            [bpm, b_mod, b_xTt], [b_X1t])
                    dma("sp", X1[:, tg:tg + N].rearrange("(k p) t -> p k t", p=128), X1t[:], [b_X1t], [b_X1])
            mk.flush(final=True)

    if upto >= 3:
        mix_pass()

    def ffn_pass():
        es, sb, ps = scope("ff_")
        with es:
            N = 256
            pz = [ps("pz%d" % i, [128, 512]) for i in range(8)]
            b_pz = [Buf() for _ in range(8)]
            pctr = [0]

            def nextps():
                i = pctr[0] % 8
                pctr[0] += 1
                return pz[i], b_pz[i]
            bc_ = Buf()
            ones_bf = sb("ones_bf", [128, 128], BF16)
            mk.op("pool", lambda E: E.memset(ones_bf[:], 1.0), (), [bc_])
            n2g = sb("n2g", [128, 8]); fg = sb("fg", [128, 8]); sc2 = sb("sc2", [128, 8, NS])
            dma("sp", n2g[:], norm2_g.rearrange("(k p) -> p k", p=128), (), [bc_], slow=True)
            dma("sp", fg[:], final_g.rearrange("(k p) -> p k", p=128), (), [bc_], slow=True)
            for s in range(NS):
                stt(sc2[:, :, s], modT[:, 32:40, s], 1.0, n2g[:], ALU.add, ALU.mult, [b_mod, bc_], [bc_])
            w1 = sb("w1", [128, 8, DFF], BF16); w3 = sb("w3", [128, 8, DFF], BF16); w2_ = sb("w2_", [128, 22, D], BF16)
            stg = [sb("stg%d" % i, [128, 1408]) for i in range(2)]; b_stg = [Buf(), Buf()]
            n = 0
            for (src, dst) in ((w_ff1, w1), (w_ff3, w3)):
                for k in range(8):
                    for hf in range(2):
                        st_ = stg[n % 2]; bs_ = b_stg[n % 2]; n += 1
                        dma("sp" if n % 2 else "act", st_[:], src[k * 128:(k + 1) * 128, hf * 1408:(hf + 1) * 1408], (), [bs_])
                        cp("pool" if n % 2 else "dve", dst[:, k, hf * 1408:(hf + 1) * 1408], st_[:], [bs_], [bc_])
            for k in range(22):
                st_ = stg[n % 2]; bs_ = b_stg[n % 2]; n += 1
                dma("sp" if n % 2 else "act", st_[:, 0:D], w_ff2[k * 128:(k + 1) * 128, :], (), [bs_])
                cp("pool" if n % 2 else "dve", w2_[:, k, :], st_[:, 0:D], [bs_], [bc_])
            FS = []
            for i_ in range(2):
                FS.append(dict(X1t=sb("X1t%d" % i_, [128, 8, N]), b_X1t=Buf(), h2=sb("h2%d" % i_, [128, 8, N], BF16), b_h2=Buf()))
            sq = sb("sq", [128, 8, N], BF16); b_sq = Buf()
            rstd = sb("rstd", [128, N]); b_rstd = Buf(); tmp = sb("tmp", [128, N]); b_tmp = Buf()
            sq2, b_sq2 = sq, b_sq; rstd2 = sb("rstd2", [128, N]); b_rstd2 = Buf()
            fm = sb("fm", [128, 22, N], BF16); b_fm = Buf()
            av = [sb("av%d" % i, [128, N]) for i in range(2)]; b_av = [Buf(), Buf()]
            X2t = sb("X2t", [128, 8, N]); b_X2t = Buf()
            ytm = sb("ytm", [128, 2, D]); b_ytm = Buf()

            def rms_bc(src, bsrc, sq_, bsq_, rs_, brs_):
                for dc in range(8):
                    act(sq_[:, dc, :], src[:, dc, :], AF.Square, [bsrc], [bsq_])
                pm_, bpm = nextps()
                for dc in range(8):
                    mm(pm_[:, 0:N], ones_bf[:], sq_[:, dc, :], dc == 0, dc == 7, [bc_, bsq_], [bpm])
                act(rs_[:], pm_[:, 0:N], AF.Sqrt, [bpm], [brs_], bias=RMS_EPS, scale=1.0 / D)
                mk.op("dve", lambda E: E.reciprocal(out=rs_[:], in_=rs_[:]), [brs_], [brs_])

            tiles = [(s_, tg_) for s_, (toff, coff, T) in enumerate(SEQ) for tg_ in range(toff, toff + T, N)]

            def ffront(s, tg, F_):
                X1t, b_X1t, h2, b_h2 = F_["X1t"], F_["b_X1t"], F_["h2"], F_["b_h2"]
                dma("sp", X1t[:], X1[:, tg:tg + N].rearrange("(k p) t -> p k t", p=128), [b_X1], [b_X1t])
                rms_bc(X1t, b_X1t, sq, b_sq, rstd, b_rstd)
                for dc in range(8):
                    tt("dve", tmp[:], X1t[:, dc, :], rstd[:], ALU.mult, [b_X1t, b_rstd], [b_tmp])
                    ts("dve", h2[:, dc, :], tmp[:], sc2[:, dc, s:s + 1], modT[:, 24 + dc, s:s + 1], ALU.mult, ALU.add,
                       [b_tmp, bc_, b_mod], [b_h2])

            def fback(s, tg, F_, pend):
                X1t, b_X1t, h2, b_h2 = F_["X1t"], F_["b_X1t"], F_["h2"], F_["b_h2"]
                for mc in range(22):
                    p1, bp1 = nextps()
                    for dc in range(8):
                        mm(p1[:, 0:N], w1[:, dc, mc * 128:(mc + 1) * 128], h2[:, dc, :], dc == 0, dc == 7, [bc_, b_h2], [bp1])
                    p3, bp3 = nextps()
                    for dc in range(8):
                        mm(p3[:, 0:N], w3[:, dc, mc * 128:(mc + 1) * 128], h2[:, dc, :], dc == 0, dc == 7, [bc_, b_h2], [bp3])
                    a_ = av[mc % 2]; ba_ = b_av[mc % 2]
                    act(a_[:], p1[:, 0:N], AF.Silu, [bp1], [ba_])
                    tt("dve", fm[:, mc, :], a_[:], p3[:, 0:N], ALU.mult, [ba_, bp3], [b_fm])
                    mk.replay(pend, 2)
                mk.replay(pend, len(pend))
                for dm in range(8):
                    pm_, bpm = nextps()
                    for mc in range(22):
                        mm(pm_[:, 0:N], w2_[:, mc, dm * 128:(dm + 1) * 128], fm[:, mc, :], mc == 0, mc == 21, [bc_, b_fm], [bpm])
                    stt(X2t[:, dm, :], pm_[:, 0:N], modT[:, 40 + dm, s:s + 1], X1t[:, dm, :], ALU.mult, ALU.add,
                        [bpm, b_mod, b_X1t], [b_X2t])
                rms_bc(X2t, b_X2t, sq2, b_sq2, rstd2, b_rstd2)
                for dc in range(8):
                    stt(X2t[:, dc, :], X2t[:, dc, :], fg[:, dc:dc + 1], rstd2[:], ALU.mult, ALU.mult,
                        [b_X2t, bc_, b_rstd2], [b_X2t])
                for j in range(N // 128):
                    for dq in range(2):
                        pm_, bpm = nextps()
                        for k in range(4):
                            dc = dq * 4 + k
                            mk.op("pe", lambda E, pm_=pm_, dc=dc, j=j, k=k: E.transpose(
                                out=pm_[:, k * 128:(k + 1) * 128], in_=X2t[:, dc, j * 128:(j + 1) * 128], identity=ident[:]),
                                [b_X2t, b_ident], [bpm])
                        cp("act" if dq == 0 else "dve", ytm[:, j, dq * 512:(dq + 1) * 512], pm_[:], [bpm], [b_ytm])
                dma("sp", y_out[tg:tg + N, :].rearrange("(j p) d -> p j d", p=128), ytm[:], [b_ytm], [])

            ffront(tiles[0][0], tiles[0][1], FS[0])
            for i_, (s_, tg_) in enumerate(tiles):
                pend = []
                if i_ + 1 < len(tiles):
                    mk.deferred = pend
                    ffront(tiles[i_ + 1][0], tiles[i_ + 1][1], FS[(i_ + 1) % 2])
                    mk.deferred = None
                fback(s_, tg_, FS[i_ % 2], pend)
            mk.flush(final=True)

    if upto >= 4:
        ffn_pass()

    outer.close()
    return nc


def core_inputs(P, x, c):
    f = np.float32
    m = {"x": x, "c": c}
    for k in ("norm1_g", "w_ada", "b_ada", "w_in", "mu_shift", "w0", "w2", "a0", "a2", "g2", "k_k", "k_a",
              "lnx_g", "lnx_b", "lam_re", "lam_im", "log_dt", "b_re", "b_im", "c_re", "c_im", "w_glu",
              "s5_out_g", "w_out", "norm2_g", "w_ff1", "w_ff3", "w_ff2", "final_g"):
        m[k] = P[k]
    m["r_k"] = P["r_k"].reshape(512)
    m["d_skip"] = P["d_skip"].reshape(512)
    m["b_glu"] = P["b_glu"].reshape(512)
    return {k: np.ascontiguousarray(v, dtype=f) for k, v in m.items()}


_T_PROMPT = 8192
_T_SAMPLE = 4096


def kernel(**inputs):
    n = 8
    P = {}
    for k, v in inputs.items():
        if k in ("x_prompt", "x_sample", "c_prompt", "c_sample"):
            continue
        v = np.asarray(v)
        P[k] = v if k == "final_g" else v[0]
    xp = np.asarray(inputs["x_prompt"]); xs = np.asarray(inputs["x_sample"])
    cpr = np.asarray(inputs["c_prompt"]); cs = np.asarray(inputs["c_sample"])
    TS = [xp.shape[1], xs.shape[1]]
    nc = build_program(TS, upto=int(os.environ.get('KUPTO', '99')))
    in_maps = []
    for b in range(n):
        x = np.concatenate([xp[b], xs[b]], axis=0)
        c = np.stack([cpr[b], cs[b]], axis=0)
        in_maps.append(core_inputs(P, x, c))
    res = run_bass_kernel_spmd(nc, in_maps, core_ids=list(range(n)))
    yp = np.stack([res.results[b]["y"][:TS[0]] for b in range(n)], axis=0).astype(np.float32)
    ys = np.stack([res.results[b]["y"][TS[0]:] for b in range(n)], axis=0).astype(np.float32)
    return (yp, ys)
```

```python
import os
import math
import numpy as np
import concourse.bass as bass
import concourse.mybir as mybir
from concourse.bass_utils import run_bass_kernel_spmd

F32 = mybir.dt.float32
BF16 = mybir.dt.bfloat16
AF = mybir.ActivationFunctionType
ALU = mybir.AluOpType
AX = mybir.AxisListType

D = 1024
DFF = 2816
NPROJ = 2432
RW = 1920
NDS = 12
RMS_EPS = 1e-6
LNX_EPS = 64e-5


class Buf:
    __slots__ = ("w", "r")

    def __init__(self):
        self.w = None
        self.r = {}


class MK:
    BLK = {"pe": "tensor", "dve": "vector", "act": "scalar", "pool": "gpsimd", "sp": "sync"}

    def __init__(self, nc, same=True):
        self.nc = nc
        self.same = same
        self.names = ["pe", "dve", "act", "pool", "sp"]
        self.sem = {k: nc.alloc_semaphore(name="s_" + k) for k in self.names}
        self.cnt = {k: 0 for k in self.names}
        self.seen = {k: {} for k in self.names}
        self.prog = {k: [] for k in self.names}
        self.dsem = [nc.alloc_semaphore(name="d%d" % i) for i in range(NDS)]
        self.dcnt = [0] * NDS
        self.dnext = 0
        self.deferred = None

    def semof(self, key):
        if isinstance(key, tuple):
            return self.dsem[key[1]]
        return self.sem[key]

    def _deps(self, e, reads, writes):
        deps = {}

        def add(k, v):
            if deps.get(k, 0) < v:
                deps[k] = v

        for b in reads:
            if b.w:
                add(*b.w)
        for b in writes:
            if b.w:
                add(*b.w)
            for k, v in b.r.items():
                add(k, v)
        out = []
        for k, v in deps.items():
            if k == e and (e == "pe" or not self.same):
                continue
            if self.seen[e].get(k, 0) >= v:
                continue
            self.seen[e][k] = v
            out.append((k, v))
        return out

    def _mark(self, tok, reads, writes):
        k, v = tok
        for b in reads:
            if b.r.get(k, 0) < v:
                b.r[k] = v
        for b in writes:
            b.w = tok
            b.r = {}

    def op(self, e, fn, reads=(), writes=()):
        if self.deferred is not None:
            self.deferred.append((0, e, fn, reads, writes))
            return
        waits = self._deps(e, reads, writes)
        self.cnt[e] += 1
        tok = (e, self.cnt[e])
        self.prog[e].append((waits, fn, self.sem[e], 1))
        self._mark(tok, reads, writes)

    def replay(self, pending, n):
        keep = self.deferred
        self.deferred = None
        last = None
        cnt = 0
        while pending and (cnt < n or last == "pe"):
            kind, e, fn, reads, writes = pending.pop(0)
            (self.dma if kind else self.op)(e, fn, reads, writes)
            last = e if not kind else None
            cnt += 1
        self.deferred = keep

    def dma(self, q, fn, reads=(), writes=()):
        if self.deferred is not None:
            self.deferred.append((1, q, fn, reads, writes))
            return
        i = self.dnext
        self.dnext = (i + 1) % NDS
        key = ("d", i)
        waits = self._deps(q, reads, writes)
        if self.dcnt[i] > 0 and self.seen[q].get(key, 0) < self.dcnt[i]:
            waits.append((key, self.dcnt[i]))
            self.seen[q][key] = self.dcnt[i]
        self.dcnt[i] += 16
        tok = (key, self.dcnt[i])
        self.prog[q].append((waits, fn, self.dsem[i], 16))
        self._mark(tok, reads, writes)

    def flush(self, final=False):
        nc = self.nc
        fin = []
        for i in (range(NDS) if final else []):
            if self.dcnt[i] > 0:
                fin.append((("d", i), self.dcnt[i]))
        for k in (self.names if final else []):
            if k != "sp" and self.cnt[k] > 0:
                fin.append((k, self.cnt[k]))
        with nc.Block() as block:
            for e in self.names:
                prog = self.prog[e]
                extra = fin if e == "sp" else []

                def body(eng, prog=prog, extra=extra):
                    for waits, fn, sem, inc in prog:
                        for k, v in waits:
                            eng.wait_ge(self.semof(k), v)
                        fn(eng).then_inc(sem, inc)
                    for k, v in extra:
                        eng.wait_ge(self.semof(k), v)

                getattr(block, self.BLK[e])(body)
        self.prog = {k: [] for k in self.names}

    def emit(self):
        self.flush(final=True)


def build_program(TS, dbg=False, upto=99):
    import contextlib
    nc = bass.Bass("TRN2", target_bir_lowering=False)
    mk = MK(nc, same=(os.environ.get("MK_SAME", "1") == "1"))
    TT = sum(TS)
    NS = len(TS)
    WP = TT + 2 * NS
    SEQ = []
    o = 0
    for s, T in enumerate(TS):
        SEQ.append((o, o + 2 * s, T))
        o += T

    def din(name, shape):
        return nc.dram_tensor(name, list(shape), F32, kind="ExternalInput").ap()

    def dscr(name, shape):
        return nc.dram_tensor(name, list(shape), F32, kind=("ExternalOutput" if dbg else "Internal")).ap()

    x_in = din("x", (TT, D))
    c_in = din("c", (NS, D))
    norm1_g = din("norm1_g", (D,))
    w_ada = din("w_ada", (D, 6 * D))
    b_ada = din("b_ada", (6 * D,))
    w_in = din("w_in", (D, NPROJ))
    mu_shift = din("mu_shift", (RW,))
    w0 = din("w0", (2, 512)); w2 = din("w2", (2, 64, 512))
    a0 = din("a0", (2, 512)); a2 = din("a2", (2, 64, 512))
    g2 = din("g2", (128, 512))
    k_k = din("k_k", (512,)); k_a = din("k_a", (512,)); r_k = din("r_k", (512,))
    lnx_g = din("lnx_g", (512,)); lnx_b = din("lnx_b", (512,))
    lam_re = din("lam_re", (2, 32, 64)); lam_im = din("lam_im", (2, 32, 64)); log_dt = din("log_dt", (2, 32))
    b_re = din("b_re", (2, 32, 64, 16)); b_im = din("b_im", (2, 32, 64, 16))
    c_re = din("c_re", (2, 32, 16, 64)); c_im = din("c_im", (2, 32, 16, 64))
    d_skip = din("d_skip", (512,)); w_glu = din("w_glu", (32, 16, 16)); b_glu = din("b_glu", (512,))
    s5_out_g = din("s5_out_g", (512,))
    w_out = din("w_out", (D, D)); norm2_g = din("norm2_g", (D,))
    w_ff1 = din("w_ff1", (D, DFF)); w_ff3 = din("w_ff3", (D, DFF)); w_ff2 = din("w_ff2", (DFF, D))
    final_g = din("final_g", (D,))
    y_out = nc.dram_tensor("y", [TT, D], F32, kind="ExternalOutput").ap()

    Pscr = dscr("Pscr", (NPROJ, WP))
    XT = dscr("XT", (D, TT))
    YD = dscr("YD", (2, 64, 8, TT))
    BD = dscr("BD", (2, 64, 8, TT))
    YS = dscr("YS", (512, TT))
    X1 = dscr("X1", (D, TT))
    MODS = dscr("MODS", (128, 48 * NS))
    b_P = Buf(); b_XT = Buf(); b_YD = Buf(); b_BD = Buf(); b_YS = Buf(); b_X1 = Buf(); b_MODS = Buf()

    def tt(e, out, a, b, op, r, w):
        mk.op(e, lambda E: E.tensor_tensor(out=out, in0=a, in1=b, op=op), r, w)

    def ts(e, out, a, s1, s2, op0, op1, r, w):
        if op1 is None:
            mk.op(e, lambda E: E.tensor_scalar(out=out, in0=a, scalar1=s1, scalar2=None, op0=op0), r, w)
        else:
            mk.op(e, lambda E: E.tensor_scalar(out=out, in0=a, scalar1=s1, scalar2=s2, op0=op0, op1=op1), r, w)

    def stt(out, a, sc, b, op0, op1, r, w):
        mk.op("dve", lambda E: E.scalar_tensor_tensor(out=out, in0=a, scalar=sc, in1=b, op0=op0, op1=op1), r, w)

    def act(out, a, func, r, w, bias=0.0, scale=1.0):
        mk.op("act", lambda E: E.activation(out=out, in_=a, func=func, bias=bias, scale=scale), r, w)

    def cp(e, out, a, r, w):
        if e == "act":
            mk.op("act", lambda E: E.activation(out=out, in_=a, func=AF.Copy), r, w)
        else:
            mk.op(e, lambda E: E.tensor_copy(out=out, in_=a), r, w)

    def mm(out, lhsT, rhs, st, sp_, r, w):
        mk.op("pe", lambda E: E.matmul(out=out, lhsT=lhsT, rhs=rhs, start=st, stop=sp_), r, w)

    F32R = mybir.dt.float32r
    USE_R = os.environ.get("RW_F32R", "1") == "1"

    def RR(ap):
        return ap.bitcast(F32R) if USE_R else ap

    def mmr(out, lhsT, rhs, r, w, st=True, sp_=True):
        mk.op("pe", lambda E: E.matmul(out=out, lhsT=lhsT.bitcast(F32R), rhs=rhs.bitcast(F32R), start=st, stop=sp_), r, w)

    def dma(q, out, in_, r, w, slow=False):
        if slow:
            mk.dma(q, lambda E: E.dma_start(out=out, in_=in_, allow_slow_non_contiguous=True), r, w)
        else:
            mk.dma(q, lambda E: E.dma_start(out=out, in_=in_), r, w)

    def scope(pfx=""):
        es = contextlib.ExitStack()

        def sb(name, shape, dt=F32):
            return es.enter_context(nc.sbuf_tensor(pfx + name, list(shape), dt))

        def ps(name, shape, dt=F32):
            return es.enter_context(nc.psum_tensor(pfx + name, list(shape), dt))
        return es, sb, ps

    def consts(sb):
        ident = sb("ident", [128, 128]); b_ident = Buf()
        mk.op("pool", lambda E: E.memset(ident[:], 1.0), (), [b_ident])
        mk.op("pool", lambda E: E.affine_select(out=ident[:], in_=ident[:], pattern=[[-1, 128]],
                                                compare_op=ALU.is_equal, fill=0.0, base=0, channel_multiplier=1),
              [b_ident], [b_ident])
        return ident, b_ident
    outer, osb, ops_ = scope("o_")
    ident, b_ident = consts(osb)
    modT = osb("modT", [128, 48, NS]); b_mod = Buf()
    sc1 = osb("sc1", [128, 8, NS]); b_sc1 = Buf()

    def pass0():
        es, sb, ps = scope("p0_")
        with es:
            ones_bf = sb("ones_bf", [128, 128], BF16); b_ones = Buf()
            mk.op("pool", lambda E: E.memset(ones_bf[:], 1.0), (), [b_ones])
            cT = sb("cT", [128, 8, NS]); b_cT = Buf()
            scT = sb("scT", [128, 8, NS]); b_scT = Buf()
            for s in range(NS):
                dma("sp", cT[:, :, s], c_in[s].rearrange("(k p) -> p k", p=128), (), [b_cT], slow=True)
            act(scT[:], cT[:], AF.Silu, [b_cT], [b_scT])
            badaT = sb("badaT", [128, 48]); b_bada = Buf()
            dma("sp", badaT[:], b_ada.rearrange("(k p) -> p k", p=128), (), [b_bada], slow=True)
            g1T = sb("g1T", [128, 8]); b_g1 = Buf()
            dma("sp", g1T[:], norm1_g.rearrange("(k p) -> p k", p=128), (), [b_g1], slow=True)
            wada_t = [sb("wada%d" % i, [128, 8, 256]) for i in range(2)]
            b_wada = [Buf(), Buf()]
            ps_mod_full = ps("ps_mod", [128, 512]); b_psmod = Buf()
            ps_mod = ps_mod_full[:, 0:4 * NS].rearrange("p (a b) -> p a b", b=NS)
            for slab in range(24):
                wt = wada_t[slab % 2]; bw = b_wada[slab % 2]
                dma("sp" if slab % 2 == 0 else "act", wt[:],
                    w_ada[:, slab * 256:(slab + 1) * 256].rearrange("(k p) n -> p k n", p=128), (), [bw])
                for j in range(2):
                    for k in range(8):
                        mm(ps_mod[:, j, :], wt[:, k, j * 128:(j + 1) * 128], scT[:, k, :], k == 0, k == 7,
                           [bw, b_scT], [b_psmod])
                for s in range(NS):
                    tt("dve", modT[:, slab * 2:(slab + 1) * 2, s], ps_mod[:, 0:2, s],
                       badaT[:, slab * 2:(slab + 1) * 2], ALU.add, [b_psmod, b_bada], [b_mod])
            for s in range(NS):
                stt(sc1[:, :, s], modT[:, 8:16, s], 1.0, g1T[:], ALU.add, ALU.mult, [b_mod, b_g1], [b_sc1])

            w_in_bf = sb("w_in_bf", [128, 8, NPROJ], BF16); b_win = Buf()
            wst = [sb("wst%d" % i, [128, NPROJ]) for i in range(2)]; b_wst = [Buf(), Buf()]
            for k in range(8):
                dma("sp", wst[k % 2][:], w_in[k * 128:(k + 1) * 128, :], (), [b_wst[k % 2]])
                cp("pool", w_in_bf[:, k, :], wst[k % 2][:], [b_wst[k % 2]], [b_win])

            NT = 512
            xtm = [sb("xtm%d" % i, [128, 4, D]) for i in range(2)]; b_xtm = [Buf(), Buf()]
            xT = sb("xT", [128, 8, NT]); b_xT = Buf()
            sq = sb("sq", [128, 8, NT], BF16); b_sq = Buf()
            rstd = sb("rstd", [128, NT]); b_rstd = Buf()
            tmp = sb("tmp0", [128, NT]); b_tmp = Buf()
            hT = sb("hT", [128, 8, NT], BF16); b_hT = Buf()
            pev = [sb("pev%d" % i, [128, NT]) for i in range(3)]; b_pev = [Buf() for _ in range(3)]
            zcol = sb("zcol", [128, 1]); b_zcol = Buf()
            mk.op("pool", lambda E: E.memset(zcol[:], 0.0), (), [b_zcol])
            pst = [ps("pst%d" % i, [128, NT]) for i in range(4)]; b_pst = [Buf() for _ in range(4)]
            psm = [ps("psm%d" % i, [128, NT]) for i in range(3)]; b_psm = [Buf() for _ in range(3)]
            for s, (toff, coff, T) in enumerate(SEQ):
                for mc in range(19):
                    for cc in (coff, coff + T + 1):
                        dma("sp", Pscr[mc * 128:(mc + 1) * 128, cc:cc + 1], zcol[:], [b_zcol], [b_P], slow=True)
                for ti in range(T // NT):
                    t0 = toff + ti * NT
                    xt = xtm[ti % 2]; bx = b_xtm[ti % 2]
                    dma("sp", xt[:], x_in[t0:t0 + NT, :].rearrange("(j p) d -> p j d", p=128), (), [bx])
                    for dc in range(8):
                        pt = pst[dc % 4]; bp = b_pst[dc % 4]
                        for j in range(4):
                            mk.op("pe", lambda E, pt=pt, xt=xt, j=j, dc=dc: E.transpose(
                                out=pt[:, j * 128:(j + 1) * 128], in_=xt[:, j, dc * 128:(dc + 1) * 128],
                                identity=ident[:]), [bx, b_ident], [bp])
                        cp("dve", xT[:, dc, :], pt[:], [bp], [b_xT])
                        act(sq[:, dc, :], pt[:], AF.Square, [bp, b_xT], [b_sq])
                    dma("act", XT[:, t0:t0 + NT].rearrange("(k p) t -> p k t", p=128), xT[:], [b_xT], [b_XT])
                    pm = psm[0]; bpm = b_psm[0]
                    for dc in range(8):
                        mm(pm[:], ones_bf[:], sq[:, dc, :], dc == 0, dc == 7, [b_sq, b_ones], [bpm])
                    act(rstd[:], pm[:], AF.Sqrt, [bpm], [b_rstd], bias=RMS_EPS, scale=1.0 / D)
                    mk.op("dve", lambda E: E.reciprocal(out=rstd[:], in_=rstd[:]), [b_rstd], [b_rstd])
                    for dc in range(8):
                        tt("dve", tmp[:], xT[:, dc, :], rstd[:], ALU.mult, [b_xT, b_rstd], [b_tmp])
                        ts("dve", hT[:, dc, :], tmp[:], sc1[:, dc, s:s + 1], modT[:, dc, s:s + 1], ALU.mult, ALU.add,
                           [b_tmp, b_sc1, b_mod], [b_hT])
                    for mc in range(19):
                        i3 = mc % 3
                        pm = psm[i3]; bpm = b_psm[i3]
                        for dc in range(8):
                            mm(pm[:], w_in_bf[:, dc, mc * 128:(mc + 1) * 128], hT[:, dc, :], dc == 0, dc == 7,
                               [b_win, b_hT], [bpm])
                        pv = pev[i3]; bpv = b_pev[i3]
                        cp("act", pv[:], pm[:], [bpm], [bpv])
                        cc = coff + 1 + ti * NT
                        dma("sp", Pscr[mc * 128:(mc + 1) * 128, cc:cc + NT], pv[:], [bpv], [b_P])
            mk.flush(final=True)

    pass0()
    def rwkv_pass(d):
        rev = (d == 1)
        es, sb, ps = scope("rw%d_" % d)
        with es:
            NT2 = 128
            psr = [ps("psr%d" % i, [128, 1024]) for i in range(4)]
            b_psr = [Buf() for _ in range(4)]
            pctr = [0]

            def nextps():
                i = pctr[0] % 4
                pctr[0] += 1
                return psr[i], b_psr[i]

            def T4(name):
                return sb(name, [64, 8, NT2]), Buf()

            def ldp(name, src512):
                t = sb(name, [64, 8]); b = Buf()
                dma("sp", t[:], src512.rearrange("(h p) -> p h", p=64), (), [b], slow=True)
                return t, b

            mu3 = sb("mu3", [64, 24]); b_mu3 = Buf()
            dma("sp", mu3[:], mu_shift[0:1536].rearrange("(g p) -> p g", p=64), (), [b_mu3], slow=True)
            hm3 = sb("hm3", [64, 24]); om3 = sb("om3", [64, 24]); b_hm3 = Buf()
            ts("dve", hm3[:], mu3[:], 0.5, None, ALU.mult, None, [b_mu3], [b_hm3])
            ts("dve", om3[:], mu3[:], -1.0, 1.0, ALU.mult, ALU.add, [b_mu3], [b_hm3])
            muw = sb("muw", [64, 2]); b_muw = Buf()
            dma("sp", muw[:, 0:1], mu_shift[1536 + 64 * d:1600 + 64 * d].rearrange("(p o) -> p o", o=1), (), [b_muw], slow=True)
            dma("sp", muw[:, 1:2], mu_shift[1664 + 64 * d:1728 + 64 * d].rearrange("(p o) -> p o", o=1), (), [b_muw], slow=True)
            hmw = sb("hmw", [64, 2]); omw = sb("omw", [64, 2]); b_hmw = Buf()
            ts("dve", hmw[:], muw[:], 0.5, None, ALU.mult, None, [b_muw], [b_hmw])
            ts("dve", omw[:], muw[:], -1.0, 1.0, ALU.mult, ALU.add, [b_muw], [b_hmw])
            w0d, b_w0d = ldp("w0d", w0[d]); a0d, b_a0d = ldp("a0d", a0[d])
            kk_, b_kk_ = ldp("kk_", k_k); ka_, b_ka_ = ldp("ka_", k_a); rk_, b_rk_ = ldp("rk_", r_k)
            omka = sb("omka", [64, 8]); b_omka = Buf()
            ts("dve", omka[:], ka_[:], -1.0, 1.0, ALU.mult, ALU.add, [b_ka_], [b_omka])
            w2d = sb("w2d", [64, 512]); a2d = sb("a2d", [64, 512]); b_w2d = Buf()
            dma("sp", w2d[:], w2[d], (), [b_w2d]); dma("sp", a2d[:], a2[d], (), [b_w2d])
            ones64 = sb("ones64", [64, 64]); b_c = Buf()
            mk.op("pool", lambda E: E.memset(ones64[:], 1.0), (), [b_c])
            maskA = sb("maskA", [64, 128]); maskL = sb("maskL", [64, 64]); MS = sb("MS", [64, 8 * NT2])
            mk.op("pool", lambda E: E.memset(maskA[:], 1.0), (), [b_c])
            mk.op("pool", lambda E: E.memset(maskL[:], 1.0), (), [b_c])
            mk.op("pool", lambda E: E.memset(MS[:], 1.0), (), [b_c])
            zc_ = 63 if rev else 0
            mk.op("pool", lambda E: E.memset(MS[:].rearrange("p (a l) -> p a l", l=64)[:, :, zc_:zc_ + 1], 0.0), [b_c], [b_c])

            def asel(ap, upper, strict):
                pat = [[1, 64]] if upper else [[-1, 64]]
                cm = -1 if upper else 1
                mk.op("pool", lambda E: E.affine_select(out=ap, in_=ap, pattern=pat, compare_op=ALU.is_ge, fill=0.0,
                                                        base=(-1 if strict else 0), channel_multiplier=cm),
                      [b_c], [b_c])
            asel(maskA[:, 0:64], not rev, True)
            asel(maskA[:, 64:128], not rev, False)
            asel(maskL[:], rev, True)
            mA = maskA[:, None, :].broadcast_to([64, 8, 128])
            mL = maskL[:, None, :].broadcast_to([64, 8, 64])
            id64 = ident[0:64, 0:64]
            idbc = ident[0:64, None, 0:64].broadcast_to([64, 8, 64])

            Lr = [sb("Lq%d" % q, [64, 8, NT2 + 2]) for q in range(2)]; b_L = [Buf() for _ in range(2)]
            Lr.append(Lr[0]); b_L.append(b_L[0])
            XW = sb("XW", [64, NT2 + 2]); XA = sb("XA", [64, NT2 + 2]); b_XW = Buf(); b_XA = Buf()
            T1, b_T1 = T4("T1")
            SH = [T4("SH%d" % q) for q in range(2)]
            (Rp, b_Rp), (Kp, b_Kp) = SH
            tt0, cp0 = tt, cp
            tw = sb("tw", [64, NT2]); b_tw = Buf()
            xwp = sb("xwp", [64, NT2]); xap = sb("xap", [64, NT2]); b_xwp = Buf(); b_xap = Buf()
            XB, b_XB = T4("XB"); E2, b_E2 = T4("E2"); AD, b_AD = T4("AD"); KR, b_KR = T4("KR")
            SS, b_SS = T4("SS"); KD, b_KD = T4("KD"); AB, b_AB = T4("AB"); BON, b_BON = XB, b_XB
            G, b_G = T1, b_T1; D1, b_D1 = E2, b_E2; D2, b_D2 = AD, b_AD; EP, b_EP = XB, b_XB; EN, b_EN = SS, b_SS
            T2, b_T2 = SS, b_SS
            SD = F32
            SETS = []
            for i_ in range(2):
                st_ = []
                for nm, shp in (("AR", [64, 8, 2, 128]), ("KT", [64, 9, NT2]), ("BT", [64, 9, NT2]), ("KH", [64, 8, NT2]),
                                ("BH", [64, 8, NT2]), ("Vp", [64, 8, NT2]), ("GL", [64, 16])):
                    st_ += [sb("%s_%d" % (nm, i_), shp), Buf()]
                SETS.append(st_)
            YT, b_YT = T4("YT")
            MT1 = sb("MT1", [64, 2, 8, 128], SD); MT2 = sb("MT2", [64, 2, 8, 128], SD); b_MT1 = Buf(); b_MT2 = Buf()
            P0 = sb("P0", [64, 17, 64], SD); b_P0 = Buf()
            PP = [sb("PP%d" % i, [64, 33, 64], SD) for i in range(2)]; b_PP = [Buf(), Buf()]
            Zt = [sb("Zt%d" % i, [64, 17, 128], SD) for i in range(2)]; b_Zt = [Buf() for _ in range(2)]
            VT = sb("VT", [64, 17, 64], SD); BHt = sb("BHt", [64, 17, 64], SD); KHt = sb("KHt", [64, 17, 64], SD)
            QT = sb("QT", [64, 2, 8, 64]); MM = sb("MM", [64, 17, 64]); DG = sb("DG", [64, 17, 64])
            b_VT = Buf(); b_BHt = Buf(); b_KHt = Buf(); b_QT = Buf(); b_MM = Buf(); b_DG = Buf()
            STt = [sb("ST%d" % i, [64, 9, 64]) for i in range(2)]; b_ST = [Buf(), Buf()]
            for t_, b__, r_ in ((Zt[0], b_Zt[0], 16), (Zt[1], b_Zt[1], 16)):
                ts("dve", RR(t_[:, r_, :]), maskA[:], 0.0, None, ALU.mult, None, [b_c], [b__])
            for t_, b__ in ((VT, b_VT), (BHt, b_BHt), (KHt, b_KHt), (MM, b_MM)):
                ts("dve", RR(t_[:, 16, :]), ones64[:], 0.0, None, ALU.mult, None, [b_c], [b__])
            for i_ in range(2):
                ts("dve", RR(STt[i_][:, 8, :]), ones64[:], 0.0, None, ALU.mult, None, [b_c], [b_ST[i_]])
                ts("dve", RR(SETS[i_][2][:, 8, :]), maskA[:], 0.0, None, ALU.mult, None, [b_c], [SETS[i_][3]])
                ts("dve", RR(SETS[i_][4][:, 8, :]), maskA[:], 0.0, None, ALU.mult, None, [b_c], [SETS[i_][5]])
            ts("dve", RR(P0[:, 16, :]), ones64[:], 0.0, None, ALU.mult, None, [b_c], [b_P0])
            for i_ in range(2):
                ts("dve", RR(PP[i_][:, 32, :]), ones64[:], 0.0, None, ALU.mult, None, [b_c], [b_PP[i_]])
            mA16 = maskA[:, None, :].broadcast_to([64, 16, 128])
            mL16 = maskL[:, None, :].broadcast_to([64, 16, 64])
            idbc16 = ident[0:64, None, 0:64].broadcast_to([64, 16, 64])

            def f16(t):
                if len(t.shape) == 3:
                    return t[:, 0:16, :]
                return t[:].rearrange("p c h n -> p (c h) n")

            def wd(t, blk, n, off=0):
                fl = t[:].rearrange("p a n -> p (a n)")
                return fl[:, blk * n + off:blk * n + off + 128]

            def pv(p, lo, n):
                return p[0:64, lo:lo + 16 * n].rearrange("p (a n) -> p a n", n=n)

            def v3(p, n):
                return p[0:64, 0:8 * n].rearrange("p (h n) -> p h n", n=n)

            def bc(t, lo, hi, n):
                return t[:, lo:hi, None].broadcast_to([64, hi - lo, n])

            def c4(t):
                return t[:].rearrange("p h (c l) -> p h c l", l=64)

            for s, (toff, coff, T) in enumerate(SEQ):
                sti_ = [0]
                ts("dve", RR(STt[0][:, 0:8, :]), STt[1][:, 0:8, :], 0.0, None, ALU.mult, None, [b_ST[1]], [b_ST[0]])
                ntile = T // NT2
                order = list(range(ntile - 1, -1, -1) if rev else range(ntile))

                def prep(ti, AR, b_AR, KT, b_KT, BT, b_BT, KH, b_KH, BH, b_BH, Vp, b_Vp, GL, b_GL):
                    ARb, b_ARb = AR, b_AR
                    tl = ti * NT2
                    c0 = coff + tl
                    tg = toff + tl
                    def ldq(q):
                        dma("sp" if q != 1 else "act", Lr[q][:],
                            Pscr[q * 512:(q + 1) * 512, c0:c0 + NT2 + 2].rearrange("(h p) t -> p h t", p=64),
                            [b_P], [b_L[q]])

                    def shq(q):
                        Lq = Lr[q]; S_, bS = (SH[q] if q < 2 else (Vp, b_Vp))
                        tt("pool", T1[:], Lq[:, :, 0:NT2], Lq[:, :, 2:NT2 + 2], ALU.add, [b_L[q]], [b_T1])
                        tt("pool", T1[:], T1[:], bc(hm3, 8 * q, 8 * q + 8, NT2), ALU.mult, [b_T1, b_hm3], [b_T1])
                        tt("pool", S_[:], Lq[:, :, 1:NT2 + 1], bc(om3, 8 * q, 8 * q + 8, NT2), ALU.mult,
                           [b_L[q], b_hm3], [bS])
                        tt("pool", S_[:], S_[:], T1[:], ALU.add, [bS, b_T1], [bS])
                    ldq(0); ldq(1)
                    dma("sp", XW[:], Pscr[1536 + 64 * d:1600 + 64 * d, c0:c0 + NT2 + 2], [b_P], [b_XW])
                    dma("act", XA[:], Pscr[1664 + 64 * d:1728 + 64 * d, c0:c0 + NT2 + 2], [b_P], [b_XA])
                    shq(0); ldq(2); shq(1); shq(2)
                    for (X_, bX, o_, bo, j) in ((XW, b_XW, xwp, b_xwp, 0), (XA, b_XA, xap, b_xap, 1)):
                        tt("dve", tw[:], X_[:, 0:NT2], X_[:, 2:NT2 + 2], ALU.add, [bX], [b_tw])
                        ts("dve", tw[:], tw[:], hmw[:, j:j + 1], None, ALU.mult, None, [b_tw, b_hmw], [b_tw])
                        stt(o_[:], X_[:, 1:NT2 + 1], omw[:, j:j + 1], tw[:], ALU.mult, ALU.add, [bX, b_hmw, b_tw], [bo])
                    act(xwp[:], xwp[:], AF.Tanh, [b_xwp], [b_xwp])
                    for hh in range(2):
                        pa, bpa = nextps()
                        for j in range(4):
                            h = 4 * hh + j
                            mm(pa[0:64, j * NT2:(j + 1) * NT2], w2d[:, h * 64:(h + 1) * 64], xwp[:], True, True,
                               [b_w2d, b_xwp], [bpa])
                        tt("dve", XB[:, 4 * hh:4 * hh + 4, :], pa[0:64, 0:4 * NT2].rearrange("p (h n) -> p h n", n=NT2),
                           bc(w0d, 4 * hh, 4 * hh + 4, NT2), ALU.add, [bpa, b_w0d], [b_XB])
                    act(XB[:], XB[:], AF.Exp, [b_XB], [b_XB], scale=-1.0)
                    act(XB[:], XB[:], AF.Ln, [b_XB], [b_XB], bias=1.0)
                    act(E2[:], XB[:], AF.Exp, [b_XB], [b_E2], bias=-0.5, scale=-1.0)
                    for hh in range(2):
                        pa, bpa = nextps()
                        for j in range(4):
                            h = 4 * hh + j
                            mm(pa[0:64, j * NT2:(j + 1) * NT2], a2d[:, h * 64:(h + 1) * 64], xap[:], True, True,
                               [b_w2d, b_xap], [bpa])
                        tt("dve", AD[:, 4 * hh:4 * hh + 4, :], pa[0:64, 0:4 * NT2].rearrange("p (h n) -> p h n", n=NT2),
                           bc(a0d, 4 * hh, 4 * hh + 4, NT2), ALU.add, [bpa, b_a0d], [b_AD])
                    act(AD[:], AD[:], AF.Sigmoid, [b_AD], [b_AD])
                    tt("pool", KR[:], Kp[:], bc(kk_, 0, 8, NT2), ALU.mult, [b_Kp, b_kk_], [b_KR])
                    tt("pool", T1[:], KR[:], KR[:], ALU.mult, [b_KR], [b_T1])
                    for hh in range(2):
                        pa, bpa = nextps()
                        for j in range(4):
                            h = 4 * hh + j
                            mm(pa[0:64, j * NT2:(j + 1) * NT2], ones64[:], T1[:, h, :], True, True, [b_c, b_T1], [bpa])
                        ts("dve", SS[:, 4 * hh:4 * hh + 4, :], pa[0:64, 0:4 * NT2].rearrange("p (h n) -> p h n", n=NT2),
                           1e-24, None, ALU.max, None, [bpa], [b_SS])
                    act(SS[:], SS[:], AF.Sqrt, [b_SS], [b_SS])
                    mk.op("dve", lambda E: E.reciprocal(out=SS[:], in_=SS[:]), [b_SS], [b_SS])
                    tt("pool", KR[:], KR[:], SS[:], ALU.mult, [b_KR, b_SS], [b_KR])
                    tt("pool", T2[:], AD[:], bc(ka_, 0, 8, NT2), ALU.mult, [b_AD, b_ka_], [b_T2])
                    tt("pool", T2[:], T2[:], bc(omka, 0, 8, NT2), ALU.add, [b_T2, b_omka], [b_T2])
                    tt("pool", KD[:], T2[:], Kp[:], ALU.mult, [b_T2, b_Kp], [b_KD])
                    tt("dve", AB[:], AD[:], KR[:], ALU.mult, [b_AD, b_KR], [b_AB])
                    tt("pool", T1[:], Rp[:], KD[:], ALU.mult, [b_Rp, b_KD], [b_T1])
                    tt("pool", T1[:], T1[:], bc(rk_, 0, 8, NT2), ALU.mult, [b_T1, b_rk_], [b_T1])
                    for hh in range(2):
                        pa, bpa = nextps()
                        for j in range(4):
                            h = 4 * hh + j
                            mm(pa[0:64, j * NT2:(j + 1) * NT2], ones64[:], T1[:, h, :], True, True, [b_c, b_T1], [bpa])
                        tt("dve", BON[:, 4 * hh:4 * hh + 4, :], pa[0:64, 0:4 * NT2].rearrange("p (h n) -> p h n", n=NT2),
                           Vp[:, 4 * hh:4 * hh + 4, :], ALU.mult, [bpa, b_Vp], [b_BON])
                    dma("sp", BD[d, :, :, tg:tg + NT2], BON[:], [b_BON], [b_BD])
                    E2f = E2[:].rearrange("p h t -> p (h t)"); Gf = G[:].rearrange("p h t -> p (h t)"); MSf = MS[:]
                    if rev:
                        E2f = E2f[:, ::-1]; Gf = Gf[:, ::-1]; MSf = MSf[:, ::-1]
                    mk.op("dve", lambda E, Gf=Gf, MSf=MSf, E2f=E2f: E.tensor_tensor_scan(
                        out=Gf, data0=MSf, data1=E2f, initial=0.0, op0=ALU.mult, op1=ALU.add), [b_E2, b_c], [b_G])
                    tt("pool", D1[:], G[:], E2[:], ALU.subtract, [b_G, b_E2], [b_D1])
                    Gv = G[:].rearrange("p h (c l) -> p (h c) l", l=64)
                    ti_ = 0 if rev else 63
                    totb = Gv[:, :, ti_:ti_ + 1].broadcast_to([64, 16, 64])
                    tt("pool", D2[:].rearrange("p h (c l) -> p (h c) l", l=64), Gv, totb, ALU.subtract, [b_G], [b_D2])
                    act(EP[:], G[:], AF.Exp, [b_G], [b_EP])
                    act(EN[:], G[:], AF.Exp, [b_G], [b_EN], scale=-1.0)
                    act(D1[:], D1[:], AF.Exp, [b_D1], [b_D1], scale=-1.0)
                    act(D2[:], D2[:], AF.Exp, [b_D2], [b_D2])
                    act(GL[:].rearrange("p (a o) -> p a o", o=1), Gv[:, :, ti_:ti_ + 1], AF.Exp, [b_G], [b_GL], scale=-1.0)
                    stt(RR(AR[:, :, :, 0:64]), c4(KR), -1.0, c4(D1), ALU.mult, ALU.mult, [b_KR, b_D1], [b_AR])
                    tt("pool", RR(AR[:, :, :, 64:128]), c4(Rp), c4(EN), ALU.mult, [b_Rp, b_EN], [b_AR])
                    tt("pool", RR(KT[:, 0:8, :]), KD[:], EP[:], ALU.mult, [b_KD, b_EP], [b_KT])
                    tt("dve", RR(BT[:, 0:8, :]), AB[:], EP[:], ALU.mult, [b_AB, b_EP], [b_BT])
                    tt("pool", KH[:], KD[:], D2[:], ALU.mult, [b_KD, b_D2], [b_KH])
                    tt("dve", BH[:], AB[:], D2[:], ALU.mult, [b_AB, b_D2], [b_BH])

                def chunk(ti, pend, AR, b_AR, KT, b_KT, BT, b_BT, KH, b_KH, BH, b_BH, Vp, b_Vp, GL, b_GL):
                    ARb, b_ARb = AR, b_AR
                    tg = toff + ti * NT2

                    def tt(*a):
                        tt0(*a)
                        mk.replay(pend, 2)

                    def cp(*a):
                        cp0(*a)
                        mk.replay(pend, 2)
                    CS = [slice(0, 64), slice(64, 128)]
                    mA8 = maskA[:, None, :].broadcast_to([64, 8, 128])
                    mL8 = maskL[:, None, :].broadcast_to([64, 8, 64])
                    idbc8 = ident[0:64, None, 0:64].broadcast_to([64, 8, 64])
                    C2 = (0, 1)

                    def blk(c):
                        return slice(c * 8, (c + 1) * 8)

                    def p8v(p, lo, n):
                        return p[0:64, lo:lo + 8 * n].rearrange("p (a n) -> p a n", n=n)
                    for c in C2:
                        p1, bp1 = nextps()
                        for h in range(8):
                            mmr(p1[0:128, h * 128:(h + 1) * 128], wd(BT, h, 128, c * 64), ARb[:, h, c, :], [b_BT, b_ARb], [bp1])
                        tt("dve", RR(MT1[:, c]), p8v(p1, 0, 128), mA8, ALU.mult, [bp1, b_c], [b_MT1])
                    for c in C2:
                        p2, bp2 = nextps()
                        for h in range(8):
                            mmr(p2[0:128, h * 128:(h + 1) * 128], wd(KT, h, 128, c * 64), ARb[:, h, c, :], [b_KT, b_ARb], [bp2])
                        tt("dve", RR(MT2[:, c]), p8v(p2, 0, 128), mA8, ALU.mult, [bp2, b_c], [b_MT2])
                    for c in C2:
                        p3, bp3 = nextps()
                        for h in range(8):
                            mmr(p3[0:128, h * 64:(h + 1) * 64], ARb[:, h, c, :], BT[:, h, CS[c]], [b_ARb, b_BT], [bp3])
                        tt("dve", RR(P0[:, blk(c), :]), p8v(p3, 0, 64), mL8, ALU.mult, [bp3, b_c], [b_P0])
                    Z0 = Zt[0]; bZ0 = b_Zt[0]
                    for c in C2:
                        p4, bp4 = nextps()
                        for h in range(8):
                            mk.op("pe", lambda E, p4=p4, o=h * 64, a=AR[:, h, c, 0:64]: E.transpose(
                                out=p4[0:64, o:o + 64], in_=a, identity=id64), [b_AR, b_ident], [bp4])
                            mk.op("pe", lambda E, p4=p4, o=512 + h * 64, a=Vp[:, h, CS[c]]: E.transpose(
                                out=p4[0:64, o:o + 64], in_=a, identity=id64), [b_Vp, b_ident], [bp4])
                        cp0("act", RR(Z0[:, blk(c), 0:64]), p8v(p4, 0, 64), [bp4], [bZ0])
                        cp("act", RR(VT[:, blk(c), :]), p8v(p4, 512, 64), [bp4], [b_VT])
                    for c in C2:
                        p5, bp5 = nextps()
                        for h in range(8):
                            mk.op("pe", lambda E, p5=p5, o=h * 64, a=BH[:, h, CS[c]]: E.transpose(
                                out=p5[0:64, o:o + 64], in_=a, identity=id64), [b_BH, b_ident], [bp5])
                            mk.op("pe", lambda E, p5=p5, o=512 + h * 64, a=KH[:, h, CS[c]]: E.transpose(
                                out=p5[0:64, o:o + 64], in_=a, identity=id64), [b_KH, b_ident], [bp5])
                        cp0("dve", RR(BHt[:, blk(c), :]), p8v(p5, 0, 64), [bp5], [b_BHt])
                        cp("dve", RR(KHt[:, blk(c), :]), p8v(p5, 512, 64), [bp5], [b_KHt])
                    for c in C2:
                        p6, bp6 = nextps()
                        for h in range(8):
                            mmr(p6[0:128, h * 64:(h + 1) * 64], MT2[:, c, h, :], VT[:, c * 8 + h, :], [b_MT2, b_VT], [bp6])
                        cp("act", RR(Z0[:, blk(c), 64:128]), p8v(p6, 0, 64), [bp6], [bZ0])
                    zi = 0
                    Pv = lambda c, h: P0[:, c * 8 + h, :]
                    Pw = lambda c, h: wd(P0, c * 8 + h, 64)
                    PTv = lambda c, h: MT1[:, c, h, 0:64]
                    PTw = lambda c, h: MT1[:, c, h, :]
                    bP = b_P0; bPT = b_MT1
                    for it in range(6):
                        Zc = Zt[zi]; bZc = b_Zt[zi]; Zn = Zt[1 - zi]; bZn = b_Zt[1 - zi]
                        PTc, Pc, PTcw, Pcw, bPTc, bPc = PTv, Pv, PTw, Pw, bPT, bP
                        if it < 5:
                            nx = it % 2
                            for c in C2:
                                p8, bp8 = nextps()
                                for h in range(8):
                                    mmr(p8[0:128, h * 64:(h + 1) * 64], PTcw(c, h), Pc(c, h), [bPTc, bPc], [bp8])
                                    mmr(p8[0:128, 512 + h * 64:512 + (h + 1) * 64], Pcw(c, h), PTc(c, h), [bPTc, bPc], [bp8])
                                cp("act", RR(PP[nx][:, 0:32, :].rearrange("p (k a) n -> p k a n", k=2)[:, :, blk(c), :]),
                                   p8[0:64, 0:1024].rearrange("p (k a n) -> p k a n", k=2, a=8), [bp8], [b_PP[nx]])
                            Pv = lambda c, h, nx=nx: PP[nx][:, c * 8 + h, :]
                            PTv = lambda c, h, nx=nx: PP[nx][:, 16 + c * 8 + h, :]
                            Pw = lambda c, h, nx=nx: wd(PP[nx], c * 8 + h, 64)
                            PTw = lambda c, h, nx=nx: wd(PP[nx], 16 + c * 8 + h, 64)
                            bP = b_PP[nx]; bPT = b_PP[nx]
                        for c in C2:
                            p7, bp7 = nextps()
                            for h in range(8):
                                mmr(p7[0:128, h * 128:(h + 1) * 128], PTcw(c, h), Zc[:, c * 8 + h, :], [bPTc, bZc], [bp7])
                            tt("dve", RR(Zn[:, blk(c), :]), p8v(p7, 0, 128), Zc[:, blk(c), :], ALU.add, [bp7, bZc], [bZn])
                        zi = 1 - zi
                    Zf = Zt[zi]; bZf = b_Zt[zi]
                    for c in C2:
                        p9, bp9 = nextps()
                        for h in range(8):
                            mmr(p9[0:128, h * 64:(h + 1) * 64], Zf[:, c * 8 + h, :], MT1[:, c, h, 64:128], [bZf, b_MT1], [bp9])
                            mmr(p9[0:128, 512 + h * 64:512 + (h + 1) * 64], Zf[:, c * 8 + h, :], BHt[:, c * 8 + h, :], [bZf, b_BHt], [bp9])
                        tt0("dve", RR(QT[:, c]), p8v(p9, 0, 64), AR[:, :, c, 64:128], ALU.add, [bp9, b_AR], [b_QT])
                        GLc = GL[:].rearrange("p (h c) -> p c h", c=2)[:, c, :, None].broadcast_to([64, 8, 64])
                        tt0("pool", DG[:, blk(c), :], idbc8, GLc, ALU.mult, [b_ident, b_GL], [b_DG])
                        tt("dve", RR(MM[:, blk(c), :]), p8v(p9, 512, 64), DG[:, blk(c), :], ALU.add, [bp9, b_DG], [b_MM])
                    for c in (range(1, -1, -1) if rev else range(2)):
                        sti = sti_[0]
                        ST = STt[sti]; bST = b_ST[sti]; STn = STt[1 - sti]; bSTn = b_ST[1 - sti]
                        p11, bp11 = nextps()
                        for h in range(8):
                            o_ = p11[0:128, h * 64:(h + 1) * 64]
                            k_ = c * 8 + h
                            mmr(o_, wd(ST, h, 64), QT[:, c, h, :], [bST, b_QT], [bp11], True, False)
                            mmr(o_, wd(Zf, k_, 128, 64), MT1[:, c, h, 64:128], [bZf, b_MT1], [bp11], False, False)
                            mmr(o_, wd(VT, k_, 64), MT2[:, c, h, 64:128], [b_VT, b_MT2], [bp11], False, True)
                        cp("act", YT[:, :, CS[c]], v3(p11, 64), [bp11], [b_YT])
                        p12, bp12 = nextps()
                        for h in range(8):
                            o_ = p12[0:128, h * 64:(h + 1) * 64]
                            k_ = c * 8 + h
                            mmr(o_, wd(MM, k_, 64), ST[:, h, :], [b_MM, bST], [bp12], True, False)
                            mmr(o_, wd(BHt, k_, 64), Zf[:, k_, 64:128], [b_BHt, bZf], [bp12], False, False)
                            mmr(o_, wd(KHt, k_, 64), VT[:, k_, :], [b_KHt, b_VT], [bp12], False, True)
                        cp("dve", RR(STn[:, 0:8, :]), v3(p12, 64), [bp12], [bSTn])
                        sti_[0] = 1 - sti
                    dma("sp", YD[d, :, :, tg:tg + NT2], YT[:], [b_YT], [b_YD])

                PIPE = os.environ.get('RW_NOPIPE') != '1'
                if PIPE:
                    prep(order[0], *SETS[0])
                for idx, ti in enumerate(order):
                    pend = []
                    if not PIPE:
                        prep(ti, *SETS[idx % 2])
                    elif idx + 1 < len(order):
                        mk.deferred = pend
                        prep(order[idx + 1], *SETS[(idx + 1) % 2])
                        mk.deferred = None
                    if os.environ.get('RW_PIPE_MODE') == 'start':
                        mk.replay(pend, len(pend))
                    chunk(ti, pend, *SETS[idx % 2])
                    mk.replay(pend, len(pend))
            mk.flush(final=True)

    if upto >= 1:
        rwkv_pass(0)
        rwkv_pass(1)

    def s5_pass():
        es, sb, ps = scope("s5_")
        with es:
            TWO_PI = 2.0 * math.pi
            pz = [ps("pz%d" % i, [128, 512]) for i in range(8)]
            b_pz = [Buf() for _ in range(8)]
            pctr = [0]

            def nextps():
                i = pctr[0] % 8
                pctr[0] += 1
                return pz[i], b_pz[i]

            NLV = 10
            identb = sb("identb", [64, 64], BF16)
            dsk = sb("dsk", [16, 32])
            SQr = sb("SQr", [64, NLV, 64]); SQi = sb("SQi", [64, NLV, 64]); SQin = sb("SQin", [64, NLV, 64])
            LTr = sb("LTr", [64, 8, 64, 16], BF16); LTi = sb("LTi", [64, 8, 64, 16], BF16)
            OTr = sb("OTr", [64, 8, 64, 16], BF16); OTn = sb("OTn", [64, 8, 64, 16], BF16)
            CRb = sb("CRb", [64, 64, 16], BF16); CInb = sb("CInb", [64, 64, 16], BF16)
            bt_ = Buf()
            es2, sb2, ps2_ = scope("s5t_")
            ones1 = sb2("ones1", [1, 64]); row = sb2("row", [1, 64])
            mk.op("pool", lambda E: E.memset(ones1[:], 1.0), (), [bt_])
            dma("sp", row[:], log_dt.rearrange("d g -> (d g)").rearrange("(o n) -> o n", o=1), (), [bt_])
            cp("dve", identb[:], ident[0:64, 0:64], [b_ident], [bt_])
            LR = sb2("LR", [64, 64]); LI = sb2("LI", [64, 64])
            dma("sp", LR[:].rearrange("p (d g) -> p d g", d=2), lam_re.rearrange("d g p -> p d g"), (), [bt_], slow=True)
            dma("act", LI[:].rearrange("p (d g) -> p d g", d=2), lam_im.rearrange("d g p -> p d g"), (), [bt_], slow=True)
            BR = sb2("BR", [64, 64, 16]); BI = sb2("BI", [64, 64, 16])
            dma("sp", BR[:].rearrange("p (d g) h -> p d g h", d=2), b_re.rearrange("d g p h -> p d g h"), (), [bt_])
            dma("act", BI[:].rearrange("p (d g) h -> p d g h", d=2), b_im.rearrange("d g p h -> p d g h"), (), [bt_])
            CR = sb2("CR", [64, 64, 16]); CI = sb2("CI", [64, 64, 16])
            cnat = sb2("cnat", [128, 8, 64])
            for (src, dst) in ((c_re, CR), (c_im, CI)):
                dma("sp", cnat[:], src.rearrange("d g h p -> (d g h) p").rearrange("(k q) p -> q k p", q=128), [bt_], [bt_])
                for k in range(8):
                    pq, bq = nextps()
                    mk.op("pe", lambda E, pq=pq, k=k: E.transpose(out=pq[0:64, 0:128], in_=cnat[:, k, :], identity=ident[:]),
                          [bt_, b_ident], [bq])
                    cp("dve", dst[:, k * 8:(k + 1) * 8, :], pq[0:64, 0:128].rearrange("p (g h) -> p g h", h=16), [bq], [bt_])
            dma("sp", dsk[:], d_skip.rearrange("(g h) -> h g", h=16), (), [bt_], slow=True)
            DT = sb2("DT", [64, 64])
            pq, bq = nextps()
            mm(pq[0:64, 0:64], ones1[:], row[:], True, True, [bt_], [bq])
            act(DT[:], pq[0:64, 0:64], AF.Exp, [bq], [bt_])

            def T64(name):
                return sb2(name, [64, 64])
            ZR = T64("ZR"); ZI = T64("ZI"); EPs = T64("EPs"); COS = T64("COS"); SIN = T64("SIN")
            tA = T64("tA"); tB = T64("tB"); tC = T64("tC"); tI = sb2("tI", [64, 64], mybir.dt.int32)
            tt("dve", ZR[:], LR[:], DT[:], ALU.mult, [bt_], [bt_])
            tt("dve", ZI[:], LI[:], DT[:], ALU.mult, [bt_], [bt_])
            act(EPs[:], ZR[:], AF.Exp, [bt_], [bt_])
            for (dst, offs) in ((SIN, 64.0), (COS, 64.25)):
                ts("dve", tA[:], ZI[:], 1.0 / TWO_PI, offs, ALU.mult, ALU.add, [bt_], [bt_])
                cp("dve", tI[:], tA[:], [bt_], [bt_])
                cp("dve", tB[:], tI[:], [bt_], [bt_])
                tt("dve", tA[:], tA[:], tB[:], ALU.subtract, [bt_], [bt_])
                ts("dve", tB[:], tA[:], 0.5, None, ALU.is_gt, None, [bt_], [bt_])
                tt("dve", tA[:], tA[:], tB[:], ALU.subtract, [bt_], [bt_])
                act(dst[:], tA[:], AF.Sin, [bt_], [bt_], scale=TWO_PI)
            PWr = sb2("PWr", [64, 9, 64]); PWi = sb2("PWi", [64, 9, 64])
            mk.op("pool", lambda E: E.memset(PWr[:, 0, :], 1.0), (), [bt_])
            mk.op("pool", lambda E: E.memset(PWi[:, 0, :], 0.0), (), [bt_])
            tt("dve", PWr[:, 1, :], EPs[:], COS[:], ALU.mult, [bt_], [bt_])
            tt("dve", PWi[:, 1, :], EPs[:], SIN[:], ALU.mult, [bt_], [bt_])

            def cmul(or_, oi_, ar, ai, br, bi, n3=None):
                tt("dve", tA[:], ai, bi, ALU.mult, [bt_], [bt_])
                tt("dve", tB[:], ai, br, ALU.mult, [bt_], [bt_])
                tt("dve", tC[:], ar, br, ALU.mult, [bt_], [bt_])
                tt("dve", or_, tC[:], tA[:], ALU.subtract, [bt_], [bt_])
                tt("dve", tC[:], ar, bi, ALU.mult, [bt_], [bt_])
                tt("dve", oi_, tC[:], tB[:], ALU.add, [bt_], [bt_])
            for j in range(2, 9):
                cmul(PWr[:, j, :], PWi[:, j, :], PWr[:, j - 1, :], PWi[:, j - 1, :], PWr[:, 1, :], PWi[:, 1, :])
            NLV = 10
            cp("dve", SQr[:, 0, :], PWr[:, 8, :], [bt_], [bt_]); cp("dve", SQi[:, 0, :], PWi[:, 8, :], [bt_], [bt_])
            for k in range(1, NLV):
                cmul(SQr[:, k, :], SQi[:, k, :], SQr[:, k - 1, :], SQi[:, k - 1, :], SQr[:, k - 1, :], SQi[:, k - 1, :])
            ts("dve", SQin[:], SQi[:], -1.0, None, ALU.mult, None, [bt_], [bt_])
            CFr = T64("CFr"); CFi = T64("CFi"); DEN = T64("DEN"); NR = T64("NR")
            ts("dve", NR[:], PWr[:, 1, :], -1.0, None, ALU.add, None, [bt_], [bt_])
            tt("dve", tA[:], LR[:], LR[:], ALU.mult, [bt_], [bt_])
            tt("dve", tB[:], LI[:], LI[:], ALU.mult, [bt_], [bt_])
            tt("dve", DEN[:], tA[:], tB[:], ALU.add, [bt_], [bt_])
            mk.op("dve", lambda E: E.reciprocal(out=DEN[:], in_=DEN[:]), [bt_], [bt_])
            tt("dve", tA[:], NR[:], LR[:], ALU.mult, [bt_], [bt_])
            tt("dve", tB[:], PWi[:, 1, :], LI[:], ALU.mult, [bt_], [bt_])
            tt("dve", tA[:], tA[:], tB[:], ALU.add, [bt_], [bt_])
            tt("dve", CFr[:], tA[:], DEN[:], ALU.mult, [bt_], [bt_])
            tt("dve", tA[:], PWi[:, 1, :], LR[:], ALU.mult, [bt_], [bt_])
            tt("dve", tB[:], NR[:], LI[:], ALU.mult, [bt_], [bt_])
            tt("dve", tA[:], tA[:], tB[:], ALU.subtract, [bt_], [bt_])
            tt("dve", CFi[:], tA[:], DEN[:], ALU.mult, [bt_], [bt_])
            BbR = sb2("BbR", [64, 64, 16]); BbI = sb2("BbI", [64, 64, 16])
            X1t = sb2("X1t", [64, 64, 16]); X2t = sb2("X2t", [64, 64, 16])

            def b16(t2):
                return t2[:, :, None].broadcast_to([64, 64, 16])

            def cmul3(or_, oi_neg, ar2, ai2, br3, bi3, e1="dve", e2="pool"):
                tt(e1, X1t[:], br3, b16(ar2), ALU.mult, [bt_], [bt_])
                tt(e1, X2t[:], bi3, b16(ai2), ALU.mult, [bt_], [bt_])
                tt(e1, or_, X1t[:], X2t[:], ALU.subtract, [bt_], [bt_])
                tt(e1, X1t[:], bi3, b16(ar2), ALU.mult, [bt_], [bt_])
                tt(e1, X2t[:], br3, b16(ai2), ALU.mult, [bt_], [bt_])
                if oi_neg[1]:
                    tt(e1, X1t[:], X1t[:], X2t[:], ALU.add, [bt_], [bt_])
                    ts(e1, oi_neg[0], X1t[:], -1.0, None, ALU.mult, None, [bt_], [bt_])
                else:
                    tt(e1, oi_neg[0], X1t[:], X2t[:], ALU.add, [bt_], [bt_])
            cmul3(BbR[:], (BbI[:], False), CFr[:], CFi[:], BR[:], BI[:])
            for j in range(8):
                cmul3(LTr[:, j], (LTi[:, j], False), PWr[:, j, :], PWi[:, j, :], BbR[:], BbI[:])
                cmul3(OTr[:, j], (OTn[:, j], True), PWr[:, j + 1, :], PWi[:, j + 1, :], CR[:], CI[:])
            cp("dve", CRb[:], CR[:], [bt_], [bt_]); ts("dve", CInb[:], CI[:], -1.0, None, ALU.mult, None, [bt_], [bt_])

            mk.flush(final=True)
            es2.close()
            KTg = [sb("KTg%d" % i, [16, 15, 16], BF16) for i in range(2)]; b_KTg = [Buf(), Buf()]
            CTg = [sb("CTg%d" % i, [16, 32, 64], BF16) for i in range(2)]; b_CTg = [Buf(), Buf()]
            UGN = 2048
            ugs = [sb("ug%d" % i, [16, UGN]) for i in range(2)]; b_ugs = [Buf(), Buf()]; ugc = [0]
            ub = [sb("ub%d" % i, [16, 8, 1024], BF16) for i in range(2)]; b_ub = [Buf(), Buf()]
            ygs = [sb("yg0", [16, 4096])] * 2; b_ygs = [Buf()] * 2; ygc = [0]
            NBM = 1024
            Wt = [[[sb("W%d%d%d" % (pp, d, c), [64, NBM + 1]) for c in range(2)] for d in range(2)] for pp in range(2)]
            b_Wt = [[Buf() for d in range(2)] for pp in range(2)]
            Sb = [[[sb("Sb%d%d%d" % (i, d, c), [64, NBM + 1], BF16) for c in range(2)] for d in range(2)] for i in range(2)]
            b_Sb = [Buf(), Buf()]
            for pp in range(2):
                for d in range(2):
                    for c in range(2):
                        mk.op("pool", lambda E, t=Wt[pp][d][c]: E.memset(t[:], 0.0), (), [b_Wt[pp][d]])
            id16 = ident[0:16, 0:16]

            def gconsts(g):
                par = g % 2
                pk, bpk = nextps()
                for idx in range(15):
                    if idx == 0:
                        terms = [(0, 0), (1, 0)]
                    elif idx < 8:
                        terms = [(0, idx)]
                    else:
                        terms = [(1, idx - 7)]
                    n = 0
                    for (d, tau) in terms:
                        gi = d * 32 + g
                        mm(pk[0:16, idx * 16:(idx + 1) * 16], LTr[:, tau, gi, :], CRb[:, gi, :], n == 0, False, [bt_], [bpk]); n += 1
                        mm(pk[0:16, idx * 16:(idx + 1) * 16], LTi[:, tau, gi, :], CInb[:, gi, :], False, n == 2 * len(terms) - 1, [bt_], [bpk]); n += 1
                cp("dve", KTg[par][:].rearrange("p a b -> p (a b)"), pk[0:16, 0:240], [bpk], [b_KTg[par]])
                stt(KTg[par][:, 0, :], id16, dsk[:, g:g + 1], KTg[par][:, 0, :], ALU.mult, ALU.add, [b_KTg[par], bt_, b_ident], [b_KTg[par]])
                for q in range(4):
                    pc_, bpc = nextps()
                    for j in range(8):
                        i = q * 8 + j
                        d = i // 16; s_ = (i // 2) % 8; c = i % 2
                        e_ = (7 - s_) if d == 0 else s_
                        src = (LTr if c == 0 else LTi)[:, e_, d * 32 + g, :]
                        mm(pc_[0:16, j * 64:(j + 1) * 64], src, identb[:], True, True, [bt_], [bpc])
                    cp("act", CTg[par][:, q * 8:(q + 1) * 8, :].rearrange("p a b -> p (a b)"),
                       pc_[0:16, 0:512], [bpc], [b_CTg[par]])

            def dims(s):
                toff, coff, T = SEQ[s]
                nblk = T // 8
                BW = min(512, nblk)
                return toff, coff, T, nblk, BW, 8 * BW, nblk // BW

            def front(g, s, st):
                par = g % 2
                toff, coff, T, nblk, BW, TW, nbt = dims(s)
                nlv = int(math.log2(nblk))
                UG = min(UGN, T)
                for hf in range(T // UG):
                    ug = ugs[ugc[0] % 2]; b_ug = b_ugs[ugc[0] % 2]; ugc[0] += 1
                    dma("sp" if hf % 2 == 0 else "act", ug[:, 0:UG],
                        Pscr[1920 + 16 * g:1936 + 16 * g, coff + 1 + hf * UG:coff + 1 + (hf + 1) * UG], [b_P], [b_ug])
                    cp("act", ub[st][:, :, hf * (UG // 8):(hf + 1) * (UG // 8)], ug[:, 0:UG].rearrange("p (b s) -> p s b", s=8),
                       [b_ug], [b_ub[st]])
                if nblk < NBM:
                    for d in range(2):
                        for c in range(2):
                            mk.op("pool", lambda E, t=Wt[0][d][c]: E.memset(t[:], 0.0), (), [b_Wt[0][d]])
                            mk.op("pool", lambda E, t=Wt[1][d][c]: E.memset(t[:], 0.0), (), [b_Wt[1][d]])
                for bt in range(nbt):
                    for d in range(2):
                        for c in range(2):
                            pw_, bpw = nextps()
                            for s_ in range(8):
                                mm(pw_[0:64, 0:BW], CTg[par][:, (d * 8 + s_) * 2 + c, :],
                                   ub[st][:, s_, bt * BW:(bt + 1) * BW],
                                   s_ == 0, s_ == 7, [b_CTg[par], b_ub[st]], [bpw])
                            o0 = bt * BW + (1 if d == 0 else 0)
                            cp("act", Wt[0][d][c][:, o0:o0 + BW], pw_[0:64, 0:BW], [bpw], [b_Wt[0][d]])
                cur = 0
                for k in range(nlv):
                    sh = 1 << k
                    n_ = nblk - sh
                    for d in range(2):
                        gi = d * 32 + g
                        lo = 1 if d == 0 else 0
                        Wc = Wt[cur][d]; Wn = Wt[1 - cur][d]
                        bWc = b_Wt[cur][d]; bWn = b_Wt[1 - cur][d]
                        if d == 0:
                            dst = slice(lo + sh, lo + nblk); srcs = slice(lo, lo + n_); keep = slice(lo, lo + sh)
                        else:
                            dst = slice(lo, lo + n_); srcs = slice(lo + sh, lo + nblk); keep = slice(lo + n_, lo + nblk)
                        ar = SQr[:, k, gi:gi + 1]; ai = SQi[:, k, gi:gi + 1]; ain = SQin[:, k, gi:gi + 1]
                        stt(Wn[0][:, dst], Wc[0][:, srcs], ar, Wc[0][:, dst], ALU.mult, ALU.add, [bWc, bt_], [bWn])
                        stt(Wn[1][:, dst], Wc[1][:, srcs], ar, Wc[1][:, dst], ALU.mult, ALU.add, [bWc, bt_], [bWn])
                        stt(Wn[0][:, dst], Wc[1][:, srcs], ain, Wn[0][:, dst], ALU.mult, ALU.add, [bWc, bWn, bt_], [bWn])
                        stt(Wn[1][:, dst], Wc[0][:, srcs], ai, Wn[1][:, dst], ALU.mult, ALU.add, [bWc, bWn, bt_], [bWn])
                        cp("pool", Wn[0][:, keep], Wc[0][:, keep], [bWc], [bWn])
                        cp("pool", Wn[1][:, keep], Wc[1][:, keep], [bWc], [bWn])
                    cur = 1 - cur
                for d in range(2):
                    for c in range(2):
                        if d == 0:
                            cp("act", Sb[st][d][c][:, 1:nblk + 1], Wt[cur][d][c][:, 1:nblk + 1], [b_Wt[cur][d]], [b_Sb[st]])
                            mk.op("pool", lambda E, t=Sb[st][d][c]: E.memset(t[:, 0:1], 0.0), (), [b_Sb[st]])
                        else:
                            cp("act", Sb[st][d][c][:, 0:nblk], Wt[cur][d][c][:, 0:nblk], [b_Wt[cur][d]], [b_Sb[st]])
                            mk.op("pool", lambda E, t=Sb[st][d][c], nblk=nblk: E.memset(t[:, nblk:nblk + 1], 0.0), (), [b_Sb[st]])

            def back(g, s, st):
                par = g % 2
                toff, coff, T, nblk, BW, TW, nbt = dims(s)
                for bt in range(nbt):
                    yg = ygs[ygc[0] % 2]; b_yg = b_ygs[ygc[0] % 2]; ygc[0] += 1
                    ubv = ub[st][:, :, bt * BW:(bt + 1) * BW]
                    ygv = yg[:, 0:TW].rearrange("p (b s) -> p s b", s=8)
                    for t in range(8):
                        py, bpy = nextps()
                        for s_ in range(8):
                            idx = 0 if s_ == t else ((t - s_) if s_ < t else (7 + s_ - t))
                            mm(py[0:16, 0:BW], KTg[par][:, idx, :], ubv[:, s_, :], s_ == 0, False, [b_KTg[par], b_ub[st]], [bpy])
                        b0 = bt * BW
                        mm(py[0:16, 0:BW], OTr[:, t, g, :], Sb[st][0][0][:, b0:b0 + BW], False, False, [bt_, b_Sb[st]], [bpy])
                        mm(py[0:16, 0:BW], OTn[:, t, g, :], Sb[st][0][1][:, b0:b0 + BW], False, False, [bt_, b_Sb[st]], [bpy])
                        mm(py[0:16, 0:BW], OTr[:, 7 - t, 32 + g, :], Sb[st][1][0][:, b0 + 1:b0 + 1 + BW], False, False, [bt_, b_Sb[st]], [bpy])
                        mm(py[0:16, 0:BW], OTn[:, 7 - t, 32 + g, :], Sb[st][1][1][:, b0 + 1:b0 + 1 + BW], False, True, [bt_, b_Sb[st]], [bpy])
                        cp("act", ygv[:, t, :], py[0:16, 0:BW], [bpy], [b_yg])
                    dma("sp", YS[16 * g:16 * g + 16, toff + bt * TW:toff + (bt + 1) * TW], yg[:, 0:TW], [b_yg], [b_YS])

            units = [(g, s) for g in range(32) for s in range(NS)]
            prev = None
            for ui, (g, s) in enumerate(units):
                if s == 0:
                    gconsts(g)
                front(g, s, ui % 2)
                if prev is not None:
                    back(prev[0], prev[1], (ui - 1) % 2)
                prev = (g, s)
            back(prev[0], prev[1], (len(units) - 1) % 2)
            mk.flush(final=True)

    if upto >= 2:
        s5_pass()

    def mix_pass():
        es, sb, ps = scope("mx_")
        with es:
            N = 256
            pz = [ps("pz%d" % i, [128, 512]) for i in range(8)]
            b_pz = [Buf() for _ in range(8)]
            pctr = [0]

            def nextps():
                i = pctr[0] % 8
                pctr[0] += 1
                return pz[i], b_pz[i]
            bc_ = Buf()
            o64 = sb("o64", [64, 64])
            mk.op("pool", lambda E: E.memset(o64[:], 1.0 / 64.0), (), [bc_])
            ones_bf = sb("ones_bf", [128, 128], BF16)
            mk.op("pool", lambda E: E.memset(ones_bf[:], 1.0), (), [bc_])
            lg = sb("lg", [64, 8]); lb = sb("lb", [64, 8])
            dma("sp", lg[:], lnx_g.rearrange("(h p) -> p h", p=64), (), [bc_], slow=True)
            dma("sp", lb[:], lnx_b.rearrange("(h p) -> p h", p=64), (), [bc_], slow=True)
            g2t = sb("g2t", [128, 512]); dma("sp", g2t[:], g2, (), [bc_])
            mug = sb("mug", [128, 1]); hmg = sb("hmg", [128, 1]); omg = sb("omg", [128, 1])
            dma("sp", mug[:], mu_shift[1792:1920].rearrange("(p o) -> p o", o=1), (), [bc_], slow=True)
            ts("dve", hmg[:], mug[:], 0.5, None, ALU.mult, None, [bc_], [bc_])
            ts("dve", omg[:], mug[:], -1.0, 1.0, ALU.mult, ALU.add, [bc_], [bc_])
            bgl = sb("bgl", [128, 4]); s5g = sb("s5g", [128, 4])
            dma("sp", bgl[:], b_glu.rearrange("(q p) -> p q", p=128), (), [bc_], slow=True)
            dma("sp", s5g[:], s5_out_g.rearrange("(q p) -> p q", p=128), (), [bc_], slow=True)
            wst = sb("wst", [128, 4, 128])
            mk.op("pool", lambda E: E.memset(wst[:], 0.0), (), [bc_])
            for g in range(32):
                r0 = (g % 8) * 16
                dma("sp" if g % 2 == 0 else "act", wst[r0:r0 + 16, g // 8, r0:r0 + 16], w_glu[g], [bc_], [bc_])
            Wbd = sb("Wbd", [128, 4, 128], BF16)
            cp("dve", Wbd[:], wst[:], [bc_], [bc_])
            wo_r = sb("wo_r", [64, 8, D], BF16); wo_s = sb("wo_s", [128, 4, D], BF16)
            stg = [sb("stg%d" % i, [128, D]) for i in range(2)]; b_stg = [Buf(), Buf()]
            for h in range(8):
                st_ = stg[h % 2]; bs_ = b_stg[h % 2]
                dma("sp", st_[0:64, :], w_out[h * 64:(h + 1) * 64, :], (), [bs_])
                cp("pool", wo_r[:, h, :], st_[0:64, :], [bs_], [bc_])
            for q in range(4):
                st_ = stg[q % 2]; bs_ = b_stg[q % 2]
                dma("sp", st_[:], w_out[512 + q * 128:512 + (q + 1) * 128, :], (), [bs_])
                cp("pool", wo_s[:, q, :], st_[:], [bs_], [bc_])

            def T4(name, dt=F32):
                return sb(name, [64, 8, N], dt), Buf()
            YF, b_YF = T4("YF"); YB, b_YB = T4("YB"); BF_, b_BF = T4("BF"); BB, b_BB = T4("BB")
            Ym, b_Ym = T4("Ym"); SQt, b_SQt = T4("SQt"); RS, b_RS = T4("RS")
            yr, b_yr = T4("yr", BF16)
            XG = sb("XG", [128, N + 2]); b_XG = Buf(); tw = sb("tw", [128, N]); b_tw = Buf()
            sg = sb("sg", [128, N]); b_sg = Buf()
            S5 = sb("S5", [128, 4, N]); b_S5 = Buf(); Z1 = sb("Z1", [128, 4, N]); b_Z1 = Buf()
            Z2 = sb("Z2", [128, 4, N]); b_Z2 = Buf(); Zb = sb("Zb", [128, 4, N], BF16); b_Zb = Buf()
            r2 = sb("r2", [128, N]); b_r2 = Buf()
            ysb = sb("ysb", [128, 4, N], BF16); b_ysb = Buf()
            xTt = sb("xTt", [128, 8, N]); b_xTt = Buf(); X1t = sb("X1t", [128, 8, N]); b_X1t = Buf()

            def bc(t, n):
                return t[:, :, None].broadcast_to([64, 8, n])

            def fl(t):
                return t[:].rearrange("p h t -> p (h t)")
            for s, (toff, coff, T) in enumerate(SEQ):
                for ti in range(T // N):
                    tl = ti * N; tg = toff + tl; c0 = coff + tl
                    dma("sp", YF[:], YD[0, :, :, tg:tg + N], [b_YD], [b_YF])
                    dma("act", YB[:], YD[1, :, :, tg:tg + N], [b_YD], [b_YB])
                    dma("sp", BF_[:], BD[0, :, :, tg:tg + N], [b_BD], [b_BF])
                    dma("act", BB[:], BD[1, :, :, tg:tg + N], [b_BD], [b_BB])
                    dma("sp", XG[:], Pscr[1792:1920, c0:c0 + N + 2], [b_P], [b_XG])
                    dma("act", S5[:], YS[:, tg:tg + N].rearrange("(q p) t -> p q t", p=128), [b_YS], [b_S5])
                    dma("sp", xTt[:], XT[:, tg:tg + N].rearrange("(k p) t -> p k t", p=128), [b_XT], [b_xTt])
                    pendS = []
                    mk.deferred = pendS
                    K0 = 2.0 * math.sqrt(2.0 / math.pi)
                    tt("pool", Z1[:], S5[:], S5[:], ALU.mult, [b_S5], [b_Z1])
                    ts("dve", Z1[:], Z1[:], 0.044715, 1.0, ALU.mult, ALU.add, [b_Z1], [b_Z1])
                    tt("pool", Z1[:], Z1[:], S5[:], ALU.mult, [b_Z1, b_S5], [b_Z1])
                    act(Z1[:], Z1[:], AF.Sigmoid, [b_Z1], [b_Z1], scale=K0)
                    tt("pool", Z1[:], Z1[:], S5[:], ALU.mult, [b_Z1, b_S5], [b_Z1])
                    cp("dve", Zb[:], Z1[:], [b_Z1], [b_Zb])
                    for q in range(4):
                        pm_, bpm = nextps()
                        mm(pm_[:, 0:N], Wbd[:, q, :], Zb[:, q, :], True, True, [bc_, b_Zb], [bpm])
                        mk.op("act", lambda E, pm_=pm_, q=q: E.activation(out=Z2[:, q, :], in_=pm_[:, 0:N], func=AF.Sigmoid,
                                                                        bias=bgl[:, q:q + 1], scale=1.0), [bpm, bc_], [b_Z2])
                    tt("pool", Z2[:], Z2[:], Z1[:], ALU.mult, [b_Z2, b_Z1], [b_Z2])
                    tt("pool", Zb[:], Z2[:], Z2[:], ALU.mult, [b_Z2], [b_Zb])
                    pm_, bpm = nextps()
                    for q in range(4):
                        mm(pm_[:, 0:N], ones_bf[:], Zb[:, q, :], q == 0, q == 3, [bc_, b_Zb], [bpm])
                    act(r2[:], pm_[:, 0:N], AF.Sqrt, [bpm], [b_r2], bias=RMS_EPS, scale=1.0 / 512.0)
                    mk.op("dve", lambda E: E.reciprocal(out=r2[:], in_=r2[:]), [b_r2], [b_r2])
                    for q in range(4):
                        stt(ysb[:, q, :], Z2[:, q, :], s5g[:, q:q + 1], r2[:], ALU.mult, ALU.mult, [b_Z2, b_r2, bc_], [b_ysb])
                    mk.deferred = None
                    tt("pool", Ym[:], YF[:], YB[:], ALU.add, [b_YF, b_YB], [b_Ym])
                    mk.replay(pendS, 2)
                    for j in range(4):
                        pm_, bpm = nextps()
                        mm(pm_[0:64, :], o64[:], fl(Ym)[:, j * 512:(j + 1) * 512], True, True, [bc_, b_Ym], [bpm])
                        tt("dve", fl(YF)[:, j * 512:(j + 1) * 512], fl(Ym)[:, j * 512:(j + 1) * 512], pm_[0:64, :],
                           ALU.subtract, [bpm, b_Ym], [b_YF])
                    tt("pool", SQt[:], YF[:], YF[:], ALU.mult, [b_YF], [b_SQt])
                    mk.replay(pendS, 2)
                    for j in range(4):
                        pm_, bpm = nextps()
                        mm(pm_[0:64, :], o64[:], fl(SQt)[:, j * 512:(j + 1) * 512], True, True, [bc_, b_SQt], [bpm])
                        act(fl(RS)[:, j * 512:(j + 1) * 512], pm_[0:64, :], AF.Sqrt, [bpm], [b_RS], bias=LNX_EPS)
                    mk.replay(pendS, 2)
                    mk.op("dve", lambda E: E.reciprocal(out=RS[:], in_=RS[:]), [b_RS], [b_RS])
                    mk.replay(pendS, 2)
                    tt("pool", Ym[:], YF[:], RS[:], ALU.mult, [b_YF, b_RS], [b_Ym])
                    mk.replay(pendS, 2)
                    tt("pool", Ym[:], Ym[:], bc(lg, N), ALU.mult, [b_Ym, bc_], [b_Ym])
                    mk.replay(pendS, 2)
                    tt("pool", Ym[:], Ym[:], bc(lb, N), ALU.add, [b_Ym, bc_], [b_Ym])
                    mk.replay(pendS, 2)
                    tt("pool", Ym[:], Ym[:], BF_[:], ALU.add, [b_Ym, b_BF], [b_Ym])
                    mk.replay(pendS, 2)
                    tt("pool", Ym[:], Ym[:], BB[:], ALU.add, [b_Ym, b_BB], [b_Ym])
                    mk.replay(pendS, 2)
                    tt("dve", tw[:], XG[:, 0:N], XG[:, 2:N + 2], ALU.add, [b_XG], [b_tw])
                    mk.replay(pendS, 2)
                    ts("dve", tw[:], tw[:], hmg[:, 0:1], None, ALU.mult, None, [b_tw, bc_], [b_tw])
                    mk.replay(pendS, 2)
                    stt(sg[:], XG[:, 1:N + 1], omg[:, 0:1], tw[:], ALU.mult, ALU.add, [b_XG, b_tw, bc_], [b_sg])
                    mk.replay(pendS, 2)
                    act(sg[:], sg[:], AF.Sigmoid, [b_sg], [b_sg])
                    mk.replay(pendS, 2)
                    for h2_ in range(4):
                        pm_, bpm = nextps()
                        for j in range(2):
                            h = 2 * h2_ + j
                            mm(pm_[0:64, j * N:(j + 1) * N], g2t[:, h * 64:(h + 1) * 64], sg[:], True, True, [bc_, b_sg], [bpm])
                        tt("dve", yr[:, 2 * h2_:2 * h2_ + 2, :], pm_[0:64, :].rearrange("p (h n) -> p h n", n=N),
                           Ym[:, 2 * h2_:2 * h2_ + 2, :], ALU.mult, [bpm, b_Ym], [b_yr])
                    mk.replay(pendS, len(pendS))
                    for dm in range(8):
                        pm_, bpm = nextps()
                        for h in range(8):
                            mm(pm_[:, 0:N], wo_r[:, h, dm * 128:(dm + 1) * 128], yr[:, h, :], h == 0, False, [bc_, b_yr], [bpm])
                        for q in range(4):
                            mm(pm_[:, 0:N], wo_s[:, q, dm * 128:(dm + 1) * 128], ysb[:, q, :], False, q == 3, [bc_, b_ysb], [bpm])
                        stt(X1t[:, dm, :], pm_[:, 0:N], modT[:, 16 + dm, s:s + 1], xTt[:, dm, :], ALU.mult, ALU.add,
                            [bpm, b_mod, b_xTt], [b_X1t])
                    dma("sp", X1[:, tg:tg + N].rearrange("(k p) t -> p k t", p=128), X1t[:], [b_X1t], [b_X1])
            mk.flush(final=True)

    if upto >= 3:
        mix_pass()

    def ffn_pass():
        es, sb, ps = scope("ff_")
        with es:
            N = 256
            pz = [ps("pz%d" % i, [128, 512]) for i in range(8)]
            b_pz = [Buf() for _ in range(8)]
            pctr = [0]

            def nextps():
                i = pctr[0] % 8
                pctr[0] += 1
                return pz[i], b_pz[i]
            bc_ = Buf()
            ones_bf = sb("ones_bf", [128, 128], BF16)
            mk.op("pool", lambda E: E.memset(ones_bf[:], 1.0), (), [bc_])
            n2g = sb("n2g", [128, 8]); fg = sb("fg", [128, 8]); sc2 = sb("sc2", [128, 8, NS])
            dma("sp", n2g[:], norm2_g.rearrange("(k p) -> p k", p=128), (), [bc_], slow=True)
            dma("sp", fg[:], final_g.rearrange("(k p) -> p k", p=128), (), [bc_], slow=True)
            for s in range(NS):
                stt(sc2[:, :, s], modT[:, 32:40, s], 1.0, n2g[:], ALU.add, ALU.mult, [b_mod, bc_], [bc_])
            w1 = sb("w1", [128, 8, DFF], BF16); w3 = sb("w3", [128, 8, DFF], BF16); w2_ = sb("w2_", [128, 22, D], BF16)
            stg = [sb("stg%d" % i, [128, 1408]) for i in range(2)]; b_stg = [Buf(), Buf()]
            n = 0
            for (src, dst) in ((w_ff1, w1), (w_ff3, w3)):
                for k in range(8):
                    for hf in range(2):
                        st_ = stg[n % 2]; bs_ = b_stg[n % 2]; n += 1
                        dma("sp" if n % 2 else "act", st_[:], src[k * 128:(k + 1) * 128, hf * 1408:(hf + 1) * 1408], (), [bs_])
                        cp("pool" if n % 2 else "dve", dst[:, k, hf * 1408:(hf + 1) * 1408], st_[:], [bs_], [bc_])
            for k in range(22):
                st_ = stg[n % 2]; bs_ = b_stg[n % 2]; n += 1
                dma("sp" if n % 2 else "act", st_[:, 0:D], w_ff2[k * 128:(k + 1) * 128, :], (), [bs_])
                cp("pool" if n % 2 else "dve", w2_[:, k, :], st_[:, 0:D], [bs_], [bc_])
            FS = []
            for i_ in range(2):
                FS.append(dict(X1t=sb("X1t%d" % i_, [128, 8, N]), b_X1t=Buf(), h2=sb("h2%d" % i_, [128, 8, N], BF16), b_h2=Buf()))
            sq = sb("sq", [128, 8, N], BF16); b_sq = Buf()
            rstd = sb("rstd", [128, N]); b_rstd = Buf(); tmp = sb("tmp", [128, N]); b_tmp = Buf()
            sq2, b_sq2 = sq, b_sq; rstd2 = sb("rstd2", [128, N]); b_rstd2 = Buf()
            fm = sb("fm", [128, 22, N], BF16); b_fm = Buf()
            av = [sb("av%d" % i, [128, N]) for i in range(2)]; b_av = [Buf(), Buf()]
            X2t = sb("X2t", [128, 8, N]); b_X2t = Buf()
            ytm = sb("ytm", [128, 2, D]); b_ytm = Buf()

            def rms_bc(src, bsrc, sq_, bsq_, rs_, brs_):
                for dc in range(8):
                    act(sq_[:, dc, :], src[:, dc, :], AF.Square, [bsrc], [bsq_])
                pm_, bpm = nextps()
                for dc in range(8):
                    mm(pm_[:, 0:N], ones_bf[:], sq_[:, dc, :], dc == 0, dc == 7, [bc_, bsq_], [bpm])
                act(rs_[:], pm_[:, 0:N], AF.Sqrt, [bpm], [brs_], bias=RMS_EPS, scale=1.0 / D)
                mk.op("dve", lambda E: E.reciprocal(out=rs_[:], in_=rs_[:]), [brs_], [brs_])

            tiles = [(s_, tg_) for s_, (toff, coff, T) in enumerate(SEQ) for tg_ in range(toff, toff + T, N)]

            def ffront(s, tg, F_):
                X1t, b_X1t, h2, b_h2 = F_["X1t"], F_["b_X1t"], F_["h2"], F_["b_h2"]
                dma("sp", X1t[:], X1[:, tg:tg + N].rearrange("(k p) t -> p k t", p=128), [b_X1], [b_X1t])
                rms_bc(X1t, b_X1t, sq, b_sq, rstd, b_rstd)
                for dc in range(8):
                    tt("dve", tmp[:], X1t[:, dc, :], rstd[:], ALU.mult, [b_X1t, b_rstd], [b_tmp])
                    ts("dve", h2[:, dc, :], tmp[:], sc2[:, dc, s:s + 1], modT[:, 24 + dc, s:s + 1], ALU.mult, ALU.add,
                       [b_tmp, bc_, b_mod], [b_h2])

            def fback(s, tg, F_, pend):
                X1t, b_X1t, h2, b_h2 = F_["X1t"], F_["b_X1t"], F_["h2"], F_["b_h2"]
                for mc in range(22):
                    p1, bp1 = nextps()
                    for dc in range(8):
                        mm(p1[:, 0:N], w1[:, dc, mc * 128:(mc + 1) * 128], h2[:, dc, :], dc == 0, dc == 7, [bc_, b_h2], [bp1])
                    p3, bp3 = nextps()
                    for dc in range(8):
                        mm(p3[:, 0:N], w3[:, dc, mc * 128:(mc + 1) * 128], h2[:, dc, :], dc == 0, dc == 7, [bc_, b_h2], [bp3])
                    a_ = av[mc % 2]; ba_ = b_av[mc % 2]
                    act(a_[:], p1[:, 0:N], AF.Silu, [bp1], [ba_])
                    tt("dve", fm[:, mc, :], a_[:], p3[:, 0:N], ALU.mult, [ba_, bp3], [b_fm])
                    mk.replay(pend, 2)
                mk.replay(pend, len(pend))
                for dm in range(8):
                    pm_, bpm = nextps()
                    for mc in range(22):
                        mm(pm_[:, 0:N], w2_[:, mc, dm * 128:(dm + 1) * 128], fm[:, mc, :], mc == 0, mc == 21, [bc_, b_fm], [bpm])
                    stt(X2t[:, dm, :], pm_[:, 0:N], modT[:, 40 + dm, s:s + 1], X1t[:, dm, :], ALU.mult, ALU.add,
                        [bpm, b_mod, b_X1t], [b_X2t])
                rms_bc(X2t, b_X2t, sq2, b_sq2, rstd2, b_rstd2)
                for dc in range(8):
                    stt(X2t[:, dc, :], X2t[:, dc, :], fg[:, dc:dc + 1], rstd2[:], ALU.mult, ALU.mult,
                        [b_X2t, bc_, b_rstd2], [b_X2t])
                for j in range(N // 128):
                    for dq in range(2):
                        pm_, bpm = nextps()
                        for k in range(4):
                            dc = dq * 4 + k
                            mk.op("pe", lambda E, pm_=pm_, dc=dc, j=j, k=k: E.transpose(
                                out=pm_[:, k * 128:(k + 1) * 128], in_=X2t[:, dc, j * 128:(j + 1) * 128], identity=ident[:]),
                                [b_X2t, b_ident], [bpm])
                        cp("act" if dq == 0 else "dve", ytm[:, j, dq * 512:(dq + 1) * 512], pm_[:], [bpm], [b_ytm])
                dma("sp", y_out[tg:tg + N, :].rearrange("(j p) d -> p j d", p=128), ytm[:], [b_ytm], [])

            ffront(tiles[0][0], tiles[0][1], FS[0])
            for i_, (s_, tg_) in enumerate(tiles):
                pend = []
                if i_ + 1 < len(tiles):
                    mk.deferred = pend
                    ffront(tiles[i_ + 1][0], tiles[i_ + 1][1], FS[(i_ + 1) % 2])
                    mk.deferred = None
                fback(s_, tg_, FS[i_ % 2], pend)
            mk.flush(final=True)

    if upto >= 4:
        ffn_pass()

    outer.close()
    return nc


def core_inputs(P, x, c):
    f = np.float32
    m = {"x": x, "c": c}
    for k in ("norm1_g", "w_ada", "b_ada", "w_in", "mu_shift", "w0", "w2", "a0", "a2", "g2", "k_k", "k_a",
              "lnx_g", "lnx_b", "lam_re", "lam_im", "log_dt", "b_re", "b_im", "c_re", "c_im", "w_glu",
              "s5_out_g", "w_out", "norm2_g", "w_ff1", "w_ff3", "w_ff2", "final_g"):
        m[k] = P[k]
    m["r_k"] = P["r_k"].reshape(512)
    m["d_skip"] = P["d_skip"].reshape(512)
    m["b_glu"] = P["b_glu"].reshape(512)
    return {k: np.ascontiguousarray(v, dtype=f) for k, v in m.items()}


_T_PROMPT = 8192
_T_SAMPLE = 4096


def kernel(**inputs):
    n = 8
    P = {}
    for k, v in inputs.items():
        if k in ("x_prompt", "x_sample", "c_prompt", "c_sample"):
            continue
        v = np.asarray(v)
        P[k] = v if k == "final_g" else v[0]
    xp = np.asarray(inputs["x_prompt"]); xs = np.asarray(inputs["x_sample"])
    cpr = np.asarray(inputs["c_prompt"]); cs = np.asarray(inputs["c_sample"])
    TS = [xp.shape[1], xs.shape[1]]
    nc = build_program(TS, upto=int(os.environ.get('KUPTO', '99')))
    in_maps = []
    for b in range(n):
        x = np.concatenate([xp[b], xs[b]], axis=0)
        c = np.stack([cpr[b], cs[b]], axis=0)
        in_maps.append(core_inputs(P, x, c))
    res = run_bass_kernel_spmd(nc, in_maps, core_ids=list(range(n)))
    yp = np.stack([res.results[b]["y"][:TS[0]] for b in range(n)], axis=0).astype(np.float32)
    ys = np.stack([res.results[b]["y"][TS[0]:] for b in range(n)], axis=0).astype(np.float32)
    return (yp, ys)
```

```python
import os
import math
import numpy as np
import concourse.bass as bass
import concourse.mybir as mybir
from concourse.bass_utils import run_bass_kernel_spmd

F32 = mybir.dt.float32
BF16 = mybir.dt.bfloat16
AF = mybir.ActivationFunctionType
ALU = mybir.AluOpType
AX = mybir.AxisListType

D = 1024
DFF = 2816
NPROJ = 2432
RW = 1920
NDS = 12
RMS_EPS = 1e-6
LNX_EPS = 64e-5


class Buf:
    __slots__ = ("w", "r")

    def __init__(self):
        self.w = None
        self.r = {}


class MK:
    BLK = {"pe": "tensor", "dve": "vector", "act": "scalar", "pool": "gpsimd", "sp": "sync"}

    def __init__(self, nc, same=True):
        self.nc = nc
        self.same = same
        self.names = ["pe", "dve", "act", "pool", "sp"]
        self.sem = {k: nc.alloc_semaphore(name="s_" + k) for k in self.names}
        self.cnt = {k: 0 for k in self.names}
        self.seen = {k: {} for k in self.names}
        self.prog = {k: [] for k in self.names}
        self.dsem = [nc.alloc_semaphore(name="d%d" % i) for i in range(NDS)]
        self.dcnt = [0] * NDS
        self.dnext = 0
        self.deferred = None

    def semof(self, key):
        if isinstance(key, tuple):
            return self.dsem[key[1]]
        return self.sem[key]

    def _deps(self, e, reads, writes):
        deps = {}

        def add(k, v):
            if deps.get(k, 0) < v:
                deps[k] = v

        for b in reads:
            if b.w:
                add(*b.w)
        for b in writes:
            if b.w:
                add(*b.w)
            for k, v in b.r.items():
                add(k, v)
        out = []
        for k, v in deps.items():
            if k == e and (e == "pe" or not self.same):
                continue
            if self.seen[e].get(k, 0) >= v:
                continue
            self.seen[e][k] = v
            out.append((k, v))
        return out

    def _mark(self, tok, reads, writes):
        k, v = tok
        for b in reads:
            if b.r.get(k, 0) < v:
                b.r[k] = v
        for b in writes:
            b.w = tok
            b.r = {}

    def op(self, e, fn, reads=(), writes=()):
        if self.deferred is not None:
            self.deferred.append((0, e, fn, reads, writes))
            return
        waits = self._deps(e, reads, writes)
        self.cnt[e] += 1
        tok = (e, self.cnt[e])
        self.prog[e].append((waits, fn, self.sem[e], 1))
        self._mark(tok, reads, writes)

    def replay(self, pending, n):
        keep = self.deferred
        self.deferred = None
        last = None
        cnt = 0
        while pending and (cnt < n or last == "pe"):
            kind, e, fn, reads, writes = pending.pop(0)
            (self.dma if kind else self.op)(e, fn, reads, writes)
            last = e if not kind else None
            cnt += 1
        self.deferred = keep

    def dma(self, q, fn, reads=(), writes=()):
        if self.deferred is not None:
            self.deferred.append((1, q, fn, reads, writes))
            return
        i = self.dnext
        self.dnext = (i + 1) % NDS
        key = ("d", i)
        waits = self._deps(q, reads, writes)
        if self.dcnt[i] > 0 and self.seen[q].get(key, 0) < self.dcnt[i]:
            waits.append((key, self.dcnt[i]))
            self.seen[q][key] = self.dcnt[i]
        self.dcnt[i] += 16
        tok = (key, self.dcnt[i])
        self.prog[q].append((waits, fn, self.dsem[i], 16))
        self._mark(tok, reads, writes)

    def flush(self, final=False):
        nc = self.nc
        fin = []
        for i in (range(NDS) if final else []):
            if self.dcnt[i] > 0:
                fin.append((("d", i), self.dcnt[i]))
        for k in (self.names if final else []):
            if k != "sp" and self.cnt[k] > 0:
                fin.append((k, self.cnt[k]))
        with nc.Block() as block:
            for e in self.names:
                prog = self.prog[e]
                extra = fin if e == "sp" else []

                def body(eng, prog=prog, extra=extra):
                    for waits, fn, sem, inc in prog:
                        for k, v in waits:
                            eng.wait_ge(self.semof(k), v)
                        fn(eng).then_inc(sem, inc)
                    for k, v in extra:
                        eng.wait_ge(self.semof(k), v)

                getattr(block, self.BLK[e])(body)
        self.prog = {k: [] for k in self.names}

    def emit(self):
        self.flush(final=True)


def build_program(TS, dbg=False, upto=99):
    import contextlib
    nc = bass.Bass("TRN2", target_bir_lowering=False)
    mk = MK(nc, same=(os.environ.get("MK_SAME", "1") == "1"))
    TT = sum(TS)
    NS = len(TS)
    WP = TT + 2 * NS
    SEQ = []
    o = 0
    for s, T in enumerate(TS):
        SEQ.append((o, o + 2 * s, T))
        o += T

    def din(name, shape):
        return nc.dram_tensor(name, list(shape), F32, kind="ExternalInput").ap()

    def dscr(name, shape):
        return nc.dram_tensor(name, list(shape), F32, kind=("ExternalOutput" if dbg else "Internal")).ap()

    x_in = din("x", (TT, D))
    c_in = din("c", (NS, D))
    norm1_g = din("norm1_g", (D,))
    w_ada = din("w_ada", (D, 6 * D))
    b_ada = din("b_ada", (6 * D,))
    w_in = din("w_in", (D, NPROJ))
    mu_shift = din("mu_shift", (RW,))
    w0 = din("w0", (2, 512)); w2 = din("w2", (2, 64, 512))
    a0 = din("a0", (2, 512)); a2 = din("a2", (2, 64, 512))
    g2 = din("g2", (128, 512))
    k_k = din("k_k", (512,)); k_a = din("k_a", (512,)); r_k = din("r_k", (512,))
    lnx_g = din("lnx_g", (512,)); lnx_b = din("lnx_b", (512,))
    lam_re = din("lam_re", (2, 32, 64)); lam_im = din("lam_im", (2, 32, 64)); log_dt = din("log_dt", (2, 32))
    b_re = din("b_re", (2, 32, 64, 16)); b_im = din("b_im", (2, 32, 64, 16))
    c_re = din("c_re", (2, 32, 16, 64)); c_im = din("c_im", (2, 32, 16, 64))
    d_skip = din("d_skip", (512,)); w_glu = din("w_glu", (32, 16, 16)); b_glu = din("b_glu", (512,))
    s5_out_g = din("s5_out_g", (512,))
    w_out = din("w_out", (D, D)); norm2_g = din("norm2_g", (D,))
    w_ff1 = din("w_ff1", (D, DFF)); w_ff3 = din("w_ff3", (D, DFF)); w_ff2 = din("w_ff2", (DFF, D))
    final_g = din("final_g", (D,))
    y_out = nc.dram_tensor("y", [TT, D], F32, kind="ExternalOutput").ap()

    Pscr = dscr("Pscr", (NPROJ, WP))
    XT = dscr("XT", (D, TT))
    YD = dscr("YD", (2, 64, 8, TT))
    BD = dscr("BD", (2, 64, 8, TT))
    YS = dscr("YS", (512, TT))
    X1 = dscr("X1", (D, TT))
    MODS = dscr("MODS", (128, 48 * NS))
    b_P = Buf(); b_XT = Buf(); b_YD = Buf(); b_BD = Buf(); b_YS = Buf(); b_X1 = Buf(); b_MODS = Buf()

    def tt(e, out, a, b, op, r, w):
        mk.op(e, lambda E: E.tensor_tensor(out=out, in0=a, in1=b, op=op), r, w)

    def ts(e, out, a, s1, s2, op0, op1, r, w):
        if op1 is None:
            mk.op(e, lambda E: E.tensor_scalar(out=out, in0=a, scalar1=s1, scalar2=None, op0=op0), r, w)
        else:
            mk.op(e, lambda E: E.tensor_scalar(out=out, in0=a, scalar1=s1, scalar2=s2, op0=op0, op1=op1), r, w)

    def stt(out, a, sc, b, op0, op1, r, w):
        mk.op("dve", lambda E: E.scalar_tensor_tensor(out=out, in0=a, scalar=sc, in1=b, op0=op0, op1=op1), r, w)

    def act(out, a, func, r, w, bias=0.0, scale=1.0):
        mk.op("act", lambda E: E.activation(out=out, in_=a, func=func, bias=bias, scale=scale), r, w)

    def cp(e, out, a, r, w):
        if e == "act":
            mk.op("act", lambda E: E.activation(out=out, in_=a, func=AF.Copy), r, w)
        else:
            mk.op(e, lambda E: E.tensor_copy(out=out, in_=a), r, w)

    def mm(out, lhsT, rhs, st, sp_, r, w):
        mk.op("pe", lambda E: E.matmul(out=out, lhsT=lhsT, rhs=rhs, start=st, stop=sp_), r, w)

    F32R = mybir.dt.float32r
    USE_R = os.environ.get("RW_F32R", "1") == "1"

    def RR(ap):
        return ap.bitcast(F32R) if USE_R else ap

    def mmr(out, lhsT, rhs, r, w, st=True, sp_=True):
        mk.op("pe", lambda E: E.matmul(out=out, lhsT=lhsT.bitcast(F32R), rhs=rhs.bitcast(F32R), start=st, stop=sp_), r, w)

    def dma(q, out, in_, r, w, slow=False):
        if slow:
            mk.dma(q, lambda E: E.dma_start(out=out, in_=in_, allow_slow_non_contiguous=True), r, w)
        else:
            mk.dma(q, lambda E: E.dma_start(out=out, in_=in_), r, w)

    def scope(pfx=""):
        es = contextlib.ExitStack()

        def sb(name, shape, dt=F32):
            return es.enter_context(nc.sbuf_tensor(pfx + name, list(shape), dt))

        def ps(name, shape, dt=F32):
            return es.enter_context(nc.psum_tensor(pfx + name, list(shape), dt))
        return es, sb, ps

    def consts(sb):
        ident = sb("ident", [128, 128]); b_ident = Buf()
        mk.op("pool", lambda E: E.memset(ident[:], 1.0), (), [b_ident])
        mk.op("pool", lambda E: E.affine_select(out=ident[:], in_=ident[:], pattern=[[-1, 128]],
                                                compare_op=ALU.is_equal, fill=0.0, base=0, channel_multiplier=1),
              [b_ident], [b_ident])
        return ident, b_ident
    outer, osb, ops_ = scope("o_")
    ident, b_ident = consts(osb)
    modT = osb("modT", [128, 48, NS]); b_mod = Buf()
    sc1 = osb("sc1", [128, 8, NS]); b_sc1 = Buf()

    def pass0():
        es, sb, ps = scope("p0_")
        with es:
            ones_bf = sb("ones_bf", [128, 128], BF16); b_ones = Buf()
            mk.op("pool", lambda E: E.memset(ones_bf[:], 1.0), (), [b_ones])
            cT = sb("cT", [128, 8, NS]); b_cT = Buf()
            scT = sb("scT", [128, 8, NS]); b_scT = Buf()
            for s in range(NS):
                dma("sp", cT[:, :, s], c_in[s].rearrange("(k p) -> p k", p=128), (), [b_cT], slow=True)
            act(scT[:], cT[:], AF.Silu, [b_cT], [b_scT])
            badaT = sb("badaT", [128, 48]); b_bada = Buf()
            dma("sp", badaT[:], b_ada.rearrange("(k p) -> p k", p=128), (), [b_bada], slow=True)
            g1T = sb("g1T", [128, 8]); b_g1 = Buf()
            dma("sp", g1T[:], norm1_g.rearrange("(k p) -> p k", p=128), (), [b_g1], slow=True)
            wada_t = [sb("wada%d" % i, [128, 8, 256]) for i in range(2)]
            b_wada = [Buf(), Buf()]
            ps_mod_full = ps("ps_mod", [128, 512]); b_psmod = Buf()
            ps_mod = ps_mod_full[:, 0:4 * NS].rearrange("p (a b) -> p a b", b=NS)
            for slab in range(24):
                wt = wada_t[slab % 2]; bw = b_wada[slab % 2]
                dma("sp" if slab % 2 == 0 else "act", wt[:],
                    w_ada[:, slab * 256:(slab + 1) * 256].rearrange("(k p) n -> p k n", p=128), (), [bw])
                for j in range(2):
                    for k in range(8):
                        mm(ps_mod[:, j, :], wt[:, k, j * 128:(j + 1) * 128], scT[:, k, :], k == 0, k == 7,
                           [bw, b_scT], [b_psmod])
                for s in range(NS):
                    tt("dve", modT[:, slab * 2:(slab + 1) * 2, s], ps_mod[:, 0:2, s],
                       badaT[:, slab * 2:(slab + 1) * 2], ALU.add, [b_psmod, b_bada], [b_mod])
            for s in range(NS):
                stt(sc1[:, :, s], modT[:, 8:16, s], 1.0, g1T[:], ALU.add, ALU.mult, [b_mod, b_g1], [b_sc1])

            w_in_bf = sb("w_in_bf", [128, 8, NPROJ], BF16); b_win = Buf()
            wst = [sb("wst%d" % i, [128, NPROJ]) for i in range(2)]; b_wst = [Buf(), Buf()]
            for k in range(8):
                dma("sp", wst[k % 2][:], w_in[k * 128:(k + 1) * 128, :], (), [b_wst[k % 2]])
                cp("pool", w_in_bf[:, k, :], wst[k % 2][:], [b_wst[k % 2]], [b_win])

            NT = 512
            xtm = [sb("xtm%d" % i, [128, 4, D]) for i in range(2)]; b_xtm = [Buf(), Buf()]
            xT = sb("xT", [128, 8, NT]); b_xT = Buf()
            sq = sb("sq", [128, 8, NT], BF16); b_sq = Buf()
            rstd = sb("rstd", [128, NT]); b_rstd = Buf()
            tmp = sb("tmp0", [128, NT]); b_tmp = Buf()
            hT = sb("hT", [128, 8, NT], BF16); b_hT = Buf()
            pev = [sb("pev%d" % i, [128, NT]) for i in range(3)]; b_pev = [Buf() for _ in range(3)]
            zcol = sb("zcol", [128, 1]); b_zcol = Buf()
            mk.op("pool", lambda E: E.memset(zcol[:], 0.0), (), [b_zcol])
            pst = [ps("pst%d" % i, [128, NT]) for i in range(4)]; b_pst = [Buf() for _ in range(4)]
            psm = [ps("psm%d" % i, [128, NT]) for i in range(3)]; b_psm = [Buf() for _ in range(3)]
            for s, (toff, coff, T) in enumerate(SEQ):
                for mc in range(19):
                    for cc in (coff, coff + T + 1):
                        dma("sp", Pscr[mc * 128:(mc + 1) * 128, cc:cc + 1], zcol[:], [b_zcol], [b_P], slow=True)
                for ti in range(T // NT):
                    t0 = toff + ti * NT
                    xt = xtm[ti % 2]; bx = b_xtm[ti % 2]
                    dma("sp", xt[:], x_in[t0:t0 + NT, :].rearrange("(j p) d -> p j d", p=128), (), [bx])
                    for dc in range(8):
                        pt = pst[dc % 4]; bp = b_pst[dc % 4]
                        for j in range(4):
                            mk.op("pe", lambda E, pt=pt, xt=xt, j=j, dc=dc: E.transpose(
                                out=pt[:, j * 128:(j + 1) * 128], in_=xt[:, j, dc * 128:(dc + 1) * 128],
                                identity=ident[:]), [bx, b_ident], [bp])
                        cp("dve", xT[:, dc, :], pt[:], [bp], [b_xT])
                        act(sq[:, dc, :], pt[:], AF.Square, [bp, b_xT], [b_sq])
                    dma("act", XT[:, t0:t0 + NT].rearrange("(k p) t -> p k t", p=128), xT[:], [b_xT], [b_XT])
                    pm = psm[0]; bpm = b_psm[0]
                    for dc in range(8):
                        mm(pm[:], ones_bf[:], sq[:, dc, :], dc == 0, dc == 7, [b_sq, b_ones], [bpm])
                    act(rstd[:], pm[:], AF.Sqrt, [bpm], [b_rstd], bias=RMS_EPS, scale=1.0 / D)
                    mk.op("dve", lambda E: E.reciprocal(out=rstd[:], in_=rstd[:]), [b_rstd], [b_rstd])
                    for dc in range(8):
                        tt("dve", tmp[:], xT[:, dc, :], rstd[:], ALU.mult, [b_xT, b_rstd], [b_tmp])
                        ts("dve", hT[:, dc, :], tmp[:], sc1[:, dc, s:s + 1], modT[:, dc, s:s + 1], ALU.mult, ALU.add,
                           [b_tmp, b_sc1, b_mod], [b_hT])
                    for mc in range(19):
                        i3 = mc % 3
                        pm = psm[i3]; bpm = b_psm[i3]
                        for dc in range(8):
                            mm(pm[:], w_in_bf[:, dc, mc * 128:(mc + 1) * 128], hT[:, dc, :], dc == 0, dc == 7,
                               [b_win, b_hT], [bpm])
                        pv = pev[i3]; bpv = b_pev[i3]
                        cp("act", pv[:], pm[:], [bpm], [bpv])
                        cc = coff + 1 + ti * NT
                        dma("sp", Pscr[mc * 128:(mc + 1) * 128, cc:cc + NT], pv[:], [bpv], [b_P])
            mk.flush(final=True)

    pass0()
    def rwkv_pass(d):
        rev = (d == 1)
        es, sb, ps = scope("rw%d_" % d)
        with es:
            NT2 = 128
            psr = [ps("psr%d" % i, [128, 1024]) for i in range(4)]
            b_psr = [Buf() for _ in range(4)]
            pctr = [0]

            def nextps():
                i = pctr[0] % 4
                pctr[0] += 1
                return psr[i], b_psr[i]

            def T4(name):
                return sb(name, [64, 8, NT2]), Buf()

            def ldp(name, src512):
                t = sb(name, [64, 8]); b = Buf()
                dma("sp", t[:], src512.rearrange("(h p) -> p h", p=64), (), [b], slow=True)
                return t, b

            mu3 = sb("mu3", [64, 24]); b_mu3 = Buf()
            dma("sp", mu3[:], mu_shift[0:1536].rearrange("(g p) -> p g", p=64), (), [b_mu3], slow=True)
            hm3 = sb("hm3", [64, 24]); om3 = sb("om3", [64, 24]); b_hm3 = Buf()
            ts("dve", hm3[:], mu3[:], 0.5, None, ALU.mult, None, [b_mu3], [b_hm3])
            ts("dve", om3[:], mu3[:], -1.0, 1.0, ALU.mult, ALU.add, [b_mu3], [b_hm3])
            muw = sb("muw", [64, 2]); b_muw = Buf()
            dma("sp", muw[:, 0:1], mu_shift[1536 + 64 * d:1600 + 64 * d].rearrange("(p o) -> p o", o=1), (), [b_muw], slow=True)
            dma("sp", muw[:, 1:2], mu_shift[1664 + 64 * d:1728 + 64 * d].rearrange("(p o) -> p o", o=1), (), [b_muw], slow=True)
            hmw = sb("hmw", [64, 2]); omw = sb("omw", [64, 2]); b_hmw = Buf()
            ts("dve", hmw[:], muw[:], 0.5, None, ALU.mult, None, [b_muw], [b_hmw])
            ts("dve", omw[:], muw[:], -1.0, 1.0, ALU.mult, ALU.add, [b_muw], [b_hmw])
            w0d, b_w0d = ldp("w0d", w0[d]); a0d, b_a0d = ldp("a0d", a0[d])
            kk_, b_kk_ = ldp("kk_", k_k); ka_, b_ka_ = ldp("ka_", k_a); rk_, b_rk_ = ldp("rk_", r_k)
            omka = sb("omka", [64, 8]); b_omka = Buf()
            ts("dve", omka[:], ka_[:], -1.0, 1.0, ALU.mult, ALU.add, [b_ka_], [b_omka])
            w2d = sb("w2d", [64, 512]); a2d = sb("a2d", [64, 512]); b_w2d = Buf()
            dma("sp", w2d[:], w2[d], (), [b_w2d]); dma("sp", a2d[:], a2[d], (), [b_w2d])
            ones64 = sb("ones64", [64, 64]); b_c = Buf()
            mk.op("pool", lambda E: E.memset(ones64[:], 1.0), (), [b_c])
            maskA = sb("maskA", [64, 128]); maskL = sb("maskL", [64, 64]); MS = sb("MS", [64, 8 * NT2])
            mk.op("pool", lambda E: E.memset(maskA[:], 1.0), (), [b_c])
            mk.op("pool", lambda E: E.memset(maskL[:], 1.0), (), [b_c])
            mk.op("pool", lambda E: E.memset(MS[:], 1.0), (), [b_c])
            zc_ = 63 if rev else 0
            mk.op("pool", lambda E: E.memset(MS[:].rearrange("p (a l) -> p a l", l=64)[:, :, zc_:zc_ + 1], 0.0), [b_c], [b_c])

            def asel(ap, upper, strict):
                pat = [[1, 64]] if upper else [[-1, 64]]
                cm = -1 if upper else 1
                mk.op("pool", lambda E: E.affine_select(out=ap, in_=ap, pattern=pat, compare_op=ALU.is_ge, fill=0.0,
                                                        base=(-1 if strict else 0), channel_multiplier=cm),
                      [b_c], [b_c])
            asel(maskA[:, 0:64], not rev, True)
            asel(maskA[:, 64:128], not rev, False)
            asel(maskL[:], rev, True)
            mA = maskA[:, None, :].broadcast_to([64, 8, 128])
            mL = maskL[:, None, :].broadcast_to([64, 8, 64])
            id64 = ident[0:64, 0:64]
            idbc = ident[0:64, None, 0:64].broadcast_to([64, 8, 64])

            Lr = [sb("Lq%d" % q, [64, 8, NT2 + 2]) for q in range(2)]; b_L = [Buf() for _ in range(2)]
            Lr.append(Lr[0]); b_L.append(b_L[0])
            XW = sb("XW", [64, NT2 + 2]); XA = sb("XA", [64, NT2 + 2]); b_XW = Buf(); b_XA = Buf()
            T1, b_T1 = T4("T1")
            SH = [T4("SH%d" % q) for q in range(2)]
            (Rp, b_Rp), (Kp, b_Kp) = SH
            tt0, cp0 = tt, cp
            tw = sb("tw", [64, NT2]); b_tw = Buf()
            xwp = sb("xwp", [64, NT2]); xap = sb("xap", [64, NT2]); b_xwp = Buf(); b_xap = Buf()
            XB, b_XB = T4("XB"); E2, b_E2 = T4("E2"); AD, b_AD = T4("AD"); KR, b_KR = T4("KR")
            SS, b_SS = T4("SS"); KD, b_KD = T4("KD"); AB, b_AB = T4("AB"); BON, b_BON = XB, b_XB
            G, b_G = T1, b_T1; D1, b_D1 = E2, b_E2; D2, b_D2 = AD, b_AD; EP, b_EP = XB, b_XB; EN, b_EN = SS, b_SS
            T2, b_T2 = SS, b_SS
            SD = F32
            SETS = []
            for i_ in range(2):
                st_ = []
                for nm, shp in (("AR", [64, 8, 2, 128]), ("KT", [64, 9, NT2]), ("BT", [64, 9, NT2]), ("KH", [64, 8, NT2]),
                                ("BH", [64, 8, NT2]), ("Vp", [64, 8, NT2]), ("GL", [64, 16])):
                    st_ += [sb("%s_%d" % (nm, i_), shp), Buf()]
                SETS.append(st_)
            YT, b_YT = T4("YT")
            MT1 = sb("MT1", [64, 2, 8, 128], SD); MT2 = sb("MT2", [64, 2, 8, 128], SD); b_MT1 = Buf(); b_MT2 = Buf()
            P0 = sb("P0", [64, 17, 64], SD); b_P0 = Buf()
            PP = [sb("PP%d" % i, [64, 33, 64], SD) for i in range(2)]; b_PP = [Buf(), Buf()]
            Zt = [sb("Zt%d" % i, [64, 17, 128], SD) for i in range(2)]; b_Zt = [Buf() for _ in range(2)]
            VT = sb("VT", [64, 17, 64], SD); BHt = sb("BHt", [64, 17, 64], SD); KHt = sb("KHt", [64, 17, 64], SD)
            QT = sb("QT", [64, 2, 8, 64]); MM = sb("MM", [64, 17, 64]); DG = sb("DG", [64, 17, 64])
            b_VT = Buf(); b_BHt = Buf(); b_KHt = Buf(); b_QT = Buf(); b_MM = Buf(); b_DG = Buf()
            STt = [sb("ST%d" % i, [64, 9, 64]) for i in range(2)]; b_ST = [Buf(), Buf()]
            for t_, b__, r_ in ((Zt[0], b_Zt[0], 16), (Zt[1], b_Zt[1], 16)):
                ts("dve", RR(t_[:, r_, :]), maskA[:], 0.0, None, ALU.mult, None, [b_c], [b__])
            for t_, b__ in ((VT, b_VT), (BHt, b_BHt), (KHt, b_KHt), (MM, b_MM)):
                ts("dve", RR(t_[:, 16, :]), ones64[:], 0.0, None, ALU.mult, None, [b_c], [b__])
            for i_ in range(2):
                ts("dve", RR(STt[i_][:, 8, :]), ones64[:], 0.0, None, ALU.mult, None, [b_c], [b_ST[i_]])
                ts("dve", RR(SETS[i_][2][:, 8, :]), maskA[:], 0.0, None, ALU.mult, None, [b_c], [SETS[i_][3]])
                ts("dve", RR(SETS[i_][4][:, 8, :]), maskA[:], 0.0, None, ALU.mult, None, [b_c], [SETS[i_][5]])
            ts("dve", RR(P0[:, 16, :]), ones64[:], 0.0, None, ALU.mult, None, [b_c], [b_P0])
            for i_ in range(2):
                ts("dve", RR(PP[i_][:, 32, :]), ones64[:], 0.0, None, ALU.mult, None, [b_c], [b_PP[i_]])
            mA16 = maskA[:, None, :].broadcast_to([64, 16, 128])
            mL16 = maskL[:, None, :].broadcast_to([64, 16, 64])
            idbc16 = ident[0:64, None, 0:64].broadcast_to([64, 16, 64])

            def f16(t):
                if len(t.shape) == 3:
                    return t[:, 0:16, :]
                return t[:].rearrange("p c h n -> p (c h) n")

            def wd(t, blk, n, off=0):
                fl = t[:].rearrange("p a n -> p (a n)")
                return fl[:, blk * n + off:blk * n + off + 128]

            def pv(p, lo, n):
                return p[0:64, lo:lo + 16 * n].rearrange("p (a n) -> p a n", n=n)

            def v3(p, n):
                return p[0:64, 0:8 * n].rearrange("p (h n) -> p h n", n=n)

            def bc(t, lo, hi, n):
                return t[:, lo:hi, None].broadcast_to([64, hi - lo, n])

            def c4(t):
                return t[:].rearrange("p h (c l) -> p h c l", l=64)

            for s, (toff, coff, T) in enumerate(SEQ):
                sti_ = [0]
                ts("dve", RR(STt[0][:, 0:8, :]), STt[1][:, 0:8, :], 0.0, None, ALU.mult, None, [b_ST[1]], [b_ST[0]])
                ntile = T // NT2
                order = list(range(ntile - 1, -1, -1) if rev else range(ntile))

                def prep(ti, AR, b_AR, KT, b_KT, BT, b_BT, KH, b_KH, BH, b_BH, Vp, b_Vp, GL, b_GL):
                    ARb, b_ARb = AR, b_AR
                    tl = ti * NT2
                    c0 = coff + tl
                    tg = toff + tl
                    def ldq(q):
                        dma("sp" if q != 1 else "act", Lr[q][:],
                            Pscr[q * 512:(q + 1) * 512, c0:c0 + NT2 + 2].rearrange("(h p) t -> p h t", p=64),
                            [b_P], [b_L[q]])

                    def shq(q):
                        Lq = Lr[q]; S_, bS = (SH[q] if q < 2 else (Vp, b_Vp))
                        tt("pool", T1[:], Lq[:, :, 0:NT2], Lq[:, :, 2:NT2 + 2], ALU.add, [b_L[q]], [b_T1])
                        tt("pool", T1[:], T1[:], bc(hm3, 8 * q, 8 * q + 8, NT2), ALU.mult, [b_T1, b_hm3], [b_T1])
                        tt("pool", S_[:], Lq[:, :, 1:NT2 + 1], bc(om3, 8 * q, 8 * q + 8, NT2), ALU.mult,
                           [b_L[q], b_hm3], [bS])
                        tt("pool", S_[:], S_[:], T1[:], ALU.add, [bS, b_T1], [bS])
                    ldq(0); ldq(1)
                    dma("sp", XW[:], Pscr[1536 + 64 * d:1600 + 64 * d, c0:c0 + NT2 + 2], [b_P], [b_XW])
                    dma("act", XA[:], Pscr[1664 + 64 * d:1728 + 64 * d, c0:c0 + NT2 + 2], [b_P], [b_XA])
                    shq(0); ldq(2); shq(1); shq(2)
                    for (X_, bX, o_, bo, j) in ((XW, b_XW, xwp, b_xwp, 0), (XA, b_XA, xap, b_xap, 1)):
                        tt("dve", tw[:], X_[:, 0:NT2], X_[:, 2:NT2 + 2], ALU.add, [bX], [b_tw])
                        ts("dve", tw[:], tw[:], hmw[:, j:j + 1], None, ALU.mult, None, [b_tw, b_hmw], [b_tw])
                        stt(o_[:], X_[:, 1:NT2 + 1], omw[:, j:j + 1], tw[:], ALU.mult, ALU.add, [bX, b_hmw, b_tw], [bo])
                    act(xwp[:], xwp[:], AF.Tanh, [b_xwp], [b_xwp])
                    for hh in range(2):
                        pa, bpa = nextps()
                        for j in range(4):
                            h = 4 * hh + j
                            mm(pa[0:64, j * NT2:(j + 1) * NT2], w2d[:, h * 64:(h + 1) * 64], xwp[:], True, True,
                               [b_w2d, b_xwp], [bpa])
                        tt("dve", XB[:, 4 * hh:4 * hh + 4, :], pa[0:64, 0:4 * NT2].rearrange("p (h n) -> p h n", n=NT2),
                           bc(w0d, 4 * hh, 4 * hh + 4, NT2), ALU.add, [bpa, b_w0d], [b_XB])
                    act(XB[:], XB[:], AF.Exp, [b_XB], [b_XB], scale=-1.0)
                    act(XB[:], XB[:], AF.Ln, [b_XB], [b_XB], bias=1.0)
                    act(E2[:], XB[:], AF.Exp, [b_XB], [b_E2], bias=-0.5, scale=-1.0)
                    for hh in range(2):
                        pa, bpa = nextps()
                        for j in range(4):
                            h = 4 * hh + j
                            mm(pa[0:64, j * NT2:(j + 1) * NT2], a2d[:, h * 64:(h + 1) * 64], xap[:], True, True,
                               [b_w2d, b_xap], [bpa])
                        tt("dve", AD[:, 4 * hh:4 * hh + 4, :], pa[0:64, 0:4 * NT2].rearrange("p (h n) -> p h n", n=NT2),
                           bc(a0d, 4 * hh, 4 * hh + 4, NT2), ALU.add, [bpa, b_a0d], [b_AD])
                    act(AD[:], AD[:], AF.Sigmoid, [b_AD], [b_AD])
                    tt("pool", KR[:], Kp[:], bc(kk_, 0, 8, NT2), ALU.mult, [b_Kp, b_kk_], [b_KR])
                    tt("pool", T1[:], KR[:], KR[:], ALU.mult, [b_KR], [b_T1])
                    for hh in range(2):
                        pa, bpa = nextps()
                        for j in range(4):
                            h = 4 * hh + j
                            mm(pa[0:64, j * NT2:(j + 1) * NT2], ones64[:], T1[:, h, :], True, True, [b_c, b_T1], [bpa])
                        ts("dve", SS[:, 4 * hh:4 * hh + 4, :], pa[0:64, 0:4 * NT2].rearrange("p (h n) -> p h n", n=NT2),
                           1e-24, None, ALU.max, None, [bpa], [b_SS])
                    act(SS[:], SS[:], AF.Sqrt, [b_SS], [b_SS])
                    mk.op("dve", lambda E: E.reciprocal(out=SS[:], in_=SS[:]), [b_SS], [b_SS])
                    tt("pool", KR[:], KR[:], SS[:], ALU.mult, [b_KR, b_SS], [b_KR])
                    tt("pool", T2[:], AD[:], bc(ka_, 0, 8, NT2), ALU.mult, [b_AD, b_ka_], [b_T2])
                    tt("pool", T2[:], T2[:], bc(omka, 0, 8, NT2), ALU.add, [b_T2, b_omka], [b_T2])
                    tt("pool", KD[:], T2[:], Kp[:], ALU.mult, [b_T2, b_Kp], [b_KD])
                    tt("dve", AB[:], AD[:], KR[:], ALU.mult, [b_AD, b_KR], [b_AB])
                    tt("pool", T1[:], Rp[:], KD[:], ALU.mult, [b_Rp, b_KD], [b_T1])
                    tt("pool", T1[:], T1[:], bc(rk_, 0, 8, NT2), ALU.mult, [b_T1, b_rk_], [b_T1])
                    for hh in range(2):
                        pa, bpa = nextps()
                        for j in range(4):
                            h = 4 * hh + j
                            mm(pa[0:64, j * NT2:(j + 1) * NT2], ones64[:], T1[:, h, :], True, True, [b_c, b_T1], [bpa])
                        tt("dve", BON[:, 4 * hh:4 * hh + 4, :], pa[0:64, 0:4 * NT2].rearrange("p (h n) -> p h n", n=NT2),
                           Vp[:, 4 * hh:4 * hh + 4, :], ALU.mult, [bpa, b_Vp], [b_BON])
                    dma("sp", BD[d, :, :, tg:tg + NT2], BON[:], [b_BON], [b_BD])
                    E2f = E2[:].rearrange("p h t -> p (h t)"); Gf = G[:].rearrange("p h t -> p (h t)"); MSf = MS[:]
                    if rev:
                        E2f = E2f[:, ::-1]; Gf = Gf[:, ::-1]; MSf = MSf[:, ::-1]
                    mk.op("dve", lambda E, Gf=Gf, MSf=MSf, E2f=E2f: E.tensor_tensor_scan(
                        out=Gf, data0=MSf, data1=E2f, initial=0.0, op0=ALU.mult, op1=ALU.add), [b_E2, b_c], [b_G])
                    tt("pool", D1[:], G[:], E2[:], ALU.subtract, [b_G, b_E2], [b_D1])
                    Gv = G[:].rearrange("p h (c l) -> p (h c) l", l=64)
                    ti_ = 0 if rev else 63
                    totb = Gv[:, :, ti_:ti_ + 1].broadcast_to([64, 16, 64])
                    tt("pool", D2[:].rearrange("p h (c l) -> p (h c) l", l=64), Gv, totb, ALU.subtract, [b_G], [b_D2])
                    act(EP[:], G[:], AF.Exp, [b_G], [b_EP])
                    act(EN[:], G[:], AF.Exp, [b_G], [b_EN], scale=-1.0)
                    act(D1[:], D1[:], AF.Exp, [b_D1], [b_D1], scale=-1.0)
                    act(D2[:], D2[:], AF.Exp, [b_D2], [b_D2])
                    act(GL[:].rearrange("p (a o) -> p a o", o=1), Gv[:, :, ti_:ti_ + 1], AF.Exp, [b_G], [b_GL], scale=-1.0)
                    stt(RR(AR[:, :, :, 0:64]), c4(KR), -1.0, c4(D1), ALU.mult, ALU.mult, [b_KR, b_D1], [b_AR])
                    tt("pool", RR(AR[:, :, :, 64:128]), c4(Rp), c4(EN), ALU.mult, [b_Rp, b_EN], [b_AR])
                    tt("pool", RR(KT[:, 0:8, :]), KD[:], EP[:], ALU.mult, [b_KD, b_EP], [b_KT])
                    tt("dve", RR(BT[:, 0:8, :]), AB[:], EP[:], ALU.mult, [b_AB, b_EP], [b_BT])
                    tt("pool", KH[:], KD[:], D2[:], ALU.mult, [b_KD, b_D2], [b_KH])
                    tt("dve", BH[:], AB[:], D2[:], ALU.mult, [b_AB, b_D2], [b_BH])

                def chunk(ti, pend, AR, b_AR, KT, b_KT, BT, b_BT, KH, b_KH, BH, b_BH, Vp, b_Vp, GL, b_GL):
                    ARb, b_ARb = AR, b_AR
                    tg = toff + ti * NT2

                    def tt(*a):
                        tt0(*a)
                        mk.replay(pend, 2)

                    def cp(*a):
                        cp0(*a)
                        mk.replay(pend, 2)
                    CS = [slice(0, 64), slice(64, 128)]
                    mA8 = maskA[:, None, :].broadcast_to([64, 8, 128])
                    mL8 = maskL[:, None, :].broadcast_to([64, 8, 64])
                    idbc8 = ident[0:64, None, 0:64].broadcast_to([64, 8, 64])
                    C2 = (0, 1)

                    def blk(c):
                        return slice(c * 8, (c + 1) * 8)

                    def p8v(p, lo, n):
                        return p[0:64, lo:lo + 8 * n].rearrange("p (a n) -> p a n", n=n)
                    for c in C2:
                        p1, bp1 = nextps()
                        for h in range(8):
                            mmr(p1[0:128, h * 128:(h + 1) * 128], wd(BT, h, 128, c * 64), ARb[:, h, c, :], [b_BT, b_ARb], [bp1])
                        tt("dve", RR(MT1[:, c]), p8v(p1, 0, 128), mA8, ALU.mult, [bp1, b_c], [b_MT1])
                    for c in C2:
                        p2, bp2 = nextps()
                        for h in range(8):
                            mmr(p2[0:128, h * 128:(h + 1) * 128], wd(KT, h, 128, c * 64), ARb[:, h, c, :], [b_KT, b_ARb], [bp2])
                        tt("dve", RR(MT2[:, c]), p8v(p2, 0, 128), mA8, ALU.mult, [bp2, b_c], [b_MT2])
                    for c in C2:
                        p3, bp3 = nextps()
                        for h in range(8):
                            mmr(p3[0:128, h * 64:(h + 1) * 64], ARb[:, h, c, :], BT[:, h, CS[c]], [b_ARb, b_BT], [bp3])
                        tt("dve", RR(P0[:, blk(c), :]), p8v(p3, 0, 64), mL8, ALU.mult, [bp3, b_c], [b_P0])
                    Z0 = Zt[0]; bZ0 = b_Zt[0]
                    for c in C2:
                        p4, bp4 = nextps()
                        for h in range(8):
                            mk.op("pe", lambda E, p4=p4, o=h * 64, a=AR[:, h, c, 0:64]: E.transpose(
                                out=p4[0:64, o:o + 64], in_=a, identity=id64), [b_AR, b_ident], [bp4])
                            mk.op("pe", lambda E, p4=p4, o=512 + h * 64, a=Vp[:, h, CS[c]]: E.transpose(
                                out=p4[0:64, o:o + 64], in_=a, identity=id64), [b_Vp, b_ident], [bp4])
                        cp0("act", RR(Z0[:, blk(c), 0:64]), p8v(p4, 0, 64), [bp4], [bZ0])
                        cp("act", RR(VT[:, blk(c), :]), p8v(p4, 512, 64), [bp4], [b_VT])
                    for c in C2:
                        p5, bp5 = nextps()
                        for h in range(8):
                            mk.op("pe", lambda E, p5=p5, o=h * 64, a=BH[:, h, CS[c]]: E.transpose(
                                out=p5[0:64, o:o + 64], in_=a, identity=id64), [b_BH, b_ident], [bp5])
                            mk.op("pe", lambda E, p5=p5, o=512 + h * 64, a=KH[:, h, CS[c]]: E.transpose(
                                out=p5[0:64, o:o + 64], in_=a, identity=id64), [b_KH, b_ident], [bp5])
                        cp0("dve", RR(BHt[:, blk(c), :]), p8v(p5, 0, 64), [bp5], [b_BHt])
                        cp("dve", RR(KHt[:, blk(c), :]), p8v(p5, 512, 64), [bp5], [b_KHt])
                    for c in C2:
                        p6, bp6 = nextps()
                        for h in range(8):
                            mmr(p6[0:128, h * 64:(h + 1) * 64], MT2[:, c, h, :], VT[:, c * 8 + h, :], [b_MT2, b_VT], [bp6])
                        cp("act", RR(Z0[:, blk(c), 64:128]), p8v(p6, 0, 64), [bp6], [bZ0])
                    zi = 0
                    Pv = lambda c, h: P0[:, c * 8 + h, :]
                    Pw = lambda c, h: wd(P0, c * 8 + h, 64)
                    PTv = lambda c, h: MT1[:, c, h, 0:64]
                    PTw = lambda c, h: MT1[:, c, h, :]
                    bP = b_P0; bPT = b_MT1
                    for it in range(6):
                        Zc = Zt[zi]; bZc = b_Zt[zi]; Zn = Zt[1 - zi]; bZn = b_Zt[1 - zi]
                        PTc, Pc, PTcw, Pcw, bPTc, bPc = PTv, Pv, PTw, Pw, bPT, bP
                        if it < 5:
                            nx = it % 2
                            for c in C2:
                                p8, bp8 = nextps()
                                for h in range(8):
                                    if it < 4:
                                        mmr(p8[0:128, h * 64:(h + 1) * 64], PTcw(c, h), Pc(c, h), [bPTc, bPc], [bp8])
                                    mmr(p8[0:128, 512 + h * 64:512 + (h + 1) * 64], Pcw(c, h), PTc(c, h), [bPTc, bPc], [bp8])
                                cp("act", RR(PP[nx][:, 0:32, :].rearrange("p (k a) n -> p k a n", k=2)[:, :, blk(c), :]),
                                   p8[0:64, 0:1024].rearrange("p (k a n) -> p k a n", k=2, a=8), [bp8], [b_PP[nx]])
                            Pv = lambda c, h, nx=nx: PP[nx][:, c * 8 + h, :]
                            PTv = lambda c, h, nx=nx: PP[nx][:, 16 + c * 8 + h, :]
                            Pw = lambda c, h, nx=nx: wd(PP[nx], c * 8 + h, 64)
                            PTw = lambda c, h, nx=nx: wd(PP[nx], 16 + c * 8 + h, 64)
                            bP = b_PP[nx]; bPT = b_PP[nx]
                        for c in C2:
                            p7, bp7 = nextps()
                            for h in range(8):
                                mmr(p7[0:128, h * 128:(h + 1) * 128], PTcw(c, h), Zc[:, c * 8 + h, :], [bPTc, bZc], [bp7])
                            tt("dve", RR(Zn[:, blk(c), :]), p8v(p7, 0, 128), Zc[:, blk(c), :], ALU.add, [bp7, bZc], [bZn])
                        zi = 1 - zi
                    Zf = Zt[zi]; bZf = b_Zt[zi]
                    for c in C2:
                        p9, bp9 = nextps()
                        for h in range(8):
                            mmr(p9[0:128, h * 64:(h + 1) * 64], Zf[:, c * 8 + h, :], MT1[:, c, h, 64:128], [bZf, b_MT1], [bp9])
                            mmr(p9[0:128, 512 + h * 64:512 + (h + 1) * 64], Zf[:, c * 8 + h, :], BHt[:, c * 8 + h, :], [bZf, b_BHt], [bp9])
                        tt0("dve", RR(QT[:, c]), p8v(p9, 0, 64), AR[:, :, c, 64:128], ALU.add, [bp9, b_AR], [b_QT])
                        GLc = GL[:].rearrange("p (h c) -> p c h", c=2)[:, c, :, None].broadcast_to([64, 8, 64])
                        tt0("pool", DG[:, blk(c), :], idbc8, GLc, ALU.mult, [b_ident, b_GL], [b_DG])
                        tt("dve", RR(MM[:, blk(c), :]), p8v(p9, 512, 64), DG[:, blk(c), :], ALU.add, [bp9, b_DG], [b_MM])
                    for c in (range(1, -1, -1) if rev else range(2)):
                        sti = sti_[0]
                        ST = STt[sti]; bST = b_ST[sti]; STn = STt[1 - sti]; bSTn = b_ST[1 - sti]
                        p11, bp11 = nextps()
                        for h in range(8):
                            o_ = p11[0:128, h * 64:(h + 1) * 64]
                            k_ = c * 8 + h
                            mmr(o_, wd(ST, h, 64), QT[:, c, h, :], [bST, b_QT], [bp11], True, False)
                            mmr(o_, wd(Zf, k_, 128, 64), MT1[:, c, h, 64:128], [bZf, b_MT1], [bp11], False, False)
                            mmr(o_, wd(VT, k_, 64), MT2[:, c, h, 64:128], [b_VT, b_MT2], [bp11], False, True)
                        cp("act", YT[:, :, CS[c]], v3(p11, 64), [bp11], [b_YT])
                        p12, bp12 = nextps()
                        for h in range(8):
                            o_ = p12[0:128, h * 64:(h + 1) * 64]
                            k_ = c * 8 + h
                            mmr(o_, wd(MM, k_, 64), ST[:, h, :], [b_MM, bST], [bp12], True, False)
                            mmr(o_, wd(BHt, k_, 64), Zf[:, k_, 64:128], [b_BHt, bZf], [bp12], False, False)
                            mmr(o_, wd(KHt, k_, 64), VT[:, k_, :], [b_KHt, b_VT], [bp12], False, True)
                        cp("dve", RR(STn[:, 0:8, :]), v3(p12, 64), [bp12], [bSTn])
                        sti_[0] = 1 - sti
                    dma("sp", YD[d, :, :, tg:tg + NT2], YT[:], [b_YT], [b_YD])

                PIPE = os.environ.get('RW_NOPIPE') != '1'
                if PIPE:
                    prep(order[0], *SETS[0])
                for idx, ti in enumerate(order):
                    pend = []
                    if not PIPE:
                        prep(ti, *SETS[idx % 2])
                    elif idx + 1 < len(order):
                        mk.deferred = pend
                        prep(order[idx + 1], *SETS[(idx + 1) % 2])
                        mk.deferred = None
                    if os.environ.get('RW_PIPE_MODE') == 'start':
                        mk.replay(pend, len(pend))
                    chunk(ti, pend, *SETS[idx % 2])
                    mk.replay(pend, len(pend))
            mk.flush(final=True)

    if upto >= 1:
        rwkv_pass(0)
        rwkv_pass(1)

    def s5_pass():
        es, sb, ps = scope("s5_")
        with es:
            TWO_PI = 2.0 * math.pi
            pz = [ps("pz%d" % i, [128, 512]) for i in range(8)]
            b_pz = [Buf() for _ in range(8)]
            pctr = [0]

            def nextps():
                i = pctr[0] % 8
                pctr[0] += 1
                return pz[i], b_pz[i]

            NLV = 10
            identb = sb("identb", [64, 64], BF16)
            dsk = sb("dsk", [16, 32])
            SQr = sb("SQr", [64, NLV, 64]); SQi = sb("SQi", [64, NLV, 64]); SQin = sb("SQin", [64, NLV, 64])
            LTr = sb("LTr", [64, 8, 64, 16], BF16); LTi = sb("LTi", [64, 8, 64, 16], BF16)
            OTr = sb("OTr", [64, 8, 64, 16], BF16); OTn = sb("OTn", [64, 8, 64, 16], BF16)
            CRb = sb("CRb", [64, 64, 16], BF16); CInb = sb("CInb", [64, 64, 16], BF16)
            bt_ = Buf()
            es2, sb2, ps2_ = scope("s5t_")
            ones1 = sb2("ones1", [1, 64]); row = sb2("row", [1, 64])
            mk.op("pool", lambda E: E.memset(ones1[:], 1.0), (), [bt_])
            dma("sp", row[:], log_dt.rearrange("d g -> (d g)").rearrange("(o n) -> o n", o=1), (), [bt_])
            cp("dve", identb[:], ident[0:64, 0:64], [b_ident], [bt_])
            LR = sb2("LR", [64, 64]); LI = sb2("LI", [64, 64])
            dma("sp", LR[:].rearrange("p (d g) -> p d g", d=2), lam_re.rearrange("d g p -> p d g"), (), [bt_], slow=True)
            dma("act", LI[:].rearrange("p (d g) -> p d g", d=2), lam_im.rearrange("d g p -> p d g"), (), [bt_], slow=True)
            BR = sb2("BR", [64, 64, 16]); BI = sb2("BI", [64, 64, 16])
            dma("sp", BR[:].rearrange("p (d g) h -> p d g h", d=2), b_re.rearrange("d g p h -> p d g h"), (), [bt_])
            dma("act", BI[:].rearrange("p (d g) h -> p d g h", d=2), b_im.rearrange("d g p h -> p d g h"), (), [bt_])
            CR = sb2("CR", [64, 64, 16]); CI = sb2("CI", [64, 64, 16])
            cnat = sb2("cnat", [128, 8, 64])
            for (src, dst) in ((c_re, CR), (c_im, CI)):
                dma("sp", cnat[:], src.rearrange("d g h p -> (d g h) p").rearrange("(k q) p -> q k p", q=128), [bt_], [bt_])
                for k in range(8):
                    pq, bq = nextps()
                    mk.op("pe", lambda E, pq=pq, k=k: E.transpose(out=pq[0:64, 0:128], in_=cnat[:, k, :], identity=ident[:]),
                          [bt_, b_ident], [bq])
                    cp("dve", dst[:, k * 8:(k + 1) * 8, :], pq[0:64, 0:128].rearrange("p (g h) -> p g h", h=16), [bq], [bt_])
            dma("sp", dsk[:], d_skip.rearrange("(g h) -> h g", h=16), (), [bt_], slow=True)
            DT = sb2("DT", [64, 64])
            pq, bq = nextps()
            mm(pq[0:64, 0:64], ones1[:], row[:], True, True, [bt_], [bq])
            act(DT[:], pq[0:64, 0:64], AF.Exp, [bq], [bt_])

            def T64(name):
                return sb2(name, [64, 64])
            ZR = T64("ZR"); ZI = T64("ZI"); EPs = T64("EPs"); COS = T64("COS"); SIN = T64("SIN")
            tA = T64("tA"); tB = T64("tB"); tC = T64("tC"); tI = sb2("tI", [64, 64], mybir.dt.int32)
            tt("dve", ZR[:], LR[:], DT[:], ALU.mult, [bt_], [bt_])
            tt("dve", ZI[:], LI[:], DT[:], ALU.mult, [bt_], [bt_])
            act(EPs[:], ZR[:], AF.Exp, [bt_], [bt_])
            for (dst, offs) in ((SIN, 64.0), (COS, 64.25)):
                ts("dve", tA[:], ZI[:], 1.0 / TWO_PI, offs, ALU.mult, ALU.add, [bt_], [bt_])
                cp("dve", tI[:], tA[:], [bt_], [bt_])
                cp("dve", tB[:], tI[:], [bt_], [bt_])
                tt("dve", tA[:], tA[:], tB[:], ALU.subtract, [bt_], [bt_])
                ts("dve", tB[:], tA[:], 0.5, None, ALU.is_gt, None, [bt_], [bt_])
                tt("dve", tA[:], tA[:], tB[:], ALU.subtract, [bt_], [bt_])
                act(dst[:], tA[:], AF.Sin, [bt_], [bt_], scale=TWO_PI)
            PWr = sb2("PWr", [64, 9, 64]); PWi = sb2("PWi", [64, 9, 64])
            mk.op("pool", lambda E: E.memset(PWr[:, 0, :], 1.0), (), [bt_])
            mk.op("pool", lambda E: E.memset(PWi[:, 0, :], 0.0), (), [bt_])
            tt("dve", PWr[:, 1, :], EPs[:], COS[:], ALU.mult, [bt_], [bt_])
            tt("dve", PWi[:, 1, :], EPs[:], SIN[:], ALU.mult, [bt_], [bt_])

            def cmul(or_, oi_, ar, ai, br, bi, n3=None):
                tt("dve", tA[:], ai, bi, ALU.mult, [bt_], [bt_])
                tt("dve", tB[:], ai, br, ALU.mult, [bt_], [bt_])
                tt("dve", tC[:], ar, br, ALU.mult, [bt_], [bt_])
                tt("dve", or_, tC[:], tA[:], ALU.subtract, [bt_], [bt_])
                tt("dve", tC[:], ar, bi, ALU.mult, [bt_], [bt_])
                tt("dve", oi_, tC[:], tB[:], ALU.add, [bt_], [bt_])
            for j in range(2, 9):
                cmul(PWr[:, j, :], PWi[:, j, :], PWr[:, j - 1, :], PWi[:, j - 1, :], PWr[:, 1, :], PWi[:, 1, :])
            NLV = 10
            cp("dve", SQr[:, 0, :], PWr[:, 8, :], [bt_], [bt_]); cp("dve", SQi[:, 0, :], PWi[:, 8, :], [bt_], [bt_])
            for k in range(1, NLV):
                cmul(SQr[:, k, :], SQi[:, k, :], SQr[:, k - 1, :], SQi[:, k - 1, :], SQr[:, k - 1, :], SQi[:, k - 1, :])
            ts("dve", SQin[:], SQi[:], -1.0, None, ALU.mult, None, [bt_], [bt_])
            CFr = T64("CFr"); CFi = T64("CFi"); DEN = T64("DEN"); NR = T64("NR")
            ts("dve", NR[:], PWr[:, 1, :], -1.0, None, ALU.add, None, [bt_], [bt_])
            tt("dve", tA[:], LR[:], LR[:], ALU.mult, [bt_], [bt_])
            tt("dve", tB[:], LI[:], LI[:], ALU.mult, [bt_], [bt_])
            tt("dve", DEN[:], tA[:], tB[:], ALU.add, [bt_], [bt_])
            mk.op("dve", lambda E: E.reciprocal(out=DEN[:], in_=DEN[:]), [bt_], [bt_])
            tt("dve", tA[:], NR[:], LR[:], ALU.mult, [bt_], [bt_])
            tt("dve", tB[:], PWi[:, 1, :], LI[:], ALU.mult, [bt_], [bt_])
            tt("dve", tA[:], tA[:], tB[:], ALU.add, [bt_], [bt_])
            tt("dve", CFr[:], tA[:], DEN[:], ALU.mult, [bt_], [bt_])
            tt("dve", tA[:], PWi[:, 1, :], LR[:], ALU.mult, [bt_], [bt_])
            tt("dve", tB[:], NR[:], LI[:], ALU.mult, [bt_], [bt_])
            tt("dve", tA[:], tA[:], tB[:], ALU.subtract, [bt_], [bt_])
            tt("dve", CFi[:], tA[:], DEN[:], ALU.mult, [bt_], [bt_])
            BbR = sb2("BbR", [64, 64, 16]); BbI = sb2("BbI", [64, 64, 16])
            X1t = sb2("X1t", [64, 64, 16]); X2t = sb2("X2t", [64, 64, 16])

            def b16(t2):
                return t2[:, :, None].broadcast_to([64, 64, 16])

            def cmul3(or_, oi_neg, ar2, ai2, br3, bi3, e1="dve", e2="pool"):
                tt(e1, X1t[:], br3, b16(ar2), ALU.mult, [bt_], [bt_])
                tt(e1, X2t[:], bi3, b16(ai2), ALU.mult, [bt_], [bt_])
                tt(e1, or_, X1t[:], X2t[:], ALU.subtract, [bt_], [bt_])
                tt(e1, X1t[:], bi3, b16(ar2), ALU.mult, [bt_], [bt_])
                tt(e1, X2t[:], br3, b16(ai2), ALU.mult, [bt_], [bt_])
                if oi_neg[1]:
                    tt(e1, X1t[:], X1t[:], X2t[:], ALU.add, [bt_], [bt_])
                    ts(e1, oi_neg[0], X1t[:], -1.0, None, ALU.mult, None, [bt_], [bt_])
                else:
                    tt(e1, oi_neg[0], X1t[:], X2t[:], ALU.add, [bt_], [bt_])
            cmul3(BbR[:], (BbI[:], False), CFr[:], CFi[:], BR[:], BI[:])
            for j in range(8):
                cmul3(LTr[:, j], (LTi[:, j], False), PWr[:, j, :], PWi[:, j, :], BbR[:], BbI[:])
                cmul3(OTr[:, j], (OTn[:, j], True), PWr[:, j + 1, :], PWi[:, j + 1, :], CR[:], CI[:])
            cp("dve", CRb[:], CR[:], [bt_], [bt_]); ts("dve", CInb[:], CI[:], -1.0, None, ALU.mult, None, [bt_], [bt_])

            mk.flush(final=True)
            es2.close()
            KTg = [sb("KTg%d" % i, [16, 15, 16], BF16) for i in range(2)]; b_KTg = [Buf(), Buf()]
            CTg = [sb("CTg%d" % i, [16, 32, 64], BF16) for i in range(2)]; b_CTg = [Buf(), Buf()]
            UGN = 2048
            ugs = [sb("ug%d" % i, [16, UGN]) for i in range(2)]; b_ugs = [Buf(), Buf()]; ugc = [0]
            ub = [sb("ub%d" % i, [16, 8, 1024], BF16) for i in range(2)]; b_ub = [Buf(), Buf()]
            ygs = [sb("yg0", [16, 4096])] * 2; b_ygs = [Buf()] * 2; ygc = [0]
            NBM = 1024
            Wt = [[[sb("W%d%d%d" % (pp, d, c), [64, NBM + 1]) for c in range(2)] for d in range(2)] for pp in range(2)]
            b_Wt = [[Buf() for d in range(2)] for pp in range(2)]
            Sb = [[[sb("Sb%d%d%d" % (i, d, c), [64, NBM + 1], BF16) for c in range(2)] for d in range(2)] for i in range(2)]
            b_Sb = [Buf(), Buf()]
            for pp in range(2):
                for d in range(2):
                    for c in range(2):
                        mk.op("pool", lambda E, t=Wt[pp][d][c]: E.memset(t[:], 0.0), (), [b_Wt[pp][d]])
            id16 = ident[0:16, 0:16]

            def gconsts(g):
                par = g % 2
                pk, bpk = nextps()
                for idx in range(15):
                    if idx == 0:
                        terms = [(0, 0), (1, 0)]
                    elif idx < 8:
                        terms = [(0, idx)]
                    else:
                        terms = [(1, idx - 7)]
                    n = 0
                    for (d, tau) in terms:
                        gi = d * 32 + g
                        mm(pk[0:16, idx * 16:(idx + 1) * 16], LTr[:, tau, gi, :], CRb[:, gi, :], n == 0, False, [bt_], [bpk]); n += 1
                        mm(pk[0:16, idx * 16:(idx + 1) * 16], LTi[:, tau, gi, :], CInb[:, gi, :], False, n == 2 * len(terms) - 1, [bt_], [bpk]); n += 1
                cp("dve", KTg[par][:].rearrange("p a b -> p (a b)"), pk[0:16, 0:240], [bpk], [b_KTg[par]])
                stt(KTg[par][:, 0, :], id16, dsk[:, g:g + 1], KTg[par][:, 0, :], ALU.mult, ALU.add, [b_KTg[par], bt_, b_ident], [b_KTg[par]])
                for q in range(4):
                    pc_, bpc = nextps()
                    for j in range(8):
                        i = q * 8 + j
                        d = i // 16; s_ = (i // 2) % 8; c = i % 2
                        e_ = (7 - s_) if d == 0 else s_
                        src = (LTr if c == 0 else LTi)[:, e_, d * 32 + g, :]
                        mm(pc_[0:16, j * 64:(j + 1) * 64], src, identb[:], True, True, [bt_], [bpc])
                    cp("act", CTg[par][:, q * 8:(q + 1) * 8, :].rearrange("p a b -> p (a b)"),
                       pc_[0:16, 0:512], [bpc], [b_CTg[par]])

            def dims(s):
                toff, coff, T = SEQ[s]
                nblk = T // 8
                BW = min(512, nblk)
                return toff, coff, T, nblk, BW, 8 * BW, nblk // BW

            def front(g, s, st):
                par = g % 2
                toff, coff, T, nblk, BW, TW, nbt = dims(s)
                nlv = int(math.log2(nblk))
                UG = min(UGN, T)
                for hf in range(T // UG):
                    ug = ugs[ugc[0] % 2]; b_ug = b_ugs[ugc[0] % 2]; ugc[0] += 1
                    dma("sp" if hf % 2 == 0 else "act", ug[:, 0:UG],
                        Pscr[1920 + 16 * g:1936 + 16 * g, coff + 1 + hf * UG:coff + 1 + (hf + 1) * UG], [b_P], [b_ug])
                    cp("act", ub[st][:, :, hf * (UG // 8):(hf + 1) * (UG // 8)], ug[:, 0:UG].rearrange("p (b s) -> p s b", s=8),
                       [b_ug], [b_ub[st]])
                if nblk < NBM:
                    for d in range(2):
                        for c in range(2):
                            mk.op("pool", lambda E, t=Wt[0][d][c]: E.memset(t[:], 0.0), (), [b_Wt[0][d]])
                            mk.op("pool", lambda E, t=Wt[1][d][c]: E.memset(t[:], 0.0), (), [b_Wt[1][d]])
                for bt in range(nbt):
                    for d in range(2):
                        for c in range(2):
                            pw_, bpw = nextps()
                            for s_ in range(8):
                                mm(pw_[0:64, 0:BW], CTg[par][:, (d * 8 + s_) * 2 + c, :],
                                   ub[st][:, s_, bt * BW:(bt + 1) * BW],
                                   s_ == 0, s_ == 7, [b_CTg[par], b_ub[st]], [bpw])
                            o0 = bt * BW + (1 if d == 0 else 0)
                            cp("act", Wt[0][d][c][:, o0:o0 + BW], pw_[0:64, 0:BW], [bpw], [b_Wt[0][d]])
                cur = 0
                for k in range(nlv):
                    sh = 1 << k
                    n_ = nblk - sh
                    for d in range(2):
                        gi = d * 32 + g
                        lo = 1 if d == 0 else 0
                        Wc = Wt[cur][d]; Wn = Wt[1 - cur][d]
                        bWc = b_Wt[cur][d]; bWn = b_Wt[1 - cur][d]
                        if d == 0:
                            dst = slice(lo + sh, lo + nblk); srcs = slice(lo, lo + n_); keep = slice(lo, lo + sh)
                        else:
                            dst = slice(lo, lo + n_); srcs = slice(lo + sh, lo + nblk); keep = slice(lo + n_, lo + nblk)
                        ar = SQr[:, k, gi:gi + 1]; ai = SQi[:, k, gi:gi + 1]; ain = SQin[:, k, gi:gi + 1]
                        stt(Wn[0][:, dst], Wc[0][:, srcs], ar, Wc[0][:, dst], ALU.mult, ALU.add, [bWc, bt_], [bWn])
                        stt(Wn[1][:, dst], Wc[1][:, srcs], ar, Wc[1][:, dst], ALU.mult, ALU.add, [bWc, bt_], [bWn])
                        stt(Wn[0][:, dst], Wc[1][:, srcs], ain, Wn[0][:, dst], ALU.mult, ALU.add, [bWc, bWn, bt_], [bWn])
                        stt(Wn[1][:, dst], Wc[0][:, srcs], ai, Wn[1][:, dst], ALU.mult, ALU.add, [bWc, bWn, bt_], [bWn])
                        cp("pool", Wn[0][:, keep], Wc[0][:, keep], [bWc], [bWn])
                        cp("pool", Wn[1][:, keep], Wc[1][:, keep], [bWc], [bWn])
                    cur = 1 - cur
                for d in range(2):
                    for c in range(2):
                        if d == 0:
                            cp("act", Sb[st][d][c][:, 1:nblk + 1], Wt[cur][d][c][:, 1:nblk + 1], [b_Wt[cur][d]], [b_Sb[st]])
                            mk.op("pool", lambda E, t=Sb[st][d][c]: E.memset(t[:, 0:1], 0.0), (), [b_Sb[st]])
                        else:
                            cp("act", Sb[st][d][c][:, 0:nblk], Wt[cur][d][c][:, 0:nblk], [b_Wt[cur][d]], [b_Sb[st]])
                            mk.op("pool", lambda E, t=Sb[st][d][c], nblk=nblk: E.memset(t[:, nblk:nblk + 1], 0.0), (), [b_Sb[st]])

            def back(g, s, st):
                par = g % 2
                toff, coff, T, nblk, BW, TW, nbt = dims(s)
                for bt in range(nbt):
                    yg = ygs[ygc[0] % 2]; b_yg = b_ygs[ygc[0] % 2]; ygc[0] += 1
                    ubv = ub[st][:, :, bt * BW:(bt + 1) * BW]
                    ygv = yg[:, 0:TW].rearrange("p (b s) -> p s b", s=8)
                    for t in range(8):
                        py, bpy = nextps()
                        for s_ in range(8):
                            idx = 0 if s_ == t else ((t - s_) if s_ < t else (7 + s_ - t))
                            mm(py[0:16, 0:BW], KTg[par][:, idx, :], ubv[:, s_, :], s_ == 0, False, [b_KTg[par], b_ub[st]], [bpy])
                        b0 = bt * BW
                        mm(py[0:16, 0:BW], OTr[:, t, g, :], Sb[st][0][0][:, b0:b0 + BW], False, False, [bt_, b_Sb[st]], [bpy])
                        mm(py[0:16, 0:BW], OTn[:, t, g, :], Sb[st][0][1][:, b0:b0 + BW], False, False, [bt_, b_Sb[st]], [bpy])
                        mm(py[0:16, 0:BW], OTr[:, 7 - t, 32 + g, :], Sb[st][1][0][:, b0 + 1:b0 + 1 + BW], False, False, [bt_, b_Sb[st]], [bpy])
                        mm(py[0:16, 0:BW], OTn[:, 7 - t, 32 + g, :], Sb[st][1][1][:, b0 + 1:b0 + 1 + BW], False, True, [bt_, b_Sb[st]], [bpy])
                        cp("act", ygv[:, t, :], py[0:16, 0:BW], [bpy], [b_yg])
                    dma("sp", YS[16 * g:16 * g + 16, toff + bt * TW:toff + (bt + 1) * TW], yg[:, 0:TW], [b_yg], [b_YS])

            units = [(g, s) for g in range(32) for s in range(NS)]
            prev = None
            for ui, (g, s) in enumerate(units):
                if s == 0:
                    gconsts(g)
                front(g, s, ui % 2)
                if prev is not None:
                    back(prev[0], prev[1], (ui - 1) % 2)
                prev = (g, s)
            back(prev[0], prev[1], (len(units) - 1) % 2)
            mk.flush(final=True)

    if upto >= 2:
        s5_pass()

    def mix_pass():
        es, sb, ps = scope("mx_")
        with es:
            N = 256
            pz = [ps("pz%d" % i, [128, 512]) for i in range(8)]
            b_pz = [Buf() for _ in range(8)]
            pctr = [0]

            def nextps():
                i = pctr[0] % 8
                pctr[0] += 1
                return pz[i], b_pz[i]
            bc_ = Buf()
            o64 = sb("o64", [64, 64])
            mk.op("pool", lambda E: E.memset(o64[:], 1.0 / 64.0), (), [bc_])
            ones_bf = sb("ones_bf", [128, 128], BF16)
            mk.op("pool", lambda E: E.memset(ones_bf[:], 1.0), (), [bc_])
            lg = sb("lg", [64, 8]); lb = sb("lb", [64, 8])
            dma("sp", lg[:], lnx_g.rearrange("(h p) -> p h", p=64), (), [bc_], slow=True)
            dma("sp", lb[:], lnx_b.rearrange("(h p) -> p h", p=64), (), [bc_], slow=True)
            g2t = sb("g2t", [128, 512]); dma("sp", g2t[:], g2, (), [bc_])
            mug = sb("mug", [128, 1]); hmg = sb("hmg", [128, 1]); omg = sb("omg", [128, 1])
            dma("sp", mug[:], mu_shift[1792:1920].rearrange("(p o) -> p o", o=1), (), [bc_], slow=True)
            ts("dve", hmg[:], mug[:], 0.5, None, ALU.mult, None, [bc_], [bc_])
            ts("dve", omg[:], mug[:], -1.0, 1.0, ALU.mult, ALU.add, [bc_], [bc_])
            bgl = sb("bgl", [128, 4]); s5g = sb("s5g", [128, 4])
            dma("sp", bgl[:], b_glu.rearrange("(q p) -> p q", p=128), (), [bc_], slow=True)
            dma("sp", s5g[:], s5_out_g.rearrange("(q p) -> p q", p=128), (), [bc_], slow=True)
            wst = sb("wst", [128, 4, 128])
            mk.op("pool", lambda E: E.memset(wst[:], 0.0), (), [bc_])
            for g in range(32):
                r0 = (g % 8) * 16
                dma("sp" if g % 2 == 0 else "act", wst[r0:r0 + 16, g // 8, r0:r0 + 16], w_glu[g], [bc_], [bc_])
            Wbd = sb("Wbd", [128, 4, 128], BF16)
            cp("dve", Wbd[:], wst[:], [bc_], [bc_])
            wo_r = sb("wo_r", [64, 8, D], BF16); wo_s = sb("wo_s", [128, 4, D], BF16)
            stg = [sb("stg%d" % i, [128, D]) for i in range(2)]; b_stg = [Buf(), Buf()]
            for h in range(8):
                st_ = stg[h % 2]; bs_ = b_stg[h % 2]
                dma("sp", st_[0:64, :], w_out[h * 64:(h + 1) * 64, :], (), [bs_])
                cp("pool", wo_r[:, h, :], st_[0:64, :], [bs_], [bc_])
            for q in range(4):
                st_ = stg[q % 2]; bs_ = b_stg[q % 2]
                dma("sp", st_[:], w_out[512 + q * 128:512 + (q + 1) * 128, :], (), [bs_])
                cp("pool", wo_s[:, q, :], st_[:], [bs_], [bc_])

            def T4(name, dt=F32):
                return sb(name, [64, 8, N], dt), Buf()
            YF, b_YF = T4("YF"); YB, b_YB = T4("YB"); BF_, b_BF = T4("BF"); BB, b_BB = T4("BB")
            Ym, b_Ym = T4("Ym"); SQt, b_SQt = T4("SQt"); RS, b_RS = T4("RS")
            yr, b_yr = T4("yr", BF16)
            XG = sb("XG", [128, N + 2]); b_XG = Buf(); tw = sb("tw", [128, N]); b_tw = Buf()
            sg = sb("sg", [128, N]); b_sg = Buf()
            S5 = sb("S5", [128, 4, N]); b_S5 = Buf(); Z1 = sb("Z1", [128, 4, N]); b_Z1 = Buf()
            Z2 = sb("Z2", [128, 4, N]); b_Z2 = Buf(); Zb = sb("Zb", [128, 4, N], BF16); b_Zb = Buf()
            r2 = sb("r2", [128, N]); b_r2 = Buf()
            ysb = sb("ysb", [128, 4, N], BF16); b_ysb = Buf()
            xTt = sb("xTt", [128, 8, N]); b_xTt = Buf(); X1t = sb("X1t", [128, 8, N]); b_X1t = Buf()

            def bc(t, n):
                return t[:, :, None].broadcast_to([64, 8, n])

            def fl(t):
                return t[:].rearrange("p h t -> p (h t)")
            for s, (toff, coff, T) in enumerate(SEQ):
                for ti in range(T // N):
                    tl = ti * N; tg = toff + tl; c0 = coff + tl
                    dma("sp", YF[:], YD[0, :, :, tg:tg + N], [b_YD], [b_YF])
                    dma("act", YB[:], YD[1, :, :, tg:tg + N], [b_YD], [b_YB])
                    dma("sp", BF_[:], BD[0, :, :, tg:tg + N], [b_BD], [b_BF])
                    dma("act", BB[:], BD[1, :, :, tg:tg + N], [b_BD], [b_BB])
                    dma("sp", XG[:], Pscr[1792:1920, c0:c0 + N + 2], [b_P], [b_XG])
                    dma("act", S5[:], YS[:, tg:tg + N].rearrange("(q p) t -> p q t", p=128), [b_YS], [b_S5])
                    dma("sp", xTt[:], XT[:, tg:tg + N].rearrange("(k p) t -> p k t", p=128), [b_XT], [b_xTt])
                    pendS = []
                    mk.deferred = pendS
                    K0 = 2.0 * math.sqrt(2.0 / math.pi)
                    tt("pool", Z1[:], S5[:], S5[:], ALU.mult, [b_S5], [b_Z1])
                    ts("dve", Z1[:], Z1[:], 0.044715, 1.0, ALU.mult, ALU.add, [b_Z1], [b_Z1])
                    tt("pool", Z1[:], Z1[:], S5[:], ALU.mult, [b_Z1, b_S5], [b_Z1])
                    act(Z1[:], Z1[:], AF.Sigmoid, [b_Z1], [b_Z1], scale=K0)
                    tt("pool", Z1[:], Z1[:], S5[:], ALU.mult, [b_Z1, b_S5], [b_Z1])
                    cp("dve", Zb[:], Z1[:], [b_Z1], [b_Zb])
                    for q in range(4):
                        pm_, bpm = nextps()
                        mm(pm_[:, 0:N], Wbd[:, q, :], Zb[:, q, :], True, True, [bc_, b_Zb], [bpm])
                        mk.op("act", lambda E, pm_=pm_, q=q: E.activation(out=Z2[:, q, :], in_=pm_[:, 0:N], func=AF.Sigmoid,
                                                                        bias=bgl[:, q:q + 1], scale=1.0), [bpm, bc_], [b_Z2])
                    tt("pool", Z2[:], Z2[:], Z1[:], ALU.mult, [b_Z2, b_Z1], [b_Z2])
                    tt("pool", Zb[:], Z2[:], Z2[:], ALU.mult, [b_Z2], [b_Zb])
                    pm_, bpm = nextps()
                    for q in range(4):
                        mm(pm_[:, 0:N], ones_bf[:], Zb[:, q, :], q == 0, q == 3, [bc_, b_Zb], [bpm])
                    act(r2[:], pm_[:, 0:N], AF.Sqrt, [bpm], [b_r2], bias=RMS_EPS, scale=1.0 / 512.0)
                    mk.op("dve", lambda E: E.reciprocal(out=r2[:], in_=r2[:]), [b_r2], [b_r2])
                    for q in range(4):
                        stt(ysb[:, q, :], Z2[:, q, :], s5g[:, q:q + 1], r2[:], ALU.mult, ALU.mult, [b_Z2, b_r2, bc_], [b_ysb])
                    mk.deferred = None
                    tt("pool", Ym[:], YF[:], YB[:], ALU.add, [b_YF, b_YB], [b_Ym])
                    mk.replay(pendS, 2)
                    for j in range(4):
                        pm_, bpm = nextps()
                        mm(pm_[0:64, :], o64[:], fl(Ym)[:, j * 512:(j + 1) * 512], True, True, [bc_, b_Ym], [bpm])
                        tt("dve", fl(YF)[:, j * 512:(j + 1) * 512], fl(Ym)[:, j * 512:(j + 1) * 512], pm_[0:64, :],
                           ALU.subtract, [bpm, b_Ym], [b_YF])
                    tt("pool", SQt[:], YF[:], YF[:], ALU.mult, [b_YF], [b_SQt])
                    mk.replay(pendS, 2)
                    for j in range(4):
                        pm_, bpm = nextps()
                        mm(pm_[0:64, :], o64[:], fl(SQt)[:, j * 512:(j + 1) * 512], True, True, [bc_, b_SQt], [bpm])
                        act(fl(RS)[:, j * 512:(j + 1) * 512], pm_[0:64, :], AF.Sqrt, [bpm], [b_RS], bias=LNX_EPS)
                    mk.replay(pendS, 2)
                    mk.op("dve", lambda E: E.reciprocal(out=RS[:], in_=RS[:]), [b_RS], [b_RS])
                    mk.replay(pendS, 2)
                    tt("pool", Ym[:], YF[:], RS[:], ALU.mult, [b_YF, b_RS], [b_Ym])
                    mk.replay(pendS, 2)
                    tt("pool", Ym[:], Ym[:], bc(lg, N), ALU.mult, [b_Ym, bc_], [b_Ym])
                    mk.replay(pendS, 2)
                    tt("pool", Ym[:], Ym[:], bc(lb, N), ALU.add, [b_Ym, bc_], [b_Ym])
                    mk.replay(pendS, 2)
                    tt("pool", Ym[:], Ym[:], BF_[:], ALU.add, [b_Ym, b_BF], [b_Ym])
                    mk.replay(pendS, 2)
                    tt("pool", Ym[:], Ym[:], BB[:], ALU.add, [b_Ym, b_BB], [b_Ym])
                    mk.replay(pendS, 2)
                    tt("dve", tw[:], XG[:, 0:N], XG[:, 2:N + 2], ALU.add, [b_XG], [b_tw])
                    mk.replay(pendS, 2)
                    ts("dve", tw[:], tw[:], hmg[:, 0:1], None, ALU.mult, None, [b_tw, bc_], [b_tw])
                    mk.replay(pendS, 2)
                    stt(sg[:], XG[:, 1:N + 1], omg[:, 0:1], tw[:], ALU.mult, ALU.add, [b_XG, b_tw, bc_], [b_sg])
                    mk.replay(pendS, 2)
                    act(sg[:], sg[:], AF.Sigmoid, [b_sg], [b_sg])
                    mk.replay(pendS, 2)
                    for h2_ in range(4):
                        pm_, bpm = nextps()
                        for j in range(2):
                            h = 2 * h2_ + j
                            mm(pm_[0:64, j * N:(j + 1) * N], g2t[:, h * 64:(h + 1) * 64], sg[:], True, True, [bc_, b_sg], [bpm])
                        tt("dve", yr[:, 2 * h2_:2 * h2_ + 2, :], pm_[0:64, :].rearrange("p (h n) -> p h n", n=N),
                           Ym[:, 2 * h2_:2 * h2_ + 2, :], ALU.mult, [bpm, b_Ym], [b_yr])
                    mk.replay(pendS, len(pendS))
                    for dm in range(8):
                        pm_, bpm = nextps()
                        for h in range(8):
                            mm(pm_[:, 0:N], wo_r[:, h, dm * 128:(dm + 1) * 128], yr[:, h, :], h == 0, False, [bc_, b_yr], [bpm])
                        for q in range(4):
                            mm(pm_[:, 0:N], wo_s[:, q, dm * 128:(dm + 1) * 128], ysb[:, q, :], False, q == 3, [bc_, b_ysb], [bpm])
                        stt(X1t[:, dm, :], pm_[:, 0:N], modT[:, 16 + dm, s:s + 1], xTt[:, dm, :], ALU.mult, ALU.add,
                            [bpm, b_mod, b_xTt], [b_X1t])
                    dma("sp", X1[:, tg:tg + N].rearrange("(k p) t -> p k t", p=128), X1t[:], [b_X1t], [b_X1])
            mk.flush(final=True)

    if upto >= 3:
        mix_pass()

    def ffn_pass():
        es, sb, ps = scope("ff_")
        with es:
            N = 256
            pz = [ps("pz%d" % i, [128, 512]) for i in range(8)]
            b_pz = [Buf() for _ in range(8)]
            pctr = [0]

            def nextps():
                i = pctr[0] % 8
                pctr[0] += 1
                return pz[i], b_pz[i]
            bc_ = Buf()
            ones_bf = sb("ones_bf", [128, 128], BF16)
            mk.op("pool", lambda E: E.memset(ones_bf[:], 1.0), (), [bc_])
            n2g = sb("n2g", [128, 8]); fg = sb("fg", [128, 8]); sc2 = sb("sc2", [128, 8, NS])
            dma("sp", n2g[:], norm2_g.rearrange("(k p) -> p k", p=128), (), [bc_], slow=True)
            dma("sp", fg[:], final_g.rearrange("(k p) -> p k", p=128), (), [bc_], slow=True)
            for s in range(NS):
                stt(sc2[:, :, s], modT[:, 32:40, s], 1.0, n2g[:], ALU.add, ALU.mult, [b_mod, bc_], [bc_])
            w1 = sb("w1", [128, 8, DFF], BF16); w3 = sb("w3", [128, 8, DFF], BF16); w2_ = sb("w2_", [128, 22, D], BF16)
            stg = [sb("stg%d" % i, [128, 1408]) for i in range(2)]; b_stg = [Buf(), Buf()]
            n = 0
            for (src, dst) in ((w_ff1, w1), (w_ff3, w3)):
                for k in range(8):
                    for hf in range(2):
                        st_ = stg[n % 2]; bs_ = b_stg[n % 2]; n += 1
                        dma("sp" if n % 2 else "act", st_[:], src[k * 128:(k + 1) * 128, hf * 1408:(hf + 1) * 1408], (), [bs_])
                        cp("pool" if n % 2 else "dve", dst[:, k, hf * 1408:(hf + 1) * 1408], st_[:], [bs_], [bc_])
            for k in range(22):
                st_ = stg[n % 2]; bs_ = b_stg[n % 2]; n += 1
                dma("sp" if n % 2 else "act", st_[:, 0:D], w_ff2[k * 128:(k + 1) * 128, :], (), [bs_])
                cp("pool" if n % 2 else "dve", w2_[:, k, :], st_[:, 0:D], [bs_], [bc_])
            FS = []
            for i_ in range(2):
                FS.append(dict(X1t=sb("X1t%d" % i_, [128, 8, N]), b_X1t=Buf(), h2=sb("h2%d" % i_, [128, 8, N], BF16), b_h2=Buf()))
            sq = sb("sq", [128, 8, N], BF16); b_sq = Buf()
            rstd = sb("rstd", [128, N]); b_rstd = Buf(); tmp = sb("tmp", [128, N]); b_tmp = Buf()
            sq2, b_sq2 = sq, b_sq; rstd2 = sb("rstd2", [128, N]); b_rstd2 = Buf()
            fm = sb("fm", [128, 22, N], BF16); b_fm = Buf()
            av = [sb("av%d" % i, [128, N]) for i in range(2)]; b_av = [Buf(), Buf()]
            X2t = sb("X2t", [128, 8, N]); b_X2t = Buf()
            ytm = sb("ytm", [128, 2, D]); b_ytm = Buf()

            def rms_bc(src, bsrc, sq_, bsq_, rs_, brs_):
                for dc in range(8):
                    act(sq_[:, dc, :], src[:, dc, :], AF.Square, [bsrc], [bsq_])
                pm_, bpm = nextps()
                for dc in range(8):
                    mm(pm_[:, 0:N], ones_bf[:], sq_[:, dc, :], dc == 0, dc == 7, [bc_, bsq_], [bpm])
                act(rs_[:], pm_[:, 0:N], AF.Sqrt, [bpm], [brs_], bias=RMS_EPS, scale=1.0 / D)
                mk.op("dve", lambda E: E.reciprocal(out=rs_[:], in_=rs_[:]), [brs_], [brs_])

            tiles = [(s_, tg_) for s_, (toff, coff, T) in enumerate(SEQ) for tg_ in range(toff, toff + T, N)]

            def ffront(s, tg, F_):
                X1t, b_X1t, h2, b_h2 = F_["X1t"], F_["b_X1t"], F_["h2"], F_["b_h2"]
                dma("sp", X1t[:], X1[:, tg:tg + N].rearrange("(k p) t -> p k t", p=128), [b_X1], [b_X1t])
                rms_bc(X1t, b_X1t, sq, b_sq, rstd, b_rstd)
                for dc in range(8):
                    tt("dve", tmp[:], X1t[:, dc, :], rstd[:], ALU.mult, [b_X1t, b_rstd], [b_tmp])
                    ts("dve", h2[:, dc, :], tmp[:], sc2[:, dc, s:s + 1], modT[:, 24 + dc, s:s + 1], ALU.mult, ALU.add,
                       [b_tmp, bc_, b_mod], [b_h2])

            def fback(s, tg, F_, pend):
                X1t, b_X1t, h2, b_h2 = F_["X1t"], F_["b_X1t"], F_["h2"], F_["b_h2"]
                for mc in range(22):
                    p1, bp1 = nextps()
                    for dc in range(8):
                        mm(p1[:, 0:N], w1[:, dc, mc * 128:(mc + 1) * 128], h2[:, dc, :], dc == 0, dc == 7, [bc_, b_h2], [bp1])
                    p3, bp3 = nextps()
                    for dc in range(8):
                        mm(p3[:, 0:N], w3[:, dc, mc * 128:(mc + 1) * 128], h2[:, dc, :], dc == 0, dc == 7, [bc_, b_h2], [bp3])
                    a_ = av[mc % 2]; ba_ = b_av[mc % 2]
                    act(a_[:], p1[:, 0:N], AF.Silu, [bp1], [ba_])
                    tt("dve", fm[:, mc, :], a_[:], p3[:, 0:N], ALU.mult, [ba_, bp3], [b_fm])
                    mk.replay(pend, 2)
                mk.replay(pend, len(pend))
                for dm in range(8):
                    pm_, bpm = nextps()
                    for mc in range(22):
                        mm(pm_[:, 0:N], w2_[:, mc, dm * 128:(dm + 1) * 128], fm[:, mc, :], mc == 0, mc == 21, [bc_, b_fm], [bpm])
                    stt(X2t[:, dm, :], pm_[:, 0:N], modT[:, 40 + dm, s:s + 1], X1t[:, dm, :], ALU.mult, ALU.add,
                        [bpm, b_mod, b_X1t], [b_X2t])
                rms_bc(X2t, b_X2t, sq2, b_sq2, rstd2, b_rstd2)
                for dc in range(8):
                    stt(X2t[:, dc, :], X2t[:, dc, :], fg[:, dc:dc + 1], rstd2[:], ALU.mult, ALU.mult,
                        [b_X2t, bc_, b_rstd2], [b_X2t])
                for j in range(N // 128):
                    for dq in range(2):
                        pm_, bpm = nextps()
                        for k in range(4):
                            dc = dq * 4 + k
                            mk.op("pe", lambda E, pm_=pm_, dc=dc, j=j, k=k: E.transpose(
                                out=pm_[:, k * 128:(k + 1) * 128], in_=X2t[:, dc, j * 128:(j + 1) * 128], identity=ident[:]),
                                [b_X2t, b_ident], [bpm])
                        cp("act" if dq == 0 else "dve", ytm[:, j, dq * 512:(dq + 1) * 512], pm_[:], [bpm], [b_ytm])
                dma("sp", y_out[tg:tg + N, :].rearrange("(j p) d -> p j d", p=128), ytm[:], [b_ytm], [])

            ffront(tiles[0][0], tiles[0][1], FS[0])
            for i_, (s_, tg_) in enumerate(tiles):
                pend = []
                if i_ + 1 < len(tiles):
                    mk.deferred = pend
                    ffront(tiles[i_ + 1][0], tiles[i_ + 1][1], FS[(i_ + 1) % 2])
                    mk.deferred = None
                fback(s_, tg_, FS[i_ % 2], pend)
            mk.flush(final=True)

    if upto >= 4:
        ffn_pass()

    outer.close()
    return nc


def core_inputs(P, x, c):
    f = np.float32
    m = {"x": x, "c": c}
    for k in ("norm1_g", "w_ada", "b_ada", "w_in", "mu_shift", "w0", "w2", "a0", "a2", "g2", "k_k", "k_a",
              "lnx_g", "lnx_b", "lam_re", "lam_im", "log_dt", "b_re", "b_im", "c_re", "c_im", "w_glu",
              "s5_out_g", "w_out", "norm2_g", "w_ff1", "w_ff3", "w_ff2", "final_g"):
        m[k] = P[k]
    m["r_k"] = P["r_k"].reshape(512)
    m["d_skip"] = P["d_skip"].reshape(512)
    m["b_glu"] = P["b_glu"].reshape(512)
    return {k: np.ascontiguousarray(v, dtype=f) for k, v in m.items()}


_T_PROMPT = 8192
_T_SAMPLE = 4096


def kernel(**inputs):
    n = 8
    P = {}
    for k, v in inputs.items():
        if k in ("x_prompt", "x_sample", "c_prompt", "c_sample"):
            continue
        v = np.asarray(v)
        P[k] = v if k == "final_g" else v[0]
    xp = np.asarray(inputs["x_prompt"]); xs = np.asarray(inputs["x_sample"])
    cpr = np.asarray(inputs["c_prompt"]); cs = np.asarray(inputs["c_sample"])
    TS = [xp.shape[1], xs.shape[1]]
    nc = build_program(TS, upto=int(os.environ.get('KUPTO', '99')))
    in_maps = []
    for b in range(n):
        x = np.concatenate([xp[b], xs[b]], axis=0)
        c = np.stack([cpr[b], cs[b]], axis=0)
        in_maps.append(core_inputs(P, x, c))
    res = run_bass_kernel_spmd(nc, in_maps, core_ids=list(range(n)))
    yp = np.stack([res.results[b]["y"][:TS[0]] for b in range(n)], axis=0).astype(np.float32)
    ys = np.stack([res.results[b]["y"][TS[0]:] for b in range(n)], axis=0).astype(np.float32)
    return (yp, ys)
```

```python
import os
import math
import numpy as np
import concourse.bass as bass
import concourse.mybir as mybir
from concourse.bass_utils import run_bass_kernel_spmd

F32 = mybir.dt.float32
BF16 = mybir.dt.bfloat16
AF = mybir.ActivationFunctionType
ALU = mybir.AluOpType
AX = mybir.AxisListType

D = 1024
DFF = 2816
NPROJ = 2432
RW = 1920
NDS = 12
RMS_EPS = 1e-6
LNX_EPS = 64e-5


class Buf:
    __slots__ = ("w", "r")

    def __init__(self):
        self.w = None
        self.r = {}


class MK:
    BLK = {"pe": "tensor", "dve": "vector", "act": "scalar", "pool": "gpsimd", "sp": "sync"}

    def __init__(self, nc, same=True):
        self.nc = nc
        self.same = same
        self.names = ["pe", "dve", "act", "pool", "sp"]
        self.sem = {k: nc.alloc_semaphore(name="s_" + k) for k in self.names}
        self.cnt = {k: 0 for k in self.names}
        self.seen = {k: {} for k in self.names}
        self.prog = {k: [] for k in self.names}
        self.dsem = [nc.alloc_semaphore(name="d%d" % i) for i in range(NDS)]
        self.dcnt = [0] * NDS
        self.dnext = 0
        self.deferred = None

    def semof(self, key):
        if isinstance(key, tuple):
            return self.dsem[key[1]]
        return self.sem[key]

    def _deps(self, e, reads, writes):
        deps = {}

        def add(k, v):
            if deps.get(k, 0) < v:
                deps[k] = v

        for b in reads:
            if b.w:
                add(*b.w)
        for b in writes:
            if b.w:
                add(*b.w)
            for k, v in b.r.items():
                add(k, v)
        out = []
        for k, v in deps.items():
            if k == e and (e == "pe" or not self.same):
                continue
            if self.seen[e].get(k, 0) >= v:
                continue
            self.seen[e][k] = v
            out.append((k, v))
        return out

    def _mark(self, tok, reads, writes):
        k, v = tok
        for b in reads:
            if b.r.get(k, 0) < v:
                b.r[k] = v
        for b in writes:
            b.w = tok
            b.r = {}

    def op(self, e, fn, reads=(), writes=()):
        if self.deferred is not None:
            self.deferred.append((0, e, fn, reads, writes))
            return
        waits = self._deps(e, reads, writes)
        self.cnt[e] += 1
        tok = (e, self.cnt[e])
        self.prog[e].append((waits, fn, self.sem[e], 1))
        self._mark(tok, reads, writes)

    def replay(self, pending, n):
        keep = self.deferred
        self.deferred = None
        last = None
        cnt = 0
        while pending and (cnt < n or last == "pe"):
            kind, e, fn, reads, writes = pending.pop(0)
            (self.dma if kind else self.op)(e, fn, reads, writes)
            last = e if not kind else None
            cnt += 1
        self.deferred = keep

    def dma(self, q, fn, reads=(), writes=()):
        if self.deferred is not None:
            self.deferred.append((1, q, fn, reads, writes))
            return
        i = self.dnext
        self.dnext = (i + 1) % NDS
        key = ("d", i)
        waits = self._deps(q, reads, writes)
        if self.dcnt[i] > 0 and self.seen[q].get(key, 0) < self.dcnt[i]:
            waits.append((key, self.dcnt[i]))
            self.seen[q][key] = self.dcnt[i]
        self.dcnt[i] += 16
        tok = (key, self.dcnt[i])
        self.prog[q].append((waits, fn, self.dsem[i], 16))
        self._mark(tok, reads, writes)

    def flush(self, final=False):
        nc = self.nc
        fin = []
        for i in (range(NDS) if final else []):
            if self.dcnt[i] > 0:
                fin.append((("d", i), self.dcnt[i]))
        for k in (self.names if final else []):
            if k != "sp" and self.cnt[k] > 0:
                fin.append((k, self.cnt[k]))
        with nc.Block() as block:
            for e in self.names:
                prog = self.prog[e]
                extra = fin if e == "sp" else []

                def body(eng, prog=prog, extra=extra):
                    for waits, fn, sem, inc in prog:
                        for k, v in waits:
                            eng.wait_ge(self.semof(k), v)
                        fn(eng).then_inc(sem, inc)
                    for k, v in extra:
                        eng.wait_ge(self.semof(k), v)

                getattr(block, self.BLK[e])(body)
        self.prog = {k: [] for k in self.names}

    def emit(self):
        self.flush(final=True)


def build_program(TS, dbg=False, upto=99):
    import contextlib
    nc = bass.Bass("TRN2", target_bir_lowering=False)
    mk = MK(nc, same=(os.environ.get("MK_SAME", "1") == "1"))
    TT = sum(TS)
    NS = len(TS)
    WP = TT + 2 * NS
    SEQ = []
    o = 0
    for s, T in enumerate(TS):
        SEQ.append((o, o + 2 * s, T))
        o += T

    def din(name, shape):
        return nc.dram_tensor(name, list(shape), F32, kind="ExternalInput").ap()

    def dscr(name, shape):
        return nc.dram_tensor(name, list(shape), F32, kind=("ExternalOutput" if dbg else "Internal")).ap()

    x_in = din("x", (TT, D))
    c_in = din("c", (NS, D))
    norm1_g = din("norm1_g", (D,))
    w_ada = din("w_ada", (D, 6 * D))
    b_ada = din("b_ada", (6 * D,))
    w_in = din("w_in", (D, NPROJ))
    mu_shift = din("mu_shift", (RW,))
    w0 = din("w0", (2, 512)); w2 = din("w2", (2, 64, 512))
    a0 = din("a0", (2, 512)); a2 = din("a2", (2, 64, 512))
    g2 = din("g2", (128, 512))
    k_k = din("k_k", (512,)); k_a = din("k_a", (512,)); r_k = din("r_k", (512,))
    lnx_g = din("lnx_g", (512,)); lnx_b = din("lnx_b", (512,))
    lam_re = din("lam_re", (2, 32, 64)); lam_im = din("lam_im", (2, 32, 64)); log_dt = din("log_dt", (2, 32))
    b_re = din("b_re", (2, 32, 64, 16)); b_im = din("b_im", (2, 32, 64, 16))
    c_re = din("c_re", (2, 32, 16, 64)); c_im = din("c_im", (2, 32, 16, 64))
    d_skip = din("d_skip", (512,)); w_glu = din("w_glu", (32, 16, 16)); b_glu = din("b_glu", (512,))
    s5_out_g = din("s5_out_g", (512,))
    w_out = din("w_out", (D, D)); norm2_g = din("norm2_g", (D,))
    w_ff1 = din("w_ff1", (D, DFF)); w_ff3 = din("w_ff3", (D, DFF)); w_ff2 = din("w_ff2", (DFF, D))
    final_g = din("final_g", (D,))
    y_out = nc.dram_tensor("y", [TT, D], F32, kind="ExternalOutput").ap()

    Pscr = dscr("Pscr", (NPROJ, WP))
    XT = dscr("XT", (D, TT))
    YD = dscr("YD", (2, 64, 8, TT))
    BD = dscr("BD", (2, 64, 8, TT))
    YS = dscr("YS", (512, TT))
    X1 = dscr("X1", (D, TT))
    MODS = dscr("MODS", (128, 48 * NS))
    b_P = Buf(); b_XT = Buf(); b_YD = Buf(); b_BD = Buf(); b_YS = Buf(); b_X1 = Buf(); b_MODS = Buf()

    def tt(e, out, a, b, op, r, w):
        mk.op(e, lambda E: E.tensor_tensor(out=out, in0=a, in1=b, op=op), r, w)

    def ts(e, out, a, s1, s2, op0, op1, r, w):
        if op1 is None:
            mk.op(e, lambda E: E.tensor_scalar(out=out, in0=a, scalar1=s1, scalar2=None, op0=op0), r, w)
        else:
            mk.op(e, lambda E: E.tensor_scalar(out=out, in0=a, scalar1=s1, scalar2=s2, op0=op0, op1=op1), r, w)

    def stt(out, a, sc, b, op0, op1, r, w):
        mk.op("dve", lambda E: E.scalar_tensor_tensor(out=out, in0=a, scalar=sc, in1=b, op0=op0, op1=op1), r, w)

    def act(out, a, func, r, w, bias=0.0, scale=1.0):
        mk.op("act", lambda E: E.activation(out=out, in_=a, func=func, bias=bias, scale=scale), r, w)

    def cp(e, out, a, r, w):
        if e == "act":
            mk.op("act", lambda E: E.activation(out=out, in_=a, func=AF.Copy), r, w)
        else:
            mk.op(e, lambda E: E.tensor_copy(out=out, in_=a), r, w)

    def mm(out, lhsT, rhs, st, sp_, r, w):
        mk.op("pe", lambda E: E.matmul(out=out, lhsT=lhsT, rhs=rhs, start=st, stop=sp_), r, w)

    F32R = mybir.dt.float32r
    USE_R = os.environ.get("RW_F32R", "1") == "1"

    def RR(ap):
        return ap.bitcast(F32R) if USE_R else ap

    def mmr(out, lhsT, rhs, r, w, st=True, sp_=True):
        mk.op("pe", lambda E: E.matmul(out=out, lhsT=lhsT.bitcast(F32R), rhs=rhs.bitcast(F32R), start=st, stop=sp_), r, w)

    def dma(q, out, in_, r, w, slow=False):
        if slow:
            mk.dma(q, lambda E: E.dma_start(out=out, in_=in_, allow_slow_non_contiguous=True), r, w)
        else:
            mk.dma(q, lambda E: E.dma_start(out=out, in_=in_), r, w)

    def scope(pfx=""):
        es = contextlib.ExitStack()

        def sb(name, shape, dt=F32):
            return es.enter_context(nc.sbuf_tensor(pfx + name, list(shape), dt))

        def ps(name, shape, dt=F32):
            return es.enter_context(nc.psum_tensor(pfx + name, list(shape), dt))
        return es, sb, ps

    def consts(sb):
        ident = sb("ident", [128, 128]); b_ident = Buf()
        mk.op("pool", lambda E: E.memset(ident[:], 1.0), (), [b_ident])
        mk.op("pool", lambda E: E.affine_select(out=ident[:], in_=ident[:], pattern=[[-1, 128]],
                                                compare_op=ALU.is_equal, fill=0.0, base=0, channel_multiplier=1),
              [b_ident], [b_ident])
        return ident, b_ident
    outer, osb, ops_ = scope("o_")
    ident, b_ident = consts(osb)
    modT = osb("modT", [128, 48, NS]); b_mod = Buf()
    sc1 = osb("sc1", [128, 8, NS]); b_sc1 = Buf()

    def pass0():
        es, sb, ps = scope("p0_")
        with es:
            ones_bf = sb("ones_bf", [128, 128], BF16); b_ones = Buf()
            mk.op("pool", lambda E: E.memset(ones_bf[:], 1.0), (), [b_ones])
            cT = sb("cT", [128, 8, NS]); b_cT = Buf()
            scT = sb("scT", [128, 8, NS]); b_scT = Buf()
            for s in range(NS):
                dma("sp", cT[:, :, s], c_in[s].rearrange("(k p) -> p k", p=128), (), [b_cT], slow=True)
            act(scT[:], cT[:], AF.Silu, [b_cT], [b_scT])
            badaT = sb("badaT", [128, 48]); b_bada = Buf()
            dma("sp", badaT[:], b_ada.rearrange("(k p) -> p k", p=128), (), [b_bada], slow=True)
            g1T = sb("g1T", [128, 8]); b_g1 = Buf()
            dma("sp", g1T[:], norm1_g.rearrange("(k p) -> p k", p=128), (), [b_g1], slow=True)
            wada_t = [sb("wada%d" % i, [128, 8, 256]) for i in range(2)]
            b_wada = [Buf(), Buf()]
            ps_mod_full = ps("ps_mod", [128, 512]); b_psmod = Buf()
            ps_mod = ps_mod_full[:, 0:4 * NS].rearrange("p (a b) -> p a b", b=NS)
            for slab in range(24):
                wt = wada_t[slab % 2]; bw = b_wada[slab % 2]
                dma("sp" if slab % 2 == 0 else "act", wt[:],
                    w_ada[:, slab * 256:(slab + 1) * 256].rearrange("(k p) n -> p k n", p=128), (), [bw])
                for j in range(2):
                    for k in range(8):
                        mm(ps_mod[:, j, :], wt[:, k, j * 128:(j + 1) * 128], scT[:, k, :], k == 0, k == 7,
                           [bw, b_scT], [b_psmod])
                for s in range(NS):
                    tt("dve", modT[:, slab * 2:(slab + 1) * 2, s], ps_mod[:, 0:2, s],
                       badaT[:, slab * 2:(slab + 1) * 2], ALU.add, [b_psmod, b_bada], [b_mod])
            for s in range(NS):
                stt(sc1[:, :, s], modT[:, 8:16, s], 1.0, g1T[:], ALU.add, ALU.mult, [b_mod, b_g1], [b_sc1])

            w_in_bf = sb("w_in_bf", [128, 8, NPROJ], BF16); b_win = Buf()
            wst = [sb("wst%d" % i, [128, NPROJ]) for i in range(2)]; b_wst = [Buf(), Buf()]
            for k in range(8):
                dma("sp", wst[k % 2][:], w_in[k * 128:(k + 1) * 128, :], (), [b_wst[k % 2]])
                cp("pool", w_in_bf[:, k, :], wst[k % 2][:], [b_wst[k % 2]], [b_win])

            NT = 512
            xtm = [sb("xtm%d" % i, [128, 4, D]) for i in range(2)]; b_xtm = [Buf(), Buf()]
            xT = sb("xT", [128, 8, NT]); b_xT = Buf()
            sq = sb("sq", [128, 8, NT], BF16); b_sq = Buf()
            rstd = sb("rstd", [128, NT]); b_rstd = Buf()
            tmp = sb("tmp0", [128, NT]); b_tmp = Buf()
            hT = sb("hT", [128, 8, NT], BF16); b_hT = Buf()
            pev = [sb("pev%d" % i, [128, NT]) for i in range(3)]; b_pev = [Buf() for _ in range(3)]
            zcol = sb("zcol", [128, 1]); b_zcol = Buf()
            mk.op("pool", lambda E: E.memset(zcol[:], 0.0), (), [b_zcol])
            pst = [ps("pst%d" % i, [128, NT]) for i in range(4)]; b_pst = [Buf() for _ in range(4)]
            psm = [ps("psm%d" % i, [128, NT]) for i in range(3)]; b_psm = [Buf() for _ in range(3)]
            for s, (toff, coff, T) in enumerate(SEQ):
                for mc in range(19):
                    for cc in (coff, coff + T + 1):
                        dma("sp", Pscr[mc * 128:(mc + 1) * 128, cc:cc + 1], zcol[:], [b_zcol], [b_P], slow=True)
                for ti in range(T // NT):
                    t0 = toff + ti * NT
                    xt = xtm[ti % 2]; bx = b_xtm[ti % 2]
                    dma("sp", xt[:], x_in[t0:t0 + NT, :].rearrange("(j p) d -> p j d", p=128), (), [bx])
                    for dc in range(8):
                        pt = pst[dc % 4]; bp = b_pst[dc % 4]
                        for j in range(4):
                            mk.op("pe", lambda E, pt=pt, xt=xt, j=j, dc=dc: E.transpose(
                                out=pt[:, j * 128:(j + 1) * 128], in_=xt[:, j, dc * 128:(dc + 1) * 128],
                                identity=ident[:]), [bx, b_ident], [bp])
                        cp("dve", xT[:, dc, :], pt[:], [bp], [b_xT])
                        act(sq[:, dc, :], pt[:], AF.Square, [bp, b_xT], [b_sq])
                    dma("act", XT[:, t0:t0 + NT].rearrange("(k p) t -> p k t", p=128), xT[:], [b_xT], [b_XT])
                    pm = psm[0]; bpm = b_psm[0]
                    for dc in range(8):
                        mm(pm[:], ones_bf[:], sq[:, dc, :], dc == 0, dc == 7, [b_sq, b_ones], [bpm])
                    act(rstd[:], pm[:], AF.Sqrt, [bpm], [b_rstd], bias=RMS_EPS, scale=1.0 / D)
                    mk.op("dve", lambda E: E.reciprocal(out=rstd[:], in_=rstd[:]), [b_rstd], [b_rstd])
                    for dc in range(8):
                        tt("dve", tmp[:], xT[:, dc, :], rstd[:], ALU.mult, [b_xT, b_rstd], [b_tmp])
                        ts("dve", hT[:, dc, :], tmp[:], sc1[:, dc, s:s + 1], modT[:, dc, s:s + 1], ALU.mult, ALU.add,
                           [b_tmp, b_sc1, b_mod], [b_hT])
                    for mc in range(19):
                        i3 = mc % 3
                        pm = psm[i3]; bpm = b_psm[i3]
                        for dc in range(8):
                            mm(pm[:], w_in_bf[:, dc, mc * 128:(mc + 1) * 128], hT[:, dc, :], dc == 0, dc == 7,
                               [b_win, b_hT], [bpm])
                        pv = pev[i3]; bpv = b_pev[i3]
                        cp("act", pv[:], pm[:], [bpm], [bpv])
                        cc = coff + 1 + ti * NT
                        dma("sp", Pscr[mc * 128:(mc + 1) * 128, cc:cc + NT], pv[:], [bpv], [b_P])
            mk.flush(final=True)

    pass0()
    def rwkv_pass(d):
        rev = (d == 1)
        es, sb, ps = scope("rw%d_" % d)
        with es:
            NT2 = 128
            psr = [ps("psr%d" % i, [128, 1024]) for i in range(4)]
            b_psr = [Buf() for _ in range(4)]
            pctr = [0]

            def nextps():
                i = pctr[0] % 4
                pctr[0] += 1
                return psr[i], b_psr[i]

            def T4(name):
                return sb(name, [64, 8, NT2]), Buf()

            def ldp(name, src512):
                t = sb(name, [64, 8]); b = Buf()
                dma("sp", t[:], src512.rearrange("(h p) -> p h", p=64), (), [b], slow=True)
                return t, b

            mu3 = sb("mu3", [64, 24]); b_mu3 = Buf()
            dma("sp", mu3[:], mu_shift[0:1536].rearrange("(g p) -> p g", p=64), (), [b_mu3], slow=True)
            hm3 = sb("hm3", [64, 24]); om3 = sb("om3", [64, 24]); b_hm3 = Buf()
            ts("dve", hm3[:], mu3[:], 0.5, None, ALU.mult, None, [b_mu3], [b_hm3])
            ts("dve", om3[:], mu3[:], -1.0, 1.0, ALU.mult, ALU.add, [b_mu3], [b_hm3])
            muw = sb("muw", [64, 2]); b_muw = Buf()
            dma("sp", muw[:, 0:1], mu_shift[1536 + 64 * d:1600 + 64 * d].rearrange("(p o) -> p o", o=1), (), [b_muw], slow=True)
            dma("sp", muw[:, 1:2], mu_shift[1664 + 64 * d:1728 + 64 * d].rearrange("(p o) -> p o", o=1), (), [b_muw], slow=True)
            hmw = sb("hmw", [64, 2]); omw = sb("omw", [64, 2]); b_hmw = Buf()
            ts("dve", hmw[:], muw[:], 0.5, None, ALU.mult, None, [b_muw], [b_hmw])
            ts("dve", omw[:], muw[:], -1.0, 1.0, ALU.mult, ALU.add, [b_muw], [b_hmw])
            w0d, b_w0d = ldp("w0d", w0[d]); a0d, b_a0d = ldp("a0d", a0[d])
            kk_, b_kk_ = ldp("kk_", k_k); ka_, b_ka_ = ldp("ka_", k_a); rk_, b_rk_ = ldp("rk_", r_k)
            omka = sb("omka", [64, 8]); b_omka = Buf()
            ts("dve", omka[:], ka_[:], -1.0, 1.0, ALU.mult, ALU.add, [b_ka_], [b_omka])
            w2d = sb("w2d", [64, 512]); a2d = sb("a2d", [64, 512]); b_w2d = Buf()
            dma("sp", w2d[:], w2[d], (), [b_w2d]); dma("sp", a2d[:], a2[d], (), [b_w2d])
            ones64 = sb("ones64", [64, 64]); b_c = Buf()
            mk.op("pool", lambda E: E.memset(ones64[:], 1.0), (), [b_c])
            ones64b = sb("ones64b", [64, 64], BF16)
            mk.op("pool", lambda E: E.memset(ones64b[:], 1.0), (), [b_c])
            maskA = sb("maskA", [64, 128]); maskL = sb("maskL", [64, 64]); MS = sb("MS", [64, 8 * NT2])
            mk.op("pool", lambda E: E.memset(maskA[:], 1.0), (), [b_c])
            mk.op("pool", lambda E: E.memset(maskL[:], 1.0), (), [b_c])
            mk.op("pool", lambda E: E.memset(MS[:], 1.0), (), [b_c])
            zc_ = 63 if rev else 0
            mk.op("pool", lambda E: E.memset(MS[:].rearrange("p (a l) -> p a l", l=64)[:, :, zc_:zc_ + 1], 0.0), [b_c], [b_c])

            def asel(ap, upper, strict):
                pat = [[1, 64]] if upper else [[-1, 64]]
                cm = -1 if upper else 1
                mk.op("pool", lambda E: E.affine_select(out=ap, in_=ap, pattern=pat, compare_op=ALU.is_ge, fill=0.0,
                                                        base=(-1 if strict else 0), channel_multiplier=cm),
                      [b_c], [b_c])
            asel(maskA[:, 0:64], not rev, True)
            asel(maskA[:, 64:128], not rev, False)
            asel(maskL[:], rev, True)
            mA = maskA[:, None, :].broadcast_to([64, 8, 128])
            mL = maskL[:, None, :].broadcast_to([64, 8, 64])
            id64 = ident[0:64, 0:64]
            idbc = ident[0:64, None, 0:64].broadcast_to([64, 8, 64])

            Lr = [sb("Lq%d" % q, [64, 8, NT2 + 2]) for q in range(2)]; b_L = [Buf() for _ in range(2)]
            Lr.append(Lr[0]); b_L.append(b_L[0])
            XW = sb("XW", [64, NT2 + 2]); XA = sb("XA", [64, NT2 + 2]); b_XW = Buf(); b_XA = Buf()
            T1, b_T1 = T4("T1")
            SH = [T4("SH%d" % q) for q in range(2)]
            (Rp, b_Rp), (Kp, b_Kp) = SH
            tt0, cp0 = tt, cp
            tw = sb("tw", [64, NT2]); b_tw = Buf()
            xwp = sb("xwp", [64, NT2]); xap = sb("xap", [64, NT2]); b_xwp = Buf(); b_xap = Buf()
            XB, b_XB = T4("XB"); E2, b_E2 = T4("E2"); AD, b_AD = T4("AD"); KR, b_KR = T4("KR")
            SS, b_SS = T4("SS"); KD, b_KD = T4("KD"); AB, b_AB = T4("AB"); BON, b_BON = XB, b_XB
            G, b_G = T1, b_T1; D1, b_D1 = E2, b_E2; D2, b_D2 = AD, b_AD; EP, b_EP = XB, b_XB; EN, b_EN = SS, b_SS
            T2, b_T2 = SS, b_SS
            SD = F32
            SETS = []
            for i_ in range(2):
                st_ = []
                for nm, shp in (("AR", [64, 8, 2, 128]), ("KT", [64, 9, NT2]), ("BT", [64, 9, NT2]), ("KH", [64, 8, NT2]),
                                ("BH", [64, 8, NT2]), ("Vp", [64, 8, NT2]), ("GL", [64, 16])):
                    st_ += [sb("%s_%d" % (nm, i_), shp), Buf()]
                SETS.append(st_)
            YT, b_YT = T4("YT")
            MT1 = sb("MT1", [64, 2, 8, 128], SD); MT2 = sb("MT2", [64, 2, 8, 128], SD); b_MT1 = Buf(); b_MT2 = Buf()
            P0 = sb("P0", [64, 17, 64], SD); b_P0 = Buf()
            PP = [sb("PP%d" % i, [64, 33, 64], SD) for i in range(2)]; b_PP = [Buf(), Buf()]
            Zt = [sb("Zt%d" % i, [64, 17, 128], SD) for i in range(2)]; b_Zt = [Buf() for _ in range(2)]
            VT = sb("VT", [64, 17, 64], SD); BHt = sb("BHt", [64, 17, 64], SD); KHt = sb("KHt", [64, 17, 64], SD)
            QT = sb("QT", [64, 2, 8, 64]); MM = sb("MM", [64, 17, 64]); DG = sb("DG", [64, 17, 64])
            b_VT = Buf(); b_BHt = Buf(); b_KHt = Buf(); b_QT = Buf(); b_MM = Buf(); b_DG = Buf()
            STt = [sb("ST%d" % i, [64, 9, 64]) for i in range(2)]; b_ST = [Buf(), Buf()]
            for t_, b__, r_ in ((Zt[0], b_Zt[0], 16), (Zt[1], b_Zt[1], 16)):
                ts("dve", RR(t_[:, r_, :]), maskA[:], 0.0, None, ALU.mult, None, [b_c], [b__])
            for t_, b__ in ((VT, b_VT), (BHt, b_BHt), (KHt, b_KHt), (MM, b_MM)):
                ts("dve", RR(t_[:, 16, :]), ones64[:], 0.0, None, ALU.mult, None, [b_c], [b__])
            for i_ in range(2):
                ts("dve", RR(STt[i_][:, 8, :]), ones64[:], 0.0, None, ALU.mult, None, [b_c], [b_ST[i_]])
                ts("dve", RR(SETS[i_][2][:, 8, :]), maskA[:], 0.0, None, ALU.mult, None, [b_c], [SETS[i_][3]])
                ts("dve", RR(SETS[i_][4][:, 8, :]), maskA[:], 0.0, None, ALU.mult, None, [b_c], [SETS[i_][5]])
            ts("dve", RR(P0[:, 16, :]), ones64[:], 0.0, None, ALU.mult, None, [b_c], [b_P0])
            for i_ in range(2):
                ts("dve", RR(PP[i_][:, 32, :]), ones64[:], 0.0, None, ALU.mult, None, [b_c], [b_PP[i_]])
            mA16 = maskA[:, None, :].broadcast_to([64, 16, 128])
            mL16 = maskL[:, None, :].broadcast_to([64, 16, 64])
            idbc16 = ident[0:64, None, 0:64].broadcast_to([64, 16, 64])

            def f16(t):
                if len(t.shape) == 3:
                    return t[:, 0:16, :]
                return t[:].rearrange("p c h n -> p (c h) n")

            def wd(t, blk, n, off=0):
                fl = t[:].rearrange("p a n -> p (a n)")
                return fl[:, blk * n + off:blk * n + off + 128]

            def pv(p, lo, n):
                return p[0:64, lo:lo + 16 * n].rearrange("p (a n) -> p a n", n=n)

            def v3(p, n):
                return p[0:64, 0:8 * n].rearrange("p (h n) -> p h n", n=n)

            def bc(t, lo, hi, n):
                return t[:, lo:hi, None].broadcast_to([64, hi - lo, n])

            def c4(t):
                return t[:].rearrange("p h (c l) -> p h c l", l=64)

            for s, (toff, coff, T) in enumerate(SEQ):
                sti_ = [0]
                ts("dve", RR(STt[0][:, 0:8, :]), STt[1][:, 0:8, :], 0.0, None, ALU.mult, None, [b_ST[1]], [b_ST[0]])
                ntile = T // NT2
                order = list(range(ntile - 1, -1, -1) if rev else range(ntile))

                def prep(ti, AR, b_AR, KT, b_KT, BT, b_BT, KH, b_KH, BH, b_BH, Vp, b_Vp, GL, b_GL):
                    ARb, b_ARb = AR, b_AR
                    tl = ti * NT2
                    c0 = coff + tl
                    tg = toff + tl
                    def ldq(q):
                        dma("sp" if q != 1 else "act", Lr[q][:],
                            Pscr[q * 512:(q + 1) * 512, c0:c0 + NT2 + 2].rearrange("(h p) t -> p h t", p=64),
                            [b_P], [b_L[q]])

                    def shq(q):
                        Lq = Lr[q]; S_, bS = (SH[q] if q < 2 else (Vp, b_Vp))
                        tt("pool", T1[:], Lq[:, :, 0:NT2], Lq[:, :, 2:NT2 + 2], ALU.add, [b_L[q]], [b_T1])
                        tt("pool", T1[:], T1[:], bc(hm3, 8 * q, 8 * q + 8, NT2), ALU.mult, [b_T1, b_hm3], [b_T1])
                        tt("pool", S_[:], Lq[:, :, 1:NT2 + 1], bc(om3, 8 * q, 8 * q + 8, NT2), ALU.mult,
                           [b_L[q], b_hm3], [bS])
                        tt("pool", S_[:], S_[:], T1[:], ALU.add, [bS, b_T1], [bS])
                    ldq(0); ldq(1)
                    dma("sp", XW[:], Pscr[1536 + 64 * d:1600 + 64 * d, c0:c0 + NT2 + 2], [b_P], [b_XW])
                    dma("act", XA[:], Pscr[1664 + 64 * d:1728 + 64 * d, c0:c0 + NT2 + 2], [b_P], [b_XA])
                    shq(0); ldq(2); shq(1); shq(2)
                    for (X_, bX, o_, bo, j) in ((XW, b_XW, xwp, b_xwp, 0), (XA, b_XA, xap, b_xap, 1)):
                        tt("dve", tw[:], X_[:, 0:NT2], X_[:, 2:NT2 + 2], ALU.add, [bX], [b_tw])
                        ts("dve", tw[:], tw[:], hmw[:, j:j + 1], None, ALU.mult, None, [b_tw, b_hmw], [b_tw])
                        stt(o_[:], X_[:, 1:NT2 + 1], omw[:, j:j + 1], tw[:], ALU.mult, ALU.add, [bX, b_hmw, b_tw], [bo])
                    act(xwp[:], xwp[:], AF.Tanh, [b_xwp], [b_xwp])
                    for hh in range(2):
                        pa, bpa = nextps()
                        for j in range(4):
                            h = 4 * hh + j
                            mm(pa[0:64, j * NT2:(j + 1) * NT2], w2d[:, h * 64:(h + 1) * 64], xwp[:], True, True,
                               [b_w2d, b_xwp], [bpa])
                        tt("dve", XB[:, 4 * hh:4 * hh + 4, :], pa[0:64, 0:4 * NT2].rearrange("p (h n) -> p h n", n=NT2),
                           bc(w0d, 4 * hh, 4 * hh + 4, NT2), ALU.add, [bpa, b_w0d], [b_XB])
                    act(XB[:], XB[:], AF.Exp, [b_XB], [b_XB], scale=-1.0)
                    act(XB[:], XB[:], AF.Ln, [b_XB], [b_XB], bias=1.0)
                    act(E2[:], XB[:], AF.Exp, [b_XB], [b_E2], bias=-0.5, scale=-1.0)
                    for hh in range(2):
                        pa, bpa = nextps()
                        for j in range(4):
                            h = 4 * hh + j
                            mm(pa[0:64, j * NT2:(j + 1) * NT2], a2d[:, h * 64:(h + 1) * 64], xap[:], True, True,
                               [b_w2d, b_xap], [bpa])
                        tt("dve", AD[:, 4 * hh:4 * hh + 4, :], pa[0:64, 0:4 * NT2].rearrange("p (h n) -> p h n", n=NT2),
                           bc(a0d, 4 * hh, 4 * hh + 4, NT2), ALU.add, [bpa, b_a0d], [b_AD])
                    act(AD[:], AD[:], AF.Sigmoid, [b_AD], [b_AD])
                    tt("pool", KR[:], Kp[:], bc(kk_, 0, 8, NT2), ALU.mult, [b_Kp, b_kk_], [b_KR])
                    SQb = XB[:].rearrange("p h t -> p (h t)").bitcast(BF16).rearrange("p (h t) -> p h t", h=8)[:, :, 0:NT2]
                    tt("pool", SQb, KR[:], KR[:], ALU.mult, [b_KR], [b_XB])
                    for hh in range(2):
                        pa, bpa = nextps()
                        for j in range(4):
                            h = 4 * hh + j
                            mm(pa[0:64, j * NT2:(j + 1) * NT2], ones64b[:], SQb[:, h, :], True, True, [b_c, b_XB], [bpa])
                        ts("dve", SS[:, 4 * hh:4 * hh + 4, :], pa[0:64, 0:4 * NT2].rearrange("p (h n) -> p h n", n=NT2),
                           1e-24, None, ALU.max, None, [bpa], [b_SS])
                    act(SS[:], SS[:], AF.Sqrt, [b_SS], [b_SS])
                    mk.op("dve", lambda E: E.reciprocal(out=SS[:], in_=SS[:]), [b_SS], [b_SS])
                    tt("pool", KR[:], KR[:], SS[:], ALU.mult, [b_KR, b_SS], [b_KR])
                    tt("pool", T2[:], AD[:], bc(ka_, 0, 8, NT2), ALU.mult, [b_AD, b_ka_], [b_T2])
                    tt("pool", T2[:], T2[:], bc(omka, 0, 8, NT2), ALU.add, [b_T2, b_omka], [b_T2])
                    tt("pool", KD[:], T2[:], Kp[:], ALU.mult, [b_T2, b_Kp], [b_KD])
                    tt("dve", AB[:], AD[:], KR[:], ALU.mult, [b_AD, b_KR], [b_AB])
                    tt("pool", T1[:], Rp[:], KD[:], ALU.mult, [b_Rp, b_KD], [b_T1])
                    tt("pool", SQb, T1[:], bc(rk_, 0, 8, NT2), ALU.mult, [b_T1, b_rk_], [b_XB])
                    for hh in range(2):
                        pa, bpa = nextps()
                        for j in range(4):
                            h = 4 * hh + j
                            mm(pa[0:64, j * NT2:(j + 1) * NT2], ones64b[:], SQb[:, h, :], True, True, [b_c, b_XB], [bpa])
                        tt("dve", BON[:, 4 * hh:4 * hh + 4, :], pa[0:64, 0:4 * NT2].rearrange("p (h n) -> p h n", n=NT2),
                           Vp[:, 4 * hh:4 * hh + 4, :], ALU.mult, [bpa, b_Vp], [b_BON])
                    dma("sp", BD[d, :, :, tg:tg + NT2], BON[:], [b_BON], [b_BD])
                    E2f = E2[:].rearrange("p h t -> p (h t)"); Gf = G[:].rearrange("p h t -> p (h t)"); MSf = MS[:]
                    if rev:
                        E2f = E2f[:, ::-1]; Gf = Gf[:, ::-1]; MSf = MSf[:, ::-1]
                    mk.op("dve", lambda E, Gf=Gf, MSf=MSf, E2f=E2f: E.tensor_tensor_scan(
                        out=Gf, data0=MSf, data1=E2f, initial=0.0, op0=ALU.mult, op1=ALU.add), [b_E2, b_c], [b_G])
                    tt("pool", D1[:], G[:], E2[:], ALU.subtract, [b_G, b_E2], [b_D1])
                    Gv = G[:].rearrange("p h (c l) -> p (h c) l", l=64)
                    ti_ = 0 if rev else 63
                    totb = Gv[:, :, ti_:ti_ + 1].broadcast_to([64, 16, 64])
                    tt("pool", D2[:].rearrange("p h (c l) -> p (h c) l", l=64), Gv, totb, ALU.subtract, [b_G], [b_D2])
                    act(EP[:], G[:], AF.Exp, [b_G], [b_EP])
                    act(EN[:], G[:], AF.Exp, [b_G], [b_EN], scale=-1.0)
                    act(D1[:], D1[:], AF.Exp, [b_D1], [b_D1], scale=-1.0)
                    act(D2[:], D2[:], AF.Exp, [b_D2], [b_D2])
                    act(GL[:].rearrange("p (a o) -> p a o", o=1), Gv[:, :, ti_:ti_ + 1], AF.Exp, [b_G], [b_GL], scale=-1.0)
                    stt(RR(AR[:, :, :, 0:64]), c4(KR), -1.0, c4(D1), ALU.mult, ALU.mult, [b_KR, b_D1], [b_AR])
                    tt("pool", RR(AR[:, :, :, 64:128]), c4(Rp), c4(EN), ALU.mult, [b_Rp, b_EN], [b_AR])
                    tt("pool", RR(KT[:, 0:8, :]), KD[:], EP[:], ALU.mult, [b_KD, b_EP], [b_KT])
                    tt("dve", RR(BT[:, 0:8, :]), AB[:], EP[:], ALU.mult, [b_AB, b_EP], [b_BT])
                    tt("pool", KH[:], KD[:], D2[:], ALU.mult, [b_KD, b_D2], [b_KH])
                    tt("dve", BH[:], AB[:], D2[:], ALU.mult, [b_AB, b_D2], [b_BH])

                def chunk(ti, pend, AR, b_AR, KT, b_KT, BT, b_BT, KH, b_KH, BH, b_BH, Vp, b_Vp, GL, b_GL):
                    ARb, b_ARb = AR, b_AR
                    tg = toff + ti * NT2

                    def tt(*a):
                        tt0(*a)
                        mk.replay(pend, 2)

                    def cp(*a):
                        cp0(*a)
                        mk.replay(pend, 2)
                    CS = [slice(0, 64), slice(64, 128)]
                    mA8 = maskA[:, None, :].broadcast_to([64, 8, 128])
                    mL8 = maskL[:, None, :].broadcast_to([64, 8, 64])
                    idbc8 = ident[0:64, None, 0:64].broadcast_to([64, 8, 64])
                    C2 = (0, 1)

                    def blk(c):
                        return slice(c * 8, (c + 1) * 8)

                    def p8v(p, lo, n):
                        return p[0:64, lo:lo + 8 * n].rearrange("p (a n) -> p a n", n=n)
                    for c in C2:
                        p1, bp1 = nextps()
                        for h in range(8):
                            mmr(p1[0:128, h * 128:(h + 1) * 128], wd(BT, h, 128, c * 64), ARb[:, h, c, :], [b_BT, b_ARb], [bp1])
                        tt("dve", RR(MT1[:, c]), p8v(p1, 0, 128), mA8, ALU.mult, [bp1, b_c], [b_MT1])
                    for c in C2:
                        p2, bp2 = nextps()
                        for h in range(8):
                            mmr(p2[0:128, h * 128:(h + 1) * 128], wd(KT, h, 128, c * 64), ARb[:, h, c, :], [b_KT, b_ARb], [bp2])
                        tt("dve", RR(MT2[:, c]), p8v(p2, 0, 128), mA8, ALU.mult, [bp2, b_c], [b_MT2])
                    for c in C2:
                        p3, bp3 = nextps()
                        for h in range(8):
                            mmr(p3[0:128, h * 64:(h + 1) * 64], ARb[:, h, c, :], BT[:, h, CS[c]], [b_ARb, b_BT], [bp3])
                        tt("dve", RR(P0[:, blk(c), :]), p8v(p3, 0, 64), mL8, ALU.mult, [bp3, b_c], [b_P0])
                    Z0 = Zt[0]; bZ0 = b_Zt[0]
                    for c in C2:
                        p4, bp4 = nextps()
                        for h in range(8):
                            mk.op("pe", lambda E, p4=p4, o=h * 64, a=AR[:, h, c, 0:64]: E.transpose(
                                out=p4[0:64, o:o + 64], in_=a, identity=id64), [b_AR, b_ident], [bp4])
                            mk.op("pe", lambda E, p4=p4, o=512 + h * 64, a=Vp[:, h, CS[c]]: E.transpose(
                                out=p4[0:64, o:o + 64], in_=a, identity=id64), [b_Vp, b_ident], [bp4])
                        cp0("act", RR(Z0[:, blk(c), 0:64]), p8v(p4, 0, 64), [bp4], [bZ0])
                        cp("act", RR(VT[:, blk(c), :]), p8v(p4, 512, 64), [bp4], [b_VT])
                    for c in C2:
                        p5, bp5 = nextps()
                        for h in range(8):
                            mk.op("pe", lambda E, p5=p5, o=h * 64, a=BH[:, h, CS[c]]: E.transpose(
                                out=p5[0:64, o:o + 64], in_=a, identity=id64), [b_BH, b_ident], [bp5])
                            mk.op("pe", lambda E, p5=p5, o=512 + h * 64, a=KH[:, h, CS[c]]: E.transpose(
                                out=p5[0:64, o:o + 64], in_=a, identity=id64), [b_KH, b_ident], [bp5])
                        cp0("dve", RR(BHt[:, blk(c), :]), p8v(p5, 0, 64), [bp5], [b_BHt])
                        cp("dve", RR(KHt[:, blk(c), :]), p8v(p5, 512, 64), [bp5], [b_KHt])
                    for c in C2:
                        p6, bp6 = nextps()
                        for h in range(8):
                            mmr(p6[0:128, h * 64:(h + 1) * 64], MT2[:, c, h, :], VT[:, c * 8 + h, :], [b_MT2, b_VT], [bp6])
                        cp("act", RR(Z0[:, blk(c), 64:128]), p8v(p6, 0, 64), [bp6], [bZ0])
                    zi = 0
                    Pv = lambda c, h: P0[:, c * 8 + h, :]
                    Pw = lambda c, h: wd(P0, c * 8 + h, 64)
                    PTv = lambda c, h: MT1[:, c, h, 0:64]
                    PTw = lambda c, h: MT1[:, c, h, :]
                    bP = b_P0; bPT = b_MT1
                    for it in range(6):
                        Zc = Zt[zi]; bZc = b_Zt[zi]; Zn = Zt[1 - zi]; bZn = b_Zt[1 - zi]
                        PTc, Pc, PTcw, Pcw, bPTc, bPc = PTv, Pv, PTw, Pw, bPT, bP
                        if it < 5:
                            nx = it % 2
                            for c in C2:
                                p8, bp8 = nextps()
                                for h in range(8):
                                    if it < 4:
                                        mmr(p8[0:128, h * 64:(h + 1) * 64], PTcw(c, h), Pc(c, h), [bPTc, bPc], [bp8])
                                    mmr(p8[0:128, 512 + h * 64:512 + (h + 1) * 64], Pcw(c, h), PTc(c, h), [bPTc, bPc], [bp8])
                                cp("act", RR(PP[nx][:, 0:32, :].rearrange("p (k a) n -> p k a n", k=2)[:, :, blk(c), :]),
                                   p8[0:64, 0:1024].rearrange("p (k a n) -> p k a n", k=2, a=8), [bp8], [b_PP[nx]])
                            Pv = lambda c, h, nx=nx: PP[nx][:, c * 8 + h, :]
                            PTv = lambda c, h, nx=nx: PP[nx][:, 16 + c * 8 + h, :]
                            Pw = lambda c, h, nx=nx: wd(PP[nx], c * 8 + h, 64)
                            PTw = lambda c, h, nx=nx: wd(PP[nx], 16 + c * 8 + h, 64)
                            bP = b_PP[nx]; bPT = b_PP[nx]
                        for c in C2:
                            p7, bp7 = nextps()
                            for h in range(8):
                                mmr(p7[0:128, h * 128:(h + 1) * 128], PTcw(c, h), Zc[:, c * 8 + h, :], [bPTc, bZc], [bp7])
                            tt("dve", RR(Zn[:, blk(c), :]), p8v(p7, 0, 128), Zc[:, blk(c), :], ALU.add, [bp7, bZc], [bZn])
                        zi = 1 - zi
                    Zf = Zt[zi]; bZf = b_Zt[zi]
                    for c in C2:
                        p9, bp9 = nextps()
                        for h in range(8):
                            mmr(p9[0:128, h * 64:(h + 1) * 64], Zf[:, c * 8 + h, :], MT1[:, c, h, 64:128], [bZf, b_MT1], [bp9])
                            mmr(p9[0:128, 512 + h * 64:512 + (h + 1) * 64], Zf[:, c * 8 + h, :], BHt[:, c * 8 + h, :], [bZf, b_BHt], [bp9])
                        tt0("dve", RR(QT[:, c]), p8v(p9, 0, 64), AR[:, :, c, 64:128], ALU.add, [bp9, b_AR], [b_QT])
                        GLc = GL[:].rearrange("p (h c) -> p c h", c=2)[:, c, :, None].broadcast_to([64, 8, 64])
                        tt0("pool", DG[:, blk(c), :], idbc8, GLc, ALU.mult, [b_ident, b_GL], [b_DG])
                        tt("dve", RR(MM[:, blk(c), :]), p8v(p9, 512, 64), DG[:, blk(c), :], ALU.add, [bp9, b_DG], [b_MM])
                    for c in (range(1, -1, -1) if rev else range(2)):
                        sti = sti_[0]
                        ST = STt[sti]; bST = b_ST[sti]; STn = STt[1 - sti]; bSTn = b_ST[1 - sti]
                        p11, bp11 = nextps()
                        for h in range(8):
                            o_ = p11[0:128, h * 64:(h + 1) * 64]
                            k_ = c * 8 + h
                            mmr(o_, wd(ST, h, 64), QT[:, c, h, :], [bST, b_QT], [bp11], True, False)
                            mmr(o_, wd(Zf, k_, 128, 64), MT1[:, c, h, 64:128], [bZf, b_MT1], [bp11], False, False)
                            mmr(o_, wd(VT, k_, 64), MT2[:, c, h, 64:128], [b_VT, b_MT2], [bp11], False, True)
                        cp("act", YT[:, :, CS[c]], v3(p11, 64), [bp11], [b_YT])
                        p12, bp12 = nextps()
                        for h in range(8):
                            o_ = p12[0:128, h * 64:(h + 1) * 64]
                            k_ = c * 8 + h
                            mmr(o_, wd(MM, k_, 64), ST[:, h, :], [b_MM, bST], [bp12], True, False)
                            mmr(o_, wd(BHt, k_, 64), Zf[:, k_, 64:128], [b_BHt, bZf], [bp12], False, False)
                            mmr(o_, wd(KHt, k_, 64), VT[:, k_, :], [b_KHt, b_VT], [bp12], False, True)
                        cp("dve", RR(STn[:, 0:8, :]), v3(p12, 64), [bp12], [bSTn])
                        sti_[0] = 1 - sti
                    dma("sp", YD[d, :, :, tg:tg + NT2], YT[:], [b_YT], [b_YD])

                PIPE = os.environ.get('RW_NOPIPE') != '1'
                if PIPE:
                    prep(order[0], *SETS[0])
                for idx, ti in enumerate(order):
                    pend = []
                    if not PIPE:
                        prep(ti, *SETS[idx % 2])
                    elif idx + 1 < len(order):
                        mk.deferred = pend
                        prep(order[idx + 1], *SETS[(idx + 1) % 2])
                        mk.deferred = None
                    if os.environ.get('RW_PIPE_MODE') == 'start':
                        mk.replay(pend, len(pend))
                    chunk(ti, pend, *SETS[idx % 2])
                    mk.replay(pend, len(pend))
            mk.flush(final=True)

    if upto >= 1:
        rwkv_pass(0)
        rwkv_pass(1)

    def s5_pass():
        es, sb, ps = scope("s5_")
        with es:
            TWO_PI = 2.0 * math.pi
            pz = [ps("pz%d" % i, [128, 512]) for i in range(8)]
            b_pz = [Buf() for _ in range(8)]
            pctr = [0]

            def nextps():
                i = pctr[0] % 8
                pctr[0] += 1
                return pz[i], b_pz[i]

            NLV = 10
            identb = sb("identb", [64, 64], BF16)
            dsk = sb("dsk", [16, 32])
            SQr = sb("SQr", [64, NLV, 64]); SQi = sb("SQi", [64, NLV, 64]); SQin = sb("SQin", [64, NLV, 64])
            LTr = sb("LTr", [64, 8, 64, 16], BF16); LTi = sb("LTi", [64, 8, 64, 16], BF16)
            OTr = sb("OTr", [64, 8, 64, 16], BF16); OTn = sb("OTn", [64, 8, 64, 16], BF16)
            CRb = sb("CRb", [64, 64, 16], BF16); CInb = sb("CInb", [64, 64, 16], BF16)
            bt_ = Buf()
            es2, sb2, ps2_ = scope("s5t_")
            ones1 = sb2("ones1", [1, 64]); row = sb2("row", [1, 64])
            mk.op("pool", lambda E: E.memset(ones1[:], 1.0), (), [bt_])
            dma("sp", row[:], log_dt.rearrange("d g -> (d g)").rearrange("(o n) -> o n", o=1), (), [bt_])
            cp("dve", identb[:], ident[0:64, 0:64], [b_ident], [bt_])
            LR = sb2("LR", [64, 64]); LI = sb2("LI", [64, 64])
            dma("sp", LR[:].rearrange("p (d g) -> p d g", d=2), lam_re.rearrange("d g p -> p d g"), (), [bt_], slow=True)
            dma("act", LI[:].rearrange("p (d g) -> p d g", d=2), lam_im.rearrange("d g p -> p d g"), (), [bt_], slow=True)
            BR = sb2("BR", [64, 64, 16]); BI = sb2("BI", [64, 64, 16])
            dma("sp", BR[:].rearrange("p (d g) h -> p d g h", d=2), b_re.rearrange("d g p h -> p d g h"), (), [bt_])
            dma("act", BI[:].rearrange("p (d g) h -> p d g h", d=2), b_im.rearrange("d g p h -> p d g h"), (), [bt_])
            CR = sb2("CR", [64, 64, 16]); CI = sb2("CI", [64, 64, 16])
            cnat = sb2("cnat", [128, 8, 64])
            for (src, dst) in ((c_re, CR), (c_im, CI)):
                dma("sp", cnat[:], src.rearrange("d g h p -> (d g h) p").rearrange("(k q) p -> q k p", q=128), [bt_], [bt_])
                for k in range(8):
                    pq, bq = nextps()
                    mk.op("pe", lambda E, pq=pq, k=k: E.transpose(out=pq[0:64, 0:128], in_=cnat[:, k, :], identity=ident[:]),
                          [bt_, b_ident], [bq])
                    cp("dve", dst[:, k * 8:(k + 1) * 8, :], pq[0:64, 0:128].rearrange("p (g h) -> p g h", h=16), [bq], [bt_])
            dma("sp", dsk[:], d_skip.rearrange("(g h) -> h g", h=16), (), [bt_], slow=True)
            DT = sb2("DT", [64, 64])
            pq, bq = nextps()
            mm(pq[0:64, 0:64], ones1[:], row[:], True, True, [bt_], [bq])
            act(DT[:], pq[0:64, 0:64], AF.Exp, [bq], [bt_])

            def T64(name):
                return sb2(name, [64, 64])
            ZR = T64("ZR"); ZI = T64("ZI"); EPs = T64("EPs"); COS = T64("COS"); SIN = T64("SIN")
            tA = T64("tA"); tB = T64("tB"); tC = T64("tC"); tI = sb2("tI", [64, 64], mybir.dt.int32)
            tt("dve", ZR[:], LR[:], DT[:], ALU.mult, [bt_], [bt_])
            tt("dve", ZI[:], LI[:], DT[:], ALU.mult, [bt_], [bt_])
            act(EPs[:], ZR[:], AF.Exp, [bt_], [bt_])
            for (dst, offs) in ((SIN, 64.0), (COS, 64.25)):
                ts("dve", tA[:], ZI[:], 1.0 / TWO_PI, offs, ALU.mult, ALU.add, [bt_], [bt_])
                cp("dve", tI[:], tA[:], [bt_], [bt_])
                cp("dve", tB[:], tI[:], [bt_], [bt_])
                tt("dve", tA[:], tA[:], tB[:], ALU.subtract, [bt_], [bt_])
                ts("dve", tB[:], tA[:], 0.5, None, ALU.is_gt, None, [bt_], [bt_])
                tt("dve", tA[:], tA[:], tB[:], ALU.subtract, [bt_], [bt_])
                act(dst[:], tA[:], AF.Sin, [bt_], [bt_], scale=TWO_PI)
            PWr = sb2("PWr", [64, 9, 64]); PWi = sb2("PWi", [64, 9, 64])
            mk.op("pool", lambda E: E.memset(PWr[:, 0, :], 1.0), (), [bt_])
            mk.op("pool", lambda E: E.memset(PWi[:, 0, :], 0.0), (), [bt_])
            tt("dve", PWr[:, 1, :], EPs[:], COS[:], ALU.mult, [bt_], [bt_])
            tt("dve", PWi[:, 1, :], EPs[:], SIN[:], ALU.mult, [bt_], [bt_])

            def cmul(or_, oi_, ar, ai, br, bi, n3=None):
                tt("dve", tA[:], ai, bi, ALU.mult, [bt_], [bt_])
                tt("dve", tB[:], ai, br, ALU.mult, [bt_], [bt_])
                tt("dve", tC[:], ar, br, ALU.mult, [bt_], [bt_])
                tt("dve", or_, tC[:], tA[:], ALU.subtract, [bt_], [bt_])
                tt("dve", tC[:], ar, bi, ALU.mult, [bt_], [bt_])
                tt("dve", oi_, tC[:], tB[:], ALU.add, [bt_], [bt_])
            for j in range(2, 9):
                cmul(PWr[:, j, :], PWi[:, j, :], PWr[:, j - 1, :], PWi[:, j - 1, :], PWr[:, 1, :], PWi[:, 1, :])
            NLV = 10
            cp("dve", SQr[:, 0, :], PWr[:, 8, :], [bt_], [bt_]); cp("dve", SQi[:, 0, :], PWi[:, 8, :], [bt_], [bt_])
            for k in range(1, NLV):
                cmul(SQr[:, k, :], SQi[:, k, :], SQr[:, k - 1, :], SQi[:, k - 1, :], SQr[:, k - 1, :], SQi[:, k - 1, :])
            ts("dve", SQin[:], SQi[:], -1.0, None, ALU.mult, None, [bt_], [bt_])
            CFr = T64("CFr"); CFi = T64("CFi"); DEN = T64("DEN"); NR = T64("NR")
            ts("dve", NR[:], PWr[:, 1, :], -1.0, None, ALU.add, None, [bt_], [bt_])
            tt("dve", tA[:], LR[:], LR[:], ALU.mult, [bt_], [bt_])
            tt("dve", tB[:], LI[:], LI[:], ALU.mult, [bt_], [bt_])
            tt("dve", DEN[:], tA[:], tB[:], ALU.add, [bt_], [bt_])
            mk.op("dve", lambda E: E.reciprocal(out=DEN[:], in_=DEN[:]), [bt_], [bt_])
            tt("dve", tA[:], NR[:], LR[:], ALU.mult, [bt_], [bt_])
            tt("dve", tB[:], PWi[:, 1, :], LI[:], ALU.mult, [bt_], [bt_])
            tt("dve", tA[:], tA[:], tB[:], ALU.add, [bt_], [bt_])
            tt("dve", CFr[:], tA[:], DEN[:], ALU.mult, [bt_], [bt_])
            tt("dve", tA[:], PWi[:, 1, :], LR[:], ALU.mult, [bt_], [bt_])
            tt("dve", tB[:], NR[:], LI[:], ALU.mult, [bt_], [bt_])
            tt("dve", tA[:], tA[:], tB[:], ALU.subtract, [bt_], [bt_])
            tt("dve", CFi[:], tA[:], DEN[:], ALU.mult, [bt_], [bt_])
            BbR = sb2("BbR", [64, 64, 16]); BbI = sb2("BbI", [64, 64, 16])
            X1t = sb2("X1t", [64, 64, 16]); X2t = sb2("X2t", [64, 64, 16])

            def b16(t2):
                return t2[:, :, None].broadcast_to([64, 64, 16])

            def cmul3(or_, oi_neg, ar2, ai2, br3, bi3, e1="dve", e2="pool"):
                tt(e1, X1t[:], br3, b16(ar2), ALU.mult, [bt_], [bt_])
                tt(e1, X2t[:], bi3, b16(ai2), ALU.mult, [bt_], [bt_])
                tt(e1, or_, X1t[:], X2t[:], ALU.subtract, [bt_], [bt_])
                tt(e1, X1t[:], bi3, b16(ar2), ALU.mult, [bt_], [bt_])
                tt(e1, X2t[:], br3, b16(ai2), ALU.mult, [bt_], [bt_])
                if oi_neg[1]:
                    tt(e1, X1t[:], X1t[:], X2t[:], ALU.add, [bt_], [bt_])
                    ts(e1, oi_neg[0], X1t[:], -1.0, None, ALU.mult, None, [bt_], [bt_])
                else:
                    tt(e1, oi_neg[0], X1t[:], X2t[:], ALU.add, [bt_], [bt_])
            cmul3(BbR[:], (BbI[:], False), CFr[:], CFi[:], BR[:], BI[:])
            for j in range(8):
                cmul3(LTr[:, j], (LTi[:, j], False), PWr[:, j, :], PWi[:, j, :], BbR[:], BbI[:])
                cmul3(OTr[:, j], (OTn[:, j], True), PWr[:, j + 1, :], PWi[:, j + 1, :], CR[:], CI[:])
            cp("dve", CRb[:], CR[:], [bt_], [bt_]); ts("dve", CInb[:], CI[:], -1.0, None, ALU.mult, None, [bt_], [bt_])

            mk.flush(final=True)
            es2.close()
            KTg = [sb("KTg%d" % i, [16, 15, 16], BF16) for i in range(2)]; b_KTg = [Buf(), Buf()]
            CTg = [sb("CTg%d" % i, [16, 32, 64], BF16) for i in range(2)]; b_CTg = [Buf(), Buf()]
            UGN = 2048
            ugs = [sb("ug%d" % i, [16, UGN]) for i in range(2)]; b_ugs = [Buf(), Buf()]; ugc = [0]
            ub = [sb("ub%d" % i, [16, 8, 1024], BF16) for i in range(2)]; b_ub = [Buf(), Buf()]
            ygs = [sb("yg0", [16, 4096])] * 2; b_ygs = [Buf()] * 2; ygc = [0]
            NBM = 1024
            Wt = [[[sb("W%d%d%d" % (pp, d, c), [64, NBM + 1]) for c in range(2)] for d in range(2)] for pp in range(2)]
            b_Wt = [[Buf() for d in range(2)] for pp in range(2)]
            Sb = [[[sb("Sb%d%d%d" % (i, d, c), [64, NBM + 1], BF16) for c in range(2)] for d in range(2)] for i in range(2)]
            b_Sb = [Buf(), Buf()]
            for pp in range(2):
                for d in range(2):
                    for c in range(2):
                        mk.op("pool", lambda E, t=Wt[pp][d][c]: E.memset(t[:], 0.0), (), [b_Wt[pp][d]])
            id16 = ident[0:16, 0:16]

            def gconsts(g):
                par = g % 2
                pk, bpk = nextps()
                for idx in range(15):
                    if idx == 0:
                        terms = [(0, 0), (1, 0)]
                    elif idx < 8:
                        terms = [(0, idx)]
                    else:
                        terms = [(1, idx - 7)]
                    n = 0
                    for (d, tau) in terms:
                        gi = d * 32 + g
                        mm(pk[0:16, idx * 16:(idx + 1) * 16], LTr[:, tau, gi, :], CRb[:, gi, :], n == 0, False, [bt_], [bpk]); n += 1
                        mm(pk[0:16, idx * 16:(idx + 1) * 16], LTi[:, tau, gi, :], CInb[:, gi, :], False, n == 2 * len(terms) - 1, [bt_], [bpk]); n += 1
                cp("dve", KTg[par][:].rearrange("p a b -> p (a b)"), pk[0:16, 0:240], [bpk], [b_KTg[par]])
                stt(KTg[par][:, 0, :], id16, dsk[:, g:g + 1], KTg[par][:, 0, :], ALU.mult, ALU.add, [b_KTg[par], bt_, b_ident], [b_KTg[par]])
                for q in range(4):
                    pc_, bpc = nextps()
                    for j in range(8):
                        i = q * 8 + j
                        d = i // 16; s_ = (i // 2) % 8; c = i % 2
                        e_ = (7 - s_) if d == 0 else s_
                        src = (LTr if c == 0 else LTi)[:, e_, d * 32 + g, :]
                        mm(pc_[0:16, j * 64:(j + 1) * 64], src, identb[:], True, True, [bt_], [bpc])
                    cp("act", CTg[par][:, q * 8:(q + 1) * 8, :].rearrange("p a b -> p (a b)"),
                       pc_[0:16, 0:512], [bpc], [b_CTg[par]])

            def dims(s):
                toff, coff, T = SEQ[s]
                nblk = T // 8
                BW = min(512, nblk)
                return toff, coff, T, nblk, BW, 8 * BW, nblk // BW

            def front(g, s, st):
                par = g % 2
                toff, coff, T, nblk, BW, TW, nbt = dims(s)
                nlv = int(math.log2(nblk))
                UG = min(UGN, T)
                for hf in range(T // UG):
                    ug = ugs[ugc[0] % 2]; b_ug = b_ugs[ugc[0] % 2]; ugc[0] += 1
                    dma("sp" if hf % 2 == 0 else "act", ug[:, 0:UG],
                        Pscr[1920 + 16 * g:1936 + 16 * g, coff + 1 + hf * UG:coff + 1 + (hf + 1) * UG], [b_P], [b_ug])
                    cp("act", ub[st][:, :, hf * (UG // 8):(hf + 1) * (UG // 8)], ug[:, 0:UG].rearrange("p (b s) -> p s b", s=8),
                       [b_ug], [b_ub[st]])
                if nblk < NBM:
                    for d in range(2):
                        for c in range(2):
                            mk.op("pool", lambda E, t=Wt[0][d][c]: E.memset(t[:], 0.0), (), [b_Wt[0][d]])
                            mk.op("pool", lambda E, t=Wt[1][d][c]: E.memset(t[:], 0.0), (), [b_Wt[1][d]])
                for bt in range(nbt):
                    for d in range(2):
                        for c in range(2):
                            pw_, bpw = nextps()
                            for s_ in range(8):
                                mm(pw_[0:64, 0:BW], CTg[par][:, (d * 8 + s_) * 2 + c, :],
                                   ub[st][:, s_, bt * BW:(bt + 1) * BW],
                                   s_ == 0, s_ == 7, [b_CTg[par], b_ub[st]], [bpw])
                            o0 = bt * BW + (1 if d == 0 else 0)
                            cp("act", Wt[0][d][c][:, o0:o0 + BW], pw_[0:64, 0:BW], [bpw], [b_Wt[0][d]])
                cur = 0
                for k in range(nlv):
                    sh = 1 << k
                    n_ = nblk - sh
                    for d in range(2):
                        gi = d * 32 + g
                        lo = 1 if d == 0 else 0
                        Wc = Wt[cur][d]; Wn = Wt[1 - cur][d]
                        bWc = b_Wt[cur][d]; bWn = b_Wt[1 - cur][d]
                        if d == 0:
                            dst = slice(lo + sh, lo + nblk); srcs = slice(lo, lo + n_); keep = slice(lo, lo + sh)
                        else:
                            dst = slice(lo, lo + n_); srcs = slice(lo + sh, lo + nblk); keep = slice(lo + n_, lo + nblk)
                        ar = SQr[:, k, gi:gi + 1]; ai = SQi[:, k, gi:gi + 1]; ain = SQin[:, k, gi:gi + 1]
                        stt(Wn[0][:, dst], Wc[0][:, srcs], ar, Wc[0][:, dst], ALU.mult, ALU.add, [bWc, bt_], [bWn])
                        stt(Wn[1][:, dst], Wc[1][:, srcs], ar, Wc[1][:, dst], ALU.mult, ALU.add, [bWc, bt_], [bWn])
                        stt(Wn[0][:, dst], Wc[1][:, srcs], ain, Wn[0][:, dst], ALU.mult, ALU.add, [bWc, bWn, bt_], [bWn])
                        stt(Wn[1][:, dst], Wc[0][:, srcs], ai, Wn[1][:, dst], ALU.mult, ALU.add, [bWc, bWn, bt_], [bWn])
                        cp("pool", Wn[0][:, keep], Wc[0][:, keep], [bWc], [bWn])
                        cp("pool", Wn[1][:, keep], Wc[1][:, keep], [bWc], [bWn])
                    cur = 1 - cur
                for d in range(2):
                    for c in range(2):
                        if d == 0:
                            cp("act", Sb[st][d][c][:, 1:nblk + 1], Wt[cur][d][c][:, 1:nblk + 1], [b_Wt[cur][d]], [b_Sb[st]])
                            mk.op("pool", lambda E, t=Sb[st][d][c]: E.memset(t[:, 0:1], 0.0), (), [b_Sb[st]])
                        else:
                            cp("act", Sb[st][d][c][:, 0:nblk], Wt[cur][d][c][:, 0:nblk], [b_Wt[cur][d]], [b_Sb[st]])
                            mk.op("pool", lambda E, t=Sb[st][d][c], nblk=nblk: E.memset(t[:, nblk:nblk + 1], 0.0), (), [b_Sb[st]])

            def back(g, s, st):
                par = g % 2
                toff, coff, T, nblk, BW, TW, nbt = dims(s)
                for bt in range(nbt):
                    yg = ygs[ygc[0] % 2]; b_yg = b_ygs[ygc[0] % 2]; ygc[0] += 1
                    ubv = ub[st][:, :, bt * BW:(bt + 1) * BW]
                    ygv = yg[:, 0:TW].rearrange("p (b s) -> p s b", s=8)
                    for t in range(8):
                        py, bpy = nextps()
                        for s_ in range(8):
                            idx = 0 if s_ == t else ((t - s_) if s_ < t else (7 + s_ - t))
                            mm(py[0:16, 0:BW], KTg[par][:, idx, :], ubv[:, s_, :], s_ == 0, False, [b_KTg[par], b_ub[st]], [bpy])
                        b0 = bt * BW
                        mm(py[0:16, 0:BW], OTr[:, t, g, :], Sb[st][0][0][:, b0:b0 + BW], False, False, [bt_, b_Sb[st]], [bpy])
                        mm(py[0:16, 0:BW], OTn[:, t, g, :], Sb[st][0][1][:, b0:b0 + BW], False, False, [bt_, b_Sb[st]], [bpy])
                        mm(py[0:16, 0:BW], OTr[:, 7 - t, 32 + g, :], Sb[st][1][0][:, b0 + 1:b0 + 1 + BW], False, False, [bt_, b_Sb[st]], [bpy])
                        mm(py[0:16, 0:BW], OTn[:, 7 - t, 32 + g, :], Sb[st][1][1][:, b0 + 1:b0 + 1 + BW], False, True, [bt_, b_Sb[st]], [bpy])
                        cp("act", ygv[:, t, :], py[0:16, 0:BW], [bpy], [b_yg])
                    dma("sp", YS[16 * g:16 * g + 16, toff + bt * TW:toff + (bt + 1) * TW], yg[:, 0:TW], [b_yg], [b_YS])

            units = [(g, s) for g in range(32) for s in range(NS)]
            prev = None
            for ui, (g, s) in enumerate(units):
                if s == 0:
                    gconsts(g)
                front(g, s, ui % 2)
                if prev is not None:
                    back(prev[0], prev[1], (ui - 1) % 2)
                prev = (g, s)
            back(prev[0], prev[1], (len(units) - 1) % 2)
            mk.flush(final=True)

    if upto >= 2:
        s5_pass()

    def mix_pass():
        es, sb, ps = scope("mx_")
        with es:
            N = 256
            pz = [ps("pz%d" % i, [128, 512]) for i in range(8)]
            b_pz = [Buf() for _ in range(8)]
            pctr = [0]

            def nextps():
                i = pctr[0] % 8
                pctr[0] += 1
                return pz[i], b_pz[i]
            bc_ = Buf()
            o64 = sb("o64", [64, 64])
            mk.op("pool", lambda E: E.memset(o64[:], 1.0 / 64.0), (), [bc_])
            ones_bf = sb("ones_bf", [128, 128], BF16)
            mk.op("pool", lambda E: E.memset(ones_bf[:], 1.0), (), [bc_])
            lg = sb("lg", [64, 8]); lb = sb("lb", [64, 8])
            dma("sp", lg[:], lnx_g.rearrange("(h p) -> p h", p=64), (), [bc_], slow=True)
            dma("sp", lb[:], lnx_b.rearrange("(h p) -> p h", p=64), (), [bc_], slow=True)
            g2t = sb("g2t", [128, 512]); dma("sp", g2t[:], g2, (), [bc_])
            mug = sb("mug", [128, 1]); hmg = sb("hmg", [128, 1]); omg = sb("omg", [128, 1])
            dma("sp", mug[:], mu_shift[1792:1920].rearrange("(p o) -> p o", o=1), (), [bc_], slow=True)
            ts("dve", hmg[:], mug[:], 0.5, None, ALU.mult, None, [bc_], [bc_])
            ts("dve", omg[:], mug[:], -1.0, 1.0, ALU.mult, ALU.add, [bc_], [bc_])
            bgl = sb("bgl", [128, 4]); s5g = sb("s5g", [128, 4])
            dma("sp", bgl[:], b_glu.rearrange("(q p) -> p q", p=128), (), [bc_], slow=True)
            dma("sp", s5g[:], s5_out_g.rearrange("(q p) -> p q", p=128), (), [bc_], slow=True)
            wst = sb("wst", [128, 4, 128])
            mk.op("pool", lambda E: E.memset(wst[:], 0.0), (), [bc_])
            for g in range(32):
                r0 = (g % 8) * 16
                dma("sp" if g % 2 == 0 else "act", wst[r0:r0 + 16, g // 8, r0:r0 + 16], w_glu[g], [bc_], [bc_])
            Wbd = sb("Wbd", [128, 4, 128], BF16)
            cp("dve", Wbd[:], wst[:], [bc_], [bc_])
            wo_r = sb("wo_r", [64, 8, D], BF16); wo_s = sb("wo_s", [128, 4, D], BF16)
            stg = [sb("stg%d" % i, [128, D]) for i in range(2)]; b_stg = [Buf(), Buf()]
            for h in range(8):
                st_ = stg[h % 2]; bs_ = b_stg[h % 2]
                dma("sp", st_[0:64, :], w_out[h * 64:(h + 1) * 64, :], (), [bs_])
                cp("pool", wo_r[:, h, :], st_[0:64, :], [bs_], [bc_])
            for q in range(4):
                st_ = stg[q % 2]; bs_ = b_stg[q % 2]
                dma("sp", st_[:], w_out[512 + q * 128:512 + (q + 1) * 128, :], (), [bs_])
                cp("pool", wo_s[:, q, :], st_[:], [bs_], [bc_])

            def T4(name, dt=F32):
                return sb(name, [64, 8, N], dt), Buf()
            YF, b_YF = T4("YF"); YB, b_YB = T4("YB"); BF_, b_BF = T4("BF"); BB, b_BB = T4("BB")
            Ym, b_Ym = T4("Ym"); SQt, b_SQt = T4("SQt"); RS, b_RS = T4("RS")
            yr, b_yr = T4("yr", BF16)
            XG = sb("XG", [128, N + 2]); b_XG = Buf(); tw = sb("tw", [128, N]); b_tw = Buf()
            sg = sb("sg", [128, N]); b_sg = Buf()
            S5 = sb("S5", [128, 4, N]); b_S5 = Buf(); Z1 = sb("Z1", [128, 4, N]); b_Z1 = Buf()
            Z2 = sb("Z2", [128, 4, N]); b_Z2 = Buf(); Zb = sb("Zb", [128, 4, N], BF16); b_Zb = Buf()
            r2 = sb("r2", [128, N]); b_r2 = Buf()
            ysb = sb("ysb", [128, 4, N], BF16); b_ysb = Buf()
            xTt = sb("xTt", [128, 8, N]); b_xTt = Buf(); X1t = sb("X1t", [128, 8, N]); b_X1t = Buf()

            def bc(t, n):
                return t[:, :, None].broadcast_to([64, 8, n])

            def fl(t):
                return t[:].rearrange("p h t -> p (h t)")
            for s, (toff, coff, T) in enumerate(SEQ):
                for ti in range(T // N):
                    tl = ti * N; tg = toff + tl; c0 = coff + tl
                    dma("sp", YF[:], YD[0, :, :, tg:tg + N], [b_YD], [b_YF])
                    dma("act", YB[:], YD[1, :, :, tg:tg + N], [b_YD], [b_YB])
                    dma("sp", BF_[:], BD[0, :, :, tg:tg + N], [b_BD], [b_BF])
                    dma("act", BB[:], BD[1, :, :, tg:tg + N], [b_BD], [b_BB])
                    dma("sp", XG[:], Pscr[1792:1920, c0:c0 + N + 2], [b_P], [b_XG])
                    dma("act", S5[:], YS[:, tg:tg + N].rearrange("(q p) t -> p q t", p=128), [b_YS], [b_S5])
                    dma("sp", xTt[:], XT[:, tg:tg + N].rearrange("(k p) t -> p k t", p=128), [b_XT], [b_xTt])
                    pendS = []
                    mk.deferred = pendS
                    K0 = 2.0 * math.sqrt(2.0 / math.pi)
                    tt("pool", Z1[:], S5[:], S5[:], ALU.mult, [b_S5], [b_Z1])
                    ts("dve", Z1[:], Z1[:], 0.044715, 1.0, ALU.mult, ALU.add, [b_Z1], [b_Z1])
                    tt("pool", Z1[:], Z1[:], S5[:], ALU.mult, [b_Z1, b_S5], [b_Z1])
                    act(Z1[:], Z1[:], AF.Sigmoid, [b_Z1], [b_Z1], scale=K0)
                    tt("pool", Z1[:], Z1[:], S5[:], ALU.mult, [b_Z1, b_S5], [b_Z1])
                    cp("dve", Zb[:], Z1[:], [b_Z1], [b_Zb])
                    for q in range(4):
                        pm_, bpm = nextps()
                        mm(pm_[:, 0:N], Wbd[:, q, :], Zb[:, q, :], True, True, [bc_, b_Zb], [bpm])
                        mk.op("act", lambda E, pm_=pm_, q=q: E.activation(out=Z2[:, q, :], in_=pm_[:, 0:N], func=AF.Sigmoid,
                                                                        bias=bgl[:, q:q + 1], scale=1.0), [bpm, bc_], [b_Z2])
                    tt("pool", Z2[:], Z2[:], Z1[:], ALU.mult, [b_Z2, b_Z1], [b_Z2])
                    tt("pool", Zb[:], Z2[:], Z2[:], ALU.mult, [b_Z2], [b_Zb])
                    pm_, bpm = nextps()
                    for q in range(4):
                        mm(pm_[:, 0:N], ones_bf[:], Zb[:, q, :], q == 0, q == 3, [bc_, b_Zb], [bpm])
                    act(r2[:], pm_[:, 0:N], AF.Sqrt, [bpm], [b_r2], bias=RMS_EPS, scale=1.0 / 512.0)
                    mk.op("dve", lambda E: E.reciprocal(out=r2[:], in_=r2[:]), [b_r2], [b_r2])
                    for q in range(4):
                        stt(ysb[:, q, :], Z2[:, q, :], s5g[:, q:q + 1], r2[:], ALU.mult, ALU.mult, [b_Z2, b_r2, bc_], [b_ysb])
                    mk.deferred = None
                    tt("pool", Ym[:], YF[:], YB[:], ALU.add, [b_YF, b_YB], [b_Ym])
                    mk.replay(pendS, 2)
                    for j in range(4):
                        pm_, bpm = nextps()
                        mm(pm_[0:64, :], o64[:], fl(Ym)[:, j * 512:(j + 1) * 512], True, True, [bc_, b_Ym], [bpm])
                        tt("dve", fl(YF)[:, j * 512:(j + 1) * 512], fl(Ym)[:, j * 512:(j + 1) * 512], pm_[0:64, :],
                           ALU.subtract, [bpm, b_Ym], [b_YF])
                    tt("pool", SQt[:], YF[:], YF[:], ALU.mult, [b_YF], [b_SQt])
                    mk.replay(pendS, 2)
                    for j in range(4):
                        pm_, bpm = nextps()
                        mm(pm_[0:64, :], o64[:], fl(SQt)[:, j * 512:(j + 1) * 512], True, True, [bc_, b_SQt], [bpm])
                        act(fl(RS)[:, j * 512:(j + 1) * 512], pm_[0:64, :], AF.Sqrt, [bpm], [b_RS], bias=LNX_EPS)
                    mk.replay(pendS, 2)
                    mk.op("dve", lambda E: E.reciprocal(out=RS[:], in_=RS[:]), [b_RS], [b_RS])
                    mk.replay(pendS, 2)
                    tt("pool", Ym[:], YF[:], RS[:], ALU.mult, [b_YF, b_RS], [b_Ym])
                    mk.replay(pendS, 2)
                    tt("pool", Ym[:], Ym[:], bc(lg, N), ALU.mult, [b_Ym, bc_], [b_Ym])
                    mk.replay(pendS, 2)
                    tt("pool", Ym[:], Ym[:], bc(lb, N), ALU.add, [b_Ym, bc_], [b_Ym])
                    mk.replay(pendS, 2)
                    tt("pool", Ym[:], Ym[:], BF_[:], ALU.add, [b_Ym, b_BF], [b_Ym])
                    mk.replay(pendS, 2)
                    tt("pool", Ym[:], Ym[:], BB[:], ALU.add, [b_Ym, b_BB], [b_Ym])
                    mk.replay(pendS, 2)
                    tt("dve", tw[:], XG[:, 0:N], XG[:, 2:N + 2], ALU.add, [b_XG], [b_tw])
                    mk.replay(pendS, 2)
                    ts("dve", tw[:], tw[:], hmg[:, 0:1], None, ALU.mult, None, [b_tw, bc_], [b_tw])
                    mk.replay(pendS, 2)
                    stt(sg[:], XG[:, 1:N + 1], omg[:, 0:1], tw[:], ALU.mult, ALU.add, [b_XG, b_tw, bc_], [b_sg])
                    mk.replay(pendS, 2)
                    act(sg[:], sg[:], AF.Sigmoid, [b_sg], [b_sg])
                    mk.replay(pendS, 2)
                    for h2_ in range(4):
                        pm_, bpm = nextps()
                        for j in range(2):
                            h = 2 * h2_ + j
                            mm(pm_[0:64, j * N:(j + 1) * N], g2t[:, h * 64:(h + 1) * 64], sg[:], True, True, [bc_, b_sg], [bpm])
                        tt("dve", yr[:, 2 * h2_:2 * h2_ + 2, :], pm_[0:64, :].rearrange("p (h n) -> p h n", n=N),
                           Ym[:, 2 * h2_:2 * h2_ + 2, :], ALU.mult, [bpm, b_Ym], [b_yr])
                    mk.replay(pendS, len(pendS))
                    for dm in range(8):
                        pm_, bpm = nextps()
                        for h in range(8):
                            mm(pm_[:, 0:N], wo_r[:, h, dm * 128:(dm + 1) * 128], yr[:, h, :], h == 0, False, [bc_, b_yr], [bpm])
                        for q in range(4):
                            mm(pm_[:, 0:N], wo_s[:, q, dm * 128:(dm + 1) * 128], ysb[:, q, :], False, q == 3, [bc_, b_ysb], [bpm])
                        stt(X1t[:, dm, :], pm_[:, 0:N], modT[:, 16 + dm, s:s + 1], xTt[:, dm, :], ALU.mult, ALU.add,
                            [bpm, b_mod, b_xTt], [b_X1t])
                    dma("sp", X1[:, tg:tg + N].rearrange("(k p) t -> p k t", p=128), X1t[:], [b_X1t], [b_X1])
            mk.flush(final=True)

    if upto >= 3:
        mix_pass()

    def ffn_pass():
        es, sb, ps = scope("ff_")
        with es:
            N = 256
            pz = [ps("pz%d" % i, [128, 512]) for i in range(8)]
            b_pz = [Buf() for _ in range(8)]
            pctr = [0]

            def nextps():
                i = pctr[0] % 8
                pctr[0] += 1
                return pz[i], b_pz[i]
            bc_ = Buf()
            ones_bf = sb("ones_bf", [128, 128], BF16)
            mk.op("pool", lambda E: E.memset(ones_bf[:], 1.0), (), [bc_])
            n2g = sb("n2g", [128, 8]); fg = sb("fg", [128, 8]); sc2 = sb("sc2", [128, 8, NS])
            dma("sp", n2g[:], norm2_g.rearrange("(k p) -> p k", p=128), (), [bc_], slow=True)
            dma("sp", fg[:], final_g.rearrange("(k p) -> p k", p=128), (), [bc_], slow=True)
            for s in range(NS):
                stt(sc2[:, :, s], modT[:, 32:40, s], 1.0, n2g[:], ALU.add, ALU.mult, [b_mod, bc_], [bc_])
            w1 = sb("w1", [128, 8, DFF], BF16); w3 = sb("w3", [128, 8, DFF], BF16); w2_ = sb("w2_", [128, 22, D], BF16)
            stg = [sb("stg%d" % i, [128, 1408]) for i in range(2)]; b_stg = [Buf(), Buf()]
            n = 0
            for (src, dst) in ((w_ff1, w1), (w_ff3, w3)):
                for k in range(8):
                    for hf in range(2):
                        st_ = stg[n % 2]; bs_ = b_stg[n % 2]; n += 1
                        dma("sp" if n % 2 else "act", st_[:], src[k * 128:(k + 1) * 128, hf * 1408:(hf + 1) * 1408], (), [bs_])
                        cp("pool" if n % 2 else "dve", dst[:, k, hf * 1408:(hf + 1) * 1408], st_[:], [bs_], [bc_])
            for k in range(22):
                st_ = stg[n % 2]; bs_ = b_stg[n % 2]; n += 1
                dma("sp" if n % 2 else "act", st_[:, 0:D], w_ff2[k * 128:(k + 1) * 128, :], (), [bs_])
                cp("pool" if n % 2 else "dve", w2_[:, k, :], st_[:, 0:D], [bs_], [bc_])
            FS = []
            for i_ in range(2):
                FS.append(dict(X1t=sb("X1t%d" % i_, [128, 8, N]), b_X1t=Buf(), h2=sb("h2%d" % i_, [128, 8, N], BF16), b_h2=Buf()))
            sq = sb("sq", [128, 8, N], BF16); b_sq = Buf()
            rstd = sb("rstd", [128, N]); b_rstd = Buf(); tmp = sb("tmp", [128, N]); b_tmp = Buf()
            sq2, b_sq2 = sq, b_sq; rstd2 = sb("rstd2", [128, N]); b_rstd2 = Buf()
            fm = sb("fm", [128, 22, N], BF16); b_fm = Buf()
            av = [sb("av%d" % i, [128, N]) for i in range(2)]; b_av = [Buf(), Buf()]
            X2t = sb("X2t", [128, 8, N]); b_X2t = Buf()
            ytm = sb("ytm", [128, 2, D]); b_ytm = Buf()

            def rms_bc(src, bsrc, sq_, bsq_, rs_, brs_):
                for dc in range(8):
                    act(sq_[:, dc, :], src[:, dc, :], AF.Square, [bsrc], [bsq_])
                pm_, bpm = nextps()
                for dc in range(8):
                    mm(pm_[:, 0:N], ones_bf[:], sq_[:, dc, :], dc == 0, dc == 7, [bc_, bsq_], [bpm])
                act(rs_[:], pm_[:, 0:N], AF.Sqrt, [bpm], [brs_], bias=RMS_EPS, scale=1.0 / D)
                mk.op("dve", lambda E: E.reciprocal(out=rs_[:], in_=rs_[:]), [brs_], [brs_])

            tiles = [(s_, tg_) for s_, (toff, coff, T) in enumerate(SEQ) for tg_ in range(toff, toff + T, N)]

            def ffront(s, tg, F_):
                X1t, b_X1t, h2, b_h2 = F_["X1t"], F_["b_X1t"], F_["h2"], F_["b_h2"]
                dma("sp", X1t[:], X1[:, tg:tg + N].rearrange("(k p) t -> p k t", p=128), [b_X1], [b_X1t])
                rms_bc(X1t, b_X1t, sq, b_sq, rstd, b_rstd)
                for dc in range(8):
                    tt("dve", tmp[:], X1t[:, dc, :], rstd[:], ALU.mult, [b_X1t, b_rstd], [b_tmp])
                    ts("dve", h2[:, dc, :], tmp[:], sc2[:, dc, s:s + 1], modT[:, 24 + dc, s:s + 1], ALU.mult, ALU.add,
                       [b_tmp, bc_, b_mod], [b_h2])

            def fback(s, tg, F_, pend):
                X1t, b_X1t, h2, b_h2 = F_["X1t"], F_["b_X1t"], F_["h2"], F_["b_h2"]
                for mc in range(22):
                    p1, bp1 = nextps()
                    for dc in range(8):
                        mm(p1[:, 0:N], w1[:, dc, mc * 128:(mc + 1) * 128], h2[:, dc, :], dc == 0, dc == 7, [bc_, b_h2], [bp1])
                    p3, bp3 = nextps()
                    for dc in range(8):
                        mm(p3[:, 0:N], w3[:, dc, mc * 128:(mc + 1) * 128], h2[:, dc, :], dc == 0, dc == 7, [bc_, b_h2], [bp3])
                    a_ = av[mc % 2]; ba_ = b_av[mc % 2]
                    act(a_[:], p1[:, 0:N], AF.Silu, [bp1], [ba_])
                    tt("dve", fm[:, mc, :], a_[:], p3[:, 0:N], ALU.mult, [ba_, bp3], [b_fm])
                    mk.replay(pend, 2)
                mk.replay(pend, len(pend))
                for dm in range(8):
                    pm_, bpm = nextps()
                    for mc in range(22):
                        mm(pm_[:, 0:N], w2_[:, mc, dm * 128:(dm + 1) * 128], fm[:, mc, :], mc == 0, mc == 21, [bc_, b_fm], [bpm])
                    stt(X2t[:, dm, :], pm_[:, 0:N], modT[:, 40 + dm, s:s + 1], X1t[:, dm, :], ALU.mult, ALU.add,
                        [bpm, b_mod, b_X1t], [b_X2t])
                rms_bc(X2t, b_X2t, sq2, b_sq2, rstd2, b_rstd2)
                for dc in range(8):
                    stt(X2t[:, dc, :], X2t[:, dc, :], fg[:, dc:dc + 1], rstd2[:], ALU.mult, ALU.mult,
                        [b_X2t, bc_, b_rstd2], [b_X2t])
                for j in range(N // 128):
                    for dq in range(2):
                        pm_, bpm = nextps()
                        for k in range(4):
                            dc = dq * 4 + k
                            mk.op("pe", lambda E, pm_=pm_, dc=dc, j=j, k=k: E.transpose(
                                out=pm_[:, k * 128:(k + 1) * 128], in_=X2t[:, dc, j * 128:(j + 1) * 128], identity=ident[:]),
                                [b_X2t, b_ident], [bpm])
                        cp("act" if dq == 0 else "dve", ytm[:, j, dq * 512:(dq + 1) * 512], pm_[:], [bpm], [b_ytm])
                dma("sp", y_out[tg:tg + N, :].rearrange("(j p) d -> p j d", p=128), ytm[:], [b_ytm], [])

            ffront(tiles[0][0], tiles[0][1], FS[0])
            for i_, (s_, tg_) in enumerate(tiles):
                pend = []
                if i_ + 1 < len(tiles):
                    mk.deferred = pend
                    ffront(tiles[i_ + 1][0], tiles[i_ + 1][1], FS[(i_ + 1) % 2])
                    mk.deferred = None
                fback(s_, tg_, FS[i_ % 2], pend)
            mk.flush(final=True)

    if upto >= 4:
        ffn_pass()

    outer.close()
    return nc


def core_inputs(P, x, c):
    f = np.float32
    m = {"x": x, "c": c}
    for k in ("norm1_g", "w_ada", "b_ada", "w_in", "mu_shift", "w0", "w2", "a0", "a2", "g2", "k_k", "k_a",
              "lnx_g", "lnx_b", "lam_re", "lam_im", "log_dt", "b_re", "b_im", "c_re", "c_im", "w_glu",
              "s5_out_g", "w_out", "norm2_g", "w_ff1", "w_ff3", "w_ff2", "final_g"):
        m[k] = P[k]
    m["r_k"] = P["r_k"].reshape(512)
    m["d_skip"] = P["d_skip"].reshape(512)
    m["b_glu"] = P["b_glu"].reshape(512)
    return {k: np.ascontiguousarray(v, dtype=f) for k, v in m.items()}


_T_PROMPT = 8192
_T_SAMPLE = 4096


def kernel(**inputs):
    n = 8
    P = {}
    for k, v in inputs.items():
        if k in ("x_prompt", "x_sample", "c_prompt", "c_sample"):
            continue
        v = np.asarray(v)
        P[k] = v if k == "final_g" else v[0]
    xp = np.asarray(inputs["x_prompt"]); xs = np.asarray(inputs["x_sample"])
    cpr = np.asarray(inputs["c_prompt"]); cs = np.asarray(inputs["c_sample"])
    TS = [xp.shape[1], xs.shape[1]]
    nc = build_program(TS, upto=int(os.environ.get('KUPTO', '99')))
    in_maps = []
    for b in range(n):
        x = np.concatenate([xp[b], xs[b]], axis=0)
        c = np.stack([cpr[b], cs[b]], axis=0)
        in_maps.append(core_inputs(P, x, c))
    res = run_bass_kernel_spmd(nc, in_maps, core_ids=list(range(n)))
    yp = np.stack([res.results[b]["y"][:TS[0]] for b in range(n)], axis=0).astype(np.float32)
    ys = np.stack([res.results[b]["y"][TS[0]:] for b in range(n)], axis=0).astype(np.float32)
    return (yp, ys)
```
